# Optimizing a Trainium2 kernel written in Bass

```python
import jax, jax.numpy as jnp
from jax import lax
import numpy as np

D_MODEL = 1024
BATCH = 8
SEQ = 4096
DEPTH = 2
DEC_BATCH = 4
DEC_SEQ = 4096
PAST_LEN = 128

GRID_W = 64
HEAD_DIM = 64
D_RWKV = 512
N_RWKV_HEADS = D_RWKV // HEAD_DIM
D_NAT = 512
N_NAT_HEADS = D_NAT // HEAD_DIM
LORA_W = 64
LORA_A = 64
LORA_G = 128
N_DIR = 2
WIN_H = 8
WIN_W = 16
N_MEM = 256
N_XATTN_HEADS = 4
XATTN_HEAD_DIM = D_MODEL // N_XATTN_HEADS
D_FF = 4 * D_MODEL
NORM_EPS = 1e-6
GN_EPS = 1e-5 * HEAD_DIM

RWKV_SPLITS = [D_RWKV, 2 * D_RWKV, 3 * D_RWKV,
               3 * D_RWKV + N_DIR * LORA_W,
               3 * D_RWKV + N_DIR * LORA_W + N_DIR * LORA_A]
RWKV_COLS = 3 * D_RWKV + N_DIR * LORA_W + N_DIR * LORA_A + LORA_G
NAT_COLS = 3 * D_NAT
GATE_COLS = 2 * D_MODEL
D_IN = RWKV_COLS + NAT_COLS + GATE_COLS

kernel_name = "rwkv7_natten_hybrid_encoder"


def rmsnorm(x, g):
    xf = x.astype(jnp.float32)
    y = xf * lax.rsqrt(jnp.mean(xf * xf, axis=-1, keepdims=True) + NORM_EPS)
    return (y * g.astype(jnp.float32)).astype(x.dtype)


def centred_shift(p, mu_prev, mu_next):
    zero = jnp.zeros_like(p[:, :1])
    p_prev = jnp.concatenate([zero, p[:, :-1]], axis=1)
    p_next = jnp.concatenate([p[:, 1:], zero], axis=1)
    return p + mu_prev * (p_prev - p) + mu_next * (p_next - p)


def _wkv7_step(S, inp):
    r, w, k, v, a, b = inp
    Sa = jnp.einsum('bhij,bhj->bhi', S, a)
    S = S * w[:, :, None, :] + Sa[..., None] * b[:, :, None, :] + v[..., None] * k[:, :, None, :]
    return S, jnp.einsum('bhij,bhj->bhi', S, r)


def wkv7_scan(r, w, k, v, a, b, reverse):
    B, T, H, N = r.shape
    xs = tuple(jnp.moveaxis(t, 1, 0) for t in (r, w, k, v, a, b))
    S0 = jnp.zeros((B, H, N, N), jnp.float32)
    _, y = lax.scan(_wkv7_step, S0, xs, reverse=reverse)
    return jnp.moveaxis(y, 0, 1)


def rwkv7_branch(p, mu_prev, mu_next, w0, w_up, a0, a_up, g_up, k_k, k_a, r_k, gn_g, gn_b):
    B, T, _ = p.shape
    H, N = N_RWKV_HEADS, HEAD_DIM
    f32 = jnp.float32
    p = centred_shift(p.astype(f32), mu_prev.astype(f32), mu_next.astype(f32))
    r, k, v, wd, ad, gd = jnp.split(p, RWKV_SPLITS, axis=-1)
    wd = wd.reshape(B, T, N_DIR, LORA_W)
    ad = ad.reshape(B, T, N_DIR, LORA_A)
    w_raw = w0.astype(f32) + jnp.einsum('btdl,dlc->btdc', jnp.tanh(wd), w_up.astype(f32))
    decay = jnp.exp(-jnp.exp(-jax.nn.softplus(-w_raw) - 0.5))
    a = jax.nn.sigmoid(a0.astype(f32) + jnp.einsum('btdl,dlc->btdc', ad, a_up.astype(f32)))
    g = jax.nn.sigmoid(gd) @ g_up.astype(f32)
    heads = lambda t: t.reshape(B, T, H, N)
    kk = heads(k * k_k.astype(f32))
    kk = kk / jnp.maximum(jnp.sqrt(jnp.sum(kk * kk, axis=-1, keepdims=True)), 1e-12)
    kk = kk.reshape(B, T, D_RWKV)
    rh, vh = heads(r), heads(v)
    out = jnp.zeros((B, T, H, N), f32)
    bonus = jnp.zeros((B, T, H, 1), f32)
    for d, rev in ((0, False), (1, True)):
        a_d = a[:, :, d]
        k_d = k * (1.0 + (a_d - 1.0) * k_a.astype(f32))
        out = out + wkv7_scan(rh, heads(decay[:, :, d]), heads(k_d), vh,
                              heads(-kk), heads(kk * a_d), rev)
        bonus = bonus + jnp.sum(rh * heads(k_d) * r_k.astype(f32), axis=-1, keepdims=True)
    mean = jnp.mean(out, axis=-1, keepdims=True)
    var = jnp.mean(jnp.square(out - mean), axis=-1, keepdims=True)
    out = ((out - mean) * lax.rsqrt(var + GN_EPS)).reshape(B, T, D_RWKV)
    out = out * gn_g.astype(f32) + gn_b.astype(f32)
    out = out + (bonus * vh).reshape(B, T, D_RWKV)
    return out * g


def neighbourhood_attention(q, k, v, rpb):
    B, T, _ = q.shape
    rows = T // GRID_W
    kh = min(WIN_H, rows)
    kw = WIN_W
    H, N = N_NAT_HEADS, HEAD_DIM
    f32 = jnp.float32
    grid = lambda t: t.reshape(B, rows, GRID_W, H, N).transpose(0, 3, 1, 2, 4)
    qg, kg, vg = grid(q), grid(k), grid(v)
    cols = np.arange(GRID_W)
    col_start = np.clip(cols - kw // 2, 0, GRID_W - kw)
    col_idx = col_start[:, None] + np.arange(kw)[None, :]
    dj = col_idx - cols[:, None] + (WIN_W - 1)
    row_start = np.clip(np.arange(rows) - kh // 2, 0, rows - kh).astype(np.int32)
    rpb_cols = rpb.astype(f32)[:, :, dj]
    scale = HEAD_DIM ** -0.5

    def one_row(args):
        i, rs, q_row = args
        k_rows = lax.dynamic_slice_in_dim(kg, rs, kh, axis=2)
        v_rows = lax.dynamic_slice_in_dim(vg, rs, kh, axis=2)
        k_win = k_rows[:, :, :, col_idx].astype(f32)
        v_win = v_rows[:, :, :, col_idx].astype(f32)
        di = rs + jnp.arange(kh) - i + (WIN_H - 1)
        bias = rpb_cols[:, di].transpose(0, 2, 1, 3)
        s = jnp.einsum('bhwn,bhrwcn->bhwrc', q_row.astype(f32), k_win) * scale + bias[None]
        pr = jax.nn.softmax(s.reshape(B, H, GRID_W, kh * kw), axis=-1).reshape(B, H, GRID_W, kh, kw)
        return jnp.einsum('bhwrc,bhrwcn->bhwn', pr, v_win).astype(q.dtype)

    out = lax.map(one_row, (jnp.arange(rows, dtype=jnp.int32), jnp.asarray(row_start),
                            jnp.moveaxis(qg, 2, 0)))
    return out.transpose(1, 0, 3, 2, 4).reshape(B, T, D_NAT)


def memory_cross_attention(h, mem, g_mem, w_q, w_kv, w_o):
    B, T, _ = h.shape
    M = mem.shape[1]
    q = (h @ w_q).reshape(B, T, N_XATTN_HEADS, XATTN_HEAD_DIM)
    k, v = jnp.split(rmsnorm(mem, g_mem) @ w_kv, 2, axis=-1)
    k = k.reshape(B, M, N_XATTN_HEADS, XATTN_HEAD_DIM)
    v = v.reshape(B, M, N_XATTN_HEADS, XATTN_HEAD_DIM)
    s = jnp.einsum('bthd,bmhd->bhtm', q.astype(jnp.float32), k.astype(jnp.float32)) * XATTN_HEAD_DIM ** -0.5
    pr = jax.nn.softmax(s, axis=-1)
    o = jnp.einsum('bhtm,bmhd->bthd', pr, v.astype(jnp.float32)).reshape(B, T, D_MODEL)
    return o.astype(h.dtype) @ w_o


def trunk(x, mem, norm_mix, w_in, mu_prev, mu_next, w0, w_up, a0, a_up, g_up, k_k, k_a, r_k,
          gn_g, gn_b, rpb, w_br_rwkv, w_br_nat, w_out, norm_x, norm_mem, w_xq, w_xkv, w_xo,
          norm_ff, w_ff1, w_ff2, norm_final):
    for l in range(DEPTH):
        p = rmsnorm(x, norm_mix[l]) @ w_in[l]
        p_rwkv = p[..., :RWKV_COLS]
        p_nat = p[..., RWKV_COLS:RWKV_COLS + NAT_COLS]
        gate_r, gate_n = jnp.split(p[..., RWKV_COLS + NAT_COLS:], 2, axis=-1)
        y_r = rwkv7_branch(p_rwkv, mu_prev[l], mu_next[l], w0[l], w_up[l], a0[l], a_up[l],
                           g_up[l], k_k[l], k_a[l], r_k[l], gn_g[l], gn_b[l]).astype(x.dtype)
        nq, nk, nv = jnp.split(p_nat, 3, axis=-1)
        y_n = neighbourhood_attention(nq, nk, nv, rpb[l])
        mixed = jax.nn.sigmoid(gate_r) * (y_r @ w_br_rwkv[l]) + jax.nn.sigmoid(gate_n) * (y_n @ w_br_nat[l])
        x = x + mixed @ w_out[l]
        x = x + memory_cross_attention(rmsnorm(x, norm_x[l]), mem, norm_mem[l], w_xq[l], w_xkv[l], w_xo[l])
        hf = rmsnorm(x, norm_ff[l]) @ w_ff1[l]
        x = x + jnp.square(jax.nn.relu(hf)) @ w_ff2[l]
    return rmsnorm(x, norm_final)


def setup_inputs(seed: int = 0) -> dict:
    key = jax.random.key(seed)
    ks = iter(jax.random.split(key, 40))
    nrm = lambda shape, s: jax.random.normal(next(ks), shape, jnp.float32) * s
    uni = lambda shape, lo, hi: jax.random.uniform(next(ks), shape, jnp.float32, lo, hi)
    L, D = DEPTH, D_MODEL
    return {
        "x_prompt": nrm((BATCH, SEQ, D), 1.0),
        "x_sample": nrm((DEC_BATCH, DEC_SEQ, D), 1.0),
        "mem_prompt": nrm((BATCH, N_MEM, D), 1.0),
        "mem_sample": nrm((DEC_BATCH, N_MEM, D), 1.0),
        "norm_mix": 1.0 + nrm((L, D), 0.05),
        "w_in": nrm((L, D, D_IN), D ** -0.5),
        "mu_prev": uni((L, RWKV_COLS), 0.0, 0.5),
        "mu_next": uni((L, RWKV_COLS), 0.0, 0.5),
        "w0": uni((L, N_DIR, D_RWKV), -3.0, 1.0),
        "w_up": nrm((L, N_DIR, LORA_W, D_RWKV), 0.1 * LORA_W ** -0.5),
        "a0": nrm((L, N_DIR, D_RWKV), 0.1),
        "a_up": nrm((L, N_DIR, LORA_A, D_RWKV), 0.5 * LORA_A ** -0.5),
        "g_up": nrm((L, LORA_G, D_RWKV), LORA_G ** -0.5),
        "k_k": 0.85 + nrm((L, D_RWKV), 0.05),
        "k_a": 1.0 + nrm((L, D_RWKV), 0.05),
        "r_k": nrm((L, N_RWKV_HEADS, HEAD_DIM), 0.1),
        "gn_g": 1.0 + nrm((L, D_RWKV), 0.05),
        "gn_b": nrm((L, D_RWKV), 0.01),
        "rpb": nrm((L, N_NAT_HEADS, 2 * WIN_H - 1, 2 * WIN_W - 1), 0.1),
        "w_br_rwkv": nrm((L, D_RWKV, D), D_RWKV ** -0.5),
        "w_br_nat": nrm((L, D_NAT, D), D_NAT ** -0.5),
        "w_out": nrm((L, D, D), D ** -0.5),
        "norm_x": 1.0 + nrm((L, D), 0.05),
        "norm_mem": 1.0 + nrm((L, D), 0.05),
        "w_xq": nrm((L, D, D), D ** -0.5),
        "w_xkv": nrm((L, D, 2 * D), D ** -0.5),
        "w_xo": nrm((L, D, D), D ** -0.5),
        "norm_ff": 1.0 + nrm((L, D), 0.05),
        "w_ff1": nrm((L, D, D_FF), D ** -0.5),
        "w_ff2": nrm((L, D_FF, D), D_FF ** -0.5),
        "norm_final": 1.0 + nrm((D,), 0.05),
    }


def reference(x_prompt, x_sample, mem_prompt, mem_sample, norm_mix, w_in, mu_prev, mu_next, w0, w_up,
              a0, a_up, g_up, k_k, k_a, r_k, gn_g, gn_b, rpb, w_br_rwkv, w_br_nat, w_out, norm_x,
              norm_mem, w_xq, w_xkv, w_xo, norm_ff, w_ff1, w_ff2, norm_final):
    y_prompt = trunk(x_prompt, mem_prompt, norm_mix, w_in, mu_prev, mu_next, w0, w_up, a0, a_up, g_up,
                     k_k, k_a, r_k, gn_g, gn_b, rpb, w_br_rwkv, w_br_nat, w_out, norm_x, norm_mem,
                     w_xq, w_xkv, w_xo, norm_ff, w_ff1, w_ff2, norm_final)
    y_sample = trunk(x_sample, mem_sample, norm_mix, w_in, mu_prev, mu_next, w0, w_up, a0, a_up, g_up,
                     k_k, k_a, r_k, gn_g, gn_b, rpb, w_br_rwkv, w_br_nat, w_out, norm_x, norm_mem,
                     w_xq, w_xkv, w_xo, norm_ff, w_ff1, w_ff2, norm_final)
    return (y_prompt, y_sample)
```

```python
import numpy as np
from contextlib import ExitStack
import concourse.bass as bass
import concourse.mybir as mybir
from concourse.bass_utils import run_bass_kernel_spmd

F32 = mybir.dt.float32
BF16 = mybir.dt.bfloat16
AF = mybir.ActivationFunctionType
ALU = mybir.AluOpType
AX = mybir.AxisListType

D = 1024
DIN = 5504
RW = 1920
NQ0 = 1920
NV0 = 2944
G0 = 3456
NMEM = 256
DFF = 4096
EPS = 1e-6
GN_EPS = 1e-5 * 64
CH = 128

EPOCH = 12000
ENGS = ("pe", "dve", "act", "pool", "sp")


class Res:
    __slots__ = ("name", "last_w", "readers", "dsem", "w_is_dma")

    def __init__(self, name=""):
        self.name = name
        self.last_w = None
        self.readers = []
        self.dsem = None
        self.w_is_dma = False


class Sched:
    def __init__(self, nc, stack):
        self.nc = nc
        self.stack = stack
        self.ops = {e: [] for e in ENGS}
        self.cnt = {e: 0 for e in ENGS}
        self.esems = {e: [] for e in ENGS}
        self.known = {e: {} for e in ENGS}
        self.sems = {}
        self.nsem = 0
        self.same_engine_raw = True
        self.dma_latest = {}
        self.free_dsems = []

    def new_sem(self, tag):
        h = self.stack.enter_context(self.nc.semaphore(f"{tag}_{self.nsem}"))
        sid = self.nsem
        self.nsem += 1
        self.sems[sid] = h
        return sid

    def _eng_point(self, eng):
        n = self.cnt[eng]
        self.cnt[eng] = n + 1
        ep, v = divmod(n, EPOCH)
        while len(self.esems[eng]) <= ep:
            self.esems[eng].append(self.new_sem(f"e{eng}"))
        return (self.esems[eng][ep], v + 1)

    def _dma_point(self, res):
        d = res.dsem
        if d is None or d[1] + 16 > EPOCH:
            if self.free_dsems and d is None:
                d = self.free_dsems.pop()
            else:
                d = [self.new_sem("d"), 0]
            res.dsem = d
        d[1] += 16
        self.dma_latest[d[0]] = d[1]
        return (d[0], d[1])

    def recycle(self, resources):
        seen = set()
        for r in resources:
            d = r.dsem
            if d is not None and id(d) not in seen and d[1] + 64 < EPOCH:
                seen.add(id(d))
                self.free_dsems.append(d)
            r.dsem = None

    def _need(self, eng, waits, pt):
        sid, val = pt
        if self.known[eng].get(sid, 0) >= val:
            return
        if waits.get(sid, 0) < val:
            waits[sid] = val

    def _deps(self, eng, reads, writes, is_dma=False):
        waits = {}
        for r in reads:
            if r.last_w is not None:
                w_eng = r.last_w[2]
                if w_eng != eng or is_dma or (self.same_engine_raw and eng != "pe"):
                    self._need(eng, waits, r.last_w[:2])
        for w in writes:
            if w.last_w is not None:
                w_eng = w.last_w[2]
                same_dma_group = is_dma and w.w_is_dma and not w.readers
                if (w_eng != eng or is_dma) and not same_dma_group:
                    self._need(eng, waits, w.last_w[:2])
            for (sid, val, r_eng) in w.readers:
                if r_eng != eng or is_dma:
                    self._need(eng, waits, (sid, val))
        for sid, val in waits.items():
            self.known[eng][sid] = val
        return list(waits.items())

    def op(self, eng, fn, reads=(), writes=()):
        waits = self._deps(eng, reads, writes)
        pt = self._eng_point(eng)
        self.ops[eng].append((waits, fn, pt[0], 1))
        for r in reads:
            r.readers.append((pt[0], pt[1], eng))
        for w in writes:
            w.last_w = (pt[0], pt[1], eng)
            w.readers = []
            w.w_is_dma = False
        return pt

    def dma(self, queue, out, in_, reads=(), writes=(), **kw):
        waits = self._deps(queue, reads, writes, is_dma=True)
        pt = self._dma_point(writes[0])

        def fn(e, out=out, in_=in_, kw=kw):
            return e.dma_start(out=out, in_=in_, **kw)
        self.ops[queue].append((waits, fn, pt[0], 16))
        for r in reads:
            r.readers.append((pt[0], pt[1], "dma"))
        for w in writes:
            w.last_w = (pt[0], pt[1], "dma")
            w.readers = []
            w.w_is_dma = True
            if w is not writes[0]:
                w.dsem = writes[0].dsem
        return pt

    def final_wait(self, eng, resources):
        waits = {}
        for r in resources:
            if r.last_w is not None:
                sid, val = r.last_w[:2]
                waits[sid] = max(waits.get(sid, 0), val)
        self.ops[eng].append((list(waits.items()), None, None, 0))

    def barrier(self):
        pts = {}
        for e in ENGS:
            n = self.cnt[e]
            if n > 0:
                ep, v = divmod(n - 1, EPOCH)
                pts[self.esems[e][ep]] = v + 1
        for sid, val in self.dma_latest.items():
            pts[sid] = max(pts.get(sid, 0), val)
        for e in ENGS:
            waits = []
            for sid, val in pts.items():
                if self.known[e].get(sid, 0) < val:
                    waits.append((sid, val))
                    self.known[e][sid] = val
            self.ops[e].append((waits, None, None, 0))

    def emit(self):
        nc = self.nc
        sems = self.sems

        def run(engname):
            def body(e):
                for (waits, fn, sid_, inc) in self.ops[engname]:
                    for sid, val in waits:
                        e.wait_ge(sems[sid], val)
                    if fn is None:
                        continue
                    ins = fn(e)
                    ins.then_inc(sems[sid_], inc)
            return body

        with nc.Block() as block:
            block.tensor(run("pe"))
            block.vector(run("dve"))
            block.scalar(run("act"))
            block.gpsimd(run("pool"))
            block.sync(run("sp"))
        self.ops = {e: [] for e in ENGS}


class Rot:
    def __init__(self, items):
        self.items = items
        self.i = 0

    def next(self):
        it = self.items[self.i % len(self.items)]
        self.i += 1
        return it


WEIGHT_SPECS = [
    ("norm_mix", (D,)), ("w_in", (D, DIN)), ("mu_prev", (RW,)), ("mu_next", (RW,)),
    ("w0", (2, 512)), ("w_up", (2, 64, 512)), ("a0", (2, 512)), ("a_up", (2, 64, 512)),
    ("g_up", (128, 512)), ("k_k", (512,)), ("k_a", (512,)), ("r_k", (8, 64)),
    ("gn_g", (512,)), ("gn_b", (512,)), ("rpb", (8, 15, 31)),
    ("w_br_rwkv", (512, D)), ("w_br_nat", (512, D)), ("w_out", (D, D)),
    ("norm_x", (D,)), ("norm_mem", (D,)), ("w_xq", (D, D)), ("w_xkv", (D, 2 * D)), ("w_xo", (D, D)),
    ("norm_ff", (D,)), ("w_ff1", (D, DFF)), ("w_ff2", (DFF, D)),
]


class Builder:
    def __init__(self, T, NS, DEPTH, dbg=(), stop=None, ext_in=(), phases=None):
        self.T, self.NS, self.DEPTH = T, NS, DEPTH
        self.ext_in = set(ext_in)
        self.phases = phases or ("p1", "p3", "p2", "p4", "p5", "p6")
        self.dbg = set(dbg)
        self.stop = stop
        self.nc = bass.Bass("TRN2", target_bir_lowering=False)
        nc = self.nc
        self.x_in = nc.dram_tensor("x", [NS, T, D], F32, kind="ExternalInput").ap()
        self.mem_in = nc.dram_tensor("mem", [NS, NMEM, D], F32, kind="ExternalInput").ap()
        self.w = {}
        for name, shp in WEIGHT_SPECS:
            self.w[name] = nc.dram_tensor(name, [DEPTH] + list(shp), F32, kind="ExternalInput").ap()
        self.w["norm_final"] = nc.dram_tensor("norm_final", [D], F32, kind="ExternalInput").ap()
        self.y_out = nc.dram_tensor("y", [NS, T, D], F32, kind="ExternalOutput").ap()
        self.scr = {}
        self.scr_res = {}

    def scratch(self, name, shape, dt):
        kind = "ExternalOutput" if name in self.dbg else ("ExternalInput" if name in self.ext_in else "Internal")
        t = self.nc.dram_tensor(name, shape, dt, kind=kind).ap()
        self.scr[name] = t
        return t

    def dres(self, key):
        r = self.scr_res.get(key)
        if r is None:
            r = Res(str(key))
            self.scr_res[key] = r
        return r

    def build(self):
        nc = self.nc
        T, NS = self.T, self.NS
        self.scratch("xres", [NS, T, D], F32)
        self.scratch("pf", [NS, RW, T], F32)
        self.scratch("nq", [NS, 512, T], BF16)
        self.scratch("nk", [NS, 512, T], BF16)
        self.scratch("nv", [NS, T, 512], BF16)
        self.scratch("sg", [NS, 2048, T], BF16)
        self.scratch("yr", [NS, T, 512], BF16)
        self.scratch("yn", [NS, T, 512], BF16)
        self.scratch("yf", [NS, T, 512], F32)
        self.scratch("bon", [NS, T, 8], F32)
        with ExitStack() as st:
            self.S = Sched(nc, st)
            done = False
            for l in range(self.DEPTH):
                for ph in self.phases:
                    getattr(self, "phase_" + ph)(l)
                    if self.stop == (l, ph):
                        done = True
                        break
                if done:
                    break
            with ExitStack() as ph:
                self.S.final_wait("pool", list(self.scr_res.values()))
                self.S.emit()
        return nc

    def begin(self):
        self.S.barrier()
        self.phc = getattr(self, "phc", 0) + 1
        self.ph = ExitStack()
        self.ph_res = []
        return self.ph

    def end(self):
        self.S.emit()
        self.S.recycle(self.ph_res)
        self.ph.close()

    def tile(self, name, shape, dt):
        t = self.ph.enter_context(self.nc.sbuf_tensor(f"{name}_{self.phc}", shape, dt))
        r = Res(name)
        self.ph_res.append(r)
        return t, r

    def psum(self, name, shape, dt):
        t = self.ph.enter_context(self.nc.psum_tensor(f"{name}_{self.phc}", shape, dt))
        r = Res(name)
        self.ph_res.append(r)
        return t, r

    def rot(self, name, shape, dt, n, ps=False):
        f = self.psum if ps else self.tile
        return Rot([f(f"{name}{i}", shape, dt) for i in range(n)])

    def make_ident(self):
        S = self.S
        idf, Ridf = self.tile("identf", [128, 128], F32)
        idb, Ridb = self.tile("identb", [128, 128], BF16)

        def mk(e):
            e.memset(idf[:], 0.0)
            return e.affine_select(out=idf[:], in_=idf[:], pattern=[[-1, 128]], compare_op=ALU.not_equal,
                                   fill=1.0, base=0, channel_multiplier=1)
        S.op("pool", mk, writes=[Ridf])
        S.op("dve", lambda e: e.tensor_copy(out=idb[:], in_=idf[:]), reads=[Ridf], writes=[Ridb])
        return idb, Ridb, idf, Ridf

    def load_weight(self, dst, Rdst, src2d, K, cols, stg, col0=0, kc0=0):
        S = self.S
        nk = K // 128
        srcv = src2d.rearrange("(kc p) c -> p kc c", p=128)
        engs = ("pool", "dve", "act")
        i = 0
        for kc in range(nk):
            for c0 in range(0, cols, 2048):
                cw = min(2048, cols - c0)
                stile, Rst = stg.next()
                S.dma("sp", stile[:, 0:cw], srcv[:, kc, c0:c0 + cw], writes=[Rst])
                eng = engs[i % 3]
                i += 1
                o = dst[:, kc0 + kc, col0 + c0:col0 + c0 + cw]
                if eng == "act":
                    S.op("act", lambda e, o=o, s=stile, cw=cw: e.copy(out=o, in_=s[:, 0:cw]), reads=[Rst], writes=[Rdst])
                else:
                    S.op(eng, lambda e, o=o, s=stile, cw=cw: e.tensor_copy(out=o, in_=s[:, 0:cw]), reads=[Rst], writes=[Rdst])

    def load_bcast(self, dst, Rdst, src1d, n):
        self.S.dma("sp", dst[:, 0:n], src1d.rearrange("(o n) -> o n", o=1).partition_broadcast(128), writes=[Rdst])

    def norm_to_hT(self, xt, Rxt, b, gb, Rgb, hT, RhT, tcol, ident, Rid, tmp):
        S = self.S
        junk, Rjunk = tmp["junk"].next()
        ss, Rss = tmp["ss"].next()
        h, Rh = tmp["h"].next()
        pT, RpT = tmp["pT"].next()
        S.op("act", lambda e: e.activation(out=junk[:], in_=xt[:, b, :], func=AF.Square, accum_out=ss[:, 0:1]),
             reads=[Rxt], writes=[Rjunk, Rss])

        S.op("act", lambda e: e.activation(out=ss[:, 1:2], in_=ss[:, 0:1], func=AF.Sqrt, scale=1.0 / D, bias=self.eps_t[:, 0:1]),
             reads=[Rss, self.Reps], writes=[Rss])
        S.op("dve", lambda e: e.reciprocal(out=ss[:, 2:3], in_=ss[:, 1:2]), reads=[Rss], writes=[Rss])
        S.op("dve", lambda e: e.scalar_tensor_tensor(out=h[:], in0=xt[:, b, :], scalar=ss[:, 2:3], in1=gb[:],
                                                     op0=ALU.mult, op1=ALU.mult),
             reads=[Rxt, Rss, Rgb], writes=[Rh])

        def tr(e):
            for kc in range(8):
                ins = e.transpose(out=pT[:, kc * 128:(kc + 1) * 128], in_=h[:, kc * 128:(kc + 1) * 128], identity=ident[:])
            return ins
        S.op("pe", tr, reads=[Rh, Rid], writes=[RpT])
        S.op("act", lambda e: e.copy(out=hT[:, :, tcol:tcol + 128], in_=pT[:].rearrange("p (k t) -> p k t", k=8)),
             reads=[RpT], writes=[RhT])
        return ss

    def norm_tmp(self):
        self.eps_t, Re = self.tile("eps_t", [128, 2], F32)
        eps_t = self.eps_t
        self.S.op("pool", lambda e: e.memset(eps_t[:], EPS), writes=[Re])
        self.Reps = Re
        return {
            "junk": self.rot("junk", [128, D], BF16, 1),
            "ss": self.rot("ss", [128, 4], F32, 4),
            "h": self.rot("h", [128, D], BF16, 2),
            "pT": self.rot("pT", [128, 1024], BF16, 2, ps=True),
        }

    def phase_p1(self, l):
        S, T, NS = self.S, self.T, self.NS
        self.begin()
        TT = 512
        win, Rwin = self.tile("win", [128, 8, DIN], BF16)
        stg = self.rot("wstg", [128, 2048], F32, 2)
        gb, Rgb = self.tile("gb", [128, D], F32)
        ident, Rid, _, _ = self.make_ident()
        self.load_bcast(gb, Rgb, self.w["norm_mix"][l], D)
        self.load_weight(win, Rwin, self.w["w_in"][l], D, DIN, stg)
        tmp = self.norm_tmp()
        xts = self.rot("xt", [128, 4, D], F32, 2)
        hTs = self.rot("hT", [128, 8, TT], BF16, 2)
        pss = self.rot("ps", [128, 512], F32, 4, ps=True)
        of32 = self.rot("of32", [128, 512], F32, 3)
        obf = self.rot("obf", [128, 512], BF16, 4)
        xsrc = self.x_in if l == 0 else self.scr["xres"]
        pf, nq, nk, nv, sg = (self.scr[k] for k in ("pf", "nq", "nk", "nv", "sg"))
        ev = 0
        for s in range(NS):
            for t0 in range(0, T, TT):
                xt, Rxt = xts.next()
                hT, RhT = hTs.next()
                xr = [self.dres(("xres", s, t0))] if l > 0 else []
                S.dma("sp", xt[:], xsrc[s, t0:t0 + TT, :].rearrange("(b p) d -> p b d", p=128), reads=xr, writes=[Rxt])
                for b in range(4):
                    self.norm_to_hT(xt, Rxt, b, gb, Rgb, hT, RhT, b * 128, ident, Rid, tmp)
                for cc in range(DIN // 128):
                    c0 = cc * 128
                    if NV0 <= c0 < G0:
                        continue
                    ps, Rps = pss.next()

                    def mm(e, ps=ps, c0=c0, hT=hT):
                        for kc in range(8):
                            ins = e.matmul(ps[:], lhsT=win[:, kc, c0:c0 + 128], rhs=hT[:, kc, :], start=(kc == 0), stop=(kc == 7))
                        return ins
                    S.op("pe", mm, reads=[Rwin, RhT], writes=[Rps])
                    if c0 < RW:
                        o, Ro = of32.next()
                        eng = "dve" if ev % 2 == 0 else "act"
                        ev += 1
                        if eng == "dve":
                            S.op("dve", lambda e, o=o, ps=ps: e.tensor_copy(out=o[:], in_=ps[:]), reads=[Rps], writes=[Ro])
                        else:
                            S.op("act", lambda e, o=o, ps=ps: e.copy(out=o[:], in_=ps[:]), reads=[Rps], writes=[Ro])
                        S.dma("pool", pf[s, c0:c0 + 128, t0:t0 + TT], o[:], reads=[Ro], writes=[self.dres(("pf", s))])
                    elif c0 < NV0:
                        o, Ro = obf.next()
                        S.op("dve", lambda e, o=o, ps=ps: e.tensor_copy(out=o[:], in_=ps[:]), reads=[Rps], writes=[Ro])
                        if c0 < NQ0 + 512:
                            S.dma("pool", nq[s, c0 - NQ0:c0 - NQ0 + 128, t0:t0 + TT], o[:], reads=[Ro], writes=[self.dres(("nq", s))])
                        else:
                            c1 = c0 - NQ0 - 512
                            S.dma("pool", nk[s, c1:c1 + 128, t0:t0 + TT], o[:], reads=[Ro], writes=[self.dres(("nk", s))])
                    else:
                        o, Ro = obf.next()
                        S.op("act", lambda e, o=o, ps=ps: e.activation(out=o[:], in_=ps[:], func=AF.Sigmoid), reads=[Rps], writes=[Ro])
                        c1 = c0 - G0
                        S.dma("pool", sg[s, c1:c1 + 128, t0:t0 + TT], o[:], reads=[Ro], writes=[self.dres(("sg", s))])
                for b in range(4):
                    ps, Rps = pss.next()

                    def mmv(e, ps=ps, b=b, hT=hT):
                        for kc in range(8):
                            ins = e.matmul(ps[:], lhsT=hT[:, kc, b * 128:(b + 1) * 128], rhs=win[:, kc, NV0:NV0 + 512], start=(kc == 0), stop=(kc == 7))
                        return ins
                    S.op("pe", mmv, reads=[Rwin, RhT], writes=[Rps])
                    o, Ro = obf.next()
                    S.op("dve", lambda e, o=o, ps=ps: e.tensor_copy(out=o[:], in_=ps[:]), reads=[Rps], writes=[Ro])
                    S.dma("pool", nv[s, t0 + b * 128:t0 + (b + 1) * 128, :], o[:], reads=[Ro], writes=[self.dres(("nv", s))])
        self.end()

    def phase_p2(self, l):
        S, T, NS = self.S, self.T, self.NS
        self.begin()
        C = 128
        NCH = T // C
        LD = 0.6065306597126334
        pf, yf, yr = self.scr["pf"], self.scr["yf"], self.scr["yr"]
        bon = self.scr["bon"]
        op = S.op
        bc = lambda ap, shape: ap.broadcast_to(shape)
        identb, Ridb, identf, Ridf = self.make_ident()
        mup, Rmup = self.tile("mup", [128, 15], F32)
        mun, Rmun = self.tile("mun", [128, 15], F32)
        w0t, Rw0 = self.tile("w0t", [128, 2, 4], F32)
        a0t, Ra0 = self.tile("a0t", [128, 2, 4], F32)
        kkc, Rkkc = self.tile("kkc", [128, 4], F32)
        kac, Rkac = self.tile("kac", [128, 4], F32)
        omka, Romka = self.tile("omka", [128, 4], F32)
        rkc, Rrkc = self.tile("rkc", [128, 4], F32)
        S.dma("sp", mup[:], self.w["mu_prev"][l].rearrange("(c p) -> p c", p=128), writes=[Rmup], allow_slow_non_contiguous=True)
        S.dma("sp", mun[:], self.w["mu_next"][l].rearrange("(c p) -> p c", p=128), writes=[Rmun], allow_slow_non_contiguous=True)
        for d in range(2):
            S.dma("sp", w0t[:, d, :], self.w["w0"][l, d].rearrange("(c p) -> p c", p=128), writes=[Rw0], allow_slow_non_contiguous=True)
            S.dma("sp", a0t[:, d, :], self.w["a0"][l, d].rearrange("(c p) -> p c", p=128), writes=[Ra0], allow_slow_non_contiguous=True)
        S.dma("sp", kkc[:], self.w["k_k"][l].rearrange("(c p) -> p c", p=128), writes=[Rkkc], allow_slow_non_contiguous=True)
        S.dma("sp", kac[:], self.w["k_a"][l].rearrange("(c p) -> p c", p=128), writes=[Rkac], allow_slow_non_contiguous=True)
        S.dma("sp", rkc[:], self.w["r_k"][l].rearrange("h n -> (h n)").rearrange("(c p) -> p c", p=128), writes=[Rrkc], allow_slow_non_contiguous=True)
        op("dve", lambda e: e.tensor_scalar(out=omka[:], in0=kac[:], scalar1=-1.0, scalar2=1.0, op0=ALU.mult, op1=ALU.add), reads=[Rkac], writes=[Romka])
        gng, Rgng = self.tile("gng", [128, 512], F32)
        gnb, Rgnb = self.tile("gnb", [128, 512], F32)
        self.load_bcast(gng, Rgng, self.w["gn_g"][l], 512)
        self.load_bcast(gnb, Rgnb, self.w["gn_b"][l], 512)
        wstg, Rwstg = self.tile("lstg", [128, 3, 512], F32)
        S.dma("sp", wstg[:, 0, :], self.w["w_up"][l].rearrange("d l c -> (d l) c"), writes=[Rwstg])
        S.dma("sp", wstg[:, 1, :], self.w["a_up"][l].rearrange("d l c -> (d l) c"), writes=[Rwstg])
        S.dma("sp", wstg[:, 2, :], self.w["g_up"][l], writes=[Rwstg])
        wupz, Rwupz = self.tile("wupz", [128, 2, 512], BF16)
        aupz, Raupz = self.tile("aupz", [128, 2, 512], BF16)
        gup, Rgup = self.tile("gup", [128, 512], BF16)
        op("pool", lambda e: e.memset(wupz[:].rearrange("p d c -> p (d c)"), 0.0), writes=[Rwupz])
        op("pool", lambda e: e.memset(aupz[:].rearrange("p d c -> p (d c)"), 0.0), writes=[Raupz])
        for d in range(2):
            op("dve", lambda e, d=d: e.tensor_copy(out=wupz[d * 64:(d + 1) * 64, d, :], in_=wstg[d * 64:(d + 1) * 64, 0, :]), reads=[Rwstg, Rwupz], writes=[Rwupz])
            op("dve", lambda e, d=d: e.tensor_copy(out=aupz[d * 64:(d + 1) * 64, d, :], in_=wstg[d * 64:(d + 1) * 64, 1, :]), reads=[Rwstg, Raupz], writes=[Raupz])
        op("dve", lambda e: e.tensor_copy(out=gup[:], in_=wstg[:, 2, :]), reads=[Rwstg], writes=[Rgup])
        onesf, Ronesf = self.tile("onesf", [128, 128], F32)
        op("pool", lambda e: e.memset(onesf[:], 1.0), writes=[Ronesf])
        blk1, Rblk1 = self.tile("blk1", [128, 128], F32)
        hsel, Rhsel = self.tile("hsel", [128, 2], F32)
        bdm, Rbdm = self.tile("bdm", [128, 4, 2, 64], F32)

        def mkconst(e):
            e.memset(blk1[:], 0.0)
            e.memset(hsel[:], 0.0)
            e.memset(bdm[:].rearrange("p c h i -> p (c h i)"), 0.0)
            e.memset(blk1[0:64, 0:64], 1.0)
            e.memset(blk1[64:128, 64:128], 1.0)
            e.memset(hsel[0:64, 0:1], 1.0)
            e.memset(hsel[64:128, 1:2], 1.0)
            e.memset(bdm[0:64, :, 0, :], 1.0)
            return e.memset(bdm[64:128, :, 1, :], 1.0)
        op("dve", mkconst, writes=[Rblk1, Rhsel, Rbdm])
        gne, Rgne = self.tile("gne", [128, 1], F32)
        op("pool", lambda e: e.memset(gne[:], GN_EPS), writes=[Rgne])
        mbase, Rmbase = self.tile("mbase", [128, 4, 128], F32)

        op("pool", lambda e: e.memset(mbase[:].rearrange("p a x -> p (a x)"), 1.0), writes=[Rmbase])
        op("pool", lambda e: e.affine_select(out=mbase[:, 0, :], in_=mbase[:, 0, :], pattern=[[1, 128]], compare_op=ALU.is_gt, fill=0.0, base=0, channel_multiplier=-1), reads=[Rmbase], writes=[Rmbase])
        op("pool", lambda e: e.affine_select(out=mbase[:, 1, :], in_=mbase[:, 1, :], pattern=[[1, 128]], compare_op=ALU.is_ge, fill=0.0, base=0, channel_multiplier=-1), reads=[Rmbase], writes=[Rmbase])
        op("pool", lambda e: e.affine_select(out=mbase[:, 2, :], in_=mbase[:, 2, :], pattern=[[-1, 128]], compare_op=ALU.is_gt, fill=0.0, base=0, channel_multiplier=1), reads=[Rmbase], writes=[Rmbase])
        op("pool", lambda e: e.affine_select(out=mbase[:, 3, :], in_=mbase[:, 3, :], pattern=[[-1, 128]], compare_op=ALU.is_ge, fill=0.0, base=0, channel_multiplier=1), reads=[Rmbase], writes=[Rmbase])
        mG, RmG = self.tile("mG", [128, 2, 2, 2, 128], BF16)
        mL, RmL = self.tile("mL", [128, 2, 2, 128], BF16)
        for d in range(2):
            si, ii, li = (0, 1, 2) if d == 0 else (2, 3, 0)
            for h2 in range(2):
                op("dve", lambda e, d=d, h2=h2, si=si: e.tensor_copy(out=mG[:, d, h2, 0, :], in_=mbase[:, si, :]), reads=[Rmbase], writes=[RmG])
                op("dve", lambda e, d=d, h2=h2, ii=ii: e.tensor_copy(out=mG[:, d, h2, 1, :], in_=mbase[:, ii, :]), reads=[Rmbase], writes=[RmG])
                op("dve", lambda e, d=d, h2=h2, li=li: e.tensor_copy(out=mL[:, d, h2, :], in_=mbase[:, li, :]), reads=[Rmbase], writes=[RmL])
        Eg, REg = self.tile("Eg", [8, 3, 128], F32)
        op("pool", lambda e: e.memset(Eg[:].rearrange("p a x -> p (a x)"), 1.0), writes=[REg])
        for gi, b_ in enumerate((16, 32, 64)):
            op("pool", lambda e, gi=gi, b_=b_: e.affine_select(out=Eg[:, gi, :], in_=Eg[:, gi, :], pattern=[[1, 128]], compare_op=ALU.is_ge, fill=0.0, base=0, channel_multiplier=-b_),
               reads=[REg], writes=[REg])
            op("pool", lambda e, gi=gi, b_=b_: e.affine_select(out=Eg[:, gi, :], in_=Eg[:, gi, :], pattern=[[-1, 128]], compare_op=ALU.is_ge, fill=0.0, base=b_ - 1, channel_multiplier=b_),
               reads=[REg], writes=[REg])
        bdf, Rbdf = self.tile("bdf", [128, 4, 128], F32)
        op("pool", lambda e: e.memset(bdf[:, 3, :], 1.0), writes=[Rbdf])
        pbm, Rpbm = self.psum("pbm", [128, 512], F32)

        def mmE(e):
            for gi in range(3):
                ins = e.matmul(pbm[:, gi * 128:(gi + 1) * 128], lhsT=Eg[:, gi, :], rhs=Eg[:, gi, :], start=True, stop=True)
            return ins
        op("pe", mmE, reads=[REg], writes=[Rpbm])
        op("dve", lambda e: e.tensor_copy(out=bdf[:, 0:3, :].rearrange("p a x -> p (a x)"), in_=pbm[:, 0:384]), reads=[Rpbm, Rbdf], writes=[Rbdf])
        bd16, Rbd16 = self.tile("bd16", [128, 2, 128], BF16)
        offm, Roffm = self.tile("offm", [128, 3, 2, 128], BF16)
        for h2 in range(2):
            op("dve", lambda e, h2=h2: e.tensor_copy(out=bd16[:, h2, :], in_=bdf[:, 0, :]), reads=[Rbdf], writes=[Rbd16])
            for lv in range(3):
                op("dve", lambda e, h2=h2, lv=lv: e.tensor_tensor(out=offm[:, lv, h2, :], in0=bdf[:, lv + 1, :], in1=bdf[:, lv, :], op=ALU.subtract),
                   reads=[Rbdf], writes=[Roffm])
        Ps = self.rot("P", [128, 15, 130], F32, 1)
        tAs = self.rot("tA", [128, 15, 128], F32, 1)
        tBs = self.rot("tB", [128, 15, 128], F32, 1)
        phs = self.rot("ph", [128, 15, 128], F32, 1)
        f4 = lambda n, k=1: self.rot(n, [128, 4, 128], F32, k)
        b4 = lambda n, k=1: self.rot(n, [128, 4, 128], BF16, k)
        lws, avs, pres, cums, cprs = f4("lw"), f4("av"), f4("pre"), f4("cum"), f4("cpr")
        ecs, ecps, eis = f4("ec"), f4("ecp"), f4("ei")
        kraws, sqs, nrms, kkvs, tts, kds, bvs = (f4(n) for n in ("kraw", "sq", "nrm", "kkv", "tt", "kd", "bv"))
        Kfs, Bfs, rks = kraws, nrms, sqs
        ATs, RTs, KTs, BTs, KhTs, BhTs, vTs = (b4(n, 2) for n in ("AT", "RT", "KT", "BT", "KhT", "BhT", "vT"))
        twds = self.rot("twd", [128, 128], BF16, 2)
        adbs = self.rot("adb", [128, 128], BF16, 2)
        sgds = self.rot("sgd", [128, 128], BF16, 2)
        gCs = self.rot("gC", [128, 4], F32, 2)
        ARbds = self.rot("ARbd", [128, 4, 2, 2, 128], BF16, 2)
        Bbds = self.rot("Bbd", [128, 4, 2, 128], BF16, 2)
        for (t_, r_) in ARbds.items:
            op("pool", lambda e, t_=t_: e.memset(t_[:].rearrange("p c h x t -> p (c h x t)"), 0.0), writes=[r_])
        for (t_, r_) in Bbds.items:
            op("pool", lambda e, t_=t_: e.memset(t_[:].rearrange("p c h t -> p (c h t)"), 0.0), writes=[r_])
        VKs = self.rot("VK", [128, 1024], BF16, 2)
        Bhats = self.rot("Bhat", [128, 512], BF16, 2)
        GKs = self.rot("GK", [128, 4, 2, 2, 128], BF16, 1)
        GBs = self.rot("GB", [128, 4, 2, 2, 128], BF16, 1)
        Lhs = self.rot("Lh", [128, 4, 2, 128], BF16, 1)
        lmq = [[self.tile(f"lmq{h}_{k}", [128, 4, 128], BF16) for k in range(2)] for h in range(8)]
        zzt = [self.tile(f"zz{h}", [128, 2, 128], BF16) for h in range(8)]
        ttt = [[self.tile(f"tt{h}_{k}", [128, 2, 128], BF16) for k in range(2)] for h in range(8)]
        L16s = self.rot("L16", [128, 4, 2, 128], BF16, 1)
        M16s = self.rot("M16", [128, 4, 2, 128], BF16, 1)
        St, RSt = self.tile("St", [128, 4, 2, 64], F32)
        Sbd, RSbd = self.tile("Sbd", [128, 4, 2, 64], BF16)
        stmp, Rstmp = self.tile("stmp", [128, 4, 2, 64], F32)
        Wsbs = self.rot("Wsb", [128, 512], BF16, 2)
        Usbs = self.rot("Usb", [128, 512], BF16, 2)
        Ysbs = self.rot("Ysb", [128, 512], F32, 2)
        bons = self.rot("bonv", [128, 8], F32, 2)
        bon0s = self.rot("bon0", [128, 8], F32, 2)
        yfls = self.rot("yfl", [128, 512], F32, 2)
        gtms = self.rot("gtm", [128, 512], F32, 2)
        sqvs = self.rot("sqv", [128, 512], F32, 1)
        ycs = self.rot("yc", [128, 512], F32, 1)
        vvs = self.rot("vv", [128, 512], F32, 1)
        stat = self.rot("stat", [128, 6, 8], F32, 2)
        yos = self.rot("yo", [128, 512], BF16, 2)
        pbanks = self.rot("pb", [128, 512], F32, 6, ps=True)
        ptr = self.rot("ptr", [128, 1024], BF16, 1, ps=True)
        pfv = [pf[s].rearrange("(c p) t -> p c t", p=128) for s in range(NS)]

        for s in range(NS):
            for d in range(2):
                op("pool", lambda e: e.memset(St[:].rearrange("p c h i -> p (c h i)"), 0.0), writes=[RSt])
                op("pool", lambda e: e.memset(Sbd[:].rearrange("p c h i -> p (c h i)"), 0.0), writes=[RSbd])
                order = range(NCH) if d == 0 else range(NCH - 1, -1, -1)
                def body(ci, s=s, d=d):
                    t0 = ci * C
                    P, RP = Ps.next()
                    lo, hi = max(t0 - 1, 0), min(t0 + C + 1, T)
                    if t0 == 0:
                        op("pool", lambda e, P=P: e.memset(P[:, :, 0:1], 0.0), writes=[RP])
                    if t0 + C == T:
                        op("pool", lambda e, P=P: e.memset(P[:, :, 129:130], 0.0), writes=[RP])
                    S.dma("sp", P[:, :, lo - (t0 - 1):hi - (t0 - 1)], pfv[s][:, :, lo:hi], reads=[self.dres(("pf", s))], writes=[RP])
                    tA, RtA = tAs.next()
                    tB, RtB = tBs.next()
                    ph, Rph = phs.next()
                    op("dve", lambda e, tA=tA, P=P: e.tensor_tensor(out=tA[:], in0=P[:, :, 0:128], in1=P[:, :, 1:129], op=ALU.subtract), reads=[RP], writes=[RtA])
                    op("pool", lambda e, tB=tB, P=P: e.tensor_tensor(out=tB[:], in0=P[:, :, 2:130], in1=P[:, :, 1:129], op=ALU.subtract), reads=[RP], writes=[RtB])
                    op("dve", lambda e, tA=tA: e.tensor_tensor(out=tA[:], in0=tA[:], in1=bc(mup[:].unsqueeze(2), [128, 15, 128]), op=ALU.mult), reads=[RtA, Rmup], writes=[RtA])
                    op("pool", lambda e, tB=tB: e.tensor_tensor(out=tB[:], in0=tB[:], in1=bc(mun[:].unsqueeze(2), [128, 15, 128]), op=ALU.mult), reads=[RtB, Rmun], writes=[RtB])
                    op("dve", lambda e, tA=tA, tB=tB: e.tensor_tensor(out=tA[:], in0=tA[:], in1=tB[:], op=ALU.add), reads=[RtA, RtB], writes=[RtA])
                    op("dve", lambda e, tA=tA, P=P, ph=ph: e.tensor_tensor(out=ph[:], in0=tA[:], in1=P[:, :, 1:129], op=ALU.add), reads=[RtA, RP], writes=[Rph])
                    rh, kh, vh = ph[:, 0:4, :], ph[:, 4:8, :], ph[:, 8:12, :]
                    yield 'a'
                    twd, Rtwd = twds.next()
                    adb, Radb = adbs.next()
                    sgd, Rsgd = sgds.next()
                    op("act", lambda e, twd=twd, ph=ph: e.activation(out=twd[:], in_=ph[:, 12, :], func=AF.Tanh), reads=[Rph], writes=[Rtwd])
                    op("dve", lambda e, adb=adb, ph=ph: e.tensor_copy(out=adb[:], in_=ph[:, 13, :]), reads=[Rph], writes=[Radb])
                    psw, Rpsw = pbanks.next()
                    psa, Rpsa = pbanks.next()

                    def mmlora(e, psw=psw, psa=psa, twd=twd, adb=adb, d=d):
                        for cc in range(4):
                            e.matmul(psw[:, cc * 128:(cc + 1) * 128], lhsT=wupz[:, d, cc * 128:(cc + 1) * 128], rhs=twd[:], start=True, stop=True)
                        for cc in range(4):
                            ins = e.matmul(psa[:, cc * 128:(cc + 1) * 128], lhsT=aupz[:, d, cc * 128:(cc + 1) * 128], rhs=adb[:], start=True, stop=True)
                        return ins
                    op("pe", mmlora, reads=[Rwupz, Raupz, Rtwd, Radb], writes=[Rpsw, Rpsa])
                    yield 'a'
                    lw, Rlw = lws.next()
                    av, Rav = avs.next()

                    def sigw(e, lw=lw, psw=psw, d=d):
                        for cc in range(4):
                            ins = e.activation(out=lw[:, cc, :], in_=psw[:, cc * 128:(cc + 1) * 128], func=AF.Sigmoid, bias=w0t[:, d, cc:cc + 1])
                        return ins
                    op("act", sigw, reads=[Rpsw, Rw0], writes=[Rlw])

                    def siga(e, av=av, psa=psa, d=d):
                        for cc in range(4):
                            ins = e.activation(out=av[:, cc, :], in_=psa[:, cc * 128:(cc + 1) * 128], func=AF.Sigmoid, bias=a0t[:, d, cc:cc + 1])
                        return ins
                    op("act", siga, reads=[Rpsa, Ra0], writes=[Rav])
                    yield 'a'
                    yield 'a'
                    pre, Rpre = pres.next()
                    cum, Rcum = cums.next()
                    cpr, Rcpr = cprs.next()

                    def scan(e, pre=pre, lw=lw):
                        for cc in range(4):
                            ins = e.tensor_tensor_scan(out=pre[:, cc, :], data0=onesf[:], data1=lw[:, cc, :], initial=0.0, op0=ALU.mult, op1=ALU.add)
                        return ins
                    op("dve", scan, reads=[Rlw, Ronesf], writes=[Rpre])
                    yield 'a'
                    if d == 0:
                        cum, Rcum = pre, Rpre
                    else:
                        op("dve", lambda e, cum=cum, pre=pre, lw=lw: e.scalar_tensor_tensor(out=cum[:], in0=pre[:], scalar=-1.0, in1=lw[:], op0=ALU.mult, op1=ALU.add),
                           reads=[Rpre, Rlw], writes=[Rcum])
                        op("dve", lambda e, cum=cum, pre=pre: e.tensor_tensor(out=cum[:], in0=cum[:], in1=bc(pre[:, :, 127:128], [128, 4, 128]), op=ALU.add),
                           reads=[Rcum, Rpre], writes=[Rcum])
                    op("pool", lambda e, cpr=cpr, cum=cum, lw=lw: e.tensor_tensor(out=cpr[:], in0=cum[:], in1=lw[:], op=ALU.subtract), reads=[Rcum, Rlw], writes=[Rcpr])
                    ec, Rec = ecs.next()
                    ecp, Recp = ecps.next()
                    ei, Rei = eis.next()
                    op("act", lambda e, ec=ec, cum=cum: e.activation(out=ec[:], in_=cum[:], func=AF.Exp, scale=-LD), reads=[Rcum], writes=[Rec])
                    op("act", lambda e, ecp=ecp, cpr=cpr: e.activation(out=ecp[:], in_=cpr[:], func=AF.Exp, scale=-LD), reads=[Rcpr], writes=[Recp])
                    op("act", lambda e, ei=ei, cum=cum: e.activation(out=ei[:], in_=cum[:], func=AF.Exp, scale=LD), reads=[Rcum], writes=[Rei])
                    yield 'a'
                    gC, RgC = gCs.next()
                    ce = 127 if d == 0 else 0
                    op("dve", lambda e, gC=gC, ec=ec, ce=ce: e.tensor_copy(out=gC[:], in_=ec[:, :, ce]), reads=[Rec], writes=[RgC])
                    yield 'a'
                    kraw, Rkraw = kraws.next()
                    sq, Rsq = sqs.next()
                    nrm, Rnrm = nrms.next()
                    kkv, Rkkv = kkvs.next()
                    tt, Rtt = tts.next()
                    kd, Rkd = kds.next()
                    bv, Rbv = bvs.next()
                    op("dve", lambda e, kraw=kraw, ph=ph: e.tensor_tensor(out=kraw[:], in0=ph[:, 4:8, :], in1=bc(kkc[:].unsqueeze(2), [128, 4, 128]), op=ALU.mult), reads=[Rph, Rkkc], writes=[Rkraw])
                    op("pool", lambda e, sq=sq, kraw=kraw: e.tensor_tensor(out=sq[:], in0=kraw[:], in1=kraw[:], op=ALU.mult), reads=[Rkraw], writes=[Rsq])
                    psn, Rpsn = pbanks.next()
                    op("pe", lambda e, psn=psn, sq=sq: e.matmul(psn[:], lhsT=blk1[:], rhs=sq[:].rearrange("p c t -> p (c t)"), start=True, stop=True), reads=[Rblk1, Rsq], writes=[Rpsn])
                    yield 'a'
                    op("act", lambda e, nrm=nrm, psn=psn: e.activation(out=nrm[:].rearrange("p c t -> p (c t)"), in_=psn[:], func=AF.Sqrt), reads=[Rpsn], writes=[Rnrm])
                    op("dve", lambda e, nrm=nrm: e.tensor_scalar_max(out=nrm[:], in0=nrm[:], scalar1=1e-12), reads=[Rnrm], writes=[Rnrm])
                    op("dve", lambda e, nrm=nrm: e.reciprocal(out=nrm[:], in_=nrm[:]), reads=[Rnrm], writes=[Rnrm])
                    op("dve", lambda e, kkv=kkv, kraw=kraw, nrm=nrm: e.tensor_tensor(out=kkv[:], in0=kraw[:], in1=nrm[:], op=ALU.mult), reads=[Rkraw, Rnrm], writes=[Rkkv])
                    yield 'a'
                    op("pool", lambda e, tt=tt, av=av: e.tensor_tensor(out=tt[:], in0=av[:], in1=bc(kac[:].unsqueeze(2), [128, 4, 128]), op=ALU.mult), reads=[Rav, Rkac], writes=[Rtt])
                    op("pool", lambda e, tt=tt: e.tensor_tensor(out=tt[:], in0=tt[:], in1=bc(omka[:].unsqueeze(2), [128, 4, 128]), op=ALU.add), reads=[Rtt, Romka], writes=[Rtt])
                    op("dve", lambda e, kd=kd, ph=ph, tt=tt: e.tensor_tensor(out=kd[:], in0=ph[:, 4:8, :], in1=tt[:], op=ALU.mult), reads=[Rph, Rtt], writes=[Rkd])
                    yield 'a'
                    op("pool", lambda e, bv=bv, kkv=kkv, av=av: e.tensor_tensor(out=bv[:], in0=kkv[:], in1=av[:], op=ALU.mult), reads=[Rkkv, Rav], writes=[Rbv])
                    yield 'a'
                    AT, RAT = ATs.next()
                    RT, RRT = RTs.next()
                    KT, RKT = KTs.next()
                    BT, RBT = BTs.next()
                    KhT, RKhT = KhTs.next()
                    BhT, RBhT = BhTs.next()
                    vT, RvT = vTs.next()
                    Kf, RKf = Kfs.next()
                    Bf, RBf = Bfs.next()
                    rk, Rrk = rks.next()
                    op("dve", lambda e, AT=AT, kkv=kkv, ecp=ecp: e.scalar_tensor_tensor(out=AT[:], in0=kkv[:], scalar=-1.0, in1=ecp[:], op0=ALU.mult, op1=ALU.mult), reads=[Rkkv, Recp], writes=[RAT])
                    op("dve", lambda e, RT=RT, ph=ph, ec=ec: e.tensor_tensor(out=RT[:], in0=ph[:, 0:4, :], in1=ec[:], op=ALU.mult), reads=[Rph, Rec], writes=[RRT])
                    op("dve", lambda e, Kf=Kf, kd=kd, ei=ei: e.tensor_tensor(out=Kf[:], in0=kd[:], in1=ei[:], op=ALU.mult), reads=[Rkd, Rei], writes=[RKf])
                    yield 'a'
                    op("pool", lambda e, Bf=Bf, bv=bv, ei=ei: e.tensor_tensor(out=Bf[:], in0=bv[:], in1=ei[:], op=ALU.mult), reads=[Rbv, Rei], writes=[RBf])
                    op("act", lambda e, KT=KT, Kf=Kf: e.copy(out=KT[:], in_=Kf[:]), reads=[RKf], writes=[RKT])
                    op("act", lambda e, BT=BT, Bf=Bf: e.copy(out=BT[:], in_=Bf[:]), reads=[RBf], writes=[RBT])
                    yield 'a'
                    op("dve", lambda e, KhT=KhT, Kf=Kf, gC=gC: e.tensor_tensor(out=KhT[:], in0=Kf[:], in1=bc(gC[:].unsqueeze(2), [128, 4, 128]), op=ALU.mult), reads=[RKf, RgC], writes=[RKhT])
                    op("pool", lambda e, BhT=BhT, Bf=Bf, gC=gC: e.tensor_tensor(out=BhT[:], in0=Bf[:], in1=bc(gC[:].unsqueeze(2), [128, 4, 128]), op=ALU.mult), reads=[RBf, RgC], writes=[RBhT])
                    op("act", lambda e, vT=vT, ph=ph: e.copy(out=vT[:], in_=ph[:, 8:12, :]), reads=[Rph], writes=[RvT])
                    op("pool", lambda e, rk=rk, ph=ph, kd=kd: e.tensor_tensor(out=rk[:], in0=ph[:, 0:4, :], in1=kd[:], op=ALU.mult), reads=[Rph, Rkd], writes=[Rrk])
                    op("pool", lambda e, rk=rk: e.tensor_tensor(out=rk[:], in0=rk[:], in1=bc(rkc[:].unsqueeze(2), [128, 4, 128]), op=ALU.mult), reads=[Rrk, Rrkc], writes=[Rrk])
                    yield 'a'
                    ARbd, RARbd = ARbds.next()
                    Bbd, RBbd = Bbds.next()
                    for h2 in range(2):
                        pl = slice(h2 * 64, (h2 + 1) * 64)
                        op("pool", lambda e, pl=pl, h2=h2, AT=AT: e.tensor_copy(out=ARbd[pl, :, h2, 0, :], in_=AT[pl, :, :]), reads=[RAT, RARbd], writes=[RARbd])
                        op("pool", lambda e, pl=pl, h2=h2, RT=RT: e.tensor_copy(out=ARbd[pl, :, h2, 1, :], in_=RT[pl, :, :]), reads=[RRT, RARbd], writes=[RARbd])
                        op("pool", lambda e, pl=pl, h2=h2, BT=BT: e.tensor_copy(out=Bbd[pl, :, h2, :], in_=BT[pl, :, :]), reads=[RBT, RBbd], writes=[RBbd])
                    yield 'a'
                    pt, Rpt = ptr.next()
                    VK, RVK = VKs.next()
                    Vtm, RVtm = VK[:, 0:512], RVK
                    Khat, RKhat = VK[:, 512:1024], RVK
                    Bhat, RBhat = Bhats.next()

                    def tr1(e, pt=pt, vT=vT, KhT=KhT):
                        for cc in range(4):
                            e.transpose(out=pt[:, cc * 128:(cc + 1) * 128], in_=vT[:, cc, :], identity=identb[:])
                        for cc in range(4):
                            ins = e.transpose(out=pt[:, 512 + cc * 128:512 + (cc + 1) * 128], in_=KhT[:, cc, :], identity=identb[:])
                        return ins
                    op("pe", tr1, reads=[RvT, RKhT, Ridb], writes=[Rpt])
                    yield 'a'
                    op("act", lambda e, VK=VK, pt=pt: e.copy(out=VK[:], in_=pt[:]), reads=[Rpt], writes=[RVK])
                    pt2, Rpt2 = ptr.next()

                    def tr2(e, pt2=pt2, BhT=BhT):
                        for cc in range(4):
                            ins = e.transpose(out=pt2[:, cc * 128:(cc + 1) * 128], in_=BhT[:, cc, :], identity=identb[:])
                        return ins
                    op("pe", tr2, reads=[RBhT, Ridb], writes=[Rpt2])
                    yield 'a'
                    op("act", lambda e, Bhat=Bhat, pt2=pt2: e.copy(out=Bhat[:], in_=pt2[:, 0:512]), reads=[Rpt2], writes=[RBhat])
                    psb, Rpsb = pbanks.next()
                    bonv, Rbonv = bons.next()

                    def mmbon(e, psb=psb, rk=rk):
                        for cc in range(4):
                            ins = e.transpose(out=psb[:, cc * 128:(cc + 1) * 128], in_=rk[:, cc, :], identity=identf[:])
                        return ins
                    op("pe", mmbon, reads=[Rrk, Ridf], writes=[Rpsb])
                    op("dve", lambda e, bonv=bonv, psb=psb: e.tensor_reduce(out=bonv[:], in_=psb[:].rearrange("p (h i) -> p h i", i=64), axis=AX.X, op=ALU.add),
                       reads=[Rpsb], writes=[Rbonv])
                    if d == 1:
                        op("act", lambda e, sgd=sgd, ph=ph: e.activation(out=sgd[:], in_=ph[:, 14, :], func=AF.Sigmoid), reads=[Rph], writes=[Rsgd])
                        psg, Rpsg = pbanks.next()
                        gtm, Rgtm = gtms.next()
                        op("pe", lambda e, psg=psg, sgd=sgd: e.matmul(psg[:], lhsT=sgd[:], rhs=gup[:], start=True, stop=True), reads=[Rsgd, Rgup], writes=[Rpsg])
                        op("act", lambda e, gtm=gtm, psg=psg: e.copy(out=gtm[:], in_=psg[:]), reads=[Rpsg], writes=[Rgtm])
                    yield 'B'
                    GK, RGK = GKs.next()
                    GB, RGB = GBs.next()
                    Lh, RLh = Lhs.next()
                    for cc in range(4):
                        p1, Rp1 = pbanks.next()
                        p2, Rp2 = pbanks.next()
                        p3, Rp3 = pbanks.next()

                        def mmg(e, cc=cc, p1=p1, p2=p2, p3=p3, KT=KT, BT=BT, AT=AT):
                            e.matmul(p1[:], lhsT=KT[:, cc, :], rhs=ARbd[:, cc].rearrange("p h x t -> p (h x t)"), start=True, stop=True)
                            e.matmul(p2[:], lhsT=BT[:, cc, :], rhs=ARbd[:, cc].rearrange("p h x t -> p (h x t)"), start=True, stop=True)
                            return e.matmul(p3[:, 0:256], lhsT=AT[:, cc, :], rhs=Bbd[:, cc].rearrange("p h t -> p (h t)"), start=True, stop=True)
                        op("pe", mmg, reads=[RKT, RBT, RAT, RARbd, RBbd], writes=[Rp1, Rp2, Rp3])
                        op("dve", lambda e, cc=cc, p1=p1, GK=GK, d=d: e.tensor_tensor(out=GK[:, cc].rearrange("p h x t -> p (h x t)"), in0=p1[:],
                                                                                 in1=mG[:, d].rearrange("p h x t -> p (h x t)"), op=ALU.mult), reads=[Rp1, RmG], writes=[RGK])
                        op("dve", lambda e, cc=cc, p2=p2, GB=GB, d=d: e.tensor_tensor(out=GB[:, cc].rearrange("p h x t -> p (h x t)"), in0=p2[:],
                                                                                 in1=mG[:, d].rearrange("p h x t -> p (h x t)"), op=ALU.mult), reads=[Rp2, RmG], writes=[RGB])
                        op("dve", lambda e, cc=cc, p3=p3, Lh=Lh, d=d: e.tensor_tensor(out=Lh[:, cc].rearrange("p h t -> p (h t)"), in0=p3[:, 0:256],
                                                                                 in1=mL[:, d].rearrange("p h t -> p (h t)"), op=ALU.mult), reads=[Rp3, RmL], writes=[RLh])
                        yield 'b'
                    L16, RL16 = L16s.next()
                    M16, RM16 = M16s.next()
                    for cc in range(4):
                        op("pool", lambda e, cc=cc, L16=L16, Lh=Lh: e.tensor_tensor(out=L16[:, cc].rearrange("p h t -> p (h t)"), in0=Lh[:, cc].rearrange("p h t -> p (h t)"),
                                                                              in1=bd16[:].rearrange("p h t -> p (h t)"), op=ALU.mult), reads=[RLh, Rbd16], writes=[RL16])
                        op("pool", lambda e, cc=cc, M16=M16, GB=GB: e.tensor_tensor(out=M16[:, cc], in0=GB[:, cc, :, 0, :], in1=bd16[:], op=ALU.mult), reads=[RGB, Rbd16], writes=[RM16])
                    cur = []
                    for h in range(8):
                        cur.append((L16[:, h // 2, h % 2, :], RL16, M16[:, h // 2, h % 2, :], RM16, identb[:], Ridb, identb[:], Ridb))
                    for lev in range(4):
                        for h in range(8):
                            Lp, RLp, Mp, RMp, Qp, RQp, Pp, RPp = cur[h]
                            pq, Rpq = pbanks.next()
                            lt, Rlt = lmq[h][lev % 2]

                            def mmi(e, pq=pq, Lp=Lp, Mp=Mp, Qp=Qp, Pp=Pp, lev=lev):
                                e.matmul(pq[:, 256:384], lhsT=identb[:], rhs=Qp, start=True, stop=False)
                                e.matmul(pq[:, 256:384], lhsT=Lp, rhs=Qp, start=False, stop=True)
                                e.matmul(pq[:, 384:512], lhsT=identb[:], rhs=Pp, start=True, stop=False)
                                ins = e.matmul(pq[:, 384:512], lhsT=Mp, rhs=Pp, start=False, stop=True)
                                if lev < 3:
                                    e.matmul(pq[:, 0:128], lhsT=Mp, rhs=Lp, start=True, stop=True)
                                    ins = e.matmul(pq[:, 128:256], lhsT=Lp, rhs=Mp, start=True, stop=True)
                                return ins
                            op("pe", mmi, reads=[RLp, RMp, RQp, RPp, Ridb], writes=[Rpq])
                            lo_ = 0 if lev < 3 else 256
                            eng = "act" if (h + lev) % 3 != 0 else "dve"
                            if eng == "dve":
                                op("dve", lambda e, lt=lt, pq=pq, lo_=lo_: e.tensor_copy(out=lt[:].rearrange("p a t -> p (a t)")[:, lo_:512], in_=pq[:, lo_:512]), reads=[Rpq], writes=[Rlt])
                            else:
                                op("act", lambda e, lt=lt, pq=pq, lo_=lo_: e.copy(out=lt[:].rearrange("p a t -> p (a t)")[:, lo_:512], in_=pq[:, lo_:512]), reads=[Rpq], writes=[Rlt])
                            cur[h] = (lt[:, 0, :], Rlt, lt[:, 1, :], Rlt, lt[:, 2, :], Rlt, lt[:, 3, :], Rlt)
                            if h % 2 == 1:
                                yield 'b'
                    tcur = [(cur[h][6], cur[h][7], cur[h][4], cur[h][5]) for h in range(8)]
                    for lv in range(3):
                        last_lv = (lv == 2)
                        for h in range(8):
                            Tk, RTk, TTk, RTTk = tcur[h]
                            Lm = Lh[:, h // 2, h % 2, :]
                            Mm = GB[:, h // 2, h % 2, 0, :]
                            pa, Rpa = pbanks.next()
                            zz, Rzz = zzt[h]

                            def mmz(e, pa=pa, Tk=Tk, TTk=TTk, Lm=Lm, Mm=Mm, last_lv=last_lv):
                                ins = e.matmul(pa[:, 128:256], lhsT=Lm, rhs=TTk, start=True, stop=True)
                                if not last_lv:
                                    ins = e.matmul(pa[:, 0:128], lhsT=Mm, rhs=Tk, start=True, stop=True)
                                return ins
                            op("pe", mmz, reads=[RLh, RGB, RTk, RTTk], writes=[Rpa])
                            z0 = 128 if last_lv else 0
                            op("dve", lambda e, zz=zz, pa=pa, lv=lv, z0=z0: e.tensor_tensor(out=zz[:].rearrange("p a t -> p (a t)")[:, z0:256], in0=pa[:, z0:256],
                                                                                          in1=offm[:, lv].rearrange("p a t -> p (a t)")[:, z0:256], op=ALU.mult),
                               reads=[Rpa, Roffm], writes=[Rzz])
                            pb_, Rpb_ = pbanks.next()
                            tn, Rtn = ttt[h][lv % 2]

                            def mmt(e, pb_=pb_, Tk=Tk, TTk=TTk, zz=zz, last_lv=last_lv):
                                e.matmul(pb_[:, 128:256], lhsT=identb[:], rhs=TTk, start=True, stop=False)
                                ins = e.matmul(pb_[:, 128:256], lhsT=Tk, rhs=zz[:, 1, :], start=False, stop=True)
                                if not last_lv:
                                    e.matmul(pb_[:, 0:128], lhsT=identb[:], rhs=Tk, start=True, stop=False)
                                    ins = e.matmul(pb_[:, 0:128], lhsT=TTk, rhs=zz[:, 0, :], start=False, stop=True)
                                return ins
                            op("pe", mmt, reads=[RTk, RTTk, Rzz, Ridb], writes=[Rpb_])
                            op("act", lambda e, tn=tn, pb_=pb_, z0=z0: e.copy(out=tn[:].rearrange("p a t -> p (a t)")[:, z0:256], in_=pb_[:, z0:256]), reads=[Rpb_], writes=[Rtn])
                            tcur[h] = (tn[:, 0, :], Rtn, tn[:, 1, :], Rtn)
                            if h % 2 == 1:
                                yield 'b'
                    cur = [(None, None, None, None, tcur[h][2], tcur[h][3]) for h in range(8)]
                    yield 'C'
                    pW, RpW = pbanks.next()
                    Wsb, RWsb = Wsbs.next()
                    Usb, RUsb = Usbs.next()
                    Ysb, RYsb = Ysbs.next()

                    def mmW(e, pW=pW, AT=AT, GK=GK, Vtm=Vtm):
                        for cc in range(4):
                            e.matmul(pW[:, cc * 128:(cc + 1) * 128], lhsT=AT[:, cc, :], rhs=Sbd[:, cc].rearrange("p h i -> p (h i)"), start=True, stop=False)
                            for h2 in range(2):
                                h = 2 * cc + h2
                                ins = e.matmul(pW[:, h * 64:(h + 1) * 64], lhsT=GK[:, cc, h2, 0, :], rhs=Vtm[:, h * 64:(h + 1) * 64], start=False, stop=True)
                        return ins
                    op("pe", mmW, reads=[RAT, RSbd, RGK, RVtm], writes=[RpW])
                    op("dve", lambda e, Wsb=Wsb, pW=pW: e.tensor_copy(out=Wsb[:], in_=pW[:]), reads=[RpW], writes=[RWsb])
                    pU, RpU = pbanks.next()

                    def mmU(e, pU=pU, Wsb=Wsb, cur=list(cur)):
                        for h in range(8):
                            ins = e.matmul(pU[:, h * 64:(h + 1) * 64], lhsT=cur[h][4], rhs=Wsb[:, h * 64:(h + 1) * 64], start=True, stop=True)
                        return ins
                    op("pe", mmU, reads=[RWsb] + [cur[h][5] for h in range(8)], writes=[RpU])
                    op("dve", lambda e, Usb=Usb, pU=pU: e.tensor_copy(out=Usb[:], in_=pU[:]), reads=[RpU], writes=[RUsb])
                    pY, RpY = pbanks.next()

                    def mmY(e, pY=pY, RT=RT, GK=GK, GB=GB, Vtm=Vtm, Usb=Usb):
                        for cc in range(4):
                            e.matmul(pY[:, cc * 128:(cc + 1) * 128], lhsT=RT[:, cc, :], rhs=Sbd[:, cc].rearrange("p h i -> p (h i)"), start=True, stop=False)
                            for h2 in range(2):
                                h = 2 * cc + h2
                                e.matmul(pY[:, h * 64:(h + 1) * 64], lhsT=GK[:, cc, h2, 1, :], rhs=Vtm[:, h * 64:(h + 1) * 64], start=False, stop=False)
                                ins = e.matmul(pY[:, h * 64:(h + 1) * 64], lhsT=GB[:, cc, h2, 1, :], rhs=Usb[:, h * 64:(h + 1) * 64], start=False, stop=True)
                        return ins
                    op("pe", mmY, reads=[RRT, RSbd, RGK, RGB, RVtm, RUsb], writes=[RpY])
                    op("act", lambda e, Ysb=Ysb, pY=pY: e.copy(out=Ysb[:], in_=pY[:]), reads=[RpY], writes=[RYsb])
                    pS, RpS = pbanks.next()

                    def mmS(e, pS=pS, Khat=Khat, Bhat=Bhat, Vtm=Vtm, Usb=Usb):
                        for cc in range(4):
                            cs = slice(cc * 128, (cc + 1) * 128)
                            e.matmul(pS[:, cs], lhsT=Khat[:, cs], rhs=Vtm[:, cs], start=True, stop=False)
                            ins = e.matmul(pS[:, cs], lhsT=Bhat[:, cs], rhs=Usb[:, cs], start=False, stop=True)
                        return ins
                    op("pe", mmS, reads=[RKhat, RBhat, RVtm, RUsb], writes=[RpS])
                    Sf = St[:].rearrange("p c h i -> p c (h i)")
                    op("dve", lambda e, pS=pS: e.tensor_tensor(out=stmp[:].rearrange("p c h i -> p (c h i)"), in0=pS[:], in1=bdm[:].rearrange("p c h i -> p (c h i)"), op=ALU.mult),
                       reads=[RpS, Rbdm], writes=[Rstmp])
                    op("dve", lambda e, gC=gC: e.tensor_tensor(out=Sf, in0=Sf, in1=bc(gC[:].unsqueeze(2), [128, 4, 128]), op=ALU.mult), reads=[RSt, RgC], writes=[RSt])
                    op("dve", lambda e: e.tensor_tensor(out=St[:].rearrange("p c h i -> p (c h i)"), in0=St[:].rearrange("p c h i -> p (c h i)"),
                                                        in1=stmp[:].rearrange("p c h i -> p (c h i)"), op=ALU.add), reads=[RSt, Rstmp], writes=[RSt])
                    op("act", lambda e: e.copy(out=Sbd[:].rearrange("p c h i -> p (c h i)"), in_=St[:].rearrange("p c h i -> p (c h i)")), reads=[RSt], writes=[RSbd])
                    if d == 0:
                        S.dma("pool", yf[s, t0:t0 + C, :], Ysb[:], reads=[RYsb], writes=[self.dres(("yf", s))])
                        S.dma("pool", bon[s, t0:t0 + C, :], bonv[:], reads=[Rbonv], writes=[self.dres(("bon", s))])
                    else:
                        yfl, Ryfl = yfls.next()
                        bon0, Rbon0 = bon0s.next()
                        S.dma("sp", yfl[:], yf[s, t0:t0 + C, :], reads=[self.dres(("yf", s))], writes=[Ryfl])
                        S.dma("sp", bon0[:], bon[s, t0:t0 + C, :], reads=[self.dres(("bon", s))], writes=[Rbon0])
                        sqv, Rsqv = sqvs.next()
                        yc, Ryc = ycs.next()
                        vv, Rvv = vvs.next()
                        st, Rst = stat.next()
                        yo, Ryo = yos.next()
                        y3 = lambda t_: t_[:].rearrange("p (h i) -> p h i", i=64)
                        b3 = lambda a_: bc(a_.unsqueeze(2), [128, 8, 64])
                        op("dve", lambda e, Ysb=Ysb, yfl=yfl: e.tensor_tensor(out=Ysb[:], in0=Ysb[:], in1=yfl[:], op=ALU.add), reads=[RYsb, Ryfl], writes=[RYsb])
                        op("pool", lambda e, sqv=sqv, Ysb=Ysb: e.tensor_tensor(out=sqv[:], in0=Ysb[:], in1=Ysb[:], op=ALU.mult), reads=[RYsb], writes=[Rsqv])
                        op("dve", lambda e, st=st, Ysb=Ysb: e.tensor_reduce(out=st[:, 0, :], in_=y3(Ysb), axis=AX.X, op=ALU.add), reads=[RYsb], writes=[Rst])
                        op("dve", lambda e, st=st, sqv=sqv: e.tensor_reduce(out=st[:, 1, :], in_=y3(sqv), axis=AX.X, op=ALU.add), reads=[Rsqv, Rst], writes=[Rst])
                        op("dve", lambda e, st=st: e.tensor_scalar(out=st[:, 2, :], in0=st[:, 0, :], scalar1=1.0 / 64, scalar2=None, op0=ALU.mult), reads=[Rst], writes=[Rst])
                        op("dve", lambda e, st=st: e.tensor_tensor(out=st[:, 3, :], in0=st[:, 2, :], in1=st[:, 2, :], op=ALU.mult), reads=[Rst], writes=[Rst])
                        op("dve", lambda e, st=st: e.scalar_tensor_tensor(out=st[:, 4, :], in0=st[:, 1, :], scalar=1.0 / 64, in1=st[:, 3, :], op0=ALU.mult, op1=ALU.subtract),
                           reads=[Rst], writes=[Rst])
                        op("act", lambda e, st=st: e.activation(out=st[:, 5, :], in_=st[:, 4, :], func=AF.Sqrt, bias=gne[:, 0:1]), reads=[Rst, Rgne], writes=[Rst])
                        op("dve", lambda e, st=st: e.reciprocal(out=st[:, 5, :], in_=st[:, 5, :]), reads=[Rst], writes=[Rst])
                        op("dve", lambda e, yc=yc, Ysb=Ysb, st=st: e.tensor_tensor(out=y3(yc), in0=y3(Ysb), in1=b3(st[:, 2, :]), op=ALU.subtract), reads=[RYsb, Rst], writes=[Ryc])
                        op("dve", lambda e, yc=yc, st=st: e.tensor_tensor(out=y3(yc), in0=y3(yc), in1=b3(st[:, 5, :]), op=ALU.mult), reads=[Ryc, Rst], writes=[Ryc])
                        op("pool", lambda e, yc=yc: e.tensor_tensor(out=yc[:], in0=yc[:], in1=gng[:], op=ALU.mult), reads=[Ryc, Rgng], writes=[Ryc])
                        op("pool", lambda e, yc=yc: e.tensor_tensor(out=yc[:], in0=yc[:], in1=gnb[:], op=ALU.add), reads=[Ryc, Rgnb], writes=[Ryc])
                        op("dve", lambda e, bonv=bonv, bon0=bon0: e.tensor_tensor(out=bonv[:], in0=bonv[:], in1=bon0[:], op=ALU.add), reads=[Rbonv, Rbon0], writes=[Rbonv])
                        op("dve", lambda e, vv=vv, Vtm=Vtm, bonv=bonv: e.tensor_tensor(out=y3(vv), in0=Vtm.rearrange("p (h i) -> p h i", i=64), in1=b3(bonv[:]), op=ALU.mult), reads=[RVtm, Rbonv], writes=[Rvv])
                        op("pool", lambda e, yc=yc, vv=vv: e.tensor_tensor(out=yc[:], in0=yc[:], in1=vv[:], op=ALU.add), reads=[Ryc, Rvv], writes=[Ryc])
                        op("dve", lambda e, yo=yo, yc=yc, gtm=gtm: e.tensor_tensor(out=yo[:], in0=yc[:], in1=gtm[:], op=ALU.mult), reads=[Ryc, Rgtm], writes=[Ryo])
                        S.dma("pool", yr[s, t0:t0 + C, :], yo[:], reads=[Ryo], writes=[self.dres(("yr", s))])

                def run_until(g, tags):
                    while True:
                        try:
                            t_ = next(g)
                        except StopIteration:
                            return None
                        if t_ in tags:
                            return t_
                order = list(order)
                gens = [body(ci) for ci in order]
                run_until(gens[0], ('B',))
                for i_ in range(len(gens)):
                    g_cur = gens[i_]
                    g_nxt = gens[i_ + 1] if i_ + 1 < len(gens) else None
                    cur_done = False
                    nxt_done = g_nxt is None
                    while not (cur_done and nxt_done):
                        if not cur_done:
                            if run_until(g_cur, ('b', 'C')) == 'C':
                                cur_done = True
                        if not nxt_done:
                            if run_until(g_nxt, ('a', 'B')) == 'B':
                                nxt_done = True
                    run_until(g_cur, ())

        self.end()

    def phase_p3(self, l):
        S, T, NS = self.S, self.T, self.NS
        self.begin()
        rows = T // 64
        nblk = T // 128
        nq, nk, nv, yn = (self.scr[k] for k in ("nq", "nk", "nv", "yn"))
        rp, Rrp = self.tile("rp", [120, 31], F32)
        S.dma("sp", rp[:], self.w["rpb"][l].rearrange("h a b -> (h a) b"), writes=[Rrp])
        _, _, idf, Ridf = self.make_ident()
        Rt, RRt = self.tile("Rt", [31, 8, 15], F32)
        Jm, RJ = self.tile("Jm", [31, 160], F32)
        tps = self.rot("tps", [128, 512], F32, 4, ps=True)
        tp0, Rtp0 = tps.next()
        S.op("pe", lambda e: e.transpose(out=tp0[0:31, 0:120], in_=rp[:, :], identity=idf[0:120, 0:120]), reads=[Rrp, Ridf], writes=[Rtp0])
        S.op("dve", lambda e: e.tensor_copy(out=Rt[:].rearrange("p h a -> p (h a)"), in_=tp0[0:31, 0:120]), reads=[Rtp0], writes=[RRt])

        def mkJ(e):
            e.memset(Jm[:], 0.0)
            return e.affine_select(out=Jm[:], in_=Jm[:], pattern=[[-1, 160]], compare_op=ALU.not_equal, fill=1.0, base=48, channel_multiplier=1)
        S.op("pool", mkJ, writes=[RJ])
        TE0, RTE0 = self.tile("TE0", [64, 8, 15, 64], BF16)
        for q0 in range(0, 64, 4):
            tp, Rtp = tps.next()

            def mmT(e, tp=tp, q0=q0):
                for qi in range(4):
                    qc = q0 + qi
                    ins = e.matmul(tp[0:64, qi * 120:(qi + 1) * 120], lhsT=Jm[:, 63 - qc:127 - qc], rhs=Rt[:].rearrange("p h a -> p (h a)"),
                                   start=True, stop=True)
                return ins
            S.op("pe", mmT, reads=[RJ, RRt], writes=[Rtp])
            S.op("act", lambda e, tp=tp, q0=q0: e.activation(out=TE0[:, :, :, q0:q0 + 4].rearrange("p h a q -> p (h a) q"),
                                                             in_=tp[0:64, 0:480].rearrange("p (q x) -> p x q", q=4), func=AF.Exp),
                 reads=[Rtp], writes=[RTE0])
        A, RA = self.tile("mA", [128, 64], F32)
        Q, RQ = self.tile("mQ", [128, 64], F32)
        Q2, RQ2 = self.tile("mQ2", [128, 64], F32)
        cm, Rcm = self.tile("cm", [128, 64], F32)

        def io(e):
            e.iota(A[0:64, :], pattern=[[-1, 64]], base=0, channel_multiplier=1, allow_small_or_imprecise_dtypes=True)
            e.iota(A[64:128, :], pattern=[[-1, 64]], base=0, channel_multiplier=1, allow_small_or_imprecise_dtypes=True)
            return e.iota(Q[:], pattern=[[1, 64]], base=0, channel_multiplier=0, allow_small_or_imprecise_dtypes=True)
        S.op("pool", io, writes=[RA, RQ])
        S.op("dve", lambda e: e.tensor_scalar(out=Q2[:], in0=Q[:], scalar1=8.0, scalar2=56.0, op0=ALU.max, op1=ALU.min), reads=[RQ], writes=[RQ2])
        S.op("dve", lambda e: e.tensor_tensor(out=Q[:], in0=Q[:], in1=Q2[:], op=ALU.subtract), reads=[RQ, RQ2], writes=[RQ])
        S.op("dve", lambda e: e.tensor_tensor(out=A[:], in0=A[:], in1=Q[:], op=ALU.add), reads=[RA, RQ], writes=[RA])
        S.op("dve", lambda e: e.tensor_single_scalar(out=Q[:], in_=A[:], scalar=-8.0, op=ALU.is_ge), reads=[RA], writes=[RQ])
        S.op("dve", lambda e: e.tensor_single_scalar(out=Q2[:], in_=A[:], scalar=7.0, op=ALU.is_le), reads=[RA], writes=[RQ2])
        S.op("dve", lambda e: e.tensor_tensor(out=cm[:], in0=Q[:], in1=Q2[:], op=ALU.mult), reads=[RQ, RQ2], writes=[Rcm])
        TE, RTE = self.tile("TE", [128, 8, 14, 64], BF16)
        S.op("dve", lambda e: e.tensor_tensor(out=TE0[:].rearrange("p h a q -> p (h a) q"), in0=TE0[:].rearrange("p h a q -> p (h a) q"),
                                              in1=cm[0:64, :].unsqueeze(1).broadcast_to([64, 120, 64]), op=ALU.mult),
             reads=[RTE0, Rcm], writes=[RTE0])
        S.op("pool", lambda e: e.tensor_copy(out=TE[0:64, :, :, :].rearrange("p h a q -> p h (a q)"),
                                             in_=TE0[:, :, 0:14, :].rearrange("p h a q -> p h (a q)")), reads=[RTE0], writes=[RTE])
        S.dma("sp", TE[64:128, :, :, :].rearrange("p h a q -> p h (a q)"), TE0[:, :, 1:15, :].rearrange("p h a q -> p h (a q)"),
              reads=[RTE0], writes=[RTE])
        qT, RqT = self.tile("nqT", [128, 4, T], BF16)
        kT, RkT = self.tile("nkT", [128, 4, T], BF16)
        Ve, RVe = self.tile("nVe", [128, nblk, 8, 65], BF16)
        Vo, RVo = self.tile("nVo", [128, nblk, 8, 65], BF16)
        Vs, RVs = self.tile("nVs", [128, nblk // 2, 512], BF16)
        S.op("pool", lambda e: e.memset(Ve[:], 1.0), writes=[RVe])
        S.op("pool", lambda e: e.memset(Vo[:], 1.0), writes=[RVo])
        pss = Rot([(t_[:].rearrange("p (h q) -> p h q", q=64), r_) for (t_, r_) in tps.items])
        pos_raw = self.rot("ops", [128, 512], F32, 4, ps=True)
        Ets = self.rot("Et", [128, 8, 64], BF16, 3)
        Es = self.rot("E", [128, 4, 8, 64], BF16, 2)
        recs = self.rot("rec", [64, 8], F32, 2)
        outs = self.rot("yno", [64, 8, 64], BF16, 3)
        for s in range(NS):
            S.dma("sp", qT[:], nq[s].rearrange("(c p) t -> p c t", p=128), reads=[self.dres(("nq", s))], writes=[RqT])
            S.dma("sp", kT[:], nk[s].rearrange("(c p) t -> p c t", p=128), reads=[self.dres(("nk", s))], writes=[RkT])
            hb = nblk // 2
            for (Vx, RVx, off, nb_tot) in ((Ve, RVe, 0, nblk), (Vo, RVo, 64, nblk - 1)):
                for b0 in range(0, nb_tot, hb):
                    nb_ = min(hb, nb_tot - b0)
                    S.dma("sp", Vs[:, 0:nb_, :], nv[s, off + b0 * 128:off + (b0 + nb_) * 128, :].rearrange("(b p) c -> p b c", p=128),
                          reads=[self.dres(("nv", s))], writes=[RVs])
                    S.op("pool", lambda e, Vx=Vx, b0=b0, nb_=nb_: e.tensor_copy(
                        out=Vx[:, b0:b0 + nb_, :, 0:64].rearrange("p b h n -> p (b h) n"),
                        in_=Vs[:, 0:nb_, :].rearrange("p b (h n) -> p (b h) n", n=64)), reads=[RVs], writes=[RVx])
            for i in range(rows):
                rs = min(max(i - 4, 0), rows - 8)
                E, RE = Es.next()
                for blk in range(4):
                    kr = rs + 2 * blk
                    tok0 = kr * 64
                    dib = kr - i + 7
                    psA, RpsA = pss.next()
                    psB, RpsB = pss.next()
                    Et, REt = Ets.next()

                    def mms(e, psA=psA, psB=psB, tok0=tok0, i=i):
                        for h in range(8):
                            pb = (h % 2) * 64
                            ps = psA if h % 2 == 0 else psB
                            ins = e.matmul(ps[:, h // 2, :], lhsT=kT[pb:pb + 64, h // 2, tok0:tok0 + 128], rhs=qT[pb:pb + 64, h // 2, i * 64:(i + 1) * 64],
                                           start=True, stop=True)
                        return ins
                    S.op("pe", mms, reads=[RkT, RqT], writes=[RpsA, RpsB])
                    S.op("act", lambda e, psA=psA, Et=Et: e.activation(out=Et[:, 0:4, :], in_=psA[:, 0:4, :], func=AF.Exp, scale=0.125), reads=[RpsA], writes=[REt])
                    S.op("act", lambda e, psB=psB, Et=Et: e.activation(out=Et[:, 4:8, :], in_=psB[:, 0:4, :], func=AF.Exp, scale=0.125), reads=[RpsB], writes=[REt])
                    S.op("dve", lambda e, Et=Et, E=E, blk=blk, dib=dib: e.tensor_tensor(
                        out=E[:, blk, :, :].rearrange("p (two c) q -> p two c q", two=2), in0=Et[:].rearrange("p (two c) q -> p two c q", two=2),
                        in1=TE[:].rearrange("p (c two) a q -> p two c a q", two=2)[:, :, :, dib, :], op=ALU.mult),
                         reads=[REt, RTE], writes=[RE])
                poA_, RpoA = pos_raw.next()
                poB_, RpoB = pos_raw.next()
                poA = poA_[0:64, 0:260].rearrange("p (h n) -> p h n", n=65)
                poB = poB_[0:64, 0:260].rearrange("p (h n) -> p h n", n=65)

                def mmo(e, E=E, rs=rs, poA=poA, poB=poB):
                    for h in range(8):
                        po = poA if h < 4 else poB
                        for blk in range(4):
                            kr = rs + 2 * blk
                            Vx = Ve if kr % 2 == 0 else Vo
                            ins = e.matmul(po[:, h % 4, :], lhsT=E[:, blk, (h % 2) * 4 + h // 2, :], rhs=Vx[:, kr // 2, h, :], start=(blk == 0), stop=(blk == 3))
                    return ins
                S.op("pe", mmo, reads=[RE, RVe, RVo], writes=[RpoA, RpoB])
                rec, Rrec = recs.next()
                o, Ro = outs.next()
                S.op("dve", lambda e, rec=rec, poA=poA: e.reciprocal(out=rec[:, 0:4], in_=poA[:, :, 64]), reads=[RpoA], writes=[Rrec])
                S.op("dve", lambda e, rec=rec, poB=poB: e.reciprocal(out=rec[:, 4:8], in_=poB[:, :, 64]), reads=[RpoB], writes=[Rrec])
                S.op("dve", lambda e, rec=rec, poA=poA, o=o: e.tensor_tensor(out=o[:, 0:4, :], in0=poA[:, :, 0:64],
                                                                            in1=rec[:, 0:4].unsqueeze(2).broadcast_to([64, 4, 64]), op=ALU.mult),
                     reads=[RpoA, Rrec], writes=[Ro])
                S.op("dve", lambda e, rec=rec, poB=poB, o=o: e.tensor_tensor(out=o[:, 4:8, :], in0=poB[:, :, 0:64],
                                                                            in1=rec[:, 4:8].unsqueeze(2).broadcast_to([64, 4, 64]), op=ALU.mult),
                     reads=[RpoB, Rrec], writes=[Ro])
                S.dma("pool", yn[s, i * 64:(i + 1) * 64, :], o[:].rearrange("p h n -> p (h n)"), reads=[Ro], writes=[self.dres(("yn", s))])
        self.end()

    def xupdate(self, xt, Rxt, nb, aT, RaT, nkc, W, RW, pss):
        S = self.S
        for b in range(nb):
            for half in range(2):
                ps, Rps = pss.next()

                def mm(e, ps=ps, b=b, half=half):
                    for kc in range(nkc):
                        ins = e.matmul(ps[:], lhsT=aT[:, kc, b * 128:(b + 1) * 128], rhs=W[:, kc, half * 512:(half + 1) * 512],
                                       start=(kc == 0), stop=(kc == nkc - 1))
                    return ins
                S.op("pe", mm, reads=[RaT, RW], writes=[Rps])
                S.op("dve", lambda e, ps=ps, b=b, half=half: e.tensor_tensor(
                    out=xt[:, b, half * 512:(half + 1) * 512], in0=xt[:, b, half * 512:(half + 1) * 512], in1=ps[:], op=ALU.add),
                    reads=[Rps, Rxt], writes=[Rxt])

    def phase_p4(self, l):
        S, T, NS = self.S, self.T, self.NS
        self.begin()
        TT = 512
        wbr, Rwbr = self.tile("wbr", [128, 8, D], BF16)
        wout, Rwout = self.tile("wout", [128, 8, D], BF16)
        stg = self.rot("wstg", [128, 2048], F32, 2)
        ident, Rid, _, _ = self.make_ident()
        self.load_weight(wbr, Rwbr, self.w["w_br_rwkv"][l], 512, D, stg, kc0=0)
        self.load_weight(wbr, Rwbr, self.w["w_br_nat"][l], 512, D, stg, kc0=4)
        self.load_weight(wout, Rwout, self.w["w_out"][l], D, D, stg)
        xts = self.rot("xt", [128, 4, D], F32, 2)
        yts = self.rot("yt", [128, 4, 1024], BF16, 2)
        yTs = self.rot("yT", [128, 8, TT], BF16, 2)
        sgts = self.rot("sgt", [128, 16, TT], BF16, 2)
        mTs = self.rot("mT", [128, 8, TT], BF16, 2)
        m1s = self.rot("m1", [128, TT], F32, 2)
        m2s = self.rot("m2", [128, TT], F32, 2)
        pTs = self.rot("pT", [128, 1024], BF16, 2, ps=True)
        pss = self.rot("ps", [128, 512], F32, 6, ps=True)
        xsrc = self.x_in if l == 0 else self.scr["xres"]
        xres, yr, yn, sg = (self.scr[k] for k in ("xres", "yr", "yn", "sg"))
        for s in range(NS):
            for t0 in range(0, T, TT):
                xt, Rxt = xts.next()
                yt, Ryt = yts.next()
                yT, RyT = yTs.next()
                sgt, Rsgt = sgts.next()
                mT, RmT = mTs.next()
                Rx = self.dres(("xres", s, t0))
                S.dma("sp", yt[:, :, 0:512], yr[s, t0:t0 + TT, :].rearrange("(b p) c -> p b c", p=128), reads=[self.dres(("yr", s))], writes=[Ryt])
                S.dma("sp", yt[:, :, 512:1024], yn[s, t0:t0 + TT, :].rearrange("(b p) c -> p b c", p=128), reads=[self.dres(("yn", s))], writes=[Ryt])
                S.dma("sp", sgt[:], sg[s, :, t0:t0 + TT].rearrange("(c p) t -> p c t", p=128), reads=[self.dres(("sg", s))], writes=[Rsgt])
                S.dma("sp", xt[:], xsrc[s, t0:t0 + TT, :].rearrange("(b p) d -> p b d", p=128), reads=([Rx] if l > 0 else []), writes=[Rxt])
                for b in range(4):
                    pT, RpT = pTs.next()

                    def tr(e, pT=pT, b=b, yt=yt):
                        for c in range(8):
                            ins = e.transpose(out=pT[:, c * 128:(c + 1) * 128], in_=yt[:, b, c * 128:(c + 1) * 128], identity=ident[:])
                        return ins
                    S.op("pe", tr, reads=[Ryt, Rid], writes=[RpT])
                    S.op("act", lambda e, pT=pT, b=b, yT=yT: e.copy(out=yT[:, :, b * 128:(b + 1) * 128], in_=pT[:].rearrange("p (k t) -> p k t", k=8)),
                         reads=[RpT], writes=[RyT])
                for oc in range(8):
                    ps1, Rps1 = pss.next()
                    ps2, Rps2 = pss.next()
                    m1, Rm1 = m1s.next()
                    m2, Rm2 = m2s.next()

                    def mm(e, ps1=ps1, ps2=ps2, oc=oc, yT=yT):
                        for kc in range(4):
                            e.matmul(ps1[:], lhsT=wbr[:, kc, oc * 128:(oc + 1) * 128], rhs=yT[:, kc, :], start=(kc == 0), stop=(kc == 3))
                        for kc in range(4):
                            ins = e.matmul(ps2[:], lhsT=wbr[:, 4 + kc, oc * 128:(oc + 1) * 128], rhs=yT[:, 4 + kc, :], start=(kc == 0), stop=(kc == 3))
                        return ins
                    S.op("pe", mm, reads=[Rwbr, RyT], writes=[Rps1, Rps2])
                    S.op("dve", lambda e, m1=m1, ps1=ps1, oc=oc, sgt=sgt: e.tensor_tensor(out=m1[:], in0=ps1[:], in1=sgt[:, oc, :], op=ALU.mult),
                         reads=[Rps1, Rsgt], writes=[Rm1])
                    S.op("dve", lambda e, m2=m2, ps2=ps2, oc=oc, sgt=sgt: e.tensor_tensor(out=m2[:], in0=ps2[:], in1=sgt[:, 8 + oc, :], op=ALU.mult),
                         reads=[Rps2, Rsgt], writes=[Rm2])
                    S.op("pool", lambda e, m1=m1, m2=m2, oc=oc, mT=mT: e.tensor_tensor(out=mT[:, oc, :], in0=m1[:], in1=m2[:], op=ALU.add),
                         reads=[Rm1, Rm2], writes=[RmT])
                self.xupdate(xt, Rxt, 4, mT, RmT, 8, wout, Rwout, pss)
                S.dma("pool", xres[s, t0:t0 + TT, :].rearrange("(b p) d -> p b d", p=128), xt[:], reads=[Rxt], writes=[Rx])
        self.end()

    def phase_p5(self, l):
        S, T, NS = self.S, self.T, self.NS
        self.begin()
        TT = 512
        wq, Rwq = self.tile("wq", [128, 8, D], BF16)
        wo, Rwo = self.tile("wo", [128, 8, D], BF16)
        wkv, Rwkv = self.tile("wkv", [128, 8, 2 * D], BF16)
        stg = self.rot("wstg", [128, 2048], F32, 2)
        ident, Rid, _, _ = self.make_ident()
        gb, Rgb = self.tile("gb", [128, D], F32)
        gm, Rgm = self.tile("gm", [128, D], F32)
        ones, Rones = self.tile("ones", [128, 128], BF16)
        S.op("pool", lambda e: e.memset(ones[:], 1.0), writes=[Rones])
        self.load_bcast(gb, Rgb, self.w["norm_x"][l], D)
        self.load_bcast(gm, Rgm, self.w["norm_mem"][l], D)
        self.load_weight(wkv, Rwkv, self.w["w_xkv"][l], D, 2 * D, stg)
        self.load_weight(wq, Rwq, self.w["w_xq"][l], D, D, stg)
        self.load_weight(wo, Rwo, self.w["w_xo"][l], D, D, stg)
        tmp = self.norm_tmp()
        xts = self.rot("xt", [128, 4, D], F32, 2)
        hTs = self.rot("hT", [128, 8, TT], BF16, 2)
        qTs = self.rot("qT", [128, 8, TT], BF16, 1)
        oTs = self.rot("oT", [128, 8, TT], BF16, 2)
        Es = self.rot("E", [128, 2, TT], BF16, 2)
        rdens = self.rot("rden", [128, TT], F32, 2)
        mt, Rmt = self.tile("memt", [128, 2, D], F32)
        mnT, RmnT = self.tile("memnT", [128, 8, NMEM], BF16)
        kT, RkT = self.tile("kT", [128, 8, NMEM], BF16)
        Vm, RVm = self.tile("Vm", [128, 2, D], BF16)
        pss = self.rot("ps", [128, 512], F32, 6, ps=True)
        xres = self.scr["xres"]
        for s in range(NS):
            S.dma("sp", mt[:], self.mem_in[s].rearrange("(b p) d -> p b d", p=128), writes=[Rmt])
            for b in range(2):
                self.norm_to_hT(mt, Rmt, b, gm, Rgm, mnT, RmnT, b * 128, ident, Rid, tmp)
            for cc in range(8):
                ps, Rps = pss.next()

                def mmk(e, ps=ps, cc=cc):
                    for kc in range(8):
                        ins = e.matmul(ps[:, 0:NMEM], lhsT=wkv[:, kc, cc * 128:(cc + 1) * 128], rhs=mnT[:, kc, :], start=(kc == 0), stop=(kc == 7))
                    return ins
                S.op("pe", mmk, reads=[Rwkv, RmnT], writes=[Rps])
                S.op("dve", lambda e, ps=ps, cc=cc: e.tensor_copy(out=kT[:, cc, :], in_=ps[:, 0:NMEM]), reads=[Rps], writes=[RkT])
            for mb in range(2):
                for half in range(2):
                    ps, Rps = pss.next()

                    def mmv(e, ps=ps, mb=mb, half=half):
                        for kc in range(8):
                            ins = e.matmul(ps[:], lhsT=mnT[:, kc, mb * 128:(mb + 1) * 128], rhs=wkv[:, kc, D + half * 512:D + (half + 1) * 512],
                                           start=(kc == 0), stop=(kc == 7))
                        return ins
                    S.op("pe", mmv, reads=[Rwkv, RmnT], writes=[Rps])
                    S.op("dve", lambda e, ps=ps, mb=mb, half=half: e.tensor_copy(out=Vm[:, mb, half * 512:(half + 1) * 512], in_=ps[:]),
                         reads=[Rps], writes=[RVm])
            for t0 in range(0, T, TT):
                xt, Rxt = xts.next()
                hT, RhT = hTs.next()
                qT, RqT = qTs.next()
                oT, RoT = oTs.next()
                Rx = self.dres(("xres", s, t0))
                S.dma("sp", xt[:], xres[s, t0:t0 + TT, :].rearrange("(b p) d -> p b d", p=128), reads=[Rx], writes=[Rxt])
                for b in range(4):
                    self.norm_to_hT(xt, Rxt, b, gb, Rgb, hT, RhT, b * 128, ident, Rid, tmp)
                for cc in range(8):
                    ps, Rps = pss.next()

                    def mmq(e, ps=ps, cc=cc, hT=hT):
                        for kc in range(8):
                            ins = e.matmul(ps[:], lhsT=wq[:, kc, cc * 128:(cc + 1) * 128], rhs=hT[:, kc, :], start=(kc == 0), stop=(kc == 7))
                        return ins
                    S.op("pe", mmq, reads=[Rwq, RhT], writes=[Rps])
                    S.op("dve", lambda e, ps=ps, cc=cc, qT=qT: e.tensor_copy(out=qT[:, cc, :], in_=ps[:]), reads=[Rps], writes=[RqT])
                for hd in range(4):
                    E, RE = Es.next()
                    rden, Rrden = rdens.next()
                    for mb in range(2):
                        ps, Rps = pss.next()

                        def mms(e, ps=ps, mb=mb, hd=hd, qT=qT):
                            for j in range(2):
                                ins = e.matmul(ps[:], lhsT=kT[:, 2 * hd + j, mb * 128:(mb + 1) * 128], rhs=qT[:, 2 * hd + j, :], start=(j == 0), stop=(j == 1))
                            return ins
                        S.op("pe", mms, reads=[RkT, RqT], writes=[Rps])
                        S.op("act", lambda e, ps=ps, mb=mb, E=E: e.activation(out=E[:, mb, :], in_=ps[:], func=AF.Exp, scale=1.0 / 16.0),
                             reads=[Rps], writes=[RE])
                    ps, Rps = pss.next()

                    def mmd(e, ps=ps, E=E):
                        for mb in range(2):
                            ins = e.matmul(ps[:], lhsT=ones[:], rhs=E[:, mb, :], start=(mb == 0), stop=(mb == 1))
                        return ins
                    S.op("pe", mmd, reads=[Rones, RE], writes=[Rps])
                    S.op("dve", lambda e, ps=ps, rden=rden: e.reciprocal(out=rden[:], in_=ps[:]), reads=[Rps], writes=[Rrden])
                    for j in range(2):
                        ps, Rps = pss.next()

                        def mmo(e, ps=ps, hd=hd, j=j, E=E):
                            for mb in range(2):
                                c0 = hd * 256 + j * 128
                                ins = e.matmul(ps[:], lhsT=Vm[:, mb, c0:c0 + 128], rhs=E[:, mb, :], start=(mb == 0), stop=(mb == 1))
                            return ins
                        S.op("pe", mmo, reads=[RVm, RE], writes=[Rps])
                        S.op("dve", lambda e, ps=ps, hd=hd, j=j, oT=oT, rden=rden: e.tensor_tensor(out=oT[:, 2 * hd + j, :], in0=ps[:], in1=rden[:], op=ALU.mult),
                             reads=[Rps, Rrden], writes=[RoT])
                self.xupdate(xt, Rxt, 4, oT, RoT, 8, wo, Rwo, pss)
                S.dma("pool", xres[s, t0:t0 + TT, :].rearrange("(b p) d -> p b d", p=128), xt[:], reads=[Rxt], writes=[Rx])
        self.end()

    def phase_p6(self, l):
        S, T, NS = self.S, self.T, self.NS
        self.begin()
        TT = 256
        NB = TT // 128
        last = (l == self.DEPTH - 1)
        w1, Rw1 = self.tile("w1", [128, 8, DFF], BF16)
        w2, Rw2 = self.tile("w2", [128, 32, D], BF16)
        stg = self.rot("wstg", [128, 2048], F32, 2)
        ident, Rid, _, _ = self.make_ident()
        gb, Rgb = self.tile("gb", [128, D], F32)
        self.load_bcast(gb, Rgb, self.w["norm_ff"][l], D)
        if last:
            gf, Rgf = self.tile("gf", [128, D], F32)
            self.load_bcast(gf, Rgf, self.w["norm_final"], D)
        self.load_weight(w1, Rw1, self.w["w_ff1"][l], D, DFF, stg)
        self.load_weight(w2, Rw2, self.w["w_ff2"][l], DFF, D, stg)
        tmp = self.norm_tmp()
        xts = self.rot("xt", [128, NB, D], F32, 2)
        hTs = self.rot("hT", [128, 8, TT], BF16, 2)
        uTs = self.rot("uT", [128, 32, TT], BF16, 1)
        rls = self.rot("rl", [128, TT], BF16, 3)
        pss = self.rot("ps", [128, 512], F32, 6, ps=True)
        xres = self.scr["xres"]
        for s in range(NS):
            for t0 in range(0, T, TT):
                xt, Rxt = xts.next()
                hT, RhT = hTs.next()
                uT, RuT = uTs.next()
                Rx = self.dres(("xres", s, t0 // 512 * 512))
                S.dma("sp", xt[:], xres[s, t0:t0 + TT, :].rearrange("(b p) d -> p b d", p=128), reads=[Rx], writes=[Rxt])
                for b in range(NB):
                    self.norm_to_hT(xt, Rxt, b, gb, Rgb, hT, RhT, b * 128, ident, Rid, tmp)
                for fc in range(32):
                    ps, Rps = pss.next()
                    rl, Rrl = rls.next()

                    def mm1(e, ps=ps, fc=fc, hT=hT):
                        for kc in range(8):
                            ins = e.matmul(ps[:, 0:TT], lhsT=w1[:, kc, fc * 128:(fc + 1) * 128], rhs=hT[:, kc, :], start=(kc == 0), stop=(kc == 7))
                        return ins
                    S.op("pe", mm1, reads=[Rw1, RhT], writes=[Rps])
                    S.op("act", lambda e, ps=ps, rl=rl: e.activation(out=rl[:], in_=ps[:, 0:TT], func=AF.Relu), reads=[Rps], writes=[Rrl])
                    S.op("pool", lambda e, rl=rl, fc=fc, uT=uT: e.tensor_tensor(out=uT[:, fc, :], in0=rl[:], in1=rl[:], op=ALU.mult),
                         reads=[Rrl], writes=[RuT])
                self.xupdate(xt, Rxt, NB, uT, RuT, 32, w2, Rw2, pss)
                if not last:
                    S.dma("pool", xres[s, t0:t0 + TT, :].rearrange("(b p) d -> p b d", p=128), xt[:], reads=[Rxt], writes=[Rx])
                else:
                    for b in range(NB):
                        junk, Rjunk = tmp["junk"].next()
                        ss, Rss = tmp["ss"].next()
                        S.op("act", lambda e, junk=junk, ss=ss, b=b, xt=xt: e.activation(out=junk[:], in_=xt[:, b, :], func=AF.Square, accum_out=ss[:, 0:1]),
                             reads=[Rxt], writes=[Rjunk, Rss])
                        S.op("act", lambda e, ss=ss: e.activation(out=ss[:, 1:2], in_=ss[:, 0:1], func=AF.Sqrt, scale=1.0 / D, bias=self.eps_t[:, 0:1]),
                             reads=[Rss, self.Reps], writes=[Rss])
                        S.op("dve", lambda e, ss=ss: e.reciprocal(out=ss[:, 2:3], in_=ss[:, 1:2]), reads=[Rss], writes=[Rss])
                        S.op("dve", lambda e, ss=ss, b=b, xt=xt: e.scalar_tensor_tensor(out=xt[:, b, :], in0=xt[:, b, :], scalar=ss[:, 2:3], in1=gf[:],
                                                                                         op0=ALU.mult, op1=ALU.mult),
                             reads=[Rxt, Rss, Rgf], writes=[Rxt])
                    Ry = self.dres(("y", s))
                    S.dma("pool", self.y_out[s, t0:t0 + TT, :].rearrange("(b p) d -> p b d", p=128), xt[:], reads=[Rxt], writes=[Ry])
        self.end()


_CACHE = {}


def _get_nc(T, NS, DEPTH, dbg=(), stop=None):
    key = (T, NS, DEPTH, tuple(dbg), stop)
    if key not in _CACHE:
        _CACHE[key] = Builder(T, NS, DEPTH, dbg, stop).build()
    return _CACHE[key]


def kernel(**inputs):
    NCORES, NS, T, DEPTH = 8, 2, 4096, 2
    xs = np.concatenate([np.asarray(inputs["x_prompt"], np.float32), np.asarray(inputs["x_sample"], np.float32)], 0)
    ms = np.concatenate([np.asarray(inputs["mem_prompt"], np.float32), np.asarray(inputs["mem_sample"], np.float32)], 0)
    nseq = xs.shape[0]
    slots = [[c, 8 + c if 8 + c < nseq else c] for c in range(NCORES)]
    wnames = [n for n, _ in WEIGHT_SPECS] + ["norm_final"]
    wmap = {n: np.ascontiguousarray(np.asarray(inputs[n], np.float32)) for n in wnames}
    nc = _get_nc(T, NS, DEPTH)
    in_maps = []
    for c in range(NCORES):
        m = dict(wmap)
        m["x"] = np.ascontiguousarray(xs[slots[c]])
        m["mem"] = np.ascontiguousarray(ms[slots[c]])
        in_maps.append(m)
    res = run_bass_kernel_spmd(nc, in_maps, core_ids=list(range(NCORES)))
    y = np.zeros_like(xs)
    for c in range(NCORES):
        yc = res.results[c]["y"]
        y[slots[c][0]] = yc[0]
        if slots[c][1] != slots[c][0]:
            y[slots[c][1]] = yc[1]
    nb = np.asarray(inputs["x_prompt"]).shape[0]
    return (y[:nb], y[nb:])
```

```python
import numpy as np
from contextlib import ExitStack
import concourse.bass as bass
import concourse.mybir as mybir
from concourse.bass_utils import run_bass_kernel_spmd

F32 = mybir.dt.float32
BF16 = mybir.dt.bfloat16
AF = mybir.ActivationFunctionType
ALU = mybir.AluOpType
AX = mybir.AxisListType

D = 1024
DIN = 5504
RW = 1920
NQ0 = 1920
NV0 = 2944
G0 = 3456
NMEM = 256
DFF = 4096
EPS = 1e-6
GN_EPS = 1e-5 * 64
CH = 128

EPOCH = 12000
ENGS = ("pe", "dve", "act", "pool", "sp")


class Res:
    __slots__ = ("name", "last_w", "readers", "dsem", "w_is_dma")

    def __init__(self, name=""):
        self.name = name
        self.last_w = None
        self.readers = []
        self.dsem = None
        self.w_is_dma = False


class Sched:
    def __init__(self, nc, stack):
        self.nc = nc
        self.stack = stack
        self.ops = {e: [] for e in ENGS}
        self.cnt = {e: 0 for e in ENGS}
        self.esems = {e: [] for e in ENGS}
        self.known = {e: {} for e in ENGS}
        self.sems = {}
        self.nsem = 0
        self.same_engine_raw = True
        self.dma_latest = {}
        self.free_dsems = []

    def new_sem(self, tag):
        h = self.stack.enter_context(self.nc.semaphore(f"{tag}_{self.nsem}"))
        sid = self.nsem
        self.nsem += 1
        self.sems[sid] = h
        return sid

    def _eng_point(self, eng):
        n = self.cnt[eng]
        self.cnt[eng] = n + 1
        ep, v = divmod(n, EPOCH)
        while len(self.esems[eng]) <= ep:
            self.esems[eng].append(self.new_sem(f"e{eng}"))
        return (self.esems[eng][ep], v + 1)

    def _dma_point(self, res):
        d = res.dsem
        if d is None or d[1] + 16 > EPOCH:
            if self.free_dsems and d is None:
                d = self.free_dsems.pop()
            else:
                d = [self.new_sem("d"), 0]
            res.dsem = d
        d[1] += 16
        self.dma_latest[d[0]] = d[1]
        return (d[0], d[1])

    def recycle(self, resources):
        seen = set()
        for r in resources:
            d = r.dsem
            if d is not None and id(d) not in seen and d[1] + 64 < EPOCH:
                seen.add(id(d))
                self.free_dsems.append(d)
            r.dsem = None

    def _need(self, eng, waits, pt):
        sid, val = pt
        if self.known[eng].get(sid, 0) >= val:
            return
        if waits.get(sid, 0) < val:
            waits[sid] = val

    def _deps(self, eng, reads, writes, is_dma=False):
        waits = {}
        for r in reads:
            if r.last_w is not None:
                w_eng = r.last_w[2]
                if w_eng != eng or is_dma or (self.same_engine_raw and eng != "pe"):
                    self._need(eng, waits, r.last_w[:2])
        for w in writes:
            if w.last_w is not None:
                w_eng = w.last_w[2]
                same_dma_group = is_dma and w.w_is_dma and not w.readers
                if (w_eng != eng or is_dma) and not same_dma_group:
                    self._need(eng, waits, w.last_w[:2])
            for (sid, val, r_eng) in w.readers:
                if r_eng != eng or is_dma:
                    self._need(eng, waits, (sid, val))
        for sid, val in waits.items():
            self.known[eng][sid] = val
        return list(waits.items())

    def op(self, eng, fn, reads=(), writes=()):
        waits = self._deps(eng, reads, writes)
        pt = self._eng_point(eng)
        self.ops[eng].append((waits, fn, pt[0], 1))
        for r in reads:
            r.readers.append((pt[0], pt[1], eng))
        for w in writes:
            w.last_w = (pt[0], pt[1], eng)
            w.readers = []
            w.w_is_dma = False
        return pt

    def dma(self, queue, out, in_, reads=(), writes=(), **kw):
        waits = self._deps(queue, reads, writes, is_dma=True)
        pt = self._dma_point(writes[0])

        def fn(e, out=out, in_=in_, kw=kw):
            return e.dma_start(out=out, in_=in_, **kw)
        self.ops[queue].append((waits, fn, pt[0], 16))
        for r in reads:
            r.readers.append((pt[0], pt[1], "dma"))
        for w in writes:
            w.last_w = (pt[0], pt[1], "dma")
            w.readers = []
            w.w_is_dma = True
            if w is not writes[0]:
                w.dsem = writes[0].dsem
        return pt

    def final_wait(self, eng, resources):
        waits = {}
        for r in resources:
            if r.last_w is not None:
                sid, val = r.last_w[:2]
                waits[sid] = max(waits.get(sid, 0), val)
        self.ops[eng].append((list(waits.items()), None, None, 0))

    def barrier(self):
        pts = {}
        for e in ENGS:
            n = self.cnt[e]
            if n > 0:
                ep, v = divmod(n - 1, EPOCH)
                pts[self.esems[e][ep]] = v + 1
        for sid, val in self.dma_latest.items():
            pts[sid] = max(pts.get(sid, 0), val)
        for e in ENGS:
            waits = []
            for sid, val in pts.items():
                if self.known[e].get(sid, 0) < val:
                    waits.append((sid, val))
                    self.known[e][sid] = val
            self.ops[e].append((waits, None, None, 0))

    def emit(self):
        nc = self.nc
        sems = self.sems

        def run(engname):
            def body(e):
                for (waits, fn, sid_, inc) in self.ops[engname]:
                    for sid, val in waits:
                        e.wait_ge(sems[sid], val)
                    if fn is None:
                        continue
                    ins = fn(e)
                    ins.then_inc(sems[sid_], inc)
            return body

        with nc.Block() as block:
            block.tensor(run("pe"))
            block.vector(run("dve"))
            block.scalar(run("act"))
            block.gpsimd(run("pool"))
            block.sync(run("sp"))
        self.ops = {e: [] for e in ENGS}


class Rot:
    def __init__(self, items):
        self.items = items
        self.i = 0

    def next(self):
        it = self.items[self.i % len(self.items)]
        self.i += 1
        return it


WEIGHT_SPECS = [
    ("norm_mix", (D,)), ("w_in", (D, DIN)), ("mu_prev", (RW,)), ("mu_next", (RW,)),
    ("w0", (2, 512)), ("w_up", (2, 64, 512)), ("a0", (2, 512)), ("a_up", (2, 64, 512)),
    ("g_up", (128, 512)), ("k_k", (512,)), ("k_a", (512,)), ("r_k", (8, 64)),
    ("gn_g", (512,)), ("gn_b", (512,)), ("rpb", (8, 15, 31)),
    ("w_br_rwkv", (512, D)), ("w_br_nat", (512, D)), ("w_out", (D, D)),
    ("norm_x", (D,)), ("norm_mem", (D,)), ("w_xq", (D, D)), ("w_xkv", (D, 2 * D)), ("w_xo", (D, D)),
    ("norm_ff", (D,)), ("w_ff1", (D, DFF)), ("w_ff2", (DFF, D)),
]


class Builder:
    def __init__(self, T, NS, DEPTH, dbg=(), stop=None, ext_in=(), phases=None):
        self.T, self.NS, self.DEPTH = T, NS, DEPTH
        self.ext_in = set(ext_in)
        self.phases = phases or ("p1", "p3", "p2", "p4", "p5", "p6")
        self.dbg = set(dbg)
        self.stop = stop
        self.nc = bass.Bass("TRN2", target_bir_lowering=False)
        nc = self.nc
        self.x_in = nc.dram_tensor("x", [NS, T, D], F32, kind="ExternalInput").ap()
        self.mem_in = nc.dram_tensor("mem", [NS, NMEM, D], F32, kind="ExternalInput").ap()
        self.w = {}
        for name, shp in WEIGHT_SPECS:
            self.w[name] = nc.dram_tensor(name, [DEPTH] + list(shp), F32, kind="ExternalInput").ap()
        self.w["norm_final"] = nc.dram_tensor("norm_final", [D], F32, kind="ExternalInput").ap()
        self.y_out = nc.dram_tensor("y", [NS, T, D], F32, kind="ExternalOutput").ap()
        self.scr = {}
        self.scr_res = {}

    def scratch(self, name, shape, dt):
        kind = "ExternalOutput" if name in self.dbg else ("ExternalInput" if name in self.ext_in else "Internal")
        t = self.nc.dram_tensor(name, shape, dt, kind=kind).ap()
        self.scr[name] = t
        return t

    def dres(self, key):
        r = self.scr_res.get(key)
        if r is None:
            r = Res(str(key))
            self.scr_res[key] = r
        return r

    def build(self):
        nc = self.nc
        T, NS = self.T, self.NS
        self.scratch("xres", [NS, T, D], F32)
        self.scratch("pf", [NS, RW, T], F32)
        self.scratch("nq", [NS, 512, T], BF16)
        self.scratch("nk", [NS, 512, T], BF16)
        self.scratch("nv", [NS, T, 512], BF16)
        self.scratch("sg", [NS, 2048, T], BF16)
        self.scratch("yr", [NS, T, 512], BF16)
        self.scratch("yn", [NS, T, 512], BF16)
        self.scratch("yf", [NS, T, 512], F32)
        self.scratch("bon", [NS, T, 8], F32)
        with ExitStack() as st:
            self.S = Sched(nc, st)
            done = False
            for l in range(self.DEPTH):
                for ph in self.phases:
                    getattr(self, "phase_" + ph)(l)
                    if self.stop == (l, ph):
                        done = True
                        break
                if done:
                    break
            with ExitStack() as ph:
                self.S.final_wait("pool", list(self.scr_res.values()))
                self.S.emit()
        return nc

    def begin(self):
        self.S.barrier()
        self.phc = getattr(self, "phc", 0) + 1
        self.ph = ExitStack()
        self.ph_res = []
        return self.ph

    def end(self):
        self.S.emit()
        self.S.recycle(self.ph_res)
        self.ph.close()

    def tile(self, name, shape, dt):
        t = self.ph.enter_context(self.nc.sbuf_tensor(f"{name}_{self.phc}", shape, dt))
        r = Res(name)
        self.ph_res.append(r)
        return t, r

    def psum(self, name, shape, dt):
        t = self.ph.enter_context(self.nc.psum_tensor(f"{name}_{self.phc}", shape, dt))
        r = Res(name)
        self.ph_res.append(r)
        return t, r

    def rot(self, name, shape, dt, n, ps=False):
        f = self.psum if ps else self.tile
        return Rot([f(f"{name}{i}", shape, dt) for i in range(n)])

    def make_ident(self):
        S = self.S
        idf, Ridf = self.tile("identf", [128, 128], F32)
        idb, Ridb = self.tile("identb", [128, 128], BF16)

        S.op("pool", lambda e: e.memset(idf[:], 0.0), writes=[Ridf])
        S.op("pool", lambda e: e.affine_select(out=idf[:], in_=idf[:], pattern=[[-1, 128]], compare_op=ALU.not_equal,
                                               fill=1.0, base=0, channel_multiplier=1), reads=[Ridf], writes=[Ridf])
        S.op("dve", lambda e: e.tensor_copy(out=idb[:], in_=idf[:]), reads=[Ridf], writes=[Ridb])
        return idb, Ridb, idf, Ridf

    def load_weight(self, dst, Rdst, src2d, K, cols, stg, col0=0, kc0=0):
        S = self.S
        nk = K // 128
        srcv = src2d.rearrange("(kc p) c -> p kc c", p=128)
        engs = ("pool", "dve", "act")
        i = 0
        for kc in range(nk):
            for c0 in range(0, cols, 2048):
                cw = min(2048, cols - c0)
                stile, Rst = stg.next()
                S.dma("sp", stile[:, 0:cw], srcv[:, kc, c0:c0 + cw], writes=[Rst])
                eng = engs[i % 3]
                i += 1
                o = dst[:, kc0 + kc, col0 + c0:col0 + c0 + cw]
                if eng == "act":
                    S.op("act", lambda e, o=o, s=stile, cw=cw: e.copy(out=o, in_=s[:, 0:cw]), reads=[Rst], writes=[Rdst])
                else:
                    S.op(eng, lambda e, o=o, s=stile, cw=cw: e.tensor_copy(out=o, in_=s[:, 0:cw]), reads=[Rst], writes=[Rdst])

    def load_bcast(self, dst, Rdst, src1d, n):
        self.S.dma("sp", dst[:, 0:n], src1d.rearrange("(o n) -> o n", o=1).partition_broadcast(128), writes=[Rdst])

    def norm_to_hT(self, xt, Rxt, b, gb, Rgb, hT, RhT, tcol, ident, Rid, tmp):
        S = self.S
        junk, Rjunk = tmp["junk"].next()
        ss, Rss = tmp["ss"].next()
        h, Rh = tmp["h"].next()
        pT, RpT = tmp["pT"].next()
        S.op("act", lambda e: e.activation(out=junk[:], in_=xt[:, b, :], func=AF.Square, accum_out=ss[:, 0:1]),
             reads=[Rxt], writes=[Rjunk, Rss])

        S.op("act", lambda e: e.activation(out=ss[:, 1:2], in_=ss[:, 0:1], func=AF.Sqrt, scale=1.0 / D, bias=self.eps_t[:, 0:1]),
             reads=[Rss, self.Reps], writes=[Rss])
        S.op("dve", lambda e: e.reciprocal(out=ss[:, 2:3], in_=ss[:, 1:2]), reads=[Rss], writes=[Rss])
        S.op("dve", lambda e: e.scalar_tensor_tensor(out=h[:], in0=xt[:, b, :], scalar=ss[:, 2:3], in1=gb[:],
                                                     op0=ALU.mult, op1=ALU.mult),
             reads=[Rxt, Rss, Rgb], writes=[Rh])

        def tr(e):
            for kc in range(8):
                ins = e.transpose(out=pT[:, kc * 128:(kc + 1) * 128], in_=h[:, kc * 128:(kc + 1) * 128], identity=ident[:])
            return ins
        S.op("pe", tr, reads=[Rh, Rid], writes=[RpT])
        S.op("act", lambda e: e.copy(out=hT[:, :, tcol:tcol + 128], in_=pT[:].rearrange("p (k t) -> p k t", k=8)),
             reads=[RpT], writes=[RhT])
        return ss

    def norm_tmp(self):
        self.eps_t, Re = self.tile("eps_t", [128, 2], F32)
        eps_t = self.eps_t
        self.S.op("pool", lambda e: e.memset(eps_t[:], EPS), writes=[Re])
        self.Reps = Re
        return {
            "junk": self.rot("junk", [128, D], BF16, 1),
            "ss": self.rot("ss", [128, 4], F32, 4),
            "h": self.rot("h", [128, D], BF16, 2),
            "pT": self.rot("pT", [128, 1024], BF16, 2, ps=True),
        }

    def phase_p1(self, l):
        S, T, NS = self.S, self.T, self.NS
        self.begin()
        TT = 512
        win, Rwin = self.tile("win", [128, 8, DIN], BF16)
        stg = self.rot("wstg", [128, 2048], F32, 2)
        gb, Rgb = self.tile("gb", [128, D], F32)
        ident, Rid, _, _ = self.make_ident()
        self.load_bcast(gb, Rgb, self.w["norm_mix"][l], D)
        self.load_weight(win, Rwin, self.w["w_in"][l], D, DIN, stg)
        tmp = self.norm_tmp()
        xts = self.rot("xt", [128, 4, D], F32, 2)
        hTs = self.rot("hT", [128, 8, TT], BF16, 2)
        pss = self.rot("ps", [128, 512], F32, 4, ps=True)
        of32 = self.rot("of32", [128, 512], F32, 3)
        obf = self.rot("obf", [128, 512], BF16, 4)
        xsrc = self.x_in if l == 0 else self.scr["xres"]
        pf, nq, nk, nv, sg = (self.scr[k] for k in ("pf", "nq", "nk", "nv", "sg"))
        ev = 0
        for s in range(NS):
            for t0 in range(0, T, TT):
                xt, Rxt = xts.next()
                hT, RhT = hTs.next()
                xr = [self.dres(("xres", s, t0))] if l > 0 else []
                S.dma("sp", xt[:], xsrc[s, t0:t0 + TT, :].rearrange("(b p) d -> p b d", p=128), reads=xr, writes=[Rxt])
                for b in range(4):
                    self.norm_to_hT(xt, Rxt, b, gb, Rgb, hT, RhT, b * 128, ident, Rid, tmp)
                for cc in range(DIN // 128):
                    c0 = cc * 128
                    if NV0 <= c0 < G0:
                        continue
                    ps, Rps = pss.next()

                    def mm(e, ps=ps, c0=c0, hT=hT):
                        for kc in range(8):
                            ins = e.matmul(ps[:], lhsT=win[:, kc, c0:c0 + 128], rhs=hT[:, kc, :], start=(kc == 0), stop=(kc == 7))
                        return ins
                    S.op("pe", mm, reads=[Rwin, RhT], writes=[Rps])
                    if c0 < RW:
                        o, Ro = of32.next()
                        eng = "dve" if ev % 2 == 0 else "act"
                        ev += 1
                        if eng == "dve":
                            S.op("dve", lambda e, o=o, ps=ps: e.tensor_copy(out=o[:], in_=ps[:]), reads=[Rps], writes=[Ro])
                        else:
                            S.op("act", lambda e, o=o, ps=ps: e.copy(out=o[:], in_=ps[:]), reads=[Rps], writes=[Ro])
                        S.dma("pool", pf[s, c0:c0 + 128, t0:t0 + TT], o[:], reads=[Ro], writes=[self.dres(("pf", s))])
                    elif c0 < NV0:
                        o, Ro = obf.next()
                        S.op("dve", lambda e, o=o, ps=ps: e.tensor_copy(out=o[:], in_=ps[:]), reads=[Rps], writes=[Ro])
                        if c0 < NQ0 + 512:
                            S.dma("pool", nq[s, c0 - NQ0:c0 - NQ0 + 128, t0:t0 + TT], o[:], reads=[Ro], writes=[self.dres(("nq", s))])
                        else:
                            c1 = c0 - NQ0 - 512
                            S.dma("pool", nk[s, c1:c1 + 128, t0:t0 + TT], o[:], reads=[Ro], writes=[self.dres(("nk", s))])
                    else:
                        o, Ro = obf.next()
                        S.op("act", lambda e, o=o, ps=ps: e.activation(out=o[:], in_=ps[:], func=AF.Sigmoid), reads=[Rps], writes=[Ro])
                        c1 = c0 - G0
                        S.dma("pool", sg[s, c1:c1 + 128, t0:t0 + TT], o[:], reads=[Ro], writes=[self.dres(("sg", s))])
                for b in range(4):
                    ps, Rps = pss.next()

                    def mmv(e, ps=ps, b=b, hT=hT):
                        for kc in range(8):
                            ins = e.matmul(ps[:], lhsT=hT[:, kc, b * 128:(b + 1) * 128], rhs=win[:, kc, NV0:NV0 + 512], start=(kc == 0), stop=(kc == 7))
                        return ins
                    S.op("pe", mmv, reads=[Rwin, RhT], writes=[Rps])
                    o, Ro = obf.next()
                    S.op("dve", lambda e, o=o, ps=ps: e.tensor_copy(out=o[:], in_=ps[:]), reads=[Rps], writes=[Ro])
                    S.dma("pool", nv[s, t0 + b * 128:t0 + (b + 1) * 128, :], o[:], reads=[Ro], writes=[self.dres(("nv", s))])
        self.end()

    def phase_p2(self, l):
        S, T, NS = self.S, self.T, self.NS
        self.begin()
        C = 128
        NCH = T // C
        LD = 0.6065306597126334
        pf, yf, yr = self.scr["pf"], self.scr["yf"], self.scr["yr"]
        bon = self.scr["bon"]
        op = S.op
        bc = lambda ap, shape: ap.broadcast_to(shape)
        identb, Ridb, identf, Ridf = self.make_ident()
        mup, Rmup = self.tile("mup", [128, 15], F32)
        mun, Rmun = self.tile("mun", [128, 15], F32)
        w0t, Rw0 = self.tile("w0t", [128, 2, 4], F32)
        a0t, Ra0 = self.tile("a0t", [128, 2, 4], F32)
        kkc, Rkkc = self.tile("kkc", [128, 4], F32)
        kac, Rkac = self.tile("kac", [128, 4], F32)
        omka, Romka = self.tile("omka", [128, 4], F32)
        rkc, Rrkc = self.tile("rkc", [128, 4], F32)
        S.dma("sp", mup[:], self.w["mu_prev"][l].rearrange("(c p) -> p c", p=128), writes=[Rmup], allow_slow_non_contiguous=True)
        S.dma("sp", mun[:], self.w["mu_next"][l].rearrange("(c p) -> p c", p=128), writes=[Rmun], allow_slow_non_contiguous=True)
        for d in range(2):
            S.dma("sp", w0t[:, d, :], self.w["w0"][l, d].rearrange("(c p) -> p c", p=128), writes=[Rw0], allow_slow_non_contiguous=True)
            S.dma("sp", a0t[:, d, :], self.w["a0"][l, d].rearrange("(c p) -> p c", p=128), writes=[Ra0], allow_slow_non_contiguous=True)
        S.dma("sp", kkc[:], self.w["k_k"][l].rearrange("(c p) -> p c", p=128), writes=[Rkkc], allow_slow_non_contiguous=True)
        S.dma("sp", kac[:], self.w["k_a"][l].rearrange("(c p) -> p c", p=128), writes=[Rkac], allow_slow_non_contiguous=True)
        S.dma("sp", rkc[:], self.w["r_k"][l].rearrange("h n -> (h n)").rearrange("(c p) -> p c", p=128), writes=[Rrkc], allow_slow_non_contiguous=True)
        op("dve", lambda e: e.tensor_scalar(out=omka[:], in0=kac[:], scalar1=-1.0, scalar2=1.0, op0=ALU.mult, op1=ALU.add), reads=[Rkac], writes=[Romka])
        gng, Rgng = self.tile("gng", [128, 512], F32)
        gnb, Rgnb = self.tile("gnb", [128, 512], F32)
        self.load_bcast(gng, Rgng, self.w["gn_g"][l], 512)
        self.load_bcast(gnb, Rgnb, self.w["gn_b"][l], 512)
        wstg, Rwstg = self.tile("lstg", [128, 3, 512], F32)
        S.dma("sp", wstg[:, 0, :], self.w["w_up"][l].rearrange("d l c -> (d l) c"), writes=[Rwstg])
        S.dma("sp", wstg[:, 1, :], self.w["a_up"][l].rearrange("d l c -> (d l) c"), writes=[Rwstg])
        S.dma("sp", wstg[:, 2, :], self.w["g_up"][l], writes=[Rwstg])
        wupz, Rwupz = self.tile("wupz", [128, 2, 512], BF16)
        aupz, Raupz = self.tile("aupz", [128, 2, 512], BF16)
        gup, Rgup = self.tile("gup", [128, 512], BF16)
        op("pool", lambda e: e.memset(wupz[:].rearrange("p d c -> p (d c)"), 0.0), writes=[Rwupz])
        op("pool", lambda e: e.memset(aupz[:].rearrange("p d c -> p (d c)"), 0.0), writes=[Raupz])
        for d in range(2):
            op("dve", lambda e, d=d: e.tensor_copy(out=wupz[d * 64:(d + 1) * 64, d, :], in_=wstg[d * 64:(d + 1) * 64, 0, :]), reads=[Rwstg, Rwupz], writes=[Rwupz])
            op("dve", lambda e, d=d: e.tensor_copy(out=aupz[d * 64:(d + 1) * 64, d, :], in_=wstg[d * 64:(d + 1) * 64, 1, :]), reads=[Rwstg, Raupz], writes=[Raupz])
        op("dve", lambda e: e.tensor_copy(out=gup[:], in_=wstg[:, 2, :]), reads=[Rwstg], writes=[Rgup])
        onesf, Ronesf = self.tile("onesf", [128, 128], F32)
        op("pool", lambda e: e.memset(onesf[:], 1.0), writes=[Ronesf])
        blk1, Rblk1 = self.tile("blk1", [128, 128], F32)
        hsel, Rhsel = self.tile("hsel", [128, 2], F32)
        bdm, Rbdm = self.tile("bdm", [128, 4, 2, 64], F32)

        def mkz(e):
            e.memset(blk1[:], 0.0)
            e.memset(hsel[:], 0.0)
            return e.memset(bdm[:].rearrange("p c h i -> p (c h i)"), 0.0)
        op("dve", mkz, writes=[Rblk1, Rhsel, Rbdm])

        def mko(e):
            e.memset(blk1[0:64, 0:64], 1.0)
            e.memset(blk1[64:128, 64:128], 1.0)
            e.memset(hsel[0:64, 0:1], 1.0)
            e.memset(hsel[64:128, 1:2], 1.0)
            e.memset(bdm[0:64, :, 0, :], 1.0)
            return e.memset(bdm[64:128, :, 1, :], 1.0)
        op("dve", mko, reads=[Rblk1, Rhsel, Rbdm], writes=[Rblk1, Rhsel, Rbdm])
        gne, Rgne = self.tile("gne", [128, 1], F32)
        op("pool", lambda e: e.memset(gne[:], GN_EPS), writes=[Rgne])
        mbase, Rmbase = self.tile("mbase", [128, 4, 128], F32)

        op("pool", lambda e: e.memset(mbase[:].rearrange("p a x -> p (a x)"), 1.0), writes=[Rmbase])
        op("pool", lambda e: e.affine_select(out=mbase[:, 0, :], in_=mbase[:, 0, :], pattern=[[1, 128]], compare_op=ALU.is_gt, fill=0.0, base=0, channel_multiplier=-1), reads=[Rmbase], writes=[Rmbase])
        op("pool", lambda e: e.affine_select(out=mbase[:, 1, :], in_=mbase[:, 1, :], pattern=[[1, 128]], compare_op=ALU.is_ge, fill=0.0, base=0, channel_multiplier=-1), reads=[Rmbase], writes=[Rmbase])
        op("pool", lambda e: e.affine_select(out=mbase[:, 2, :], in_=mbase[:, 2, :], pattern=[[-1, 128]], compare_op=ALU.is_gt, fill=0.0, base=0, channel_multiplier=1), reads=[Rmbase], writes=[Rmbase])
        op("pool", lambda e: e.affine_select(out=mbase[:, 3, :], in_=mbase[:, 3, :], pattern=[[-1, 128]], compare_op=ALU.is_ge, fill=0.0, base=0, channel_multiplier=1), reads=[Rmbase], writes=[Rmbase])
        mG, RmG = self.tile("mG", [128, 2, 2, 2, 128], BF16)
        mL, RmL = self.tile("mL", [128, 2, 2, 128], BF16)
        for d in range(2):
            si, ii, li = (0, 1, 2) if d == 0 else (2, 3, 0)
            for h2 in range(2):
                op("dve", lambda e, d=d, h2=h2, si=si: e.tensor_copy(out=mG[:, d, h2, 0, :], in_=mbase[:, si, :]), reads=[Rmbase], writes=[RmG])
                op("dve", lambda e, d=d, h2=h2, ii=ii: e.tensor_copy(out=mG[:, d, h2, 1, :], in_=mbase[:, ii, :]), reads=[Rmbase], writes=[RmG])
                op("dve", lambda e, d=d, h2=h2, li=li: e.tensor_copy(out=mL[:, d, h2, :], in_=mbase[:, li, :]), reads=[Rmbase], writes=[RmL])
        Eg, REg = self.tile("Eg", [8, 3, 128], F32)
        op("pool", lambda e: e.memset(Eg[:].rearrange("p a x -> p (a x)"), 1.0), writes=[REg])
        for gi, b_ in enumerate((16, 32, 64)):
            op("pool", lambda e, gi=gi, b_=b_: e.affine_select(out=Eg[:, gi, :], in_=Eg[:, gi, :], pattern=[[1, 128]], compare_op=ALU.is_ge, fill=0.0, base=0, channel_multiplier=-b_),
               reads=[REg], writes=[REg])
            op("pool", lambda e, gi=gi, b_=b_: e.affine_select(out=Eg[:, gi, :], in_=Eg[:, gi, :], pattern=[[-1, 128]], compare_op=ALU.is_ge, fill=0.0, base=b_ - 1, channel_multiplier=b_),
               reads=[REg], writes=[REg])
        bdf, Rbdf = self.tile("bdf", [128, 4, 128], F32)
        op("pool", lambda e: e.memset(bdf[:, 3, :], 1.0), writes=[Rbdf])
        pbm, Rpbm = self.psum("pbm", [128, 512], F32)

        def mmE(e):
            for gi in range(3):
                ins = e.matmul(pbm[:, gi * 128:(gi + 1) * 128], lhsT=Eg[:, gi, :], rhs=Eg[:, gi, :], start=True, stop=True)
            return ins
        op("pe", mmE, reads=[REg], writes=[Rpbm])
        op("dve", lambda e: e.tensor_copy(out=bdf[:, 0:3, :].rearrange("p a x -> p (a x)"), in_=pbm[:, 0:384]), reads=[Rpbm, Rbdf], writes=[Rbdf])
        bd16, Rbd16 = self.tile("bd16", [128, 2, 128], BF16)
        offm, Roffm = self.tile("offm", [128, 3, 4, 128], BF16)
        for h2 in range(4):
            if h2 < 2:
                op("dve", lambda e, h2=h2: e.tensor_copy(out=bd16[:, h2, :], in_=bdf[:, 0, :]), reads=[Rbdf], writes=[Rbd16])
            for lv in range(3):
                op("dve", lambda e, h2=h2, lv=lv: e.tensor_tensor(out=offm[:, lv, h2, :], in0=bdf[:, lv + 1, :], in1=bdf[:, lv, :], op=ALU.subtract),
                   reads=[Rbdf], writes=[Roffm])
        Ps = self.rot("P", [128, 15, 130], F32, 1)
        tAs = self.rot("tA", [128, 15, 128], F32, 1)
        tBs = self.rot("tB", [128, 15, 128], F32, 1)
        phs = self.rot("ph", [128, 15, 128], F32, 1)
        f4 = lambda n, k=1: self.rot(n, [128, 4, 128], F32, k)
        b4 = lambda n, k=1: self.rot(n, [128, 4, 128], BF16, k)
        lws, avs, pres, cums, cprs = f4("lw"), f4("av"), f4("pre"), f4("cum"), f4("cpr")
        ecs, ecps, eis = f4("ec"), f4("ecp"), f4("ei")
        kraws, sqs, nrms, kkvs, tts, kds, bvs = (f4(n) for n in ("kraw", "sq", "nrm", "kkv", "tt", "kd", "bv"))
        Kfs, Bfs, rks = kraws, nrms, sqs
        ATs, RTs, KTs, BTs, KhTs, BhTs, vTs = (b4(n, 2) for n in ("AT", "RT", "KT", "BT", "KhT", "BhT", "vT"))
        twds = self.rot("twd", [128, 128], BF16, 2)
        adbs = self.rot("adb", [128, 128], BF16, 2)
        sgds = self.rot("sgd", [128, 128], BF16, 2)
        gCs = self.rot("gC", [128, 4], F32, 2)
        ARbds = self.rot("ARbd", [128, 4, 2, 2, 128], BF16, 2)
        Bbds = self.rot("Bbd", [128, 4, 2, 128], BF16, 2)
        for (t_, r_) in ARbds.items:
            op("pool", lambda e, t_=t_: e.memset(t_[:].rearrange("p c h x t -> p (c h x t)"), 0.0), writes=[r_])
        for (t_, r_) in Bbds.items:
            op("pool", lambda e, t_=t_: e.memset(t_[:].rearrange("p c h t -> p (c h t)"), 0.0), writes=[r_])
        VKs = self.rot("VK", [128, 1024], BF16, 2)
        Bhats = self.rot("Bhat", [128, 512], BF16, 2)
        GKs = self.rot("GK", [128, 4, 2, 2, 128], BF16, 1)
        GBs = self.rot("GB", [128, 4, 2, 2, 128], BF16, 1)
        Lhs = self.rot("Lh", [128, 4, 2, 128], BF16, 1)
        lmq = [[self.tile(f"lmq{h}_{k}", [128, 4, 128], BF16) for k in range(2)] for h in range(8)]
        zzt = [self.tile(f"zz{h}", [128, 2, 2, 128], BF16) for h in range(4)]
        ttt = [[self.tile(f"tt{h}_{k}", [128, 2, 2, 128], BF16) for k in range(2)] for h in range(4)]
        L16s = self.rot("L16", [128, 4, 2, 128], BF16, 1)
        M16s = self.rot("M16", [128, 4, 2, 128], BF16, 1)
        St, RSt = self.tile("St", [128, 4, 2, 64], F32)
        Sbd, RSbd = self.tile("Sbd", [128, 4, 2, 64], BF16)
        stmp, Rstmp = self.tile("stmp", [128, 4, 2, 64], F32)
        Wsbs = self.rot("Wsb", [128, 512], BF16, 2)
        Usbs = self.rot("Usb", [128, 512], BF16, 2)
        Ysbs = self.rot("Ysb", [128, 512], F32, 2)
        bons = self.rot("bonv", [128, 8], F32, 2)
        bon0s = self.rot("bon0", [128, 8], F32, 2)
        yfls = self.rot("yfl", [128, 512], F32, 2)
        gtms = self.rot("gtm", [128, 512], F32, 2)
        sqvs = self.rot("sqv", [128, 512], F32, 1)
        ycs = self.rot("yc", [128, 512], F32, 1)
        vvs = self.rot("vv", [128, 512], F32, 1)
        stat = self.rot("stat", [128, 6, 8], F32, 2)
        yos = self.rot("yo", [128, 512], BF16, 2)
        pbanks = self.rot("pb", [128, 512], F32, 6, ps=True)
        ptr = self.rot("ptr", [128, 1024], BF16, 1, ps=True)
        pfv = [pf[s].rearrange("(c p) t -> p c t", p=128) for s in range(NS)]

        for s in range(NS):
            for d in range(2):
                op("pool", lambda e: e.memset(St[:].rearrange("p c h i -> p (c h i)"), 0.0), writes=[RSt])
                op("pool", lambda e: e.memset(Sbd[:].rearrange("p c h i -> p (c h i)"), 0.0), writes=[RSbd])
                order = range(NCH) if d == 0 else range(NCH - 1, -1, -1)
                def body(ci, s=s, d=d):
                    t0 = ci * C
                    P, RP = Ps.next()
                    lo, hi = max(t0 - 1, 0), min(t0 + C + 1, T)
                    if t0 == 0:
                        op("pool", lambda e, P=P: e.memset(P[:, :, 0:1], 0.0), writes=[RP])
                    if t0 + C == T:
                        op("pool", lambda e, P=P: e.memset(P[:, :, 129:130], 0.0), writes=[RP])
                    S.dma("sp", P[:, :, lo - (t0 - 1):hi - (t0 - 1)], pfv[s][:, :, lo:hi], reads=[self.dres(("pf", s))], writes=[RP])
                    tA, RtA = tAs.next()
                    tB, RtB = tBs.next()
                    ph, Rph = phs.next()
                    op("dve", lambda e, tA=tA, P=P: e.tensor_tensor(out=tA[:], in0=P[:, :, 0:128], in1=P[:, :, 1:129], op=ALU.subtract), reads=[RP], writes=[RtA])
                    op("pool", lambda e, tB=tB, P=P: e.tensor_tensor(out=tB[:], in0=P[:, :, 2:130], in1=P[:, :, 1:129], op=ALU.subtract), reads=[RP], writes=[RtB])
                    op("dve", lambda e, tA=tA: e.tensor_tensor(out=tA[:], in0=tA[:], in1=bc(mup[:].unsqueeze(2), [128, 15, 128]), op=ALU.mult), reads=[RtA, Rmup], writes=[RtA])
                    op("pool", lambda e, tB=tB: e.tensor_tensor(out=tB[:], in0=tB[:], in1=bc(mun[:].unsqueeze(2), [128, 15, 128]), op=ALU.mult), reads=[RtB, Rmun], writes=[RtB])
                    op("dve", lambda e, tA=tA, tB=tB: e.tensor_tensor(out=tA[:], in0=tA[:], in1=tB[:], op=ALU.add), reads=[RtA, RtB], writes=[RtA])
                    op("dve", lambda e, tA=tA, P=P, ph=ph: e.tensor_tensor(out=ph[:], in0=tA[:], in1=P[:, :, 1:129], op=ALU.add), reads=[RtA, RP], writes=[Rph])
                    rh, kh, vh = ph[:, 0:4, :], ph[:, 4:8, :], ph[:, 8:12, :]
                    yield 'a'
                    twd, Rtwd = twds.next()
                    adb, Radb = adbs.next()
                    sgd, Rsgd = sgds.next()
                    op("act", lambda e, twd=twd, ph=ph: e.activation(out=twd[:], in_=ph[:, 12, :], func=AF.Tanh), reads=[Rph], writes=[Rtwd])
                    op("dve", lambda e, adb=adb, ph=ph: e.tensor_copy(out=adb[:], in_=ph[:, 13, :]), reads=[Rph], writes=[Radb])
                    psw, Rpsw = pbanks.next()
                    psa, Rpsa = pbanks.next()

                    def mmlora(e, psw=psw, psa=psa, twd=twd, adb=adb, d=d):
                        for cc in range(4):
                            e.matmul(psw[:, cc * 128:(cc + 1) * 128], lhsT=wupz[:, d, cc * 128:(cc + 1) * 128], rhs=twd[:], start=True, stop=True)
                        for cc in range(4):
                            ins = e.matmul(psa[:, cc * 128:(cc + 1) * 128], lhsT=aupz[:, d, cc * 128:(cc + 1) * 128], rhs=adb[:], start=True, stop=True)
                        return ins
                    op("pe", mmlora, reads=[Rwupz, Raupz, Rtwd, Radb], writes=[Rpsw, Rpsa])
                    yield 'a'
                    lw, Rlw = lws.next()
                    av, Rav = avs.next()

                    def sigw(e, lw=lw, psw=psw, d=d):
                        for cc in range(4):
                            ins = e.activation(out=lw[:, cc, :], in_=psw[:, cc * 128:(cc + 1) * 128], func=AF.Sigmoid, bias=w0t[:, d, cc:cc + 1])
                        return ins
                    op("act", sigw, reads=[Rpsw, Rw0], writes=[Rlw])

                    def siga(e, av=av, psa=psa, d=d):
                        for cc in range(4):
                            ins = e.activation(out=av[:, cc, :], in_=psa[:, cc * 128:(cc + 1) * 128], func=AF.Sigmoid, bias=a0t[:, d, cc:cc + 1])
                        return ins
                    op("act", siga, reads=[Rpsa, Ra0], writes=[Rav])
                    yield 'a'
                    yield 'a'
                    pre, Rpre = pres.next()
                    cum, Rcum = cums.next()
                    cpr, Rcpr = cprs.next()

                    def scan(e, pre=pre, lw=lw):
                        for cc in range(4):
                            ins = e.tensor_tensor_scan(out=pre[:, cc, :], data0=onesf[:], data1=lw[:, cc, :], initial=0.0, op0=ALU.mult, op1=ALU.add)
                        return ins
                    op("dve", scan, reads=[Rlw, Ronesf], writes=[Rpre])
                    yield 'a'
                    if d == 0:
                        cum, Rcum = pre, Rpre
                    else:
                        op("dve", lambda e, cum=cum, pre=pre, lw=lw: e.scalar_tensor_tensor(out=cum[:], in0=pre[:], scalar=-1.0, in1=lw[:], op0=ALU.mult, op1=ALU.add),
                           reads=[Rpre, Rlw], writes=[Rcum])
                        op("dve", lambda e, cum=cum, pre=pre: e.tensor_tensor(out=cum[:], in0=cum[:], in1=bc(pre[:, :, 127:128], [128, 4, 128]), op=ALU.add),
                           reads=[Rcum, Rpre], writes=[Rcum])
                    op("pool", lambda e, cpr=cpr, cum=cum, lw=lw: e.tensor_tensor(out=cpr[:], in0=cum[:], in1=lw[:], op=ALU.subtract), reads=[Rcum, Rlw], writes=[Rcpr])
                    ec, Rec = ecs.next()
                    ecp, Recp = ecps.next()
                    ei, Rei = eis.next()
                    op("act", lambda e, ec=ec, cum=cum: e.activation(out=ec[:], in_=cum[:], func=AF.Exp, scale=-LD), reads=[Rcum], writes=[Rec])
                    op("act", lambda e, ecp=ecp, cpr=cpr: e.activation(out=ecp[:], in_=cpr[:], func=AF.Exp, scale=-LD), reads=[Rcpr], writes=[Recp])
                    op("act", lambda e, ei=ei, cum=cum: e.activation(out=ei[:], in_=cum[:], func=AF.Exp, scale=LD), reads=[Rcum], writes=[Rei])
                    yield 'a'
                    gC, RgC = gCs.next()
                    ce = 127 if d == 0 else 0
                    op("dve", lambda e, gC=gC, ec=ec, ce=ce: e.tensor_copy(out=gC[:], in_=ec[:, :, ce]), reads=[Rec], writes=[RgC])
                    yield 'a'
                    kraw, Rkraw = kraws.next()
                    sq, Rsq = sqs.next()
                    nrm, Rnrm = nrms.next()
                    kkv, Rkkv = kkvs.next()
                    tt, Rtt = tts.next()
                    kd, Rkd = kds.next()
                    bv, Rbv = bvs.next()
                    op("dve", lambda e, kraw=kraw, ph=ph: e.tensor_tensor(out=kraw[:], in0=ph[:, 4:8, :], in1=bc(kkc[:].unsqueeze(2), [128, 4, 128]), op=ALU.mult), reads=[Rph, Rkkc], writes=[Rkraw])
                    op("pool", lambda e, sq=sq, kraw=kraw: e.tensor_tensor(out=sq[:], in0=kraw[:], in1=kraw[:], op=ALU.mult), reads=[Rkraw], writes=[Rsq])
                    psn, Rpsn = pbanks.next()
                    op("pe", lambda e, psn=psn, sq=sq: e.matmul(psn[:], lhsT=blk1[:], rhs=sq[:].rearrange("p c t -> p (c t)"), start=True, stop=True), reads=[Rblk1, Rsq], writes=[Rpsn])
                    yield 'a'
                    op("act", lambda e, nrm=nrm, psn=psn: e.activation(out=nrm[:].rearrange("p c t -> p (c t)"), in_=psn[:], func=AF.Sqrt), reads=[Rpsn], writes=[Rnrm])
                    op("dve", lambda e, nrm=nrm: e.tensor_scalar_max(out=nrm[:], in0=nrm[:], scalar1=1e-12), reads=[Rnrm], writes=[Rnrm])
                    op("dve", lambda e, nrm=nrm: e.reciprocal(out=nrm[:], in_=nrm[:]), reads=[Rnrm], writes=[Rnrm])
                    op("dve", lambda e, kkv=kkv, kraw=kraw, nrm=nrm: e.tensor_tensor(out=kkv[:], in0=kraw[:], in1=nrm[:], op=ALU.mult), reads=[Rkraw, Rnrm], writes=[Rkkv])
                    yield 'a'
                    op("pool", lambda e, tt=tt, av=av: e.tensor_tensor(out=tt[:], in0=av[:], in1=bc(kac[:].unsqueeze(2), [128, 4, 128]), op=ALU.mult), reads=[Rav, Rkac], writes=[Rtt])
                    op("pool", lambda e, tt=tt: e.tensor_tensor(out=tt[:], in0=tt[:], in1=bc(omka[:].unsqueeze(2), [128, 4, 128]), op=ALU.add), reads=[Rtt, Romka], writes=[Rtt])
                    op("dve", lambda e, kd=kd, ph=ph, tt=tt: e.tensor_tensor(out=kd[:], in0=ph[:, 4:8, :], in1=tt[:], op=ALU.mult), reads=[Rph, Rtt], writes=[Rkd])
                    yield 'a'
                    op("pool", lambda e, bv=bv, kkv=kkv, av=av: e.tensor_tensor(out=bv[:], in0=kkv[:], in1=av[:], op=ALU.mult), reads=[Rkkv, Rav], writes=[Rbv])
                    yield 'a'
                    AT, RAT = ATs.next()
                    RT, RRT = RTs.next()
                    KT, RKT = KTs.next()
                    BT, RBT = BTs.next()
                    KhT, RKhT = KhTs.next()
                    BhT, RBhT = BhTs.next()
                    vT, RvT = vTs.next()
                    Kf, RKf = Kfs.next()
                    Bf, RBf = Bfs.next()
                    rk, Rrk = rks.next()
                    op("dve", lambda e, AT=AT, kkv=kkv, ecp=ecp: e.scalar_tensor_tensor(out=AT[:], in0=kkv[:], scalar=-1.0, in1=ecp[:], op0=ALU.mult, op1=ALU.mult), reads=[Rkkv, Recp], writes=[RAT])
                    op("dve", lambda e, RT=RT, ph=ph, ec=ec: e.tensor_tensor(out=RT[:], in0=ph[:, 0:4, :], in1=ec[:], op=ALU.mult), reads=[Rph, Rec], writes=[RRT])
                    op("dve", lambda e, Kf=Kf, kd=kd, ei=ei: e.tensor_tensor(out=Kf[:], in0=kd[:], in1=ei[:], op=ALU.mult), reads=[Rkd, Rei], writes=[RKf])
                    yield 'a'
                    op("pool", lambda e, Bf=Bf, bv=bv, ei=ei: e.tensor_tensor(out=Bf[:], in0=bv[:], in1=ei[:], op=ALU.mult), reads=[Rbv, Rei], writes=[RBf])
                    op("act", lambda e, KT=KT, Kf=Kf: e.copy(out=KT[:], in_=Kf[:]), reads=[RKf], writes=[RKT])
                    op("act", lambda e, BT=BT, Bf=Bf: e.copy(out=BT[:], in_=Bf[:]), reads=[RBf], writes=[RBT])
                    yield 'a'
                    op("dve", lambda e, KhT=KhT, Kf=Kf, gC=gC: e.tensor_tensor(out=KhT[:], in0=Kf[:], in1=bc(gC[:].unsqueeze(2), [128, 4, 128]), op=ALU.mult), reads=[RKf, RgC], writes=[RKhT])
                    op("pool", lambda e, BhT=BhT, Bf=Bf, gC=gC: e.tensor_tensor(out=BhT[:], in0=Bf[:], in1=bc(gC[:].unsqueeze(2), [128, 4, 128]), op=ALU.mult), reads=[RBf, RgC], writes=[RBhT])
                    op("act", lambda e, vT=vT, ph=ph: e.copy(out=vT[:], in_=ph[:, 8:12, :]), reads=[Rph], writes=[RvT])
                    op("pool", lambda e, rk=rk, ph=ph, kd=kd: e.tensor_tensor(out=rk[:], in0=ph[:, 0:4, :], in1=kd[:], op=ALU.mult), reads=[Rph, Rkd], writes=[Rrk])
                    op("pool", lambda e, rk=rk: e.tensor_tensor(out=rk[:], in0=rk[:], in1=bc(rkc[:].unsqueeze(2), [128, 4, 128]), op=ALU.mult), reads=[Rrk, Rrkc], writes=[Rrk])
                    yield 'a'
                    ARbd, RARbd = ARbds.next()
                    Bbd, RBbd = Bbds.next()
                    for h2 in range(2):
                        pl = slice(h2 * 64, (h2 + 1) * 64)
                        op("pool", lambda e, pl=pl, h2=h2, AT=AT: e.tensor_copy(out=ARbd[pl, :, h2, 0, :], in_=AT[pl, :, :]), reads=[RAT, RARbd], writes=[RARbd])
                        op("pool", lambda e, pl=pl, h2=h2, RT=RT: e.tensor_copy(out=ARbd[pl, :, h2, 1, :], in_=RT[pl, :, :]), reads=[RRT, RARbd], writes=[RARbd])
                        op("pool", lambda e, pl=pl, h2=h2, BT=BT: e.tensor_copy(out=Bbd[pl, :, h2, :], in_=BT[pl, :, :]), reads=[RBT, RBbd], writes=[RBbd])
                    yield 'a'
                    pt, Rpt = ptr.next()
                    VK, RVK = VKs.next()
                    Vtm, RVtm = VK[:, 0:512], RVK
                    Khat, RKhat = VK[:, 512:1024], RVK
                    Bhat, RBhat = Bhats.next()

                    def tr1(e, pt=pt, vT=vT, KhT=KhT):
                        for cc in range(4):
                            e.transpose(out=pt[:, cc * 128:(cc + 1) * 128], in_=vT[:, cc, :], identity=identb[:])
                        for cc in range(4):
                            ins = e.transpose(out=pt[:, 512 + cc * 128:512 + (cc + 1) * 128], in_=KhT[:, cc, :], identity=identb[:])
                        return ins
                    op("pe", tr1, reads=[RvT, RKhT, Ridb], writes=[Rpt])
                    yield 'a'
                    op("act", lambda e, VK=VK, pt=pt: e.copy(out=VK[:], in_=pt[:]), reads=[Rpt], writes=[RVK])
                    pt2, Rpt2 = ptr.next()

                    def tr2(e, pt2=pt2, BhT=BhT):
                        for cc in range(4):
                            ins = e.transpose(out=pt2[:, cc * 128:(cc + 1) * 128], in_=BhT[:, cc, :], identity=identb[:])
                        return ins
                    op("pe", tr2, reads=[RBhT, Ridb], writes=[Rpt2])
                    yield 'a'
                    op("act", lambda e, Bhat=Bhat, pt2=pt2: e.copy(out=Bhat[:], in_=pt2[:, 0:512]), reads=[Rpt2], writes=[RBhat])
                    psb, Rpsb = pbanks.next()
                    bonv, Rbonv = bons.next()

                    def mmbon(e, psb=psb, rk=rk):
                        for cc in range(4):
                            ins = e.transpose(out=psb[:, cc * 128:(cc + 1) * 128], in_=rk[:, cc, :], identity=identf[:])
                        return ins
                    op("pe", mmbon, reads=[Rrk, Ridf], writes=[Rpsb])
                    op("dve", lambda e, bonv=bonv, psb=psb: e.tensor_reduce(out=bonv[:], in_=psb[:].rearrange("p (h i) -> p h i", i=64), axis=AX.X, op=ALU.add),
                       reads=[Rpsb], writes=[Rbonv])
                    if d == 1:
                        op("act", lambda e, sgd=sgd, ph=ph: e.activation(out=sgd[:], in_=ph[:, 14, :], func=AF.Sigmoid), reads=[Rph], writes=[Rsgd])
                        psg, Rpsg = pbanks.next()
                        gtm, Rgtm = gtms.next()
                        op("pe", lambda e, psg=psg, sgd=sgd: e.matmul(psg[:], lhsT=sgd[:], rhs=gup[:], start=True, stop=True), reads=[Rsgd, Rgup], writes=[Rpsg])
                        op("act", lambda e, gtm=gtm, psg=psg: e.copy(out=gtm[:], in_=psg[:]), reads=[Rpsg], writes=[Rgtm])
                    yield 'B'
                    GK, RGK = GKs.next()
                    GB, RGB = GBs.next()
                    Lh, RLh = Lhs.next()
                    for cc in range(4):
                        p1, Rp1 = pbanks.next()
                        p2, Rp2 = pbanks.next()
                        p3, Rp3 = pbanks.next()

                        def mmg(e, cc=cc, p1=p1, p2=p2, p3=p3, KT=KT, BT=BT, AT=AT):
                            e.matmul(p1[:], lhsT=KT[:, cc, :], rhs=ARbd[:, cc].rearrange("p h x t -> p (h x t)"), start=True, stop=True)
                            e.matmul(p2[:], lhsT=BT[:, cc, :], rhs=ARbd[:, cc].rearrange("p h x t -> p (h x t)"), start=True, stop=True)
                            return e.matmul(p3[:, 0:256], lhsT=AT[:, cc, :], rhs=Bbd[:, cc].rearrange("p h t -> p (h t)"), start=True, stop=True)
                        op("pe", mmg, reads=[RKT, RBT, RAT, RARbd, RBbd], writes=[Rp1, Rp2, Rp3])
                        op("dve", lambda e, cc=cc, p1=p1, GK=GK, d=d: e.tensor_tensor(out=GK[:, cc].rearrange("p h x t -> p (h x t)"), in0=p1[:],
                                                                                 in1=mG[:, d].rearrange("p h x t -> p (h x t)"), op=ALU.mult), reads=[Rp1, RmG], writes=[RGK])
                        op("dve", lambda e, cc=cc, p2=p2, GB=GB, d=d: e.tensor_tensor(out=GB[:, cc].rearrange("p h x t -> p (h x t)"), in0=p2[:],
                                                                                 in1=mG[:, d].rearrange("p h x t -> p (h x t)"), op=ALU.mult), reads=[Rp2, RmG], writes=[RGB])
                        op("dve", lambda e, cc=cc, p3=p3, Lh=Lh, d=d: e.tensor_tensor(out=Lh[:, cc].rearrange("p h t -> p (h t)"), in0=p3[:, 0:256],
                                                                                 in1=mL[:, d].rearrange("p h t -> p (h t)"), op=ALU.mult), reads=[Rp3, RmL], writes=[RLh])
                        yield 'b'
                    L16, RL16 = L16s.next()
                    M16, RM16 = M16s.next()
                    for cc in range(4):
                        op("pool", lambda e, cc=cc, L16=L16, Lh=Lh: e.tensor_tensor(out=L16[:, cc].rearrange("p h t -> p (h t)"), in0=Lh[:, cc].rearrange("p h t -> p (h t)"),
                                                                              in1=bd16[:].rearrange("p h t -> p (h t)"), op=ALU.mult), reads=[RLh, Rbd16], writes=[RL16])
                        op("pool", lambda e, cc=cc, M16=M16, GB=GB: e.tensor_tensor(out=M16[:, cc], in0=GB[:, cc, :, 0, :], in1=bd16[:], op=ALU.mult), reads=[RGB, Rbd16], writes=[RM16])
                    cur = []
                    for h in range(8):
                        cur.append((L16[:, h // 2, h % 2, :], RL16, M16[:, h // 2, h % 2, :], RM16, identb[:], Ridb, identb[:], Ridb))
                    for lev in range(4):
                        for h in range(8):
                            Lp, RLp, Mp, RMp, Qp, RQp, Pp, RPp = cur[h]
                            pq, Rpq = pbanks.next()
                            lt, Rlt = lmq[h][lev % 2]

                            def mmi(e, pq=pq, Lp=Lp, Mp=Mp, Qp=Qp, Pp=Pp, lev=lev):
                                e.matmul(pq[:, 256:384], lhsT=identb[:], rhs=Qp, start=True, stop=False)
                                e.matmul(pq[:, 256:384], lhsT=Lp, rhs=Qp, start=False, stop=True)
                                e.matmul(pq[:, 384:512], lhsT=identb[:], rhs=Pp, start=True, stop=False)
                                ins = e.matmul(pq[:, 384:512], lhsT=Mp, rhs=Pp, start=False, stop=True)
                                if lev < 3:
                                    e.matmul(pq[:, 0:128], lhsT=Mp, rhs=Lp, start=True, stop=True)
                                    ins = e.matmul(pq[:, 128:256], lhsT=Lp, rhs=Mp, start=True, stop=True)
                                return ins
                            op("pe", mmi, reads=[RLp, RMp, RQp, RPp, Ridb], writes=[Rpq])
                            lo_ = 0 if lev < 3 else 256
                            eng = "act" if (h + lev) % 3 != 0 else "dve"
                            if eng == "dve":
                                op("dve", lambda e, lt=lt, pq=pq, lo_=lo_: e.tensor_copy(out=lt[:].rearrange("p a t -> p (a t)")[:, lo_:512], in_=pq[:, lo_:512]), reads=[Rpq], writes=[Rlt])
                            else:
                                op("act", lambda e, lt=lt, pq=pq, lo_=lo_: e.copy(out=lt[:].rearrange("p a t -> p (a t)")[:, lo_:512], in_=pq[:, lo_:512]), reads=[Rpq], writes=[Rlt])
                            cur[h] = (lt[:, 0, :], Rlt, lt[:, 1, :], Rlt, lt[:, 2, :], Rlt, lt[:, 3, :], Rlt)
                            if h % 2 == 1:
                                yield 'b'
                    tcur = [(cur[h][6], cur[h][7], cur[h][4], cur[h][5]) for h in range(8)]
                    for lv in range(3):
                        last_lv = (lv == 2)
                        pas = []
                        for pr in range(4):
                            pa, Rpa = pbanks.next()
                            pas.append((pa, Rpa))

                            def mmz(e, pa=pa, pr=pr, tc=list(tcur), last_lv=last_lv, Lh=Lh, GB=GB):
                                for h2 in range(2):
                                    h = 2 * pr + h2
                                    Tk, RTk, TTk, RTTk = tc[h]
                                    ins = e.matmul(pa[:, h2 * 256 + 128:h2 * 256 + 256], lhsT=Lh[:, pr, h2, :], rhs=TTk, start=True, stop=True)
                                    if not last_lv:
                                        ins = e.matmul(pa[:, h2 * 256:h2 * 256 + 128], lhsT=GB[:, pr, h2, 0, :], rhs=Tk, start=True, stop=True)
                                return ins
                            op("pe", mmz, reads=[RLh, RGB, tcur[2 * pr][1], tcur[2 * pr][3], tcur[2 * pr + 1][1], tcur[2 * pr + 1][3]], writes=[Rpa])
                        for pr in range(4):
                            pa, Rpa = pas[pr]
                            zz, Rzz = zzt[pr]
                            if not last_lv:
                                op("dve", lambda e, zz=zz, pa=pa, lv=lv: e.tensor_tensor(out=zz[:].rearrange("p h a t -> p (h a t)"), in0=pa[:],
                                                                                        in1=offm[:, lv].rearrange("p a t -> p (a t)"), op=ALU.mult),
                                   reads=[Rpa, Roffm], writes=[Rzz])
                            else:
                                op("dve", lambda e, zz=zz, pa=pa, lv=lv: e.tensor_tensor(out=zz[:, :, 1, :], in0=pa[:].rearrange("p (h a t) -> p h a t", h=2, a=2)[:, :, 1, :],
                                                                                        in1=offm[:, lv, 0:2, :], op=ALU.mult),
                                   reads=[Rpa, Roffm], writes=[Rzz])
                        yield 'b'
                        pbs = []
                        for pr in range(4):
                            pb_, Rpb_ = pbanks.next()
                            pbs.append((pb_, Rpb_))
                            zz, Rzz = zzt[pr]

                            def mmt(e, pb_=pb_, pr=pr, tc=list(tcur), zz=zz, last_lv=last_lv):
                                for h2 in range(2):
                                    h = 2 * pr + h2
                                    Tk, RTk, TTk, RTTk = tc[h]
                                    c0 = h2 * 256
                                    e.matmul(pb_[:, c0 + 128:c0 + 256], lhsT=identb[:], rhs=TTk, start=True, stop=False)
                                    ins = e.matmul(pb_[:, c0 + 128:c0 + 256], lhsT=Tk, rhs=zz[:, h2, 1, :], start=False, stop=True)
                                    if not last_lv:
                                        e.matmul(pb_[:, c0:c0 + 128], lhsT=identb[:], rhs=Tk, start=True, stop=False)
                                        ins = e.matmul(pb_[:, c0:c0 + 128], lhsT=TTk, rhs=zz[:, h2, 0, :], start=False, stop=True)
                                return ins
                            op("pe", mmt, reads=[Rzz, Ridb, tcur[2 * pr][1], tcur[2 * pr][3], tcur[2 * pr + 1][1], tcur[2 * pr + 1][3]], writes=[Rpb_])
                        for pr in range(4):
                            pb_, Rpb_ = pbs[pr]
                            tn, Rtn = ttt[pr][lv % 2]
                            eng = "act" if pr % 2 == 0 else "dve"
                            if not last_lv:
                                src, dst = pb_[:], tn[:].rearrange("p h a t -> p (h a t)")
                            else:
                                src, dst = pb_[:].rearrange("p (h a t) -> p h a t", h=2, a=2)[:, :, 1, :], tn[:, :, 1, :]
                            if eng == "act":
                                op("act", lambda e, src=src, dst=dst: e.copy(out=dst, in_=src), reads=[Rpb_], writes=[Rtn])
                            else:
                                op("dve", lambda e, src=src, dst=dst: e.tensor_copy(out=dst, in_=src), reads=[Rpb_], writes=[Rtn])
                            for h2 in range(2):
                                tcur[2 * pr + h2] = (tn[:, h2, 0, :], Rtn, tn[:, h2, 1, :], Rtn)
                        yield 'b'
                    cur = [(None, None, None, None, tcur[h][2], tcur[h][3]) for h in range(8)]
                    yield 'C'
                    pW, RpW = pbanks.next()
                    Wsb, RWsb = Wsbs.next()
                    Usb, RUsb = Usbs.next()
                    Ysb, RYsb = Ysbs.next()

                    def mmW(e, pW=pW, AT=AT, GK=GK, Vtm=Vtm):
                        for cc in range(4):
                            e.matmul(pW[:, cc * 128:(cc + 1) * 128], lhsT=AT[:, cc, :], rhs=Sbd[:, cc].rearrange("p h i -> p (h i)"), start=True, stop=False)
                            for h2 in range(2):
                                h = 2 * cc + h2
                                ins = e.matmul(pW[:, h * 64:(h + 1) * 64], lhsT=GK[:, cc, h2, 0, :], rhs=Vtm[:, h * 64:(h + 1) * 64], start=False, stop=True)
                        return ins
                    op("pe", mmW, reads=[RAT, RSbd, RGK, RVtm], writes=[RpW])
                    op("dve", lambda e, Wsb=Wsb, pW=pW: e.tensor_copy(out=Wsb[:], in_=pW[:]), reads=[RpW], writes=[RWsb])
                    pU, RpU = pbanks.next()

                    def mmU(e, pU=pU, Wsb=Wsb, cur=list(cur)):
                        for h in range(8):
                            ins = e.matmul(pU[:, h * 64:(h + 1) * 64], lhsT=cur[h][4], rhs=Wsb[:, h * 64:(h + 1) * 64], start=True, stop=True)
                        return ins
                    op("pe", mmU, reads=[RWsb] + [cur[h][5] for h in range(8)], writes=[RpU])
                    op("dve", lambda e, Usb=Usb, pU=pU: e.tensor_copy(out=Usb[:], in_=pU[:]), reads=[RpU], writes=[RUsb])
                    pY, RpY = pbanks.next()

                    def mmY(e, pY=pY, RT=RT, GK=GK, GB=GB, Vtm=Vtm, Usb=Usb):
                        for cc in range(4):
                            e.matmul(pY[:, cc * 128:(cc + 1) * 128], lhsT=RT[:, cc, :], rhs=Sbd[:, cc].rearrange("p h i -> p (h i)"), start=True, stop=False)
                            for h2 in range(2):
                                h = 2 * cc + h2
                                e.matmul(pY[:, h * 64:(h + 1) * 64], lhsT=GK[:, cc, h2, 1, :], rhs=Vtm[:, h * 64:(h + 1) * 64], start=False, stop=False)
                                ins = e.matmul(pY[:, h * 64:(h + 1) * 64], lhsT=GB[:, cc, h2, 1, :], rhs=Usb[:, h * 64:(h + 1) * 64], start=False, stop=True)
                        return ins
                    op("pe", mmY, reads=[RRT, RSbd, RGK, RGB, RVtm, RUsb], writes=[RpY])
                    op("act", lambda e, Ysb=Ysb, pY=pY: e.copy(out=Ysb[:], in_=pY[:]), reads=[RpY], writes=[RYsb])
                    pS, RpS = pbanks.next()

                    def mmS(e, pS=pS, Khat=Khat, Bhat=Bhat, Vtm=Vtm, Usb=Usb):
                        for cc in range(4):
                            cs = slice(cc * 128, (cc + 1) * 128)
                            e.matmul(pS[:, cs], lhsT=Khat[:, cs], rhs=Vtm[:, cs], start=True, stop=False)
                            ins = e.matmul(pS[:, cs], lhsT=Bhat[:, cs], rhs=Usb[:, cs], start=False, stop=True)
                        return ins
                    op("pe", mmS, reads=[RKhat, RBhat, RVtm, RUsb], writes=[RpS])
                    Sf = St[:].rearrange("p c h i -> p c (h i)")
                    op("dve", lambda e, pS=pS: e.tensor_tensor(out=stmp[:].rearrange("p c h i -> p (c h i)"), in0=pS[:], in1=bdm[:].rearrange("p c h i -> p (c h i)"), op=ALU.mult),
                       reads=[RpS, Rbdm], writes=[Rstmp])
                    op("dve", lambda e, gC=gC: e.tensor_tensor(out=Sf, in0=Sf, in1=bc(gC[:].unsqueeze(2), [128, 4, 128]), op=ALU.mult), reads=[RSt, RgC], writes=[RSt])
                    op("dve", lambda e: e.tensor_tensor(out=St[:].rearrange("p c h i -> p (c h i)"), in0=St[:].rearrange("p c h i -> p (c h i)"),
                                                        in1=stmp[:].rearrange("p c h i -> p (c h i)"), op=ALU.add), reads=[RSt, Rstmp], writes=[RSt])
                    op("act", lambda e: e.copy(out=Sbd[:].rearrange("p c h i -> p (c h i)"), in_=St[:].rearrange("p c h i -> p (c h i)")), reads=[RSt], writes=[RSbd])
                    if d == 0:
                        S.dma("pool", yf[s, t0:t0 + C, :], Ysb[:], reads=[RYsb], writes=[self.dres(("yf", s))])
                        S.dma("pool", bon[s, t0:t0 + C, :], bonv[:], reads=[Rbonv], writes=[self.dres(("bon", s))])
                    else:
                        yfl, Ryfl = yfls.next()
                        bon0, Rbon0 = bon0s.next()
                        S.dma("sp", yfl[:], yf[s, t0:t0 + C, :], reads=[self.dres(("yf", s))], writes=[Ryfl])
                        S.dma("sp", bon0[:], bon[s, t0:t0 + C, :], reads=[self.dres(("bon", s))], writes=[Rbon0])
                        sqv, Rsqv = sqvs.next()
                        yc, Ryc = ycs.next()
                        vv, Rvv = vvs.next()
                        st, Rst = stat.next()
                        yo, Ryo = yos.next()
                        y3 = lambda t_: t_[:].rearrange("p (h i) -> p h i", i=64)
                        b3 = lambda a_: bc(a_.unsqueeze(2), [128, 8, 64])
                        op("dve", lambda e, Ysb=Ysb, yfl=yfl: e.tensor_tensor(out=Ysb[:], in0=Ysb[:], in1=yfl[:], op=ALU.add), reads=[RYsb, Ryfl], writes=[RYsb])
                        op("pool", lambda e, sqv=sqv, Ysb=Ysb: e.tensor_tensor(out=sqv[:], in0=Ysb[:], in1=Ysb[:], op=ALU.mult), reads=[RYsb], writes=[Rsqv])
                        op("dve", lambda e, st=st, Ysb=Ysb: e.tensor_reduce(out=st[:, 0, :], in_=y3(Ysb), axis=AX.X, op=ALU.add), reads=[RYsb], writes=[Rst])
                        op("dve", lambda e, st=st, sqv=sqv: e.tensor_reduce(out=st[:, 1, :], in_=y3(sqv), axis=AX.X, op=ALU.add), reads=[Rsqv, Rst], writes=[Rst])
                        op("dve", lambda e, st=st: e.tensor_scalar(out=st[:, 2, :], in0=st[:, 0, :], scalar1=1.0 / 64, scalar2=None, op0=ALU.mult), reads=[Rst], writes=[Rst])
                        op("dve", lambda e, st=st: e.tensor_tensor(out=st[:, 3, :], in0=st[:, 2, :], in1=st[:, 2, :], op=ALU.mult), reads=[Rst], writes=[Rst])
                        op("dve", lambda e, st=st: e.scalar_tensor_tensor(out=st[:, 4, :], in0=st[:, 1, :], scalar=1.0 / 64, in1=st[:, 3, :], op0=ALU.mult, op1=ALU.subtract),
                           reads=[Rst], writes=[Rst])
                        op("act", lambda e, st=st: e.activation(out=st[:, 5, :], in_=st[:, 4, :], func=AF.Sqrt, bias=gne[:, 0:1]), reads=[Rst, Rgne], writes=[Rst])
                        op("dve", lambda e, st=st: e.reciprocal(out=st[:, 5, :], in_=st[:, 5, :]), reads=[Rst], writes=[Rst])
                        op("dve", lambda e, yc=yc, Ysb=Ysb, st=st: e.tensor_tensor(out=y3(yc), in0=y3(Ysb), in1=b3(st[:, 2, :]), op=ALU.subtract), reads=[RYsb, Rst], writes=[Ryc])
                        op("dve", lambda e, yc=yc, st=st: e.tensor_tensor(out=y3(yc), in0=y3(yc), in1=b3(st[:, 5, :]), op=ALU.mult), reads=[Ryc, Rst], writes=[Ryc])
                        op("pool", lambda e, yc=yc: e.tensor_tensor(out=yc[:], in0=yc[:], in1=gng[:], op=ALU.mult), reads=[Ryc, Rgng], writes=[Ryc])
                        op("pool", lambda e, yc=yc: e.tensor_tensor(out=yc[:], in0=yc[:], in1=gnb[:], op=ALU.add), reads=[Ryc, Rgnb], writes=[Ryc])
                        op("dve", lambda e, bonv=bonv, bon0=bon0: e.tensor_tensor(out=bonv[:], in0=bonv[:], in1=bon0[:], op=ALU.add), reads=[Rbonv, Rbon0], writes=[Rbonv])
                        op("dve", lambda e, vv=vv, Vtm=Vtm, bonv=bonv: e.tensor_tensor(out=y3(vv), in0=Vtm.rearrange("p (h i) -> p h i", i=64), in1=b3(bonv[:]), op=ALU.mult), reads=[RVtm, Rbonv], writes=[Rvv])
                        op("pool", lambda e, yc=yc, vv=vv: e.tensor_tensor(out=yc[:], in0=yc[:], in1=vv[:], op=ALU.add), reads=[Ryc, Rvv], writes=[Ryc])
                        op("dve", lambda e, yo=yo, yc=yc, gtm=gtm: e.tensor_tensor(out=yo[:], in0=yc[:], in1=gtm[:], op=ALU.mult), reads=[Ryc, Rgtm], writes=[Ryo])
                        S.dma("pool", yr[s, t0:t0 + C, :], yo[:], reads=[Ryo], writes=[self.dres(("yr", s))])

                def run_until(g, tags):
                    while True:
                        try:
                            t_ = next(g)
                        except StopIteration:
                            return None
                        if t_ in tags:
                            return t_
                order = list(order)
                gens = [body(ci) for ci in order]
                run_until(gens[0], ('B',))
                for i_ in range(len(gens)):
                    g_cur = gens[i_]
                    g_nxt = gens[i_ + 1] if i_ + 1 < len(gens) else None
                    cur_done = False
                    nxt_done = g_nxt is None
                    while not (cur_done and nxt_done):
                        if not cur_done:
                            if run_until(g_cur, ('b', 'C')) == 'C':
                                cur_done = True
                        if not nxt_done:
                            if run_until(g_nxt, ('a', 'B')) == 'B':
                                nxt_done = True
                    run_until(g_cur, ())

        self.end()

    def phase_p3(self, l):
        S, T, NS = self.S, self.T, self.NS
        self.begin()
        rows = T // 64
        nblk = T // 128
        nq, nk, nv, yn = (self.scr[k] for k in ("nq", "nk", "nv", "yn"))
        rp, Rrp = self.tile("rp", [120, 31], F32)
        S.dma("sp", rp[:], self.w["rpb"][l].rearrange("h a b -> (h a) b"), writes=[Rrp])
        _, _, idf, Ridf = self.make_ident()
        Rt, RRt = self.tile("Rt", [31, 8, 15], F32)
        Jm, RJ = self.tile("Jm", [31, 160], F32)
        tps = self.rot("tps", [128, 512], F32, 4, ps=True)
        tp0, Rtp0 = tps.next()
        S.op("pe", lambda e: e.transpose(out=tp0[0:31, 0:120], in_=rp[:, :], identity=idf[0:120, 0:120]), reads=[Rrp, Ridf], writes=[Rtp0])
        S.op("dve", lambda e: e.tensor_copy(out=Rt[:].rearrange("p h a -> p (h a)"), in_=tp0[0:31, 0:120]), reads=[Rtp0], writes=[RRt])

        S.op("pool", lambda e: e.memset(Jm[:], 0.0), writes=[RJ])
        S.op("pool", lambda e: e.affine_select(out=Jm[:], in_=Jm[:], pattern=[[-1, 160]], compare_op=ALU.not_equal, fill=1.0, base=48, channel_multiplier=1),
             reads=[RJ], writes=[RJ])
        TE0, RTE0 = self.tile("TE0", [64, 8, 15, 64], BF16)
        for q0 in range(0, 64, 4):
            tp, Rtp = tps.next()

            def mmT(e, tp=tp, q0=q0):
                for qi in range(4):
                    qc = q0 + qi
                    ins = e.matmul(tp[0:64, qi * 120:(qi + 1) * 120], lhsT=Jm[:, 63 - qc:127 - qc], rhs=Rt[:].rearrange("p h a -> p (h a)"),
                                   start=True, stop=True)
                return ins
            S.op("pe", mmT, reads=[RJ, RRt], writes=[Rtp])
            S.op("act", lambda e, tp=tp, q0=q0: e.activation(out=TE0[:, :, :, q0:q0 + 4].rearrange("p h a q -> p (h a) q"),
                                                             in_=tp[0:64, 0:480].rearrange("p (q x) -> p x q", q=4), func=AF.Exp),
                 reads=[Rtp], writes=[RTE0])
        A, RA = self.tile("mA", [128, 64], F32)
        Q, RQ = self.tile("mQ", [128, 64], F32)
        Q2, RQ2 = self.tile("mQ2", [128, 64], F32)
        cm, Rcm = self.tile("cm", [128, 64], F32)

        def io(e):
            e.iota(A[0:64, :], pattern=[[-1, 64]], base=0, channel_multiplier=1, allow_small_or_imprecise_dtypes=True)
            e.iota(A[64:128, :], pattern=[[-1, 64]], base=0, channel_multiplier=1, allow_small_or_imprecise_dtypes=True)
            return e.iota(Q[:], pattern=[[1, 64]], base=0, channel_multiplier=0, allow_small_or_imprecise_dtypes=True)
        S.op("pool", io, writes=[RA, RQ])
        S.op("dve", lambda e: e.tensor_scalar(out=Q2[:], in0=Q[:], scalar1=8.0, scalar2=56.0, op0=ALU.max, op1=ALU.min), reads=[RQ], writes=[RQ2])
        S.op("dve", lambda e: e.tensor_tensor(out=Q[:], in0=Q[:], in1=Q2[:], op=ALU.subtract), reads=[RQ, RQ2], writes=[RQ])
        S.op("dve", lambda e: e.tensor_tensor(out=A[:], in0=A[:], in1=Q[:], op=ALU.add), reads=[RA, RQ], writes=[RA])
        S.op("dve", lambda e: e.tensor_single_scalar(out=Q[:], in_=A[:], scalar=-8.0, op=ALU.is_ge), reads=[RA], writes=[RQ])
        S.op("dve", lambda e: e.tensor_single_scalar(out=Q2[:], in_=A[:], scalar=7.0, op=ALU.is_le), reads=[RA], writes=[RQ2])
        S.op("dve", lambda e: e.tensor_tensor(out=cm[:], in0=Q[:], in1=Q2[:], op=ALU.mult), reads=[RQ, RQ2], writes=[Rcm])
        TE, RTE = self.tile("TE", [128, 8, 14, 64], BF16)
        S.op("dve", lambda e: e.tensor_tensor(out=TE0[:].rearrange("p h a q -> p (h a) q"), in0=TE0[:].rearrange("p h a q -> p (h a) q"),
                                              in1=cm[0:64, :].unsqueeze(1).broadcast_to([64, 120, 64]), op=ALU.mult),
             reads=[RTE0, Rcm], writes=[RTE0])
        S.op("pool", lambda e: e.tensor_copy(out=TE[0:64, :, :, :].rearrange("p h a q -> p h (a q)"),
                                             in_=TE0[:, :, 0:14, :].rearrange("p h a q -> p h (a q)")), reads=[RTE0], writes=[RTE])
        S.dma("sp", TE[64:128, :, :, :].rearrange("p h a q -> p h (a q)"), TE0[:, :, 1:15, :].rearrange("p h a q -> p h (a q)"),
              reads=[RTE0], writes=[RTE])
        qT, RqT = self.tile("nqT", [128, 4, T], BF16)
        kT, RkT = self.tile("nkT", [128, 4, T], BF16)
        Ve, RVe = self.tile("nVe", [128, nblk, 8, 65], BF16)
        Vo, RVo = self.tile("nVo", [128, nblk, 8, 65], BF16)
        Vs, RVs = self.tile("nVs", [128, nblk // 2, 512], BF16)
        S.op("pool", lambda e: e.memset(Ve[:], 1.0), writes=[RVe])
        S.op("pool", lambda e: e.memset(Vo[:], 1.0), writes=[RVo])
        pss = Rot([(t_[:].rearrange("p (h q) -> p h q", q=64), r_) for (t_, r_) in tps.items])
        pos_raw = self.rot("ops", [128, 512], F32, 4, ps=True)
        Ets = self.rot("Et", [128, 8, 64], BF16, 3)
        Es = self.rot("E", [128, 4, 8, 64], BF16, 2)
        recs = self.rot("rec", [64, 8], F32, 2)
        outs = self.rot("yno", [64, 8, 64], BF16, 3)
        for s in range(NS):
            S.dma("sp", qT[:], nq[s].rearrange("(c p) t -> p c t", p=128), reads=[self.dres(("nq", s))], writes=[RqT])
            S.dma("sp", kT[:], nk[s].rearrange("(c p) t -> p c t", p=128), reads=[self.dres(("nk", s))], writes=[RkT])
            hb = nblk // 2
            for (Vx, RVx, off, nb_tot) in ((Ve, RVe, 0, nblk), (Vo, RVo, 64, nblk - 1)):
                for b0 in range(0, nb_tot, hb):
                    nb_ = min(hb, nb_tot - b0)
                    S.dma("sp", Vs[:, 0:nb_, :], nv[s, off + b0 * 128:off + (b0 + nb_) * 128, :].rearrange("(b p) c -> p b c", p=128),
                          reads=[self.dres(("nv", s))], writes=[RVs])
                    S.op("pool", lambda e, Vx=Vx, b0=b0, nb_=nb_: e.tensor_copy(
                        out=Vx[:, b0:b0 + nb_, :, 0:64].rearrange("p b h n -> p (b h) n"),
                        in_=Vs[:, 0:nb_, :].rearrange("p b (h n) -> p (b h) n", n=64)), reads=[RVs], writes=[RVx])
            for i in range(rows):
                rs = min(max(i - 4, 0), rows - 8)
                E, RE = Es.next()
                for blk in range(4):
                    kr = rs + 2 * blk
                    tok0 = kr * 64
                    dib = kr - i + 7
                    psA, RpsA = pss.next()
                    psB, RpsB = pss.next()
                    Et, REt = Ets.next()

                    def mms(e, psA=psA, psB=psB, tok0=tok0, i=i):
                        for h in range(8):
                            pb = (h % 2) * 64
                            ps = psA if h % 2 == 0 else psB
                            ins = e.matmul(ps[:, h // 2, :], lhsT=kT[pb:pb + 64, h // 2, tok0:tok0 + 128], rhs=qT[pb:pb + 64, h // 2, i * 64:(i + 1) * 64],
                                           start=True, stop=True)
                        return ins
                    S.op("pe", mms, reads=[RkT, RqT], writes=[RpsA, RpsB])
                    S.op("act", lambda e, psA=psA, Et=Et: e.activation(out=Et[:, 0:4, :], in_=psA[:, 0:4, :], func=AF.Exp, scale=0.125), reads=[RpsA], writes=[REt])
                    S.op("act", lambda e, psB=psB, Et=Et: e.activation(out=Et[:, 4:8, :], in_=psB[:, 0:4, :], func=AF.Exp, scale=0.125), reads=[RpsB], writes=[REt])
                    S.op("dve", lambda e, Et=Et, E=E, blk=blk, dib=dib: e.tensor_tensor(
                        out=E[:, blk, :, :].rearrange("p (two c) q -> p two c q", two=2), in0=Et[:].rearrange("p (two c) q -> p two c q", two=2),
                        in1=TE[:].rearrange("p (c two) a q -> p two c a q", two=2)[:, :, :, dib, :], op=ALU.mult),
                         reads=[REt, RTE], writes=[RE])
                poA_, RpoA = pos_raw.next()
                poB_, RpoB = pos_raw.next()
                poA = poA_[0:64, 0:260].rearrange("p (h n) -> p h n", n=65)
                poB = poB_[0:64, 0:260].rearrange("p (h n) -> p h n", n=65)

                def mmo(e, E=E, rs=rs, poA=poA, poB=poB):
                    for h in range(8):
                        po = poA if h < 4 else poB
                        for blk in range(4):
                            kr = rs + 2 * blk
                            Vx = Ve if kr % 2 == 0 else Vo
                            ins = e.matmul(po[:, h % 4, :], lhsT=E[:, blk, (h % 2) * 4 + h // 2, :], rhs=Vx[:, kr // 2, h, :], start=(blk == 0), stop=(blk == 3))
                    return ins
                S.op("pe", mmo, reads=[RE, RVe, RVo], writes=[RpoA, RpoB])
                rec, Rrec = recs.next()
                o, Ro = outs.next()
                S.op("dve", lambda e, rec=rec, poA=poA: e.reciprocal(out=rec[:, 0:4], in_=poA[:, :, 64]), reads=[RpoA], writes=[Rrec])
                S.op("dve", lambda e, rec=rec, poB=poB: e.reciprocal(out=rec[:, 4:8], in_=poB[:, :, 64]), reads=[RpoB], writes=[Rrec])
                S.op("dve", lambda e, rec=rec, poA=poA, o=o: e.tensor_tensor(out=o[:, 0:4, :], in0=poA[:, :, 0:64],
                                                                            in1=rec[:, 0:4].unsqueeze(2).broadcast_to([64, 4, 64]), op=ALU.mult),
                     reads=[RpoA, Rrec], writes=[Ro])
                S.op("dve", lambda e, rec=rec, poB=poB, o=o: e.tensor_tensor(out=o[:, 4:8, :], in0=poB[:, :, 0:64],
                                                                            in1=rec[:, 4:8].unsqueeze(2).broadcast_to([64, 4, 64]), op=ALU.mult),
                     reads=[RpoB, Rrec], writes=[Ro])
                S.dma("pool", yn[s, i * 64:(i + 1) * 64, :], o[:].rearrange("p h n -> p (h n)"), reads=[Ro], writes=[self.dres(("yn", s))])
        self.end()

    def xupdate(self, xt, Rxt, nb, aT, RaT, nkc, W, RW, pss):
        S = self.S
        for b in range(nb):
            for half in range(2):
                ps, Rps = pss.next()

                def mm(e, ps=ps, b=b, half=half):
                    for kc in range(nkc):
                        ins = e.matmul(ps[:], lhsT=aT[:, kc, b * 128:(b + 1) * 128], rhs=W[:, kc, half * 512:(half + 1) * 512],
                                       start=(kc == 0), stop=(kc == nkc - 1))
                    return ins
                S.op("pe", mm, reads=[RaT, RW], writes=[Rps])
                S.op("dve", lambda e, ps=ps, b=b, half=half: e.tensor_tensor(
                    out=xt[:, b, half * 512:(half + 1) * 512], in0=xt[:, b, half * 512:(half + 1) * 512], in1=ps[:], op=ALU.add),
                    reads=[Rps, Rxt], writes=[Rxt])

    def phase_p4(self, l):
        S, T, NS = self.S, self.T, self.NS
        self.begin()
        TT = 512
        wbr, Rwbr = self.tile("wbr", [128, 8, D], BF16)
        wout, Rwout = self.tile("wout", [128, 8, D], BF16)
        stg = self.rot("wstg", [128, 2048], F32, 2)
        ident, Rid, _, _ = self.make_ident()
        self.load_weight(wbr, Rwbr, self.w["w_br_rwkv"][l], 512, D, stg, kc0=0)
        self.load_weight(wbr, Rwbr, self.w["w_br_nat"][l], 512, D, stg, kc0=4)
        self.load_weight(wout, Rwout, self.w["w_out"][l], D, D, stg)
        xts = self.rot("xt", [128, 4, D], F32, 2)
        yts = self.rot("yt", [128, 4, 1024], BF16, 2)
        yTs = self.rot("yT", [128, 8, TT], BF16, 2)
        sgts = self.rot("sgt", [128, 16, TT], BF16, 2)
        mTs = self.rot("mT", [128, 8, TT], BF16, 2)
        m1s = self.rot("m1", [128, TT], F32, 2)
        m2s = self.rot("m2", [128, TT], F32, 2)
        pTs = self.rot("pT", [128, 1024], BF16, 2, ps=True)
        pss = self.rot("ps", [128, 512], F32, 6, ps=True)
        xsrc = self.x_in if l == 0 else self.scr["xres"]
        xres, yr, yn, sg = (self.scr[k] for k in ("xres", "yr", "yn", "sg"))
        for s in range(NS):
            for t0 in range(0, T, TT):
                xt, Rxt = xts.next()
                yt, Ryt = yts.next()
                yT, RyT = yTs.next()
                sgt, Rsgt = sgts.next()
                mT, RmT = mTs.next()
                Rx = self.dres(("xres", s, t0))
                S.dma("sp", yt[:, :, 0:512], yr[s, t0:t0 + TT, :].rearrange("(b p) c -> p b c", p=128), reads=[self.dres(("yr", s))], writes=[Ryt])
                S.dma("sp", yt[:, :, 512:1024], yn[s, t0:t0 + TT, :].rearrange("(b p) c -> p b c", p=128), reads=[self.dres(("yn", s))], writes=[Ryt])
                S.dma("sp", sgt[:], sg[s, :, t0:t0 + TT].rearrange("(c p) t -> p c t", p=128), reads=[self.dres(("sg", s))], writes=[Rsgt])
                S.dma("sp", xt[:], xsrc[s, t0:t0 + TT, :].rearrange("(b p) d -> p b d", p=128), reads=([Rx] if l > 0 else []), writes=[Rxt])
                for b in range(4):
                    pT, RpT = pTs.next()

                    def tr(e, pT=pT, b=b, yt=yt):
                        for c in range(8):
                            ins = e.transpose(out=pT[:, c * 128:(c + 1) * 128], in_=yt[:, b, c * 128:(c + 1) * 128], identity=ident[:])
                        return ins
                    S.op("pe", tr, reads=[Ryt, Rid], writes=[RpT])
                    S.op("act", lambda e, pT=pT, b=b, yT=yT: e.copy(out=yT[:, :, b * 128:(b + 1) * 128], in_=pT[:].rearrange("p (k t) -> p k t", k=8)),
                         reads=[RpT], writes=[RyT])
                for oc in range(8):
                    ps1, Rps1 = pss.next()
                    ps2, Rps2 = pss.next()
                    m1, Rm1 = m1s.next()
                    m2, Rm2 = m2s.next()

                    def mm(e, ps1=ps1, ps2=ps2, oc=oc, yT=yT):
                        for kc in range(4):
                            e.matmul(ps1[:], lhsT=wbr[:, kc, oc * 128:(oc + 1) * 128], rhs=yT[:, kc, :], start=(kc == 0), stop=(kc == 3))
                        for kc in range(4):
                            ins = e.matmul(ps2[:], lhsT=wbr[:, 4 + kc, oc * 128:(oc + 1) * 128], rhs=yT[:, 4 + kc, :], start=(kc == 0), stop=(kc == 3))
                        return ins
                    S.op("pe", mm, reads=[Rwbr, RyT], writes=[Rps1, Rps2])
                    S.op("dve", lambda e, m1=m1, ps1=ps1, oc=oc, sgt=sgt: e.tensor_tensor(out=m1[:], in0=ps1[:], in1=sgt[:, oc, :], op=ALU.mult),
                         reads=[Rps1, Rsgt], writes=[Rm1])
                    S.op("dve", lambda e, m2=m2, ps2=ps2, oc=oc, sgt=sgt: e.tensor_tensor(out=m2[:], in0=ps2[:], in1=sgt[:, 8 + oc, :], op=ALU.mult),
                         reads=[Rps2, Rsgt], writes=[Rm2])
                    S.op("pool", lambda e, m1=m1, m2=m2, oc=oc, mT=mT: e.tensor_tensor(out=mT[:, oc, :], in0=m1[:], in1=m2[:], op=ALU.add),
                         reads=[Rm1, Rm2], writes=[RmT])
                self.xupdate(xt, Rxt, 4, mT, RmT, 8, wout, Rwout, pss)
                S.dma("pool", xres[s, t0:t0 + TT, :].rearrange("(b p) d -> p b d", p=128), xt[:], reads=[Rxt], writes=[Rx])
        self.end()

    def phase_p5(self, l):
        S, T, NS = self.S, self.T, self.NS
        self.begin()
        TT = 512
        wq, Rwq = self.tile("wq", [128, 8, D], BF16)
        wo, Rwo = self.tile("wo", [128, 8, D], BF16)
        wkv, Rwkv = self.tile("wkv", [128, 8, 2 * D], BF16)
        stg = self.rot("wstg", [128, 2048], F32, 2)
        ident, Rid, _, _ = self.make_ident()
        gb, Rgb = self.tile("gb", [128, D], F32)
        gm, Rgm = self.tile("gm", [128, D], F32)
        ones, Rones = self.tile("ones", [128, 128], BF16)
        S.op("pool", lambda e: e.memset(ones[:], 1.0), writes=[Rones])
        self.load_bcast(gb, Rgb, self.w["norm_x"][l], D)
        self.load_bcast(gm, Rgm, self.w["norm_mem"][l], D)
        self.load_weight(wkv, Rwkv, self.w["w_xkv"][l], D, 2 * D, stg)
        self.load_weight(wq, Rwq, self.w["w_xq"][l], D, D, stg)
        self.load_weight(wo, Rwo, self.w["w_xo"][l], D, D, stg)
        tmp = self.norm_tmp()
        xts = self.rot("xt", [128, 4, D], F32, 2)
        hTs = self.rot("hT", [128, 8, TT], BF16, 2)
        qTs = self.rot("qT", [128, 8, TT], BF16, 1)
        oTs = self.rot("oT", [128, 8, TT], BF16, 2)
        Es = self.rot("E", [128, 2, TT], BF16, 2)
        rdens = self.rot("rden", [128, TT], F32, 2)
        mt, Rmt = self.tile("memt", [128, 2, D], F32)
        mnT, RmnT = self.tile("memnT", [128, 8, NMEM], BF16)
        kT, RkT = self.tile("kT", [128, 8, NMEM], BF16)
        Vm, RVm = self.tile("Vm", [128, 2, D], BF16)
        pss = self.rot("ps", [128, 512], F32, 6, ps=True)
        xres = self.scr["xres"]
        for s in range(NS):
            S.dma("sp", mt[:], self.mem_in[s].rearrange("(b p) d -> p b d", p=128), writes=[Rmt])
            for b in range(2):
                self.norm_to_hT(mt, Rmt, b, gm, Rgm, mnT, RmnT, b * 128, ident, Rid, tmp)
            for cc in range(8):
                ps, Rps = pss.next()

                def mmk(e, ps=ps, cc=cc):
                    for kc in range(8):
                        ins = e.matmul(ps[:, 0:NMEM], lhsT=wkv[:, kc, cc * 128:(cc + 1) * 128], rhs=mnT[:, kc, :], start=(kc == 0), stop=(kc == 7))
                    return ins
                S.op("pe", mmk, reads=[Rwkv, RmnT], writes=[Rps])
                S.op("dve", lambda e, ps=ps, cc=cc: e.tensor_copy(out=kT[:, cc, :], in_=ps[:, 0:NMEM]), reads=[Rps], writes=[RkT])
            for mb in range(2):
                for half in range(2):
                    ps, Rps = pss.next()

                    def mmv(e, ps=ps, mb=mb, half=half):
                        for kc in range(8):
                            ins = e.matmul(ps[:], lhsT=mnT[:, kc, mb * 128:(mb + 1) * 128], rhs=wkv[:, kc, D + half * 512:D + (half + 1) * 512],
                                           start=(kc == 0), stop=(kc == 7))
                        return ins
                    S.op("pe", mmv, reads=[Rwkv, RmnT], writes=[Rps])
                    S.op("dve", lambda e, ps=ps, mb=mb, half=half: e.tensor_copy(out=Vm[:, mb, half * 512:(half + 1) * 512], in_=ps[:]),
                         reads=[Rps], writes=[RVm])
            for t0 in range(0, T, TT):
                xt, Rxt = xts.next()
                hT, RhT = hTs.next()
                qT, RqT = qTs.next()
                oT, RoT = oTs.next()
                Rx = self.dres(("xres", s, t0))
                S.dma("sp", xt[:], xres[s, t0:t0 + TT, :].rearrange("(b p) d -> p b d", p=128), reads=[Rx], writes=[Rxt])
                for b in range(4):
                    self.norm_to_hT(xt, Rxt, b, gb, Rgb, hT, RhT, b * 128, ident, Rid, tmp)
                for cc in range(8):
                    ps, Rps = pss.next()

                    def mmq(e, ps=ps, cc=cc, hT=hT):
                        for kc in range(8):
                            ins = e.matmul(ps[:], lhsT=wq[:, kc, cc * 128:(cc + 1) * 128], rhs=hT[:, kc, :], start=(kc == 0), stop=(kc == 7))
                        return ins
                    S.op("pe", mmq, reads=[Rwq, RhT], writes=[Rps])
                    S.op("dve", lambda e, ps=ps, cc=cc, qT=qT: e.tensor_copy(out=qT[:, cc, :], in_=ps[:]), reads=[Rps], writes=[RqT])
                for hd in range(4):
                    E, RE = Es.next()
                    rden, Rrden = rdens.next()
                    for mb in range(2):
                        ps, Rps = pss.next()

                        def mms(e, ps=ps, mb=mb, hd=hd, qT=qT):
                            for j in range(2):
                                ins = e.matmul(ps[:], lhsT=kT[:, 2 * hd + j, mb * 128:(mb + 1) * 128], rhs=qT[:, 2 * hd + j, :], start=(j == 0), stop=(j == 1))
                            return ins
                        S.op("pe", mms, reads=[RkT, RqT], writes=[Rps])
                        S.op("act", lambda e, ps=ps, mb=mb, E=E: e.activation(out=E[:, mb, :], in_=ps[:], func=AF.Exp, scale=1.0 / 16.0),
                             reads=[Rps], writes=[RE])
                    ps, Rps = pss.next()

                    def mmd(e, ps=ps, E=E):
                        for mb in range(2):
                            ins = e.matmul(ps[:], lhsT=ones[:], rhs=E[:, mb, :], start=(mb == 0), stop=(mb == 1))
                        return ins
                    S.op("pe", mmd, reads=[Rones, RE], writes=[Rps])
                    S.op("dve", lambda e, ps=ps, rden=rden: e.reciprocal(out=rden[:], in_=ps[:]), reads=[Rps], writes=[Rrden])
                    for j in range(2):
                        ps, Rps = pss.next()

                        def mmo(e, ps=ps, hd=hd, j=j, E=E):
                            for mb in range(2):
                                c0 = hd * 256 + j * 128
                                ins = e.matmul(ps[:], lhsT=Vm[:, mb, c0:c0 + 128], rhs=E[:, mb, :], start=(mb == 0), stop=(mb == 1))
                            return ins
                        S.op("pe", mmo, reads=[RVm, RE], writes=[Rps])
                        S.op("dve", lambda e, ps=ps, hd=hd, j=j, oT=oT, rden=rden: e.tensor_tensor(out=oT[:, 2 * hd + j, :], in0=ps[:], in1=rden[:], op=ALU.mult),
                             reads=[Rps, Rrden], writes=[RoT])
                self.xupdate(xt, Rxt, 4, oT, RoT, 8, wo, Rwo, pss)
                S.dma("pool", xres[s, t0:t0 + TT, :].rearrange("(b p) d -> p b d", p=128), xt[:], reads=[Rxt], writes=[Rx])
        self.end()

    def phase_p6(self, l):
        S, T, NS = self.S, self.T, self.NS
        self.begin()
        TT = 256
        NB = TT // 128
        last = (l == self.DEPTH - 1)
        w1, Rw1 = self.tile("w1", [128, 8, DFF], BF16)
        w2, Rw2 = self.tile("w2", [128, 32, D], BF16)
        stg = self.rot("wstg", [128, 2048], F32, 2)
        ident, Rid, _, _ = self.make_ident()
        gb, Rgb = self.tile("gb", [128, D], F32)
        self.load_bcast(gb, Rgb, self.w["norm_ff"][l], D)
        if last:
            gf, Rgf = self.tile("gf", [128, D], F32)
            self.load_bcast(gf, Rgf, self.w["norm_final"], D)
        self.load_weight(w1, Rw1, self.w["w_ff1"][l], D, DFF, stg)
        self.load_weight(w2, Rw2, self.w["w_ff2"][l], DFF, D, stg)
        tmp = self.norm_tmp()
        xts = self.rot("xt", [128, NB, D], F32, 2)
        hTs = self.rot("hT", [128, 8, TT], BF16, 2)
        uTs = self.rot("uT", [128, 32, TT], BF16, 1)
        rls = self.rot("rl", [128, TT], BF16, 3)
        pss = self.rot("ps", [128, 512], F32, 6, ps=True)
        xres = self.scr["xres"]
        for s in range(NS):
            for t0 in range(0, T, TT):
                xt, Rxt = xts.next()
                hT, RhT = hTs.next()
                uT, RuT = uTs.next()
                Rx = self.dres(("xres", s, t0 // 512 * 512))
                S.dma("sp", xt[:], xres[s, t0:t0 + TT, :].rearrange("(b p) d -> p b d", p=128), reads=[Rx], writes=[Rxt])
                for b in range(NB):
                    self.norm_to_hT(xt, Rxt, b, gb, Rgb, hT, RhT, b * 128, ident, Rid, tmp)
                for fc in range(32):
                    ps, Rps = pss.next()
                    rl, Rrl = rls.next()

                    def mm1(e, ps=ps, fc=fc, hT=hT):
                        for kc in range(8):
                            ins = e.matmul(ps[:, 0:TT], lhsT=w1[:, kc, fc * 128:(fc + 1) * 128], rhs=hT[:, kc, :], start=(kc == 0), stop=(kc == 7))
                        return ins
                    S.op("pe", mm1, reads=[Rw1, RhT], writes=[Rps])
                    S.op("act", lambda e, ps=ps, rl=rl: e.activation(out=rl[:], in_=ps[:, 0:TT], func=AF.Relu), reads=[Rps], writes=[Rrl])
                    S.op("pool", lambda e, rl=rl, fc=fc, uT=uT: e.tensor_tensor(out=uT[:, fc, :], in0=rl[:], in1=rl[:], op=ALU.mult),
                         reads=[Rrl], writes=[RuT])
                self.xupdate(xt, Rxt, NB, uT, RuT, 32, w2, Rw2, pss)
                if not last:
                    S.dma("pool", xres[s, t0:t0 + TT, :].rearrange("(b p) d -> p b d", p=128), xt[:], reads=[Rxt], writes=[Rx])
                else:
                    for b in range(NB):
                        junk, Rjunk = tmp["junk"].next()
                        ss, Rss = tmp["ss"].next()
                        S.op("act", lambda e, junk=junk, ss=ss, b=b, xt=xt: e.activation(out=junk[:], in_=xt[:, b, :], func=AF.Square, accum_out=ss[:, 0:1]),
                             reads=[Rxt], writes=[Rjunk, Rss])
                        S.op("act", lambda e, ss=ss: e.activation(out=ss[:, 1:2], in_=ss[:, 0:1], func=AF.Sqrt, scale=1.0 / D, bias=self.eps_t[:, 0:1]),
                             reads=[Rss, self.Reps], writes=[Rss])
                        S.op("dve", lambda e, ss=ss: e.reciprocal(out=ss[:, 2:3], in_=ss[:, 1:2]), reads=[Rss], writes=[Rss])
                        S.op("dve", lambda e, ss=ss, b=b, xt=xt: e.scalar_tensor_tensor(out=xt[:, b, :], in0=xt[:, b, :], scalar=ss[:, 2:3], in1=gf[:],
                                                                                         op0=ALU.mult, op1=ALU.mult),
                             reads=[Rxt, Rss, Rgf], writes=[Rxt])
                    Ry = self.dres(("y", s))
                    S.dma("pool", self.y_out[s, t0:t0 + TT, :].rearrange("(b p) d -> p b d", p=128), xt[:], reads=[Rxt], writes=[Ry])
        self.end()


_CACHE = {}


def _get_nc(T, NS, DEPTH, dbg=(), stop=None):
    key = (T, NS, DEPTH, tuple(dbg), stop)
    if key not in _CACHE:
        _CACHE[key] = Builder(T, NS, DEPTH, dbg, stop).build()
    return _CACHE[key]


def kernel(**inputs):
    NCORES, NS, T, DEPTH = 8, 2, 4096, 2
    xs = np.concatenate([np.asarray(inputs["x_prompt"], np.float32), np.asarray(inputs["x_sample"], np.float32)], 0)
    ms = np.concatenate([np.asarray(inputs["mem_prompt"], np.float32), np.asarray(inputs["mem_sample"], np.float32)], 0)
    nseq = xs.shape[0]
    slots = [[c, 8 + c if 8 + c < nseq else c] for c in range(NCORES)]
    wnames = [n for n, _ in WEIGHT_SPECS] + ["norm_final"]
    wmap = {n: np.ascontiguousarray(np.asarray(inputs[n], np.float32)) for n in wnames}
    nc = _get_nc(T, NS, DEPTH)
    in_maps = []
    for c in range(NCORES):
        m = dict(wmap)
        m["x"] = np.ascontiguousarray(xs[slots[c]])
        m["mem"] = np.ascontiguousarray(ms[slots[c]])
        in_maps.append(m)
    res = run_bass_kernel_spmd(nc, in_maps, core_ids=list(range(NCORES)))
    y = np.zeros_like(xs)
    for c in range(NCORES):
        yc = res.results[c]["y"]
        y[slots[c][0]] = yc[0]
        if slots[c][1] != slots[c][0]:
            y[slots[c][1]] = yc[1]
    nb = np.asarray(inputs["x_prompt"]).shape[0]
    return (y[:nb], y[nb:])
```

```python
import numpy as np
from contextlib import ExitStack
import concourse.bass as bass
import concourse.mybir as mybir
from concourse.bass_utils import run_bass_kernel_spmd

F32 = mybir.dt.float32
BF16 = mybir.dt.bfloat16
AF = mybir.ActivationFunctionType
ALU = mybir.AluOpType
AX = mybir.AxisListType

D = 1024
DIN = 5504
RW = 1920
NQ0 = 1920
NV0 = 2944
G0 = 3456
NMEM = 256
DFF = 4096
EPS = 1e-6
GN_EPS = 1e-5 * 64
CH = 128

EPOCH = 12000
ENGS = ("pe", "dve", "act", "pool", "sp")


class Res:
    __slots__ = ("name", "last_w", "readers", "dsem", "w_is_dma")

    def __init__(self, name=""):
        self.name = name
        self.last_w = None
        self.readers = []
        self.dsem = None
        self.w_is_dma = False


class Sched:
    def __init__(self, nc, stack):
        self.nc = nc
        self.stack = stack
        self.ops = {e: [] for e in ENGS}
        self.cnt = {e: 0 for e in ENGS}
        self.esems = {e: [] for e in ENGS}
        self.known = {e: {} for e in ENGS}
        self.sems = {}
        self.nsem = 0
        self.same_engine_raw = True
        self.dma_latest = {}
        self.free_dsems = []

    def new_sem(self, tag):
        h = self.stack.enter_context(self.nc.semaphore(f"{tag}_{self.nsem}"))
        sid = self.nsem
        self.nsem += 1
        self.sems[sid] = h
        return sid

    def _eng_point(self, eng):
        n = self.cnt[eng]
        self.cnt[eng] = n + 1
        ep, v = divmod(n, EPOCH)
        while len(self.esems[eng]) <= ep:
            self.esems[eng].append(self.new_sem(f"e{eng}"))
        return (self.esems[eng][ep], v + 1)

    def _dma_point(self, res):
        d = res.dsem
        if d is None or d[1] + 16 > EPOCH:
            if self.free_dsems and d is None:
                d = self.free_dsems.pop()
            else:
                d = [self.new_sem("d"), 0]
            res.dsem = d
        d[1] += 16
        self.dma_latest[d[0]] = d[1]
        return (d[0], d[1])

    def recycle(self, resources):
        seen = set()
        for r in resources:
            d = r.dsem
            if d is not None and id(d) not in seen and d[1] + 64 < EPOCH:
                seen.add(id(d))
                self.free_dsems.append(d)
            r.dsem = None

    def _need(self, eng, waits, pt):
        sid, val = pt
        if self.known[eng].get(sid, 0) >= val:
            return
        if waits.get(sid, 0) < val:
            waits[sid] = val

    def _deps(self, eng, reads, writes, is_dma=False):
        waits = {}
        for r in reads:
            if r.last_w is not None:
                w_eng = r.last_w[2]
                if w_eng != eng or is_dma or (self.same_engine_raw and eng != "pe"):
                    self._need(eng, waits, r.last_w[:2])
        for w in writes:
            if w.last_w is not None:
                w_eng = w.last_w[2]
                same_dma_group = is_dma and w.w_is_dma and not w.readers
                if (w_eng != eng or is_dma) and not same_dma_group:
                    self._need(eng, waits, w.last_w[:2])
            for (sid, val, r_eng) in w.readers:
                if r_eng != eng or is_dma:
                    self._need(eng, waits, (sid, val))
        for sid, val in waits.items():
            self.known[eng][sid] = val
        return list(waits.items())

    def op(self, eng, fn, reads=(), writes=()):
        waits = self._deps(eng, reads, writes)
        pt = self._eng_point(eng)
        self.ops[eng].append((waits, fn, pt[0], 1))
        for r in reads:
            r.readers.append((pt[0], pt[1], eng))
        for w in writes:
            w.last_w = (pt[0], pt[1], eng)
            w.readers = []
            w.w_is_dma = False
        return pt

    def dma(self, queue, out, in_, reads=(), writes=(), **kw):
        waits = self._deps(queue, reads, writes, is_dma=True)
        pt = self._dma_point(writes[0])

        def fn(e, out=out, in_=in_, kw=kw):
            return e.dma_start(out=out, in_=in_, **kw)
        self.ops[queue].append((waits, fn, pt[0], 16))
        for r in reads:
            r.readers.append((pt[0], pt[1], "dma"))
        for w in writes:
            w.last_w = (pt[0], pt[1], "dma")
            w.readers = []
            w.w_is_dma = True
            if w is not writes[0]:
                w.dsem = writes[0].dsem
        return pt

    def final_wait(self, eng, resources):
        waits = {}
        for r in resources:
            if r.last_w is not None:
                sid, val = r.last_w[:2]
                waits[sid] = max(waits.get(sid, 0), val)
        self.ops[eng].append((list(waits.items()), None, None, 0))

    def barrier(self):
        pts = {}
        for e in ENGS:
            n = self.cnt[e]
            if n > 0:
                ep, v = divmod(n - 1, EPOCH)
                pts[self.esems[e][ep]] = v + 1
        for sid, val in self.dma_latest.items():
            pts[sid] = max(pts.get(sid, 0), val)
        for e in ENGS:
            waits = []
            for sid, val in pts.items():
                if self.known[e].get(sid, 0) < val:
                    waits.append((sid, val))
                    self.known[e][sid] = val
            self.ops[e].append((waits, None, None, 0))

    def emit(self):
        nc = self.nc
        sems = self.sems

        def run(engname):
            def body(e):
                for (waits, fn, sid_, inc) in self.ops[engname]:
                    for sid, val in waits:
                        e.wait_ge(sems[sid], val)
                    if fn is None:
                        continue
                    ins = fn(e)
                    ins.then_inc(sems[sid_], inc)
            return body

        with nc.Block() as block:
            block.tensor(run("pe"))
            block.vector(run("dve"))
            block.scalar(run("act"))
            block.gpsimd(run("pool"))
            block.sync(run("sp"))
        self.ops = {e: [] for e in ENGS}


class Rot:
    def __init__(self, items):
        self.items = items
        self.i = 0

    def next(self):
        it = self.items[self.i % len(self.items)]
        self.i += 1
        return it


WEIGHT_SPECS = [
    ("norm_mix", (D,)), ("w_in", (D, DIN)), ("mu_prev", (RW,)), ("mu_next", (RW,)),
    ("w0", (2, 512)), ("w_up", (2, 64, 512)), ("a0", (2, 512)), ("a_up", (2, 64, 512)),
    ("g_up", (128, 512)), ("k_k", (512,)), ("k_a", (512,)), ("r_k", (8, 64)),
    ("gn_g", (512,)), ("gn_b", (512,)), ("rpb", (8, 15, 31)),
    ("w_br_rwkv", (512, D)), ("w_br_nat", (512, D)), ("w_out", (D, D)),
    ("norm_x", (D,)), ("norm_mem", (D,)), ("w_xq", (D, D)), ("w_xkv", (D, 2 * D)), ("w_xo", (D, D)),
    ("norm_ff", (D,)), ("w_ff1", (D, DFF)), ("w_ff2", (DFF, D)),
]


class Builder:
    def __init__(self, T, NS, DEPTH, dbg=(), stop=None, ext_in=(), phases=None):
        self.T, self.NS, self.DEPTH = T, NS, DEPTH
        self.ext_in = set(ext_in)
        self.phases = phases or ("p1", "p3", "p2", "p4", "p5", "p6")
        self.dbg = set(dbg)
        self.stop = stop
        self.nc = bass.Bass("TRN2", target_bir_lowering=False)
        nc = self.nc
        self.x_in = nc.dram_tensor("x", [NS, T, D], F32, kind="ExternalInput").ap()
        self.mem_in = nc.dram_tensor("mem", [NS, NMEM, D], F32, kind="ExternalInput").ap()
        self.w = {}
        for name, shp in WEIGHT_SPECS:
            self.w[name] = nc.dram_tensor(name, [DEPTH] + list(shp), F32, kind="ExternalInput").ap()
        self.w["norm_final"] = nc.dram_tensor("norm_final", [D], F32, kind="ExternalInput").ap()
        self.y_out = nc.dram_tensor("y", [NS, T, D], F32, kind="ExternalOutput").ap()
        self.scr = {}
        self.scr_res = {}

    def scratch(self, name, shape, dt):
        kind = "ExternalOutput" if name in self.dbg else ("ExternalInput" if name in self.ext_in else "Internal")
        t = self.nc.dram_tensor(name, shape, dt, kind=kind).ap()
        self.scr[name] = t
        return t

    def dres(self, key):
        r = self.scr_res.get(key)
        if r is None:
            r = Res(str(key))
            self.scr_res[key] = r
        return r

    def build(self):
        nc = self.nc
        T, NS = self.T, self.NS
        self.scratch("xres", [NS, T, D], F32)
        self.scratch("pf", [NS, RW, T], F32)
        self.scratch("nq", [NS, 512, T], BF16)
        self.scratch("nk", [NS, 512, T], BF16)
        self.scratch("nv", [NS, T, 512], BF16)
        self.scratch("sg", [NS, 2048, T], BF16)
        self.scratch("yr", [NS, T, 512], BF16)
        self.scratch("yn", [NS, T, 512], BF16)
        self.scratch("yf", [NS, T, 512], F32)
        self.scratch("bon", [NS, T, 8], F32)
        self.scratch("phd", [NS, T // 128, 128, 1920], F32)
        with ExitStack() as st:
            self.S = Sched(nc, st)
            done = False
            for l in range(self.DEPTH):
                for ph in self.phases:
                    getattr(self, "phase_" + ph)(l)
                    if self.stop == (l, ph):
                        done = True
                        break
                if done:
                    break
            with ExitStack() as ph:
                self.S.final_wait("pool", list(self.scr_res.values()))
                self.S.emit()
        return nc

    def begin(self):
        self.S.barrier()
        self.phc = getattr(self, "phc", 0) + 1
        self.ph = ExitStack()
        self.ph_res = []
        return self.ph

    def end(self):
        self.S.emit()
        self.S.recycle(self.ph_res)
        self.ph.close()

    def tile(self, name, shape, dt):
        t = self.ph.enter_context(self.nc.sbuf_tensor(f"{name}_{self.phc}", shape, dt))
        r = Res(name)
        self.ph_res.append(r)
        return t, r

    def psum(self, name, shape, dt):
        t = self.ph.enter_context(self.nc.psum_tensor(f"{name}_{self.phc}", shape, dt))
        r = Res(name)
        self.ph_res.append(r)
        return t, r

    def rot(self, name, shape, dt, n, ps=False):
        f = self.psum if ps else self.tile
        return Rot([f(f"{name}{i}", shape, dt) for i in range(n)])

    def make_ident(self):
        S = self.S
        idf, Ridf = self.tile("identf", [128, 128], F32)
        idb, Ridb = self.tile("identb", [128, 128], BF16)

        S.op("pool", lambda e: e.memset(idf[:], 0.0), writes=[Ridf])
        S.op("pool", lambda e: e.affine_select(out=idf[:], in_=idf[:], pattern=[[-1, 128]], compare_op=ALU.not_equal,
                                               fill=1.0, base=0, channel_multiplier=1), reads=[Ridf], writes=[Ridf])
        S.op("dve", lambda e: e.tensor_copy(out=idb[:], in_=idf[:]), reads=[Ridf], writes=[Ridb])
        return idb, Ridb, idf, Ridf

    def load_weight(self, dst, Rdst, src2d, K, cols, stg, col0=0, kc0=0):
        S = self.S
        nk = K // 128
        srcv = src2d.rearrange("(kc p) c -> p kc c", p=128)
        engs = ("pool", "dve", "act")
        i = 0
        for kc in range(nk):
            for c0 in range(0, cols, 2048):
                cw = min(2048, cols - c0)
                stile, Rst = stg.next()
                S.dma("sp", stile[:, 0:cw], srcv[:, kc, c0:c0 + cw], writes=[Rst])
                eng = engs[i % 3]
                i += 1
                o = dst[:, kc0 + kc, col0 + c0:col0 + c0 + cw]
                if eng == "act":
                    S.op("act", lambda e, o=o, s=stile, cw=cw: e.copy(out=o, in_=s[:, 0:cw]), reads=[Rst], writes=[Rdst])
                else:
                    S.op(eng, lambda e, o=o, s=stile, cw=cw: e.tensor_copy(out=o, in_=s[:, 0:cw]), reads=[Rst], writes=[Rdst])

    def load_bcast(self, dst, Rdst, src1d, n):
        self.S.dma("sp", dst[:, 0:n], src1d.rearrange("(o n) -> o n", o=1).partition_broadcast(128), writes=[Rdst])

    def norm_to_hT(self, xt, Rxt, b, gb, Rgb, hT, RhT, tcol, ident, Rid, tmp):
        S = self.S
        junk, Rjunk = tmp["junk"].next()
        ss, Rss = tmp["ss"].next()
        h, Rh = tmp["h"].next()
        pT, RpT = tmp["pT"].next()
        S.op("act", lambda e: e.activation(out=junk[:], in_=xt[:, b, :], func=AF.Square, accum_out=ss[:, 0:1]),
             reads=[Rxt], writes=[Rjunk, Rss])

        S.op("act", lambda e: e.activation(out=ss[:, 1:2], in_=ss[:, 0:1], func=AF.Sqrt, scale=1.0 / D, bias=self.eps_t[:, 0:1]),
             reads=[Rss, self.Reps], writes=[Rss])
        S.op("dve", lambda e: e.reciprocal(out=ss[:, 2:3], in_=ss[:, 1:2]), reads=[Rss], writes=[Rss])
        S.op("dve", lambda e: e.scalar_tensor_tensor(out=h[:], in0=xt[:, b, :], scalar=ss[:, 2:3], in1=gb[:],
                                                     op0=ALU.mult, op1=ALU.mult),
             reads=[Rxt, Rss, Rgb], writes=[Rh])

        def tr(e):
            for kc in range(8):
                ins = e.transpose(out=pT[:, kc * 128:(kc + 1) * 128], in_=h[:, kc * 128:(kc + 1) * 128], identity=ident[:])
            return ins
        S.op("pe", tr, reads=[Rh, Rid], writes=[RpT])
        S.op("act", lambda e: e.copy(out=hT[:, :, tcol:tcol + 128], in_=pT[:].rearrange("p (k t) -> p k t", k=8)),
             reads=[RpT], writes=[RhT])
        return ss

    def norm_tmp(self):
        self.eps_t, Re = self.tile("eps_t", [128, 2], F32)
        eps_t = self.eps_t
        self.S.op("pool", lambda e: e.memset(eps_t[:], EPS), writes=[Re])
        self.Reps = Re
        return {
            "junk": self.rot("junk", [128, D], BF16, 1),
            "ss": self.rot("ss", [128, 4], F32, 4),
            "h": self.rot("h", [128, D], BF16, 2),
            "pT": self.rot("pT", [128, 1024], BF16, 2, ps=True),
        }

    def phase_p1(self, l):
        S, T, NS = self.S, self.T, self.NS
        self.begin()
        TT = 512
        win, Rwin = self.tile("win", [128, 8, DIN], BF16)
        stg = self.rot("wstg", [128, 2048], F32, 2)
        gb, Rgb = self.tile("gb", [128, D], F32)
        ident, Rid, _, _ = self.make_ident()
        self.load_bcast(gb, Rgb, self.w["norm_mix"][l], D)
        self.load_weight(win, Rwin, self.w["w_in"][l], D, DIN, stg)
        tmp = self.norm_tmp()
        xts = self.rot("xt", [128, 4, D], F32, 2)
        hTs = self.rot("hT", [128, 8, TT], BF16, 2)
        pss = self.rot("ps", [128, 512], F32, 4, ps=True)
        of32 = self.rot("of32", [128, 512], F32, 3)
        obf = self.rot("obf", [128, 512], BF16, 4)
        xsrc = self.x_in if l == 0 else self.scr["xres"]
        pf, nq, nk, nv, sg = (self.scr[k] for k in ("pf", "nq", "nk", "nv", "sg"))
        ev = 0
        for s in range(NS):
            for t0 in range(0, T, TT):
                xt, Rxt = xts.next()
                hT, RhT = hTs.next()
                xr = [self.dres(("xres", s, t0))] if l > 0 else []
                S.dma("sp", xt[:], xsrc[s, t0:t0 + TT, :].rearrange("(b p) d -> p b d", p=128), reads=xr, writes=[Rxt])
                for b in range(4):
                    self.norm_to_hT(xt, Rxt, b, gb, Rgb, hT, RhT, b * 128, ident, Rid, tmp)
                for cc in range(DIN // 128):
                    c0 = cc * 128
                    if NV0 <= c0 < G0:
                        continue
                    ps, Rps = pss.next()

                    def mm(e, ps=ps, c0=c0, hT=hT):
                        for kc in range(8):
                            ins = e.matmul(ps[:], lhsT=win[:, kc, c0:c0 + 128], rhs=hT[:, kc, :], start=(kc == 0), stop=(kc == 7))
                        return ins
                    S.op("pe", mm, reads=[Rwin, RhT], writes=[Rps])
                    if c0 < RW:
                        o, Ro = of32.next()
                        eng = "dve" if ev % 2 == 0 else "act"
                        ev += 1
                        if eng == "dve":
                            S.op("dve", lambda e, o=o, ps=ps: e.tensor_copy(out=o[:], in_=ps[:]), reads=[Rps], writes=[Ro])
                        else:
                            S.op("act", lambda e, o=o, ps=ps: e.copy(out=o[:], in_=ps[:]), reads=[Rps], writes=[Ro])
                        S.dma("pool", pf[s, c0:c0 + 128, t0:t0 + TT], o[:], reads=[Ro], writes=[self.dres(("pf", s))])
                    elif c0 < NV0:
                        o, Ro = obf.next()
                        S.op("dve", lambda e, o=o, ps=ps: e.tensor_copy(out=o[:], in_=ps[:]), reads=[Rps], writes=[Ro])
                        if c0 < NQ0 + 512:
                            S.dma("pool", nq[s, c0 - NQ0:c0 - NQ0 + 128, t0:t0 + TT], o[:], reads=[Ro], writes=[self.dres(("nq", s))])
                        else:
                            c1 = c0 - NQ0 - 512
                            S.dma("pool", nk[s, c1:c1 + 128, t0:t0 + TT], o[:], reads=[Ro], writes=[self.dres(("nk", s))])
                    else:
                        o, Ro = obf.next()
                        S.op("act", lambda e, o=o, ps=ps: e.activation(out=o[:], in_=ps[:], func=AF.Sigmoid), reads=[Rps], writes=[Ro])
                        c1 = c0 - G0
                        S.dma("pool", sg[s, c1:c1 + 128, t0:t0 + TT], o[:], reads=[Ro], writes=[self.dres(("sg", s))])
                for b in range(4):
                    ps, Rps = pss.next()

                    def mmv(e, ps=ps, b=b, hT=hT):
                        for kc in range(8):
                            ins = e.matmul(ps[:], lhsT=hT[:, kc, b * 128:(b + 1) * 128], rhs=win[:, kc, NV0:NV0 + 512], start=(kc == 0), stop=(kc == 7))
                        return ins
                    S.op("pe", mmv, reads=[Rwin, RhT], writes=[Rps])
                    o, Ro = obf.next()
                    S.op("dve", lambda e, o=o, ps=ps: e.tensor_copy(out=o[:], in_=ps[:]), reads=[Rps], writes=[Ro])
                    S.dma("pool", nv[s, t0 + b * 128:t0 + (b + 1) * 128, :], o[:], reads=[Ro], writes=[self.dres(("nv", s))])
        self.end()

    def phase_p2(self, l):
        S, T, NS = self.S, self.T, self.NS
        self.begin()
        C = 128
        NCH = T // C
        LD = 0.6065306597126334
        pf, yf, yr = self.scr["pf"], self.scr["yf"], self.scr["yr"]
        bon = self.scr["bon"]
        phd = self.scr["phd"]
        op = S.op
        bc = lambda ap, shape: ap.broadcast_to(shape)
        identb, Ridb, identf, Ridf = self.make_ident()
        mup, Rmup = self.tile("mup", [128, 15], F32)
        mun, Rmun = self.tile("mun", [128, 15], F32)
        w0t, Rw0 = self.tile("w0t", [128, 2, 4], F32)
        a0t, Ra0 = self.tile("a0t", [128, 2, 4], F32)
        kkc, Rkkc = self.tile("kkc", [128, 4], F32)
        kac, Rkac = self.tile("kac", [128, 4], F32)
        omka, Romka = self.tile("omka", [128, 4], F32)
        rkc, Rrkc = self.tile("rkc", [128, 4], F32)
        S.dma("sp", mup[:], self.w["mu_prev"][l].rearrange("(c p) -> p c", p=128), writes=[Rmup], allow_slow_non_contiguous=True)
        S.dma("sp", mun[:], self.w["mu_next"][l].rearrange("(c p) -> p c", p=128), writes=[Rmun], allow_slow_non_contiguous=True)
        for d in range(2):
            S.dma("sp", w0t[:, d, :], self.w["w0"][l, d].rearrange("(c p) -> p c", p=128), writes=[Rw0], allow_slow_non_contiguous=True)
            S.dma("sp", a0t[:, d, :], self.w["a0"][l, d].rearrange("(c p) -> p c", p=128), writes=[Ra0], allow_slow_non_contiguous=True)
        S.dma("sp", kkc[:], self.w["k_k"][l].rearrange("(c p) -> p c", p=128), writes=[Rkkc], allow_slow_non_contiguous=True)
        S.dma("sp", kac[:], self.w["k_a"][l].rearrange("(c p) -> p c", p=128), writes=[Rkac], allow_slow_non_contiguous=True)
        S.dma("sp", rkc[:], self.w["r_k"][l].rearrange("h n -> (h n)").rearrange("(c p) -> p c", p=128), writes=[Rrkc], allow_slow_non_contiguous=True)
        op("dve", lambda e: e.tensor_scalar(out=omka[:], in0=kac[:], scalar1=-1.0, scalar2=1.0, op0=ALU.mult, op1=ALU.add), reads=[Rkac], writes=[Romka])
        gng, Rgng = self.tile("gng", [128, 512], F32)
        gnb, Rgnb = self.tile("gnb", [128, 512], F32)
        self.load_bcast(gng, Rgng, self.w["gn_g"][l], 512)
        self.load_bcast(gnb, Rgnb, self.w["gn_b"][l], 512)
        wstg, Rwstg = self.tile("lstg", [128, 3, 512], F32)
        S.dma("sp", wstg[:, 0, :], self.w["w_up"][l].rearrange("d l c -> (d l) c"), writes=[Rwstg])
        S.dma("sp", wstg[:, 1, :], self.w["a_up"][l].rearrange("d l c -> (d l) c"), writes=[Rwstg])
        S.dma("sp", wstg[:, 2, :], self.w["g_up"][l], writes=[Rwstg])
        wupz, Rwupz = self.tile("wupz", [128, 2, 512], BF16)
        aupz, Raupz = self.tile("aupz", [128, 2, 512], BF16)
        gup, Rgup = self.tile("gup", [128, 512], BF16)
        op("pool", lambda e: e.memset(wupz[:].rearrange("p d c -> p (d c)"), 0.0), writes=[Rwupz])
        op("pool", lambda e: e.memset(aupz[:].rearrange("p d c -> p (d c)"), 0.0), writes=[Raupz])
        for d in range(2):
            op("dve", lambda e, d=d: e.tensor_copy(out=wupz[d * 64:(d + 1) * 64, d, :], in_=wstg[d * 64:(d + 1) * 64, 0, :]), reads=[Rwstg, Rwupz], writes=[Rwupz])
            op("dve", lambda e, d=d: e.tensor_copy(out=aupz[d * 64:(d + 1) * 64, d, :], in_=wstg[d * 64:(d + 1) * 64, 1, :]), reads=[Rwstg, Raupz], writes=[Raupz])
        op("dve", lambda e: e.tensor_copy(out=gup[:], in_=wstg[:, 2, :]), reads=[Rwstg], writes=[Rgup])
        onesf, Ronesf = self.tile("onesf", [128, 128], F32)
        op("pool", lambda e: e.memset(onesf[:], 1.0), writes=[Ronesf])
        blk1, Rblk1 = self.tile("blk1", [128, 128], F32)
        hsel, Rhsel = self.tile("hsel", [128, 2], F32)
        bdm, Rbdm = self.tile("bdm", [128, 4, 2, 64], F32)

        def mkz(e):
            e.memset(blk1[:], 0.0)
            e.memset(hsel[:], 0.0)
            return e.memset(bdm[:].rearrange("p c h i -> p (c h i)"), 0.0)
        op("dve", mkz, writes=[Rblk1, Rhsel, Rbdm])

        def mko(e):
            e.memset(blk1[0:64, 0:64], 1.0)
            e.memset(blk1[64:128, 64:128], 1.0)
            e.memset(hsel[0:64, 0:1], 1.0)
            e.memset(hsel[64:128, 1:2], 1.0)
            e.memset(bdm[0:64, :, 0, :], 1.0)
            return e.memset(bdm[64:128, :, 1, :], 1.0)
        op("dve", mko, reads=[Rblk1, Rhsel, Rbdm], writes=[Rblk1, Rhsel, Rbdm])
        gne, Rgne = self.tile("gne", [128, 1], F32)
        op("pool", lambda e: e.memset(gne[:], GN_EPS), writes=[Rgne])
        mbase, Rmbase = self.tile("mbase", [128, 4, 128], F32)

        op("pool", lambda e: e.memset(mbase[:].rearrange("p a x -> p (a x)"), 1.0), writes=[Rmbase])
        op("pool", lambda e: e.affine_select(out=mbase[:, 0, :], in_=mbase[:, 0, :], pattern=[[1, 128]], compare_op=ALU.is_gt, fill=0.0, base=0, channel_multiplier=-1), reads=[Rmbase], writes=[Rmbase])
        op("pool", lambda e: e.affine_select(out=mbase[:, 1, :], in_=mbase[:, 1, :], pattern=[[1, 128]], compare_op=ALU.is_ge, fill=0.0, base=0, channel_multiplier=-1), reads=[Rmbase], writes=[Rmbase])
        op("pool", lambda e: e.affine_select(out=mbase[:, 2, :], in_=mbase[:, 2, :], pattern=[[-1, 128]], compare_op=ALU.is_gt, fill=0.0, base=0, channel_multiplier=1), reads=[Rmbase], writes=[Rmbase])
        op("pool", lambda e: e.affine_select(out=mbase[:, 3, :], in_=mbase[:, 3, :], pattern=[[-1, 128]], compare_op=ALU.is_ge, fill=0.0, base=0, channel_multiplier=1), reads=[Rmbase], writes=[Rmbase])
        mG, RmG = self.tile("mG", [128, 2, 2, 2, 128], BF16)
        mL, RmL = self.tile("mL", [128, 2, 2, 128], BF16)
        for d in range(2):
            si, ii, li = (0, 1, 2) if d == 0 else (2, 3, 0)
            for h2 in range(2):
                op("dve", lambda e, d=d, h2=h2, si=si: e.tensor_copy(out=mG[:, d, h2, 0, :], in_=mbase[:, si, :]), reads=[Rmbase], writes=[RmG])
                op("dve", lambda e, d=d, h2=h2, ii=ii: e.tensor_copy(out=mG[:, d, h2, 1, :], in_=mbase[:, ii, :]), reads=[Rmbase], writes=[RmG])
                op("dve", lambda e, d=d, h2=h2, li=li: e.tensor_copy(out=mL[:, d, h2, :], in_=mbase[:, li, :]), reads=[Rmbase], writes=[RmL])
        Eg, REg = self.tile("Eg", [8, 3, 128], F32)
        op("pool", lambda e: e.memset(Eg[:].rearrange("p a x -> p (a x)"), 1.0), writes=[REg])
        for gi, b_ in enumerate((16, 32, 64)):
            op("pool", lambda e, gi=gi, b_=b_: e.affine_select(out=Eg[:, gi, :], in_=Eg[:, gi, :], pattern=[[1, 128]], compare_op=ALU.is_ge, fill=0.0, base=0, channel_multiplier=-b_),
               reads=[REg], writes=[REg])
            op("pool", lambda e, gi=gi, b_=b_: e.affine_select(out=Eg[:, gi, :], in_=Eg[:, gi, :], pattern=[[-1, 128]], compare_op=ALU.is_ge, fill=0.0, base=b_ - 1, channel_multiplier=b_),
               reads=[REg], writes=[REg])
        bdf, Rbdf = self.tile("bdf", [128, 4, 128], F32)
        op("pool", lambda e: e.memset(bdf[:, 3, :], 1.0), writes=[Rbdf])
        pbm, Rpbm = self.psum("pbm", [128, 512], F32)

        def mmE(e):
            for gi in range(3):
                ins = e.matmul(pbm[:, gi * 128:(gi + 1) * 128], lhsT=Eg[:, gi, :], rhs=Eg[:, gi, :], start=True, stop=True)
            return ins
        op("pe", mmE, reads=[REg], writes=[Rpbm])
        op("dve", lambda e: e.tensor_copy(out=bdf[:, 0:3, :].rearrange("p a x -> p (a x)"), in_=pbm[:, 0:384]), reads=[Rpbm, Rbdf], writes=[Rbdf])
        bd16, Rbd16 = self.tile("bd16", [128, 2, 128], BF16)
        offm, Roffm = self.tile("offm", [128, 3, 4, 128], BF16)
        for h2 in range(4):
            if h2 < 2:
                op("dve", lambda e, h2=h2: e.tensor_copy(out=bd16[:, h2, :], in_=bdf[:, 0, :]), reads=[Rbdf], writes=[Rbd16])
            for lv in range(3):
                op("dve", lambda e, h2=h2, lv=lv: e.tensor_tensor(out=offm[:, lv, h2, :], in0=bdf[:, lv + 1, :], in1=bdf[:, lv, :], op=ALU.subtract),
                   reads=[Rbdf], writes=[Roffm])
        Ps = self.rot("P", [128, 15, 130], F32, 1)
        tAs = self.rot("tA", [128, 15, 128], F32, 1)
        tBs = self.rot("tB", [128, 15, 128], F32, 1)
        phs = self.rot("ph", [128, 15, 128], F32, 1)
        f4 = lambda n, k=1: self.rot(n, [128, 4, 128], F32, k)
        b4 = lambda n, k=1: self.rot(n, [128, 4, 128], BF16, k)
        lws, avs, pres, cums, cprs = f4("lw"), f4("av"), f4("pre"), f4("cum"), f4("cpr")
        ecs, ecps, eis = f4("ec"), f4("ecp"), f4("ei")
        kraws, sqs, nrms, kkvs, tts, kds, bvs = (f4(n) for n in ("kraw", "sq", "nrm", "kkv", "tt", "kd", "bv"))
        Kfs, Bfs, rks = kraws, nrms, sqs
        ATs, RTs, KTs, BTs = (b4(n, 2) for n in ("AT", "RT", "KT", "BT"))
        KhTs, BhTs, vTs = (b4(n, 1) for n in ("KhT", "BhT", "vT"))
        twds = self.rot("twd", [128, 128], BF16, 2)
        adbs = self.rot("adb", [128, 128], BF16, 2)
        sgds = self.rot("sgd", [128, 128], BF16, 2)
        gCs = self.rot("gC", [128, 4], F32, 2)
        ARbds = self.rot("ARbd", [128, 4, 2, 2, 128], BF16, 2)
        Bbds = self.rot("Bbd", [128, 4, 2, 128], BF16, 2)
        for (t_, r_) in ARbds.items:
            op("pool", lambda e, t_=t_: e.memset(t_[:].rearrange("p c h x t -> p (c h x t)"), 0.0), writes=[r_])
        for (t_, r_) in Bbds.items:
            op("pool", lambda e, t_=t_: e.memset(t_[:].rearrange("p c h t -> p (c h t)"), 0.0), writes=[r_])
        VKs = self.rot("VK", [128, 1024], BF16, 2)
        Bhats = self.rot("Bhat", [128, 512], BF16, 2)
        GKs = self.rot("GK", [128, 4, 2, 2, 128], BF16, 2)
        GBs = self.rot("GB", [128, 4, 2, 2, 128], BF16, 2)
        Lhs = self.rot("Lh", [128, 4, 2, 128], BF16, 1)
        lmq = [[self.tile(f"lmq{h}_{k}", [128, 4, 128], BF16) for k in range(2)] for h in range(8)]
        zzt = [self.tile(f"zz{h}", [128, 2, 2, 128], BF16) for h in range(4)]
        ttt = [[self.tile(f"tt{h}_{k}", [128, 2, 2, 128], BF16) for k in range(2)] for h in range(4)]
        tfin = [self.rot(f"tfin{h}", [128, 2, 2, 128], BF16, 2) for h in range(4)]
        L16s = self.rot("L16", [128, 4, 2, 128], BF16, 1)
        M16s = self.rot("M16", [128, 4, 2, 128], BF16, 1)
        St, RSt = self.tile("St", [128, 4, 2, 64], F32)
        Sbd, RSbd = self.tile("Sbd", [128, 4, 2, 64], BF16)
        stmp, Rstmp = self.tile("stmp", [128, 4, 2, 64], F32)
        Wsbs = self.rot("Wsb", [128, 512], BF16, 1)
        Usbs = self.rot("Usb", [128, 512], BF16, 1)
        Ysbs = self.rot("Ysb", [128, 512], F32, 2)
        bons = self.rot("bonv", [128, 8], F32, 2)
        bon0s = self.rot("bon0", [128, 8], F32, 2)
        yfls = self.rot("yfl", [128, 512], F32, 1)
        gtms = self.rot("gtm", [128, 512], F32, 2)
        sqvs = self.rot("sqv", [128, 512], F32, 1)
        ycs = self.rot("yc", [128, 512], F32, 1)
        vvs = sqvs
        stat = self.rot("stat", [128, 6, 8], F32, 2)
        yos = self.rot("yo", [128, 512], BF16, 1)
        pbanks = self.rot("pb", [128, 512], F32, 6, ps=True)
        ptr = self.rot("ptr", [128, 1024], BF16, 1, ps=True)
        pfv = [pf[s].rearrange("(c p) t -> p c t", p=128) for s in range(NS)]

        for s in range(NS):
            for d in range(2):
                op("pool", lambda e: e.memset(St[:].rearrange("p c h i -> p (c h i)"), 0.0), writes=[RSt])
                op("pool", lambda e: e.memset(Sbd[:].rearrange("p c h i -> p (c h i)"), 0.0), writes=[RSbd])
                order = range(NCH) if d == 0 else range(NCH - 1, -1, -1)
                def body(ci, s=s, d=d):
                    t0 = ci * C
                    ph, Rph = phs.next()
                    if d == 0:
                        P, RP = Ps.next()
                        lo, hi = max(t0 - 1, 0), min(t0 + C + 1, T)
                        if t0 == 0:
                            op("pool", lambda e, P=P: e.memset(P[:, :, 0:1], 0.0), writes=[RP])
                        if t0 + C == T:
                            op("pool", lambda e, P=P: e.memset(P[:, :, 129:130], 0.0), writes=[RP])
                        S.dma("sp", P[:, :, lo - (t0 - 1):hi - (t0 - 1)], pfv[s][:, :, lo:hi], reads=[self.dres(("pf", s))], writes=[RP])
                        tA, RtA = tAs.next()
                        tB, RtB = tBs.next()
                        for (c0_, c1_) in ((0, 5), (5, 10), (10, 15)):
                            cs_ = slice(c0_, c1_)
                            nn = c1_ - c0_
                            op("dve", lambda e, cs_=cs_: e.tensor_tensor(out=tA[:, cs_, :], in0=P[:, cs_, 0:128], in1=P[:, cs_, 1:129], op=ALU.subtract), reads=[RP], writes=[RtA])
                            op("pool", lambda e, cs_=cs_: e.tensor_tensor(out=tB[:, cs_, :], in0=P[:, cs_, 2:130], in1=P[:, cs_, 1:129], op=ALU.subtract), reads=[RP], writes=[RtB])
                            op("dve", lambda e, cs_=cs_, nn=nn: e.tensor_tensor(out=tA[:, cs_, :], in0=tA[:, cs_, :], in1=bc(mup[:, cs_].unsqueeze(2), [128, nn, 128]), op=ALU.mult), reads=[RtA, Rmup], writes=[RtA])
                            op("pool", lambda e, cs_=cs_, nn=nn: e.tensor_tensor(out=tB[:, cs_, :], in0=tB[:, cs_, :], in1=bc(mun[:, cs_].unsqueeze(2), [128, nn, 128]), op=ALU.mult), reads=[RtB, Rmun], writes=[RtB])
                            yield 'a'
                            op("dve", lambda e, cs_=cs_: e.tensor_tensor(out=tA[:, cs_, :], in0=tA[:, cs_, :], in1=tB[:, cs_, :], op=ALU.add), reads=[RtA, RtB], writes=[RtA])
                            op("pool", lambda e, cs_=cs_: e.tensor_tensor(out=ph[:, cs_, :], in0=tA[:, cs_, :], in1=P[:, cs_, 1:129], op=ALU.add), reads=[RtA, RP], writes=[Rph])
                            yield 'a'
                        S.dma("pool", phd[s, ci], ph[:].rearrange("p c t -> p (c t)"), reads=[Rph], writes=[self.dres(("phd", s))])
                    else:
                        S.dma("sp", ph[:].rearrange("p c t -> p (c t)"), phd[s, ci], reads=[self.dres(("phd", s))], writes=[Rph])
                    rh, kh, vh = ph[:, 0:4, :], ph[:, 4:8, :], ph[:, 8:12, :]
                    yield 'a'
                    twd, Rtwd = twds.next()
                    adb, Radb = adbs.next()
                    sgd, Rsgd = sgds.next()
                    op("act", lambda e, twd=twd, ph=ph: e.activation(out=twd[:], in_=ph[:, 12, :], func=AF.Tanh), reads=[Rph], writes=[Rtwd])
                    op("dve", lambda e, adb=adb, ph=ph: e.tensor_copy(out=adb[:], in_=ph[:, 13, :]), reads=[Rph], writes=[Radb])
                    psw, Rpsw = pbanks.next()
                    psa, Rpsa = pbanks.next()

                    def mmlora(e, psw=psw, psa=psa, twd=twd, adb=adb, d=d):
                        for cc in range(4):
                            e.matmul(psw[:, cc * 128:(cc + 1) * 128], lhsT=wupz[:, d, cc * 128:(cc + 1) * 128], rhs=twd[:], start=True, stop=True)
                        for cc in range(4):
                            ins = e.matmul(psa[:, cc * 128:(cc + 1) * 128], lhsT=aupz[:, d, cc * 128:(cc + 1) * 128], rhs=adb[:], start=True, stop=True)
                        return ins
                    op("pe", mmlora, reads=[Rwupz, Raupz, Rtwd, Radb], writes=[Rpsw, Rpsa])
                    yield 'a'
                    lw, Rlw = lws.next()
                    av, Rav = avs.next()

                    def sigw(e, lw=lw, psw=psw, d=d):
                        for cc in range(4):
                            ins = e.activation(out=lw[:, cc, :], in_=psw[:, cc * 128:(cc + 1) * 128], func=AF.Sigmoid, bias=w0t[:, d, cc:cc + 1])
                        return ins
                    op("act", sigw, reads=[Rpsw, Rw0], writes=[Rlw])

                    def siga(e, av=av, psa=psa, d=d):
                        for cc in range(4):
                            ins = e.activation(out=av[:, cc, :], in_=psa[:, cc * 128:(cc + 1) * 128], func=AF.Sigmoid, bias=a0t[:, d, cc:cc + 1])
                        return ins
                    op("act", siga, reads=[Rpsa, Ra0], writes=[Rav])
                    yield 'a'
                    yield 'a'
                    pre, Rpre = pres.next()
                    cum, Rcum = cums.next()
                    cpr, Rcpr = cprs.next()

                    def scan(e, pre=pre, lw=lw):
                        for cc in range(4):
                            ins = e.tensor_tensor_scan(out=pre[:, cc, :], data0=onesf[:], data1=lw[:, cc, :], initial=0.0, op0=ALU.mult, op1=ALU.add)
                        return ins
                    op("dve", scan, reads=[Rlw, Ronesf], writes=[Rpre])
                    yield 'a'
                    if d == 0:
                        cum, Rcum = pre, Rpre
                    else:
                        op("dve", lambda e, cum=cum, pre=pre, lw=lw: e.scalar_tensor_tensor(out=cum[:], in0=pre[:], scalar=-1.0, in1=lw[:], op0=ALU.mult, op1=ALU.add),
                           reads=[Rpre, Rlw], writes=[Rcum])
                        op("dve", lambda e, cum=cum, pre=pre: e.tensor_tensor(out=cum[:], in0=cum[:], in1=bc(pre[:, :, 127:128], [128, 4, 128]), op=ALU.add),
                           reads=[Rcum, Rpre], writes=[Rcum])
                    op("pool", lambda e, cpr=cpr, cum=cum, lw=lw: e.tensor_tensor(out=cpr[:], in0=cum[:], in1=lw[:], op=ALU.subtract), reads=[Rcum, Rlw], writes=[Rcpr])
                    ec, Rec = ecs.next()
                    ecp, Recp = ecps.next()
                    ei, Rei = eis.next()
                    op("act", lambda e, ec=ec, cum=cum: e.activation(out=ec[:], in_=cum[:], func=AF.Exp, scale=-LD), reads=[Rcum], writes=[Rec])
                    op("act", lambda e, ecp=ecp, cpr=cpr: e.activation(out=ecp[:], in_=cpr[:], func=AF.Exp, scale=-LD), reads=[Rcpr], writes=[Recp])
                    op("act", lambda e, ei=ei, cum=cum: e.activation(out=ei[:], in_=cum[:], func=AF.Exp, scale=LD), reads=[Rcum], writes=[Rei])
                    yield 'a'
                    gC, RgC = gCs.next()
                    ce = 127 if d == 0 else 0
                    op("dve", lambda e, gC=gC, ec=ec, ce=ce: e.tensor_copy(out=gC[:], in_=ec[:, :, ce]), reads=[Rec], writes=[RgC])
                    yield 'a'
                    kraw, Rkraw = kraws.next()
                    sq, Rsq = sqs.next()
                    nrm, Rnrm = nrms.next()
                    kkv, Rkkv = kkvs.next()
                    tt, Rtt = tts.next()
                    kd, Rkd = kds.next()
                    bv, Rbv = bvs.next()
                    op("dve", lambda e, kraw=kraw, ph=ph: e.tensor_tensor(out=kraw[:], in0=ph[:, 4:8, :], in1=bc(kkc[:].unsqueeze(2), [128, 4, 128]), op=ALU.mult), reads=[Rph, Rkkc], writes=[Rkraw])
                    op("pool", lambda e, sq=sq, kraw=kraw: e.tensor_tensor(out=sq[:], in0=kraw[:], in1=kraw[:], op=ALU.mult), reads=[Rkraw], writes=[Rsq])
                    psn, Rpsn = pbanks.next()
                    op("pe", lambda e, psn=psn, sq=sq: e.matmul(psn[:], lhsT=blk1[:], rhs=sq[:].rearrange("p c t -> p (c t)"), start=True, stop=True), reads=[Rblk1, Rsq], writes=[Rpsn])
                    yield 'a'
                    op("act", lambda e, nrm=nrm, psn=psn: e.activation(out=nrm[:].rearrange("p c t -> p (c t)"), in_=psn[:], func=AF.Sqrt), reads=[Rpsn], writes=[Rnrm])
                    op("dve", lambda e, nrm=nrm: e.tensor_scalar_max(out=nrm[:], in0=nrm[:], scalar1=1e-12), reads=[Rnrm], writes=[Rnrm])
                    op("dve", lambda e, nrm=nrm: e.reciprocal(out=nrm[:], in_=nrm[:]), reads=[Rnrm], writes=[Rnrm])
                    op("dve", lambda e, kkv=kkv, kraw=kraw, nrm=nrm: e.tensor_tensor(out=kkv[:], in0=kraw[:], in1=nrm[:], op=ALU.mult), reads=[Rkraw, Rnrm], writes=[Rkkv])
                    yield 'a'
                    op("pool", lambda e, tt=tt, av=av: e.tensor_tensor(out=tt[:], in0=av[:], in1=bc(kac[:].unsqueeze(2), [128, 4, 128]), op=ALU.mult), reads=[Rav, Rkac], writes=[Rtt])
                    op("pool", lambda e, tt=tt: e.tensor_tensor(out=tt[:], in0=tt[:], in1=bc(omka[:].unsqueeze(2), [128, 4, 128]), op=ALU.add), reads=[Rtt, Romka], writes=[Rtt])
                    op("dve", lambda e, kd=kd, ph=ph, tt=tt: e.tensor_tensor(out=kd[:], in0=ph[:, 4:8, :], in1=tt[:], op=ALU.mult), reads=[Rph, Rtt], writes=[Rkd])
                    yield 'a'
                    op("pool", lambda e, bv=bv, kkv=kkv, av=av: e.tensor_tensor(out=bv[:], in0=kkv[:], in1=av[:], op=ALU.mult), reads=[Rkkv, Rav], writes=[Rbv])
                    yield 'a'
                    AT, RAT = ATs.next()
                    RT, RRT = RTs.next()
                    KT, RKT = KTs.next()
                    BT, RBT = BTs.next()
                    KhT, RKhT = KhTs.next()
                    BhT, RBhT = BhTs.next()
                    vT, RvT = vTs.next()
                    Kf, RKf = Kfs.next()
                    Bf, RBf = Bfs.next()
                    rk, Rrk = rks.next()
                    op("dve", lambda e, AT=AT, kkv=kkv, ecp=ecp: e.scalar_tensor_tensor(out=AT[:], in0=kkv[:], scalar=-1.0, in1=ecp[:], op0=ALU.mult, op1=ALU.mult), reads=[Rkkv, Recp], writes=[RAT])
                    op("dve", lambda e, RT=RT, ph=ph, ec=ec: e.tensor_tensor(out=RT[:], in0=ph[:, 0:4, :], in1=ec[:], op=ALU.mult), reads=[Rph, Rec], writes=[RRT])
                    op("dve", lambda e, Kf=Kf, kd=kd, ei=ei: e.tensor_tensor(out=Kf[:], in0=kd[:], in1=ei[:], op=ALU.mult), reads=[Rkd, Rei], writes=[RKf])
                    yield 'a'
                    op("pool", lambda e, Bf=Bf, bv=bv, ei=ei: e.tensor_tensor(out=Bf[:], in0=bv[:], in1=ei[:], op=ALU.mult), reads=[Rbv, Rei], writes=[RBf])
                    op("act", lambda e, KT=KT, Kf=Kf: e.copy(out=KT[:], in_=Kf[:]), reads=[RKf], writes=[RKT])
                    op("act", lambda e, BT=BT, Bf=Bf: e.copy(out=BT[:], in_=Bf[:]), reads=[RBf], writes=[RBT])
                    yield 'a'
                    op("dve", lambda e, KhT=KhT, Kf=Kf, gC=gC: e.tensor_tensor(out=KhT[:], in0=Kf[:], in1=bc(gC[:].unsqueeze(2), [128, 4, 128]), op=ALU.mult), reads=[RKf, RgC], writes=[RKhT])
                    op("pool", lambda e, BhT=BhT, Bf=Bf, gC=gC: e.tensor_tensor(out=BhT[:], in0=Bf[:], in1=bc(gC[:].unsqueeze(2), [128, 4, 128]), op=ALU.mult), reads=[RBf, RgC], writes=[RBhT])
                    op("act", lambda e, vT=vT, ph=ph: e.copy(out=vT[:], in_=ph[:, 8:12, :]), reads=[Rph], writes=[RvT])
                    op("pool", lambda e, rk=rk, ph=ph, kd=kd: e.tensor_tensor(out=rk[:], in0=ph[:, 0:4, :], in1=kd[:], op=ALU.mult), reads=[Rph, Rkd], writes=[Rrk])
                    op("pool", lambda e, rk=rk: e.tensor_tensor(out=rk[:], in0=rk[:], in1=bc(rkc[:].unsqueeze(2), [128, 4, 128]), op=ALU.mult), reads=[Rrk, Rrkc], writes=[Rrk])
                    yield 'a'
                    ARbd, RARbd = ARbds.next()
                    Bbd, RBbd = Bbds.next()
                    for h2 in range(2):
                        pl = slice(h2 * 64, (h2 + 1) * 64)
                        op("pool", lambda e, pl=pl, h2=h2, AT=AT: e.tensor_copy(out=ARbd[pl, :, h2, 0, :], in_=AT[pl, :, :]), reads=[RAT, RARbd], writes=[RARbd])
                        op("pool", lambda e, pl=pl, h2=h2, RT=RT: e.tensor_copy(out=ARbd[pl, :, h2, 1, :], in_=RT[pl, :, :]), reads=[RRT, RARbd], writes=[RARbd])
                        op("pool", lambda e, pl=pl, h2=h2, BT=BT: e.tensor_copy(out=Bbd[pl, :, h2, :], in_=BT[pl, :, :]), reads=[RBT, RBbd], writes=[RBbd])
                    yield 'a'
                    pt, Rpt = ptr.next()
                    VK, RVK = VKs.next()
                    Vtm, RVtm = VK[:, 0:512], RVK
                    Khat, RKhat = VK[:, 512:1024], RVK
                    Bhat, RBhat = Bhats.next()

                    def tr1(e, pt=pt, vT=vT, KhT=KhT):
                        for cc in range(4):
                            e.transpose(out=pt[:, cc * 128:(cc + 1) * 128], in_=vT[:, cc, :], identity=identb[:])
                        for cc in range(4):
                            ins = e.transpose(out=pt[:, 512 + cc * 128:512 + (cc + 1) * 128], in_=KhT[:, cc, :], identity=identb[:])
                        return ins
                    op("pe", tr1, reads=[RvT, RKhT, Ridb], writes=[Rpt])
                    yield 'a'
                    op("act", lambda e, VK=VK, pt=pt: e.copy(out=VK[:], in_=pt[:]), reads=[Rpt], writes=[RVK])
                    pt2, Rpt2 = ptr.next()

                    def tr2(e, pt2=pt2, BhT=BhT):
                        for cc in range(4):
                            ins = e.transpose(out=pt2[:, cc * 128:(cc + 1) * 128], in_=BhT[:, cc, :], identity=identb[:])
                        return ins
                    op("pe", tr2, reads=[RBhT, Ridb], writes=[Rpt2])
                    yield 'a'
                    op("act", lambda e, Bhat=Bhat, pt2=pt2: e.copy(out=Bhat[:], in_=pt2[:, 0:512]), reads=[Rpt2], writes=[RBhat])
                    psb, Rpsb = pbanks.next()
                    bonv, Rbonv = bons.next()

                    def mmbon(e, psb=psb, rk=rk):
                        for cc in range(4):
                            ins = e.transpose(out=psb[:, cc * 128:(cc + 1) * 128], in_=rk[:, cc, :], identity=identf[:])
                        return ins
                    op("pe", mmbon, reads=[Rrk, Ridf], writes=[Rpsb])
                    op("dve", lambda e, bonv=bonv, psb=psb: e.tensor_reduce(out=bonv[:], in_=psb[:].rearrange("p (h i) -> p h i", i=64), axis=AX.X, op=ALU.add),
                       reads=[Rpsb], writes=[Rbonv])
                    if d == 1:
                        op("act", lambda e, sgd=sgd, ph=ph: e.activation(out=sgd[:], in_=ph[:, 14, :], func=AF.Sigmoid), reads=[Rph], writes=[Rsgd])
                        psg, Rpsg = pbanks.next()
                        gtm, Rgtm = gtms.next()
                        op("pe", lambda e, psg=psg, sgd=sgd: e.matmul(psg[:], lhsT=sgd[:], rhs=gup[:], start=True, stop=True), reads=[Rsgd, Rgup], writes=[Rpsg])
                        op("act", lambda e, gtm=gtm, psg=psg: e.copy(out=gtm[:], in_=psg[:]), reads=[Rpsg], writes=[Rgtm])
                    yield 'B'
                    GK, RGK = GKs.next()
                    GB, RGB = GBs.next()
                    Lh, RLh = Lhs.next()
                    for cc in range(4):
                        p1, Rp1 = pbanks.next()
                        p2, Rp2 = pbanks.next()
                        p3, Rp3 = pbanks.next()

                        def mmg(e, cc=cc, p1=p1, p2=p2, p3=p3, KT=KT, BT=BT, AT=AT):
                            e.matmul(p1[:], lhsT=KT[:, cc, :], rhs=ARbd[:, cc].rearrange("p h x t -> p (h x t)"), start=True, stop=True)
                            e.matmul(p2[:], lhsT=BT[:, cc, :], rhs=ARbd[:, cc].rearrange("p h x t -> p (h x t)"), start=True, stop=True)
                            return e.matmul(p3[:, 0:256], lhsT=AT[:, cc, :], rhs=Bbd[:, cc].rearrange("p h t -> p (h t)"), start=True, stop=True)
                        op("pe", mmg, reads=[RKT, RBT, RAT, RARbd, RBbd], writes=[Rp1, Rp2, Rp3])
                        op("dve", lambda e, cc=cc, p1=p1, GK=GK, d=d: e.tensor_tensor(out=GK[:, cc].rearrange("p h x t -> p (h x t)"), in0=p1[:],
                                                                                 in1=mG[:, d].rearrange("p h x t -> p (h x t)"), op=ALU.mult), reads=[Rp1, RmG], writes=[RGK])
                        op("dve", lambda e, cc=cc, p2=p2, GB=GB, d=d: e.tensor_tensor(out=GB[:, cc].rearrange("p h x t -> p (h x t)"), in0=p2[:],
                                                                                 in1=mG[:, d].rearrange("p h x t -> p (h x t)"), op=ALU.mult), reads=[Rp2, RmG], writes=[RGB])
                        op("dve", lambda e, cc=cc, p3=p3, Lh=Lh, d=d: e.tensor_tensor(out=Lh[:, cc].rearrange("p h t -> p (h t)"), in0=p3[:, 0:256],
                                                                                 in1=mL[:, d].rearrange("p h t -> p (h t)"), op=ALU.mult), reads=[Rp3, RmL], writes=[RLh])
                        yield 'b'
                    L16, RL16 = L16s.next()
                    M16, RM16 = M16s.next()
                    for cc in range(4):
                        op("pool", lambda e, cc=cc, L16=L16, Lh=Lh: e.tensor_tensor(out=L16[:, cc].rearrange("p h t -> p (h t)"), in0=Lh[:, cc].rearrange("p h t -> p (h t)"),
                                                                              in1=bd16[:].rearrange("p h t -> p (h t)"), op=ALU.mult), reads=[RLh, Rbd16], writes=[RL16])
                        op("pool", lambda e, cc=cc, M16=M16, GB=GB: e.tensor_tensor(out=M16[:, cc], in0=GB[:, cc, :, 0, :], in1=bd16[:], op=ALU.mult), reads=[RGB, Rbd16], writes=[RM16])
                    cur = []
                    for h in range(8):
                        cur.append((L16[:, h // 2, h % 2, :], RL16, M16[:, h // 2, h % 2, :], RM16, identb[:], Ridb, identb[:], Ridb))
                    for lev in range(4):
                        for h in range(8):
                            Lp, RLp, Mp, RMp, Qp, RQp, Pp, RPp = cur[h]
                            pq, Rpq = pbanks.next()
                            lt, Rlt = lmq[h][lev % 2]

                            def mmi(e, pq=pq, Lp=Lp, Mp=Mp, Qp=Qp, Pp=Pp, lev=lev):
                                e.matmul(pq[:, 256:384], lhsT=identb[:], rhs=Qp, start=True, stop=False)
                                e.matmul(pq[:, 256:384], lhsT=Lp, rhs=Qp, start=False, stop=True)
                                e.matmul(pq[:, 384:512], lhsT=identb[:], rhs=Pp, start=True, stop=False)
                                ins = e.matmul(pq[:, 384:512], lhsT=Mp, rhs=Pp, start=False, stop=True)
                                if lev < 3:
                                    e.matmul(pq[:, 0:128], lhsT=Mp, rhs=Lp, start=True, stop=True)
                                    ins = e.matmul(pq[:, 128:256], lhsT=Lp, rhs=Mp, start=True, stop=True)
                                return ins
                            op("pe", mmi, reads=[RLp, RMp, RQp, RPp, Ridb], writes=[Rpq])
                            lo_ = 0 if lev < 3 else 256
                            eng = "act" if (h + lev) % 3 != 0 else "dve"
                            if eng == "dve":
                                op("dve", lambda e, lt=lt, pq=pq, lo_=lo_: e.tensor_copy(out=lt[:].rearrange("p a t -> p (a t)")[:, lo_:512], in_=pq[:, lo_:512]), reads=[Rpq], writes=[Rlt])
                            else:
                                op("act", lambda e, lt=lt, pq=pq, lo_=lo_: e.copy(out=lt[:].rearrange("p a t -> p (a t)")[:, lo_:512], in_=pq[:, lo_:512]), reads=[Rpq], writes=[Rlt])
                            cur[h] = (lt[:, 0, :], Rlt, lt[:, 1, :], Rlt, lt[:, 2, :], Rlt, lt[:, 3, :], Rlt)
                            if h % 2 == 1:
                                yield 'b'
                    tcur = [(cur[h][6], cur[h][7], cur[h][4], cur[h][5]) for h in range(8)]
                    for lv in range(3):
                        last_lv = (lv == 2)
                        pas = []
                        for pr in range(4):
                            pa, Rpa = pbanks.next()
                            pas.append((pa, Rpa))

                            def mmz(e, pa=pa, pr=pr, tc=list(tcur), last_lv=last_lv, Lh=Lh, GB=GB):
                                for h2 in range(2):
                                    h = 2 * pr + h2
                                    Tk, RTk, TTk, RTTk = tc[h]
                                    ins = e.matmul(pa[:, h2 * 256 + 128:h2 * 256 + 256], lhsT=Lh[:, pr, h2, :], rhs=TTk, start=True, stop=True)
                                    if not last_lv:
                                        ins = e.matmul(pa[:, h2 * 256:h2 * 256 + 128], lhsT=GB[:, pr, h2, 0, :], rhs=Tk, start=True, stop=True)
                                return ins
                            op("pe", mmz, reads=[RLh, RGB, tcur[2 * pr][1], tcur[2 * pr][3], tcur[2 * pr + 1][1], tcur[2 * pr + 1][3]], writes=[Rpa])
                        for pr in range(4):
                            pa, Rpa = pas[pr]
                            zz, Rzz = zzt[pr]
                            if not last_lv:
                                op("dve", lambda e, zz=zz, pa=pa, lv=lv: e.tensor_tensor(out=zz[:].rearrange("p h a t -> p (h a t)"), in0=pa[:],
                                                                                        in1=offm[:, lv].rearrange("p a t -> p (a t)"), op=ALU.mult),
                                   reads=[Rpa, Roffm], writes=[Rzz])
                            else:
                                op("dve", lambda e, zz=zz, pa=pa, lv=lv: e.tensor_tensor(out=zz[:, :, 1, :], in0=pa[:].rearrange("p (h a t) -> p h a t", h=2, a=2)[:, :, 1, :],
                                                                                        in1=offm[:, lv, 0:2, :], op=ALU.mult),
                                   reads=[Rpa, Roffm], writes=[Rzz])
                        yield 'b'
                        pbs = []
                        for pr in range(4):
                            pb_, Rpb_ = pbanks.next()
                            pbs.append((pb_, Rpb_))
                            zz, Rzz = zzt[pr]

                            def mmt(e, pb_=pb_, pr=pr, tc=list(tcur), zz=zz, last_lv=last_lv):
                                for h2 in range(2):
                                    h = 2 * pr + h2
                                    Tk, RTk, TTk, RTTk = tc[h]
                                    c0 = h2 * 256
                                    e.matmul(pb_[:, c0 + 128:c0 + 256], lhsT=identb[:], rhs=TTk, start=True, stop=False)
                                    ins = e.matmul(pb_[:, c0 + 128:c0 + 256], lhsT=Tk, rhs=zz[:, h2, 1, :], start=False, stop=True)
                                    if not last_lv:
                                        e.matmul(pb_[:, c0:c0 + 128], lhsT=identb[:], rhs=Tk, start=True, stop=False)
                                        ins = e.matmul(pb_[:, c0:c0 + 128], lhsT=TTk, rhs=zz[:, h2, 0, :], start=False, stop=True)
                                return ins
                            op("pe", mmt, reads=[Rzz, Ridb, tcur[2 * pr][1], tcur[2 * pr][3], tcur[2 * pr + 1][1], tcur[2 * pr + 1][3]], writes=[Rpb_])
                        for pr in range(4):
                            pb_, Rpb_ = pbs[pr]
                            tn, Rtn = ttt[pr][lv % 2] if not last_lv else tfin[pr].next()
                            eng = "act" if pr % 2 == 0 else "dve"
                            if not last_lv:
                                src, dst = pb_[:], tn[:].rearrange("p h a t -> p (h a t)")
                            else:
                                src, dst = pb_[:].rearrange("p (h a t) -> p h a t", h=2, a=2)[:, :, 1, :], tn[:, :, 1, :]
                            if eng == "act":
                                op("act", lambda e, src=src, dst=dst: e.copy(out=dst, in_=src), reads=[Rpb_], writes=[Rtn])
                            else:
                                op("dve", lambda e, src=src, dst=dst: e.tensor_copy(out=dst, in_=src), reads=[Rpb_], writes=[Rtn])
                            for h2 in range(2):
                                tcur[2 * pr + h2] = (tn[:, h2, 0, :], Rtn, tn[:, h2, 1, :], Rtn)
                        yield 'b'
                    cur = [(None, None, None, None, tcur[h][2], tcur[h][3]) for h in range(8)]
                    yield 'C'
                    pW, RpW = pbanks.next()
                    Wsb, RWsb = Wsbs.next()
                    Usb, RUsb = Usbs.next()
                    Ysb, RYsb = Ysbs.next()

                    def mmW(e, pW=pW, AT=AT, GK=GK, Vtm=Vtm):
                        for cc in range(4):
                            e.matmul(pW[:, cc * 128:(cc + 1) * 128], lhsT=AT[:, cc, :], rhs=Sbd[:, cc].rearrange("p h i -> p (h i)"), start=True, stop=False)
                            for h2 in range(2):
                                h = 2 * cc + h2
                                ins = e.matmul(pW[:, h * 64:(h + 1) * 64], lhsT=GK[:, cc, h2, 0, :], rhs=Vtm[:, h * 64:(h + 1) * 64], start=False, stop=True)
                        return ins
                    op("pe", mmW, reads=[RAT, RSbd, RGK, RVtm], writes=[RpW])
                    op("dve", lambda e, Wsb=Wsb, pW=pW: e.tensor_copy(out=Wsb[:], in_=pW[:]), reads=[RpW], writes=[RWsb])
                    yield 'c'
                    pU, RpU = pbanks.next()

                    def mmU(e, pU=pU, Wsb=Wsb, cur=list(cur)):
                        for h in range(8):
                            ins = e.matmul(pU[:, h * 64:(h + 1) * 64], lhsT=cur[h][4], rhs=Wsb[:, h * 64:(h + 1) * 64], start=True, stop=True)
                        return ins
                    op("pe", mmU, reads=[RWsb] + [cur[h][5] for h in range(8)], writes=[RpU])
                    op("dve", lambda e, Usb=Usb, pU=pU: e.tensor_copy(out=Usb[:], in_=pU[:]), reads=[RpU], writes=[RUsb])
                    yield 'c'
                    pY, RpY = pbanks.next()

                    def mmY(e, pY=pY, RT=RT, GK=GK, GB=GB, Vtm=Vtm, Usb=Usb):
                        for cc in range(4):
                            e.matmul(pY[:, cc * 128:(cc + 1) * 128], lhsT=RT[:, cc, :], rhs=Sbd[:, cc].rearrange("p h i -> p (h i)"), start=True, stop=False)
                            for h2 in range(2):
                                h = 2 * cc + h2
                                e.matmul(pY[:, h * 64:(h + 1) * 64], lhsT=GK[:, cc, h2, 1, :], rhs=Vtm[:, h * 64:(h + 1) * 64], start=False, stop=False)
                                ins = e.matmul(pY[:, h * 64:(h + 1) * 64], lhsT=GB[:, cc, h2, 1, :], rhs=Usb[:, h * 64:(h + 1) * 64], start=False, stop=True)
                        return ins
                    op("pe", mmY, reads=[RRT, RSbd, RGK, RGB, RVtm, RUsb], writes=[RpY])
                    op("act", lambda e, Ysb=Ysb, pY=pY: e.copy(out=Ysb[:], in_=pY[:]), reads=[RpY], writes=[RYsb])
                    yield 'c'
                    pS, RpS = pbanks.next()

                    def mmS(e, pS=pS, Khat=Khat, Bhat=Bhat, Vtm=Vtm, Usb=Usb):
                        for cc in range(4):
                            cs = slice(cc * 128, (cc + 1) * 128)
                            e.matmul(pS[:, cs], lhsT=Khat[:, cs], rhs=Vtm[:, cs], start=True, stop=False)
                            ins = e.matmul(pS[:, cs], lhsT=Bhat[:, cs], rhs=Usb[:, cs], start=False, stop=True)
                        return ins
                    op("pe", mmS, reads=[RKhat, RBhat, RVtm, RUsb], writes=[RpS])
                    Sf = St[:].rearrange("p c h i -> p c (h i)")
                    op("dve", lambda e, pS=pS: e.tensor_tensor(out=stmp[:].rearrange("p c h i -> p (c h i)"), in0=pS[:], in1=bdm[:].rearrange("p c h i -> p (c h i)"), op=ALU.mult),
                       reads=[RpS, Rbdm], writes=[Rstmp])
                    op("dve", lambda e, gC=gC: e.tensor_tensor(out=Sf, in0=Sf, in1=bc(gC[:].unsqueeze(2), [128, 4, 128]), op=ALU.mult), reads=[RSt, RgC], writes=[RSt])
                    op("dve", lambda e: e.tensor_tensor(out=St[:].rearrange("p c h i -> p (c h i)"), in0=St[:].rearrange("p c h i -> p (c h i)"),
                                                        in1=stmp[:].rearrange("p c h i -> p (c h i)"), op=ALU.add), reads=[RSt, Rstmp], writes=[RSt])
                    op("act", lambda e: e.copy(out=Sbd[:].rearrange("p c h i -> p (c h i)"), in_=St[:].rearrange("p c h i -> p (c h i)")), reads=[RSt], writes=[RSbd])
                    yield 'c'
                    if d == 0:
                        S.dma("pool", yf[s, t0:t0 + C, :], Ysb[:], reads=[RYsb], writes=[self.dres(("yf", s))])
                        S.dma("pool", bon[s, t0:t0 + C, :], bonv[:], reads=[Rbonv], writes=[self.dres(("bon", s))])
                    else:
                        yfl, Ryfl = yfls.next()
                        bon0, Rbon0 = bon0s.next()
                        S.dma("sp", yfl[:], yf[s, t0:t0 + C, :], reads=[self.dres(("yf", s))], writes=[Ryfl])
                        S.dma("sp", bon0[:], bon[s, t0:t0 + C, :], reads=[self.dres(("bon", s))], writes=[Rbon0])
                        sqv, Rsqv = sqvs.next()
                        yc, Ryc = ycs.next()
                        vv, Rvv = vvs.next()
                        st, Rst = stat.next()
                        yo, Ryo = yos.next()
                        y3 = lambda t_: t_[:].rearrange("p (h i) -> p h i", i=64)
                        b3 = lambda a_: bc(a_.unsqueeze(2), [128, 8, 64])
                        op("dve", lambda e, Ysb=Ysb, yfl=yfl: e.tensor_tensor(out=Ysb[:], in0=Ysb[:], in1=yfl[:], op=ALU.add), reads=[RYsb, Ryfl], writes=[RYsb])
                        op("pool", lambda e, sqv=sqv, Ysb=Ysb: e.tensor_tensor(out=sqv[:], in0=Ysb[:], in1=Ysb[:], op=ALU.mult), reads=[RYsb], writes=[Rsqv])
                        op("dve", lambda e, st=st, Ysb=Ysb: e.tensor_reduce(out=st[:, 0, :], in_=y3(Ysb), axis=AX.X, op=ALU.add), reads=[RYsb], writes=[Rst])
                        op("dve", lambda e, st=st, sqv=sqv: e.tensor_reduce(out=st[:, 1, :], in_=y3(sqv), axis=AX.X, op=ALU.add), reads=[Rsqv, Rst], writes=[Rst])
                        op("dve", lambda e, st=st: e.tensor_scalar(out=st[:, 2, :], in0=st[:, 0, :], scalar1=1.0 / 64, scalar2=None, op0=ALU.mult), reads=[Rst], writes=[Rst])
                        op("dve", lambda e, st=st: e.tensor_tensor(out=st[:, 3, :], in0=st[:, 2, :], in1=st[:, 2, :], op=ALU.mult), reads=[Rst], writes=[Rst])
                        op("dve", lambda e, st=st: e.scalar_tensor_tensor(out=st[:, 4, :], in0=st[:, 1, :], scalar=1.0 / 64, in1=st[:, 3, :], op0=ALU.mult, op1=ALU.subtract),
                           reads=[Rst], writes=[Rst])
                        op("act", lambda e, st=st: e.activation(out=st[:, 5, :], in_=st[:, 4, :], func=AF.Sqrt, bias=gne[:, 0:1]), reads=[Rst, Rgne], writes=[Rst])
                        op("dve", lambda e, st=st: e.reciprocal(out=st[:, 5, :], in_=st[:, 5, :]), reads=[Rst], writes=[Rst])
                        yield 'c'
                        op("dve", lambda e, yc=yc, Ysb=Ysb, st=st: e.tensor_tensor(out=y3(yc), in0=y3(Ysb), in1=b3(st[:, 2, :]), op=ALU.subtract), reads=[RYsb, Rst], writes=[Ryc])
                        op("dve", lambda e, yc=yc, st=st: e.tensor_tensor(out=y3(yc), in0=y3(yc), in1=b3(st[:, 5, :]), op=ALU.mult), reads=[Ryc, Rst], writes=[Ryc])
                        op("pool", lambda e, yc=yc: e.tensor_tensor(out=yc[:], in0=yc[:], in1=gng[:], op=ALU.mult), reads=[Ryc, Rgng], writes=[Ryc])
                        op("pool", lambda e, yc=yc: e.tensor_tensor(out=yc[:], in0=yc[:], in1=gnb[:], op=ALU.add), reads=[Ryc, Rgnb], writes=[Ryc])
                        yield 'c'
                        op("dve", lambda e, bonv=bonv, bon0=bon0: e.tensor_tensor(out=bonv[:], in0=bonv[:], in1=bon0[:], op=ALU.add), reads=[Rbonv, Rbon0], writes=[Rbonv])
                        op("dve", lambda e, vv=vv, Vtm=Vtm, bonv=bonv: e.tensor_tensor(out=y3(vv), in0=Vtm.rearrange("p (h i) -> p h i", i=64), in1=b3(bonv[:]), op=ALU.mult), reads=[RVtm, Rbonv], writes=[Rvv])
                        op("pool", lambda e, yc=yc, vv=vv: e.tensor_tensor(out=yc[:], in0=yc[:], in1=vv[:], op=ALU.add), reads=[Ryc, Rvv], writes=[Ryc])
                        op("dve", lambda e, yo=yo, yc=yc, gtm=gtm: e.tensor_tensor(out=yo[:], in0=yc[:], in1=gtm[:], op=ALU.mult), reads=[Ryc, Rgtm], writes=[Ryo])
                        S.dma("pool", yr[s, t0:t0 + C, :], yo[:], reads=[Ryo], writes=[self.dres(("yr", s))])

                def run_until(g, tags):
                    while True:
                        try:
                            t_ = next(g)
                        except StopIteration:
                            return None
                        if t_ in tags:
                            return t_
                order = list(order)
                gens = [body(ci) for ci in order]
                run_until(gens[0], ('B',))
                prev = None
                for i_ in range(len(gens)):
                    g_cur = gens[i_]
                    g_nxt = gens[i_ + 1] if i_ + 1 < len(gens) else None
                    cur_done = False
                    if prev is not None:
                        prev_done = False
                        while not prev_done:
                            if run_until(prev, ('c',)) is None:
                                prev_done = True
                            if not cur_done and run_until(g_cur, ('b', 'C')) == 'C':
                                cur_done = True
                    nxt_done = g_nxt is None
                    while not (cur_done and nxt_done):
                        if not cur_done and run_until(g_cur, ('b', 'C')) == 'C':
                            cur_done = True
                        if not nxt_done and run_until(g_nxt, ('a', 'B')) == 'B':
                            nxt_done = True
                    prev = g_cur
                run_until(prev, ())
        self.end()

    def phase_p3(self, l):
        S, T, NS = self.S, self.T, self.NS
        self.begin()
        rows = T // 64
        nblk = T // 128
        nq, nk, nv, yn = (self.scr[k] for k in ("nq", "nk", "nv", "yn"))
        rp, Rrp = self.tile("rp", [120, 31], F32)
        S.dma("sp", rp[:], self.w["rpb"][l].rearrange("h a b -> (h a) b"), writes=[Rrp])
        _, _, idf, Ridf = self.make_ident()
        Rt, RRt = self.tile("Rt", [31, 8, 15], F32)
        Jm, RJ = self.tile("Jm", [31, 160], F32)
        tps = self.rot("tps", [128, 512], F32, 4, ps=True)
        tp0, Rtp0 = tps.next()
        S.op("pe", lambda e: e.transpose(out=tp0[0:31, 0:120], in_=rp[:, :], identity=idf[0:120, 0:120]), reads=[Rrp, Ridf], writes=[Rtp0])
        S.op("dve", lambda e: e.tensor_copy(out=Rt[:].rearrange("p h a -> p (h a)"), in_=tp0[0:31, 0:120]), reads=[Rtp0], writes=[RRt])

        S.op("pool", lambda e: e.memset(Jm[:], 0.0), writes=[RJ])
        S.op("pool", lambda e: e.affine_select(out=Jm[:], in_=Jm[:], pattern=[[-1, 160]], compare_op=ALU.not_equal, fill=1.0, base=48, channel_multiplier=1),
             reads=[RJ], writes=[RJ])
        TE0, RTE0 = self.tile("TE0", [64, 8, 15, 64], BF16)
        for q0 in range(0, 64, 4):
            tp, Rtp = tps.next()

            def mmT(e, tp=tp, q0=q0):
                for qi in range(4):
                    qc = q0 + qi
                    ins = e.matmul(tp[0:64, qi * 120:(qi + 1) * 120], lhsT=Jm[:, 63 - qc:127 - qc], rhs=Rt[:].rearrange("p h a -> p (h a)"),
                                   start=True, stop=True)
                return ins
            S.op("pe", mmT, reads=[RJ, RRt], writes=[Rtp])
            S.op("act", lambda e, tp=tp, q0=q0: e.activation(out=TE0[:, :, :, q0:q0 + 4].rearrange("p h a q -> p (h a) q"),
                                                             in_=tp[0:64, 0:480].rearrange("p (q x) -> p x q", q=4), func=AF.Exp),
                 reads=[Rtp], writes=[RTE0])
        A, RA = self.tile("mA", [128, 64], F32)
        Q, RQ = self.tile("mQ", [128, 64], F32)
        Q2, RQ2 = self.tile("mQ2", [128, 64], F32)
        cm, Rcm = self.tile("cm", [128, 64], F32)

        def io(e):
            e.iota(A[0:64, :], pattern=[[-1, 64]], base=0, channel_multiplier=1, allow_small_or_imprecise_dtypes=True)
            e.iota(A[64:128, :], pattern=[[-1, 64]], base=0, channel_multiplier=1, allow_small_or_imprecise_dtypes=True)
            return e.iota(Q[:], pattern=[[1, 64]], base=0, channel_multiplier=0, allow_small_or_imprecise_dtypes=True)
        S.op("pool", io, writes=[RA, RQ])
        S.op("dve", lambda e: e.tensor_scalar(out=Q2[:], in0=Q[:], scalar1=8.0, scalar2=56.0, op0=ALU.max, op1=ALU.min), reads=[RQ], writes=[RQ2])
        S.op("dve", lambda e: e.tensor_tensor(out=Q[:], in0=Q[:], in1=Q2[:], op=ALU.subtract), reads=[RQ, RQ2], writes=[RQ])
        S.op("dve", lambda e: e.tensor_tensor(out=A[:], in0=A[:], in1=Q[:], op=ALU.add), reads=[RA, RQ], writes=[RA])
        S.op("dve", lambda e: e.tensor_single_scalar(out=Q[:], in_=A[:], scalar=-8.0, op=ALU.is_ge), reads=[RA], writes=[RQ])
        S.op("dve", lambda e: e.tensor_single_scalar(out=Q2[:], in_=A[:], scalar=7.0, op=ALU.is_le), reads=[RA], writes=[RQ2])
        S.op("dve", lambda e: e.tensor_tensor(out=cm[:], in0=Q[:], in1=Q2[:], op=ALU.mult), reads=[RQ, RQ2], writes=[Rcm])
        TE, RTE = self.tile("TE", [128, 8, 14, 64], BF16)
        S.op("dve", lambda e: e.tensor_tensor(out=TE0[:].rearrange("p h a q -> p (h a) q"), in0=TE0[:].rearrange("p h a q -> p (h a) q"),
                                              in1=cm[0:64, :].unsqueeze(1).broadcast_to([64, 120, 64]), op=ALU.mult),
             reads=[RTE0, Rcm], writes=[RTE0])
        S.op("pool", lambda e: e.tensor_copy(out=TE[0:64, :, :, :].rearrange("p h a q -> p h (a q)"),
                                             in_=TE0[:, :, 0:14, :].rearrange("p h a q -> p h (a q)")), reads=[RTE0], writes=[RTE])
        S.dma("sp", TE[64:128, :, :, :].rearrange("p h a q -> p h (a q)"), TE0[:, :, 1:15, :].rearrange("p h a q -> p h (a q)"),
              reads=[RTE0], writes=[RTE])
        qT, RqT = self.tile("nqT", [128, 4, T], BF16)
        kT, RkT = self.tile("nkT", [128, 4, T], BF16)
        Ve, RVe = self.tile("nVe", [128, nblk, 8, 65], BF16)
        Vo, RVo = self.tile("nVo", [128, nblk, 8, 65], BF16)
        Vs, RVs = self.tile("nVs", [128, nblk // 2, 512], BF16)
        S.op("pool", lambda e: e.memset(Ve[:], 1.0), writes=[RVe])
        S.op("pool", lambda e: e.memset(Vo[:], 1.0), writes=[RVo])
        pss = Rot([(t_[:].rearrange("p (h q) -> p h q", q=64), r_) for (t_, r_) in tps.items])
        pos_raw = self.rot("ops", [128, 512], F32, 4, ps=True)
        Ets = self.rot("Et", [128, 8, 64], BF16, 3)
        Es = self.rot("E", [128, 4, 8, 64], BF16, 2)
        recs = self.rot("rec", [64, 8], F32, 2)
        outs = self.rot("yno", [64, 8, 64], BF16, 3)
        for s in range(NS):
            S.dma("sp", qT[:], nq[s].rearrange("(c p) t -> p c t", p=128), reads=[self.dres(("nq", s))], writes=[RqT])
            S.dma("sp", kT[:], nk[s].rearrange("(c p) t -> p c t", p=128), reads=[self.dres(("nk", s))], writes=[RkT])
            hb = nblk // 2
            for (Vx, RVx, off, nb_tot) in ((Ve, RVe, 0, nblk), (Vo, RVo, 64, nblk - 1)):
                for b0 in range(0, nb_tot, hb):
                    nb_ = min(hb, nb_tot - b0)
                    S.dma("sp", Vs[:, 0:nb_, :], nv[s, off + b0 * 128:off + (b0 + nb_) * 128, :].rearrange("(b p) c -> p b c", p=128),
                          reads=[self.dres(("nv", s))], writes=[RVs])
                    S.op("pool", lambda e, Vx=Vx, b0=b0, nb_=nb_: e.tensor_copy(
                        out=Vx[:, b0:b0 + nb_, :, 0:64].rearrange("p b h n -> p (b h) n"),
                        in_=Vs[:, 0:nb_, :].rearrange("p b (h n) -> p (b h) n", n=64)), reads=[RVs], writes=[RVx])
            for i in range(rows):
                rs = min(max(i - 4, 0), rows - 8)
                E, RE = Es.next()
                for blk in range(4):
                    kr = rs + 2 * blk
                    tok0 = kr * 64
                    dib = kr - i + 7
                    psA, RpsA = pss.next()
                    psB, RpsB = pss.next()
                    Et, REt = Ets.next()

                    def mms(e, psA=psA, psB=psB, tok0=tok0, i=i):
                        for h in range(8):
                            pb = (h % 2) * 64
                            ps = psA if h % 2 == 0 else psB
                            ins = e.matmul(ps[:, h // 2, :], lhsT=kT[pb:pb + 64, h // 2, tok0:tok0 + 128], rhs=qT[pb:pb + 64, h // 2, i * 64:(i + 1) * 64],
                                           start=True, stop=True)
                        return ins
                    S.op("pe", mms, reads=[RkT, RqT], writes=[RpsA, RpsB])
                    S.op("act", lambda e, psA=psA, Et=Et: e.activation(out=Et[:, 0:4, :], in_=psA[:, 0:4, :], func=AF.Exp, scale=0.125), reads=[RpsA], writes=[REt])
                    S.op("act", lambda e, psB=psB, Et=Et: e.activation(out=Et[:, 4:8, :], in_=psB[:, 0:4, :], func=AF.Exp, scale=0.125), reads=[RpsB], writes=[REt])
                    S.op("dve", lambda e, Et=Et, E=E, blk=blk, dib=dib: e.tensor_tensor(
                        out=E[:, blk, :, :].rearrange("p (two c) q -> p two c q", two=2), in0=Et[:].rearrange("p (two c) q -> p two c q", two=2),
                        in1=TE[:].rearrange("p (c two) a q -> p two c a q", two=2)[:, :, :, dib, :], op=ALU.mult),
                         reads=[REt, RTE], writes=[RE])
                poA_, RpoA = pos_raw.next()
                poB_, RpoB = pos_raw.next()
                poA = poA_[0:64, 0:260].rearrange("p (h n) -> p h n", n=65)
                poB = poB_[0:64, 0:260].rearrange("p (h n) -> p h n", n=65)

                def mmo(e, E=E, rs=rs, poA=poA, poB=poB):
                    for h in range(8):
                        po = poA if h < 4 else poB
                        for blk in range(4):
                            kr = rs + 2 * blk
                            Vx = Ve if kr % 2 == 0 else Vo
                            ins = e.matmul(po[:, h % 4, :], lhsT=E[:, blk, (h % 2) * 4 + h // 2, :], rhs=Vx[:, kr // 2, h, :], start=(blk == 0), stop=(blk == 3))
                    return ins
                S.op("pe", mmo, reads=[RE, RVe, RVo], writes=[RpoA, RpoB])
                rec, Rrec = recs.next()
                o, Ro = outs.next()
                S.op("dve", lambda e, rec=rec, poA=poA: e.reciprocal(out=rec[:, 0:4], in_=poA[:, :, 64]), reads=[RpoA], writes=[Rrec])
                S.op("dve", lambda e, rec=rec, poB=poB: e.reciprocal(out=rec[:, 4:8], in_=poB[:, :, 64]), reads=[RpoB], writes=[Rrec])
                S.op("dve", lambda e, rec=rec, poA=poA, o=o: e.tensor_tensor(out=o[:, 0:4, :], in0=poA[:, :, 0:64],
                                                                            in1=rec[:, 0:4].unsqueeze(2).broadcast_to([64, 4, 64]), op=ALU.mult),
                     reads=[RpoA, Rrec], writes=[Ro])
                S.op("dve", lambda e, rec=rec, poB=poB, o=o: e.tensor_tensor(out=o[:, 4:8, :], in0=poB[:, :, 0:64],
                                                                            in1=rec[:, 4:8].unsqueeze(2).broadcast_to([64, 4, 64]), op=ALU.mult),
                     reads=[RpoB, Rrec], writes=[Ro])
                S.dma("pool", yn[s, i * 64:(i + 1) * 64, :], o[:].rearrange("p h n -> p (h n)"), reads=[Ro], writes=[self.dres(("yn", s))])
        self.end()

    def xupdate(self, xt, Rxt, nb, aT, RaT, nkc, W, RW, pss):
        S = self.S
        for b in range(nb):
            for half in range(2):
                ps, Rps = pss.next()

                def mm(e, ps=ps, b=b, half=half):
                    for kc in range(nkc):
                        ins = e.matmul(ps[:], lhsT=aT[:, kc, b * 128:(b + 1) * 128], rhs=W[:, kc, half * 512:(half + 1) * 512],
                                       start=(kc == 0), stop=(kc == nkc - 1))
                    return ins
                S.op("pe", mm, reads=[RaT, RW], writes=[Rps])
                S.op("dve", lambda e, ps=ps, b=b, half=half: e.tensor_tensor(
                    out=xt[:, b, half * 512:(half + 1) * 512], in0=xt[:, b, half * 512:(half + 1) * 512], in1=ps[:], op=ALU.add),
                    reads=[Rps, Rxt], writes=[Rxt])

    def phase_p4(self, l):
        S, T, NS = self.S, self.T, self.NS
        self.begin()
        TT = 512
        wbr, Rwbr = self.tile("wbr", [128, 8, D], BF16)
        wout, Rwout = self.tile("wout", [128, 8, D], BF16)
        stg = self.rot("wstg", [128, 2048], F32, 2)
        ident, Rid, _, _ = self.make_ident()
        self.load_weight(wbr, Rwbr, self.w["w_br_rwkv"][l], 512, D, stg, kc0=0)
        self.load_weight(wbr, Rwbr, self.w["w_br_nat"][l], 512, D, stg, kc0=4)
        self.load_weight(wout, Rwout, self.w["w_out"][l], D, D, stg)
        xts = self.rot("xt", [128, 4, D], F32, 2)
        yts = self.rot("yt", [128, 4, 1024], BF16, 2)
        yTs = self.rot("yT", [128, 8, TT], BF16, 2)
        sgts = self.rot("sgt", [128, 16, TT], BF16, 2)
        mTs = self.rot("mT", [128, 8, TT], BF16, 2)
        m1s = self.rot("m1", [128, TT], F32, 2)
        m2s = self.rot("m2", [128, TT], F32, 2)
        pTs = self.rot("pT", [128, 1024], BF16, 2, ps=True)
        pss = self.rot("ps", [128, 512], F32, 6, ps=True)
        xsrc = self.x_in if l == 0 else self.scr["xres"]
        xres, yr, yn, sg = (self.scr[k] for k in ("xres", "yr", "yn", "sg"))
        for s in range(NS):
            for t0 in range(0, T, TT):
                xt, Rxt = xts.next()
                yt, Ryt = yts.next()
                yT, RyT = yTs.next()
                sgt, Rsgt = sgts.next()
                mT, RmT = mTs.next()
                Rx = self.dres(("xres", s, t0))
                S.dma("sp", yt[:, :, 0:512], yr[s, t0:t0 + TT, :].rearrange("(b p) c -> p b c", p=128), reads=[self.dres(("yr", s))], writes=[Ryt])
                S.dma("sp", yt[:, :, 512:1024], yn[s, t0:t0 + TT, :].rearrange("(b p) c -> p b c", p=128), reads=[self.dres(("yn", s))], writes=[Ryt])
                S.dma("sp", sgt[:], sg[s, :, t0:t0 + TT].rearrange("(c p) t -> p c t", p=128), reads=[self.dres(("sg", s))], writes=[Rsgt])
                S.dma("sp", xt[:], xsrc[s, t0:t0 + TT, :].rearrange("(b p) d -> p b d", p=128), reads=([Rx] if l > 0 else []), writes=[Rxt])
                for b in range(4):
                    pT, RpT = pTs.next()

                    def tr(e, pT=pT, b=b, yt=yt):
                        for c in range(8):
                            ins = e.transpose(out=pT[:, c * 128:(c + 1) * 128], in_=yt[:, b, c * 128:(c + 1) * 128], identity=ident[:])
                        return ins
                    S.op("pe", tr, reads=[Ryt, Rid], writes=[RpT])
                    S.op("act", lambda e, pT=pT, b=b, yT=yT: e.copy(out=yT[:, :, b * 128:(b + 1) * 128], in_=pT[:].rearrange("p (k t) -> p k t", k=8)),
                         reads=[RpT], writes=[RyT])
                for oc in range(8):
                    ps1, Rps1 = pss.next()
                    ps2, Rps2 = pss.next()
                    m1, Rm1 = m1s.next()
                    m2, Rm2 = m2s.next()

                    def mm(e, ps1=ps1, ps2=ps2, oc=oc, yT=yT):
                        for kc in range(4):
                            e.matmul(ps1[:], lhsT=wbr[:, kc, oc * 128:(oc + 1) * 128], rhs=yT[:, kc, :], start=(kc == 0), stop=(kc == 3))
                        for kc in range(4):
                            ins = e.matmul(ps2[:], lhsT=wbr[:, 4 + kc, oc * 128:(oc + 1) * 128], rhs=yT[:, 4 + kc, :], start=(kc == 0), stop=(kc == 3))
                        return ins
                    S.op("pe", mm, reads=[Rwbr, RyT], writes=[Rps1, Rps2])
                    S.op("dve", lambda e, m1=m1, ps1=ps1, oc=oc, sgt=sgt: e.tensor_tensor(out=m1[:], in0=ps1[:], in1=sgt[:, oc, :], op=ALU.mult),
                         reads=[Rps1, Rsgt], writes=[Rm1])
                    S.op("dve", lambda e, m2=m2, ps2=ps2, oc=oc, sgt=sgt: e.tensor_tensor(out=m2[:], in0=ps2[:], in1=sgt[:, 8 + oc, :], op=ALU.mult),
                         reads=[Rps2, Rsgt], writes=[Rm2])
                    S.op("pool", lambda e, m1=m1, m2=m2, oc=oc, mT=mT: e.tensor_tensor(out=mT[:, oc, :], in0=m1[:], in1=m2[:], op=ALU.add),
                         reads=[Rm1, Rm2], writes=[RmT])
                self.xupdate(xt, Rxt, 4, mT, RmT, 8, wout, Rwout, pss)
                S.dma("pool", xres[s, t0:t0 + TT, :].rearrange("(b p) d -> p b d", p=128), xt[:], reads=[Rxt], writes=[Rx])
        self.end()

    def phase_p5(self, l):
        S, T, NS = self.S, self.T, self.NS
        self.begin()
        TT = 512
        wq, Rwq = self.tile("wq", [128, 8, D], BF16)
        wo, Rwo = self.tile("wo", [128, 8, D], BF16)
        wkv, Rwkv = self.tile("wkv", [128, 8, 2 * D], BF16)
        stg = self.rot("wstg", [128, 2048], F32, 2)
        ident, Rid, _, _ = self.make_ident()
        gb, Rgb = self.tile("gb", [128, D], F32)
        gm, Rgm = self.tile("gm", [128, D], F32)
        ones, Rones = self.tile("ones", [128, 128], BF16)
        S.op("pool", lambda e: e.memset(ones[:], 1.0), writes=[Rones])
        self.load_bcast(gb, Rgb, self.w["norm_x"][l], D)
        self.load_bcast(gm, Rgm, self.w["norm_mem"][l], D)
        self.load_weight(wkv, Rwkv, self.w["w_xkv"][l], D, 2 * D, stg)
        self.load_weight(wq, Rwq, self.w["w_xq"][l], D, D, stg)
        self.load_weight(wo, Rwo, self.w["w_xo"][l], D, D, stg)
        tmp = self.norm_tmp()
        xts = self.rot("xt", [128, 4, D], F32, 2)
        hTs = self.rot("hT", [128, 8, TT], BF16, 2)
        qTs = self.rot("qT", [128, 8, TT], BF16, 1)
        oTs = self.rot("oT", [128, 8, TT], BF16, 2)
        Es = self.rot("E", [128, 2, TT], BF16, 2)
        rdens = self.rot("rden", [128, TT], F32, 2)
        mt, Rmt = self.tile("memt", [128, 2, D], F32)
        mnT, RmnT = self.tile("memnT", [128, 8, NMEM], BF16)
        kT, RkT = self.tile("kT", [128, 8, NMEM], BF16)
        Vm, RVm = self.tile("Vm", [128, 2, D], BF16)
        pss = self.rot("ps", [128, 512], F32, 6, ps=True)
        xres = self.scr["xres"]
        for s in range(NS):
            S.dma("sp", mt[:], self.mem_in[s].rearrange("(b p) d -> p b d", p=128), writes=[Rmt])
            for b in range(2):
                self.norm_to_hT(mt, Rmt, b, gm, Rgm, mnT, RmnT, b * 128, ident, Rid, tmp)
            for cc in range(8):
                ps, Rps = pss.next()

                def mmk(e, ps=ps, cc=cc):
                    for kc in range(8):
                        ins = e.matmul(ps[:, 0:NMEM], lhsT=wkv[:, kc, cc * 128:(cc + 1) * 128], rhs=mnT[:, kc, :], start=(kc == 0), stop=(kc == 7))
                    return ins
                S.op("pe", mmk, reads=[Rwkv, RmnT], writes=[Rps])
                S.op("dve", lambda e, ps=ps, cc=cc: e.tensor_copy(out=kT[:, cc, :], in_=ps[:, 0:NMEM]), reads=[Rps], writes=[RkT])
            for mb in range(2):
                for half in range(2):
                    ps, Rps = pss.next()

                    def mmv(e, ps=ps, mb=mb, half=half):
                        for kc in range(8):
                            ins = e.matmul(ps[:], lhsT=mnT[:, kc, mb * 128:(mb + 1) * 128], rhs=wkv[:, kc, D + half * 512:D + (half + 1) * 512],
                                           start=(kc == 0), stop=(kc == 7))
                        return ins
                    S.op("pe", mmv, reads=[Rwkv, RmnT], writes=[Rps])
                    S.op("dve", lambda e, ps=ps, mb=mb, half=half: e.tensor_copy(out=Vm[:, mb, half * 512:(half + 1) * 512], in_=ps[:]),
                         reads=[Rps], writes=[RVm])
            for t0 in range(0, T, TT):
                xt, Rxt = xts.next()
                hT, RhT = hTs.next()
                qT, RqT = qTs.next()
                oT, RoT = oTs.next()
                Rx = self.dres(("xres", s, t0))
                S.dma("sp", xt[:], xres[s, t0:t0 + TT, :].rearrange("(b p) d -> p b d", p=128), reads=[Rx], writes=[Rxt])
                for b in range(4):
                    self.norm_to_hT(xt, Rxt, b, gb, Rgb, hT, RhT, b * 128, ident, Rid, tmp)
                for cc in range(8):
                    ps, Rps = pss.next()

                    def mmq(e, ps=ps, cc=cc, hT=hT):
                        for kc in range(8):
                            ins = e.matmul(ps[:], lhsT=wq[:, kc, cc * 128:(cc + 1) * 128], rhs=hT[:, kc, :], start=(kc == 0), stop=(kc == 7))
                        return ins
                    S.op("pe", mmq, reads=[Rwq, RhT], writes=[Rps])
                    S.op("dve", lambda e, ps=ps, cc=cc, qT=qT: e.tensor_copy(out=qT[:, cc, :], in_=ps[:]), reads=[Rps], writes=[RqT])
                for hd in range(4):
                    E, RE = Es.next()
                    rden, Rrden = rdens.next()
                    for mb in range(2):
                        ps, Rps = pss.next()

                        def mms(e, ps=ps, mb=mb, hd=hd, qT=qT):
                            for j in range(2):
                                ins = e.matmul(ps[:], lhsT=kT[:, 2 * hd + j, mb * 128:(mb + 1) * 128], rhs=qT[:, 2 * hd + j, :], start=(j == 0), stop=(j == 1))
                            return ins
                        S.op("pe", mms, reads=[RkT, RqT], writes=[Rps])
                        S.op("act", lambda e, ps=ps, mb=mb, E=E: e.activation(out=E[:, mb, :], in_=ps[:], func=AF.Exp, scale=1.0 / 16.0),
                             reads=[Rps], writes=[RE])
                    ps, Rps = pss.next()

                    def mmd(e, ps=ps, E=E):
                        for mb in range(2):
                            ins = e.matmul(ps[:], lhsT=ones[:], rhs=E[:, mb, :], start=(mb == 0), stop=(mb == 1))
                        return ins
                    S.op("pe", mmd, reads=[Rones, RE], writes=[Rps])
                    S.op("dve", lambda e, ps=ps, rden=rden: e.reciprocal(out=rden[:], in_=ps[:]), reads=[Rps], writes=[Rrden])
                    for j in range(2):
                        ps, Rps = pss.next()

                        def mmo(e, ps=ps, hd=hd, j=j, E=E):
                            for mb in range(2):
                                c0 = hd * 256 + j * 128
                                ins = e.matmul(ps[:], lhsT=Vm[:, mb, c0:c0 + 128], rhs=E[:, mb, :], start=(mb == 0), stop=(mb == 1))
                            return ins
                        S.op("pe", mmo, reads=[RVm, RE], writes=[Rps])
                        S.op("dve", lambda e, ps=ps, hd=hd, j=j, oT=oT, rden=rden: e.tensor_tensor(out=oT[:, 2 * hd + j, :], in0=ps[:], in1=rden[:], op=ALU.mult),
                             reads=[Rps, Rrden], writes=[RoT])
                self.xupdate(xt, Rxt, 4, oT, RoT, 8, wo, Rwo, pss)
                S.dma("pool", xres[s, t0:t0 + TT, :].rearrange("(b p) d -> p b d", p=128), xt[:], reads=[Rxt], writes=[Rx])
        self.end()

    def phase_p6(self, l):
        S, T, NS = self.S, self.T, self.NS
        self.begin()
        TT = 256
        NB = TT // 128
        last = (l == self.DEPTH - 1)
        w1, Rw1 = self.tile("w1", [128, 8, DFF], BF16)
        w2, Rw2 = self.tile("w2", [128, 32, D], BF16)
        stg = self.rot("wstg", [128, 2048], F32, 2)
        ident, Rid, _, _ = self.make_ident()
        gb, Rgb = self.tile("gb", [128, D], F32)
        self.load_bcast(gb, Rgb, self.w["norm_ff"][l], D)
        if last:
            gf, Rgf = self.tile("gf", [128, D], F32)
            self.load_bcast(gf, Rgf, self.w["norm_final"], D)
        self.load_weight(w1, Rw1, self.w["w_ff1"][l], D, DFF, stg)
        self.load_weight(w2, Rw2, self.w["w_ff2"][l], DFF, D, stg)
        tmp = self.norm_tmp()
        xts = self.rot("xt", [128, NB, D], F32, 2)
        hTs = self.rot("hT", [128, 8, TT], BF16, 2)
        uTs = self.rot("uT", [128, 32, TT], BF16, 1)
        rls = self.rot("rl", [128, TT], BF16, 3)
        pss = self.rot("ps", [128, 512], F32, 6, ps=True)
        xres = self.scr["xres"]
        for s in range(NS):
            for t0 in range(0, T, TT):
                xt, Rxt = xts.next()
                hT, RhT = hTs.next()
                uT, RuT = uTs.next()
                Rx = self.dres(("xres", s, t0 // 512 * 512))
                S.dma("sp", xt[:], xres[s, t0:t0 + TT, :].rearrange("(b p) d -> p b d", p=128), reads=[Rx], writes=[Rxt])
                for b in range(NB):
                    self.norm_to_hT(xt, Rxt, b, gb, Rgb, hT, RhT, b * 128, ident, Rid, tmp)
                for fc in range(32):
                    ps, Rps = pss.next()
                    rl, Rrl = rls.next()

                    def mm1(e, ps=ps, fc=fc, hT=hT):
                        for kc in range(8):
                            ins = e.matmul(ps[:, 0:TT], lhsT=w1[:, kc, fc * 128:(fc + 1) * 128], rhs=hT[:, kc, :], start=(kc == 0), stop=(kc == 7))
                        return ins
                    S.op("pe", mm1, reads=[Rw1, RhT], writes=[Rps])
                    S.op("act", lambda e, ps=ps, rl=rl: e.activation(out=rl[:], in_=ps[:, 0:TT], func=AF.Relu), reads=[Rps], writes=[Rrl])
                    S.op("pool", lambda e, rl=rl, fc=fc, uT=uT: e.tensor_tensor(out=uT[:, fc, :], in0=rl[:], in1=rl[:], op=ALU.mult),
                         reads=[Rrl], writes=[RuT])
                self.xupdate(xt, Rxt, NB, uT, RuT, 32, w2, Rw2, pss)
                if not last:
                    S.dma("pool", xres[s, t0:t0 + TT, :].rearrange("(b p) d -> p b d", p=128), xt[:], reads=[Rxt], writes=[Rx])
                else:
                    for b in range(NB):
                        junk, Rjunk = tmp["junk"].next()
                        ss, Rss = tmp["ss"].next()
                        S.op("act", lambda e, junk=junk, ss=ss, b=b, xt=xt: e.activation(out=junk[:], in_=xt[:, b, :], func=AF.Square, accum_out=ss[:, 0:1]),
                             reads=[Rxt], writes=[Rjunk, Rss])
                        S.op("act", lambda e, ss=ss: e.activation(out=ss[:, 1:2], in_=ss[:, 0:1], func=AF.Sqrt, scale=1.0 / D, bias=self.eps_t[:, 0:1]),
                             reads=[Rss, self.Reps], writes=[Rss])
                        S.op("dve", lambda e, ss=ss: e.reciprocal(out=ss[:, 2:3], in_=ss[:, 1:2]), reads=[Rss], writes=[Rss])
                        S.op("dve", lambda e, ss=ss, b=b, xt=xt: e.scalar_tensor_tensor(out=xt[:, b, :], in0=xt[:, b, :], scalar=ss[:, 2:3], in1=gf[:],
                                                                                         op0=ALU.mult, op1=ALU.mult),
                             reads=[Rxt, Rss, Rgf], writes=[Rxt])
                    Ry = self.dres(("y", s))
                    S.dma("pool", self.y_out[s, t0:t0 + TT, :].rearrange("(b p) d -> p b d", p=128), xt[:], reads=[Rxt], writes=[Ry])
        self.end()


_CACHE = {}


def _get_nc(T, NS, DEPTH, dbg=(), stop=None):
    key = (T, NS, DEPTH, tuple(dbg), stop)
    if key not in _CACHE:
        _CACHE[key] = Builder(T, NS, DEPTH, dbg, stop).build()
    return _CACHE[key]


def kernel(**inputs):
    NCORES, NS, T, DEPTH = 8, 2, 4096, 2
    xs = np.concatenate([np.asarray(inputs["x_prompt"], np.float32), np.asarray(inputs["x_sample"], np.float32)], 0)
    ms = np.concatenate([np.asarray(inputs["mem_prompt"], np.float32), np.asarray(inputs["mem_sample"], np.float32)], 0)
    nseq = xs.shape[0]
    slots = [[c, 8 + c if 8 + c < nseq else c] for c in range(NCORES)]
    wnames = [n for n, _ in WEIGHT_SPECS] + ["norm_final"]
    wmap = {n: np.ascontiguousarray(np.asarray(inputs[n], np.float32)) for n in wnames}
    nc = _get_nc(T, NS, DEPTH)
    in_maps = []
    for c in range(NCORES):
        m = dict(wmap)
        m["x"] = np.ascontiguousarray(xs[slots[c]])
        m["mem"] = np.ascontiguousarray(ms[slots[c]])
        in_maps.append(m)
    res = run_bass_kernel_spmd(nc, in_maps, core_ids=list(range(NCORES)))
    y = np.zeros_like(xs)
    for c in range(NCORES):
        yc = res.results[c]["y"]
        y[slots[c][0]] = yc[0]
        if slots[c][1] != slots[c][0]:
            y[slots[c][1]] = yc[1]
    nb = np.asarray(inputs["x_prompt"]).shape[0]
    return (y[:nb], y[nb:])
```

```python
import numpy as np
from contextlib import ExitStack
import concourse.bass as bass
import concourse.mybir as mybir
from concourse.bass_utils import run_bass_kernel_spmd

F32 = mybir.dt.float32
BF16 = mybir.dt.bfloat16
AF = mybir.ActivationFunctionType
ALU = mybir.AluOpType
AX = mybir.AxisListType

D = 1024
DIN = 5504
RW = 1920
NQ0 = 1920
NV0 = 2944
G0 = 3456
NMEM = 256
DFF = 4096
EPS = 1e-6
GN_EPS = 1e-5 * 64
CH = 128

EPOCH = 12000
ENGS = ("pe", "dve", "act", "pool", "sp")


class Res:
    __slots__ = ("name", "last_w", "readers", "dsem", "w_is_dma")

    def __init__(self, name=""):
        self.name = name
        self.last_w = None
        self.readers = []
        self.dsem = None
        self.w_is_dma = False


class Sched:
    def __init__(self, nc, stack):
        self.nc = nc
        self.stack = stack
        self.ops = {e: [] for e in ENGS}
        self.cnt = {e: 0 for e in ENGS}
        self.esems = {e: [] for e in ENGS}
        self.known = {e: {} for e in ENGS}
        self.sems = {}
        self.nsem = 0
        self.same_engine_raw = True
        self.dma_latest = {}
        self.free_dsems = []

    def new_sem(self, tag):
        h = self.stack.enter_context(self.nc.semaphore(f"{tag}_{self.nsem}"))
        sid = self.nsem
        self.nsem += 1
        self.sems[sid] = h
        return sid

    def _eng_point(self, eng):
        n = self.cnt[eng]
        self.cnt[eng] = n + 1
        ep, v = divmod(n, EPOCH)
        while len(self.esems[eng]) <= ep:
            self.esems[eng].append(self.new_sem(f"e{eng}"))
        return (self.esems[eng][ep], v + 1)

    def _dma_point(self, res):
        d = res.dsem
        if d is None or d[1] + 16 > EPOCH:
            if self.free_dsems and d is None:
                d = self.free_dsems.pop()
            else:
                d = [self.new_sem("d"), 0]
            res.dsem = d
        d[1] += 16
        self.dma_latest[d[0]] = d[1]
        return (d[0], d[1])

    def recycle(self, resources):
        seen = set()
        for r in resources:
            d = r.dsem
            if d is not None and id(d) not in seen and d[1] + 64 < EPOCH:
                seen.add(id(d))
                self.free_dsems.append(d)
            r.dsem = None

    def _need(self, eng, waits, pt):
        sid, val = pt
        if self.known[eng].get(sid, 0) >= val:
            return
        if waits.get(sid, 0) < val:
            waits[sid] = val

    def _deps(self, eng, reads, writes, is_dma=False):
        waits = {}
        for r in reads:
            if r.last_w is not None:
                w_eng = r.last_w[2]
                if w_eng != eng or is_dma or (self.same_engine_raw and eng != "pe"):
                    self._need(eng, waits, r.last_w[:2])
        for w in writes:
            if w.last_w is not None:
                w_eng = w.last_w[2]
                same_dma_group = is_dma and w.w_is_dma and not w.readers
                if (w_eng != eng or is_dma) and not same_dma_group:
                    self._need(eng, waits, w.last_w[:2])
            for (sid, val, r_eng) in w.readers:
                if r_eng != eng or is_dma:
                    self._need(eng, waits, (sid, val))
        for sid, val in waits.items():
            self.known[eng][sid] = val
        return list(waits.items())

    def op(self, eng, fn, reads=(), writes=()):
        waits = self._deps(eng, reads, writes)
        pt = self._eng_point(eng)
        self.ops[eng].append((waits, fn, pt[0], 1))
        for r in reads:
            r.readers.append((pt[0], pt[1], eng))
        for w in writes:
            w.last_w = (pt[0], pt[1], eng)
            w.readers = []
            w.w_is_dma = False
        return pt

    def dma(self, queue, out, in_, reads=(), writes=(), **kw):
        waits = self._deps(queue, reads, writes, is_dma=True)
        pt = self._dma_point(writes[0])

        def fn(e, out=out, in_=in_, kw=kw):
            return e.dma_start(out=out, in_=in_, **kw)
        self.ops[queue].append((waits, fn, pt[0], 16))
        for r in reads:
            r.readers.append((pt[0], pt[1], "dma"))
        for w in writes:
            w.last_w = (pt[0], pt[1], "dma")
            w.readers = []
            w.w_is_dma = True
            if w is not writes[0]:
                w.dsem = writes[0].dsem
        return pt

    def final_wait(self, eng, resources):
        waits = {}
        for r in resources:
            if r.last_w is not None:
                sid, val = r.last_w[:2]
                waits[sid] = max(waits.get(sid, 0), val)
        self.ops[eng].append((list(waits.items()), None, None, 0))

    def barrier(self):
        pts = {}
        for e in ENGS:
            n = self.cnt[e]
            if n > 0:
                ep, v = divmod(n - 1, EPOCH)
                pts[self.esems[e][ep]] = v + 1
        for sid, val in self.dma_latest.items():
            pts[sid] = max(pts.get(sid, 0), val)
        for e in ENGS:
            waits = []
            for sid, val in pts.items():
                if self.known[e].get(sid, 0) < val:
                    waits.append((sid, val))
                    self.known[e][sid] = val
            self.ops[e].append((waits, None, None, 0))

    def emit(self):
        nc = self.nc
        sems = self.sems

        def run(engname):
            def body(e):
                for (waits, fn, sid_, inc) in self.ops[engname]:
                    for sid, val in waits:
                        e.wait_ge(sems[sid], val)
                    if fn is None:
                        continue
                    ins = fn(e)
                    ins.then_inc(sems[sid_], inc)
            return body

        with nc.Block() as block:
            block.tensor(run("pe"))
            block.vector(run("dve"))
            block.scalar(run("act"))
            block.gpsimd(run("pool"))
            block.sync(run("sp"))
        self.ops = {e: [] for e in ENGS}


class Rot:
    def __init__(self, items):
        self.items = items
        self.i = 0

    def next(self):
        it = self.items[self.i % len(self.items)]
        self.i += 1
        return it


WEIGHT_SPECS = [
    ("norm_mix", (D,)), ("w_in", (D, DIN)), ("mu_prev", (RW,)), ("mu_next", (RW,)),
    ("w0", (2, 512)), ("w_up", (2, 64, 512)), ("a0", (2, 512)), ("a_up", (2, 64, 512)),
    ("g_up", (128, 512)), ("k_k", (512,)), ("k_a", (512,)), ("r_k", (8, 64)),
    ("gn_g", (512,)), ("gn_b", (512,)), ("rpb", (8, 15, 31)),
    ("w_br_rwkv", (512, D)), ("w_br_nat", (512, D)), ("w_out", (D, D)),
    ("norm_x", (D,)), ("norm_mem", (D,)), ("w_xq", (D, D)), ("w_xkv", (D, 2 * D)), ("w_xo", (D, D)),
    ("norm_ff", (D,)), ("w_ff1", (D, DFF)), ("w_ff2", (DFF, D)),
]


class Builder:
    def __init__(self, T, NS, DEPTH, dbg=(), stop=None, ext_in=(), phases=None):
        self.T, self.NS, self.DEPTH = T, NS, DEPTH
        self.ext_in = set(ext_in)
        self.phases = phases or ("p1", "p3", "p2", "p4", "p5", "p6")
        self.dbg = set(dbg)
        self.stop = stop
        self.nc = bass.Bass("TRN2", target_bir_lowering=False)
        nc = self.nc
        self.x_in = nc.dram_tensor("x", [NS, T, D], F32, kind="ExternalInput").ap()
        self.mem_in = nc.dram_tensor("mem", [NS, NMEM, D], F32, kind="ExternalInput").ap()
        self.w = {}
        for name, shp in WEIGHT_SPECS:
            self.w[name] = nc.dram_tensor(name, [DEPTH] + list(shp), F32, kind="ExternalInput").ap()
        self.w["norm_final"] = nc.dram_tensor("norm_final", [D], F32, kind="ExternalInput").ap()
        self.y_out = nc.dram_tensor("y", [NS, T, D], F32, kind="ExternalOutput").ap()
        self.scr = {}
        self.scr_res = {}

    def scratch(self, name, shape, dt):
        kind = "ExternalOutput" if name in self.dbg else ("ExternalInput" if name in self.ext_in else "Internal")
        t = self.nc.dram_tensor(name, shape, dt, kind=kind).ap()
        self.scr[name] = t
        return t

    def dres(self, key):
        r = self.scr_res.get(key)
        if r is None:
            r = Res(str(key))
            self.scr_res[key] = r
        return r

    def build(self):
        nc = self.nc
        T, NS = self.T, self.NS
        self.scratch("xres", [NS, T, D], F32)
        self.scratch("pf", [NS, RW, T], F32)
        self.scratch("nq", [NS, 512, T], BF16)
        self.scratch("nk", [NS, 512, T], BF16)
        self.scratch("nv", [NS, T, 512], BF16)
        self.scratch("sg", [NS, 2048, T], BF16)
        self.scratch("yr", [NS, T, 512], BF16)
        self.scratch("yn", [NS, T, 512], BF16)
        self.scratch("yf", [NS, T, 512], F32)
        self.scratch("bon", [NS, T, 8], F32)
        self.scratch("phd", [NS, T // 128, 128, 1920], F32)
        with ExitStack() as st:
            self.S = Sched(nc, st)
            done = False
            for l in range(self.DEPTH):
                for ph in self.phases:
                    getattr(self, "phase_" + ph)(l)
                    if self.stop == (l, ph):
                        done = True
                        break
                if done:
                    break
            with ExitStack() as ph:
                self.S.final_wait("pool", list(self.scr_res.values()))
                self.S.emit()
        return nc

    def begin(self):
        self.S.barrier()
        self.phc = getattr(self, "phc", 0) + 1
        self.ph = ExitStack()
        self.ph_res = []
        return self.ph

    def end(self):
        self.S.emit()
        self.S.recycle(self.ph_res)
        self.ph.close()

    def tile(self, name, shape, dt):
        t = self.ph.enter_context(self.nc.sbuf_tensor(f"{name}_{self.phc}", shape, dt))
        r = Res(name)
        self.ph_res.append(r)
        return t, r

    def psum(self, name, shape, dt):
        t = self.ph.enter_context(self.nc.psum_tensor(f"{name}_{self.phc}", shape, dt))
        r = Res(name)
        self.ph_res.append(r)
        return t, r

    def rot(self, name, shape, dt, n, ps=False):
        f = self.psum if ps else self.tile
        return Rot([f(f"{name}{i}", shape, dt) for i in range(n)])

    def make_ident(self):
        S = self.S
        idf, Ridf = self.tile("identf", [128, 128], F32)
        idb, Ridb = self.tile("identb", [128, 128], BF16)

        S.op("pool", lambda e: e.memset(idf[:], 0.0), writes=[Ridf])
        S.op("pool", lambda e: e.affine_select(out=idf[:], in_=idf[:], pattern=[[-1, 128]], compare_op=ALU.not_equal,
                                               fill=1.0, base=0, channel_multiplier=1), reads=[Ridf], writes=[Ridf])
        S.op("dve", lambda e: e.tensor_copy(out=idb[:], in_=idf[:]), reads=[Ridf], writes=[Ridb])
        return idb, Ridb, idf, Ridf

    def load_weight(self, dst, Rdst, src2d, K, cols, stg, col0=0, kc0=0):
        S = self.S
        nk = K // 128
        srcv = src2d.rearrange("(kc p) c -> p kc c", p=128)
        engs = ("pool", "dve", "act")
        i = 0
        for kc in range(nk):
            for c0 in range(0, cols, 2048):
                cw = min(2048, cols - c0)
                stile, Rst = stg.next()
                S.dma("sp", stile[:, 0:cw], srcv[:, kc, c0:c0 + cw], writes=[Rst])
                eng = engs[i % 3]
                i += 1
                o = dst[:, kc0 + kc, col0 + c0:col0 + c0 + cw]
                if eng == "act":
                    S.op("act", lambda e, o=o, s=stile, cw=cw: e.copy(out=o, in_=s[:, 0:cw]), reads=[Rst], writes=[Rdst])
                else:
                    S.op(eng, lambda e, o=o, s=stile, cw=cw: e.tensor_copy(out=o, in_=s[:, 0:cw]), reads=[Rst], writes=[Rdst])

    def load_bcast(self, dst, Rdst, src1d, n):
        self.S.dma("sp", dst[:, 0:n], src1d.rearrange("(o n) -> o n", o=1).partition_broadcast(128), writes=[Rdst])

    def norm_to_hT(self, xt, Rxt, b, gb, Rgb, hT, RhT, tcol, ident, Rid, tmp):
        S = self.S
        junk, Rjunk = tmp["junk"].next()
        ss, Rss = tmp["ss"].next()
        h, Rh = tmp["h"].next()
        pT, RpT = tmp["pT"].next()
        S.op("act", lambda e: e.activation(out=junk[:], in_=xt[:, b, :], func=AF.Square, accum_out=ss[:, 0:1]),
             reads=[Rxt], writes=[Rjunk, Rss])

        S.op("act", lambda e: e.activation(out=ss[:, 1:2], in_=ss[:, 0:1], func=AF.Sqrt, scale=1.0 / D, bias=self.eps_t[:, 0:1]),
             reads=[Rss, self.Reps], writes=[Rss])
        S.op("dve", lambda e: e.reciprocal(out=ss[:, 2:3], in_=ss[:, 1:2]), reads=[Rss], writes=[Rss])
        S.op("dve", lambda e: e.scalar_tensor_tensor(out=h[:], in0=xt[:, b, :], scalar=ss[:, 2:3], in1=gb[:],
                                                     op0=ALU.mult, op1=ALU.mult),
             reads=[Rxt, Rss, Rgb], writes=[Rh])

        def tr(e):
            for kc in range(8):
                ins = e.transpose(out=pT[:, kc * 128:(kc + 1) * 128], in_=h[:, kc * 128:(kc + 1) * 128], identity=ident[:])
            return ins
        S.op("pe", tr, reads=[Rh, Rid], writes=[RpT])
        S.op("act", lambda e: e.copy(out=hT[:, :, tcol:tcol + 128], in_=pT[:].rearrange("p (k t) -> p k t", k=8)),
             reads=[RpT], writes=[RhT])
        return ss

    def norm_tmp(self):
        self.eps_t, Re = self.tile("eps_t", [128, 2], F32)
        eps_t = self.eps_t
        self.S.op("pool", lambda e: e.memset(eps_t[:], EPS), writes=[Re])
        self.Reps = Re
        return {
            "junk": self.rot("junk", [128, D], BF16, 1),
            "ss": self.rot("ss", [128, 4], F32, 4),
            "h": self.rot("h", [128, D], BF16, 2),
            "pT": self.rot("pT", [128, 1024], BF16, 2, ps=True),
        }

    def phase_p1(self, l):
        S, T, NS = self.S, self.T, self.NS
        self.begin()
        TT = 512
        win, Rwin = self.tile("win", [128, 8, DIN], BF16)
        stg = self.rot("wstg", [128, 2048], F32, 2)
        gb, Rgb = self.tile("gb", [128, D], F32)
        ident, Rid, _, _ = self.make_ident()
        self.load_bcast(gb, Rgb, self.w["norm_mix"][l], D)
        self.load_weight(win, Rwin, self.w["w_in"][l], D, DIN, stg)
        tmp = self.norm_tmp()
        xts = self.rot("xt", [128, 4, D], F32, 2)
        hTs = self.rot("hT", [128, 8, TT], BF16, 2)
        pss = self.rot("ps", [128, 512], F32, 4, ps=True)
        of32 = self.rot("of32", [128, 512], F32, 3)
        obf = self.rot("obf", [128, 512], BF16, 4)
        xsrc = self.x_in if l == 0 else self.scr["xres"]
        pf, nq, nk, nv, sg = (self.scr[k] for k in ("pf", "nq", "nk", "nv", "sg"))
        ev = 0
        for s in range(NS):
            for t0 in range(0, T, TT):
                xt, Rxt = xts.next()
                hT, RhT = hTs.next()
                xr = [self.dres(("xres", s, t0))] if l > 0 else []
                S.dma("sp", xt[:], xsrc[s, t0:t0 + TT, :].rearrange("(b p) d -> p b d", p=128), reads=xr, writes=[Rxt])
                for b in range(4):
                    self.norm_to_hT(xt, Rxt, b, gb, Rgb, hT, RhT, b * 128, ident, Rid, tmp)
                for cc in range(DIN // 128):
                    c0 = cc * 128
                    if NV0 <= c0 < G0:
                        continue
                    ps, Rps = pss.next()

                    def mm(e, ps=ps, c0=c0, hT=hT):
                        for kc in range(8):
                            ins = e.matmul(ps[:], lhsT=win[:, kc, c0:c0 + 128], rhs=hT[:, kc, :], start=(kc == 0), stop=(kc == 7))
                        return ins
                    S.op("pe", mm, reads=[Rwin, RhT], writes=[Rps])
                    if c0 < RW:
                        o, Ro = of32.next()
                        eng = "dve" if ev % 2 == 0 else "act"
                        ev += 1
                        if eng == "dve":
                            S.op("dve", lambda e, o=o, ps=ps: e.tensor_copy(out=o[:], in_=ps[:]), reads=[Rps], writes=[Ro])
                        else:
                            S.op("act", lambda e, o=o, ps=ps: e.copy(out=o[:], in_=ps[:]), reads=[Rps], writes=[Ro])
                        S.dma("pool", pf[s, c0:c0 + 128, t0:t0 + TT], o[:], reads=[Ro], writes=[self.dres(("pf", s))])
                    elif c0 < NV0:
                        o, Ro = obf.next()
                        S.op("dve", lambda e, o=o, ps=ps: e.tensor_copy(out=o[:], in_=ps[:]), reads=[Rps], writes=[Ro])
                        if c0 < NQ0 + 512:
                            S.dma("pool", nq[s, c0 - NQ0:c0 - NQ0 + 128, t0:t0 + TT], o[:], reads=[Ro], writes=[self.dres(("nq", s))])
                        else:
                            c1 = c0 - NQ0 - 512
                            S.dma("pool", nk[s, c1:c1 + 128, t0:t0 + TT], o[:], reads=[Ro], writes=[self.dres(("nk", s))])
                    else:
                        o, Ro = obf.next()
                        S.op("act", lambda e, o=o, ps=ps: e.activation(out=o[:], in_=ps[:], func=AF.Sigmoid), reads=[Rps], writes=[Ro])
                        c1 = c0 - G0
                        S.dma("pool", sg[s, c1:c1 + 128, t0:t0 + TT], o[:], reads=[Ro], writes=[self.dres(("sg", s))])
                for b in range(4):
                    ps, Rps = pss.next()

                    def mmv(e, ps=ps, b=b, hT=hT):
                        for kc in range(8):
                            ins = e.matmul(ps[:], lhsT=hT[:, kc, b * 128:(b + 1) * 128], rhs=win[:, kc, NV0:NV0 + 512], start=(kc == 0), stop=(kc == 7))
                        return ins
                    S.op("pe", mmv, reads=[Rwin, RhT], writes=[Rps])
                    o, Ro = obf.next()
                    S.op("dve", lambda e, o=o, ps=ps: e.tensor_copy(out=o[:], in_=ps[:]), reads=[Rps], writes=[Ro])
                    S.dma("pool", nv[s, t0 + b * 128:t0 + (b + 1) * 128, :], o[:], reads=[Ro], writes=[self.dres(("nv", s))])
        self.end()

    def phase_p2(self, l):
        S, T, NS = self.S, self.T, self.NS
        self.begin()
        C = 128
        NCH = T // C
        LD = 0.6065306597126334
        pf, yf, yr = self.scr["pf"], self.scr["yf"], self.scr["yr"]
        bon = self.scr["bon"]
        phd = self.scr["phd"]
        op = S.op
        bc = lambda ap, shape: ap.broadcast_to(shape)
        identb, Ridb, identf, Ridf = self.make_ident()
        mup, Rmup = self.tile("mup", [128, 15], F32)
        mun, Rmun = self.tile("mun", [128, 15], F32)
        w0t, Rw0 = self.tile("w0t", [128, 2, 4], F32)
        a0t, Ra0 = self.tile("a0t", [128, 2, 4], F32)
        kkc, Rkkc = self.tile("kkc", [128, 4], F32)
        kac, Rkac = self.tile("kac", [128, 4], F32)
        omka, Romka = self.tile("omka", [128, 4], F32)
        rkc, Rrkc = self.tile("rkc", [128, 4], F32)
        S.dma("sp", mup[:], self.w["mu_prev"][l].rearrange("(c p) -> p c", p=128), writes=[Rmup], allow_slow_non_contiguous=True)
        S.dma("sp", mun[:], self.w["mu_next"][l].rearrange("(c p) -> p c", p=128), writes=[Rmun], allow_slow_non_contiguous=True)
        for d in range(2):
            S.dma("sp", w0t[:, d, :], self.w["w0"][l, d].rearrange("(c p) -> p c", p=128), writes=[Rw0], allow_slow_non_contiguous=True)
            S.dma("sp", a0t[:, d, :], self.w["a0"][l, d].rearrange("(c p) -> p c", p=128), writes=[Ra0], allow_slow_non_contiguous=True)
        S.dma("sp", kkc[:], self.w["k_k"][l].rearrange("(c p) -> p c", p=128), writes=[Rkkc], allow_slow_non_contiguous=True)
        S.dma("sp", kac[:], self.w["k_a"][l].rearrange("(c p) -> p c", p=128), writes=[Rkac], allow_slow_non_contiguous=True)
        S.dma("sp", rkc[:], self.w["r_k"][l].rearrange("h n -> (h n)").rearrange("(c p) -> p c", p=128), writes=[Rrkc], allow_slow_non_contiguous=True)
        op("dve", lambda e: e.tensor_scalar(out=omka[:], in0=kac[:], scalar1=-1.0, scalar2=1.0, op0=ALU.mult, op1=ALU.add), reads=[Rkac], writes=[Romka])
        gng, Rgng = self.tile("gng", [128, 512], F32)
        gnb, Rgnb = self.tile("gnb", [128, 512], F32)
        self.load_bcast(gng, Rgng, self.w["gn_g"][l], 512)
        self.load_bcast(gnb, Rgnb, self.w["gn_b"][l], 512)
        wstg, Rwstg = self.tile("lstg", [128, 3, 512], F32)
        S.dma("sp", wstg[:, 0, :], self.w["w_up"][l].rearrange("d l c -> (d l) c"), writes=[Rwstg])
        S.dma("sp", wstg[:, 1, :], self.w["a_up"][l].rearrange("d l c -> (d l) c"), writes=[Rwstg])
        S.dma("sp", wstg[:, 2, :], self.w["g_up"][l], writes=[Rwstg])
        wupz, Rwupz = self.tile("wupz", [128, 2, 512], BF16)
        aupz, Raupz = self.tile("aupz", [128, 2, 512], BF16)
        gup, Rgup = self.tile("gup", [128, 512], BF16)
        op("pool", lambda e: e.memset(wupz[:].rearrange("p d c -> p (d c)"), 0.0), writes=[Rwupz])
        op("pool", lambda e: e.memset(aupz[:].rearrange("p d c -> p (d c)"), 0.0), writes=[Raupz])
        for d in range(2):
            op("dve", lambda e, d=d: e.tensor_copy(out=wupz[d * 64:(d + 1) * 64, d, :], in_=wstg[d * 64:(d + 1) * 64, 0, :]), reads=[Rwstg, Rwupz], writes=[Rwupz])
            op("dve", lambda e, d=d: e.tensor_copy(out=aupz[d * 64:(d + 1) * 64, d, :], in_=wstg[d * 64:(d + 1) * 64, 1, :]), reads=[Rwstg, Raupz], writes=[Raupz])
        op("dve", lambda e: e.tensor_copy(out=gup[:], in_=wstg[:, 2, :]), reads=[Rwstg], writes=[Rgup])
        onesf, Ronesf = self.tile("onesf", [128, 128], F32)
        op("pool", lambda e: e.memset(onesf[:], 1.0), writes=[Ronesf])
        blk1, Rblk1 = self.tile("blk1", [128, 128], F32)
        hsel, Rhsel = self.tile("hsel", [128, 2], F32)
        bdm, Rbdm = self.tile("bdm", [128, 4, 2, 64], F32)

        def mkz(e):
            e.memset(blk1[:], 0.0)
            e.memset(hsel[:], 0.0)
            return e.memset(bdm[:].rearrange("p c h i -> p (c h i)"), 0.0)
        op("dve", mkz, writes=[Rblk1, Rhsel, Rbdm])

        def mko(e):
            e.memset(blk1[0:64, 0:64], 1.0)
            e.memset(blk1[64:128, 64:128], 1.0)
            e.memset(hsel[0:64, 0:1], 1.0)
            e.memset(hsel[64:128, 1:2], 1.0)
            e.memset(bdm[0:64, :, 0, :], 1.0)
            return e.memset(bdm[64:128, :, 1, :], 1.0)
        op("dve", mko, reads=[Rblk1, Rhsel, Rbdm], writes=[Rblk1, Rhsel, Rbdm])
        gne, Rgne = self.tile("gne", [128, 1], F32)
        op("pool", lambda e: e.memset(gne[:], GN_EPS), writes=[Rgne])
        mbase, Rmbase = self.tile("mbase", [128, 4, 128], F32)

        op("pool", lambda e: e.memset(mbase[:].rearrange("p a x -> p (a x)"), 1.0), writes=[Rmbase])
        op("pool", lambda e: e.affine_select(out=mbase[:, 0, :], in_=mbase[:, 0, :], pattern=[[1, 128]], compare_op=ALU.is_gt, fill=0.0, base=0, channel_multiplier=-1), reads=[Rmbase], writes=[Rmbase])
        op("pool", lambda e: e.affine_select(out=mbase[:, 1, :], in_=mbase[:, 1, :], pattern=[[1, 128]], compare_op=ALU.is_ge, fill=0.0, base=0, channel_multiplier=-1), reads=[Rmbase], writes=[Rmbase])
        op("pool", lambda e: e.affine_select(out=mbase[:, 2, :], in_=mbase[:, 2, :], pattern=[[-1, 128]], compare_op=ALU.is_gt, fill=0.0, base=0, channel_multiplier=1), reads=[Rmbase], writes=[Rmbase])
        op("pool", lambda e: e.affine_select(out=mbase[:, 3, :], in_=mbase[:, 3, :], pattern=[[-1, 128]], compare_op=ALU.is_ge, fill=0.0, base=0, channel_multiplier=1), reads=[Rmbase], writes=[Rmbase])
        mG, RmG = self.tile("mG", [128, 2, 2, 2, 128], BF16)
        mL, RmL = self.tile("mL", [128, 2, 2, 128], BF16)
        for d in range(2):
            si, ii, li = (0, 1, 2) if d == 0 else (2, 3, 0)
            for h2 in range(2):
                op("dve", lambda e, d=d, h2=h2, si=si: e.tensor_copy(out=mG[:, d, h2, 0, :], in_=mbase[:, si, :]), reads=[Rmbase], writes=[RmG])
                op("dve", lambda e, d=d, h2=h2, ii=ii: e.tensor_copy(out=mG[:, d, h2, 1, :], in_=mbase[:, ii, :]), reads=[Rmbase], writes=[RmG])
                op("dve", lambda e, d=d, h2=h2, li=li: e.tensor_copy(out=mL[:, d, h2, :], in_=mbase[:, li, :]), reads=[Rmbase], writes=[RmL])
        Eg, REg = self.tile("Eg", [8, 3, 128], F32)
        op("pool", lambda e: e.memset(Eg[:].rearrange("p a x -> p (a x)"), 1.0), writes=[REg])
        for gi, b_ in enumerate((16, 32, 64)):
            op("pool", lambda e, gi=gi, b_=b_: e.affine_select(out=Eg[:, gi, :], in_=Eg[:, gi, :], pattern=[[1, 128]], compare_op=ALU.is_ge, fill=0.0, base=0, channel_multiplier=-b_),
               reads=[REg], writes=[REg])
            op("pool", lambda e, gi=gi, b_=b_: e.affine_select(out=Eg[:, gi, :], in_=Eg[:, gi, :], pattern=[[-1, 128]], compare_op=ALU.is_ge, fill=0.0, base=b_ - 1, channel_multiplier=b_),
               reads=[REg], writes=[REg])
        bdf, Rbdf = self.tile("bdf", [128, 4, 128], F32)
        op("pool", lambda e: e.memset(bdf[:, 3, :], 1.0), writes=[Rbdf])
        pbm, Rpbm = self.psum("pbm", [128, 512], F32)

        def mmE(e):
            for gi in range(3):
                ins = e.matmul(pbm[:, gi * 128:(gi + 1) * 128], lhsT=Eg[:, gi, :], rhs=Eg[:, gi, :], start=True, stop=True)
            return ins
        op("pe", mmE, reads=[REg], writes=[Rpbm])
        op("dve", lambda e: e.tensor_copy(out=bdf[:, 0:3, :].rearrange("p a x -> p (a x)"), in_=pbm[:, 0:384]), reads=[Rpbm, Rbdf], writes=[Rbdf])
        bd16, Rbd16 = self.tile("bd16", [128, 2, 128], BF16)
        offm, Roffm = self.tile("offm", [128, 3, 4, 128], BF16)
        for h2 in range(4):
            if h2 < 2:
                op("dve", lambda e, h2=h2: e.tensor_copy(out=bd16[:, h2, :], in_=bdf[:, 0, :]), reads=[Rbdf], writes=[Rbd16])
            for lv in range(3):
                op("dve", lambda e, h2=h2, lv=lv: e.tensor_tensor(out=offm[:, lv, h2, :], in0=bdf[:, lv + 1, :], in1=bdf[:, lv, :], op=ALU.subtract),
                   reads=[Rbdf], writes=[Roffm])
        Ps = self.rot("P", [128, 15, 130], F32, 1)
        tAs = self.rot("tA", [128, 15, 128], F32, 1)
        tBs = self.rot("tB", [128, 15, 128], F32, 1)
        phs = self.rot("ph", [128, 15, 128], F32, 1)
        f4 = lambda n, k=1: self.rot(n, [128, 4, 128], F32, k)
        b4 = lambda n, k=1: self.rot(n, [128, 4, 128], BF16, k)
        lws, avs, pres, cums, cprs = f4("lw"), f4("av"), f4("pre"), f4("cum"), f4("cpr")
        ecs, ecps, eis = f4("ec"), f4("ecp"), f4("ei")
        kraws, sqs, nrms, kkvs, tts, kds, bvs = (f4(n) for n in ("kraw", "sq", "nrm", "kkv", "tt", "kd", "bv"))
        Kfs, Bfs, rks = kraws, nrms, sqs
        ATs, RTs, KTs, BTs = (b4(n, 2) for n in ("AT", "RT", "KT", "BT"))
        KhTs, BhTs, vTs = (b4(n, 1) for n in ("KhT", "BhT", "vT"))
        twds = self.rot("twd", [128, 128], BF16, 2)
        adbs = self.rot("adb", [128, 128], BF16, 2)
        sgds = self.rot("sgd", [128, 128], BF16, 2)
        gCs = self.rot("gC", [128, 4], F32, 2)
        ARbds = self.rot("ARbd", [128, 4, 2, 2, 128], BF16, 2)
        Bbds = self.rot("Bbd", [128, 4, 2, 128], BF16, 2)
        for (t_, r_) in ARbds.items:
            op("pool", lambda e, t_=t_: e.memset(t_[:].rearrange("p c h x t -> p (c h x t)"), 0.0), writes=[r_])
        for (t_, r_) in Bbds.items:
            op("pool", lambda e, t_=t_: e.memset(t_[:].rearrange("p c h t -> p (c h t)"), 0.0), writes=[r_])
        VKs = self.rot("VK", [128, 1024], BF16, 2)
        Bhats = self.rot("Bhat", [128, 512], BF16, 2)
        GKs = self.rot("GK", [128, 4, 2, 2, 128], BF16, 2)
        GBs = self.rot("GB", [128, 4, 2, 2, 128], BF16, 2)
        Lhs = self.rot("Lh", [128, 4, 2, 128], BF16, 1)
        lmq = [[self.tile(f"lmq{h}_{k}", [128, 4, 128], BF16) for k in range(2)] for h in range(8)]
        zzt = [self.tile(f"zz{h}", [128, 2, 2, 128], BF16) for h in range(4)]
        ttt = [[self.tile(f"tt{h}_{k}", [128, 2, 2, 128], BF16) for k in range(2)] for h in range(4)]
        tfin = [self.rot(f"tfin{h}", [128, 2, 2, 128], BF16, 2) for h in range(4)]
        L16s = self.rot("L16", [128, 4, 2, 128], BF16, 1)
        M16s = self.rot("M16", [128, 4, 2, 128], BF16, 1)
        St, RSt = self.tile("St", [128, 4, 2, 64], F32)
        Sbd, RSbd = self.tile("Sbd", [128, 4, 2, 64], BF16)
        stmp, Rstmp = self.tile("stmp", [128, 4, 2, 64], F32)
        Wsbs = self.rot("Wsb", [128, 512], BF16, 1)
        Usbs = self.rot("Usb", [128, 512], BF16, 1)
        Ysbs = self.rot("Ysb", [128, 512], F32, 2)
        bons = self.rot("bonv", [128, 8], F32, 2)
        bon0s = self.rot("bon0", [128, 8], F32, 2)
        yfls = self.rot("yfl", [128, 512], F32, 1)
        gtms = self.rot("gtm", [128, 512], F32, 2)
        sqvs = self.rot("sqv", [128, 512], F32, 1)
        ycs = self.rot("yc", [128, 512], F32, 1)
        vvs = sqvs
        stat = self.rot("stat", [128, 6, 8], F32, 2)
        yos = self.rot("yo", [128, 512], BF16, 1)
        pbanks = self.rot("pb", [128, 512], F32, 6, ps=True)
        ptr = self.rot("ptr", [128, 1024], BF16, 1, ps=True)
        pfv = [pf[s].rearrange("(c p) t -> p c t", p=128) for s in range(NS)]

        for s in range(NS):
            for d in range(2):
                op("pool", lambda e: e.memset(St[:].rearrange("p c h i -> p (c h i)"), 0.0), writes=[RSt])
                op("pool", lambda e: e.memset(Sbd[:].rearrange("p c h i -> p (c h i)"), 0.0), writes=[RSbd])
                order = range(NCH) if d == 0 else range(NCH - 1, -1, -1)
                def body(ci, s=s, d=d):
                    t0 = ci * C
                    ph, Rph = phs.next()
                    if d == 0:
                        P, RP = Ps.next()
                        lo, hi = max(t0 - 1, 0), min(t0 + C + 1, T)
                        if t0 == 0:
                            op("pool", lambda e, P=P: e.memset(P[:, :, 0:1], 0.0), writes=[RP])
                        if t0 + C == T:
                            op("pool", lambda e, P=P: e.memset(P[:, :, 129:130], 0.0), writes=[RP])
                        S.dma("sp", P[:, :, lo - (t0 - 1):hi - (t0 - 1)], pfv[s][:, :, lo:hi], reads=[self.dres(("pf", s))], writes=[RP])
                        tA, RtA = tAs.next()
                        tB, RtB = tBs.next()
                        for (c0_, c1_) in ((0, 5), (5, 10), (10, 15)):
                            cs_ = slice(c0_, c1_)
                            nn = c1_ - c0_
                            op("dve", lambda e, cs_=cs_: e.tensor_tensor(out=tA[:, cs_, :], in0=P[:, cs_, 0:128], in1=P[:, cs_, 1:129], op=ALU.subtract), reads=[RP], writes=[RtA])
                            op("pool", lambda e, cs_=cs_: e.tensor_tensor(out=tB[:, cs_, :], in0=P[:, cs_, 2:130], in1=P[:, cs_, 1:129], op=ALU.subtract), reads=[RP], writes=[RtB])
                            op("dve", lambda e, cs_=cs_, nn=nn: e.tensor_tensor(out=tA[:, cs_, :], in0=tA[:, cs_, :], in1=bc(mup[:, cs_].unsqueeze(2), [128, nn, 128]), op=ALU.mult), reads=[RtA, Rmup], writes=[RtA])
                            op("pool", lambda e, cs_=cs_, nn=nn: e.tensor_tensor(out=tB[:, cs_, :], in0=tB[:, cs_, :], in1=bc(mun[:, cs_].unsqueeze(2), [128, nn, 128]), op=ALU.mult), reads=[RtB, Rmun], writes=[RtB])
                            yield 'a'
                            op("dve", lambda e, cs_=cs_: e.tensor_tensor(out=tA[:, cs_, :], in0=tA[:, cs_, :], in1=tB[:, cs_, :], op=ALU.add), reads=[RtA, RtB], writes=[RtA])
                            op("pool", lambda e, cs_=cs_: e.tensor_tensor(out=ph[:, cs_, :], in0=tA[:, cs_, :], in1=P[:, cs_, 1:129], op=ALU.add), reads=[RtA, RP], writes=[Rph])
                            yield 'a'
                        S.dma("pool", phd[s, ci], ph[:].rearrange("p c t -> p (c t)"), reads=[Rph], writes=[self.dres(("phd", s))])
                    else:
                        S.dma("sp", ph[:].rearrange("p c t -> p (c t)"), phd[s, ci], reads=[self.dres(("phd", s))], writes=[Rph])
                    rh, kh, vh = ph[:, 0:4, :], ph[:, 4:8, :], ph[:, 8:12, :]
                    yield 'a'
                    twd, Rtwd = twds.next()
                    adb, Radb = adbs.next()
                    sgd, Rsgd = sgds.next()
                    op("act", lambda e, twd=twd, ph=ph: e.activation(out=twd[:], in_=ph[:, 12, :], func=AF.Tanh), reads=[Rph], writes=[Rtwd])
                    op("dve", lambda e, adb=adb, ph=ph: e.tensor_copy(out=adb[:], in_=ph[:, 13, :]), reads=[Rph], writes=[Radb])
                    psw, Rpsw = pbanks.next()
                    psa, Rpsa = pbanks.next()

                    def mmlora(e, psw=psw, psa=psa, twd=twd, adb=adb, d=d):
                        for cc in range(4):
                            e.matmul(psw[:, cc * 128:(cc + 1) * 128], lhsT=wupz[:, d, cc * 128:(cc + 1) * 128], rhs=twd[:], start=True, stop=True)
                        for cc in range(4):
                            ins = e.matmul(psa[:, cc * 128:(cc + 1) * 128], lhsT=aupz[:, d, cc * 128:(cc + 1) * 128], rhs=adb[:], start=True, stop=True)
                        return ins
                    op("pe", mmlora, reads=[Rwupz, Raupz, Rtwd, Radb], writes=[Rpsw, Rpsa])
                    yield 'a'
                    lw, Rlw = lws.next()
                    av, Rav = avs.next()

                    def sigw(e, lw=lw, psw=psw, d=d):
                        for cc in range(4):
                            ins = e.activation(out=lw[:, cc, :], in_=psw[:, cc * 128:(cc + 1) * 128], func=AF.Sigmoid, bias=w0t[:, d, cc:cc + 1])
                        return ins
                    op("act", sigw, reads=[Rpsw, Rw0], writes=[Rlw])

                    def siga(e, av=av, psa=psa, d=d):
                        for cc in range(4):
                            ins = e.activation(out=av[:, cc, :], in_=psa[:, cc * 128:(cc + 1) * 128], func=AF.Sigmoid, bias=a0t[:, d, cc:cc + 1])
                        return ins
                    op("act", siga, reads=[Rpsa, Ra0], writes=[Rav])
                    yield 'a'
                    yield 'a'
                    pre, Rpre = pres.next()
                    cum, Rcum = cums.next()
                    cpr, Rcpr = cprs.next()

                    def scan(e, pre=pre, lw=lw):
                        for cc in range(4):
                            ins = e.tensor_tensor_scan(out=pre[:, cc, :], data0=onesf[:], data1=lw[:, cc, :], initial=0.0, op0=ALU.mult, op1=ALU.add)
                        return ins
                    op("dve", scan, reads=[Rlw, Ronesf], writes=[Rpre])
                    yield 'a'
                    if d == 0:
                        cum, Rcum = pre, Rpre
                    else:
                        op("dve", lambda e, cum=cum, pre=pre, lw=lw: e.scalar_tensor_tensor(out=cum[:], in0=pre[:], scalar=-1.0, in1=lw[:], op0=ALU.mult, op1=ALU.add),
                           reads=[Rpre, Rlw], writes=[Rcum])
                        op("dve", lambda e, cum=cum, pre=pre: e.tensor_tensor(out=cum[:], in0=cum[:], in1=bc(pre[:, :, 127:128], [128, 4, 128]), op=ALU.add),
                           reads=[Rcum, Rpre], writes=[Rcum])
                    op("pool", lambda e, cpr=cpr, cum=cum, lw=lw: e.tensor_tensor(out=cpr[:], in0=cum[:], in1=lw[:], op=ALU.subtract), reads=[Rcum, Rlw], writes=[Rcpr])
                    ec, Rec = ecs.next()
                    ecp, Recp = ecps.next()
                    ei, Rei = eis.next()
                    op("act", lambda e, ec=ec, cum=cum: e.activation(out=ec[:], in_=cum[:], func=AF.Exp, scale=-LD), reads=[Rcum], writes=[Rec])
                    op("act", lambda e, ecp=ecp, cpr=cpr: e.activation(out=ecp[:], in_=cpr[:], func=AF.Exp, scale=-LD), reads=[Rcpr], writes=[Recp])
                    op("act", lambda e, ei=ei, cum=cum: e.activation(out=ei[:], in_=cum[:], func=AF.Exp, scale=LD), reads=[Rcum], writes=[Rei])
                    yield 'a'
                    gC, RgC = gCs.next()
                    ce = 127 if d == 0 else 0
                    op("dve", lambda e, gC=gC, ec=ec, ce=ce: e.tensor_copy(out=gC[:], in_=ec[:, :, ce]), reads=[Rec], writes=[RgC])
                    yield 'a'
                    kraw, Rkraw = kraws.next()
                    sq, Rsq = sqs.next()
                    nrm, Rnrm = nrms.next()
                    kkv, Rkkv = kkvs.next()
                    tt, Rtt = tts.next()
                    kd, Rkd = kds.next()
                    bv, Rbv = bvs.next()
                    op("dve", lambda e, kraw=kraw, ph=ph: e.tensor_tensor(out=kraw[:], in0=ph[:, 4:8, :], in1=bc(kkc[:].unsqueeze(2), [128, 4, 128]), op=ALU.mult), reads=[Rph, Rkkc], writes=[Rkraw])
                    op("pool", lambda e, sq=sq, kraw=kraw: e.tensor_tensor(out=sq[:], in0=kraw[:], in1=kraw[:], op=ALU.mult), reads=[Rkraw], writes=[Rsq])
                    psn, Rpsn = pbanks.next()
                    op("pe", lambda e, psn=psn, sq=sq: e.matmul(psn[:], lhsT=blk1[:], rhs=sq[:].rearrange("p c t -> p (c t)"), start=True, stop=True), reads=[Rblk1, Rsq], writes=[Rpsn])
                    yield 'a'
                    op("act", lambda e, nrm=nrm, psn=psn: e.activation(out=nrm[:].rearrange("p c t -> p (c t)"), in_=psn[:], func=AF.Sqrt), reads=[Rpsn], writes=[Rnrm])
                    op("dve", lambda e, nrm=nrm: e.tensor_scalar_max(out=nrm[:], in0=nrm[:], scalar1=1e-12), reads=[Rnrm], writes=[Rnrm])
                    op("dve", lambda e, nrm=nrm: e.reciprocal(out=nrm[:], in_=nrm[:]), reads=[Rnrm], writes=[Rnrm])
                    op("dve", lambda e, kkv=kkv, kraw=kraw, nrm=nrm: e.tensor_tensor(out=kkv[:], in0=kraw[:], in1=nrm[:], op=ALU.mult), reads=[Rkraw, Rnrm], writes=[Rkkv])
                    yield 'a'
                    op("pool", lambda e, tt=tt, av=av: e.tensor_tensor(out=tt[:], in0=av[:], in1=bc(kac[:].unsqueeze(2), [128, 4, 128]), op=ALU.mult), reads=[Rav, Rkac], writes=[Rtt])
                    op("pool", lambda e, tt=tt: e.tensor_tensor(out=tt[:], in0=tt[:], in1=bc(omka[:].unsqueeze(2), [128, 4, 128]), op=ALU.add), reads=[Rtt, Romka], writes=[Rtt])
                    op("dve", lambda e, kd=kd, ph=ph, tt=tt: e.tensor_tensor(out=kd[:], in0=ph[:, 4:8, :], in1=tt[:], op=ALU.mult), reads=[Rph, Rtt], writes=[Rkd])
                    yield 'a'
                    op("pool", lambda e, bv=bv, kkv=kkv, av=av: e.tensor_tensor(out=bv[:], in0=kkv[:], in1=av[:], op=ALU.mult), reads=[Rkkv, Rav], writes=[Rbv])
                    yield 'a'
                    AT, RAT = ATs.next()
                    RT, RRT = RTs.next()
                    KT, RKT = KTs.next()
                    BT, RBT = BTs.next()
                    KhT, RKhT = KhTs.next()
                    BhT, RBhT = BhTs.next()
                    vT, RvT = vTs.next()
                    Kf, RKf = Kfs.next()
                    Bf, RBf = Bfs.next()
                    rk, Rrk = rks.next()
                    op("dve", lambda e, AT=AT, kkv=kkv, ecp=ecp: e.scalar_tensor_tensor(out=AT[:], in0=kkv[:], scalar=-1.0, in1=ecp[:], op0=ALU.mult, op1=ALU.mult), reads=[Rkkv, Recp], writes=[RAT])
                    op("dve", lambda e, RT=RT, ph=ph, ec=ec: e.tensor_tensor(out=RT[:], in0=ph[:, 0:4, :], in1=ec[:], op=ALU.mult), reads=[Rph, Rec], writes=[RRT])
                    op("dve", lambda e, Kf=Kf, kd=kd, ei=ei: e.tensor_tensor(out=Kf[:], in0=kd[:], in1=ei[:], op=ALU.mult), reads=[Rkd, Rei], writes=[RKf])
                    yield 'a'
                    op("pool", lambda e, Bf=Bf, bv=bv, ei=ei: e.tensor_tensor(out=Bf[:], in0=bv[:], in1=ei[:], op=ALU.mult), reads=[Rbv, Rei], writes=[RBf])
                    op("act", lambda e, KT=KT, Kf=Kf: e.copy(out=KT[:], in_=Kf[:]), reads=[RKf], writes=[RKT])
                    op("act", lambda e, BT=BT, Bf=Bf: e.copy(out=BT[:], in_=Bf[:]), reads=[RBf], writes=[RBT])
                    yield 'a'
                    op("dve", lambda e, KhT=KhT, Kf=Kf, gC=gC: e.tensor_tensor(out=KhT[:], in0=Kf[:], in1=bc(gC[:].unsqueeze(2), [128, 4, 128]), op=ALU.mult), reads=[RKf, RgC], writes=[RKhT])
                    op("pool", lambda e, BhT=BhT, Bf=Bf, gC=gC: e.tensor_tensor(out=BhT[:], in0=Bf[:], in1=bc(gC[:].unsqueeze(2), [128, 4, 128]), op=ALU.mult), reads=[RBf, RgC], writes=[RBhT])
                    op("act", lambda e, vT=vT, ph=ph: e.copy(out=vT[:], in_=ph[:, 8:12, :]), reads=[Rph], writes=[RvT])
                    op("pool", lambda e, rk=rk, ph=ph, kd=kd: e.tensor_tensor(out=rk[:], in0=ph[:, 0:4, :], in1=kd[:], op=ALU.mult), reads=[Rph, Rkd], writes=[Rrk])
                    op("pool", lambda e, rk=rk: e.tensor_tensor(out=rk[:], in0=rk[:], in1=bc(rkc[:].unsqueeze(2), [128, 4, 128]), op=ALU.mult), reads=[Rrk, Rrkc], writes=[Rrk])
                    yield 'a'
                    ARbd, RARbd = ARbds.next()
                    Bbd, RBbd = Bbds.next()
                    for h2 in range(2):
                        pl = slice(h2 * 64, (h2 + 1) * 64)
                        op("act", lambda e, pl=pl, h2=h2, AT=AT: e.copy(out=ARbd[pl, :, h2, 0, :], in_=AT[pl, :, :]), reads=[RAT, RARbd], writes=[RARbd])
                        op("act", lambda e, pl=pl, h2=h2, RT=RT: e.copy(out=ARbd[pl, :, h2, 1, :], in_=RT[pl, :, :]), reads=[RRT, RARbd], writes=[RARbd])
                        op("act", lambda e, pl=pl, h2=h2, BT=BT: e.copy(out=Bbd[pl, :, h2, :], in_=BT[pl, :, :]), reads=[RBT, RBbd], writes=[RBbd])
                    yield 'a'
                    pt, Rpt = ptr.next()
                    VK, RVK = VKs.next()
                    Vtm, RVtm = VK[:, 0:512], RVK
                    Khat, RKhat = VK[:, 512:1024], RVK
                    Bhat, RBhat = Bhats.next()

                    def tr1(e, pt=pt, vT=vT, KhT=KhT):
                        for cc in range(4):
                            e.transpose(out=pt[:, cc * 128:(cc + 1) * 128], in_=vT[:, cc, :], identity=identb[:])
                        for cc in range(4):
                            ins = e.transpose(out=pt[:, 512 + cc * 128:512 + (cc + 1) * 128], in_=KhT[:, cc, :], identity=identb[:])
                        return ins
                    op("pe", tr1, reads=[RvT, RKhT, Ridb], writes=[Rpt])
                    yield 'a'
                    op("act", lambda e, VK=VK, pt=pt: e.copy(out=VK[:], in_=pt[:]), reads=[Rpt], writes=[RVK])
                    pt2, Rpt2 = ptr.next()

                    def tr2(e, pt2=pt2, BhT=BhT):
                        for cc in range(4):
                            ins = e.transpose(out=pt2[:, cc * 128:(cc + 1) * 128], in_=BhT[:, cc, :], identity=identb[:])
                        return ins
                    op("pe", tr2, reads=[RBhT, Ridb], writes=[Rpt2])
                    yield 'a'
                    op("act", lambda e, Bhat=Bhat, pt2=pt2: e.copy(out=Bhat[:], in_=pt2[:, 0:512]), reads=[Rpt2], writes=[RBhat])
                    psb, Rpsb = pbanks.next()
                    bonv, Rbonv = bons.next()

                    def mmbon(e, psb=psb, rk=rk):
                        for cc in range(4):
                            ins = e.transpose(out=psb[:, cc * 128:(cc + 1) * 128], in_=rk[:, cc, :], identity=identf[:])
                        return ins
                    op("pe", mmbon, reads=[Rrk, Ridf], writes=[Rpsb])
                    op("dve", lambda e, bonv=bonv, psb=psb: e.tensor_reduce(out=bonv[:], in_=psb[:].rearrange("p (h i) -> p h i", i=64), axis=AX.X, op=ALU.add),
                       reads=[Rpsb], writes=[Rbonv])
                    if d == 1:
                        op("act", lambda e, sgd=sgd, ph=ph: e.activation(out=sgd[:], in_=ph[:, 14, :], func=AF.Sigmoid), reads=[Rph], writes=[Rsgd])
                        psg, Rpsg = pbanks.next()
                        gtm, Rgtm = gtms.next()
                        op("pe", lambda e, psg=psg, sgd=sgd: e.matmul(psg[:], lhsT=sgd[:], rhs=gup[:], start=True, stop=True), reads=[Rsgd, Rgup], writes=[Rpsg])
                        op("act", lambda e, gtm=gtm, psg=psg: e.copy(out=gtm[:], in_=psg[:]), reads=[Rpsg], writes=[Rgtm])
                    yield 'B'
                    GK, RGK = GKs.next()
                    GB, RGB = GBs.next()
                    Lh, RLh = Lhs.next()
                    for cc in range(4):
                        p1, Rp1 = pbanks.next()
                        p2, Rp2 = pbanks.next()
                        p3, Rp3 = pbanks.next()

                        def mmg(e, cc=cc, p1=p1, p2=p2, p3=p3, KT=KT, BT=BT, AT=AT):
                            e.matmul(p1[:], lhsT=KT[:, cc, :], rhs=ARbd[:, cc].rearrange("p h x t -> p (h x t)"), start=True, stop=True)
                            e.matmul(p2[:], lhsT=BT[:, cc, :], rhs=ARbd[:, cc].rearrange("p h x t -> p (h x t)"), start=True, stop=True)
                            return e.matmul(p3[:, 0:256], lhsT=AT[:, cc, :], rhs=Bbd[:, cc].rearrange("p h t -> p (h t)"), start=True, stop=True)
                        op("pe", mmg, reads=[RKT, RBT, RAT, RARbd, RBbd], writes=[Rp1, Rp2, Rp3])
                        op("dve", lambda e, cc=cc, p1=p1, GK=GK, d=d: e.tensor_tensor(out=GK[:, cc].rearrange("p h x t -> p (h x t)"), in0=p1[:],
                                                                                 in1=mG[:, d].rearrange("p h x t -> p (h x t)"), op=ALU.mult), reads=[Rp1, RmG], writes=[RGK])
                        op("dve", lambda e, cc=cc, p2=p2, GB=GB, d=d: e.tensor_tensor(out=GB[:, cc].rearrange("p h x t -> p (h x t)"), in0=p2[:],
                                                                                 in1=mG[:, d].rearrange("p h x t -> p (h x t)"), op=ALU.mult), reads=[Rp2, RmG], writes=[RGB])
                        op("dve", lambda e, cc=cc, p3=p3, Lh=Lh, d=d: e.tensor_tensor(out=Lh[:, cc].rearrange("p h t -> p (h t)"), in0=p3[:, 0:256],
                                                                                 in1=mL[:, d].rearrange("p h t -> p (h t)"), op=ALU.mult), reads=[Rp3, RmL], writes=[RLh])
                        yield 'b'
                    L16, RL16 = L16s.next()
                    M16, RM16 = M16s.next()
                    for cc in range(4):
                        op("pool", lambda e, cc=cc, L16=L16, Lh=Lh: e.tensor_tensor(out=L16[:, cc].rearrange("p h t -> p (h t)"), in0=Lh[:, cc].rearrange("p h t -> p (h t)"),
                                                                              in1=bd16[:].rearrange("p h t -> p (h t)"), op=ALU.mult), reads=[RLh, Rbd16], writes=[RL16])
                        op("pool", lambda e, cc=cc, M16=M16, GB=GB: e.tensor_tensor(out=M16[:, cc], in0=GB[:, cc, :, 0, :], in1=bd16[:], op=ALU.mult), reads=[RGB, Rbd16], writes=[RM16])
                    cur = []
                    for h in range(8):
                        cur.append((L16[:, h // 2, h % 2, :], RL16, M16[:, h // 2, h % 2, :], RM16, identb[:], Ridb, identb[:], Ridb))
                    for lev in range(4):
                        for h in range(8):
                            Lp, RLp, Mp, RMp, Qp, RQp, Pp, RPp = cur[h]
                            pq, Rpq = pbanks.next()
                            lt, Rlt = lmq[h][lev % 2]

                            def mmi(e, pq=pq, Lp=Lp, Mp=Mp, Qp=Qp, Pp=Pp, lev=lev):
                                e.matmul(pq[:, 256:384], lhsT=identb[:], rhs=Qp, start=True, stop=False)
                                e.matmul(pq[:, 256:384], lhsT=Lp, rhs=Qp, start=False, stop=True)
                                e.matmul(pq[:, 384:512], lhsT=identb[:], rhs=Pp, start=True, stop=False)
                                ins = e.matmul(pq[:, 384:512], lhsT=Mp, rhs=Pp, start=False, stop=True)
                                if lev < 3:
                                    e.matmul(pq[:, 0:128], lhsT=Mp, rhs=Lp, start=True, stop=True)
                                    ins = e.matmul(pq[:, 128:256], lhsT=Lp, rhs=Mp, start=True, stop=True)
                                return ins
                            op("pe", mmi, reads=[RLp, RMp, RQp, RPp, Ridb], writes=[Rpq])
                            lo_ = 0 if lev < 3 else 256
                            eng = "act"
                            if eng == "dve":
                                op("dve", lambda e, lt=lt, pq=pq, lo_=lo_: e.tensor_copy(out=lt[:].rearrange("p a t -> p (a t)")[:, lo_:512], in_=pq[:, lo_:512]), reads=[Rpq], writes=[Rlt])
                            else:
                                op("act", lambda e, lt=lt, pq=pq, lo_=lo_: e.copy(out=lt[:].rearrange("p a t -> p (a t)")[:, lo_:512], in_=pq[:, lo_:512]), reads=[Rpq], writes=[Rlt])
                            cur[h] = (lt[:, 0, :], Rlt, lt[:, 1, :], Rlt, lt[:, 2, :], Rlt, lt[:, 3, :], Rlt)
                            if h % 2 == 1:
                                yield 'b'
                    tcur = [(cur[h][6], cur[h][7], cur[h][4], cur[h][5]) for h in range(8)]
                    for lv in range(3):
                        last_lv = (lv == 2)
                        pas = []
                        for pr in range(4):
                            pa, Rpa = pbanks.next()
                            pas.append((pa, Rpa))

                            def mmz(e, pa=pa, pr=pr, tc=list(tcur), last_lv=last_lv, Lh=Lh, GB=GB):
                                for h2 in range(2):
                                    h = 2 * pr + h2
                                    Tk, RTk, TTk, RTTk = tc[h]
                                    ins = e.matmul(pa[:, h2 * 256 + 128:h2 * 256 + 256], lhsT=Lh[:, pr, h2, :], rhs=TTk, start=True, stop=True)
                                    if not last_lv:
                                        ins = e.matmul(pa[:, h2 * 256:h2 * 256 + 128], lhsT=GB[:, pr, h2, 0, :], rhs=Tk, start=True, stop=True)
                                return ins
                            op("pe", mmz, reads=[RLh, RGB, tcur[2 * pr][1], tcur[2 * pr][3], tcur[2 * pr + 1][1], tcur[2 * pr + 1][3]], writes=[Rpa])
                        for pr in range(4):
                            pa, Rpa = pas[pr]
                            zz, Rzz = zzt[pr]
                            if not last_lv:
                                op("dve", lambda e, zz=zz, pa=pa, lv=lv: e.tensor_tensor(out=zz[:].rearrange("p h a t -> p (h a t)"), in0=pa[:],
                                                                                        in1=offm[:, lv].rearrange("p a t -> p (a t)"), op=ALU.mult),
                                   reads=[Rpa, Roffm], writes=[Rzz])
                            else:
                                op("dve", lambda e, zz=zz, pa=pa, lv=lv: e.tensor_tensor(out=zz[:, :, 1, :], in0=pa[:].rearrange("p (h a t) -> p h a t", h=2, a=2)[:, :, 1, :],
                                                                                        in1=offm[:, lv, 0:2, :], op=ALU.mult),
                                   reads=[Rpa, Roffm], writes=[Rzz])
                        yield 'b'
                        pbs = []
                        for pr in range(4):
                            pb_, Rpb_ = pbanks.next()
                            pbs.append((pb_, Rpb_))
                            zz, Rzz = zzt[pr]

                            def mmt(e, pb_=pb_, pr=pr, tc=list(tcur), zz=zz, last_lv=last_lv):
                                for h2 in range(2):
                                    h = 2 * pr + h2
                                    Tk, RTk, TTk, RTTk = tc[h]
                                    c0 = h2 * 256
                                    e.matmul(pb_[:, c0 + 128:c0 + 256], lhsT=identb[:], rhs=TTk, start=True, stop=False)
                                    ins = e.matmul(pb_[:, c0 + 128:c0 + 256], lhsT=Tk, rhs=zz[:, h2, 1, :], start=False, stop=True)
                                    if not last_lv:
                                        e.matmul(pb_[:, c0:c0 + 128], lhsT=identb[:], rhs=Tk, start=True, stop=False)
                                        ins = e.matmul(pb_[:, c0:c0 + 128], lhsT=TTk, rhs=zz[:, h2, 0, :], start=False, stop=True)
                                return ins
                            op("pe", mmt, reads=[Rzz, Ridb, tcur[2 * pr][1], tcur[2 * pr][3], tcur[2 * pr + 1][1], tcur[2 * pr + 1][3]], writes=[Rpb_])
                        for pr in range(4):
                            pb_, Rpb_ = pbs[pr]
                            tn, Rtn = ttt[pr][lv % 2] if not last_lv else tfin[pr].next()
                            eng = "act"
                            if not last_lv:
                                src, dst = pb_[:], tn[:].rearrange("p h a t -> p (h a t)")
                            else:
                                src, dst = pb_[:].rearrange("p (h a t) -> p h a t", h=2, a=2)[:, :, 1, :], tn[:, :, 1, :]
                            if eng == "act":
                                op("act", lambda e, src=src, dst=dst: e.copy(out=dst, in_=src), reads=[Rpb_], writes=[Rtn])
                            else:
                                op("dve", lambda e, src=src, dst=dst: e.tensor_copy(out=dst, in_=src), reads=[Rpb_], writes=[Rtn])
                            for h2 in range(2):
                                tcur[2 * pr + h2] = (tn[:, h2, 0, :], Rtn, tn[:, h2, 1, :], Rtn)
                        yield 'b'
                    cur = [(None, None, None, None, tcur[h][2], tcur[h][3]) for h in range(8)]
                    yield 'C'
                    pW, RpW = pbanks.next()
                    Wsb, RWsb = Wsbs.next()
                    Usb, RUsb = Usbs.next()
                    Ysb, RYsb = Ysbs.next()

                    def mmW(e, pW=pW, AT=AT, GK=GK, Vtm=Vtm):
                        for cc in range(4):
                            e.matmul(pW[:, cc * 128:(cc + 1) * 128], lhsT=AT[:, cc, :], rhs=Sbd[:, cc].rearrange("p h i -> p (h i)"), start=True, stop=False)
                            for h2 in range(2):
                                h = 2 * cc + h2
                                ins = e.matmul(pW[:, h * 64:(h + 1) * 64], lhsT=GK[:, cc, h2, 0, :], rhs=Vtm[:, h * 64:(h + 1) * 64], start=False, stop=True)
                        return ins
                    op("pe", mmW, reads=[RAT, RSbd, RGK, RVtm], writes=[RpW])
                    op("dve", lambda e, Wsb=Wsb, pW=pW: e.tensor_copy(out=Wsb[:], in_=pW[:]), reads=[RpW], writes=[RWsb])
                    yield 'c'
                    pU, RpU = pbanks.next()

                    def mmU(e, pU=pU, Wsb=Wsb, cur=list(cur)):
                        for h in range(8):
                            ins = e.matmul(pU[:, h * 64:(h + 1) * 64], lhsT=cur[h][4], rhs=Wsb[:, h * 64:(h + 1) * 64], start=True, stop=True)
                        return ins
                    op("pe", mmU, reads=[RWsb] + [cur[h][5] for h in range(8)], writes=[RpU])
                    op("dve", lambda e, Usb=Usb, pU=pU: e.tensor_copy(out=Usb[:], in_=pU[:]), reads=[RpU], writes=[RUsb])
                    yield 'c'
                    pY, RpY = pbanks.next()

                    def mmY(e, pY=pY, RT=RT, GK=GK, GB=GB, Vtm=Vtm, Usb=Usb):
                        for cc in range(4):
                            e.matmul(pY[:, cc * 128:(cc + 1) * 128], lhsT=RT[:, cc, :], rhs=Sbd[:, cc].rearrange("p h i -> p (h i)"), start=True, stop=False)
                            for h2 in range(2):
                                h = 2 * cc + h2
                                e.matmul(pY[:, h * 64:(h + 1) * 64], lhsT=GK[:, cc, h2, 1, :], rhs=Vtm[:, h * 64:(h + 1) * 64], start=False, stop=False)
                                ins = e.matmul(pY[:, h * 64:(h + 1) * 64], lhsT=GB[:, cc, h2, 1, :], rhs=Usb[:, h * 64:(h + 1) * 64], start=False, stop=True)
                        return ins
                    op("pe", mmY, reads=[RRT, RSbd, RGK, RGB, RVtm, RUsb], writes=[RpY])
                    op("act", lambda e, Ysb=Ysb, pY=pY: e.copy(out=Ysb[:], in_=pY[:]), reads=[RpY], writes=[RYsb])
                    yield 'c'
                    pS, RpS = pbanks.next()

                    def mmS(e, pS=pS, Khat=Khat, Bhat=Bhat, Vtm=Vtm, Usb=Usb):
                        for cc in range(4):
                            cs = slice(cc * 128, (cc + 1) * 128)
                            e.matmul(pS[:, cs], lhsT=Khat[:, cs], rhs=Vtm[:, cs], start=True, stop=False)
                            ins = e.matmul(pS[:, cs], lhsT=Bhat[:, cs], rhs=Usb[:, cs], start=False, stop=True)
                        return ins
                    op("pe", mmS, reads=[RKhat, RBhat, RVtm, RUsb], writes=[RpS])
                    Sf = St[:].rearrange("p c h i -> p c (h i)")
                    op("dve", lambda e, pS=pS: e.tensor_tensor(out=stmp[:].rearrange("p c h i -> p (c h i)"), in0=pS[:], in1=bdm[:].rearrange("p c h i -> p (c h i)"), op=ALU.mult),
                       reads=[RpS, Rbdm], writes=[Rstmp])
                    op("dve", lambda e, gC=gC: e.tensor_tensor(out=Sf, in0=Sf, in1=bc(gC[:].unsqueeze(2), [128, 4, 128]), op=ALU.mult), reads=[RSt, RgC], writes=[RSt])
                    op("dve", lambda e: e.tensor_tensor(out=St[:].rearrange("p c h i -> p (c h i)"), in0=St[:].rearrange("p c h i -> p (c h i)"),
                                                        in1=stmp[:].rearrange("p c h i -> p (c h i)"), op=ALU.add), reads=[RSt, Rstmp], writes=[RSt])
                    op("act", lambda e: e.copy(out=Sbd[:].rearrange("p c h i -> p (c h i)"), in_=St[:].rearrange("p c h i -> p (c h i)")), reads=[RSt], writes=[RSbd])
                    yield 'c'
                    if d == 0:
                        S.dma("pool", yf[s, t0:t0 + C, :], Ysb[:], reads=[RYsb], writes=[self.dres(("yf", s))])
                        S.dma("pool", bon[s, t0:t0 + C, :], bonv[:], reads=[Rbonv], writes=[self.dres(("bon", s))])
                    else:
                        yfl, Ryfl = yfls.next()
                        bon0, Rbon0 = bon0s.next()
                        S.dma("sp", yfl[:], yf[s, t0:t0 + C, :], reads=[self.dres(("yf", s))], writes=[Ryfl])
                        S.dma("sp", bon0[:], bon[s, t0:t0 + C, :], reads=[self.dres(("bon", s))], writes=[Rbon0])
                        sqv, Rsqv = sqvs.next()
                        yc, Ryc = ycs.next()
                        vv, Rvv = vvs.next()
                        st, Rst = stat.next()
                        yo, Ryo = yos.next()
                        y3 = lambda t_: t_[:].rearrange("p (h i) -> p h i", i=64)
                        b3 = lambda a_: bc(a_.unsqueeze(2), [128, 8, 64])
                        op("dve", lambda e, Ysb=Ysb, yfl=yfl: e.tensor_tensor(out=Ysb[:], in0=Ysb[:], in1=yfl[:], op=ALU.add), reads=[RYsb, Ryfl], writes=[RYsb])
                        op("pool", lambda e, sqv=sqv, Ysb=Ysb: e.tensor_tensor(out=sqv[:], in0=Ysb[:], in1=Ysb[:], op=ALU.mult), reads=[RYsb], writes=[Rsqv])
                        op("dve", lambda e, st=st, Ysb=Ysb: e.tensor_reduce(out=st[:, 0, :], in_=y3(Ysb), axis=AX.X, op=ALU.add), reads=[RYsb], writes=[Rst])
                        op("dve", lambda e, st=st, sqv=sqv: e.tensor_reduce(out=st[:, 1, :], in_=y3(sqv), axis=AX.X, op=ALU.add), reads=[Rsqv, Rst], writes=[Rst])
                        op("dve", lambda e, st=st: e.tensor_scalar(out=st[:, 2, :], in0=st[:, 0, :], scalar1=1.0 / 64, scalar2=None, op0=ALU.mult), reads=[Rst], writes=[Rst])
                        op("dve", lambda e, st=st: e.tensor_tensor(out=st[:, 3, :], in0=st[:, 2, :], in1=st[:, 2, :], op=ALU.mult), reads=[Rst], writes=[Rst])
                        op("dve", lambda e, st=st: e.scalar_tensor_tensor(out=st[:, 4, :], in0=st[:, 1, :], scalar=1.0 / 64, in1=st[:, 3, :], op0=ALU.mult, op1=ALU.subtract),
                           reads=[Rst], writes=[Rst])
                        op("act", lambda e, st=st: e.activation(out=st[:, 5, :], in_=st[:, 4, :], func=AF.Sqrt, bias=gne[:, 0:1]), reads=[Rst, Rgne], writes=[Rst])
                        op("dve", lambda e, st=st: e.reciprocal(out=st[:, 5, :], in_=st[:, 5, :]), reads=[Rst], writes=[Rst])
                        yield 'c'
                        op("dve", lambda e, yc=yc, Ysb=Ysb, st=st: e.tensor_tensor(out=y3(yc), in0=y3(Ysb), in1=b3(st[:, 2, :]), op=ALU.subtract), reads=[RYsb, Rst], writes=[Ryc])
                        op("dve", lambda e, yc=yc, st=st: e.tensor_tensor(out=y3(yc), in0=y3(yc), in1=b3(st[:, 5, :]), op=ALU.mult), reads=[Ryc, Rst], writes=[Ryc])
                        op("pool", lambda e, yc=yc: e.tensor_tensor(out=yc[:], in0=yc[:], in1=gng[:], op=ALU.mult), reads=[Ryc, Rgng], writes=[Ryc])
                        op("pool", lambda e, yc=yc: e.tensor_tensor(out=yc[:], in0=yc[:], in1=gnb[:], op=ALU.add), reads=[Ryc, Rgnb], writes=[Ryc])
                        yield 'c'
                        op("dve", lambda e, bonv=bonv, bon0=bon0: e.tensor_tensor(out=bonv[:], in0=bonv[:], in1=bon0[:], op=ALU.add), reads=[Rbonv, Rbon0], writes=[Rbonv])
                        op("dve", lambda e, vv=vv, Vtm=Vtm, bonv=bonv: e.tensor_tensor(out=y3(vv), in0=Vtm.rearrange("p (h i) -> p h i", i=64), in1=b3(bonv[:]), op=ALU.mult), reads=[RVtm, Rbonv], writes=[Rvv])
                        op("pool", lambda e, yc=yc, vv=vv: e.tensor_tensor(out=yc[:], in0=yc[:], in1=vv[:], op=ALU.add), reads=[Ryc, Rvv], writes=[Ryc])
                        op("dve", lambda e, yo=yo, yc=yc, gtm=gtm: e.tensor_tensor(out=yo[:], in0=yc[:], in1=gtm[:], op=ALU.mult), reads=[Ryc, Rgtm], writes=[Ryo])
                        S.dma("pool", yr[s, t0:t0 + C, :], yo[:], reads=[Ryo], writes=[self.dres(("yr", s))])

                def run_until(g, tags):
                    while True:
                        try:
                            t_ = next(g)
                        except StopIteration:
                            return None
                        if t_ in tags:
                            return t_
                order = list(order)
                gens = [body(ci) for ci in order]
                run_until(gens[0], ('B',))
                prev = None
                for i_ in range(len(gens)):
                    g_cur = gens[i_]
                    g_nxt = gens[i_ + 1] if i_ + 1 < len(gens) else None
                    cur_done = False
                    if prev is not None:
                        prev_done = False
                        while not prev_done:
                            if run_until(prev, ('c',)) is None:
                                prev_done = True
                            if not cur_done and run_until(g_cur, ('b', 'C')) == 'C':
                                cur_done = True
                    nxt_done = g_nxt is None
                    while not (cur_done and nxt_done):
                        if not cur_done and run_until(g_cur, ('b', 'C')) == 'C':
                            cur_done = True
                        if not nxt_done and run_until(g_nxt, ('a', 'B')) == 'B':
                            nxt_done = True
                    prev = g_cur
                run_until(prev, ())
        self.end()

    def phase_p3(self, l):
        S, T, NS = self.S, self.T, self.NS
        self.begin()
        rows = T // 64
        nblk = T // 128
        nq, nk, nv, yn = (self.scr[k] for k in ("nq", "nk", "nv", "yn"))
        rp, Rrp = self.tile("rp", [120, 31], F32)
        S.dma("sp", rp[:], self.w["rpb"][l].rearrange("h a b -> (h a) b"), writes=[Rrp])
        _, _, idf, Ridf = self.make_ident()
        Rt, RRt = self.tile("Rt", [31, 8, 15], F32)
        Jm, RJ = self.tile("Jm", [31, 160], F32)
        tps = self.rot("tps", [128, 512], F32, 4, ps=True)
        tp0, Rtp0 = tps.next()
        S.op("pe", lambda e: e.transpose(out=tp0[0:31, 0:120], in_=rp[:, :], identity=idf[0:120, 0:120]), reads=[Rrp, Ridf], writes=[Rtp0])
        S.op("dve", lambda e: e.tensor_copy(out=Rt[:].rearrange("p h a -> p (h a)"), in_=tp0[0:31, 0:120]), reads=[Rtp0], writes=[RRt])

        S.op("pool", lambda e: e.memset(Jm[:], 0.0), writes=[RJ])
        S.op("pool", lambda e: e.affine_select(out=Jm[:], in_=Jm[:], pattern=[[-1, 160]], compare_op=ALU.not_equal, fill=1.0, base=48, channel_multiplier=1),
             reads=[RJ], writes=[RJ])
        TE0, RTE0 = self.tile("TE0", [64, 8, 15, 64], BF16)
        for q0 in range(0, 64, 4):
            tp, Rtp = tps.next()

            def mmT(e, tp=tp, q0=q0):
                for qi in range(4):
                    qc = q0 + qi
                    ins = e.matmul(tp[0:64, qi * 120:(qi + 1) * 120], lhsT=Jm[:, 63 - qc:127 - qc], rhs=Rt[:].rearrange("p h a -> p (h a)"),
                                   start=True, stop=True)
                return ins
            S.op("pe", mmT, reads=[RJ, RRt], writes=[Rtp])
            S.op("act", lambda e, tp=tp, q0=q0: e.activation(out=TE0[:, :, :, q0:q0 + 4].rearrange("p h a q -> p (h a) q"),
                                                             in_=tp[0:64, 0:480].rearrange("p (q x) -> p x q", q=4), func=AF.Exp),
                 reads=[Rtp], writes=[RTE0])
        A, RA = self.tile("mA", [128, 64], F32)
        Q, RQ = self.tile("mQ", [128, 64], F32)
        Q2, RQ2 = self.tile("mQ2", [128, 64], F32)
        cm, Rcm = self.tile("cm", [128, 64], F32)

        def io(e):
            e.iota(A[0:64, :], pattern=[[-1, 64]], base=0, channel_multiplier=1, allow_small_or_imprecise_dtypes=True)
            e.iota(A[64:128, :], pattern=[[-1, 64]], base=0, channel_multiplier=1, allow_small_or_imprecise_dtypes=True)
            return e.iota(Q[:], pattern=[[1, 64]], base=0, channel_multiplier=0, allow_small_or_imprecise_dtypes=True)
        S.op("pool", io, writes=[RA, RQ])
        S.op("dve", lambda e: e.tensor_scalar(out=Q2[:], in0=Q[:], scalar1=8.0, scalar2=56.0, op0=ALU.max, op1=ALU.min), reads=[RQ], writes=[RQ2])
        S.op("dve", lambda e: e.tensor_tensor(out=Q[:], in0=Q[:], in1=Q2[:], op=ALU.subtract), reads=[RQ, RQ2], writes=[RQ])
        S.op("dve", lambda e: e.tensor_tensor(out=A[:], in0=A[:], in1=Q[:], op=ALU.add), reads=[RA, RQ], writes=[RA])
        S.op("dve", lambda e: e.tensor_single_scalar(out=Q[:], in_=A[:], scalar=-8.0, op=ALU.is_ge), reads=[RA], writes=[RQ])
        S.op("dve", lambda e: e.tensor_single_scalar(out=Q2[:], in_=A[:], scalar=7.0, op=ALU.is_le), reads=[RA], writes=[RQ2])
        S.op("dve", lambda e: e.tensor_tensor(out=cm[:], in0=Q[:], in1=Q2[:], op=ALU.mult), reads=[RQ, RQ2], writes=[Rcm])
        TE, RTE = self.tile("TE", [128, 8, 14, 64], BF16)
        S.op("dve", lambda e: e.tensor_tensor(out=TE0[:].rearrange("p h a q -> p (h a) q"), in0=TE0[:].rearrange("p h a q -> p (h a) q"),
                                              in1=cm[0:64, :].unsqueeze(1).broadcast_to([64, 120, 64]), op=ALU.mult),
             reads=[RTE0, Rcm], writes=[RTE0])
        S.op("pool", lambda e: e.tensor_copy(out=TE[0:64, :, :, :].rearrange("p h a q -> p h (a q)"),
                                             in_=TE0[:, :, 0:14, :].rearrange("p h a q -> p h (a q)")), reads=[RTE0], writes=[RTE])
        S.dma("sp", TE[64:128, :, :, :].rearrange("p h a q -> p h (a q)"), TE0[:, :, 1:15, :].rearrange("p h a q -> p h (a q)"),
              reads=[RTE0], writes=[RTE])
        qT, RqT = self.tile("nqT", [128, 4, T], BF16)
        kT, RkT = self.tile("nkT", [128, 4, T], BF16)
        Ve, RVe = self.tile("nVe", [128, nblk, 8, 65], BF16)
        Vo, RVo = self.tile("nVo", [128, nblk, 8, 65], BF16)
        Vs, RVs = self.tile("nVs", [128, nblk // 2, 512], BF16)
        S.op("pool", lambda e: e.memset(Ve[:], 1.0), writes=[RVe])
        S.op("pool", lambda e: e.memset(Vo[:], 1.0), writes=[RVo])
        pss = Rot([(t_[:].rearrange("p (h q) -> p h q", q=64), r_) for (t_, r_) in tps.items])
        pos_raw = self.rot("ops", [128, 512], F32, 4, ps=True)
        Ets = self.rot("Et", [128, 8, 64], BF16, 3)
        Es = self.rot("E", [128, 4, 8, 64], BF16, 2)
        recs = self.rot("rec", [64, 8], F32, 2)
        outs = self.rot("yno", [64, 8, 64], BF16, 3)
        for s in range(NS):
            S.dma("sp", qT[:], nq[s].rearrange("(c p) t -> p c t", p=128), reads=[self.dres(("nq", s))], writes=[RqT])
            S.dma("sp", kT[:], nk[s].rearrange("(c p) t -> p c t", p=128), reads=[self.dres(("nk", s))], writes=[RkT])
            hb = nblk // 2
            for (Vx, RVx, off, nb_tot) in ((Ve, RVe, 0, nblk), (Vo, RVo, 64, nblk - 1)):
                for b0 in range(0, nb_tot, hb):
                    nb_ = min(hb, nb_tot - b0)
                    S.dma("sp", Vs[:, 0:nb_, :], nv[s, off + b0 * 128:off + (b0 + nb_) * 128, :].rearrange("(b p) c -> p b c", p=128),
                          reads=[self.dres(("nv", s))], writes=[RVs])
                    S.op("pool", lambda e, Vx=Vx, b0=b0, nb_=nb_: e.tensor_copy(
                        out=Vx[:, b0:b0 + nb_, :, 0:64].rearrange("p b h n -> p (b h) n"),
                        in_=Vs[:, 0:nb_, :].rearrange("p b (h n) -> p (b h) n", n=64)), reads=[RVs], writes=[RVx])
            for i in range(rows):
                rs = min(max(i - 4, 0), rows - 8)
                E, RE = Es.next()
                for blk in range(4):
                    kr = rs + 2 * blk
                    tok0 = kr * 64
                    dib = kr - i + 7
                    psA, RpsA = pss.next()
                    psB, RpsB = pss.next()
                    Et, REt = Ets.next()

                    def mms(e, psA=psA, psB=psB, tok0=tok0, i=i):
                        for h in range(8):
                            pb = (h % 2) * 64
                            ps = psA if h % 2 == 0 else psB
                            ins = e.matmul(ps[:, h // 2, :], lhsT=kT[pb:pb + 64, h // 2, tok0:tok0 + 128], rhs=qT[pb:pb + 64, h // 2, i * 64:(i + 1) * 64],
                                           start=True, stop=True)
                        return ins
                    S.op("pe", mms, reads=[RkT, RqT], writes=[RpsA, RpsB])
                    S.op("act", lambda e, psA=psA, Et=Et: e.activation(out=Et[:, 0:4, :], in_=psA[:, 0:4, :], func=AF.Exp, scale=0.125), reads=[RpsA], writes=[REt])
                    S.op("act", lambda e, psB=psB, Et=Et: e.activation(out=Et[:, 4:8, :], in_=psB[:, 0:4, :], func=AF.Exp, scale=0.125), reads=[RpsB], writes=[REt])
                    S.op("dve", lambda e, Et=Et, E=E, blk=blk, dib=dib: e.tensor_tensor(
                        out=E[:, blk, :, :].rearrange("p (two c) q -> p two c q", two=2), in0=Et[:].rearrange("p (two c) q -> p two c q", two=2),
                        in1=TE[:].rearrange("p (c two) a q -> p two c a q", two=2)[:, :, :, dib, :], op=ALU.mult),
                         reads=[REt, RTE], writes=[RE])
                poA_, RpoA = pos_raw.next()
                poB_, RpoB = pos_raw.next()
                poA = poA_[0:64, 0:260].rearrange("p (h n) -> p h n", n=65)
                poB = poB_[0:64, 0:260].rearrange("p (h n) -> p h n", n=65)

                def mmo(e, E=E, rs=rs, poA=poA, poB=poB):
                    for h in range(8):
                        po = poA if h < 4 else poB
                        for blk in range(4):
                            kr = rs + 2 * blk
                            Vx = Ve if kr % 2 == 0 else Vo
                            ins = e.matmul(po[:, h % 4, :], lhsT=E[:, blk, (h % 2) * 4 + h // 2, :], rhs=Vx[:, kr // 2, h, :], start=(blk == 0), stop=(blk == 3))
                    return ins
                S.op("pe", mmo, reads=[RE, RVe, RVo], writes=[RpoA, RpoB])
                rec, Rrec = recs.next()
                o, Ro = outs.next()
                S.op("dve", lambda e, rec=rec, poA=poA: e.reciprocal(out=rec[:, 0:4], in_=poA[:, :, 64]), reads=[RpoA], writes=[Rrec])
                S.op("dve", lambda e, rec=rec, poB=poB: e.reciprocal(out=rec[:, 4:8], in_=poB[:, :, 64]), reads=[RpoB], writes=[Rrec])
                S.op("dve", lambda e, rec=rec, poA=poA, o=o: e.tensor_tensor(out=o[:, 0:4, :], in0=poA[:, :, 0:64],
                                                                            in1=rec[:, 0:4].unsqueeze(2).broadcast_to([64, 4, 64]), op=ALU.mult),
                     reads=[RpoA, Rrec], writes=[Ro])
                S.op("dve", lambda e, rec=rec, poB=poB, o=o: e.tensor_tensor(out=o[:, 4:8, :], in0=poB[:, :, 0:64],
                                                                            in1=rec[:, 4:8].unsqueeze(2).broadcast_to([64, 4, 64]), op=ALU.mult),
                     reads=[RpoB, Rrec], writes=[Ro])
                S.dma("pool", yn[s, i * 64:(i + 1) * 64, :], o[:].rearrange("p h n -> p (h n)"), reads=[Ro], writes=[self.dres(("yn", s))])
        self.end()

    def xupdate(self, xt, Rxt, nb, aT, RaT, nkc, W, RW, pss):
        S = self.S
        for b in range(nb):
            for half in range(2):
                ps, Rps = pss.next()

                def mm(e, ps=ps, b=b, half=half):
                    for kc in range(nkc):
                        ins = e.matmul(ps[:], lhsT=aT[:, kc, b * 128:(b + 1) * 128], rhs=W[:, kc, half * 512:(half + 1) * 512],
                                       start=(kc == 0), stop=(kc == nkc - 1))
                    return ins
                S.op("pe", mm, reads=[RaT, RW], writes=[Rps])
                S.op("dve", lambda e, ps=ps, b=b, half=half: e.tensor_tensor(
                    out=xt[:, b, half * 512:(half + 1) * 512], in0=xt[:, b, half * 512:(half + 1) * 512], in1=ps[:], op=ALU.add),
                    reads=[Rps, Rxt], writes=[Rxt])

    def phase_p4(self, l):
        S, T, NS = self.S, self.T, self.NS
        self.begin()
        TT = 512
        wbr, Rwbr = self.tile("wbr", [128, 8, D], BF16)
        wout, Rwout = self.tile("wout", [128, 8, D], BF16)
        stg = self.rot("wstg", [128, 2048], F32, 2)
        ident, Rid, _, _ = self.make_ident()
        self.load_weight(wbr, Rwbr, self.w["w_br_rwkv"][l], 512, D, stg, kc0=0)
        self.load_weight(wbr, Rwbr, self.w["w_br_nat"][l], 512, D, stg, kc0=4)
        self.load_weight(wout, Rwout, self.w["w_out"][l], D, D, stg)
        xts = self.rot("xt", [128, 4, D], F32, 2)
        yts = self.rot("yt", [128, 4, 1024], BF16, 2)
        yTs = self.rot("yT", [128, 8, TT], BF16, 2)
        sgts = self.rot("sgt", [128, 16, TT], BF16, 2)
        mTs = self.rot("mT", [128, 8, TT], BF16, 2)
        m1s = self.rot("m1", [128, TT], F32, 2)
        m2s = self.rot("m2", [128, TT], F32, 2)
        pTs = self.rot("pT", [128, 1024], BF16, 2, ps=True)
        pss = self.rot("ps", [128, 512], F32, 6, ps=True)
        xsrc = self.x_in if l == 0 else self.scr["xres"]
        xres, yr, yn, sg = (self.scr[k] for k in ("xres", "yr", "yn", "sg"))
        for s in range(NS):
            for t0 in range(0, T, TT):
                xt, Rxt = xts.next()
                yt, Ryt = yts.next()
                yT, RyT = yTs.next()
                sgt, Rsgt = sgts.next()
                mT, RmT = mTs.next()
                Rx = self.dres(("xres", s, t0))
                S.dma("sp", yt[:, :, 0:512], yr[s, t0:t0 + TT, :].rearrange("(b p) c -> p b c", p=128), reads=[self.dres(("yr", s))], writes=[Ryt])
                S.dma("sp", yt[:, :, 512:1024], yn[s, t0:t0 + TT, :].rearrange("(b p) c -> p b c", p=128), reads=[self.dres(("yn", s))], writes=[Ryt])
                S.dma("sp", sgt[:], sg[s, :, t0:t0 + TT].rearrange("(c p) t -> p c t", p=128), reads=[self.dres(("sg", s))], writes=[Rsgt])
                S.dma("sp", xt[:], xsrc[s, t0:t0 + TT, :].rearrange("(b p) d -> p b d", p=128), reads=([Rx] if l > 0 else []), writes=[Rxt])
                for b in range(4):
                    pT, RpT = pTs.next()

                    def tr(e, pT=pT, b=b, yt=yt):
                        for c in range(8):
                            ins = e.transpose(out=pT[:, c * 128:(c + 1) * 128], in_=yt[:, b, c * 128:(c + 1) * 128], identity=ident[:])
                        return ins
                    S.op("pe", tr, reads=[Ryt, Rid], writes=[RpT])
                    S.op("act", lambda e, pT=pT, b=b, yT=yT: e.copy(out=yT[:, :, b * 128:(b + 1) * 128], in_=pT[:].rearrange("p (k t) -> p k t", k=8)),
                         reads=[RpT], writes=[RyT])
                for oc in range(8):
                    ps1, Rps1 = pss.next()
                    ps2, Rps2 = pss.next()
                    m1, Rm1 = m1s.next()
                    m2, Rm2 = m2s.next()

                    def mm(e, ps1=ps1, ps2=ps2, oc=oc, yT=yT):
                        for kc in range(4):
                            e.matmul(ps1[:], lhsT=wbr[:, kc, oc * 128:(oc + 1) * 128], rhs=yT[:, kc, :], start=(kc == 0), stop=(kc == 3))
                        for kc in range(4):
                            ins = e.matmul(ps2[:], lhsT=wbr[:, 4 + kc, oc * 128:(oc + 1) * 128], rhs=yT[:, 4 + kc, :], start=(kc == 0), stop=(kc == 3))
                        return ins
                    S.op("pe", mm, reads=[Rwbr, RyT], writes=[Rps1, Rps2])
                    S.op("dve", lambda e, m1=m1, ps1=ps1, oc=oc, sgt=sgt: e.tensor_tensor(out=m1[:], in0=ps1[:], in1=sgt[:, oc, :], op=ALU.mult),
                         reads=[Rps1, Rsgt], writes=[Rm1])
                    S.op("dve", lambda e, m2=m2, ps2=ps2, oc=oc, sgt=sgt: e.tensor_tensor(out=m2[:], in0=ps2[:], in1=sgt[:, 8 + oc, :], op=ALU.mult),
                         reads=[Rps2, Rsgt], writes=[Rm2])
                    S.op("pool", lambda e, m1=m1, m2=m2, oc=oc, mT=mT: e.tensor_tensor(out=mT[:, oc, :], in0=m1[:], in1=m2[:], op=ALU.add),
                         reads=[Rm1, Rm2], writes=[RmT])
                self.xupdate(xt, Rxt, 4, mT, RmT, 8, wout, Rwout, pss)
                S.dma("pool", xres[s, t0:t0 + TT, :].rearrange("(b p) d -> p b d", p=128), xt[:], reads=[Rxt], writes=[Rx])
        self.end()

    def phase_p5(self, l):
        S, T, NS = self.S, self.T, self.NS
        self.begin()
        TT = 512
        wq, Rwq = self.tile("wq", [128, 8, D], BF16)
        wo, Rwo = self.tile("wo", [128, 8, D], BF16)
        wkv, Rwkv = self.tile("wkv", [128, 8, 2 * D], BF16)
        stg = self.rot("wstg", [128, 2048], F32, 2)
        ident, Rid, _, _ = self.make_ident()
        gb, Rgb = self.tile("gb", [128, D], F32)
        gm, Rgm = self.tile("gm", [128, D], F32)
        ones, Rones = self.tile("ones", [128, 128], BF16)
        S.op("pool", lambda e: e.memset(ones[:], 1.0), writes=[Rones])
        self.load_bcast(gb, Rgb, self.w["norm_x"][l], D)
        self.load_bcast(gm, Rgm, self.w["norm_mem"][l], D)
        self.load_weight(wkv, Rwkv, self.w["w_xkv"][l], D, 2 * D, stg)
        self.load_weight(wq, Rwq, self.w["w_xq"][l], D, D, stg)
        self.load_weight(wo, Rwo, self.w["w_xo"][l], D, D, stg)
        tmp = self.norm_tmp()
        xts = self.rot("xt", [128, 4, D], F32, 2)
        hTs = self.rot("hT", [128, 8, TT], BF16, 2)
        qTs = self.rot("qT", [128, 8, TT], BF16, 1)
        oTs = self.rot("oT", [128, 8, TT], BF16, 2)
        Es = self.rot("E", [128, 2, TT], BF16, 2)
        rdens = self.rot("rden", [128, TT], F32, 2)
        mt, Rmt = self.tile("memt", [128, 2, D], F32)
        mnT, RmnT = self.tile("memnT", [128, 8, NMEM], BF16)
        kT, RkT = self.tile("kT", [128, 8, NMEM], BF16)
        Vm, RVm = self.tile("Vm", [128, 2, D], BF16)
        pss = self.rot("ps", [128, 512], F32, 6, ps=True)
        xres = self.scr["xres"]
        for s in range(NS):
            S.dma("sp", mt[:], self.mem_in[s].rearrange("(b p) d -> p b d", p=128), writes=[Rmt])
            for b in range(2):
                self.norm_to_hT(mt, Rmt, b, gm, Rgm, mnT, RmnT, b * 128, ident, Rid, tmp)
            for cc in range(8):
                ps, Rps = pss.next()

                def mmk(e, ps=ps, cc=cc):
                    for kc in range(8):
                        ins = e.matmul(ps[:, 0:NMEM], lhsT=wkv[:, kc, cc * 128:(cc + 1) * 128], rhs=mnT[:, kc, :], start=(kc == 0), stop=(kc == 7))
                    return ins
                S.op("pe", mmk, reads=[Rwkv, RmnT], writes=[Rps])
                S.op("dve", lambda e, ps=ps, cc=cc: e.tensor_copy(out=kT[:, cc, :], in_=ps[:, 0:NMEM]), reads=[Rps], writes=[RkT])
            for mb in range(2):
                for half in range(2):
                    ps, Rps = pss.next()

                    def mmv(e, ps=ps, mb=mb, half=half):
                        for kc in range(8):
                            ins = e.matmul(ps[:], lhsT=mnT[:, kc, mb * 128:(mb + 1) * 128], rhs=wkv[:, kc, D + half * 512:D + (half + 1) * 512],
                                           start=(kc == 0), stop=(kc == 7))
                        return ins
                    S.op("pe", mmv, reads=[Rwkv, RmnT], writes=[Rps])
                    S.op("dve", lambda e, ps=ps, mb=mb, half=half: e.tensor_copy(out=Vm[:, mb, half * 512:(half + 1) * 512], in_=ps[:]),
                         reads=[Rps], writes=[RVm])
            for t0 in range(0, T, TT):
                xt, Rxt = xts.next()
                hT, RhT = hTs.next()
                qT, RqT = qTs.next()
                oT, RoT = oTs.next()
                Rx = self.dres(("xres", s, t0))
                S.dma("sp", xt[:], xres[s, t0:t0 + TT, :].rearrange("(b p) d -> p b d", p=128), reads=[Rx], writes=[Rxt])
                for b in range(4):
                    self.norm_to_hT(xt, Rxt, b, gb, Rgb, hT, RhT, b * 128, ident, Rid, tmp)
                for cc in range(8):
                    ps, Rps = pss.next()

                    def mmq(e, ps=ps, cc=cc, hT=hT):
                        for kc in range(8):
                            ins = e.matmul(ps[:], lhsT=wq[:, kc, cc * 128:(cc + 1) * 128], rhs=hT[:, kc, :], start=(kc == 0), stop=(kc == 7))
                        return ins
                    S.op("pe", mmq, reads=[Rwq, RhT], writes=[Rps])
                    S.op("dve", lambda e, ps=ps, cc=cc, qT=qT: e.tensor_copy(out=qT[:, cc, :], in_=ps[:]), reads=[Rps], writes=[RqT])
                for hd in range(4):
                    E, RE = Es.next()
                    rden, Rrden = rdens.next()
                    for mb in range(2):
                        ps, Rps = pss.next()

                        def mms(e, ps=ps, mb=mb, hd=hd, qT=qT):
                            for j in range(2):
                                ins = e.matmul(ps[:], lhsT=kT[:, 2 * hd + j, mb * 128:(mb + 1) * 128], rhs=qT[:, 2 * hd + j, :], start=(j == 0), stop=(j == 1))
                            return ins
                        S.op("pe", mms, reads=[RkT, RqT], writes=[Rps])
                        S.op("act", lambda e, ps=ps, mb=mb, E=E: e.activation(out=E[:, mb, :], in_=ps[:], func=AF.Exp, scale=1.0 / 16.0),
                             reads=[Rps], writes=[RE])
                    ps, Rps = pss.next()

                    def mmd(e, ps=ps, E=E):
                        for mb in range(2):
                            ins = e.matmul(ps[:], lhsT=ones[:], rhs=E[:, mb, :], start=(mb == 0), stop=(mb == 1))
                        return ins
                    S.op("pe", mmd, reads=[Rones, RE], writes=[Rps])
                    S.op("dve", lambda e, ps=ps, rden=rden: e.reciprocal(out=rden[:], in_=ps[:]), reads=[Rps], writes=[Rrden])
                    for j in range(2):
                        ps, Rps = pss.next()

                        def mmo(e, ps=ps, hd=hd, j=j, E=E):
                            for mb in range(2):
                                c0 = hd * 256 + j * 128
                                ins = e.matmul(ps[:], lhsT=Vm[:, mb, c0:c0 + 128], rhs=E[:, mb, :], start=(mb == 0), stop=(mb == 1))
                            return ins
                        S.op("pe", mmo, reads=[RVm, RE], writes=[Rps])
                        S.op("dve", lambda e, ps=ps, hd=hd, j=j, oT=oT, rden=rden: e.tensor_tensor(out=oT[:, 2 * hd + j, :], in0=ps[:], in1=rden[:], op=ALU.mult),
                             reads=[Rps, Rrden], writes=[RoT])
                self.xupdate(xt, Rxt, 4, oT, RoT, 8, wo, Rwo, pss)
                S.dma("pool", xres[s, t0:t0 + TT, :].rearrange("(b p) d -> p b d", p=128), xt[:], reads=[Rxt], writes=[Rx])
        self.end()

    def phase_p6(self, l):
        S, T, NS = self.S, self.T, self.NS
        self.begin()
        TT = 256
        NB = TT // 128
        last = (l == self.DEPTH - 1)
        w1, Rw1 = self.tile("w1", [128, 8, DFF], BF16)
        w2, Rw2 = self.tile("w2", [128, 32, D], BF16)
        stg = self.rot("wstg", [128, 2048], F32, 2)
        ident, Rid, _, _ = self.make_ident()
        gb, Rgb = self.tile("gb", [128, D], F32)
        self.load_bcast(gb, Rgb, self.w["norm_ff"][l], D)
        if last:
            gf, Rgf = self.tile("gf", [128, D], F32)
            self.load_bcast(gf, Rgf, self.w["norm_final"], D)
        self.load_weight(w1, Rw1, self.w["w_ff1"][l], D, DFF, stg)
        self.load_weight(w2, Rw2, self.w["w_ff2"][l], DFF, D, stg)
        tmp = self.norm_tmp()
        xts = self.rot("xt", [128, NB, D], F32, 2)
        hTs = self.rot("hT", [128, 8, TT], BF16, 2)
        uTs = self.rot("uT", [128, 32, TT], BF16, 1)
        rls = self.rot("rl", [128, TT], BF16, 3)
        pss = self.rot("ps", [128, 512], F32, 6, ps=True)
        xres = self.scr["xres"]
        for s in range(NS):
            for t0 in range(0, T, TT):
                xt, Rxt = xts.next()
                hT, RhT = hTs.next()
                uT, RuT = uTs.next()
                Rx = self.dres(("xres", s, t0 // 512 * 512))
                S.dma("sp", xt[:], xres[s, t0:t0 + TT, :].rearrange("(b p) d -> p b d", p=128), reads=[Rx], writes=[Rxt])
                for b in range(NB):
                    self.norm_to_hT(xt, Rxt, b, gb, Rgb, hT, RhT, b * 128, ident, Rid, tmp)
                for fc in range(32):
                    ps, Rps = pss.next()
                    rl, Rrl = rls.next()

                    def mm1(e, ps=ps, fc=fc, hT=hT):
                        for kc in range(8):
                            ins = e.matmul(ps[:, 0:TT], lhsT=w1[:, kc, fc * 128:(fc + 1) * 128], rhs=hT[:, kc, :], start=(kc == 0), stop=(kc == 7))
                        return ins
                    S.op("pe", mm1, reads=[Rw1, RhT], writes=[Rps])
                    S.op("act", lambda e, ps=ps, rl=rl: e.activation(out=rl[:], in_=ps[:, 0:TT], func=AF.Relu), reads=[Rps], writes=[Rrl])
                    S.op("pool", lambda e, rl=rl, fc=fc, uT=uT: e.tensor_tensor(out=uT[:, fc, :], in0=rl[:], in1=rl[:], op=ALU.mult),
                         reads=[Rrl], writes=[RuT])
                self.xupdate(xt, Rxt, NB, uT, RuT, 32, w2, Rw2, pss)
                if not last:
                    S.dma("pool", xres[s, t0:t0 + TT, :].rearrange("(b p) d -> p b d", p=128), xt[:], reads=[Rxt], writes=[Rx])
                else:
                    for b in range(NB):
                        junk, Rjunk = tmp["junk"].next()
                        ss, Rss = tmp["ss"].next()
                        S.op("act", lambda e, junk=junk, ss=ss, b=b, xt=xt: e.activation(out=junk[:], in_=xt[:, b, :], func=AF.Square, accum_out=ss[:, 0:1]),
                             reads=[Rxt], writes=[Rjunk, Rss])
                        S.op("act", lambda e, ss=ss: e.activation(out=ss[:, 1:2], in_=ss[:, 0:1], func=AF.Sqrt, scale=1.0 / D, bias=self.eps_t[:, 0:1]),
                             reads=[Rss, self.Reps], writes=[Rss])
                        S.op("dve", lambda e, ss=ss: e.reciprocal(out=ss[:, 2:3], in_=ss[:, 1:2]), reads=[Rss], writes=[Rss])
                        S.op("dve", lambda e, ss=ss, b=b, xt=xt: e.scalar_tensor_tensor(out=xt[:, b, :], in0=xt[:, b, :], scalar=ss[:, 2:3], in1=gf[:],
                                                                                         op0=ALU.mult, op1=ALU.mult),
                             reads=[Rxt, Rss, Rgf], writes=[Rxt])
                    Ry = self.dres(("y", s))
                    S.dma("pool", self.y_out[s, t0:t0 + TT, :].rearrange("(b p) d -> p b d", p=128), xt[:], reads=[Rxt], writes=[Ry])
        self.end()


_CACHE = {}


def _get_nc(T, NS, DEPTH, dbg=(), stop=None):
    key = (T, NS, DEPTH, tuple(dbg), stop)
    if key not in _CACHE:
        _CACHE[key] = Builder(T, NS, DEPTH, dbg, stop).build()
    return _CACHE[key]


def kernel(**inputs):
    NCORES, NS, T, DEPTH = 8, 2, 4096, 2
    xs = np.concatenate([np.asarray(inputs["x_prompt"], np.float32), np.asarray(inputs["x_sample"], np.float32)], 0)
    ms = np.concatenate([np.asarray(inputs["mem_prompt"], np.float32), np.asarray(inputs["mem_sample"], np.float32)], 0)
    nseq = xs.shape[0]
    slots = [[c, 8 + c if 8 + c < nseq else c] for c in range(NCORES)]
    wnames = [n for n, _ in WEIGHT_SPECS] + ["norm_final"]
    wmap = {n: np.ascontiguousarray(np.asarray(inputs[n], np.float32)) for n in wnames}
    nc = _get_nc(T, NS, DEPTH)
    in_maps = []
    for c in range(NCORES):
        m = dict(wmap)
        m["x"] = np.ascontiguousarray(xs[slots[c]])
        m["mem"] = np.ascontiguousarray(ms[slots[c]])
        in_maps.append(m)
    res = run_bass_kernel_spmd(nc, in_maps, core_ids=list(range(NCORES)))
    y = np.zeros_like(xs)
    for c in range(NCORES):
        yc = res.results[c]["y"]
        y[slots[c][0]] = yc[0]
        if slots[c][1] != slots[c][0]:
            y[slots[c][1]] = yc[1]
    nb = np.asarray(inputs["x_prompt"]).shape[0]
    return (y[:nb], y[nb:])
```

```python
import numpy as np
from contextlib import ExitStack
import concourse.bass as bass
import concourse.mybir as mybir
from concourse.bass_utils import run_bass_kernel_spmd

F32 = mybir.dt.float32
BF16 = mybir.dt.bfloat16
AF = mybir.ActivationFunctionType
ALU = mybir.AluOpType
AX = mybir.AxisListType

D = 1024
DIN = 5504
RW = 1920
NQ0 = 1920
NV0 = 2944
G0 = 3456
NMEM = 256
DFF = 4096
EPS = 1e-6
GN_EPS = 1e-5 * 64
CH = 128

EPOCH = 12000
ENGS = ("pe", "dve", "act", "pool", "sp")


class Res:
    __slots__ = ("name", "last_w", "readers", "dsem", "w_is_dma")

    def __init__(self, name=""):
        self.name = name
        self.last_w = None
        self.readers = []
        self.dsem = None
        self.w_is_dma = False


class Sched:
    def __init__(self, nc, stack):
        self.nc = nc
        self.stack = stack
        self.ops = {e: [] for e in ENGS}
        self.cnt = {e: 0 for e in ENGS}
        self.esems = {e: [] for e in ENGS}
        self.known = {e: {} for e in ENGS}
        self.sems = {}
        self.nsem = 0
        self.same_engine_raw = True
        self.dma_latest = {}
        self.free_dsems = []

    def new_sem(self, tag):
        h = self.stack.enter_context(self.nc.semaphore(f"{tag}_{self.nsem}"))
        sid = self.nsem
        self.nsem += 1
        self.sems[sid] = h
        return sid

    def _eng_point(self, eng):
        n = self.cnt[eng]
        self.cnt[eng] = n + 1
        ep, v = divmod(n, EPOCH)
        while len(self.esems[eng]) <= ep:
            self.esems[eng].append(self.new_sem(f"e{eng}"))
        return (self.esems[eng][ep], v + 1)

    def _dma_point(self, res):
        d = res.dsem
        if d is None or d[1] + 16 > EPOCH:
            if self.free_dsems and d is None:
                d = self.free_dsems.pop()
            else:
                d = [self.new_sem("d"), 0]
            res.dsem = d
        d[1] += 16
        self.dma_latest[d[0]] = d[1]
        return (d[0], d[1])

    def recycle(self, resources):
        seen = set()
        for r in resources:
            d = r.dsem
            if d is not None and id(d) not in seen and d[1] + 64 < EPOCH:
                seen.add(id(d))
                self.free_dsems.append(d)
            r.dsem = None

    def _need(self, eng, waits, pt):
        sid, val = pt
        if self.known[eng].get(sid, 0) >= val:
            return
        if waits.get(sid, 0) < val:
            waits[sid] = val

    def _deps(self, eng, reads, writes, is_dma=False):
        waits = {}
        for r in reads:
            if r.last_w is not None:
                w_eng = r.last_w[2]
                if w_eng != eng or is_dma or (self.same_engine_raw and eng != "pe"):
                    self._need(eng, waits, r.last_w[:2])
        for w in writes:
            if w.last_w is not None:
                w_eng = w.last_w[2]
                same_dma_group = is_dma and w.w_is_dma and not w.readers
                if (w_eng != eng or is_dma) and not same_dma_group:
                    self._need(eng, waits, w.last_w[:2])
            for (sid, val, r_eng) in w.readers:
                if r_eng != eng or is_dma:
                    self._need(eng, waits, (sid, val))
        for sid, val in waits.items():
            self.known[eng][sid] = val
        return list(waits.items())

    def op(self, eng, fn, reads=(), writes=()):
        waits = self._deps(eng, reads, writes)
        pt = self._eng_point(eng)
        self.ops[eng].append((waits, fn, pt[0], 1))
        for r in reads:
            r.readers.append((pt[0], pt[1], eng))
        for w in writes:
            w.last_w = (pt[0], pt[1], eng)
            w.readers = []
            w.w_is_dma = False
        return pt

    def dma(self, queue, out, in_, reads=(), writes=(), **kw):
        waits = self._deps(queue, reads, writes, is_dma=True)
        pt = self._dma_point(writes[0])

        def fn(e, out=out, in_=in_, kw=kw):
            return e.dma_start(out=out, in_=in_, **kw)
        self.ops[queue].append((waits, fn, pt[0], 16))
        for r in reads:
            r.readers.append((pt[0], pt[1], "dma"))
        for w in writes:
            w.last_w = (pt[0], pt[1], "dma")
            w.readers = []
            w.w_is_dma = True
            if w is not writes[0]:
                w.dsem = writes[0].dsem
        return pt

    def final_wait(self, eng, resources):
        waits = {}
        for r in resources:
            if r.last_w is not None:
                sid, val = r.last_w[:2]
                waits[sid] = max(waits.get(sid, 0), val)
        self.ops[eng].append((list(waits.items()), None, None, 0))

    def barrier(self):
        pts = {}
        for e in ENGS:
            n = self.cnt[e]
            if n > 0:
                ep, v = divmod(n - 1, EPOCH)
                pts[self.esems[e][ep]] = v + 1
        for sid, val in self.dma_latest.items():
            pts[sid] = max(pts.get(sid, 0), val)
        for e in ENGS:
            waits = []
            for sid, val in pts.items():
                if self.known[e].get(sid, 0) < val:
                    waits.append((sid, val))
                    self.known[e][sid] = val
            self.ops[e].append((waits, None, None, 0))

    def emit(self):
        nc = self.nc
        sems = self.sems

        def run(engname):
            def body(e):
                for (waits, fn, sid_, inc) in self.ops[engname]:
                    for sid, val in waits:
                        e.wait_ge(sems[sid], val)
                    if fn is None:
                        continue
                    ins = fn(e)
                    ins.then_inc(sems[sid_], inc)
            return body

        with nc.Block() as block:
            block.tensor(run("pe"))
            block.vector(run("dve"))
            block.scalar(run("act"))
            block.gpsimd(run("pool"))
            block.sync(run("sp"))
        self.ops = {e: [] for e in ENGS}


class Rot:
    def __init__(self, items):
        self.items = items
        self.i = 0

    def next(self):
        it = self.items[self.i % len(self.items)]
        self.i += 1
        return it


WEIGHT_SPECS = [
    ("norm_mix", (D,)), ("w_in", (D, DIN)), ("mu_prev", (RW,)), ("mu_next", (RW,)),
    ("w0", (2, 512)), ("w_up", (2, 64, 512)), ("a0", (2, 512)), ("a_up", (2, 64, 512)),
    ("g_up", (128, 512)), ("k_k", (512,)), ("k_a", (512,)), ("r_k", (8, 64)),
    ("gn_g", (512,)), ("gn_b", (512,)), ("rpb", (8, 15, 31)),
    ("w_br_rwkv", (512, D)), ("w_br_nat", (512, D)), ("w_out", (D, D)),
    ("norm_x", (D,)), ("norm_mem", (D,)), ("w_xq", (D, D)), ("w_xkv", (D, 2 * D)), ("w_xo", (D, D)),
    ("norm_ff", (D,)), ("w_ff1", (D, DFF)), ("w_ff2", (DFF, D)),
]


class Builder:
    def __init__(self, T, NS, DEPTH, dbg=(), stop=None, ext_in=(), phases=None):
        self.T, self.NS, self.DEPTH = T, NS, DEPTH
        self.ext_in = set(ext_in)
        self.phases = phases or ("p1", "p3", "p2", "p4", "p5", "p6")
        self.dbg = set(dbg)
        self.stop = stop
        self.nc = bass.Bass("TRN2", target_bir_lowering=False)
        nc = self.nc
        self.x_in = nc.dram_tensor("x", [NS, T, D], F32, kind="ExternalInput").ap()
        self.mem_in = nc.dram_tensor("mem", [NS, NMEM, D], F32, kind="ExternalInput").ap()
        self.w = {}
        for name, shp in WEIGHT_SPECS:
            self.w[name] = nc.dram_tensor(name, [DEPTH] + list(shp), F32, kind="ExternalInput").ap()
        self.w["norm_final"] = nc.dram_tensor("norm_final", [D], F32, kind="ExternalInput").ap()
        self.y_out = nc.dram_tensor("y", [NS, T, D], F32, kind="ExternalOutput").ap()
        self.scr = {}
        self.scr_res = {}

    def scratch(self, name, shape, dt):
        kind = "ExternalOutput" if name in self.dbg else ("ExternalInput" if name in self.ext_in else "Internal")
        t = self.nc.dram_tensor(name, shape, dt, kind=kind).ap()
        self.scr[name] = t
        return t

    def dres(self, key):
        r = self.scr_res.get(key)
        if r is None:
            r = Res(str(key))
            self.scr_res[key] = r
        return r

    def build(self):
        nc = self.nc
        T, NS = self.T, self.NS
        self.scratch("xres", [NS, T, D], F32)
        self.scratch("pf", [NS, RW, T], F32)
        self.scratch("nq", [NS, 512, T], BF16)
        self.scratch("nk", [NS, 512, T], BF16)
        self.scratch("nv", [NS, T, 512], BF16)
        self.scratch("sg", [NS, 2048, T], BF16)
        self.scratch("yr", [NS, T, 512], BF16)
        self.scratch("yn", [NS, T, 512], BF16)
        self.scratch("yf", [NS, T, 512], F32)
        self.scratch("bon", [NS, T, 8], F32)
        self.scratch("phd", [NS, T // 128, 128, 1920], F32)
        with ExitStack() as st:
            self.S = Sched(nc, st)
            done = False
            for l in range(self.DEPTH):
                for ph in self.phases:
                    getattr(self, "phase_" + ph)(l)
                    if self.stop == (l, ph):
                        done = True
                        break
                if done:
                    break
            with ExitStack() as ph:
                self.S.final_wait("pool", list(self.scr_res.values()))
                self.S.emit()
        return nc

    def begin(self):
        self.S.barrier()
        self.phc = getattr(self, "phc", 0) + 1
        self.ph = ExitStack()
        self.ph_res = []
        return self.ph

    def end(self):
        self.S.emit()
        self.S.recycle(self.ph_res)
        self.ph.close()

    def tile(self, name, shape, dt):
        t = self.ph.enter_context(self.nc.sbuf_tensor(f"{name}_{self.phc}", shape, dt))
        r = Res(name)
        self.ph_res.append(r)
        return t, r

    def psum(self, name, shape, dt):
        t = self.ph.enter_context(self.nc.psum_tensor(f"{name}_{self.phc}", shape, dt))
        r = Res(name)
        self.ph_res.append(r)
        return t, r

    def rot(self, name, shape, dt, n, ps=False):
        f = self.psum if ps else self.tile
        return Rot([f(f"{name}{i}", shape, dt) for i in range(n)])

    def make_ident(self):
        S = self.S
        idf, Ridf = self.tile("identf", [128, 128], F32)
        idb, Ridb = self.tile("identb", [128, 128], BF16)

        S.op("pool", lambda e: e.memset(idf[:], 0.0), writes=[Ridf])
        S.op("pool", lambda e: e.affine_select(out=idf[:], in_=idf[:], pattern=[[-1, 128]], compare_op=ALU.not_equal,
                                               fill=1.0, base=0, channel_multiplier=1), reads=[Ridf], writes=[Ridf])
        S.op("dve", lambda e: e.tensor_copy(out=idb[:], in_=idf[:]), reads=[Ridf], writes=[Ridb])
        return idb, Ridb, idf, Ridf

    def load_weight(self, dst, Rdst, src2d, K, cols, stg, col0=0, kc0=0):
        S = self.S
        nk = K // 128
        srcv = src2d.rearrange("(kc p) c -> p kc c", p=128)
        engs = ("pool", "dve", "act")
        i = 0
        for kc in range(nk):
            for c0 in range(0, cols, 2048):
                cw = min(2048, cols - c0)
                stile, Rst = stg.next()
                S.dma("sp", stile[:, 0:cw], srcv[:, kc, c0:c0 + cw], writes=[Rst])
                eng = engs[i % 3]
                i += 1
                o = dst[:, kc0 + kc, col0 + c0:col0 + c0 + cw]
                if eng == "act":
                    S.op("act", lambda e, o=o, s=stile, cw=cw: e.copy(out=o, in_=s[:, 0:cw]), reads=[Rst], writes=[Rdst])
                else:
                    S.op(eng, lambda e, o=o, s=stile, cw=cw: e.tensor_copy(out=o, in_=s[:, 0:cw]), reads=[Rst], writes=[Rdst])

    def load_bcast(self, dst, Rdst, src1d, n):
        self.S.dma("sp", dst[:, 0:n], src1d.rearrange("(o n) -> o n", o=1).partition_broadcast(128), writes=[Rdst])

    def norm_to_hT(self, xt, Rxt, b, gb, Rgb, hT, RhT, tcol, ident, Rid, tmp):
        S = self.S
        junk, Rjunk = tmp["junk"].next()
        ss, Rss = tmp["ss"].next()
        h, Rh = tmp["h"].next()
        pT, RpT = tmp["pT"].next()
        S.op("act", lambda e: e.activation(out=junk[:], in_=xt[:, b, :], func=AF.Square, accum_out=ss[:, 0:1]),
             reads=[Rxt], writes=[Rjunk, Rss])

        S.op("act", lambda e: e.activation(out=ss[:, 1:2], in_=ss[:, 0:1], func=AF.Sqrt, scale=1.0 / D, bias=self.eps_t[:, 0:1]),
             reads=[Rss, self.Reps], writes=[Rss])
        S.op("dve", lambda e: e.reciprocal(out=ss[:, 2:3], in_=ss[:, 1:2]), reads=[Rss], writes=[Rss])
        S.op("dve", lambda e: e.scalar_tensor_tensor(out=h[:], in0=xt[:, b, :], scalar=ss[:, 2:3], in1=gb[:],
                                                     op0=ALU.mult, op1=ALU.mult),
             reads=[Rxt, Rss, Rgb], writes=[Rh])

        def tr(e):
            for kc in range(8):
                ins = e.transpose(out=pT[:, kc * 128:(kc + 1) * 128], in_=h[:, kc * 128:(kc + 1) * 128], identity=ident[:])
            return ins
        S.op("pe", tr, reads=[Rh, Rid], writes=[RpT])
        S.op("act", lambda e: e.copy(out=hT[:, :, tcol:tcol + 128], in_=pT[:].rearrange("p (k t) -> p k t", k=8)),
             reads=[RpT], writes=[RhT])
        return ss

    def norm_tmp(self):
        self.eps_t, Re = self.tile("eps_t", [128, 2], F32)
        eps_t = self.eps_t
        self.S.op("pool", lambda e: e.memset(eps_t[:], EPS), writes=[Re])
        self.Reps = Re
        return {
            "junk": self.rot("junk", [128, D], BF16, 1),
            "ss": self.rot("ss", [128, 4], F32, 4),
            "h": self.rot("h", [128, D], BF16, 2),
            "pT": self.rot("pT", [128, 1024], BF16, 2, ps=True),
        }

    def phase_p1(self, l):
        S, T, NS = self.S, self.T, self.NS
        self.begin()
        TT = 512
        win, Rwin = self.tile("win", [128, 8, DIN], BF16)
        stg = self.rot("wstg", [128, 2048], F32, 2)
        gb, Rgb = self.tile("gb", [128, D], F32)
        ident, Rid, _, _ = self.make_ident()
        self.load_bcast(gb, Rgb, self.w["norm_mix"][l], D)
        self.load_weight(win, Rwin, self.w["w_in"][l], D, DIN, stg)
        tmp = self.norm_tmp()
        xts = self.rot("xt", [128, 4, D], F32, 2)
        hTs = self.rot("hT", [128, 8, TT], BF16, 2)
        pss = self.rot("ps", [128, 512], F32, 4, ps=True)
        of32 = self.rot("of32", [128, 512], F32, 3)
        obf = self.rot("obf", [128, 512], BF16, 4)
        xsrc = self.x_in if l == 0 else self.scr["xres"]
        pf, nq, nk, nv, sg = (self.scr[k] for k in ("pf", "nq", "nk", "nv", "sg"))
        ev = 0
        for s in range(NS):
            for t0 in range(0, T, TT):
                xt, Rxt = xts.next()
                hT, RhT = hTs.next()
                xr = [self.dres(("xres", s, t0))] if l > 0 else []
                S.dma("sp", xt[:], xsrc[s, t0:t0 + TT, :].rearrange("(b p) d -> p b d", p=128), reads=xr, writes=[Rxt])
                for b in range(4):
                    self.norm_to_hT(xt, Rxt, b, gb, Rgb, hT, RhT, b * 128, ident, Rid, tmp)
                for cc in range(DIN // 128):
                    c0 = cc * 128
                    if NV0 <= c0 < G0:
                        continue
                    ps, Rps = pss.next()

                    def mm(e, ps=ps, c0=c0, hT=hT):
                        for kc in range(8):
                            ins = e.matmul(ps[:], lhsT=win[:, kc, c0:c0 + 128], rhs=hT[:, kc, :], start=(kc == 0), stop=(kc == 7))
                        return ins
                    S.op("pe", mm, reads=[Rwin, RhT], writes=[Rps])
                    if c0 < RW:
                        o, Ro = of32.next()
                        eng = "dve" if ev % 2 == 0 else "act"
                        ev += 1
                        if eng == "dve":
                            S.op("dve", lambda e, o=o, ps=ps: e.tensor_copy(out=o[:], in_=ps[:]), reads=[Rps], writes=[Ro])
                        else:
                            S.op("act", lambda e, o=o, ps=ps: e.copy(out=o[:], in_=ps[:]), reads=[Rps], writes=[Ro])
                        S.dma("pool", pf[s, c0:c0 + 128, t0:t0 + TT], o[:], reads=[Ro], writes=[self.dres(("pf", s))])
                    elif c0 < NV0:
                        o, Ro = obf.next()
                        S.op("dve", lambda e, o=o, ps=ps: e.tensor_copy(out=o[:], in_=ps[:]), reads=[Rps], writes=[Ro])
                        if c0 < NQ0 + 512:
                            S.dma("pool", nq[s, c0 - NQ0:c0 - NQ0 + 128, t0:t0 + TT], o[:], reads=[Ro], writes=[self.dres(("nq", s))])
                        else:
                            c1 = c0 - NQ0 - 512
                            S.dma("pool", nk[s, c1:c1 + 128, t0:t0 + TT], o[:], reads=[Ro], writes=[self.dres(("nk", s))])
                    else:
                        o, Ro = obf.next()
                        S.op("act", lambda e, o=o, ps=ps: e.activation(out=o[:], in_=ps[:], func=AF.Sigmoid), reads=[Rps], writes=[Ro])
                        c1 = c0 - G0
                        S.dma("pool", sg[s, c1:c1 + 128, t0:t0 + TT], o[:], reads=[Ro], writes=[self.dres(("sg", s))])
                for b in range(4):
                    ps, Rps = pss.next()

                    def mmv(e, ps=ps, b=b, hT=hT):
                        for kc in range(8):
                            ins = e.matmul(ps[:], lhsT=hT[:, kc, b * 128:(b + 1) * 128], rhs=win[:, kc, NV0:NV0 + 512], start=(kc == 0), stop=(kc == 7))
                        return ins
                    S.op("pe", mmv, reads=[Rwin, RhT], writes=[Rps])
                    o, Ro = obf.next()
                    S.op("dve", lambda e, o=o, ps=ps: e.tensor_copy(out=o[:], in_=ps[:]), reads=[Rps], writes=[Ro])
                    S.dma("pool", nv[s, t0 + b * 128:t0 + (b + 1) * 128, :], o[:], reads=[Ro], writes=[self.dres(("nv", s))])
        self.end()

    def phase_p2(self, l):
        S, T, NS = self.S, self.T, self.NS
        self.begin()
        C = 128
        NCH = T // C
        LD = 0.6065306597126334
        pf, yf, yr = self.scr["pf"], self.scr["yf"], self.scr["yr"]
        bon = self.scr["bon"]
        phd = self.scr["phd"]
        op = S.op
        bc = lambda ap, shape: ap.broadcast_to(shape)
        identb, Ridb, identf, Ridf = self.make_ident()
        mup, Rmup = self.tile("mup", [128, 15], F32)
        mun, Rmun = self.tile("mun", [128, 15], F32)
        w0t, Rw0 = self.tile("w0t", [128, 2, 4], F32)
        a0t, Ra0 = self.tile("a0t", [128, 2, 4], F32)
        kkc, Rkkc = self.tile("kkc", [128, 4], F32)
        kac, Rkac = self.tile("kac", [128, 4], F32)
        omka, Romka = self.tile("omka", [128, 4], F32)
        rkc, Rrkc = self.tile("rkc", [128, 4], F32)
        S.dma("sp", mup[:], self.w["mu_prev"][l].rearrange("(c p) -> p c", p=128), writes=[Rmup], allow_slow_non_contiguous=True)
        S.dma("sp", mun[:], self.w["mu_next"][l].rearrange("(c p) -> p c", p=128), writes=[Rmun], allow_slow_non_contiguous=True)
        for d in range(2):
            S.dma("sp", w0t[:, d, :], self.w["w0"][l, d].rearrange("(c p) -> p c", p=128), writes=[Rw0], allow_slow_non_contiguous=True)
            S.dma("sp", a0t[:, d, :], self.w["a0"][l, d].rearrange("(c p) -> p c", p=128), writes=[Ra0], allow_slow_non_contiguous=True)
        S.dma("sp", kkc[:], self.w["k_k"][l].rearrange("(c p) -> p c", p=128), writes=[Rkkc], allow_slow_non_contiguous=True)
        S.dma("sp", kac[:], self.w["k_a"][l].rearrange("(c p) -> p c", p=128), writes=[Rkac], allow_slow_non_contiguous=True)
        S.dma("sp", rkc[:], self.w["r_k"][l].rearrange("h n -> (h n)").rearrange("(c p) -> p c", p=128), writes=[Rrkc], allow_slow_non_contiguous=True)
        op("dve", lambda e: e.tensor_scalar(out=omka[:], in0=kac[:], scalar1=-1.0, scalar2=1.0, op0=ALU.mult, op1=ALU.add), reads=[Rkac], writes=[Romka])
        gng, Rgng = self.tile("gng", [128, 512], F32)
        gnb, Rgnb = self.tile("gnb", [128, 512], F32)
        self.load_bcast(gng, Rgng, self.w["gn_g"][l], 512)
        self.load_bcast(gnb, Rgnb, self.w["gn_b"][l], 512)
        wstg, Rwstg = self.tile("lstg", [128, 3, 512], F32)
        S.dma("sp", wstg[:, 0, :], self.w["w_up"][l].rearrange("d l c -> (d l) c"), writes=[Rwstg])
        S.dma("sp", wstg[:, 1, :], self.w["a_up"][l].rearrange("d l c -> (d l) c"), writes=[Rwstg])
        S.dma("sp", wstg[:, 2, :], self.w["g_up"][l], writes=[Rwstg])
        wupz, Rwupz = self.tile("wupz", [128, 2, 512], BF16)
        aupz, Raupz = self.tile("aupz", [128, 2, 512], BF16)
        gup, Rgup = self.tile("gup", [128, 512], BF16)
        op("pool", lambda e: e.memset(wupz[:].rearrange("p d c -> p (d c)"), 0.0), writes=[Rwupz])
        op("pool", lambda e: e.memset(aupz[:].rearrange("p d c -> p (d c)"), 0.0), writes=[Raupz])
        for d in range(2):
            op("dve", lambda e, d=d: e.tensor_copy(out=wupz[d * 64:(d + 1) * 64, d, :], in_=wstg[d * 64:(d + 1) * 64, 0, :]), reads=[Rwstg, Rwupz], writes=[Rwupz])
            op("dve", lambda e, d=d: e.tensor_copy(out=aupz[d * 64:(d + 1) * 64, d, :], in_=wstg[d * 64:(d + 1) * 64, 1, :]), reads=[Rwstg, Raupz], writes=[Raupz])
        op("dve", lambda e: e.tensor_copy(out=gup[:], in_=wstg[:, 2, :]), reads=[Rwstg], writes=[Rgup])
        onesf, Ronesf = self.tile("onesf", [128, 128], F32)
        op("pool", lambda e: e.memset(onesf[:], 1.0), writes=[Ronesf])
        blk1, Rblk1 = self.tile("blk1", [128, 128], F32)
        hsel, Rhsel = self.tile("hsel", [128, 2], F32)
        bdm, Rbdm = self.tile("bdm", [128, 4, 2, 64], F32)

        def mkz(e):
            e.memset(blk1[:], 0.0)
            e.memset(hsel[:], 0.0)
            return e.memset(bdm[:].rearrange("p c h i -> p (c h i)"), 0.0)
        op("dve", mkz, writes=[Rblk1, Rhsel, Rbdm])

        def mko(e):
            e.memset(blk1[0:64, 0:64], 1.0)
            e.memset(blk1[64:128, 64:128], 1.0)
            e.memset(hsel[0:64, 0:1], 1.0)
            e.memset(hsel[64:128, 1:2], 1.0)
            e.memset(bdm[0:64, :, 0, :], 1.0)
            return e.memset(bdm[64:128, :, 1, :], 1.0)
        op("dve", mko, reads=[Rblk1, Rhsel, Rbdm], writes=[Rblk1, Rhsel, Rbdm])
        gne, Rgne = self.tile("gne", [128, 1], F32)
        op("pool", lambda e: e.memset(gne[:], GN_EPS), writes=[Rgne])
        mbase, Rmbase = self.tile("mbase", [128, 4, 128], F32)

        op("pool", lambda e: e.memset(mbase[:].rearrange("p a x -> p (a x)"), 1.0), writes=[Rmbase])
        op("pool", lambda e: e.affine_select(out=mbase[:, 0, :], in_=mbase[:, 0, :], pattern=[[1, 128]], compare_op=ALU.is_gt, fill=0.0, base=0, channel_multiplier=-1), reads=[Rmbase], writes=[Rmbase])
        op("pool", lambda e: e.affine_select(out=mbase[:, 1, :], in_=mbase[:, 1, :], pattern=[[1, 128]], compare_op=ALU.is_ge, fill=0.0, base=0, channel_multiplier=-1), reads=[Rmbase], writes=[Rmbase])
        op("pool", lambda e: e.affine_select(out=mbase[:, 2, :], in_=mbase[:, 2, :], pattern=[[-1, 128]], compare_op=ALU.is_gt, fill=0.0, base=0, channel_multiplier=1), reads=[Rmbase], writes=[Rmbase])
        op("pool", lambda e: e.affine_select(out=mbase[:, 3, :], in_=mbase[:, 3, :], pattern=[[-1, 128]], compare_op=ALU.is_ge, fill=0.0, base=0, channel_multiplier=1), reads=[Rmbase], writes=[Rmbase])
        mG, RmG = self.tile("mG", [128, 2, 2, 2, 128], BF16)
        mL, RmL = self.tile("mL", [128, 2, 2, 128], BF16)
        for d in range(2):
            si, ii, li = (0, 1, 2) if d == 0 else (2, 3, 0)
            for h2 in range(2):
                op("dve", lambda e, d=d, h2=h2, si=si: e.tensor_copy(out=mG[:, d, h2, 0, :], in_=mbase[:, si, :]), reads=[Rmbase], writes=[RmG])
                op("dve", lambda e, d=d, h2=h2, ii=ii: e.tensor_copy(out=mG[:, d, h2, 1, :], in_=mbase[:, ii, :]), reads=[Rmbase], writes=[RmG])
                op("dve", lambda e, d=d, h2=h2, li=li: e.tensor_copy(out=mL[:, d, h2, :], in_=mbase[:, li, :]), reads=[Rmbase], writes=[RmL])
        Eg, REg = self.tile("Eg", [8, 3, 128], F32)
        op("pool", lambda e: e.memset(Eg[:].rearrange("p a x -> p (a x)"), 1.0), writes=[REg])
        for gi, b_ in enumerate((16, 32, 64)):
            op("pool", lambda e, gi=gi, b_=b_: e.affine_select(out=Eg[:, gi, :], in_=Eg[:, gi, :], pattern=[[1, 128]], compare_op=ALU.is_ge, fill=0.0, base=0, channel_multiplier=-b_),
               reads=[REg], writes=[REg])
            op("pool", lambda e, gi=gi, b_=b_: e.affine_select(out=Eg[:, gi, :], in_=Eg[:, gi, :], pattern=[[-1, 128]], compare_op=ALU.is_ge, fill=0.0, base=b_ - 1, channel_multiplier=b_),
               reads=[REg], writes=[REg])
        bdf, Rbdf = self.tile("bdf", [128, 4, 128], F32)
        op("pool", lambda e: e.memset(bdf[:, 3, :], 1.0), writes=[Rbdf])
        pbm, Rpbm = self.psum("pbm", [128, 512], F32)

        def mmE(e):
            for gi in range(3):
                ins = e.matmul(pbm[:, gi * 128:(gi + 1) * 128], lhsT=Eg[:, gi, :], rhs=Eg[:, gi, :], start=True, stop=True)
            return ins
        op("pe", mmE, reads=[REg], writes=[Rpbm])
        op("dve", lambda e: e.tensor_copy(out=bdf[:, 0:3, :].rearrange("p a x -> p (a x)"), in_=pbm[:, 0:384]), reads=[Rpbm, Rbdf], writes=[Rbdf])
        bd16, Rbd16 = self.tile("bd16", [128, 2, 128], BF16)
        offm, Roffm = self.tile("offm", [128, 3, 4, 128], BF16)
        for h2 in range(4):
            if h2 < 2:
                op("dve", lambda e, h2=h2: e.tensor_copy(out=bd16[:, h2, :], in_=bdf[:, 0, :]), reads=[Rbdf], writes=[Rbd16])
            for lv in range(3):
                op("dve", lambda e, h2=h2, lv=lv: e.tensor_tensor(out=offm[:, lv, h2, :], in0=bdf[:, lv + 1, :], in1=bdf[:, lv, :], op=ALU.subtract),
                   reads=[Rbdf], writes=[Roffm])
        Ps = self.rot("P", [128, 15, 130], F32, 1)
        tAs = self.rot("tA", [128, 15, 128], F32, 1)
        tBs = self.rot("tB", [128, 15, 128], F32, 1)
        phs = self.rot("ph", [128, 15, 128], F32, 1)
        f4 = lambda n, k=1: self.rot(n, [128, 4, 128], F32, k)
        b4 = lambda n, k=1: self.rot(n, [128, 4, 128], BF16, k)
        lws, avs, pres, cums, cprs = f4("lw"), f4("av"), f4("pre"), f4("cum"), f4("cpr")
        ecs, ecps, eis = f4("ec"), f4("ecp"), f4("ei")
        kraws, sqs, nrms, kkvs, tts, kds, bvs = (f4(n) for n in ("kraw", "sq", "nrm", "kkv", "tt", "kd", "bv"))
        Kfs, Bfs, rks = kraws, nrms, sqs
        ATs, RTs, KTs, BTs = (b4(n, 2) for n in ("AT", "RT", "KT", "BT"))
        KhTs, BhTs, vTs = (b4(n, 1) for n in ("KhT", "BhT", "vT"))
        twds = self.rot("twd", [128, 128], BF16, 2)
        adbs = self.rot("adb", [128, 128], BF16, 2)
        sgds = self.rot("sgd", [128, 128], BF16, 2)
        gCs = self.rot("gC", [128, 4], F32, 2)
        ARbds = self.rot("ARbd", [128, 4, 2, 2, 128], BF16, 2)
        Bbds = self.rot("Bbd", [128, 4, 2, 128], BF16, 2)
        for (t_, r_) in ARbds.items:
            op("pool", lambda e, t_=t_: e.memset(t_[:].rearrange("p c h x t -> p (c h x t)"), 0.0), writes=[r_])
        for (t_, r_) in Bbds.items:
            op("pool", lambda e, t_=t_: e.memset(t_[:].rearrange("p c h t -> p (c h t)"), 0.0), writes=[r_])
        VKs = self.rot("VK", [128, 1024], BF16, 2)
        Bhats = self.rot("Bhat", [128, 512], BF16, 2)
        GKs = self.rot("GK", [128, 4, 2, 2, 128], BF16, 2)
        GBs = self.rot("GB", [128, 4, 2, 2, 128], BF16, 2)
        Lhs = self.rot("Lh", [128, 4, 2, 128], BF16, 1)
        lmq = [[self.tile(f"lmq{h}_{k}", [128, 4, 128], BF16) for k in range(2)] for h in range(8)]
        zzt = [self.tile(f"zz{h}", [128, 2, 2, 128], BF16) for h in range(4)]
        ttt = [[self.tile(f"tt{h}_{k}", [128, 2, 2, 128], BF16) for k in range(2)] for h in range(4)]
        tfin = [self.rot(f"tfin{h}", [128, 2, 2, 128], BF16, 2) for h in range(4)]
        L16s = self.rot("L16", [128, 4, 2, 128], BF16, 1)
        M16s = self.rot("M16", [128, 4, 2, 128], BF16, 1)
        St, RSt = self.tile("St", [128, 4, 2, 64], F32)
        Sbd, RSbd = self.tile("Sbd", [128, 4, 2, 64], BF16)
        stmp, Rstmp = self.tile("stmp", [128, 4, 2, 64], F32)
        Wsbs = self.rot("Wsb", [128, 512], BF16, 1)
        Usbs = self.rot("Usb", [128, 512], BF16, 1)
        Ysbs = self.rot("Ysb", [128, 512], F32, 2)
        bons = self.rot("bonv", [128, 8], F32, 2)
        bon0s = self.rot("bon0", [128, 8], F32, 2)
        yfls = self.rot("yfl", [128, 512], F32, 1)
        gtms = self.rot("gtm", [128, 512], F32, 2)
        sqvs = self.rot("sqv", [128, 512], F32, 1)
        ycs = self.rot("yc", [128, 512], F32, 1)
        vvs = sqvs
        stat = self.rot("stat", [128, 6, 8], F32, 2)
        yos = self.rot("yo", [128, 512], BF16, 1)
        pbanks = self.rot("pb", [128, 512], F32, 6, ps=True)
        ptr = self.rot("ptr", [128, 1024], BF16, 1, ps=True)
        pfv = [pf[s].rearrange("(c p) t -> p c t", p=128) for s in range(NS)]

        for s in range(NS):
            for d in range(2):
                op("pool", lambda e: e.memset(St[:].rearrange("p c h i -> p (c h i)"), 0.0), writes=[RSt])
                op("pool", lambda e: e.memset(Sbd[:].rearrange("p c h i -> p (c h i)"), 0.0), writes=[RSbd])
                order = range(NCH) if d == 0 else range(NCH - 1, -1, -1)
                def body(ci, s=s, d=d):
                    t0 = ci * C
                    ph, Rph = phs.next()
                    if d == 0:
                        P, RP = Ps.next()
                        lo, hi = max(t0 - 1, 0), min(t0 + C + 1, T)
                        if t0 == 0:
                            op("pool", lambda e, P=P: e.memset(P[:, :, 0:1], 0.0), writes=[RP])
                        if t0 + C == T:
                            op("pool", lambda e, P=P: e.memset(P[:, :, 129:130], 0.0), writes=[RP])
                        S.dma("sp", P[:, :, lo - (t0 - 1):hi - (t0 - 1)], pfv[s][:, :, lo:hi], reads=[self.dres(("pf", s))], writes=[RP])
                        tA, RtA = tAs.next()
                        tB, RtB = tBs.next()
                        for (c0_, c1_) in ((0, 5), (5, 10), (10, 15)):
                            cs_ = slice(c0_, c1_)
                            nn = c1_ - c0_
                            op("dve", lambda e, cs_=cs_: e.tensor_tensor(out=tA[:, cs_, :], in0=P[:, cs_, 0:128], in1=P[:, cs_, 1:129], op=ALU.subtract), reads=[RP], writes=[RtA])
                            op("pool", lambda e, cs_=cs_: e.tensor_tensor(out=tB[:, cs_, :], in0=P[:, cs_, 2:130], in1=P[:, cs_, 1:129], op=ALU.subtract), reads=[RP], writes=[RtB])
                            def scl(e, c0_=c0_, c1_=c1_):
                                for c_ in range(c0_, c1_):
                                    ins = e.activation(out=tA[:, c_, :], in_=tA[:, c_, :], func=AF.Copy, scale=mup[:, c_:c_ + 1])
                                return ins
                            op("act", scl, reads=[RtA, Rmup], writes=[RtA])
                            op("pool", lambda e, cs_=cs_, nn=nn: e.tensor_tensor(out=tB[:, cs_, :], in0=tB[:, cs_, :], in1=bc(mun[:, cs_].unsqueeze(2), [128, nn, 128]), op=ALU.mult), reads=[RtB, Rmun], writes=[RtB])
                            yield 'a'
                            op("dve", lambda e, cs_=cs_: e.tensor_tensor(out=tA[:, cs_, :], in0=tA[:, cs_, :], in1=tB[:, cs_, :], op=ALU.add), reads=[RtA, RtB], writes=[RtA])
                            op("pool", lambda e, cs_=cs_: e.tensor_tensor(out=ph[:, cs_, :], in0=tA[:, cs_, :], in1=P[:, cs_, 1:129], op=ALU.add), reads=[RtA, RP], writes=[Rph])
                            yield 'a'
                        S.dma("pool", phd[s, ci], ph[:].rearrange("p c t -> p (c t)"), reads=[Rph], writes=[self.dres(("phd", s))])
                    else:
                        S.dma("sp", ph[:].rearrange("p c t -> p (c t)"), phd[s, ci], reads=[self.dres(("phd", s))], writes=[Rph])
                    rh, kh, vh = ph[:, 0:4, :], ph[:, 4:8, :], ph[:, 8:12, :]
                    yield 'a'
                    twd, Rtwd = twds.next()
                    adb, Radb = adbs.next()
                    sgd, Rsgd = sgds.next()
                    op("act", lambda e, twd=twd, ph=ph: e.activation(out=twd[:], in_=ph[:, 12, :], func=AF.Tanh), reads=[Rph], writes=[Rtwd])
                    op("dve", lambda e, adb=adb, ph=ph: e.tensor_copy(out=adb[:], in_=ph[:, 13, :]), reads=[Rph], writes=[Radb])
                    psw, Rpsw = pbanks.next()
                    psa, Rpsa = pbanks.next()

                    def mmlora(e, psw=psw, psa=psa, twd=twd, adb=adb, d=d):
                        for cc in range(4):
                            e.matmul(psw[:, cc * 128:(cc + 1) * 128], lhsT=wupz[:, d, cc * 128:(cc + 1) * 128], rhs=twd[:], start=True, stop=True)
                        for cc in range(4):
                            ins = e.matmul(psa[:, cc * 128:(cc + 1) * 128], lhsT=aupz[:, d, cc * 128:(cc + 1) * 128], rhs=adb[:], start=True, stop=True)
                        return ins
                    op("pe", mmlora, reads=[Rwupz, Raupz, Rtwd, Radb], writes=[Rpsw, Rpsa])
                    yield 'a'
                    lw, Rlw = lws.next()
                    av, Rav = avs.next()

                    def sigw(e, lw=lw, psw=psw, d=d):
                        for cc in range(4):
                            ins = e.activation(out=lw[:, cc, :], in_=psw[:, cc * 128:(cc + 1) * 128], func=AF.Sigmoid, bias=w0t[:, d, cc:cc + 1])
                        return ins
                    op("act", sigw, reads=[Rpsw, Rw0], writes=[Rlw])

                    def siga(e, av=av, psa=psa, d=d):
                        for cc in range(4):
                            ins = e.activation(out=av[:, cc, :], in_=psa[:, cc * 128:(cc + 1) * 128], func=AF.Sigmoid, bias=a0t[:, d, cc:cc + 1])
                        return ins
                    op("act", siga, reads=[Rpsa, Ra0], writes=[Rav])
                    yield 'a'
                    yield 'a'
                    pre, Rpre = pres.next()
                    cum, Rcum = cums.next()
                    cpr, Rcpr = cprs.next()

                    def scan(e, pre=pre, lw=lw):
                        for cc in range(4):
                            ins = e.tensor_tensor_scan(out=pre[:, cc, :], data0=onesf[:], data1=lw[:, cc, :], initial=0.0, op0=ALU.mult, op1=ALU.add)
                        return ins
                    op("dve", scan, reads=[Rlw, Ronesf], writes=[Rpre])
                    yield 'a'
                    if d == 0:
                        cum, Rcum = pre, Rpre
                    else:
                        op("dve", lambda e, cum=cum, pre=pre, lw=lw: e.scalar_tensor_tensor(out=cum[:], in0=pre[:], scalar=-1.0, in1=lw[:], op0=ALU.mult, op1=ALU.add),
                           reads=[Rpre, Rlw], writes=[Rcum])
                        op("dve", lambda e, cum=cum, pre=pre: e.tensor_tensor(out=cum[:], in0=cum[:], in1=bc(pre[:, :, 127:128], [128, 4, 128]), op=ALU.add),
                           reads=[Rcum, Rpre], writes=[Rcum])
                    op("pool", lambda e, cpr=cpr, cum=cum, lw=lw: e.tensor_tensor(out=cpr[:], in0=cum[:], in1=lw[:], op=ALU.subtract), reads=[Rcum, Rlw], writes=[Rcpr])
                    ec, Rec = ecs.next()
                    ecp, Recp = ecps.next()
                    ei, Rei = eis.next()
                    op("act", lambda e, ec=ec, cum=cum: e.activation(out=ec[:], in_=cum[:], func=AF.Exp, scale=-LD), reads=[Rcum], writes=[Rec])
                    op("act", lambda e, ecp=ecp, cpr=cpr: e.activation(out=ecp[:], in_=cpr[:], func=AF.Exp, scale=-LD), reads=[Rcpr], writes=[Recp])
                    op("act", lambda e, ei=ei, cum=cum: e.activation(out=ei[:], in_=cum[:], func=AF.Exp, scale=LD), reads=[Rcum], writes=[Rei])
                    yield 'a'
                    gC, RgC = gCs.next()
                    ce = 127 if d == 0 else 0
                    op("dve", lambda e, gC=gC, ec=ec, ce=ce: e.tensor_copy(out=gC[:], in_=ec[:, :, ce]), reads=[Rec], writes=[RgC])
                    yield 'a'
                    kraw, Rkraw = kraws.next()
                    sq, Rsq = sqs.next()
                    nrm, Rnrm = nrms.next()
                    kkv, Rkkv = kkvs.next()
                    tt, Rtt = tts.next()
                    kd, Rkd = kds.next()
                    bv, Rbv = bvs.next()
                    op("dve", lambda e, kraw=kraw, ph=ph: e.tensor_tensor(out=kraw[:], in0=ph[:, 4:8, :], in1=bc(kkc[:].unsqueeze(2), [128, 4, 128]), op=ALU.mult), reads=[Rph, Rkkc], writes=[Rkraw])
                    op("pool", lambda e, sq=sq, kraw=kraw: e.tensor_tensor(out=sq[:], in0=kraw[:], in1=kraw[:], op=ALU.mult), reads=[Rkraw], writes=[Rsq])
                    psn, Rpsn = pbanks.next()
                    op("pe", lambda e, psn=psn, sq=sq: e.matmul(psn[:], lhsT=blk1[:], rhs=sq[:].rearrange("p c t -> p (c t)"), start=True, stop=True), reads=[Rblk1, Rsq], writes=[Rpsn])
                    yield 'a'
                    op("act", lambda e, nrm=nrm, psn=psn: e.activation(out=nrm[:].rearrange("p c t -> p (c t)"), in_=psn[:], func=AF.Ln), reads=[Rpsn], writes=[Rnrm])
                    op("act", lambda e, nrm=nrm: e.activation(out=nrm[:], in_=nrm[:], func=AF.Exp, scale=-0.5), reads=[Rnrm], writes=[Rnrm])
                    op("dve", lambda e, nrm=nrm: e.tensor_scalar_min(out=nrm[:], in0=nrm[:], scalar1=1e12), reads=[Rnrm], writes=[Rnrm])
                    op("dve", lambda e, kkv=kkv, kraw=kraw, nrm=nrm: e.tensor_tensor(out=kkv[:], in0=kraw[:], in1=nrm[:], op=ALU.mult), reads=[Rkraw, Rnrm], writes=[Rkkv])
                    yield 'a'
                    op("pool", lambda e, tt=tt, av=av: e.tensor_tensor(out=tt[:], in0=av[:], in1=bc(kac[:].unsqueeze(2), [128, 4, 128]), op=ALU.mult), reads=[Rav, Rkac], writes=[Rtt])
                    op("pool", lambda e, tt=tt: e.tensor_tensor(out=tt[:], in0=tt[:], in1=bc(omka[:].unsqueeze(2), [128, 4, 128]), op=ALU.add), reads=[Rtt, Romka], writes=[Rtt])
                    op("dve", lambda e, kd=kd, ph=ph, tt=tt: e.tensor_tensor(out=kd[:], in0=ph[:, 4:8, :], in1=tt[:], op=ALU.mult), reads=[Rph, Rtt], writes=[Rkd])
                    yield 'a'
                    op("pool", lambda e, bv=bv, kkv=kkv, av=av: e.tensor_tensor(out=bv[:], in0=kkv[:], in1=av[:], op=ALU.mult), reads=[Rkkv, Rav], writes=[Rbv])
                    yield 'a'
                    AT, RAT = ATs.next()
                    RT, RRT = RTs.next()
                    KT, RKT = KTs.next()
                    BT, RBT = BTs.next()
                    KhT, RKhT = KhTs.next()
                    BhT, RBhT = BhTs.next()
                    vT, RvT = vTs.next()
                    Kf, RKf = Kfs.next()
                    Bf, RBf = Bfs.next()
                    rk, Rrk = rks.next()
                    op("dve", lambda e, AT=AT, kkv=kkv, ecp=ecp: e.scalar_tensor_tensor(out=AT[:], in0=kkv[:], scalar=-1.0, in1=ecp[:], op0=ALU.mult, op1=ALU.mult), reads=[Rkkv, Recp], writes=[RAT])
                    op("dve", lambda e, RT=RT, ph=ph, ec=ec: e.tensor_tensor(out=RT[:], in0=ph[:, 0:4, :], in1=ec[:], op=ALU.mult), reads=[Rph, Rec], writes=[RRT])
                    op("dve", lambda e, Kf=Kf, kd=kd, ei=ei: e.tensor_tensor(out=Kf[:], in0=kd[:], in1=ei[:], op=ALU.mult), reads=[Rkd, Rei], writes=[RKf])
                    yield 'a'
                    op("pool", lambda e, Bf=Bf, bv=bv, ei=ei: e.tensor_tensor(out=Bf[:], in0=bv[:], in1=ei[:], op=ALU.mult), reads=[Rbv, Rei], writes=[RBf])
                    op("act", lambda e, KT=KT, Kf=Kf: e.copy(out=KT[:], in_=Kf[:]), reads=[RKf], writes=[RKT])
                    op("act", lambda e, BT=BT, Bf=Bf: e.copy(out=BT[:], in_=Bf[:]), reads=[RBf], writes=[RBT])
                    yield 'a'
                    def khs(e, KhT=KhT, Kf=Kf, gC=gC):
                        for c_ in range(4):
                            ins = e.activation(out=KhT[:, c_, :], in_=Kf[:, c_, :], func=AF.Copy, scale=gC[:, c_:c_ + 1])
                        return ins
                    op("act", khs, reads=[RKf, RgC], writes=[RKhT])
                    op("pool", lambda e, BhT=BhT, Bf=Bf, gC=gC: e.tensor_tensor(out=BhT[:], in0=Bf[:], in1=bc(gC[:].unsqueeze(2), [128, 4, 128]), op=ALU.mult), reads=[RBf, RgC], writes=[RBhT])
                    op("act", lambda e, vT=vT, ph=ph: e.copy(out=vT[:], in_=ph[:, 8:12, :]), reads=[Rph], writes=[RvT])
                    op("pool", lambda e, rk=rk, ph=ph, kd=kd: e.tensor_tensor(out=rk[:], in0=ph[:, 0:4, :], in1=kd[:], op=ALU.mult), reads=[Rph, Rkd], writes=[Rrk])
                    op("pool", lambda e, rk=rk: e.tensor_tensor(out=rk[:], in0=rk[:], in1=bc(rkc[:].unsqueeze(2), [128, 4, 128]), op=ALU.mult), reads=[Rrk, Rrkc], writes=[Rrk])
                    yield 'a'
                    ARbd, RARbd = ARbds.next()
                    Bbd, RBbd = Bbds.next()
                    for h2 in range(2):
                        pl = slice(h2 * 64, (h2 + 1) * 64)
                        op("act", lambda e, pl=pl, h2=h2, AT=AT: e.copy(out=ARbd[pl, :, h2, 0, :], in_=AT[pl, :, :]), reads=[RAT, RARbd], writes=[RARbd])
                        op("act", lambda e, pl=pl, h2=h2, RT=RT: e.copy(out=ARbd[pl, :, h2, 1, :], in_=RT[pl, :, :]), reads=[RRT, RARbd], writes=[RARbd])
                        op("act", lambda e, pl=pl, h2=h2, BT=BT: e.copy(out=Bbd[pl, :, h2, :], in_=BT[pl, :, :]), reads=[RBT, RBbd], writes=[RBbd])
                    yield 'a'
                    pt, Rpt = ptr.next()
                    VK, RVK = VKs.next()
                    Vtm, RVtm = VK[:, 0:512], RVK
                    Khat, RKhat = VK[:, 512:1024], RVK
                    Bhat, RBhat = Bhats.next()

                    def tr1(e, pt=pt, vT=vT, KhT=KhT):
                        for cc in range(4):
                            e.transpose(out=pt[:, cc * 128:(cc + 1) * 128], in_=vT[:, cc, :], identity=identb[:])
                        for cc in range(4):
                            ins = e.transpose(out=pt[:, 512 + cc * 128:512 + (cc + 1) * 128], in_=KhT[:, cc, :], identity=identb[:])
                        return ins
                    op("pe", tr1, reads=[RvT, RKhT, Ridb], writes=[Rpt])
                    yield 'a'
                    op("act", lambda e, VK=VK, pt=pt: e.copy(out=VK[:], in_=pt[:]), reads=[Rpt], writes=[RVK])
                    pt2, Rpt2 = ptr.next()

                    def tr2(e, pt2=pt2, BhT=BhT):
                        for cc in range(4):
                            ins = e.transpose(out=pt2[:, cc * 128:(cc + 1) * 128], in_=BhT[:, cc, :], identity=identb[:])
                        return ins
                    op("pe", tr2, reads=[RBhT, Ridb], writes=[Rpt2])
                    yield 'a'
                    op("act", lambda e, Bhat=Bhat, pt2=pt2: e.copy(out=Bhat[:], in_=pt2[:, 0:512]), reads=[Rpt2], writes=[RBhat])
                    psb, Rpsb = pbanks.next()
                    bonv, Rbonv = bons.next()

                    def mmbon(e, psb=psb, rk=rk):
                        for cc in range(4):
                            ins = e.transpose(out=psb[:, cc * 128:(cc + 1) * 128], in_=rk[:, cc, :], identity=identf[:])
                        return ins
                    op("pe", mmbon, reads=[Rrk, Ridf], writes=[Rpsb])
                    op("dve", lambda e, bonv=bonv, psb=psb: e.tensor_reduce(out=bonv[:], in_=psb[:].rearrange("p (h i) -> p h i", i=64), axis=AX.X, op=ALU.add),
                       reads=[Rpsb], writes=[Rbonv])
                    if d == 1:
                        op("act", lambda e, sgd=sgd, ph=ph: e.activation(out=sgd[:], in_=ph[:, 14, :], func=AF.Sigmoid), reads=[Rph], writes=[Rsgd])
                        psg, Rpsg = pbanks.next()
                        gtm, Rgtm = gtms.next()
                        op("pe", lambda e, psg=psg, sgd=sgd: e.matmul(psg[:], lhsT=sgd[:], rhs=gup[:], start=True, stop=True), reads=[Rsgd, Rgup], writes=[Rpsg])
                        op("act", lambda e, gtm=gtm, psg=psg: e.copy(out=gtm[:], in_=psg[:]), reads=[Rpsg], writes=[Rgtm])
                    yield 'B'
                    GK, RGK = GKs.next()
                    GB, RGB = GBs.next()
                    Lh, RLh = Lhs.next()
                    for cc in range(4):
                        p1, Rp1 = pbanks.next()
                        p2, Rp2 = pbanks.next()
                        p3, Rp3 = pbanks.next()

                        def mmg(e, cc=cc, p1=p1, p2=p2, p3=p3, KT=KT, BT=BT, AT=AT):
                            e.matmul(p1[:], lhsT=KT[:, cc, :], rhs=ARbd[:, cc].rearrange("p h x t -> p (h x t)"), start=True, stop=True)
                            e.matmul(p2[:], lhsT=BT[:, cc, :], rhs=ARbd[:, cc].rearrange("p h x t -> p (h x t)"), start=True, stop=True)
                            return e.matmul(p3[:, 0:256], lhsT=AT[:, cc, :], rhs=Bbd[:, cc].rearrange("p h t -> p (h t)"), start=True, stop=True)
                        op("pe", mmg, reads=[RKT, RBT, RAT, RARbd, RBbd], writes=[Rp1, Rp2, Rp3])
                        op("dve", lambda e, cc=cc, p1=p1, GK=GK, d=d: e.tensor_tensor(out=GK[:, cc].rearrange("p h x t -> p (h x t)"), in0=p1[:],
                                                                                 in1=mG[:, d].rearrange("p h x t -> p (h x t)"), op=ALU.mult), reads=[Rp1, RmG], writes=[RGK])
                        op("dve", lambda e, cc=cc, p2=p2, GB=GB, d=d: e.tensor_tensor(out=GB[:, cc].rearrange("p h x t -> p (h x t)"), in0=p2[:],
                                                                                 in1=mG[:, d].rearrange("p h x t -> p (h x t)"), op=ALU.mult), reads=[Rp2, RmG], writes=[RGB])
                        op("dve", lambda e, cc=cc, p3=p3, Lh=Lh, d=d: e.tensor_tensor(out=Lh[:, cc].rearrange("p h t -> p (h t)"), in0=p3[:, 0:256],
                                                                                 in1=mL[:, d].rearrange("p h t -> p (h t)"), op=ALU.mult), reads=[Rp3, RmL], writes=[RLh])
                        yield 'b'
                    L16, RL16 = L16s.next()
                    M16, RM16 = M16s.next()
                    for cc in range(4):
                        op("pool", lambda e, cc=cc, L16=L16, Lh=Lh: e.tensor_tensor(out=L16[:, cc].rearrange("p h t -> p (h t)"), in0=Lh[:, cc].rearrange("p h t -> p (h t)"),
                                                                              in1=bd16[:].rearrange("p h t -> p (h t)"), op=ALU.mult), reads=[RLh, Rbd16], writes=[RL16])
                        op("pool", lambda e, cc=cc, M16=M16, GB=GB: e.tensor_tensor(out=M16[:, cc], in0=GB[:, cc, :, 0, :], in1=bd16[:], op=ALU.mult), reads=[RGB, Rbd16], writes=[RM16])
                    cur = []
                    for h in range(8):
                        cur.append((L16[:, h // 2, h % 2, :], RL16, M16[:, h // 2, h % 2, :], RM16, identb[:], Ridb, identb[:], Ridb))
                    for lev in range(4):
                        for h in range(8):
                            Lp, RLp, Mp, RMp, Qp, RQp, Pp, RPp = cur[h]
                            pq, Rpq = pbanks.next()
                            lt, Rlt = lmq[h][lev % 2]

                            def mmi(e, pq=pq, Lp=Lp, Mp=Mp, Qp=Qp, Pp=Pp, lev=lev):
                                e.matmul(pq[:, 256:384], lhsT=identb[:], rhs=Qp, start=True, stop=False)
                                e.matmul(pq[:, 256:384], lhsT=Lp, rhs=Qp, start=False, stop=True)
                                e.matmul(pq[:, 384:512], lhsT=identb[:], rhs=Pp, start=True, stop=False)
                                ins = e.matmul(pq[:, 384:512], lhsT=Mp, rhs=Pp, start=False, stop=True)
                                if lev < 3:
                                    e.matmul(pq[:, 0:128], lhsT=Mp, rhs=Lp, start=True, stop=True)
                                    ins = e.matmul(pq[:, 128:256], lhsT=Lp, rhs=Mp, start=True, stop=True)
                                return ins
                            op("pe", mmi, reads=[RLp, RMp, RQp, RPp, Ridb], writes=[Rpq])
                            lo_ = 0 if lev < 3 else 256
                            eng = "act"
                            if eng == "dve":
                                op("dve", lambda e, lt=lt, pq=pq, lo_=lo_: e.tensor_copy(out=lt[:].rearrange("p a t -> p (a t)")[:, lo_:512], in_=pq[:, lo_:512]), reads=[Rpq], writes=[Rlt])
                            else:
                                op("act", lambda e, lt=lt, pq=pq, lo_=lo_: e.copy(out=lt[:].rearrange("p a t -> p (a t)")[:, lo_:512], in_=pq[:, lo_:512]), reads=[Rpq], writes=[Rlt])
                            cur[h] = (lt[:, 0, :], Rlt, lt[:, 1, :], Rlt, lt[:, 2, :], Rlt, lt[:, 3, :], Rlt)
                            if h % 2 == 1:
                                yield 'b'
                    tcur = [(cur[h][6], cur[h][7], cur[h][4], cur[h][5]) for h in range(8)]
                    for lv in range(3):
                        last_lv = (lv == 2)
                        pas = []
                        for pr in range(4):
                            pa, Rpa = pbanks.next()
                            pas.append((pa, Rpa))

                            def mmz(e, pa=pa, pr=pr, tc=list(tcur), last_lv=last_lv, Lh=Lh, GB=GB):
                                for h2 in range(2):
                                    h = 2 * pr + h2
                                    Tk, RTk, TTk, RTTk = tc[h]
                                    ins = e.matmul(pa[:, h2 * 256 + 128:h2 * 256 + 256], lhsT=Lh[:, pr, h2, :], rhs=TTk, start=True, stop=True)
                                    if not last_lv:
                                        ins = e.matmul(pa[:, h2 * 256:h2 * 256 + 128], lhsT=GB[:, pr, h2, 0, :], rhs=Tk, start=True, stop=True)
                                return ins
                            op("pe", mmz, reads=[RLh, RGB, tcur[2 * pr][1], tcur[2 * pr][3], tcur[2 * pr + 1][1], tcur[2 * pr + 1][3]], writes=[Rpa])
                        for pr in range(4):
                            pa, Rpa = pas[pr]
                            zz, Rzz = zzt[pr]
                            if not last_lv:
                                op("dve", lambda e, zz=zz, pa=pa, lv=lv: e.tensor_tensor(out=zz[:].rearrange("p h a t -> p (h a t)"), in0=pa[:],
                                                                                        in1=offm[:, lv].rearrange("p a t -> p (a t)"), op=ALU.mult),
                                   reads=[Rpa, Roffm], writes=[Rzz])
                            else:
                                op("dve", lambda e, zz=zz, pa=pa, lv=lv: e.tensor_tensor(out=zz[:, :, 1, :], in0=pa[:].rearrange("p (h a t) -> p h a t", h=2, a=2)[:, :, 1, :],
                                                                                        in1=offm[:, lv, 0:2, :], op=ALU.mult),
                                   reads=[Rpa, Roffm], writes=[Rzz])
                        yield 'b'
                        pbs = []
                        for pr in range(4):
                            pb_, Rpb_ = pbanks.next()
                            pbs.append((pb_, Rpb_))
                            zz, Rzz = zzt[pr]

                            def mmt(e, pb_=pb_, pr=pr, tc=list(tcur), zz=zz, last_lv=last_lv):
                                for h2 in range(2):
                                    h = 2 * pr + h2
                                    Tk, RTk, TTk, RTTk = tc[h]
                                    c0 = h2 * 256
                                    e.matmul(pb_[:, c0 + 128:c0 + 256], lhsT=identb[:], rhs=TTk, start=True, stop=False)
                                    ins = e.matmul(pb_[:, c0 + 128:c0 + 256], lhsT=Tk, rhs=zz[:, h2, 1, :], start=False, stop=True)
                                    if not last_lv:
                                        e.matmul(pb_[:, c0:c0 + 128], lhsT=identb[:], rhs=Tk, start=True, stop=False)
                                        ins = e.matmul(pb_[:, c0:c0 + 128], lhsT=TTk, rhs=zz[:, h2, 0, :], start=False, stop=True)
                                return ins
                            op("pe", mmt, reads=[Rzz, Ridb, tcur[2 * pr][1], tcur[2 * pr][3], tcur[2 * pr + 1][1], tcur[2 * pr + 1][3]], writes=[Rpb_])
                        for pr in range(4):
                            pb_, Rpb_ = pbs[pr]
                            tn, Rtn = ttt[pr][lv % 2] if not last_lv else tfin[pr].next()
                            eng = "act"
                            if not last_lv:
                                src, dst = pb_[:], tn[:].rearrange("p h a t -> p (h a t)")
                            else:
                                src, dst = pb_[:].rearrange("p (h a t) -> p h a t", h=2, a=2)[:, :, 1, :], tn[:, :, 1, :]
                            if eng == "act":
                                op("act", lambda e, src=src, dst=dst: e.copy(out=dst, in_=src), reads=[Rpb_], writes=[Rtn])
                            else:
                                op("dve", lambda e, src=src, dst=dst: e.tensor_copy(out=dst, in_=src), reads=[Rpb_], writes=[Rtn])
                            for h2 in range(2):
                                tcur[2 * pr + h2] = (tn[:, h2, 0, :], Rtn, tn[:, h2, 1, :], Rtn)
                        yield 'b'
                    cur = [(None, None, None, None, tcur[h][2], tcur[h][3]) for h in range(8)]
                    yield 'C'
                    pW, RpW = pbanks.next()
                    Wsb, RWsb = Wsbs.next()
                    Usb, RUsb = Usbs.next()
                    Ysb, RYsb = Ysbs.next()

                    def mmW(e, pW=pW, AT=AT, GK=GK, Vtm=Vtm):
                        for cc in range(4):
                            e.matmul(pW[:, cc * 128:(cc + 1) * 128], lhsT=AT[:, cc, :], rhs=Sbd[:, cc].rearrange("p h i -> p (h i)"), start=True, stop=False)
                            for h2 in range(2):
                                h = 2 * cc + h2
                                ins = e.matmul(pW[:, h * 64:(h + 1) * 64], lhsT=GK[:, cc, h2, 0, :], rhs=Vtm[:, h * 64:(h + 1) * 64], start=False, stop=True)
                        return ins
                    op("pe", mmW, reads=[RAT, RSbd, RGK, RVtm], writes=[RpW])
                    op("dve", lambda e, Wsb=Wsb, pW=pW: e.tensor_copy(out=Wsb[:], in_=pW[:]), reads=[RpW], writes=[RWsb])
                    yield 'c'
                    pU, RpU = pbanks.next()

                    def mmU(e, pU=pU, Wsb=Wsb, cur=list(cur)):
                        for h in range(8):
                            ins = e.matmul(pU[:, h * 64:(h + 1) * 64], lhsT=cur[h][4], rhs=Wsb[:, h * 64:(h + 1) * 64], start=True, stop=True)
                        return ins
                    op("pe", mmU, reads=[RWsb] + [cur[h][5] for h in range(8)], writes=[RpU])
                    op("dve", lambda e, Usb=Usb, pU=pU: e.tensor_copy(out=Usb[:], in_=pU[:]), reads=[RpU], writes=[RUsb])
                    yield 'c'
                    pY, RpY = pbanks.next()

                    def mmY(e, pY=pY, RT=RT, GK=GK, GB=GB, Vtm=Vtm, Usb=Usb):
                        for cc in range(4):
                            e.matmul(pY[:, cc * 128:(cc + 1) * 128], lhsT=RT[:, cc, :], rhs=Sbd[:, cc].rearrange("p h i -> p (h i)"), start=True, stop=False)
                            for h2 in range(2):
                                h = 2 * cc + h2
                                e.matmul(pY[:, h * 64:(h + 1) * 64], lhsT=GK[:, cc, h2, 1, :], rhs=Vtm[:, h * 64:(h + 1) * 64], start=False, stop=False)
                                ins = e.matmul(pY[:, h * 64:(h + 1) * 64], lhsT=GB[:, cc, h2, 1, :], rhs=Usb[:, h * 64:(h + 1) * 64], start=False, stop=True)
                        return ins
                    op("pe", mmY, reads=[RRT, RSbd, RGK, RGB, RVtm, RUsb], writes=[RpY])
                    op("act", lambda e, Ysb=Ysb, pY=pY: e.copy(out=Ysb[:], in_=pY[:]), reads=[RpY], writes=[RYsb])
                    yield 'c'
                    pS, RpS = pbanks.next()

                    def mmS(e, pS=pS, Khat=Khat, Bhat=Bhat, Vtm=Vtm, Usb=Usb):
                        for cc in range(4):
                            cs = slice(cc * 128, (cc + 1) * 128)
                            e.matmul(pS[:, cs], lhsT=Khat[:, cs], rhs=Vtm[:, cs], start=True, stop=False)
                            ins = e.matmul(pS[:, cs], lhsT=Bhat[:, cs], rhs=Usb[:, cs], start=False, stop=True)
                        return ins
                    op("pe", mmS, reads=[RKhat, RBhat, RVtm, RUsb], writes=[RpS])
                    Sf = St[:].rearrange("p c h i -> p c (h i)")
                    op("dve", lambda e, pS=pS: e.tensor_tensor(out=stmp[:].rearrange("p c h i -> p (c h i)"), in0=pS[:], in1=bdm[:].rearrange("p c h i -> p (c h i)"), op=ALU.mult),
                       reads=[RpS, Rbdm], writes=[Rstmp])
                    op("dve", lambda e, gC=gC: e.tensor_tensor(out=Sf, in0=Sf, in1=bc(gC[:].unsqueeze(2), [128, 4, 128]), op=ALU.mult), reads=[RSt, RgC], writes=[RSt])
                    op("dve", lambda e: e.tensor_tensor(out=St[:].rearrange("p c h i -> p (c h i)"), in0=St[:].rearrange("p c h i -> p (c h i)"),
                                                        in1=stmp[:].rearrange("p c h i -> p (c h i)"), op=ALU.add), reads=[RSt, Rstmp], writes=[RSt])
                    op("act", lambda e: e.copy(out=Sbd[:].rearrange("p c h i -> p (c h i)"), in_=St[:].rearrange("p c h i -> p (c h i)")), reads=[RSt], writes=[RSbd])
                    yield 'c'
                    if d == 0:
                        S.dma("pool", yf[s, t0:t0 + C, :], Ysb[:], reads=[RYsb], writes=[self.dres(("yf", s))])
                        S.dma("pool", bon[s, t0:t0 + C, :], bonv[:], reads=[Rbonv], writes=[self.dres(("bon", s))])
                    else:
                        yfl, Ryfl = yfls.next()
                        bon0, Rbon0 = bon0s.next()
                        S.dma("sp", yfl[:], yf[s, t0:t0 + C, :], reads=[self.dres(("yf", s))], writes=[Ryfl])
                        S.dma("sp", bon0[:], bon[s, t0:t0 + C, :], reads=[self.dres(("bon", s))], writes=[Rbon0])
                        sqv, Rsqv = sqvs.next()
                        yc, Ryc = ycs.next()
                        vv, Rvv = vvs.next()
                        st, Rst = stat.next()
                        yo, Ryo = yos.next()
                        y3 = lambda t_: t_[:].rearrange("p (h i) -> p h i", i=64)
                        b3 = lambda a_: bc(a_.unsqueeze(2), [128, 8, 64])
                        op("dve", lambda e, Ysb=Ysb, yfl=yfl: e.tensor_tensor(out=Ysb[:], in0=Ysb[:], in1=yfl[:], op=ALU.add), reads=[RYsb, Ryfl], writes=[RYsb])
                        op("pool", lambda e, sqv=sqv, Ysb=Ysb: e.tensor_tensor(out=sqv[:], in0=Ysb[:], in1=Ysb[:], op=ALU.mult), reads=[RYsb], writes=[Rsqv])
                        op("dve", lambda e, st=st, Ysb=Ysb: e.tensor_reduce(out=st[:, 0, :], in_=y3(Ysb), axis=AX.X, op=ALU.add), reads=[RYsb], writes=[Rst])
                        op("dve", lambda e, st=st, sqv=sqv: e.tensor_reduce(out=st[:, 1, :], in_=y3(sqv), axis=AX.X, op=ALU.add), reads=[Rsqv, Rst], writes=[Rst])
                        op("dve", lambda e, st=st: e.tensor_scalar(out=st[:, 2, :], in0=st[:, 0, :], scalar1=1.0 / 64, scalar2=None, op0=ALU.mult), reads=[Rst], writes=[Rst])
                        op("dve", lambda e, st=st: e.tensor_tensor(out=st[:, 3, :], in0=st[:, 2, :], in1=st[:, 2, :], op=ALU.mult), reads=[Rst], writes=[Rst])
                        op("dve", lambda e, st=st: e.scalar_tensor_tensor(out=st[:, 4, :], in0=st[:, 1, :], scalar=1.0 / 64, in1=st[:, 3, :], op0=ALU.mult, op1=ALU.subtract),
                           reads=[Rst], writes=[Rst])
                        op("act", lambda e, st=st: e.activation(out=st[:, 5, :], in_=st[:, 4, :], func=AF.Sqrt, bias=gne[:, 0:1]), reads=[Rst, Rgne], writes=[Rst])
                        op("dve", lambda e, st=st: e.reciprocal(out=st[:, 5, :], in_=st[:, 5, :]), reads=[Rst], writes=[Rst])
                        yield 'c'
                        op("dve", lambda e, yc=yc, Ysb=Ysb, st=st: e.tensor_tensor(out=y3(yc), in0=y3(Ysb), in1=b3(st[:, 2, :]), op=ALU.subtract), reads=[RYsb, Rst], writes=[Ryc])
                        op("dve", lambda e, yc=yc, st=st: e.tensor_tensor(out=y3(yc), in0=y3(yc), in1=b3(st[:, 5, :]), op=ALU.mult), reads=[Ryc, Rst], writes=[Ryc])
                        op("pool", lambda e, yc=yc: e.tensor_tensor(out=yc[:], in0=yc[:], in1=gng[:], op=ALU.mult), reads=[Ryc, Rgng], writes=[Ryc])
                        op("pool", lambda e, yc=yc: e.tensor_tensor(out=yc[:], in0=yc[:], in1=gnb[:], op=ALU.add), reads=[Ryc, Rgnb], writes=[Ryc])
                        yield 'c'
                        op("dve", lambda e, bonv=bonv, bon0=bon0: e.tensor_tensor(out=bonv[:], in0=bonv[:], in1=bon0[:], op=ALU.add), reads=[Rbonv, Rbon0], writes=[Rbonv])
                        op("dve", lambda e, vv=vv, Vtm=Vtm, bonv=bonv: e.tensor_tensor(out=y3(vv), in0=Vtm.rearrange("p (h i) -> p h i", i=64), in1=b3(bonv[:]), op=ALU.mult), reads=[RVtm, Rbonv], writes=[Rvv])
                        op("pool", lambda e, yc=yc, vv=vv: e.tensor_tensor(out=yc[:], in0=yc[:], in1=vv[:], op=ALU.add), reads=[Ryc, Rvv], writes=[Ryc])
                        op("dve", lambda e, yo=yo, yc=yc, gtm=gtm: e.tensor_tensor(out=yo[:], in0=yc[:], in1=gtm[:], op=ALU.mult), reads=[Ryc, Rgtm], writes=[Ryo])
                        S.dma("pool", yr[s, t0:t0 + C, :], yo[:], reads=[Ryo], writes=[self.dres(("yr", s))])

                def run_until(g, tags):
                    while True:
                        try:
                            t_ = next(g)
                        except StopIteration:
                            return None
                        if t_ in tags:
                            return t_
                order = list(order)
                gens = [body(ci) for ci in order]
                run_until(gens[0], ('B',))
                prev = None
                for i_ in range(len(gens)):
                    g_cur = gens[i_]
                    g_nxt = gens[i_ + 1] if i_ + 1 < len(gens) else None
                    cur_done = False
                    if prev is not None:
                        prev_done = False
                        while not prev_done:
                            if run_until(prev, ('c',)) is None:
                                prev_done = True
                            if not cur_done and run_until(g_cur, ('b', 'C')) == 'C':
                                cur_done = True
                    nxt_done = g_nxt is None
                    while not (cur_done and nxt_done):
                        if not cur_done and run_until(g_cur, ('b', 'C')) == 'C':
                            cur_done = True
                        if not nxt_done and run_until(g_nxt, ('a', 'B')) == 'B':
                            nxt_done = True
                    prev = g_cur
                run_until(prev, ())
        self.end()

    def phase_p3(self, l):
        S, T, NS = self.S, self.T, self.NS
        self.begin()
        rows = T // 64
        nblk = T // 128
        nq, nk, nv, yn = (self.scr[k] for k in ("nq", "nk", "nv", "yn"))
        rp, Rrp = self.tile("rp", [120, 31], F32)
        S.dma("sp", rp[:], self.w["rpb"][l].rearrange("h a b -> (h a) b"), writes=[Rrp])
        _, _, idf, Ridf = self.make_ident()
        Rt, RRt = self.tile("Rt", [31, 8, 15], F32)
        Jm, RJ = self.tile("Jm", [31, 160], F32)
        tps = self.rot("tps", [128, 512], F32, 4, ps=True)
        tp0, Rtp0 = tps.next()
        S.op("pe", lambda e: e.transpose(out=tp0[0:31, 0:120], in_=rp[:, :], identity=idf[0:120, 0:120]), reads=[Rrp, Ridf], writes=[Rtp0])
        S.op("dve", lambda e: e.tensor_copy(out=Rt[:].rearrange("p h a -> p (h a)"), in_=tp0[0:31, 0:120]), reads=[Rtp0], writes=[RRt])

        S.op("pool", lambda e: e.memset(Jm[:], 0.0), writes=[RJ])
        S.op("pool", lambda e: e.affine_select(out=Jm[:], in_=Jm[:], pattern=[[-1, 160]], compare_op=ALU.not_equal, fill=1.0, base=48, channel_multiplier=1),
             reads=[RJ], writes=[RJ])
        TE0, RTE0 = self.tile("TE0", [64, 8, 15, 64], BF16)
        for q0 in range(0, 64, 4):
            tp, Rtp = tps.next()

            def mmT(e, tp=tp, q0=q0):
                for qi in range(4):
                    qc = q0 + qi
                    ins = e.matmul(tp[0:64, qi * 120:(qi + 1) * 120], lhsT=Jm[:, 63 - qc:127 - qc], rhs=Rt[:].rearrange("p h a -> p (h a)"),
                                   start=True, stop=True)
                return ins
            S.op("pe", mmT, reads=[RJ, RRt], writes=[Rtp])
            S.op("act", lambda e, tp=tp, q0=q0: e.activation(out=TE0[:, :, :, q0:q0 + 4].rearrange("p h a q -> p (h a) q"),
                                                             in_=tp[0:64, 0:480].rearrange("p (q x) -> p x q", q=4), func=AF.Exp),
                 reads=[Rtp], writes=[RTE0])
        A, RA = self.tile("mA", [128, 64], F32)
        Q, RQ = self.tile("mQ", [128, 64], F32)
        Q2, RQ2 = self.tile("mQ2", [128, 64], F32)
        cm, Rcm = self.tile("cm", [128, 64], F32)

        def io(e):
            e.iota(A[0:64, :], pattern=[[-1, 64]], base=0, channel_multiplier=1, allow_small_or_imprecise_dtypes=True)
            e.iota(A[64:128, :], pattern=[[-1, 64]], base=0, channel_multiplier=1, allow_small_or_imprecise_dtypes=True)
            return e.iota(Q[:], pattern=[[1, 64]], base=0, channel_multiplier=0, allow_small_or_imprecise_dtypes=True)
        S.op("pool", io, writes=[RA, RQ])
        S.op("dve", lambda e: e.tensor_scalar(out=Q2[:], in0=Q[:], scalar1=8.0, scalar2=56.0, op0=ALU.max, op1=ALU.min), reads=[RQ], writes=[RQ2])
        S.op("dve", lambda e: e.tensor_tensor(out=Q[:], in0=Q[:], in1=Q2[:], op=ALU.subtract), reads=[RQ, RQ2], writes=[RQ])
        S.op("dve", lambda e: e.tensor_tensor(out=A[:], in0=A[:], in1=Q[:], op=ALU.add), reads=[RA, RQ], writes=[RA])
        S.op("dve", lambda e: e.tensor_single_scalar(out=Q[:], in_=A[:], scalar=-8.0, op=ALU.is_ge), reads=[RA], writes=[RQ])
        S.op("dve", lambda e: e.tensor_single_scalar(out=Q2[:], in_=A[:], scalar=7.0, op=ALU.is_le), reads=[RA], writes=[RQ2])
        S.op("dve", lambda e: e.tensor_tensor(out=cm[:], in0=Q[:], in1=Q2[:], op=ALU.mult), reads=[RQ, RQ2], writes=[Rcm])
        TE, RTE = self.tile("TE", [128, 8, 14, 64], BF16)
        S.op("dve", lambda e: e.tensor_tensor(out=TE0[:].rearrange("p h a q -> p (h a) q"), in0=TE0[:].rearrange("p h a q -> p (h a) q"),
                                              in1=cm[0:64, :].unsqueeze(1).broadcast_to([64, 120, 64]), op=ALU.mult),
             reads=[RTE0, Rcm], writes=[RTE0])
        S.op("pool", lambda e: e.tensor_copy(out=TE[0:64, :, :, :].rearrange("p h a q -> p h (a q)"),
                                             in_=TE0[:, :, 0:14, :].rearrange("p h a q -> p h (a q)")), reads=[RTE0], writes=[RTE])
        S.dma("sp", TE[64:128, :, :, :].rearrange("p h a q -> p h (a q)"), TE0[:, :, 1:15, :].rearrange("p h a q -> p h (a q)"),
              reads=[RTE0], writes=[RTE])
        qT, RqT = self.tile("nqT", [128, 4, T], BF16)
        kT, RkT = self.tile("nkT", [128, 4, T], BF16)
        Ve, RVe = self.tile("nVe", [128, nblk, 8, 65], BF16)
        Vo, RVo = self.tile("nVo", [128, nblk, 8, 65], BF16)
        Vs, RVs = self.tile("nVs", [128, nblk // 2, 512], BF16)
        S.op("pool", lambda e: e.memset(Ve[:], 1.0), writes=[RVe])
        S.op("pool", lambda e: e.memset(Vo[:], 1.0), writes=[RVo])
        pss = Rot([(t_[:].rearrange("p (h q) -> p h q", q=64), r_) for (t_, r_) in tps.items])
        pos_raw = self.rot("ops", [128, 512], F32, 4, ps=True)
        Ets = self.rot("Et", [128, 8, 64], BF16, 3)
        Es = self.rot("E", [128, 4, 8, 64], BF16, 2)
        recs = self.rot("rec", [64, 8], F32, 2)
        outs = self.rot("yno", [64, 8, 64], BF16, 3)
        for s in range(NS):
            S.dma("sp", qT[:], nq[s].rearrange("(c p) t -> p c t", p=128), reads=[self.dres(("nq", s))], writes=[RqT])
            S.dma("sp", kT[:], nk[s].rearrange("(c p) t -> p c t", p=128), reads=[self.dres(("nk", s))], writes=[RkT])
            hb = nblk // 2
            for (Vx, RVx, off, nb_tot) in ((Ve, RVe, 0, nblk), (Vo, RVo, 64, nblk - 1)):
                for b0 in range(0, nb_tot, hb):
                    nb_ = min(hb, nb_tot - b0)
                    S.dma("sp", Vs[:, 0:nb_, :], nv[s, off + b0 * 128:off + (b0 + nb_) * 128, :].rearrange("(b p) c -> p b c", p=128),
                          reads=[self.dres(("nv", s))], writes=[RVs])
                    S.op("pool", lambda e, Vx=Vx, b0=b0, nb_=nb_: e.tensor_copy(
                        out=Vx[:, b0:b0 + nb_, :, 0:64].rearrange("p b h n -> p (b h) n"),
                        in_=Vs[:, 0:nb_, :].rearrange("p b (h n) -> p (b h) n", n=64)), reads=[RVs], writes=[RVx])
            for i in range(rows):
                rs = min(max(i - 4, 0), rows - 8)
                E, RE = Es.next()
                for blk in range(4):
                    kr = rs + 2 * blk
                    tok0 = kr * 64
                    dib = kr - i + 7
                    psA, RpsA = pss.next()
                    psB, RpsB = pss.next()
                    Et, REt = Ets.next()

                    def mms(e, psA=psA, psB=psB, tok0=tok0, i=i):
                        for h in range(8):
                            pb = (h % 2) * 64
                            ps = psA if h % 2 == 0 else psB
                            ins = e.matmul(ps[:, h // 2, :], lhsT=kT[pb:pb + 64, h // 2, tok0:tok0 + 128], rhs=qT[pb:pb + 64, h // 2, i * 64:(i + 1) * 64],
                                           start=True, stop=True)
                        return ins
                    S.op("pe", mms, reads=[RkT, RqT], writes=[RpsA, RpsB])
                    S.op("act", lambda e, psA=psA, Et=Et: e.activation(out=Et[:, 0:4, :], in_=psA[:, 0:4, :], func=AF.Exp, scale=0.125), reads=[RpsA], writes=[REt])
                    S.op("act", lambda e, psB=psB, Et=Et: e.activation(out=Et[:, 4:8, :], in_=psB[:, 0:4, :], func=AF.Exp, scale=0.125), reads=[RpsB], writes=[REt])
                    S.op("dve", lambda e, Et=Et, E=E, blk=blk, dib=dib: e.tensor_tensor(
                        out=E[:, blk, :, :].rearrange("p (two c) q -> p two c q", two=2), in0=Et[:].rearrange("p (two c) q -> p two c q", two=2),
                        in1=TE[:].rearrange("p (c two) a q -> p two c a q", two=2)[:, :, :, dib, :], op=ALU.mult),
                         reads=[REt, RTE], writes=[RE])
                poA_, RpoA = pos_raw.next()
                poB_, RpoB = pos_raw.next()
                poA = poA_[0:64, 0:260].rearrange("p (h n) -> p h n", n=65)
                poB = poB_[0:64, 0:260].rearrange("p (h n) -> p h n", n=65)

                def mmo(e, E=E, rs=rs, poA=poA, poB=poB):
                    for h in range(8):
                        po = poA if h < 4 else poB
                        for blk in range(4):
                            kr = rs + 2 * blk
                            Vx = Ve if kr % 2 == 0 else Vo
                            ins = e.matmul(po[:, h % 4, :], lhsT=E[:, blk, (h % 2) * 4 + h // 2, :], rhs=Vx[:, kr // 2, h, :], start=(blk == 0), stop=(blk == 3))
                    return ins
                S.op("pe", mmo, reads=[RE, RVe, RVo], writes=[RpoA, RpoB])
                rec, Rrec = recs.next()
                o, Ro = outs.next()
                S.op("dve", lambda e, rec=rec, poA=poA: e.reciprocal(out=rec[:, 0:4], in_=poA[:, :, 64]), reads=[RpoA], writes=[Rrec])
                S.op("dve", lambda e, rec=rec, poB=poB: e.reciprocal(out=rec[:, 4:8], in_=poB[:, :, 64]), reads=[RpoB], writes=[Rrec])
                S.op("dve", lambda e, rec=rec, poA=poA, o=o: e.tensor_tensor(out=o[:, 0:4, :], in0=poA[:, :, 0:64],
                                                                            in1=rec[:, 0:4].unsqueeze(2).broadcast_to([64, 4, 64]), op=ALU.mult),
                     reads=[RpoA, Rrec], writes=[Ro])
                S.op("dve", lambda e, rec=rec, poB=poB, o=o: e.tensor_tensor(out=o[:, 4:8, :], in0=poB[:, :, 0:64],
                                                                            in1=rec[:, 4:8].unsqueeze(2).broadcast_to([64, 4, 64]), op=ALU.mult),
                     reads=[RpoB, Rrec], writes=[Ro])
                S.dma("pool", yn[s, i * 64:(i + 1) * 64, :], o[:].rearrange("p h n -> p (h n)"), reads=[Ro], writes=[self.dres(("yn", s))])
        self.end()

    def xupdate(self, xt, Rxt, nb, aT, RaT, nkc, W, RW, pss):
        S = self.S
        for b in range(nb):
            for half in range(2):
                ps, Rps = pss.next()

                def mm(e, ps=ps, b=b, half=half):
                    for kc in range(nkc):
                        ins = e.matmul(ps[:], lhsT=aT[:, kc, b * 128:(b + 1) * 128], rhs=W[:, kc, half * 512:(half + 1) * 512],
                                       start=(kc == 0), stop=(kc == nkc - 1))
                    return ins
                S.op("pe", mm, reads=[RaT, RW], writes=[Rps])
                S.op("dve", lambda e, ps=ps, b=b, half=half: e.tensor_tensor(
                    out=xt[:, b, half * 512:(half + 1) * 512], in0=xt[:, b, half * 512:(half + 1) * 512], in1=ps[:], op=ALU.add),
                    reads=[Rps, Rxt], writes=[Rxt])

    def phase_p4(self, l):
        S, T, NS = self.S, self.T, self.NS
        self.begin()
        TT = 512
        wbr, Rwbr = self.tile("wbr", [128, 8, D], BF16)
        wout, Rwout = self.tile("wout", [128, 8, D], BF16)
        stg = self.rot("wstg", [128, 2048], F32, 2)
        ident, Rid, _, _ = self.make_ident()
        self.load_weight(wbr, Rwbr, self.w["w_br_rwkv"][l], 512, D, stg, kc0=0)
        self.load_weight(wbr, Rwbr, self.w["w_br_nat"][l], 512, D, stg, kc0=4)
        self.load_weight(wout, Rwout, self.w["w_out"][l], D, D, stg)
        xts = self.rot("xt", [128, 4, D], F32, 2)
        yts = self.rot("yt", [128, 4, 1024], BF16, 2)
        yTs = self.rot("yT", [128, 8, TT], BF16, 2)
        sgts = self.rot("sgt", [128, 16, TT], BF16, 2)
        mTs = self.rot("mT", [128, 8, TT], BF16, 2)
        m1s = self.rot("m1", [128, TT], F32, 2)
        m2s = self.rot("m2", [128, TT], F32, 2)
        pTs = self.rot("pT", [128, 1024], BF16, 2, ps=True)
        pss = self.rot("ps", [128, 512], F32, 6, ps=True)
        xsrc = self.x_in if l == 0 else self.scr["xres"]
        xres, yr, yn, sg = (self.scr[k] for k in ("xres", "yr", "yn", "sg"))
        for s in range(NS):
            for t0 in range(0, T, TT):
                xt, Rxt = xts.next()
                yt, Ryt = yts.next()
                yT, RyT = yTs.next()
                sgt, Rsgt = sgts.next()
                mT, RmT = mTs.next()
                Rx = self.dres(("xres", s, t0))
                S.dma("sp", yt[:, :, 0:512], yr[s, t0:t0 + TT, :].rearrange("(b p) c -> p b c", p=128), reads=[self.dres(("yr", s))], writes=[Ryt])
                S.dma("sp", yt[:, :, 512:1024], yn[s, t0:t0 + TT, :].rearrange("(b p) c -> p b c", p=128), reads=[self.dres(("yn", s))], writes=[Ryt])
                S.dma("sp", sgt[:], sg[s, :, t0:t0 + TT].rearrange("(c p) t -> p c t", p=128), reads=[self.dres(("sg", s))], writes=[Rsgt])
                S.dma("sp", xt[:], xsrc[s, t0:t0 + TT, :].rearrange("(b p) d -> p b d", p=128), reads=([Rx] if l > 0 else []), writes=[Rxt])
                for b in range(4):
                    pT, RpT = pTs.next()

                    def tr(e, pT=pT, b=b, yt=yt):
                        for c in range(8):
                            ins = e.transpose(out=pT[:, c * 128:(c + 1) * 128], in_=yt[:, b, c * 128:(c + 1) * 128], identity=ident[:])
                        return ins
                    S.op("pe", tr, reads=[Ryt, Rid], writes=[RpT])
                    S.op("act", lambda e, pT=pT, b=b, yT=yT: e.copy(out=yT[:, :, b * 128:(b + 1) * 128], in_=pT[:].rearrange("p (k t) -> p k t", k=8)),
                         reads=[RpT], writes=[RyT])
                for oc in range(8):
                    ps1, Rps1 = pss.next()
                    ps2, Rps2 = pss.next()
                    m1, Rm1 = m1s.next()
                    m2, Rm2 = m2s.next()

                    def mm(e, ps1=ps1, ps2=ps2, oc=oc, yT=yT):
                        for kc in range(4):
                            e.matmul(ps1[:], lhsT=wbr[:, kc, oc * 128:(oc + 1) * 128], rhs=yT[:, kc, :], start=(kc == 0), stop=(kc == 3))
                        for kc in range(4):
                            ins = e.matmul(ps2[:], lhsT=wbr[:, 4 + kc, oc * 128:(oc + 1) * 128], rhs=yT[:, 4 + kc, :], start=(kc == 0), stop=(kc == 3))
                        return ins
                    S.op("pe", mm, reads=[Rwbr, RyT], writes=[Rps1, Rps2])
                    S.op("dve", lambda e, m1=m1, ps1=ps1, oc=oc, sgt=sgt: e.tensor_tensor(out=m1[:], in0=ps1[:], in1=sgt[:, oc, :], op=ALU.mult),
                         reads=[Rps1, Rsgt], writes=[Rm1])
                    S.op("dve", lambda e, m2=m2, ps2=ps2, oc=oc, sgt=sgt: e.tensor_tensor(out=m2[:], in0=ps2[:], in1=sgt[:, 8 + oc, :], op=ALU.mult),
                         reads=[Rps2, Rsgt], writes=[Rm2])
                    S.op("pool", lambda e, m1=m1, m2=m2, oc=oc, mT=mT: e.tensor_tensor(out=mT[:, oc, :], in0=m1[:], in1=m2[:], op=ALU.add),
                         reads=[Rm1, Rm2], writes=[RmT])
                self.xupdate(xt, Rxt, 4, mT, RmT, 8, wout, Rwout, pss)
                S.dma("pool", xres[s, t0:t0 + TT, :].rearrange("(b p) d -> p b d", p=128), xt[:], reads=[Rxt], writes=[Rx])
        self.end()

    def phase_p5(self, l):
        S, T, NS = self.S, self.T, self.NS
        self.begin()
        TT = 512
        wq, Rwq = self.tile("wq", [128, 8, D], BF16)
        wo, Rwo = self.tile("wo", [128, 8, D], BF16)
        wkv, Rwkv = self.tile("wkv", [128, 8, 2 * D], BF16)
        stg = self.rot("wstg", [128, 2048], F32, 2)
        ident, Rid, _, _ = self.make_ident()
        gb, Rgb = self.tile("gb", [128, D], F32)
        gm, Rgm = self.tile("gm", [128, D], F32)
        ones, Rones = self.tile("ones", [128, 128], BF16)
        S.op("pool", lambda e: e.memset(ones[:], 1.0), writes=[Rones])
        self.load_bcast(gb, Rgb, self.w["norm_x"][l], D)
        self.load_bcast(gm, Rgm, self.w["norm_mem"][l], D)
        self.load_weight(wkv, Rwkv, self.w["w_xkv"][l], D, 2 * D, stg)
        self.load_weight(wq, Rwq, self.w["w_xq"][l], D, D, stg)
        self.load_weight(wo, Rwo, self.w["w_xo"][l], D, D, stg)
        tmp = self.norm_tmp()
        xts = self.rot("xt", [128, 4, D], F32, 2)
        hTs = self.rot("hT", [128, 8, TT], BF16, 2)
        qTs = self.rot("qT", [128, 8, TT], BF16, 1)
        oTs = self.rot("oT", [128, 8, TT], BF16, 2)
        Es = self.rot("E", [128, 2, TT], BF16, 2)
        rdens = self.rot("rden", [128, TT], F32, 2)
        mt, Rmt = self.tile("memt", [128, 2, D], F32)
        mnT, RmnT = self.tile("memnT", [128, 8, NMEM], BF16)
        kT, RkT = self.tile("kT", [128, 8, NMEM], BF16)
        Vm, RVm = self.tile("Vm", [128, 2, D], BF16)
        pss = self.rot("ps", [128, 512], F32, 6, ps=True)
        xres = self.scr["xres"]
        for s in range(NS):
            S.dma("sp", mt[:], self.mem_in[s].rearrange("(b p) d -> p b d", p=128), writes=[Rmt])
            for b in range(2):
                self.norm_to_hT(mt, Rmt, b, gm, Rgm, mnT, RmnT, b * 128, ident, Rid, tmp)
            for cc in range(8):
                ps, Rps = pss.next()

                def mmk(e, ps=ps, cc=cc):
                    for kc in range(8):
                        ins = e.matmul(ps[:, 0:NMEM], lhsT=wkv[:, kc, cc * 128:(cc + 1) * 128], rhs=mnT[:, kc, :], start=(kc == 0), stop=(kc == 7))
                    return ins
                S.op("pe", mmk, reads=[Rwkv, RmnT], writes=[Rps])
                S.op("dve", lambda e, ps=ps, cc=cc: e.tensor_copy(out=kT[:, cc, :], in_=ps[:, 0:NMEM]), reads=[Rps], writes=[RkT])
            for mb in range(2):
                for half in range(2):
                    ps, Rps = pss.next()

                    def mmv(e, ps=ps, mb=mb, half=half):
                        for kc in range(8):
                            ins = e.matmul(ps[:], lhsT=mnT[:, kc, mb * 128:(mb + 1) * 128], rhs=wkv[:, kc, D + half * 512:D + (half + 1) * 512],
                                           start=(kc == 0), stop=(kc == 7))
                        return ins
                    S.op("pe", mmv, reads=[Rwkv, RmnT], writes=[Rps])
                    S.op("dve", lambda e, ps=ps, mb=mb, half=half: e.tensor_copy(out=Vm[:, mb, half * 512:(half + 1) * 512], in_=ps[:]),
                         reads=[Rps], writes=[RVm])
            for t0 in range(0, T, TT):
                xt, Rxt = xts.next()
                hT, RhT = hTs.next()
                qT, RqT = qTs.next()
                oT, RoT = oTs.next()
                Rx = self.dres(("xres", s, t0))
                S.dma("sp", xt[:], xres[s, t0:t0 + TT, :].rearrange("(b p) d -> p b d", p=128), reads=[Rx], writes=[Rxt])
                for b in range(4):
                    self.norm_to_hT(xt, Rxt, b, gb, Rgb, hT, RhT, b * 128, ident, Rid, tmp)
                for cc in range(8):
                    ps, Rps = pss.next()

                    def mmq(e, ps=ps, cc=cc, hT=hT):
                        for kc in range(8):
                            ins = e.matmul(ps[:], lhsT=wq[:, kc, cc * 128:(cc + 1) * 128], rhs=hT[:, kc, :], start=(kc == 0), stop=(kc == 7))
                        return ins
                    S.op("pe", mmq, reads=[Rwq, RhT], writes=[Rps])
                    S.op("dve", lambda e, ps=ps, cc=cc, qT=qT: e.tensor_copy(out=qT[:, cc, :], in_=ps[:]), reads=[Rps], writes=[RqT])
                for hd in range(4):
                    E, RE = Es.next()
                    rden, Rrden = rdens.next()
                    for mb in range(2):
                        ps, Rps = pss.next()

                        def mms(e, ps=ps, mb=mb, hd=hd, qT=qT):
                            for j in range(2):
                                ins = e.matmul(ps[:], lhsT=kT[:, 2 * hd + j, mb * 128:(mb + 1) * 128], rhs=qT[:, 2 * hd + j, :], start=(j == 0), stop=(j == 1))
                            return ins
                        S.op("pe", mms, reads=[RkT, RqT], writes=[Rps])
                        S.op("act", lambda e, ps=ps, mb=mb, E=E: e.activation(out=E[:, mb, :], in_=ps[:], func=AF.Exp, scale=1.0 / 16.0),
                             reads=[Rps], writes=[RE])
                    ps, Rps = pss.next()

                    def mmd(e, ps=ps, E=E):
                        for mb in range(2):
                            ins = e.matmul(ps[:], lhsT=ones[:], rhs=E[:, mb, :], start=(mb == 0), stop=(mb == 1))
                        return ins
                    S.op("pe", mmd, reads=[Rones, RE], writes=[Rps])
                    S.op("dve", lambda e, ps=ps, rden=rden: e.reciprocal(out=rden[:], in_=ps[:]), reads=[Rps], writes=[Rrden])
                    for j in range(2):
                        ps, Rps = pss.next()

                        def mmo(e, ps=ps, hd=hd, j=j, E=E):
                            for mb in range(2):
                                c0 = hd * 256 + j * 128
                                ins = e.matmul(ps[:], lhsT=Vm[:, mb, c0:c0 + 128], rhs=E[:, mb, :], start=(mb == 0), stop=(mb == 1))
                            return ins
                        S.op("pe", mmo, reads=[RVm, RE], writes=[Rps])
                        S.op("dve", lambda e, ps=ps, hd=hd, j=j, oT=oT, rden=rden: e.tensor_tensor(out=oT[:, 2 * hd + j, :], in0=ps[:], in1=rden[:], op=ALU.mult),
                             reads=[Rps, Rrden], writes=[RoT])
                self.xupdate(xt, Rxt, 4, oT, RoT, 8, wo, Rwo, pss)
                S.dma("pool", xres[s, t0:t0 + TT, :].rearrange("(b p) d -> p b d", p=128), xt[:], reads=[Rxt], writes=[Rx])
        self.end()

    def phase_p6(self, l):
        S, T, NS = self.S, self.T, self.NS
        self.begin()
        TT = 256
        NB = TT // 128
        last = (l == self.DEPTH - 1)
        w1, Rw1 = self.tile("w1", [128, 8, DFF], BF16)
        w2, Rw2 = self.tile("w2", [128, 32, D], BF16)
        stg = self.rot("wstg", [128, 2048], F32, 2)
        ident, Rid, _, _ = self.make_ident()
        gb, Rgb = self.tile("gb", [128, D], F32)
        self.load_bcast(gb, Rgb, self.w["norm_ff"][l], D)
        if last:
            gf, Rgf = self.tile("gf", [128, D], F32)
            self.load_bcast(gf, Rgf, self.w["norm_final"], D)
        self.load_weight(w1, Rw1, self.w["w_ff1"][l], D, DFF, stg)
        self.load_weight(w2, Rw2, self.w["w_ff2"][l], DFF, D, stg)
        tmp = self.norm_tmp()
        xts = self.rot("xt", [128, NB, D], F32, 2)
        hTs = self.rot("hT", [128, 8, TT], BF16, 2)
        uTs = self.rot("uT", [128, 32, TT], BF16, 1)
        rls = self.rot("rl", [128, TT], BF16, 3)
        pss = self.rot("ps", [128, 512], F32, 6, ps=True)
        xres = self.scr["xres"]
        for s in range(NS):
            for t0 in range(0, T, TT):
                xt, Rxt = xts.next()
                hT, RhT = hTs.next()
                uT, RuT = uTs.next()
                Rx = self.dres(("xres", s, t0 // 512 * 512))
                S.dma("sp", xt[:], xres[s, t0:t0 + TT, :].rearrange("(b p) d -> p b d", p=128), reads=[Rx], writes=[Rxt])
                for b in range(NB):
                    self.norm_to_hT(xt, Rxt, b, gb, Rgb, hT, RhT, b * 128, ident, Rid, tmp)
                for fc in range(32):
                    ps, Rps = pss.next()
                    rl, Rrl = rls.next()

                    def mm1(e, ps=ps, fc=fc, hT=hT):
                        for kc in range(8):
                            ins = e.matmul(ps[:, 0:TT], lhsT=w1[:, kc, fc * 128:(fc + 1) * 128], rhs=hT[:, kc, :], start=(kc == 0), stop=(kc == 7))
                        return ins
                    S.op("pe", mm1, reads=[Rw1, RhT], writes=[Rps])
                    S.op("act", lambda e, ps=ps, rl=rl: e.activation(out=rl[:], in_=ps[:, 0:TT], func=AF.Relu), reads=[Rps], writes=[Rrl])
                    S.op("pool", lambda e, rl=rl, fc=fc, uT=uT: e.tensor_tensor(out=uT[:, fc, :], in0=rl[:], in1=rl[:], op=ALU.mult),
                         reads=[Rrl], writes=[RuT])
                self.xupdate(xt, Rxt, NB, uT, RuT, 32, w2, Rw2, pss)
                if not last:
                    S.dma("pool", xres[s, t0:t0 + TT, :].rearrange("(b p) d -> p b d", p=128), xt[:], reads=[Rxt], writes=[Rx])
                else:
                    for b in range(NB):
                        junk, Rjunk = tmp["junk"].next()
                        ss, Rss = tmp["ss"].next()
                        S.op("act", lambda e, junk=junk, ss=ss, b=b, xt=xt: e.activation(out=junk[:], in_=xt[:, b, :], func=AF.Square, accum_out=ss[:, 0:1]),
                             reads=[Rxt], writes=[Rjunk, Rss])
                        S.op("act", lambda e, ss=ss: e.activation(out=ss[:, 1:2], in_=ss[:, 0:1], func=AF.Sqrt, scale=1.0 / D, bias=self.eps_t[:, 0:1]),
                             reads=[Rss, self.Reps], writes=[Rss])
                        S.op("dve", lambda e, ss=ss: e.reciprocal(out=ss[:, 2:3], in_=ss[:, 1:2]), reads=[Rss], writes=[Rss])
                        S.op("dve", lambda e, ss=ss, b=b, xt=xt: e.scalar_tensor_tensor(out=xt[:, b, :], in0=xt[:, b, :], scalar=ss[:, 2:3], in1=gf[:],
                                                                                         op0=ALU.mult, op1=ALU.mult),
                             reads=[Rxt, Rss, Rgf], writes=[Rxt])
                    Ry = self.dres(("y", s))
                    S.dma("pool", self.y_out[s, t0:t0 + TT, :].rearrange("(b p) d -> p b d", p=128), xt[:], reads=[Rxt], writes=[Ry])
        self.end()


_CACHE = {}


def _get_nc(T, NS, DEPTH, dbg=(), stop=None):
    key = (T, NS, DEPTH, tuple(dbg), stop)
    if key not in _CACHE:
        _CACHE[key] = Builder(T, NS, DEPTH, dbg, stop).build()
    return _CACHE[key]


def kernel(**inputs):
    NCORES, NS, T, DEPTH = 8, 2, 4096, 2
    xs = np.concatenate([np.asarray(inputs["x_prompt"], np.float32), np.asarray(inputs["x_sample"], np.float32)], 0)
    ms = np.concatenate([np.asarray(inputs["mem_prompt"], np.float32), np.asarray(inputs["mem_sample"], np.float32)], 0)
    nseq = xs.shape[0]
    slots = [[c, 8 + c if 8 + c < nseq else c] for c in range(NCORES)]
    wnames = [n for n, _ in WEIGHT_SPECS] + ["norm_final"]
    wmap = {n: np.ascontiguousarray(np.asarray(inputs[n], np.float32)) for n in wnames}
    nc = _get_nc(T, NS, DEPTH)
    in_maps = []
    for c in range(NCORES):
        m = dict(wmap)
        m["x"] = np.ascontiguousarray(xs[slots[c]])
        m["mem"] = np.ascontiguousarray(ms[slots[c]])
        in_maps.append(m)
    res = run_bass_kernel_spmd(nc, in_maps, core_ids=list(range(NCORES)))
    y = np.zeros_like(xs)
    for c in range(NCORES):
        yc = res.results[c]["y"]
        y[slots[c][0]] = yc[0]
        if slots[c][1] != slots[c][0]:
            y[slots[c][1]] = yc[1]
    nb = np.asarray(inputs["x_prompt"]).shape[0]
    return (y[:nb], y[nb:])
```

```python
import numpy as np
from contextlib import ExitStack
import concourse.bass as bass
import concourse.mybir as mybir
from concourse.bass_utils import run_bass_kernel_spmd

F32 = mybir.dt.float32
BF16 = mybir.dt.bfloat16
AF = mybir.ActivationFunctionType
ALU = mybir.AluOpType
AX = mybir.AxisListType

D = 1024
DIN = 5504
RW = 1920
NQ0 = 1920
NV0 = 2944
G0 = 3456
NMEM = 256
DFF = 4096
EPS = 1e-6
GN_EPS = 1e-5 * 64
CH = 128

EPOCH = 12000
ENGS = ("pe", "dve", "act", "pool", "sp")


class Res:
    __slots__ = ("name", "last_w", "readers", "dsem", "w_is_dma")

    def __init__(self, name=""):
        self.name = name
        self.last_w = None
        self.readers = []
        self.dsem = None
        self.w_is_dma = False


class Sched:
    def __init__(self, nc, stack):
        self.nc = nc
        self.stack = stack
        self.ops = {e: [] for e in ENGS}
        self.cnt = {e: 0 for e in ENGS}
        self.esems = {e: [] for e in ENGS}
        self.known = {e: {} for e in ENGS}
        self.sems = {}
        self.nsem = 0
        self.same_engine_raw = True
        self.dma_latest = {}
        self.free_dsems = []

    def new_sem(self, tag):
        h = self.stack.enter_context(self.nc.semaphore(f"{tag}_{self.nsem}"))
        sid = self.nsem
        self.nsem += 1
        self.sems[sid] = h
        return sid

    def _eng_point(self, eng):
        n = self.cnt[eng]
        self.cnt[eng] = n + 1
        ep, v = divmod(n, EPOCH)
        while len(self.esems[eng]) <= ep:
            self.esems[eng].append(self.new_sem(f"e{eng}"))
        return (self.esems[eng][ep], v + 1)

    def _dma_point(self, res):
        d = res.dsem
        if d is None or d[1] + 16 > EPOCH:
            if self.free_dsems and d is None:
                d = self.free_dsems.pop()
            else:
                d = [self.new_sem("d"), 0]
            res.dsem = d
        d[1] += 16
        self.dma_latest[d[0]] = d[1]
        return (d[0], d[1])

    def recycle(self, resources):
        seen = set()
        for r in resources:
            d = r.dsem
            if d is not None and id(d) not in seen and d[1] + 64 < EPOCH:
                seen.add(id(d))
                self.free_dsems.append(d)
            r.dsem = None

    def _need(self, eng, waits, pt):
        sid, val = pt
        if self.known[eng].get(sid, 0) >= val:
            return
        if waits.get(sid, 0) < val:
            waits[sid] = val

    def _deps(self, eng, reads, writes, is_dma=False):
        waits = {}
        for r in reads:
            if r.last_w is not None:
                w_eng = r.last_w[2]
                if w_eng != eng or is_dma or (self.same_engine_raw and eng != "pe"):
                    self._need(eng, waits, r.last_w[:2])
        for w in writes:
            if w.last_w is not None:
                w_eng = w.last_w[2]
                same_dma_group = is_dma and w.w_is_dma and not w.readers
                if (w_eng != eng or is_dma) and not same_dma_group:
                    self._need(eng, waits, w.last_w[:2])
            for (sid, val, r_eng) in w.readers:
                if r_eng != eng or is_dma:
                    self._need(eng, waits, (sid, val))
        for sid, val in waits.items():
            self.known[eng][sid] = val
        return list(waits.items())

    def op(self, eng, fn, reads=(), writes=()):
        waits = self._deps(eng, reads, writes)
        pt = self._eng_point(eng)
        self.ops[eng].append((waits, fn, pt[0], 1))
        for r in reads:
            r.readers.append((pt[0], pt[1], eng))
        for w in writes:
            w.last_w = (pt[0], pt[1], eng)
            w.readers = []
            w.w_is_dma = False
        return pt

    def dma(self, queue, out, in_, reads=(), writes=(), **kw):
        waits = self._deps(queue, reads, writes, is_dma=True)
        pt = self._dma_point(writes[0])

        def fn(e, out=out, in_=in_, kw=kw):
            return e.dma_start(out=out, in_=in_, **kw)
        self.ops[queue].append((waits, fn, pt[0], 16))
        for r in reads:
            r.readers.append((pt[0], pt[1], "dma"))
        for w in writes:
            w.last_w = (pt[0], pt[1], "dma")
            w.readers = []
            w.w_is_dma = True
            if w is not writes[0]:
                w.dsem = writes[0].dsem
        return pt

    def final_wait(self, eng, resources):
        waits = {}
        for r in resources:
            if r.last_w is not None:
                sid, val = r.last_w[:2]
                waits[sid] = max(waits.get(sid, 0), val)
        self.ops[eng].append((list(waits.items()), None, None, 0))

    def barrier(self):
        pts = {}
        for e in ENGS:
            n = self.cnt[e]
            if n > 0:
                ep, v = divmod(n - 1, EPOCH)
                pts[self.esems[e][ep]] = v + 1
        for sid, val in self.dma_latest.items():
            pts[sid] = max(pts.get(sid, 0), val)
        for e in ENGS:
            waits = []
            for sid, val in pts.items():
                if self.known[e].get(sid, 0) < val:
                    waits.append((sid, val))
                    self.known[e][sid] = val
            self.ops[e].append((waits, None, None, 0))

    def emit(self):
        nc = self.nc
        sems = self.sems

        def run(engname):
            def body(e):
                for (waits, fn, sid_, inc) in self.ops[engname]:
                    for sid, val in waits:
                        e.wait_ge(sems[sid], val)
                    if fn is None:
                        continue
                    ins = fn(e)
                    ins.then_inc(sems[sid_], inc)
            return body

        with nc.Block() as block:
            block.tensor(run("pe"))
            block.vector(run("dve"))
            block.scalar(run("act"))
            block.gpsimd(run("pool"))
            block.sync(run("sp"))
        self.ops = {e: [] for e in ENGS}


class Rot:
    def __init__(self, items):
        self.items = items
        self.i = 0

    def next(self):
        it = self.items[self.i % len(self.items)]
        self.i += 1
        return it


WEIGHT_SPECS = [
    ("norm_mix", (D,)), ("w_in", (D, DIN)), ("mu_prev", (RW,)), ("mu_next", (RW,)),
    ("w0", (2, 512)), ("w_up", (2, 64, 512)), ("a0", (2, 512)), ("a_up", (2, 64, 512)),
    ("g_up", (128, 512)), ("k_k", (512,)), ("k_a", (512,)), ("r_k", (8, 64)),
    ("gn_g", (512,)), ("gn_b", (512,)), ("rpb", (8, 15, 31)),
    ("w_br_rwkv", (512, D)), ("w_br_nat", (512, D)), ("w_out", (D, D)),
    ("norm_x", (D,)), ("norm_mem", (D,)), ("w_xq", (D, D)), ("w_xkv", (D, 2 * D)), ("w_xo", (D, D)),
    ("norm_ff", (D,)), ("w_ff1", (D, DFF)), ("w_ff2", (DFF, D)),
]


class Builder:
    def __init__(self, T, NS, DEPTH, dbg=(), stop=None, ext_in=(), phases=None):
        self.T, self.NS, self.DEPTH = T, NS, DEPTH
        self.ext_in = set(ext_in)
        self.phases = phases or ("p1", "p3", "p2", "p4", "p5", "p6")
        self.dbg = set(dbg)
        self.stop = stop
        self.nc = bass.Bass("TRN2", target_bir_lowering=False)
        nc = self.nc
        self.x_in = nc.dram_tensor("x", [NS, T, D], F32, kind="ExternalInput").ap()
        self.mem_in = nc.dram_tensor("mem", [NS, NMEM, D], F32, kind="ExternalInput").ap()
        self.w = {}
        for name, shp in WEIGHT_SPECS:
            self.w[name] = nc.dram_tensor(name, [DEPTH] + list(shp), F32, kind="ExternalInput").ap()
        self.w["norm_final"] = nc.dram_tensor("norm_final", [D], F32, kind="ExternalInput").ap()
        self.y_out = nc.dram_tensor("y", [NS, T, D], F32, kind="ExternalOutput").ap()
        self.scr = {}
        self.scr_res = {}

    def scratch(self, name, shape, dt):
        kind = "ExternalOutput" if name in self.dbg else ("ExternalInput" if name in self.ext_in else "Internal")
        t = self.nc.dram_tensor(name, shape, dt, kind=kind).ap()
        self.scr[name] = t
        return t

    def dres(self, key):
        r = self.scr_res.get(key)
        if r is None:
            r = Res(str(key))
            self.scr_res[key] = r
        return r

    def build(self):
        nc = self.nc
        T, NS = self.T, self.NS
        self.scratch("xres", [NS, T, D], F32)
        self.scratch("pf", [NS, RW, T], F32)
        self.scratch("nq", [NS, 512, T], BF16)
        self.scratch("nk", [NS, 512, T], BF16)
        self.scratch("nv", [NS, T, 512], BF16)
        self.scratch("sg", [NS, 2048, T], BF16)
        self.scratch("yr", [NS, T, 512], BF16)
        self.scratch("yn", [NS, T, 512], BF16)
        self.scratch("yf", [NS, T, 512], F32)
        self.scratch("bon", [NS, T, 8], F32)
        self.scratch("phd", [NS, T // 128, 128, 1920], F32)
        with ExitStack() as st:
            self.S = Sched(nc, st)
            done = False
            for l in range(self.DEPTH):
                for ph in self.phases:
                    getattr(self, "phase_" + ph)(l)
                    if self.stop == (l, ph):
                        done = True
                        break
                if done:
                    break
            with ExitStack() as ph:
                self.S.final_wait("pool", list(self.scr_res.values()))
                self.S.emit()
        return nc

    def begin(self):
        self.S.barrier()
        self.phc = getattr(self, "phc", 0) + 1
        self.ph = ExitStack()
        self.ph_res = []
        return self.ph

    def end(self):
        self.S.emit()
        self.S.recycle(self.ph_res)
        self.ph.close()

    def tile(self, name, shape, dt):
        t = self.ph.enter_context(self.nc.sbuf_tensor(f"{name}_{self.phc}", shape, dt))
        r = Res(name)
        self.ph_res.append(r)
        return t, r

    def psum(self, name, shape, dt):
        t = self.ph.enter_context(self.nc.psum_tensor(f"{name}_{self.phc}", shape, dt))
        r = Res(name)
        self.ph_res.append(r)
        return t, r

    def rot(self, name, shape, dt, n, ps=False):
        f = self.psum if ps else self.tile
        return Rot([f(f"{name}{i}", shape, dt) for i in range(n)])

    def make_ident(self):
        S = self.S
        idf, Ridf = self.tile("identf", [128, 128], F32)
        idb, Ridb = self.tile("identb", [128, 128], BF16)

        S.op("pool", lambda e: e.memset(idf[:], 0.0), writes=[Ridf])
        S.op("pool", lambda e: e.affine_select(out=idf[:], in_=idf[:], pattern=[[-1, 128]], compare_op=ALU.not_equal,
                                               fill=1.0, base=0, channel_multiplier=1), reads=[Ridf], writes=[Ridf])
        S.op("dve", lambda e: e.tensor_copy(out=idb[:], in_=idf[:]), reads=[Ridf], writes=[Ridb])
        return idb, Ridb, idf, Ridf

    def load_weight(self, dst, Rdst, src2d, K, cols, stg, col0=0, kc0=0):
        S = self.S
        nk = K // 128
        srcv = src2d.rearrange("(kc p) c -> p kc c", p=128)
        engs = ("pool", "dve", "act")
        i = 0
        for kc in range(nk):
            for c0 in range(0, cols, 2048):
                cw = min(2048, cols - c0)
                stile, Rst = stg.next()
                S.dma("sp", stile[:, 0:cw], srcv[:, kc, c0:c0 + cw], writes=[Rst])
                eng = engs[i % 3]
                i += 1
                o = dst[:, kc0 + kc, col0 + c0:col0 + c0 + cw]
                if eng == "act":
                    S.op("act", lambda e, o=o, s=stile, cw=cw: e.copy(out=o, in_=s[:, 0:cw]), reads=[Rst], writes=[Rdst])
                else:
                    S.op(eng, lambda e, o=o, s=stile, cw=cw: e.tensor_copy(out=o, in_=s[:, 0:cw]), reads=[Rst], writes=[Rdst])

    def load_bcast(self, dst, Rdst, src1d, n):
        self.S.dma("sp", dst[:, 0:n], src1d.rearrange("(o n) -> o n", o=1).partition_broadcast(128), writes=[Rdst])

    def norm_to_hT(self, xt, Rxt, b, gb, Rgb, hT, RhT, tcol, ident, Rid, tmp):
        S = self.S
        junk, Rjunk = tmp["junk"].next()
        ss, Rss = tmp["ss"].next()
        h, Rh = tmp["h"].next()
        pT, RpT = tmp["pT"].next()
        S.op("act", lambda e: e.activation(out=junk[:], in_=xt[:, b, :], func=AF.Square, accum_out=ss[:, 0:1]),
             reads=[Rxt], writes=[Rjunk, Rss])

        S.op("act", lambda e: e.activation(out=ss[:, 1:2], in_=ss[:, 0:1], func=AF.Sqrt, scale=1.0 / D, bias=self.eps_t[:, 0:1]),
             reads=[Rss, self.Reps], writes=[Rss])
        S.op("dve", lambda e: e.reciprocal(out=ss[:, 2:3], in_=ss[:, 1:2]), reads=[Rss], writes=[Rss])
        S.op("dve", lambda e: e.scalar_tensor_tensor(out=h[:], in0=xt[:, b, :], scalar=ss[:, 2:3], in1=gb[:],
                                                     op0=ALU.mult, op1=ALU.mult),
             reads=[Rxt, Rss, Rgb], writes=[Rh])

        def tr(e):
            for kc in range(8):
                ins = e.transpose(out=pT[:, kc * 128:(kc + 1) * 128], in_=h[:, kc * 128:(kc + 1) * 128], identity=ident[:])
            return ins
        S.op("pe", tr, reads=[Rh, Rid], writes=[RpT])
        S.op("act", lambda e: e.copy(out=hT[:, :, tcol:tcol + 128], in_=pT[:].rearrange("p (k t) -> p k t", k=8)),
             reads=[RpT], writes=[RhT])
        return ss

    def norm_tmp(self):
        self.eps_t, Re = self.tile("eps_t", [128, 2], F32)
        eps_t = self.eps_t
        self.S.op("pool", lambda e: e.memset(eps_t[:], EPS), writes=[Re])
        self.Reps = Re
        return {
            "junk": self.rot("junk", [128, D], BF16, 1),
            "ss": self.rot("ss", [128, 4], F32, 4),
            "h": self.rot("h", [128, D], BF16, 2),
            "pT": self.rot("pT", [128, 1024], BF16, 2, ps=True),
        }

    def phase_p1(self, l):
        S, T, NS = self.S, self.T, self.NS
        self.begin()
        TT = 512
        win, Rwin = self.tile("win", [128, 8, DIN], BF16)
        stg = self.rot("wstg", [128, 2048], F32, 2)
        gb, Rgb = self.tile("gb", [128, D], F32)
        ident, Rid, _, _ = self.make_ident()
        self.load_bcast(gb, Rgb, self.w["norm_mix"][l], D)
        self.load_weight(win, Rwin, self.w["w_in"][l], D, DIN, stg)
        tmp = self.norm_tmp()
        xts = self.rot("xt", [128, 4, D], F32, 2)
        hTs = self.rot("hT", [128, 8, TT], BF16, 2)
        pss = self.rot("ps", [128, 512], F32, 4, ps=True)
        of32 = self.rot("of32", [128, 512], F32, 3)
        obf = self.rot("obf", [128, 512], BF16, 4)
        xsrc = self.x_in if l == 0 else self.scr["xres"]
        pf, nq, nk, nv, sg = (self.scr[k] for k in ("pf", "nq", "nk", "nv", "sg"))
        ev = 0
        for s in range(NS):
            for t0 in range(0, T, TT):
                xt, Rxt = xts.next()
                hT, RhT = hTs.next()
                xr = [self.dres(("xres", s, t0))] if l > 0 else []
                S.dma("sp", xt[:], xsrc[s, t0:t0 + TT, :].rearrange("(b p) d -> p b d", p=128), reads=xr, writes=[Rxt])
                for b in range(4):
                    self.norm_to_hT(xt, Rxt, b, gb, Rgb, hT, RhT, b * 128, ident, Rid, tmp)
                for cc in range(DIN // 128):
                    c0 = cc * 128
                    if NV0 <= c0 < G0:
                        continue
                    ps, Rps = pss.next()

                    def mm(e, ps=ps, c0=c0, hT=hT):
                        for kc in range(8):
                            ins = e.matmul(ps[:], lhsT=win[:, kc, c0:c0 + 128], rhs=hT[:, kc, :], start=(kc == 0), stop=(kc == 7))
                        return ins
                    S.op("pe", mm, reads=[Rwin, RhT], writes=[Rps])
                    if c0 < RW:
                        o, Ro = of32.next()
                        eng = "dve" if ev % 2 == 0 else "act"
                        ev += 1
                        if eng == "dve":
                            S.op("dve", lambda e, o=o, ps=ps: e.tensor_copy(out=o[:], in_=ps[:]), reads=[Rps], writes=[Ro])
                        else:
                            S.op("act", lambda e, o=o, ps=ps: e.copy(out=o[:], in_=ps[:]), reads=[Rps], writes=[Ro])
                        S.dma("pool", pf[s, c0:c0 + 128, t0:t0 + TT], o[:], reads=[Ro], writes=[self.dres(("pf", s))])
                    elif c0 < NV0:
                        o, Ro = obf.next()
                        S.op("dve", lambda e, o=o, ps=ps: e.tensor_copy(out=o[:], in_=ps[:]), reads=[Rps], writes=[Ro])
                        if c0 < NQ0 + 512:
                            S.dma("pool", nq[s, c0 - NQ0:c0 - NQ0 + 128, t0:t0 + TT], o[:], reads=[Ro], writes=[self.dres(("nq", s))])
                        else:
                            c1 = c0 - NQ0 - 512
                            S.dma("pool", nk[s, c1:c1 + 128, t0:t0 + TT], o[:], reads=[Ro], writes=[self.dres(("nk", s))])
                    else:
                        o, Ro = obf.next()
                        S.op("act", lambda e, o=o, ps=ps: e.activation(out=o[:], in_=ps[:], func=AF.Sigmoid), reads=[Rps], writes=[Ro])
                        c1 = c0 - G0
                        S.dma("pool", sg[s, c1:c1 + 128, t0:t0 + TT], o[:], reads=[Ro], writes=[self.dres(("sg", s))])
                for b in range(4):
                    ps, Rps = pss.next()

                    def mmv(e, ps=ps, b=b, hT=hT):
                        for kc in range(8):
                            ins = e.matmul(ps[:], lhsT=hT[:, kc, b * 128:(b + 1) * 128], rhs=win[:, kc, NV0:NV0 + 512], start=(kc == 0), stop=(kc == 7))
                        return ins
                    S.op("pe", mmv, reads=[Rwin, RhT], writes=[Rps])
                    o, Ro = obf.next()
                    S.op("dve", lambda e, o=o, ps=ps: e.tensor_copy(out=o[:], in_=ps[:]), reads=[Rps], writes=[Ro])
                    S.dma("pool", nv[s, t0 + b * 128:t0 + (b + 1) * 128, :], o[:], reads=[Ro], writes=[self.dres(("nv", s))])
        self.end()

    def phase_p2(self, l):
        S, T, NS = self.S, self.T, self.NS
        self.begin()
        C = 128
        NCH = T // C
        LD = 0.6065306597126334
        pf, yf, yr = self.scr["pf"], self.scr["yf"], self.scr["yr"]
        bon = self.scr["bon"]
        phd = self.scr["phd"]
        op = S.op
        bc = lambda ap, shape: ap.broadcast_to(shape)
        identb, Ridb, identf, Ridf = self.make_ident()
        mup, Rmup = self.tile("mup", [128, 15], F32)
        mun, Rmun = self.tile("mun", [128, 15], F32)
        w0t, Rw0 = self.tile("w0t", [128, 2, 4], F32)
        a0t, Ra0 = self.tile("a0t", [128, 2, 4], F32)
        kkc, Rkkc = self.tile("kkc", [128, 4], F32)
        kac, Rkac = self.tile("kac", [128, 4], F32)
        omka, Romka = self.tile("omka", [128, 4], F32)
        rkc, Rrkc = self.tile("rkc", [128, 4], F32)
        S.dma("sp", mup[:], self.w["mu_prev"][l].rearrange("(c p) -> p c", p=128), writes=[Rmup], allow_slow_non_contiguous=True)
        S.dma("sp", mun[:], self.w["mu_next"][l].rearrange("(c p) -> p c", p=128), writes=[Rmun], allow_slow_non_contiguous=True)
        for d in range(2):
            S.dma("sp", w0t[:, d, :], self.w["w0"][l, d].rearrange("(c p) -> p c", p=128), writes=[Rw0], allow_slow_non_contiguous=True)
            S.dma("sp", a0t[:, d, :], self.w["a0"][l, d].rearrange("(c p) -> p c", p=128), writes=[Ra0], allow_slow_non_contiguous=True)
        S.dma("sp", kkc[:], self.w["k_k"][l].rearrange("(c p) -> p c", p=128), writes=[Rkkc], allow_slow_non_contiguous=True)
        S.dma("sp", kac[:], self.w["k_a"][l].rearrange("(c p) -> p c", p=128), writes=[Rkac], allow_slow_non_contiguous=True)
        S.dma("sp", rkc[:], self.w["r_k"][l].rearrange("h n -> (h n)").rearrange("(c p) -> p c", p=128), writes=[Rrkc], allow_slow_non_contiguous=True)
        op("dve", lambda e: e.tensor_scalar(out=omka[:], in0=kac[:], scalar1=-1.0, scalar2=1.0, op0=ALU.mult, op1=ALU.add), reads=[Rkac], writes=[Romka])
        gng, Rgng = self.tile("gng", [128, 512], F32)
        gnb, Rgnb = self.tile("gnb", [128, 512], F32)
        self.load_bcast(gng, Rgng, self.w["gn_g"][l], 512)
        self.load_bcast(gnb, Rgnb, self.w["gn_b"][l], 512)
        wstg, Rwstg = self.tile("lstg", [128, 3, 512], F32)
        S.dma("sp", wstg[:, 0, :], self.w["w_up"][l].rearrange("d l c -> (d l) c"), writes=[Rwstg])
        S.dma("sp", wstg[:, 1, :], self.w["a_up"][l].rearrange("d l c -> (d l) c"), writes=[Rwstg])
        S.dma("sp", wstg[:, 2, :], self.w["g_up"][l], writes=[Rwstg])
        wupz, Rwupz = self.tile("wupz", [128, 2, 512], BF16)
        aupz, Raupz = self.tile("aupz", [128, 2, 512], BF16)
        gup, Rgup = self.tile("gup", [128, 512], BF16)
        op("pool", lambda e: e.memset(wupz[:].rearrange("p d c -> p (d c)"), 0.0), writes=[Rwupz])
        op("pool", lambda e: e.memset(aupz[:].rearrange("p d c -> p (d c)"), 0.0), writes=[Raupz])
        for d in range(2):
            op("dve", lambda e, d=d: e.tensor_copy(out=wupz[d * 64:(d + 1) * 64, d, :], in_=wstg[d * 64:(d + 1) * 64, 0, :]), reads=[Rwstg, Rwupz], writes=[Rwupz])
            op("dve", lambda e, d=d: e.tensor_copy(out=aupz[d * 64:(d + 1) * 64, d, :], in_=wstg[d * 64:(d + 1) * 64, 1, :]), reads=[Rwstg, Raupz], writes=[Raupz])
        op("dve", lambda e: e.tensor_copy(out=gup[:], in_=wstg[:, 2, :]), reads=[Rwstg], writes=[Rgup])
        onesf, Ronesf = self.tile("onesf", [128, 128], F32)
        op("pool", lambda e: e.memset(onesf[:], 1.0), writes=[Ronesf])
        blk1, Rblk1 = self.tile("blk1", [128, 128], F32)
        hsel, Rhsel = self.tile("hsel", [128, 2], F32)
        bdm, Rbdm = self.tile("bdm", [128, 4, 2, 64], F32)

        def mkz(e):
            e.memset(blk1[:], 0.0)
            e.memset(hsel[:], 0.0)
            return e.memset(bdm[:].rearrange("p c h i -> p (c h i)"), 0.0)
        op("dve", mkz, writes=[Rblk1, Rhsel, Rbdm])

        def mko(e):
            e.memset(blk1[0:64, 0:64], 1.0)
            e.memset(blk1[64:128, 64:128], 1.0)
            e.memset(hsel[0:64, 0:1], 1.0)
            e.memset(hsel[64:128, 1:2], 1.0)
            e.memset(bdm[0:64, :, 0, :], 1.0)
            return e.memset(bdm[64:128, :, 1, :], 1.0)
        op("dve", mko, reads=[Rblk1, Rhsel, Rbdm], writes=[Rblk1, Rhsel, Rbdm])
        gne, Rgne = self.tile("gne", [128, 1], F32)
        op("pool", lambda e: e.memset(gne[:], GN_EPS), writes=[Rgne])
        mbase, Rmbase = self.tile("mbase", [128, 4, 128], F32)

        op("pool", lambda e: e.memset(mbase[:].rearrange("p a x -> p (a x)"), 1.0), writes=[Rmbase])
        op("pool", lambda e: e.affine_select(out=mbase[:, 0, :], in_=mbase[:, 0, :], pattern=[[1, 128]], compare_op=ALU.is_gt, fill=0.0, base=0, channel_multiplier=-1), reads=[Rmbase], writes=[Rmbase])
        op("pool", lambda e: e.affine_select(out=mbase[:, 1, :], in_=mbase[:, 1, :], pattern=[[1, 128]], compare_op=ALU.is_ge, fill=0.0, base=0, channel_multiplier=-1), reads=[Rmbase], writes=[Rmbase])
        op("pool", lambda e: e.affine_select(out=mbase[:, 2, :], in_=mbase[:, 2, :], pattern=[[-1, 128]], compare_op=ALU.is_gt, fill=0.0, base=0, channel_multiplier=1), reads=[Rmbase], writes=[Rmbase])
        op("pool", lambda e: e.affine_select(out=mbase[:, 3, :], in_=mbase[:, 3, :], pattern=[[-1, 128]], compare_op=ALU.is_ge, fill=0.0, base=0, channel_multiplier=1), reads=[Rmbase], writes=[Rmbase])
        mG, RmG = self.tile("mG", [128, 2, 2, 2, 128], BF16)
        mL, RmL = self.tile("mL", [128, 2, 2, 128], BF16)
        for d in range(2):
            si, ii, li = (0, 1, 2) if d == 0 else (2, 3, 0)
            for h2 in range(2):
                op("dve", lambda e, d=d, h2=h2, si=si: e.tensor_copy(out=mG[:, d, h2, 0, :], in_=mbase[:, si, :]), reads=[Rmbase], writes=[RmG])
                op("dve", lambda e, d=d, h2=h2, ii=ii: e.tensor_copy(out=mG[:, d, h2, 1, :], in_=mbase[:, ii, :]), reads=[Rmbase], writes=[RmG])
                op("dve", lambda e, d=d, h2=h2, li=li: e.tensor_copy(out=mL[:, d, h2, :], in_=mbase[:, li, :]), reads=[Rmbase], writes=[RmL])
        Eg, REg = self.tile("Eg", [8, 3, 128], F32)
        op("pool", lambda e: e.memset(Eg[:].rearrange("p a x -> p (a x)"), 1.0), writes=[REg])
        for gi, b_ in enumerate((16, 32, 64)):
            op("pool", lambda e, gi=gi, b_=b_: e.affine_select(out=Eg[:, gi, :], in_=Eg[:, gi, :], pattern=[[1, 128]], compare_op=ALU.is_ge, fill=0.0, base=0, channel_multiplier=-b_),
               reads=[REg], writes=[REg])
            op("pool", lambda e, gi=gi, b_=b_: e.affine_select(out=Eg[:, gi, :], in_=Eg[:, gi, :], pattern=[[-1, 128]], compare_op=ALU.is_ge, fill=0.0, base=b_ - 1, channel_multiplier=b_),
               reads=[REg], writes=[REg])
        bdf, Rbdf = self.tile("bdf", [128, 4, 128], F32)
        op("pool", lambda e: e.memset(bdf[:, 3, :], 1.0), writes=[Rbdf])
        pbm, Rpbm = self.psum("pbm", [128, 512], F32)

        def mmE(e):
            for gi in range(3):
                ins = e.matmul(pbm[:, gi * 128:(gi + 1) * 128], lhsT=Eg[:, gi, :], rhs=Eg[:, gi, :], start=True, stop=True)
            return ins
        op("pe", mmE, reads=[REg], writes=[Rpbm])
        op("dve", lambda e: e.tensor_copy(out=bdf[:, 0:3, :].rearrange("p a x -> p (a x)"), in_=pbm[:, 0:384]), reads=[Rpbm, Rbdf], writes=[Rbdf])
        bd16, Rbd16 = self.tile("bd16", [128, 2, 128], BF16)
        offm, Roffm = self.tile("offm", [128, 3, 4, 128], BF16)
        for h2 in range(4):
            if h2 < 2:
                op("dve", lambda e, h2=h2: e.tensor_copy(out=bd16[:, h2, :], in_=bdf[:, 0, :]), reads=[Rbdf], writes=[Rbd16])
            for lv in range(3):
                op("dve", lambda e, h2=h2, lv=lv: e.tensor_tensor(out=offm[:, lv, h2, :], in0=bdf[:, lv + 1, :], in1=bdf[:, lv, :], op=ALU.subtract),
                   reads=[Rbdf], writes=[Roffm])
        Ps = self.rot("P", [128, 15, 130], F32, 1)
        tAs = self.rot("tA", [128, 15, 128], F32, 1)
        tBs = self.rot("tB", [128, 15, 128], F32, 1)
        phs = self.rot("ph", [128, 15, 128], F32, 1)
        f4 = lambda n, k=1: self.rot(n, [128, 4, 128], F32, k)
        b4 = lambda n, k=1: self.rot(n, [128, 4, 128], BF16, k)
        lws, avs, pres, cums, cprs = f4("lw"), f4("av"), f4("pre"), f4("cum"), f4("cpr")
        ecs, ecps, eis = f4("ec"), f4("ecp"), f4("ei")
        kraws, sqs, nrms, kkvs, tts, kds, bvs = (f4(n) for n in ("kraw", "sq", "nrm", "kkv", "tt", "kd", "bv"))
        Kfs, Bfs, rks = kraws, nrms, sqs
        ATs, RTs, KTs, BTs = (b4(n, 2) for n in ("AT", "RT", "KT", "BT"))
        KhTs, BhTs, vTs = (b4(n, 1) for n in ("KhT", "BhT", "vT"))
        twds = self.rot("twd", [128, 128], BF16, 2)
        adbs = self.rot("adb", [128, 128], BF16, 2)
        sgds = self.rot("sgd", [128, 128], BF16, 2)
        gCs = self.rot("gC", [128, 4], F32, 2)
        ARbds = self.rot("ARbd", [128, 4, 2, 2, 128], BF16, 2)
        Bbds = self.rot("Bbd", [128, 4, 2, 128], BF16, 2)
        for (t_, r_) in ARbds.items:
            op("pool", lambda e, t_=t_: e.memset(t_[:].rearrange("p c h x t -> p (c h x t)"), 0.0), writes=[r_])
        for (t_, r_) in Bbds.items:
            op("pool", lambda e, t_=t_: e.memset(t_[:].rearrange("p c h t -> p (c h t)"), 0.0), writes=[r_])
        VKs = self.rot("VK", [128, 1024], BF16, 2)
        Bhats = self.rot("Bhat", [128, 512], BF16, 2)
        GKs = self.rot("GK", [128, 4, 2, 2, 128], BF16, 2)
        GBs = self.rot("GB", [128, 4, 2, 2, 128], BF16, 2)
        Lhs = self.rot("Lh", [128, 4, 2, 128], BF16, 1)
        lmq = [[self.tile(f"lmq{h}_{k}", [128, 4, 128], BF16) for k in range(2)] for h in range(8)]
        zzt = [self.tile(f"zz{h}", [128, 2, 2, 128], BF16) for h in range(4)]
        ttt = [[self.tile(f"tt{h}_{k}", [128, 2, 2, 128], BF16) for k in range(2)] for h in range(4)]
        tfin = [self.rot(f"tfin{h}", [128, 2, 2, 128], BF16, 2) for h in range(4)]
        L16s = self.rot("L16", [128, 4, 2, 128], BF16, 1)
        M16s = self.rot("M16", [128, 4, 2, 128], BF16, 1)
        St, RSt = self.tile("St", [128, 4, 2, 64], F32)
        Sbd, RSbd = self.tile("Sbd", [128, 4, 2, 64], BF16)
        stmp, Rstmp = self.tile("stmp", [128, 4, 2, 64], F32)
        Wsbs = self.rot("Wsb", [128, 512], BF16, 1)
        Usbs = self.rot("Usb", [128, 512], BF16, 1)
        Ysbs = self.rot("Ysb", [128, 512], F32, 2)
        bons = self.rot("bonv", [128, 8], F32, 2)
        bon0s = self.rot("bon0", [128, 8], F32, 2)
        yfls = self.rot("yfl", [128, 512], F32, 1)
        gtms = self.rot("gtm", [128, 512], F32, 2)
        sqvs = self.rot("sqv", [128, 512], F32, 1)
        ycs = self.rot("yc", [128, 512], F32, 1)
        vvs = sqvs
        stat = self.rot("stat", [128, 6, 8], F32, 2)
        yos = self.rot("yo", [128, 512], BF16, 1)
        pbanks = self.rot("pb", [128, 512], F32, 6, ps=True)
        ptr = self.rot("ptr", [128, 1024], BF16, 1, ps=True)
        Ptile, RPtile = self.tile("Ptile", [128, 8, 128], BF16)
        pfv = [pf[s].rearrange("(c p) t -> p c t", p=128) for s in range(NS)]

        for s in range(NS):
            for d in range(2):
                op("pool", lambda e: e.memset(St[:].rearrange("p c h i -> p (c h i)"), 0.0), writes=[RSt])
                op("pool", lambda e: e.memset(Sbd[:].rearrange("p c h i -> p (c h i)"), 0.0), writes=[RSbd])
                order = range(NCH) if d == 0 else range(NCH - 1, -1, -1)
                def body(ci, s=s, d=d):
                    t0 = ci * C
                    ph, Rph = phs.next()
                    if d == 0:
                        P, RP = Ps.next()
                        lo, hi = max(t0 - 1, 0), min(t0 + C + 1, T)
                        if t0 == 0:
                            op("pool", lambda e, P=P: e.memset(P[:, :, 0:1], 0.0), writes=[RP])
                        if t0 + C == T:
                            op("pool", lambda e, P=P: e.memset(P[:, :, 129:130], 0.0), writes=[RP])
                        S.dma("sp", P[:, :, lo - (t0 - 1):hi - (t0 - 1)], pfv[s][:, :, lo:hi], reads=[self.dres(("pf", s))], writes=[RP])
                        tA, RtA = tAs.next()
                        tB, RtB = tBs.next()
                        for (c0_, c1_) in ((0, 5), (5, 10), (10, 15)):
                            cs_ = slice(c0_, c1_)
                            nn = c1_ - c0_
                            op("dve", lambda e, cs_=cs_: e.tensor_tensor(out=tA[:, cs_, :], in0=P[:, cs_, 0:128], in1=P[:, cs_, 1:129], op=ALU.subtract), reads=[RP], writes=[RtA])
                            op("pool", lambda e, cs_=cs_: e.tensor_tensor(out=tB[:, cs_, :], in0=P[:, cs_, 2:130], in1=P[:, cs_, 1:129], op=ALU.subtract), reads=[RP], writes=[RtB])
                            op("dve", lambda e, cs_=cs_, nn=nn: e.tensor_tensor(out=tA[:, cs_, :], in0=tA[:, cs_, :], in1=bc(mup[:, cs_].unsqueeze(2), [128, nn, 128]), op=ALU.mult), reads=[RtA, Rmup], writes=[RtA])
                            op("pool", lambda e, cs_=cs_, nn=nn: e.tensor_tensor(out=tB[:, cs_, :], in0=tB[:, cs_, :], in1=bc(mun[:, cs_].unsqueeze(2), [128, nn, 128]), op=ALU.mult), reads=[RtB, Rmun], writes=[RtB])
                            yield 'a'
                            op("dve", lambda e, cs_=cs_: e.tensor_tensor(out=tA[:, cs_, :], in0=tA[:, cs_, :], in1=tB[:, cs_, :], op=ALU.add), reads=[RtA, RtB], writes=[RtA])
                            op("pool", lambda e, cs_=cs_: e.tensor_tensor(out=ph[:, cs_, :], in0=tA[:, cs_, :], in1=P[:, cs_, 1:129], op=ALU.add), reads=[RtA, RP], writes=[Rph])
                            yield 'a'
                        S.dma("pool", phd[s, ci], ph[:].rearrange("p c t -> p (c t)"), reads=[Rph], writes=[self.dres(("phd", s))])
                    else:
                        S.dma("sp", ph[:].rearrange("p c t -> p (c t)"), phd[s, ci], reads=[self.dres(("phd", s))], writes=[Rph])
                    rh, kh, vh = ph[:, 0:4, :], ph[:, 4:8, :], ph[:, 8:12, :]
                    yield 'a'
                    twd, Rtwd = twds.next()
                    adb, Radb = adbs.next()
                    sgd, Rsgd = sgds.next()
                    op("act", lambda e, twd=twd, ph=ph: e.activation(out=twd[:], in_=ph[:, 12, :], func=AF.Tanh), reads=[Rph], writes=[Rtwd])
                    op("dve", lambda e, adb=adb, ph=ph: e.tensor_copy(out=adb[:], in_=ph[:, 13, :]), reads=[Rph], writes=[Radb])
                    psw, Rpsw = pbanks.next()
                    psa, Rpsa = pbanks.next()

                    def mmlora(e, psw=psw, psa=psa, twd=twd, adb=adb, d=d):
                        for cc in range(4):
                            e.matmul(psw[:, cc * 128:(cc + 1) * 128], lhsT=wupz[:, d, cc * 128:(cc + 1) * 128], rhs=twd[:], start=True, stop=True)
                        for cc in range(4):
                            ins = e.matmul(psa[:, cc * 128:(cc + 1) * 128], lhsT=aupz[:, d, cc * 128:(cc + 1) * 128], rhs=adb[:], start=True, stop=True)
                        return ins
                    op("pe", mmlora, reads=[Rwupz, Raupz, Rtwd, Radb], writes=[Rpsw, Rpsa])
                    yield 'a'
                    lw, Rlw = lws.next()
                    av, Rav = avs.next()

                    def sigw(e, lw=lw, psw=psw, d=d):
                        for cc in range(4):
                            ins = e.activation(out=lw[:, cc, :], in_=psw[:, cc * 128:(cc + 1) * 128], func=AF.Sigmoid, bias=w0t[:, d, cc:cc + 1])
                        return ins
                    op("act", sigw, reads=[Rpsw, Rw0], writes=[Rlw])

                    def siga(e, av=av, psa=psa, d=d):
                        for cc in range(4):
                            ins = e.activation(out=av[:, cc, :], in_=psa[:, cc * 128:(cc + 1) * 128], func=AF.Sigmoid, bias=a0t[:, d, cc:cc + 1])
                        return ins
                    op("act", siga, reads=[Rpsa, Ra0], writes=[Rav])
                    yield 'a'
                    yield 'a'
                    pre, Rpre = pres.next()
                    cum, Rcum = cums.next()
                    cpr, Rcpr = cprs.next()

                    def scan(e, pre=pre, lw=lw):
                        for cc in range(4):
                            ins = e.tensor_tensor_scan(out=pre[:, cc, :], data0=onesf[:], data1=lw[:, cc, :], initial=0.0, op0=ALU.mult, op1=ALU.add)
                        return ins
                    op("dve", scan, reads=[Rlw, Ronesf], writes=[Rpre])
                    yield 'a'
                    if d == 0:
                        cum, Rcum = pre, Rpre
                    else:
                        op("dve", lambda e, cum=cum, pre=pre, lw=lw: e.scalar_tensor_tensor(out=cum[:], in0=pre[:], scalar=-1.0, in1=lw[:], op0=ALU.mult, op1=ALU.add),
                           reads=[Rpre, Rlw], writes=[Rcum])
                        op("dve", lambda e, cum=cum, pre=pre: e.tensor_tensor(out=cum[:], in0=cum[:], in1=bc(pre[:, :, 127:128], [128, 4, 128]), op=ALU.add),
                           reads=[Rcum, Rpre], writes=[Rcum])
                    op("pool", lambda e, cpr=cpr, cum=cum, lw=lw: e.tensor_tensor(out=cpr[:], in0=cum[:], in1=lw[:], op=ALU.subtract), reads=[Rcum, Rlw], writes=[Rcpr])
                    ec, Rec = ecs.next()
                    ecp, Recp = ecps.next()
                    ei, Rei = eis.next()
                    op("act", lambda e, ec=ec, cum=cum: e.activation(out=ec[:], in_=cum[:], func=AF.Exp, scale=-LD), reads=[Rcum], writes=[Rec])
                    op("act", lambda e, ecp=ecp, cpr=cpr: e.activation(out=ecp[:], in_=cpr[:], func=AF.Exp, scale=-LD), reads=[Rcpr], writes=[Recp])
                    op("act", lambda e, ei=ei, cum=cum: e.activation(out=ei[:], in_=cum[:], func=AF.Exp, scale=LD), reads=[Rcum], writes=[Rei])
                    yield 'a'
                    gC, RgC = gCs.next()
                    ce = 127 if d == 0 else 0
                    op("dve", lambda e, gC=gC, ec=ec, ce=ce: e.tensor_copy(out=gC[:], in_=ec[:, :, ce]), reads=[Rec], writes=[RgC])
                    yield 'a'
                    kraw, Rkraw = kraws.next()
                    sq, Rsq = sqs.next()
                    nrm, Rnrm = nrms.next()
                    kkv, Rkkv = kkvs.next()
                    tt, Rtt = tts.next()
                    kd, Rkd = kds.next()
                    bv, Rbv = bvs.next()
                    op("dve", lambda e, kraw=kraw, ph=ph: e.tensor_tensor(out=kraw[:], in0=ph[:, 4:8, :], in1=bc(kkc[:].unsqueeze(2), [128, 4, 128]), op=ALU.mult), reads=[Rph, Rkkc], writes=[Rkraw])
                    op("pool", lambda e, sq=sq, kraw=kraw: e.tensor_tensor(out=sq[:], in0=kraw[:], in1=kraw[:], op=ALU.mult), reads=[Rkraw], writes=[Rsq])
                    psn, Rpsn = pbanks.next()
                    op("pe", lambda e, psn=psn, sq=sq: e.matmul(psn[:], lhsT=blk1[:], rhs=sq[:].rearrange("p c t -> p (c t)"), start=True, stop=True), reads=[Rblk1, Rsq], writes=[Rpsn])
                    yield 'a'
                    op("act", lambda e, nrm=nrm, psn=psn: e.activation(out=nrm[:].rearrange("p c t -> p (c t)"), in_=psn[:], func=AF.Ln), reads=[Rpsn], writes=[Rnrm])
                    op("act", lambda e, nrm=nrm: e.activation(out=nrm[:], in_=nrm[:], func=AF.Exp, scale=-0.5), reads=[Rnrm], writes=[Rnrm])
                    op("dve", lambda e, nrm=nrm: e.tensor_scalar_min(out=nrm[:], in0=nrm[:], scalar1=1e12), reads=[Rnrm], writes=[Rnrm])
                    op("dve", lambda e, kkv=kkv, kraw=kraw, nrm=nrm: e.tensor_tensor(out=kkv[:], in0=kraw[:], in1=nrm[:], op=ALU.mult), reads=[Rkraw, Rnrm], writes=[Rkkv])
                    yield 'a'
                    op("pool", lambda e, tt=tt, av=av: e.tensor_tensor(out=tt[:], in0=av[:], in1=bc(kac[:].unsqueeze(2), [128, 4, 128]), op=ALU.mult), reads=[Rav, Rkac], writes=[Rtt])
                    op("pool", lambda e, tt=tt: e.tensor_tensor(out=tt[:], in0=tt[:], in1=bc(omka[:].unsqueeze(2), [128, 4, 128]), op=ALU.add), reads=[Rtt, Romka], writes=[Rtt])
                    op("dve", lambda e, kd=kd, ph=ph, tt=tt: e.tensor_tensor(out=kd[:], in0=ph[:, 4:8, :], in1=tt[:], op=ALU.mult), reads=[Rph, Rtt], writes=[Rkd])
                    yield 'a'
                    op("pool", lambda e, bv=bv, kkv=kkv, av=av: e.tensor_tensor(out=bv[:], in0=kkv[:], in1=av[:], op=ALU.mult), reads=[Rkkv, Rav], writes=[Rbv])
                    yield 'a'
                    AT, RAT = ATs.next()
                    RT, RRT = RTs.next()
                    KT, RKT = KTs.next()
                    BT, RBT = BTs.next()
                    KhT, RKhT = KhTs.next()
                    BhT, RBhT = BhTs.next()
                    vT, RvT = vTs.next()
                    Kf, RKf = Kfs.next()
                    Bf, RBf = Bfs.next()
                    rk, Rrk = rks.next()
                    op("dve", lambda e, AT=AT, kkv=kkv, ecp=ecp: e.scalar_tensor_tensor(out=AT[:], in0=kkv[:], scalar=-1.0, in1=ecp[:], op0=ALU.mult, op1=ALU.mult), reads=[Rkkv, Recp], writes=[RAT])
                    op("dve", lambda e, RT=RT, ph=ph, ec=ec: e.tensor_tensor(out=RT[:], in0=ph[:, 0:4, :], in1=ec[:], op=ALU.mult), reads=[Rph, Rec], writes=[RRT])
                    op("dve", lambda e, Kf=Kf, kd=kd, ei=ei: e.tensor_tensor(out=Kf[:], in0=kd[:], in1=ei[:], op=ALU.mult), reads=[Rkd, Rei], writes=[RKf])
                    yield 'a'
                    op("pool", lambda e, Bf=Bf, bv=bv, ei=ei: e.tensor_tensor(out=Bf[:], in0=bv[:], in1=ei[:], op=ALU.mult), reads=[Rbv, Rei], writes=[RBf])
                    op("act", lambda e, KT=KT, Kf=Kf: e.copy(out=KT[:], in_=Kf[:]), reads=[RKf], writes=[RKT])
                    op("act", lambda e, BT=BT, Bf=Bf: e.copy(out=BT[:], in_=Bf[:]), reads=[RBf], writes=[RBT])
                    yield 'a'
                    op("dve", lambda e, KhT=KhT, Kf=Kf, gC=gC: e.tensor_tensor(out=KhT[:], in0=Kf[:], in1=bc(gC[:].unsqueeze(2), [128, 4, 128]), op=ALU.mult), reads=[RKf, RgC], writes=[RKhT])
                    op("pool", lambda e, BhT=BhT, Bf=Bf, gC=gC: e.tensor_tensor(out=BhT[:], in0=Bf[:], in1=bc(gC[:].unsqueeze(2), [128, 4, 128]), op=ALU.mult), reads=[RBf, RgC], writes=[RBhT])
                    op("act", lambda e, vT=vT, ph=ph: e.copy(out=vT[:], in_=ph[:, 8:12, :]), reads=[Rph], writes=[RvT])
                    op("pool", lambda e, rk=rk, ph=ph, kd=kd: e.tensor_tensor(out=rk[:], in0=ph[:, 0:4, :], in1=kd[:], op=ALU.mult), reads=[Rph, Rkd], writes=[Rrk])
                    op("pool", lambda e, rk=rk: e.tensor_tensor(out=rk[:], in0=rk[:], in1=bc(rkc[:].unsqueeze(2), [128, 4, 128]), op=ALU.mult), reads=[Rrk, Rrkc], writes=[Rrk])
                    yield 'a'
                    ARbd, RARbd = ARbds.next()
                    Bbd, RBbd = Bbds.next()
                    for h2 in range(2):
                        pl = slice(h2 * 64, (h2 + 1) * 64)
                        op("act", lambda e, pl=pl, h2=h2, AT=AT: e.copy(out=ARbd[pl, :, h2, 0, :], in_=AT[pl, :, :]), reads=[RAT, RARbd], writes=[RARbd])
                        op("act", lambda e, pl=pl, h2=h2, RT=RT: e.copy(out=ARbd[pl, :, h2, 1, :], in_=RT[pl, :, :]), reads=[RRT, RARbd], writes=[RARbd])
                        op("act", lambda e, pl=pl, h2=h2, BT=BT: e.copy(out=Bbd[pl, :, h2, :], in_=BT[pl, :, :]), reads=[RBT, RBbd], writes=[RBbd])
                    yield 'a'
                    pt, Rpt = ptr.next()
                    VK, RVK = VKs.next()
                    Vtm, RVtm = VK[:, 0:512], RVK
                    Khat, RKhat = VK[:, 512:1024], RVK
                    Bhat, RBhat = Bhats.next()

                    def tr1(e, pt=pt, vT=vT, KhT=KhT):
                        for cc in range(4):
                            e.transpose(out=pt[:, cc * 128:(cc + 1) * 128], in_=vT[:, cc, :], identity=identb[:])
                        for cc in range(4):
                            ins = e.transpose(out=pt[:, 512 + cc * 128:512 + (cc + 1) * 128], in_=KhT[:, cc, :], identity=identb[:])
                        return ins
                    op("pe", tr1, reads=[RvT, RKhT, Ridb], writes=[Rpt])
                    yield 'a'
                    op("act", lambda e, VK=VK, pt=pt: e.copy(out=VK[:], in_=pt[:]), reads=[Rpt], writes=[RVK])
                    pt2, Rpt2 = ptr.next()

                    def tr2(e, pt2=pt2, BhT=BhT):
                        for cc in range(4):
                            ins = e.transpose(out=pt2[:, cc * 128:(cc + 1) * 128], in_=BhT[:, cc, :], identity=identb[:])
                        return ins
                    op("pe", tr2, reads=[RBhT, Ridb], writes=[Rpt2])
                    yield 'a'
                    op("act", lambda e, Bhat=Bhat, pt2=pt2: e.copy(out=Bhat[:], in_=pt2[:, 0:512]), reads=[Rpt2], writes=[RBhat])
                    psb, Rpsb = pbanks.next()
                    bonv, Rbonv = bons.next()

                    def mmbon(e, psb=psb, rk=rk):
                        for cc in range(4):
                            ins = e.transpose(out=psb[:, cc * 128:(cc + 1) * 128], in_=rk[:, cc, :], identity=identf[:])
                        return ins
                    op("pe", mmbon, reads=[Rrk, Ridf], writes=[Rpsb])
                    op("dve", lambda e, bonv=bonv, psb=psb: e.tensor_reduce(out=bonv[:], in_=psb[:].rearrange("p (h i) -> p h i", i=64), axis=AX.X, op=ALU.add),
                       reads=[Rpsb], writes=[Rbonv])
                    if d == 1:
                        op("act", lambda e, sgd=sgd, ph=ph: e.activation(out=sgd[:], in_=ph[:, 14, :], func=AF.Sigmoid), reads=[Rph], writes=[Rsgd])
                        psg, Rpsg = pbanks.next()
                        gtm, Rgtm = gtms.next()
                        op("pe", lambda e, psg=psg, sgd=sgd: e.matmul(psg[:], lhsT=sgd[:], rhs=gup[:], start=True, stop=True), reads=[Rsgd, Rgup], writes=[Rpsg])
                        op("act", lambda e, gtm=gtm, psg=psg: e.copy(out=gtm[:], in_=psg[:]), reads=[Rpsg], writes=[Rgtm])
                    yield 'B'
                    GK, RGK = GKs.next()
                    GB, RGB = GBs.next()
                    Lh, RLh = Lhs.next()
                    for cc in range(4):
                        p1, Rp1 = pbanks.next()
                        p2, Rp2 = pbanks.next()
                        p3, Rp3 = pbanks.next()

                        def mmg(e, cc=cc, p1=p1, p2=p2, p3=p3, KT=KT, BT=BT, AT=AT):
                            e.matmul(p1[:], lhsT=KT[:, cc, :], rhs=ARbd[:, cc].rearrange("p h x t -> p (h x t)"), start=True, stop=True)
                            e.matmul(p2[:], lhsT=BT[:, cc, :], rhs=ARbd[:, cc].rearrange("p h x t -> p (h x t)"), start=True, stop=True)
                            return e.matmul(p3[:, 0:256], lhsT=AT[:, cc, :], rhs=Bbd[:, cc].rearrange("p h t -> p (h t)"), start=True, stop=True)
                        op("pe", mmg, reads=[RKT, RBT, RAT, RARbd, RBbd], writes=[Rp1, Rp2, Rp3])
                        op("dve", lambda e, cc=cc, p1=p1, GK=GK, d=d: e.tensor_tensor(out=GK[:, cc].rearrange("p h x t -> p (h x t)"), in0=p1[:],
                                                                                 in1=mG[:, d].rearrange("p h x t -> p (h x t)"), op=ALU.mult), reads=[Rp1, RmG], writes=[RGK])
                        op("dve", lambda e, cc=cc, p2=p2, GB=GB, d=d: e.tensor_tensor(out=GB[:, cc].rearrange("p h x t -> p (h x t)"), in0=p2[:],
                                                                                 in1=mG[:, d].rearrange("p h x t -> p (h x t)"), op=ALU.mult), reads=[Rp2, RmG], writes=[RGB])
                        op("dve", lambda e, cc=cc, p3=p3, Lh=Lh, d=d: e.tensor_tensor(out=Lh[:, cc].rearrange("p h t -> p (h t)"), in0=p3[:, 0:256],
                                                                                 in1=mL[:, d].rearrange("p h t -> p (h t)"), op=ALU.mult), reads=[Rp3, RmL], writes=[RLh])
                        yield 'b'
                    L16, RL16 = L16s.next()
                    M16, RM16 = M16s.next()
                    for cc in range(4):
                        op("pool", lambda e, cc=cc, L16=L16, Lh=Lh: e.tensor_tensor(out=L16[:, cc].rearrange("p h t -> p (h t)"), in0=Lh[:, cc].rearrange("p h t -> p (h t)"),
                                                                              in1=bd16[:].rearrange("p h t -> p (h t)"), op=ALU.mult), reads=[RLh, Rbd16], writes=[RL16])
                        op("pool", lambda e, cc=cc, M16=M16, GB=GB: e.tensor_tensor(out=M16[:, cc], in0=GB[:, cc, :, 0, :], in1=bd16[:], op=ALU.mult), reads=[RGB, Rbd16], writes=[RM16])
                    cur = []
                    for h in range(8):
                        cur.append((L16[:, h // 2, h % 2, :], RL16, M16[:, h // 2, h % 2, :], RM16, identb[:], Ridb, identb[:], Ridb))
                    for lev in range(4):
                        for h in range(8):
                            Lp, RLp, Mp, RMp, Qp, RQp, Pp, RPp = cur[h]
                            pq, Rpq = pbanks.next()
                            lt, Rlt = lmq[h][lev % 2]

                            def mmi(e, pq=pq, Lp=Lp, Mp=Mp, Qp=Qp, Pp=Pp, lev=lev):
                                e.matmul(pq[:, 256:384], lhsT=identb[:], rhs=Qp, start=True, stop=False)
                                ins = e.matmul(pq[:, 256:384], lhsT=Lp, rhs=Qp, start=False, stop=True)
                                if lev < 3:
                                    e.matmul(pq[:, 0:128], lhsT=Mp, rhs=Lp, start=True, stop=True)
                                    ins = e.matmul(pq[:, 128:256], lhsT=Lp, rhs=Mp, start=True, stop=True)
                                return ins
                            op("pe", mmi, reads=[RLp, RMp, RQp, Ridb], writes=[Rpq])
                            lo_ = 0 if lev < 3 else 256
                            eng = "act"
                            if eng == "dve":
                                op("dve", lambda e, lt=lt, pq=pq, lo_=lo_: e.tensor_copy(out=lt[:].rearrange("p a t -> p (a t)")[:, lo_:512], in_=pq[:, lo_:512]), reads=[Rpq], writes=[Rlt])
                            else:
                                op("act", lambda e, lt=lt, pq=pq, lo_=lo_: e.copy(out=lt[:].rearrange("p a t -> p (a t)")[:, lo_:384], in_=pq[:, lo_:384]), reads=[Rpq], writes=[Rlt])
                            cur[h] = (lt[:, 0, :], Rlt, lt[:, 1, :], Rlt, lt[:, 2, :], Rlt, lt[:, 3, :], Rlt)
                            if h % 2 == 1:
                                yield 'b'
                    ptq, Rptq = ptr.next()

                    def trq(e, ptq=ptq, cur=list(cur)):
                        for h in range(8):
                            ins = e.transpose(out=ptq[:, h * 128:(h + 1) * 128], in_=cur[h][4], identity=identb[:])
                        return ins
                    op("pe", trq, reads=[cur[h][5] for h in range(8)] + [Ridb], writes=[Rptq])
                    op("act", lambda e, ptq=ptq: e.copy(out=Ptile[:].rearrange("p h t -> p (h t)"), in_=ptq[:]), reads=[Rptq], writes=[RPtile])
                    tcur = [(Ptile[:, h, :], RPtile, cur[h][4], cur[h][5]) for h in range(8)]
                    for lv in range(3):
                        last_lv = (lv == 2)
                        pas = []
                        for pr in range(4):
                            pa, Rpa = pbanks.next()
                            pas.append((pa, Rpa))

                            def mmz(e, pa=pa, pr=pr, tc=list(tcur), last_lv=last_lv, Lh=Lh, GB=GB):
                                for h2 in range(2):
                                    h = 2 * pr + h2
                                    Tk, RTk, TTk, RTTk = tc[h]
                                    ins = e.matmul(pa[:, h2 * 256 + 128:h2 * 256 + 256], lhsT=Lh[:, pr, h2, :], rhs=TTk, start=True, stop=True)
                                    if not last_lv:
                                        ins = e.matmul(pa[:, h2 * 256:h2 * 256 + 128], lhsT=GB[:, pr, h2, 0, :], rhs=Tk, start=True, stop=True)
                                return ins
                            op("pe", mmz, reads=[RLh, RGB, tcur[2 * pr][1], tcur[2 * pr][3], tcur[2 * pr + 1][1], tcur[2 * pr + 1][3]], writes=[Rpa])
                        for pr in range(4):
                            pa, Rpa = pas[pr]
                            zz, Rzz = zzt[pr]
                            if not last_lv:
                                op("dve", lambda e, zz=zz, pa=pa, lv=lv: e.tensor_tensor(out=zz[:].rearrange("p h a t -> p (h a t)"), in0=pa[:],
                                                                                        in1=offm[:, lv].rearrange("p a t -> p (a t)"), op=ALU.mult),
                                   reads=[Rpa, Roffm], writes=[Rzz])
                            else:
                                op("dve", lambda e, zz=zz, pa=pa, lv=lv: e.tensor_tensor(out=zz[:, :, 1, :], in0=pa[:].rearrange("p (h a t) -> p h a t", h=2, a=2)[:, :, 1, :],
                                                                                        in1=offm[:, lv, 0:2, :], op=ALU.mult),
                                   reads=[Rpa, Roffm], writes=[Rzz])
                        yield 'b'
                        pbs = []
                        for pr in range(4):
                            pb_, Rpb_ = pbanks.next()
                            pbs.append((pb_, Rpb_))
                            zz, Rzz = zzt[pr]

                            def mmt(e, pb_=pb_, pr=pr, tc=list(tcur), zz=zz, last_lv=last_lv):
                                for h2 in range(2):
                                    h = 2 * pr + h2
                                    Tk, RTk, TTk, RTTk = tc[h]
                                    c0 = h2 * 256
                                    e.matmul(pb_[:, c0 + 128:c0 + 256], lhsT=identb[:], rhs=TTk, start=True, stop=False)
                                    ins = e.matmul(pb_[:, c0 + 128:c0 + 256], lhsT=Tk, rhs=zz[:, h2, 1, :], start=False, stop=True)
                                    if not last_lv:
                                        e.matmul(pb_[:, c0:c0 + 128], lhsT=identb[:], rhs=Tk, start=True, stop=False)
                                        ins = e.matmul(pb_[:, c0:c0 + 128], lhsT=TTk, rhs=zz[:, h2, 0, :], start=False, stop=True)
                                return ins
                            op("pe", mmt, reads=[Rzz, Ridb, tcur[2 * pr][1], tcur[2 * pr][3], tcur[2 * pr + 1][1], tcur[2 * pr + 1][3]], writes=[Rpb_])
                        for pr in range(4):
                            pb_, Rpb_ = pbs[pr]
                            tn, Rtn = ttt[pr][lv % 2] if not last_lv else tfin[pr].next()
                            eng = "act"
                            if not last_lv:
                                src, dst = pb_[:], tn[:].rearrange("p h a t -> p (h a t)")
                            else:
                                src, dst = pb_[:].rearrange("p (h a t) -> p h a t", h=2, a=2)[:, :, 1, :], tn[:, :, 1, :]
                            if eng == "act":
                                op("act", lambda e, src=src, dst=dst: e.copy(out=dst, in_=src), reads=[Rpb_], writes=[Rtn])
                            else:
                                op("dve", lambda e, src=src, dst=dst: e.tensor_copy(out=dst, in_=src), reads=[Rpb_], writes=[Rtn])
                            for h2 in range(2):
                                tcur[2 * pr + h2] = (tn[:, h2, 0, :], Rtn, tn[:, h2, 1, :], Rtn)
                        yield 'b'
                    cur = [(None, None, None, None, tcur[h][2], tcur[h][3]) for h in range(8)]
                    yield 'C'
                    pW, RpW = pbanks.next()
                    Wsb, RWsb = Wsbs.next()
                    Usb, RUsb = Usbs.next()
                    Ysb, RYsb = Ysbs.next()

                    def mmW(e, pW=pW, AT=AT, GK=GK, Vtm=Vtm):
                        for cc in range(4):
                            e.matmul(pW[:, cc * 128:(cc + 1) * 128], lhsT=AT[:, cc, :], rhs=Sbd[:, cc].rearrange("p h i -> p (h i)"), start=True, stop=False)
                            for h2 in range(2):
                                h = 2 * cc + h2
                                ins = e.matmul(pW[:, h * 64:(h + 1) * 64], lhsT=GK[:, cc, h2, 0, :], rhs=Vtm[:, h * 64:(h + 1) * 64], start=False, stop=True)
                        return ins
                    op("pe", mmW, reads=[RAT, RSbd, RGK, RVtm], writes=[RpW])
                    op("dve", lambda e, Wsb=Wsb, pW=pW: e.tensor_copy(out=Wsb[:], in_=pW[:]), reads=[RpW], writes=[RWsb])
                    yield 'c'
                    pU, RpU = pbanks.next()

                    def mmU(e, pU=pU, Wsb=Wsb, cur=list(cur)):
                        for h in range(8):
                            ins = e.matmul(pU[:, h * 64:(h + 1) * 64], lhsT=cur[h][4], rhs=Wsb[:, h * 64:(h + 1) * 64], start=True, stop=True)
                        return ins
                    op("pe", mmU, reads=[RWsb] + [cur[h][5] for h in range(8)], writes=[RpU])
                    op("dve", lambda e, Usb=Usb, pU=pU: e.tensor_copy(out=Usb[:], in_=pU[:]), reads=[RpU], writes=[RUsb])
                    yield 'c'
                    pY, RpY = pbanks.next()

                    def mmY(e, pY=pY, RT=RT, GK=GK, GB=GB, Vtm=Vtm, Usb=Usb):
                        for cc in range(4):
                            e.matmul(pY[:, cc * 128:(cc + 1) * 128], lhsT=RT[:, cc, :], rhs=Sbd[:, cc].rearrange("p h i -> p (h i)"), start=True, stop=False)
                            for h2 in range(2):
                                h = 2 * cc + h2
                                e.matmul(pY[:, h * 64:(h + 1) * 64], lhsT=GK[:, cc, h2, 1, :], rhs=Vtm[:, h * 64:(h + 1) * 64], start=False, stop=False)
                                ins = e.matmul(pY[:, h * 64:(h + 1) * 64], lhsT=GB[:, cc, h2, 1, :], rhs=Usb[:, h * 64:(h + 1) * 64], start=False, stop=True)
                        return ins
                    op("pe", mmY, reads=[RRT, RSbd, RGK, RGB, RVtm, RUsb], writes=[RpY])
                    op("act", lambda e, Ysb=Ysb, pY=pY: e.copy(out=Ysb[:], in_=pY[:]), reads=[RpY], writes=[RYsb])
                    yield 'c'
                    pS, RpS = pbanks.next()

                    def mmS(e, pS=pS, Khat=Khat, Bhat=Bhat, Vtm=Vtm, Usb=Usb):
                        for cc in range(4):
                            cs = slice(cc * 128, (cc + 1) * 128)
                            e.matmul(pS[:, cs], lhsT=Khat[:, cs], rhs=Vtm[:, cs], start=True, stop=False)
                            ins = e.matmul(pS[:, cs], lhsT=Bhat[:, cs], rhs=Usb[:, cs], start=False, stop=True)
                        return ins
                    op("pe", mmS, reads=[RKhat, RBhat, RVtm, RUsb], writes=[RpS])
                    Sf = St[:].rearrange("p c h i -> p c (h i)")
                    op("dve", lambda e, pS=pS: e.tensor_tensor(out=stmp[:].rearrange("p c h i -> p (c h i)"), in0=pS[:], in1=bdm[:].rearrange("p c h i -> p (c h i)"), op=ALU.mult),
                       reads=[RpS, Rbdm], writes=[Rstmp])
                    op("dve", lambda e, gC=gC: e.tensor_tensor(out=Sf, in0=Sf, in1=bc(gC[:].unsqueeze(2), [128, 4, 128]), op=ALU.mult), reads=[RSt, RgC], writes=[RSt])
                    op("dve", lambda e: e.tensor_tensor(out=St[:].rearrange("p c h i -> p (c h i)"), in0=St[:].rearrange("p c h i -> p (c h i)"),
                                                        in1=stmp[:].rearrange("p c h i -> p (c h i)"), op=ALU.add), reads=[RSt, Rstmp], writes=[RSt])
                    op("act", lambda e: e.copy(out=Sbd[:].rearrange("p c h i -> p (c h i)"), in_=St[:].rearrange("p c h i -> p (c h i)")), reads=[RSt], writes=[RSbd])
                    yield 'c'
                    if d == 0:
                        S.dma("pool", yf[s, t0:t0 + C, :], Ysb[:], reads=[RYsb], writes=[self.dres(("yf", s))])
                        S.dma("pool", bon[s, t0:t0 + C, :], bonv[:], reads=[Rbonv], writes=[self.dres(("bon", s))])
                    else:
                        yfl, Ryfl = yfls.next()
                        bon0, Rbon0 = bon0s.next()
                        S.dma("sp", yfl[:], yf[s, t0:t0 + C, :], reads=[self.dres(("yf", s))], writes=[Ryfl])
                        S.dma("sp", bon0[:], bon[s, t0:t0 + C, :], reads=[self.dres(("bon", s))], writes=[Rbon0])
                        sqv, Rsqv = sqvs.next()
                        yc, Ryc = ycs.next()
                        vv, Rvv = vvs.next()
                        st, Rst = stat.next()
                        yo, Ryo = yos.next()
                        y3 = lambda t_: t_[:].rearrange("p (h i) -> p h i", i=64)
                        b3 = lambda a_: bc(a_.unsqueeze(2), [128, 8, 64])
                        op("dve", lambda e, Ysb=Ysb, yfl=yfl: e.tensor_tensor(out=Ysb[:], in0=Ysb[:], in1=yfl[:], op=ALU.add), reads=[RYsb, Ryfl], writes=[RYsb])
                        op("pool", lambda e, sqv=sqv, Ysb=Ysb: e.tensor_tensor(out=sqv[:], in0=Ysb[:], in1=Ysb[:], op=ALU.mult), reads=[RYsb], writes=[Rsqv])
                        op("dve", lambda e, st=st, Ysb=Ysb: e.tensor_reduce(out=st[:, 0, :], in_=y3(Ysb), axis=AX.X, op=ALU.add), reads=[RYsb], writes=[Rst])
                        op("dve", lambda e, st=st, sqv=sqv: e.tensor_reduce(out=st[:, 1, :], in_=y3(sqv), axis=AX.X, op=ALU.add), reads=[Rsqv, Rst], writes=[Rst])
                        op("dve", lambda e, st=st: e.tensor_scalar(out=st[:, 2, :], in0=st[:, 0, :], scalar1=1.0 / 64, scalar2=None, op0=ALU.mult), reads=[Rst], writes=[Rst])
                        op("dve", lambda e, st=st: e.tensor_tensor(out=st[:, 3, :], in0=st[:, 2, :], in1=st[:, 2, :], op=ALU.mult), reads=[Rst], writes=[Rst])
                        op("dve", lambda e, st=st: e.scalar_tensor_tensor(out=st[:, 4, :], in0=st[:, 1, :], scalar=1.0 / 64, in1=st[:, 3, :], op0=ALU.mult, op1=ALU.subtract),
                           reads=[Rst], writes=[Rst])
                        op("act", lambda e, st=st: e.activation(out=st[:, 5, :], in_=st[:, 4, :], func=AF.Sqrt, bias=gne[:, 0:1]), reads=[Rst, Rgne], writes=[Rst])
                        op("dve", lambda e, st=st: e.reciprocal(out=st[:, 5, :], in_=st[:, 5, :]), reads=[Rst], writes=[Rst])
                        yield 'c'
                        op("dve", lambda e, yc=yc, Ysb=Ysb, st=st: e.tensor_tensor(out=y3(yc), in0=y3(Ysb), in1=b3(st[:, 2, :]), op=ALU.subtract), reads=[RYsb, Rst], writes=[Ryc])
                        op("dve", lambda e, yc=yc, st=st: e.tensor_tensor(out=y3(yc), in0=y3(yc), in1=b3(st[:, 5, :]), op=ALU.mult), reads=[Ryc, Rst], writes=[Ryc])
                        op("pool", lambda e, yc=yc: e.tensor_tensor(out=yc[:], in0=yc[:], in1=gng[:], op=ALU.mult), reads=[Ryc, Rgng], writes=[Ryc])
                        op("pool", lambda e, yc=yc: e.tensor_tensor(out=yc[:], in0=yc[:], in1=gnb[:], op=ALU.add), reads=[Ryc, Rgnb], writes=[Ryc])
                        yield 'c'
                        op("dve", lambda e, bonv=bonv, bon0=bon0: e.tensor_tensor(out=bonv[:], in0=bonv[:], in1=bon0[:], op=ALU.add), reads=[Rbonv, Rbon0], writes=[Rbonv])
                        op("dve", lambda e, vv=vv, Vtm=Vtm, bonv=bonv: e.tensor_tensor(out=y3(vv), in0=Vtm.rearrange("p (h i) -> p h i", i=64), in1=b3(bonv[:]), op=ALU.mult), reads=[RVtm, Rbonv], writes=[Rvv])
                        op("pool", lambda e, yc=yc, vv=vv: e.tensor_tensor(out=yc[:], in0=yc[:], in1=vv[:], op=ALU.add), reads=[Ryc, Rvv], writes=[Ryc])
                        op("dve", lambda e, yo=yo, yc=yc, gtm=gtm: e.tensor_tensor(out=yo[:], in0=yc[:], in1=gtm[:], op=ALU.mult), reads=[Ryc, Rgtm], writes=[Ryo])
                        S.dma("pool", yr[s, t0:t0 + C, :], yo[:], reads=[Ryo], writes=[self.dres(("yr", s))])

                def run_until(g, tags):
                    while True:
                        try:
                            t_ = next(g)
                        except StopIteration:
                            return None
                        if t_ in tags:
                            return t_
                order = list(order)
                gens = [body(ci) for ci in order]
                run_until(gens[0], ('B',))
                prev = None
                for i_ in range(len(gens)):
                    g_cur = gens[i_]
                    g_nxt = gens[i_ + 1] if i_ + 1 < len(gens) else None
                    cur_done = False
                    if prev is not None:
                        prev_done = False
                        while not prev_done:
                            if run_until(prev, ('c',)) is None:
                                prev_done = True
                            if not cur_done and run_until(g_cur, ('b', 'C')) == 'C':
                                cur_done = True
                    nxt_done = g_nxt is None
                    while not (cur_done and nxt_done):
                        if not cur_done and run_until(g_cur, ('b', 'C')) == 'C':
                            cur_done = True
                        if not nxt_done and run_until(g_nxt, ('a', 'B')) == 'B':
                            nxt_done = True
                    prev = g_cur
                run_until(prev, ())
        self.end()

    def phase_p3(self, l):
        S, T, NS = self.S, self.T, self.NS
        self.begin()
        rows = T // 64
        nblk = T // 128
        nq, nk, nv, yn = (self.scr[k] for k in ("nq", "nk", "nv", "yn"))
        rp, Rrp = self.tile("rp", [120, 31], F32)
        S.dma("sp", rp[:], self.w["rpb"][l].rearrange("h a b -> (h a) b"), writes=[Rrp])
        _, _, idf, Ridf = self.make_ident()
        Rt, RRt = self.tile("Rt", [31, 8, 15], F32)
        Jm, RJ = self.tile("Jm", [31, 160], F32)
        tps = self.rot("tps", [128, 512], F32, 4, ps=True)
        tp0, Rtp0 = tps.next()
        S.op("pe", lambda e: e.transpose(out=tp0[0:31, 0:120], in_=rp[:, :], identity=idf[0:120, 0:120]), reads=[Rrp, Ridf], writes=[Rtp0])
        S.op("dve", lambda e: e.tensor_copy(out=Rt[:].rearrange("p h a -> p (h a)"), in_=tp0[0:31, 0:120]), reads=[Rtp0], writes=[RRt])

        S.op("pool", lambda e: e.memset(Jm[:], 0.0), writes=[RJ])
        S.op("pool", lambda e: e.affine_select(out=Jm[:], in_=Jm[:], pattern=[[-1, 160]], compare_op=ALU.not_equal, fill=1.0, base=48, channel_multiplier=1),
             reads=[RJ], writes=[RJ])
        TE0, RTE0 = self.tile("TE0", [64, 8, 15, 64], BF16)
        for q0 in range(0, 64, 4):
            tp, Rtp = tps.next()

            def mmT(e, tp=tp, q0=q0):
                for qi in range(4):
                    qc = q0 + qi
                    ins = e.matmul(tp[0:64, qi * 120:(qi + 1) * 120], lhsT=Jm[:, 63 - qc:127 - qc], rhs=Rt[:].rearrange("p h a -> p (h a)"),
                                   start=True, stop=True)
                return ins
            S.op("pe", mmT, reads=[RJ, RRt], writes=[Rtp])
            S.op("act", lambda e, tp=tp, q0=q0: e.activation(out=TE0[:, :, :, q0:q0 + 4].rearrange("p h a q -> p (h a) q"),
                                                             in_=tp[0:64, 0:480].rearrange("p (q x) -> p x q", q=4), func=AF.Exp),
                 reads=[Rtp], writes=[RTE0])
        A, RA = self.tile("mA", [128, 64], F32)
        Q, RQ = self.tile("mQ", [128, 64], F32)
        Q2, RQ2 = self.tile("mQ2", [128, 64], F32)
        cm, Rcm = self.tile("cm", [128, 64], F32)

        def io(e):
            e.iota(A[0:64, :], pattern=[[-1, 64]], base=0, channel_multiplier=1, allow_small_or_imprecise_dtypes=True)
            e.iota(A[64:128, :], pattern=[[-1, 64]], base=0, channel_multiplier=1, allow_small_or_imprecise_dtypes=True)
            return e.iota(Q[:], pattern=[[1, 64]], base=0, channel_multiplier=0, allow_small_or_imprecise_dtypes=True)
        S.op("pool", io, writes=[RA, RQ])
        S.op("dve", lambda e: e.tensor_scalar(out=Q2[:], in0=Q[:], scalar1=8.0, scalar2=56.0, op0=ALU.max, op1=ALU.min), reads=[RQ], writes=[RQ2])
        S.op("dve", lambda e: e.tensor_tensor(out=Q[:], in0=Q[:], in1=Q2[:], op=ALU.subtract), reads=[RQ, RQ2], writes=[RQ])
        S.op("dve", lambda e: e.tensor_tensor(out=A[:], in0=A[:], in1=Q[:], op=ALU.add), reads=[RA, RQ], writes=[RA])
        S.op("dve", lambda e: e.tensor_single_scalar(out=Q[:], in_=A[:], scalar=-8.0, op=ALU.is_ge), reads=[RA], writes=[RQ])
        S.op("dve", lambda e: e.tensor_single_scalar(out=Q2[:], in_=A[:], scalar=7.0, op=ALU.is_le), reads=[RA], writes=[RQ2])
        S.op("dve", lambda e: e.tensor_tensor(out=cm[:], in0=Q[:], in1=Q2[:], op=ALU.mult), reads=[RQ, RQ2], writes=[Rcm])
        TE, RTE = self.tile("TE", [128, 8, 14, 64], BF16)
        S.op("dve", lambda e: e.tensor_tensor(out=TE0[:].rearrange("p h a q -> p (h a) q"), in0=TE0[:].rearrange("p h a q -> p (h a) q"),
                                              in1=cm[0:64, :].unsqueeze(1).broadcast_to([64, 120, 64]), op=ALU.mult),
             reads=[RTE0, Rcm], writes=[RTE0])
        S.op("pool", lambda e: e.tensor_copy(out=TE[0:64, :, :, :].rearrange("p h a q -> p h (a q)"),
                                             in_=TE0[:, :, 0:14, :].rearrange("p h a q -> p h (a q)")), reads=[RTE0], writes=[RTE])
        S.dma("sp", TE[64:128, :, :, :].rearrange("p h a q -> p h (a q)"), TE0[:, :, 1:15, :].rearrange("p h a q -> p h (a q)"),
              reads=[RTE0], writes=[RTE])
        qT, RqT = self.tile("nqT", [128, 4, T], BF16)
        kT, RkT = self.tile("nkT", [128, 4, T], BF16)
        Ve, RVe = self.tile("nVe", [128, nblk, 8, 65], BF16)
        Vo, RVo = self.tile("nVo", [128, nblk, 8, 65], BF16)
        Vs, RVs = self.tile("nVs", [128, nblk // 2, 512], BF16)
        S.op("pool", lambda e: e.memset(Ve[:], 1.0), writes=[RVe])
        S.op("pool", lambda e: e.memset(Vo[:], 1.0), writes=[RVo])
        pss = Rot([(t_[:].rearrange("p (h q) -> p h q", q=64), r_) for (t_, r_) in tps.items])
        pos_raw = self.rot("ops", [128, 512], F32, 4, ps=True)
        Ets = self.rot("Et", [128, 8, 64], BF16, 3)
        Es = self.rot("E", [128, 4, 8, 64], BF16, 2)
        recs = self.rot("rec", [64, 8], F32, 2)
        outs = self.rot("yno", [64, 8, 64], BF16, 3)
        for s in range(NS):
            S.dma("sp", qT[:], nq[s].rearrange("(c p) t -> p c t", p=128), reads=[self.dres(("nq", s))], writes=[RqT])
            S.dma("sp", kT[:], nk[s].rearrange("(c p) t -> p c t", p=128), reads=[self.dres(("nk", s))], writes=[RkT])
            hb = nblk // 2
            for (Vx, RVx, off, nb_tot) in ((Ve, RVe, 0, nblk), (Vo, RVo, 64, nblk - 1)):
                for b0 in range(0, nb_tot, hb):
                    nb_ = min(hb, nb_tot - b0)
                    S.dma("sp", Vs[:, 0:nb_, :], nv[s, off + b0 * 128:off + (b0 + nb_) * 128, :].rearrange("(b p) c -> p b c", p=128),
                          reads=[self.dres(("nv", s))], writes=[RVs])
                    S.op("pool", lambda e, Vx=Vx, b0=b0, nb_=nb_: e.tensor_copy(
                        out=Vx[:, b0:b0 + nb_, :, 0:64].rearrange("p b h n -> p (b h) n"),
                        in_=Vs[:, 0:nb_, :].rearrange("p b (h n) -> p (b h) n", n=64)), reads=[RVs], writes=[RVx])
            pending = None
            for i in range(rows):
                rs = min(max(i - 4, 0), rows - 8)
                E, RE = Es.next()
                for blk in range(4):
                    kr = rs + 2 * blk
                    tok0 = kr * 64
                    dib = kr - i + 7
                    psA, RpsA = pss.next()
                    psB, RpsB = pss.next()
                    Et, REt = Ets.next()

                    def mms(e, psA=psA, psB=psB, tok0=tok0, i=i):
                        for h in range(8):
                            pb = (h % 2) * 64
                            ps = psA if h % 2 == 0 else psB
                            ins = e.matmul(ps[:, h // 2, :], lhsT=kT[pb:pb + 64, h // 2, tok0:tok0 + 128], rhs=qT[pb:pb + 64, h // 2, i * 64:(i + 1) * 64],
                                           start=True, stop=True)
                        return ins
                    S.op("pe", mms, reads=[RkT, RqT], writes=[RpsA, RpsB])
                    S.op("act", lambda e, psA=psA, Et=Et: e.activation(out=Et[:, 0:4, :], in_=psA[:, 0:4, :], func=AF.Exp, scale=0.125), reads=[RpsA], writes=[REt])
                    S.op("act", lambda e, psB=psB, Et=Et: e.activation(out=Et[:, 4:8, :], in_=psB[:, 0:4, :], func=AF.Exp, scale=0.125), reads=[RpsB], writes=[REt])
                    S.op("dve", lambda e, Et=Et, E=E, blk=blk, dib=dib: e.tensor_tensor(
                        out=E[:, blk, :, :].rearrange("p (two c) q -> p two c q", two=2), in0=Et[:].rearrange("p (two c) q -> p two c q", two=2),
                        in1=TE[:].rearrange("p (c two) a q -> p two c a q", two=2)[:, :, :, dib, :], op=ALU.mult),
                         reads=[REt, RTE], writes=[RE])
                def finish(i=i, rs=rs, E=E, RE=RE, s=s):
                    poA_, RpoA = pos_raw.next()
                    poB_, RpoB = pos_raw.next()
                    poA = poA_[0:64, 0:260].rearrange("p (h n) -> p h n", n=65)
                    poB = poB_[0:64, 0:260].rearrange("p (h n) -> p h n", n=65)

                    def mmo(e, E=E, rs=rs, poA=poA, poB=poB):
                        for h in range(8):
                            po = poA if h < 4 else poB
                            for blk in range(4):
                                kr = rs + 2 * blk
                                Vx = Ve if kr % 2 == 0 else Vo
                                ins = e.matmul(po[:, h % 4, :], lhsT=E[:, blk, (h % 2) * 4 + h // 2, :], rhs=Vx[:, kr // 2, h, :], start=(blk == 0), stop=(blk == 3))
                        return ins
                    S.op("pe", mmo, reads=[RE, RVe, RVo], writes=[RpoA, RpoB])
                    rec, Rrec = recs.next()
                    o, Ro = outs.next()
                    S.op("dve", lambda e, rec=rec, poA=poA: e.reciprocal(out=rec[:, 0:4], in_=poA[:, :, 64]), reads=[RpoA], writes=[Rrec])
                    S.op("dve", lambda e, rec=rec, poB=poB: e.reciprocal(out=rec[:, 4:8], in_=poB[:, :, 64]), reads=[RpoB], writes=[Rrec])
                    S.op("dve", lambda e, rec=rec, poA=poA, o=o: e.tensor_tensor(out=o[:, 0:4, :], in0=poA[:, :, 0:64],
                                                                                in1=rec[:, 0:4].unsqueeze(2).broadcast_to([64, 4, 64]), op=ALU.mult),
                         reads=[RpoA, Rrec], writes=[Ro])
                    S.op("dve", lambda e, rec=rec, poB=poB, o=o: e.tensor_tensor(out=o[:, 4:8, :], in0=poB[:, :, 0:64],
                                                                                in1=rec[:, 4:8].unsqueeze(2).broadcast_to([64, 4, 64]), op=ALU.mult),
                         reads=[RpoB, Rrec], writes=[Ro])
                    S.dma("pool", yn[s, i * 64:(i + 1) * 64, :], o[:].rearrange("p h n -> p (h n)"), reads=[Ro], writes=[self.dres(("yn", s))])
                if pending is not None:
                    pending()
                pending = finish
            pending()
            pending = None
        self.end()

    def xupdate(self, xt, Rxt, nb, aT, RaT, nkc, W, RW, pss):
        S = self.S
        for b in range(nb):
            for half in range(2):
                ps, Rps = pss.next()

                def mm(e, ps=ps, b=b, half=half):
                    for kc in range(nkc):
                        ins = e.matmul(ps[:], lhsT=aT[:, kc, b * 128:(b + 1) * 128], rhs=W[:, kc, half * 512:(half + 1) * 512],
                                       start=(kc == 0), stop=(kc == nkc - 1))
                    return ins
                S.op("pe", mm, reads=[RaT, RW], writes=[Rps])
                S.op("dve", lambda e, ps=ps, b=b, half=half: e.tensor_tensor(
                    out=xt[:, b, half * 512:(half + 1) * 512], in0=xt[:, b, half * 512:(half + 1) * 512], in1=ps[:], op=ALU.add),
                    reads=[Rps, Rxt], writes=[Rxt])

    def phase_p4(self, l):
        S, T, NS = self.S, self.T, self.NS
        self.begin()
        TT = 512
        wbr, Rwbr = self.tile("wbr", [128, 8, D], BF16)
        wout, Rwout = self.tile("wout", [128, 8, D], BF16)
        stg = self.rot("wstg", [128, 2048], F32, 2)
        ident, Rid, _, _ = self.make_ident()
        self.load_weight(wbr, Rwbr, self.w["w_br_rwkv"][l], 512, D, stg, kc0=0)
        self.load_weight(wbr, Rwbr, self.w["w_br_nat"][l], 512, D, stg, kc0=4)
        self.load_weight(wout, Rwout, self.w["w_out"][l], D, D, stg)
        xts = self.rot("xt", [128, 4, D], F32, 2)
        yts = self.rot("yt", [128, 4, 1024], BF16, 2)
        yTs = self.rot("yT", [128, 8, TT], BF16, 2)
        sgts = self.rot("sgt", [128, 16, TT], BF16, 2)
        mTs = self.rot("mT", [128, 8, TT], BF16, 2)
        m1s = self.rot("m1", [128, TT], F32, 2)
        m2s = self.rot("m2", [128, TT], F32, 2)
        pTs = self.rot("pT", [128, 1024], BF16, 2, ps=True)
        pss = self.rot("ps", [128, 512], F32, 6, ps=True)
        xsrc = self.x_in if l == 0 else self.scr["xres"]
        xres, yr, yn, sg = (self.scr[k] for k in ("xres", "yr", "yn", "sg"))
        for s in range(NS):
            for t0 in range(0, T, TT):
                xt, Rxt = xts.next()
                yt, Ryt = yts.next()
                yT, RyT = yTs.next()
                sgt, Rsgt = sgts.next()
                mT, RmT = mTs.next()
                Rx = self.dres(("xres", s, t0))
                S.dma("sp", yt[:, :, 0:512], yr[s, t0:t0 + TT, :].rearrange("(b p) c -> p b c", p=128), reads=[self.dres(("yr", s))], writes=[Ryt])
                S.dma("sp", yt[:, :, 512:1024], yn[s, t0:t0 + TT, :].rearrange("(b p) c -> p b c", p=128), reads=[self.dres(("yn", s))], writes=[Ryt])
                S.dma("sp", sgt[:], sg[s, :, t0:t0 + TT].rearrange("(c p) t -> p c t", p=128), reads=[self.dres(("sg", s))], writes=[Rsgt])
                S.dma("sp", xt[:], xsrc[s, t0:t0 + TT, :].rearrange("(b p) d -> p b d", p=128), reads=([Rx] if l > 0 else []), writes=[Rxt])
                for b in range(4):
                    pT, RpT = pTs.next()

                    def tr(e, pT=pT, b=b, yt=yt):
                        for c in range(8):
                            ins = e.transpose(out=pT[:, c * 128:(c + 1) * 128], in_=yt[:, b, c * 128:(c + 1) * 128], identity=ident[:])
                        return ins
                    S.op("pe", tr, reads=[Ryt, Rid], writes=[RpT])
                    S.op("act", lambda e, pT=pT, b=b, yT=yT: e.copy(out=yT[:, :, b * 128:(b + 1) * 128], in_=pT[:].rearrange("p (k t) -> p k t", k=8)),
                         reads=[RpT], writes=[RyT])
                for oc in range(8):
                    ps1, Rps1 = pss.next()
                    ps2, Rps2 = pss.next()
                    m1, Rm1 = m1s.next()
                    m2, Rm2 = m2s.next()

                    def mm(e, ps1=ps1, ps2=ps2, oc=oc, yT=yT):
                        for kc in range(4):
                            e.matmul(ps1[:], lhsT=wbr[:, kc, oc * 128:(oc + 1) * 128], rhs=yT[:, kc, :], start=(kc == 0), stop=(kc == 3))
                        for kc in range(4):
                            ins = e.matmul(ps2[:], lhsT=wbr[:, 4 + kc, oc * 128:(oc + 1) * 128], rhs=yT[:, 4 + kc, :], start=(kc == 0), stop=(kc == 3))
                        return ins
                    S.op("pe", mm, reads=[Rwbr, RyT], writes=[Rps1, Rps2])
                    S.op("dve", lambda e, m1=m1, ps1=ps1, oc=oc, sgt=sgt: e.tensor_tensor(out=m1[:], in0=ps1[:], in1=sgt[:, oc, :], op=ALU.mult),
                         reads=[Rps1, Rsgt], writes=[Rm1])
                    S.op("dve", lambda e, m2=m2, ps2=ps2, oc=oc, sgt=sgt: e.tensor_tensor(out=m2[:], in0=ps2[:], in1=sgt[:, 8 + oc, :], op=ALU.mult),
                         reads=[Rps2, Rsgt], writes=[Rm2])
                    S.op("pool", lambda e, m1=m1, m2=m2, oc=oc, mT=mT: e.tensor_tensor(out=mT[:, oc, :], in0=m1[:], in1=m2[:], op=ALU.add),
                         reads=[Rm1, Rm2], writes=[RmT])
                self.xupdate(xt, Rxt, 4, mT, RmT, 8, wout, Rwout, pss)
                S.dma("pool", xres[s, t0:t0 + TT, :].rearrange("(b p) d -> p b d", p=128), xt[:], reads=[Rxt], writes=[Rx])
        self.end()

    def phase_p5(self, l):
        S, T, NS = self.S, self.T, self.NS
        self.begin()
        TT = 512
        wq, Rwq = self.tile("wq", [128, 8, D], BF16)
        wo, Rwo = self.tile("wo", [128, 8, D], BF16)
        wkv, Rwkv = self.tile("wkv", [128, 8, 2 * D], BF16)
        stg = self.rot("wstg", [128, 2048], F32, 2)
        ident, Rid, _, _ = self.make_ident()
        gb, Rgb = self.tile("gb", [128, D], F32)
        gm, Rgm = self.tile("gm", [128, D], F32)
        ones, Rones = self.tile("ones", [128, 128], BF16)
        S.op("pool", lambda e: e.memset(ones[:], 1.0), writes=[Rones])
        self.load_bcast(gb, Rgb, self.w["norm_x"][l], D)
        self.load_bcast(gm, Rgm, self.w["norm_mem"][l], D)
        self.load_weight(wkv, Rwkv, self.w["w_xkv"][l], D, 2 * D, stg)
        self.load_weight(wq, Rwq, self.w["w_xq"][l], D, D, stg)
        self.load_weight(wo, Rwo, self.w["w_xo"][l], D, D, stg)
        tmp = self.norm_tmp()
        xts = self.rot("xt", [128, 4, D], F32, 2)
        hTs = self.rot("hT", [128, 8, TT], BF16, 2)
        qTs = self.rot("qT", [128, 8, TT], BF16, 1)
        oTs = self.rot("oT", [128, 8, TT], BF16, 2)
        Es = self.rot("E", [128, 2, TT], BF16, 2)
        rdens = self.rot("rden", [128, TT], F32, 2)
        mt, Rmt = self.tile("memt", [128, 2, D], F32)
        mnT, RmnT = self.tile("memnT", [128, 8, NMEM], BF16)
        kT, RkT = self.tile("kT", [128, 8, NMEM], BF16)
        Vm, RVm = self.tile("Vm", [128, 2, D], BF16)
        pss = self.rot("ps", [128, 512], F32, 6, ps=True)
        xres = self.scr["xres"]
        for s in range(NS):
            S.dma("sp", mt[:], self.mem_in[s].rearrange("(b p) d -> p b d", p=128), writes=[Rmt])
            for b in range(2):
                self.norm_to_hT(mt, Rmt, b, gm, Rgm, mnT, RmnT, b * 128, ident, Rid, tmp)
            for cc in range(8):
                ps, Rps = pss.next()

                def mmk(e, ps=ps, cc=cc):
                    for kc in range(8):
                        ins = e.matmul(ps[:, 0:NMEM], lhsT=wkv[:, kc, cc * 128:(cc + 1) * 128], rhs=mnT[:, kc, :], start=(kc == 0), stop=(kc == 7))
                    return ins
                S.op("pe", mmk, reads=[Rwkv, RmnT], writes=[Rps])
                S.op("dve", lambda e, ps=ps, cc=cc: e.tensor_copy(out=kT[:, cc, :], in_=ps[:, 0:NMEM]), reads=[Rps], writes=[RkT])
            for mb in range(2):
                for half in range(2):
                    ps, Rps = pss.next()

                    def mmv(e, ps=ps, mb=mb, half=half):
                        for kc in range(8):
                            ins = e.matmul(ps[:], lhsT=mnT[:, kc, mb * 128:(mb + 1) * 128], rhs=wkv[:, kc, D + half * 512:D + (half + 1) * 512],
                                           start=(kc == 0), stop=(kc == 7))
                        return ins
                    S.op("pe", mmv, reads=[Rwkv, RmnT], writes=[Rps])
                    S.op("dve", lambda e, ps=ps, mb=mb, half=half: e.tensor_copy(out=Vm[:, mb, half * 512:(half + 1) * 512], in_=ps[:]),
                         reads=[Rps], writes=[RVm])
            for t0 in range(0, T, TT):
                xt, Rxt = xts.next()
                hT, RhT = hTs.next()
                qT, RqT = qTs.next()
                oT, RoT = oTs.next()
                Rx = self.dres(("xres", s, t0))
                S.dma("sp", xt[:], xres[s, t0:t0 + TT, :].rearrange("(b p) d -> p b d", p=128), reads=[Rx], writes=[Rxt])
                for b in range(4):
                    self.norm_to_hT(xt, Rxt, b, gb, Rgb, hT, RhT, b * 128, ident, Rid, tmp)
                for cc in range(8):
                    ps, Rps = pss.next()

                    def mmq(e, ps=ps, cc=cc, hT=hT):
                        for kc in range(8):
                            ins = e.matmul(ps[:], lhsT=wq[:, kc, cc * 128:(cc + 1) * 128], rhs=hT[:, kc, :], start=(kc == 0), stop=(kc == 7))
                        return ins
                    S.op("pe", mmq, reads=[Rwq, RhT], writes=[Rps])
                    S.op("dve", lambda e, ps=ps, cc=cc, qT=qT: e.tensor_copy(out=qT[:, cc, :], in_=ps[:]), reads=[Rps], writes=[RqT])
                for hd in range(4):
                    E, RE = Es.next()
                    rden, Rrden = rdens.next()
                    for mb in range(2):
                        ps, Rps = pss.next()

                        def mms(e, ps=ps, mb=mb, hd=hd, qT=qT):
                            for j in range(2):
                                ins = e.matmul(ps[:], lhsT=kT[:, 2 * hd + j, mb * 128:(mb + 1) * 128], rhs=qT[:, 2 * hd + j, :], start=(j == 0), stop=(j == 1))
                            return ins
                        S.op("pe", mms, reads=[RkT, RqT], writes=[Rps])
                        S.op("act", lambda e, ps=ps, mb=mb, E=E: e.activation(out=E[:, mb, :], in_=ps[:], func=AF.Exp, scale=1.0 / 16.0),
                             reads=[Rps], writes=[RE])
                    ps, Rps = pss.next()

                    def mmd(e, ps=ps, E=E):
                        for mb in range(2):
                            ins = e.matmul(ps[:], lhsT=ones[:], rhs=E[:, mb, :], start=(mb == 0), stop=(mb == 1))
                        return ins
                    S.op("pe", mmd, reads=[Rones, RE], writes=[Rps])
                    S.op("dve", lambda e, ps=ps, rden=rden: e.reciprocal(out=rden[:], in_=ps[:]), reads=[Rps], writes=[Rrden])
                    for j in range(2):
                        ps, Rps = pss.next()

                        def mmo(e, ps=ps, hd=hd, j=j, E=E):
                            for mb in range(2):
                                c0 = hd * 256 + j * 128
                                ins = e.matmul(ps[:], lhsT=Vm[:, mb, c0:c0 + 128], rhs=E[:, mb, :], start=(mb == 0), stop=(mb == 1))
                            return ins
                        S.op("pe", mmo, reads=[RVm, RE], writes=[Rps])
                        S.op("dve", lambda e, ps=ps, hd=hd, j=j, oT=oT, rden=rden: e.tensor_tensor(out=oT[:, 2 * hd + j, :], in0=ps[:], in1=rden[:], op=ALU.mult),
                             reads=[Rps, Rrden], writes=[RoT])
                self.xupdate(xt, Rxt, 4, oT, RoT, 8, wo, Rwo, pss)
                S.dma("pool", xres[s, t0:t0 + TT, :].rearrange("(b p) d -> p b d", p=128), xt[:], reads=[Rxt], writes=[Rx])
        self.end()

    def phase_p6(self, l):
        S, T, NS = self.S, self.T, self.NS
        self.begin()
        TT = 256
        NB = TT // 128
        last = (l == self.DEPTH - 1)
        w1, Rw1 = self.tile("w1", [128, 8, DFF], BF16)
        w2, Rw2 = self.tile("w2", [128, 32, D], BF16)
        stg = self.rot("wstg", [128, 2048], F32, 2)
        ident, Rid, _, _ = self.make_ident()
        gb, Rgb = self.tile("gb", [128, D], F32)
        self.load_bcast(gb, Rgb, self.w["norm_ff"][l], D)
        if last:
            gf, Rgf = self.tile("gf", [128, D], F32)
            self.load_bcast(gf, Rgf, self.w["norm_final"], D)
        self.load_weight(w1, Rw1, self.w["w_ff1"][l], D, DFF, stg)
        self.load_weight(w2, Rw2, self.w["w_ff2"][l], DFF, D, stg)
        tmp = self.norm_tmp()
        xts = self.rot("xt", [128, NB, D], F32, 2)
        hTs = self.rot("hT", [128, 8, TT], BF16, 2)
        uTs = self.rot("uT", [128, 32, TT], BF16, 1)
        rls = self.rot("rl", [128, TT], BF16, 3)
        pss = self.rot("ps", [128, 512], F32, 6, ps=True)
        xres = self.scr["xres"]
        for s in range(NS):
            for t0 in range(0, T, TT):
                xt, Rxt = xts.next()
                hT, RhT = hTs.next()
                uT, RuT = uTs.next()
                Rx = self.dres(("xres", s, t0 // 512 * 512))
                S.dma("sp", xt[:], xres[s, t0:t0 + TT, :].rearrange("(b p) d -> p b d", p=128), reads=[Rx], writes=[Rxt])
                for b in range(NB):
                    self.norm_to_hT(xt, Rxt, b, gb, Rgb, hT, RhT, b * 128, ident, Rid, tmp)
                for fc in range(32):
                    ps, Rps = pss.next()
                    rl, Rrl = rls.next()

                    def mm1(e, ps=ps, fc=fc, hT=hT):
                        for kc in range(8):
                            ins = e.matmul(ps[:, 0:TT], lhsT=w1[:, kc, fc * 128:(fc + 1) * 128], rhs=hT[:, kc, :], start=(kc == 0), stop=(kc == 7))
                        return ins
                    S.op("pe", mm1, reads=[Rw1, RhT], writes=[Rps])
                    S.op("act", lambda e, ps=ps, rl=rl: e.activation(out=rl[:], in_=ps[:, 0:TT], func=AF.Relu), reads=[Rps], writes=[Rrl])
                    S.op("pool", lambda e, rl=rl, fc=fc, uT=uT: e.tensor_tensor(out=uT[:, fc, :], in0=rl[:], in1=rl[:], op=ALU.mult),
                         reads=[Rrl], writes=[RuT])
                self.xupdate(xt, Rxt, NB, uT, RuT, 32, w2, Rw2, pss)
                if not last:
                    S.dma("pool", xres[s, t0:t0 + TT, :].rearrange("(b p) d -> p b d", p=128), xt[:], reads=[Rxt], writes=[Rx])
                else:
                    for b in range(NB):
                        junk, Rjunk = tmp["junk"].next()
                        ss, Rss = tmp["ss"].next()
                        S.op("act", lambda e, junk=junk, ss=ss, b=b, xt=xt: e.activation(out=junk[:], in_=xt[:, b, :], func=AF.Square, accum_out=ss[:, 0:1]),
                             reads=[Rxt], writes=[Rjunk, Rss])
                        S.op("act", lambda e, ss=ss: e.activation(out=ss[:, 1:2], in_=ss[:, 0:1], func=AF.Sqrt, scale=1.0 / D, bias=self.eps_t[:, 0:1]),
                             reads=[Rss, self.Reps], writes=[Rss])
                        S.op("dve", lambda e, ss=ss: e.reciprocal(out=ss[:, 2:3], in_=ss[:, 1:2]), reads=[Rss], writes=[Rss])
                        S.op("dve", lambda e, ss=ss, b=b, xt=xt: e.scalar_tensor_tensor(out=xt[:, b, :], in0=xt[:, b, :], scalar=ss[:, 2:3], in1=gf[:],
                                                                                         op0=ALU.mult, op1=ALU.mult),
                             reads=[Rxt, Rss, Rgf], writes=[Rxt])
                    Ry = self.dres(("y", s))
                    S.dma("pool", self.y_out[s, t0:t0 + TT, :].rearrange("(b p) d -> p b d", p=128), xt[:], reads=[Rxt], writes=[Ry])
        self.end()


_CACHE = {}


def _get_nc(T, NS, DEPTH, dbg=(), stop=None):
    key = (T, NS, DEPTH, tuple(dbg), stop)
    if key not in _CACHE:
        _CACHE[key] = Builder(T, NS, DEPTH, dbg, stop).build()
    return _CACHE[key]


def kernel(**inputs):
    NCORES, NS, T, DEPTH = 8, 2, 4096, 2
    xs = np.concatenate([np.asarray(inputs["x_prompt"], np.float32), np.asarray(inputs["x_sample"], np.float32)], 0)
    ms = np.concatenate([np.asarray(inputs["mem_prompt"], np.float32), np.asarray(inputs["mem_sample"], np.float32)], 0)
    nseq = xs.shape[0]
    slots = [[c, 8 + c if 8 + c < nseq else c] for c in range(NCORES)]
    wnames = [n for n, _ in WEIGHT_SPECS] + ["norm_final"]
    wmap = {n: np.ascontiguousarray(np.asarray(inputs[n], np.float32)) for n in wnames}
    nc = _get_nc(T, NS, DEPTH)
    in_maps = []
    for c in range(NCORES):
        m = dict(wmap)
        m["x"] = np.ascontiguousarray(xs[slots[c]])
        m["mem"] = np.ascontiguousarray(ms[slots[c]])
        in_maps.append(m)
    res = run_bass_kernel_spmd(nc, in_maps, core_ids=list(range(NCORES)))
    y = np.zeros_like(xs)
    for c in range(NCORES):
        yc = res.results[c]["y"]
        y[slots[c][0]] = yc[0]
        if slots[c][1] != slots[c][0]:
            y[slots[c][1]] = yc[1]
    nb = np.asarray(inputs["x_prompt"]).shape[0]
    return (y[:nb], y[nb:])
```

```python
import numpy as np
from contextlib import ExitStack
import concourse.bass as bass
import concourse.mybir as mybir
from concourse.bass_utils import run_bass_kernel_spmd

F32 = mybir.dt.float32
BF16 = mybir.dt.bfloat16
AF = mybir.ActivationFunctionType
ALU = mybir.AluOpType
AX = mybir.AxisListType

D = 1024
DIN = 5504
RW = 1920
NQ0 = 1920
NV0 = 2944
G0 = 3456
NMEM = 256
DFF = 4096
EPS = 1e-6
GN_EPS = 1e-5 * 64
CH = 128

EPOCH = 12000
ENGS = ("pe", "dve", "act", "pool", "sp")


class Res:
    __slots__ = ("name", "last_w", "readers", "dsem", "w_is_dma")

    def __init__(self, name=""):
        self.name = name
        self.last_w = None
        self.readers = []
        self.dsem = None
        self.w_is_dma = False


class Sched:
    def __init__(self, nc, stack):
        self.nc = nc
        self.stack = stack
        self.ops = {e: [] for e in ENGS}
        self.cnt = {e: 0 for e in ENGS}
        self.esems = {e: [] for e in ENGS}
        self.known = {e: {} for e in ENGS}
        self.sems = {}
        self.nsem = 0
        self.same_engine_raw = True
        self.dma_latest = {}
        self.free_dsems = []

    def new_sem(self, tag):
        h = self.stack.enter_context(self.nc.semaphore(f"{tag}_{self.nsem}"))
        sid = self.nsem
        self.nsem += 1
        self.sems[sid] = h
        return sid

    def _eng_point(self, eng):
        n = self.cnt[eng]
        self.cnt[eng] = n + 1
        ep, v = divmod(n, EPOCH)
        while len(self.esems[eng]) <= ep:
            self.esems[eng].append(self.new_sem(f"e{eng}"))
        return (self.esems[eng][ep], v + 1)

    def _dma_point(self, res):
        d = res.dsem
        if d is None or d[1] + 16 > EPOCH:
            if self.free_dsems and d is None:
                d = self.free_dsems.pop()
            else:
                d = [self.new_sem("d"), 0]
            res.dsem = d
        d[1] += 16
        self.dma_latest[d[0]] = d[1]
        return (d[0], d[1])

    def recycle(self, resources):
        seen = set()
        for r in resources:
            d = r.dsem
            if d is not None and id(d) not in seen and d[1] + 64 < EPOCH:
                seen.add(id(d))
                self.free_dsems.append(d)
            r.dsem = None

    def _need(self, eng, waits, pt):
        sid, val = pt
        if self.known[eng].get(sid, 0) >= val:
            return
        if waits.get(sid, 0) < val:
            waits[sid] = val

    def _deps(self, eng, reads, writes, is_dma=False):
        waits = {}
        for r in reads:
            if r.last_w is not None:
                w_eng = r.last_w[2]
                if w_eng != eng or is_dma or (self.same_engine_raw and eng != "pe"):
                    self._need(eng, waits, r.last_w[:2])
        for w in writes:
            if w.last_w is not None:
                w_eng = w.last_w[2]
                same_dma_group = is_dma and w.w_is_dma and not w.readers
                if (w_eng != eng or is_dma) and not same_dma_group:
                    self._need(eng, waits, w.last_w[:2])
            for (sid, val, r_eng) in w.readers:
                if r_eng != eng or is_dma:
                    self._need(eng, waits, (sid, val))
        for sid, val in waits.items():
            self.known[eng][sid] = val
        return list(waits.items())

    def op(self, eng, fn, reads=(), writes=()):
        waits = self._deps(eng, reads, writes)
        pt = self._eng_point(eng)
        self.ops[eng].append((waits, fn, pt[0], 1))
        for r in reads:
            r.readers.append((pt[0], pt[1], eng))
        for w in writes:
            w.last_w = (pt[0], pt[1], eng)
            w.readers = []
            w.w_is_dma = False
        return pt

    def dma(self, queue, out, in_, reads=(), writes=(), **kw):
        waits = self._deps(queue, reads, writes, is_dma=True)
        pt = self._dma_point(writes[0])

        def fn(e, out=out, in_=in_, kw=kw):
            return e.dma_start(out=out, in_=in_, **kw)
        self.ops[queue].append((waits, fn, pt[0], 16))
        for r in reads:
            r.readers.append((pt[0], pt[1], "dma"))
        for w in writes:
            w.last_w = (pt[0], pt[1], "dma")
            w.readers = []
            w.w_is_dma = True
            if w is not writes[0]:
                w.dsem = writes[0].dsem
        return pt

    def final_wait(self, eng, resources):
        waits = {}
        for r in resources:
            if r.last_w is not None:
                sid, val = r.last_w[:2]
                waits[sid] = max(waits.get(sid, 0), val)
        self.ops[eng].append((list(waits.items()), None, None, 0))

    def barrier(self):
        pts = {}
        for e in ENGS:
            n = self.cnt[e]
            if n > 0:
                ep, v = divmod(n - 1, EPOCH)
                pts[self.esems[e][ep]] = v + 1
        for sid, val in self.dma_latest.items():
            pts[sid] = max(pts.get(sid, 0), val)
        for e in ENGS:
            waits = []
            for sid, val in pts.items():
                if self.known[e].get(sid, 0) < val:
                    waits.append((sid, val))
                    self.known[e][sid] = val
            self.ops[e].append((waits, None, None, 0))

    def emit(self):
        nc = self.nc
        sems = self.sems

        def run(engname):
            def body(e):
                for (waits, fn, sid_, inc) in self.ops[engname]:
                    for sid, val in waits:
                        e.wait_ge(sems[sid], val)
                    if fn is None:
                        continue
                    ins = fn(e)
                    ins.then_inc(sems[sid_], inc)
            return body

        with nc.Block() as block:
            block.tensor(run("pe"))
            block.vector(run("dve"))
            block.scalar(run("act"))
            block.gpsimd(run("pool"))
            block.sync(run("sp"))
        self.ops = {e: [] for e in ENGS}


class Rot:
    def __init__(self, items):
        self.items = items
        self.i = 0

    def next(self):
        it = self.items[self.i % len(self.items)]
        self.i += 1
        return it


WEIGHT_SPECS = [
    ("norm_mix", (D,)), ("w_in", (D, DIN)), ("mu_prev", (RW,)), ("mu_next", (RW,)),
    ("w0", (2, 512)), ("w_up", (2, 64, 512)), ("a0", (2, 512)), ("a_up", (2, 64, 512)),
    ("g_up", (128, 512)), ("k_k", (512,)), ("k_a", (512,)), ("r_k", (8, 64)),
    ("gn_g", (512,)), ("gn_b", (512,)), ("rpb", (8, 15, 31)),
    ("w_br_rwkv", (512, D)), ("w_br_nat", (512, D)), ("w_out", (D, D)),
    ("norm_x", (D,)), ("norm_mem", (D,)), ("w_xq", (D, D)), ("w_xkv", (D, 2 * D)), ("w_xo", (D, D)),
    ("norm_ff", (D,)), ("w_ff1", (D, DFF)), ("w_ff2", (DFF, D)),
]


class Builder:
    def __init__(self, T, NS, DEPTH, dbg=(), stop=None, ext_in=(), phases=None):
        self.T, self.NS, self.DEPTH = T, NS, DEPTH
        self.ext_in = set(ext_in)
        self.phases = phases or ("p1", "p3", "p2", "p4", "p5", "p6")
        self.dbg = set(dbg)
        self.stop = stop
        self.nc = bass.Bass("TRN2", target_bir_lowering=False)
        nc = self.nc
        self.x_in = nc.dram_tensor("x", [NS, T, D], F32, kind="ExternalInput").ap()
        self.mem_in = nc.dram_tensor("mem", [NS, NMEM, D], F32, kind="ExternalInput").ap()
        self.w = {}
        for name, shp in WEIGHT_SPECS:
            self.w[name] = nc.dram_tensor(name, [DEPTH] + list(shp), F32, kind="ExternalInput").ap()
        self.w["norm_final"] = nc.dram_tensor("norm_final", [D], F32, kind="ExternalInput").ap()
        self.y_out = nc.dram_tensor("y", [NS, T, D], F32, kind="ExternalOutput").ap()
        self.scr = {}
        self.scr_res = {}

    def scratch(self, name, shape, dt):
        kind = "ExternalOutput" if name in self.dbg else ("ExternalInput" if name in self.ext_in else "Internal")
        t = self.nc.dram_tensor(name, shape, dt, kind=kind).ap()
        self.scr[name] = t
        return t

    def dres(self, key):
        r = self.scr_res.get(key)
        if r is None:
            r = Res(str(key))
            self.scr_res[key] = r
        return r

    def build(self):
        nc = self.nc
        T, NS = self.T, self.NS
        self.scratch("xres", [NS, T, D], F32)
        self.scratch("pf", [NS, RW, T], F32)
        self.scratch("nq", [NS, 512, T], BF16)
        self.scratch("nk", [NS, 512, T], BF16)
        self.scratch("nv", [NS, T, 512], BF16)
        self.scratch("sg", [NS, 2048, T], BF16)
        self.scratch("yr", [NS, T, 512], BF16)
        self.scratch("yn", [NS, T, 512], BF16)
        self.scratch("yf", [NS, T, 512], F32)
        self.scratch("bon", [NS, T, 8], F32)
        self.scratch("phd", [NS, T // 128, 128, 1920], F32)
        with ExitStack() as st:
            self.S = Sched(nc, st)
            done = False
            for l in range(self.DEPTH):
                for ph in self.phases:
                    getattr(self, "phase_" + ph)(l)
                    if self.stop == (l, ph):
                        done = True
                        break
                if done:
                    break
            with ExitStack() as ph:
                self.S.final_wait("pool", list(self.scr_res.values()))
                self.S.emit()
        return nc

    def begin(self):
        self.S.barrier()
        self.phc = getattr(self, "phc", 0) + 1
        self.ph = ExitStack()
        self.ph_res = []
        return self.ph

    def end(self):
        self.S.emit()
        self.S.recycle(self.ph_res)
        self.ph.close()

    def tile(self, name, shape, dt):
        t = self.ph.enter_context(self.nc.sbuf_tensor(f"{name}_{self.phc}", shape, dt))
        r = Res(name)
        self.ph_res.append(r)
        return t, r

    def psum(self, name, shape, dt):
        t = self.ph.enter_context(self.nc.psum_tensor(f"{name}_{self.phc}", shape, dt))
        r = Res(name)
        self.ph_res.append(r)
        return t, r

    def rot(self, name, shape, dt, n, ps=False):
        f = self.psum if ps else self.tile
        return Rot([f(f"{name}{i}", shape, dt) for i in range(n)])

    def make_ident(self):
        S = self.S
        idf, Ridf = self.tile("identf", [128, 128], F32)
        idb, Ridb = self.tile("identb", [128, 128], BF16)

        S.op("pool", lambda e: e.memset(idf[:], 0.0), writes=[Ridf])
        S.op("pool", lambda e: e.affine_select(out=idf[:], in_=idf[:], pattern=[[-1, 128]], compare_op=ALU.not_equal,
                                               fill=1.0, base=0, channel_multiplier=1), reads=[Ridf], writes=[Ridf])
        S.op("dve", lambda e: e.tensor_copy(out=idb[:], in_=idf[:]), reads=[Ridf], writes=[Ridb])
        return idb, Ridb, idf, Ridf

    def load_weight(self, dst, Rdst, src2d, K, cols, stg, col0=0, kc0=0):
        S = self.S
        nk = K // 128
        srcv = src2d.rearrange("(kc p) c -> p kc c", p=128)
        engs = ("pool", "dve", "act")
        i = 0
        for kc in range(nk):
            for c0 in range(0, cols, 2048):
                cw = min(2048, cols - c0)
                stile, Rst = stg.next()
                S.dma("sp", stile[:, 0:cw], srcv[:, kc, c0:c0 + cw], writes=[Rst])
                eng = engs[i % 3]
                i += 1
                o = dst[:, kc0 + kc, col0 + c0:col0 + c0 + cw]
                if eng == "act":
                    S.op("act", lambda e, o=o, s=stile, cw=cw: e.copy(out=o, in_=s[:, 0:cw]), reads=[Rst], writes=[Rdst])
                else:
                    S.op(eng, lambda e, o=o, s=stile, cw=cw: e.tensor_copy(out=o, in_=s[:, 0:cw]), reads=[Rst], writes=[Rdst])

    def load_bcast(self, dst, Rdst, src1d, n):
        self.S.dma("sp", dst[:, 0:n], src1d.rearrange("(o n) -> o n", o=1).partition_broadcast(128), writes=[Rdst])

    def norm_to_hT(self, xt, Rxt, b, gb, Rgb, hT, RhT, tcol, ident, Rid, tmp):
        h, Rh = self.norm_A(xt, Rxt, b, gb, Rgb, tmp)
        self.norm_B(h, Rh, hT, RhT, tcol, ident, Rid, tmp)

    def norm_A(self, xt, Rxt, b, gb, Rgb, tmp):
        S = self.S
        junk, Rjunk = tmp["junk"].next()
        ss, Rss = tmp["ss"].next()
        h, Rh = tmp["h"].next()
        S.op("act", lambda e: e.activation(out=junk[:], in_=xt[:, b, :], func=AF.Square, accum_out=ss[:, 0:1]),
             reads=[Rxt], writes=[Rjunk, Rss])

        S.op("act", lambda e: e.activation(out=ss[:, 1:2], in_=ss[:, 0:1], func=AF.Sqrt, scale=1.0 / D, bias=self.eps_t[:, 0:1]),
             reads=[Rss, self.Reps], writes=[Rss])
        S.op("dve", lambda e: e.reciprocal(out=ss[:, 2:3], in_=ss[:, 1:2]), reads=[Rss], writes=[Rss])
        S.op("dve", lambda e: e.scalar_tensor_tensor(out=h[:], in0=xt[:, b, :], scalar=ss[:, 2:3], in1=gb[:],
                                                     op0=ALU.mult, op1=ALU.mult),
             reads=[Rxt, Rss, Rgb], writes=[Rh])
        return h, Rh

    def norm_B(self, h, Rh, hT, RhT, tcol, ident, Rid, tmp):
        S = self.S
        pT, RpT = tmp["pT"].next()

        def tr(e):
            for kc in range(8):
                ins = e.transpose(out=pT[:, kc * 128:(kc + 1) * 128], in_=h[:, kc * 128:(kc + 1) * 128], identity=ident[:])
            return ins
        S.op("pe", tr, reads=[Rh, Rid], writes=[RpT])
        S.op("act", lambda e: e.copy(out=hT[:, :, tcol:tcol + 128], in_=pT[:].rearrange("p (k t) -> p k t", k=8)),
             reads=[RpT], writes=[RhT])

    def norm_tmp(self, nh=2):
        self.eps_t, Re = self.tile("eps_t", [128, 2], F32)
        eps_t = self.eps_t
        self.S.op("pool", lambda e: e.memset(eps_t[:], EPS), writes=[Re])
        self.Reps = Re
        return {
            "junk": self.rot("junk", [128, D], BF16, 1),
            "ss": self.rot("ss", [128, 4], F32, 4),
            "h": self.rot("h", [128, D], BF16, nh),
            "pT": self.rot("pT", [128, 1024], BF16, 2, ps=True),
        }

    def phase_p1(self, l):
        S, T, NS = self.S, self.T, self.NS
        self.begin()
        TT = 512
        win, Rwin = self.tile("win", [128, 8, DIN], BF16)
        stg = self.rot("wstg", [128, 2048], F32, 2)
        gb, Rgb = self.tile("gb", [128, D], F32)
        ident, Rid, _, _ = self.make_ident()
        self.load_bcast(gb, Rgb, self.w["norm_mix"][l], D)
        self.load_weight(win, Rwin, self.w["w_in"][l], D, DIN, stg)
        tmp = self.norm_tmp(4)
        xts = self.rot("xt", [128, 4, D], F32, 2)
        hTs = self.rot("hT", [128, 8, TT], BF16, 2)
        pss = self.rot("ps", [128, 512], F32, 4, ps=True)
        of32 = self.rot("of32", [128, 512], F32, 3)
        obf = self.rot("obf", [128, 512], BF16, 4)
        xsrc = self.x_in if l == 0 else self.scr["xres"]
        pf, nq, nk, nv, sg = (self.scr[k] for k in ("pf", "nq", "nk", "nv", "sg"))
        ev = 0
        tiles_ = [(s, t0) for s in range(NS) for t0 in range(0, T, TT)]

        def prep_(idx):
            s, t0 = tiles_[idx]
            xt, Rxt = xts.next()
            xr = [self.dres(("xres", s, t0))] if l > 0 else []
            S.dma("sp", xt[:], xsrc[s, t0:t0 + TT, :].rearrange("(b p) d -> p b d", p=128), reads=xr, writes=[Rxt])
            return xt, Rxt, [self.norm_A(xt, Rxt, b, gb, Rgb, tmp) for b in range(4)]
        nxt_ = prep_(0)
        for idx_, (s, t0) in enumerate(tiles_):
            if True:
                xt, Rxt, hs_ = nxt_
                hT, RhT = hTs.next()
                for b in range(4):
                    self.norm_B(hs_[b][0], hs_[b][1], hT, RhT, b * 128, ident, Rid, tmp)
                nxt_ = prep_(idx_ + 1) if idx_ + 1 < len(tiles_) else None
                for cc in range(DIN // 128):
                    c0 = cc * 128
                    if NV0 <= c0 < G0:
                        continue
                    ps, Rps = pss.next()

                    def mm(e, ps=ps, c0=c0, hT=hT):
                        for kc in range(8):
                            ins = e.matmul(ps[:], lhsT=win[:, kc, c0:c0 + 128], rhs=hT[:, kc, :], start=(kc == 0), stop=(kc == 7))
                        return ins
                    S.op("pe", mm, reads=[Rwin, RhT], writes=[Rps])
                    if c0 < RW:
                        o, Ro = of32.next()
                        eng = "dve" if ev % 2 == 0 else "act"
                        ev += 1
                        if eng == "dve":
                            S.op("dve", lambda e, o=o, ps=ps: e.tensor_copy(out=o[:], in_=ps[:]), reads=[Rps], writes=[Ro])
                        else:
                            S.op("act", lambda e, o=o, ps=ps: e.copy(out=o[:], in_=ps[:]), reads=[Rps], writes=[Ro])
                        S.dma("pool", pf[s, c0:c0 + 128, t0:t0 + TT], o[:], reads=[Ro], writes=[self.dres(("pf", s))])
                    elif c0 < NV0:
                        o, Ro = obf.next()
                        S.op("dve", lambda e, o=o, ps=ps: e.tensor_copy(out=o[:], in_=ps[:]), reads=[Rps], writes=[Ro])
                        if c0 < NQ0 + 512:
                            S.dma("pool", nq[s, c0 - NQ0:c0 - NQ0 + 128, t0:t0 + TT], o[:], reads=[Ro], writes=[self.dres(("nq", s))])
                        else:
                            c1 = c0 - NQ0 - 512
                            S.dma("pool", nk[s, c1:c1 + 128, t0:t0 + TT], o[:], reads=[Ro], writes=[self.dres(("nk", s))])
                    else:
                        o, Ro = obf.next()
                        S.op("act", lambda e, o=o, ps=ps: e.activation(out=o[:], in_=ps[:], func=AF.Sigmoid), reads=[Rps], writes=[Ro])
                        c1 = c0 - G0
                        S.dma("pool", sg[s, c1:c1 + 128, t0:t0 + TT], o[:], reads=[Ro], writes=[self.dres(("sg", s))])
                for b in range(4):
                    ps, Rps = pss.next()

                    def mmv(e, ps=ps, b=b, hT=hT):
                        for kc in range(8):
                            ins = e.matmul(ps[:], lhsT=hT[:, kc, b * 128:(b + 1) * 128], rhs=win[:, kc, NV0:NV0 + 512], start=(kc == 0), stop=(kc == 7))
                        return ins
                    S.op("pe", mmv, reads=[Rwin, RhT], writes=[Rps])
                    o, Ro = obf.next()
                    S.op("dve", lambda e, o=o, ps=ps: e.tensor_copy(out=o[:], in_=ps[:]), reads=[Rps], writes=[Ro])
                    S.dma("pool", nv[s, t0 + b * 128:t0 + (b + 1) * 128, :], o[:], reads=[Ro], writes=[self.dres(("nv", s))])
        self.end()

    def phase_p2(self, l):
        S, T, NS = self.S, self.T, self.NS
        self.begin()
        C = 128
        NCH = T // C
        LD = 0.6065306597126334
        pf, yf, yr = self.scr["pf"], self.scr["yf"], self.scr["yr"]
        bon = self.scr["bon"]
        phd = self.scr["phd"]
        op = S.op
        bc = lambda ap, shape: ap.broadcast_to(shape)
        identb, Ridb, identf, Ridf = self.make_ident()
        mup, Rmup = self.tile("mup", [128, 15], F32)
        mun, Rmun = self.tile("mun", [128, 15], F32)
        w0t, Rw0 = self.tile("w0t", [128, 2, 4], F32)
        a0t, Ra0 = self.tile("a0t", [128, 2, 4], F32)
        kkc, Rkkc = self.tile("kkc", [128, 4], F32)
        kac, Rkac = self.tile("kac", [128, 4], F32)
        omka, Romka = self.tile("omka", [128, 4], F32)
        rkc, Rrkc = self.tile("rkc", [128, 4], F32)
        S.dma("sp", mup[:], self.w["mu_prev"][l].rearrange("(c p) -> p c", p=128), writes=[Rmup], allow_slow_non_contiguous=True)
        S.dma("sp", mun[:], self.w["mu_next"][l].rearrange("(c p) -> p c", p=128), writes=[Rmun], allow_slow_non_contiguous=True)
        for d in range(2):
            S.dma("sp", w0t[:, d, :], self.w["w0"][l, d].rearrange("(c p) -> p c", p=128), writes=[Rw0], allow_slow_non_contiguous=True)
            S.dma("sp", a0t[:, d, :], self.w["a0"][l, d].rearrange("(c p) -> p c", p=128), writes=[Ra0], allow_slow_non_contiguous=True)
        S.dma("sp", kkc[:], self.w["k_k"][l].rearrange("(c p) -> p c", p=128), writes=[Rkkc], allow_slow_non_contiguous=True)
        S.dma("sp", kac[:], self.w["k_a"][l].rearrange("(c p) -> p c", p=128), writes=[Rkac], allow_slow_non_contiguous=True)
        S.dma("sp", rkc[:], self.w["r_k"][l].rearrange("h n -> (h n)").rearrange("(c p) -> p c", p=128), writes=[Rrkc], allow_slow_non_contiguous=True)
        op("dve", lambda e: e.tensor_scalar(out=omka[:], in0=kac[:], scalar1=-1.0, scalar2=1.0, op0=ALU.mult, op1=ALU.add), reads=[Rkac], writes=[Romka])
        gng, Rgng = self.tile("gng", [128, 512], F32)
        gnb, Rgnb = self.tile("gnb", [128, 512], F32)
        self.load_bcast(gng, Rgng, self.w["gn_g"][l], 512)
        self.load_bcast(gnb, Rgnb, self.w["gn_b"][l], 512)
        wstg, Rwstg = self.tile("lstg", [128, 3, 512], F32)
        S.dma("sp", wstg[:, 0, :], self.w["w_up"][l].rearrange("d l c -> (d l) c"), writes=[Rwstg])
        S.dma("sp", wstg[:, 1, :], self.w["a_up"][l].rearrange("d l c -> (d l) c"), writes=[Rwstg])
        S.dma("sp", wstg[:, 2, :], self.w["g_up"][l], writes=[Rwstg])
        wupz, Rwupz = self.tile("wupz", [128, 2, 512], BF16)
        aupz, Raupz = self.tile("aupz", [128, 2, 512], BF16)
        gup, Rgup = self.tile("gup", [128, 512], BF16)
        op("pool", lambda e: e.memset(wupz[:].rearrange("p d c -> p (d c)"), 0.0), writes=[Rwupz])
        op("pool", lambda e: e.memset(aupz[:].rearrange("p d c -> p (d c)"), 0.0), writes=[Raupz])
        for d in range(2):
            op("dve", lambda e, d=d: e.tensor_copy(out=wupz[d * 64:(d + 1) * 64, d, :], in_=wstg[d * 64:(d + 1) * 64, 0, :]), reads=[Rwstg, Rwupz], writes=[Rwupz])
            op("dve", lambda e, d=d: e.tensor_copy(out=aupz[d * 64:(d + 1) * 64, d, :], in_=wstg[d * 64:(d + 1) * 64, 1, :]), reads=[Rwstg, Raupz], writes=[Raupz])
        op("dve", lambda e: e.tensor_copy(out=gup[:], in_=wstg[:, 2, :]), reads=[Rwstg], writes=[Rgup])
        onesf, Ronesf = self.tile("onesf", [128, 128], F32)
        op("pool", lambda e: e.memset(onesf[:], 1.0), writes=[Ronesf])
        blk1, Rblk1 = self.tile("blk1", [128, 128], F32)
        hsel, Rhsel = self.tile("hsel", [128, 2], F32)
        bdm, Rbdm = self.tile("bdm", [128, 4, 2, 64], F32)

        def mkz(e):
            e.memset(blk1[:], 0.0)
            e.memset(hsel[:], 0.0)
            return e.memset(bdm[:].rearrange("p c h i -> p (c h i)"), 0.0)
        op("dve", mkz, writes=[Rblk1, Rhsel, Rbdm])

        def mko(e):
            e.memset(blk1[0:64, 0:64], 1.0)
            e.memset(blk1[64:128, 64:128], 1.0)
            e.memset(hsel[0:64, 0:1], 1.0)
            e.memset(hsel[64:128, 1:2], 1.0)
            e.memset(bdm[0:64, :, 0, :], 1.0)
            return e.memset(bdm[64:128, :, 1, :], 1.0)
        op("dve", mko, reads=[Rblk1, Rhsel, Rbdm], writes=[Rblk1, Rhsel, Rbdm])
        gne, Rgne = self.tile("gne", [128, 1], F32)
        op("pool", lambda e: e.memset(gne[:], GN_EPS), writes=[Rgne])
        mbase, Rmbase = self.tile("mbase", [128, 4, 128], F32)

        op("pool", lambda e: e.memset(mbase[:].rearrange("p a x -> p (a x)"), 1.0), writes=[Rmbase])
        op("pool", lambda e: e.affine_select(out=mbase[:, 0, :], in_=mbase[:, 0, :], pattern=[[1, 128]], compare_op=ALU.is_gt, fill=0.0, base=0, channel_multiplier=-1), reads=[Rmbase], writes=[Rmbase])
        op("pool", lambda e: e.affine_select(out=mbase[:, 1, :], in_=mbase[:, 1, :], pattern=[[1, 128]], compare_op=ALU.is_ge, fill=0.0, base=0, channel_multiplier=-1), reads=[Rmbase], writes=[Rmbase])
        op("pool", lambda e: e.affine_select(out=mbase[:, 2, :], in_=mbase[:, 2, :], pattern=[[-1, 128]], compare_op=ALU.is_gt, fill=0.0, base=0, channel_multiplier=1), reads=[Rmbase], writes=[Rmbase])
        op("pool", lambda e: e.affine_select(out=mbase[:, 3, :], in_=mbase[:, 3, :], pattern=[[-1, 128]], compare_op=ALU.is_ge, fill=0.0, base=0, channel_multiplier=1), reads=[Rmbase], writes=[Rmbase])
        mG, RmG = self.tile("mG", [128, 2, 2, 2, 128], BF16)
        mL, RmL = self.tile("mL", [128, 2, 2, 128], BF16)
        for d in range(2):
            si, ii, li = (0, 1, 2) if d == 0 else (2, 3, 0)
            for h2 in range(2):
                op("dve", lambda e, d=d, h2=h2, si=si: e.tensor_copy(out=mG[:, d, h2, 0, :], in_=mbase[:, si, :]), reads=[Rmbase], writes=[RmG])
                op("dve", lambda e, d=d, h2=h2, ii=ii: e.tensor_copy(out=mG[:, d, h2, 1, :], in_=mbase[:, ii, :]), reads=[Rmbase], writes=[RmG])
                op("dve", lambda e, d=d, h2=h2, li=li: e.tensor_copy(out=mL[:, d, h2, :], in_=mbase[:, li, :]), reads=[Rmbase], writes=[RmL])
        Eg, REg = self.tile("Eg", [8, 3, 128], F32)
        op("pool", lambda e: e.memset(Eg[:].rearrange("p a x -> p (a x)"), 1.0), writes=[REg])
        for gi, b_ in enumerate((16, 32, 64)):
            op("pool", lambda e, gi=gi, b_=b_: e.affine_select(out=Eg[:, gi, :], in_=Eg[:, gi, :], pattern=[[1, 128]], compare_op=ALU.is_ge, fill=0.0, base=0, channel_multiplier=-b_),
               reads=[REg], writes=[REg])
            op("pool", lambda e, gi=gi, b_=b_: e.affine_select(out=Eg[:, gi, :], in_=Eg[:, gi, :], pattern=[[-1, 128]], compare_op=ALU.is_ge, fill=0.0, base=b_ - 1, channel_multiplier=b_),
               reads=[REg], writes=[REg])
        bdf, Rbdf = self.tile("bdf", [128, 4, 128], F32)
        op("pool", lambda e: e.memset(bdf[:, 3, :], 1.0), writes=[Rbdf])
        pbm, Rpbm = self.psum("pbm", [128, 512], F32)

        def mmE(e):
            for gi in range(3):
                ins = e.matmul(pbm[:, gi * 128:(gi + 1) * 128], lhsT=Eg[:, gi, :], rhs=Eg[:, gi, :], start=True, stop=True)
            return ins
        op("pe", mmE, reads=[REg], writes=[Rpbm])
        op("dve", lambda e: e.tensor_copy(out=bdf[:, 0:3, :].rearrange("p a x -> p (a x)"), in_=pbm[:, 0:384]), reads=[Rpbm, Rbdf], writes=[Rbdf])
        bd16, Rbd16 = self.tile("bd16", [128, 2, 128], BF16)
        offm, Roffm = self.tile("offm", [128, 3, 4, 128], BF16)
        for h2 in range(4):
            if h2 < 2:
                op("dve", lambda e, h2=h2: e.tensor_copy(out=bd16[:, h2, :], in_=bdf[:, 0, :]), reads=[Rbdf], writes=[Rbd16])
            for lv in range(3):
                op("dve", lambda e, h2=h2, lv=lv: e.tensor_tensor(out=offm[:, lv, h2, :], in0=bdf[:, lv + 1, :], in1=bdf[:, lv, :], op=ALU.subtract),
                   reads=[Rbdf], writes=[Roffm])
        Ps = self.rot("P", [128, 15, 130], F32, 1)
        tAs = self.rot("tA", [128, 15, 128], F32, 1)
        tBs = self.rot("tB", [128, 15, 128], F32, 1)
        phs = self.rot("ph", [128, 15, 128], F32, 1)
        f4 = lambda n, k=1: self.rot(n, [128, 4, 128], F32, k)
        b4 = lambda n, k=1: self.rot(n, [128, 4, 128], BF16, k)
        lws, avs, pres, cums, cprs = f4("lw"), f4("av"), f4("pre"), f4("cum"), f4("cpr")
        ecs, ecps, eis = f4("ec"), f4("ecp"), f4("ei")
        kraws, sqs, nrms, kkvs, tts, kds, bvs = (f4(n) for n in ("kraw", "sq", "nrm", "kkv", "tt", "kd", "bv"))
        Kfs, Bfs, rks = kraws, nrms, sqs
        ATs, RTs, KTs, BTs = (b4(n, 2) for n in ("AT", "RT", "KT", "BT"))
        KhTs, BhTs, vTs = (b4(n, 1) for n in ("KhT", "BhT", "vT"))
        twds = self.rot("twd", [128, 128], BF16, 2)
        adbs = self.rot("adb", [128, 128], BF16, 2)
        sgds = self.rot("sgd", [128, 128], BF16, 2)
        gCs = self.rot("gC", [128, 4], F32, 2)
        ARbds = self.rot("ARbd", [128, 4, 2, 2, 128], BF16, 2)
        Bbds = self.rot("Bbd", [128, 4, 2, 128], BF16, 2)
        for (t_, r_) in ARbds.items:
            op("pool", lambda e, t_=t_: e.memset(t_[:].rearrange("p c h x t -> p (c h x t)"), 0.0), writes=[r_])
        for (t_, r_) in Bbds.items:
            op("pool", lambda e, t_=t_: e.memset(t_[:].rearrange("p c h t -> p (c h t)"), 0.0), writes=[r_])
        VKs = self.rot("VK", [128, 1024], BF16, 2)
        Bhats = self.rot("Bhat", [128, 512], BF16, 2)
        GKs = self.rot("GK", [128, 4, 2, 2, 128], BF16, 2)
        GBs = self.rot("GB", [128, 4, 2, 2, 128], BF16, 2)
        Lhs = self.rot("Lh", [128, 4, 2, 128], BF16, 1)
        lmq = [[self.tile(f"lmq{h}_{k}", [128, 4, 128], BF16) for k in range(2)] for h in range(8)]
        zzt = [self.tile(f"zz{h}", [128, 2, 2, 128], BF16) for h in range(4)]
        ttt = [[self.tile(f"tt{h}_{k}", [128, 2, 2, 128], BF16) for k in range(2)] for h in range(4)]
        tfin = [self.rot(f"tfin{h}", [128, 2, 2, 128], BF16, 2) for h in range(4)]
        L16s = self.rot("L16", [128, 4, 2, 128], BF16, 1)
        M16s = self.rot("M16", [128, 4, 2, 128], BF16, 1)
        St, RSt = self.tile("St", [128, 4, 2, 64], F32)
        Sbd, RSbd = self.tile("Sbd", [128, 4, 2, 64], BF16)
        stmp, Rstmp = self.tile("stmp", [128, 4, 2, 64], F32)
        Wsbs = self.rot("Wsb", [128, 512], BF16, 1)
        Usbs = self.rot("Usb", [128, 512], BF16, 1)
        Ysbs = self.rot("Ysb", [128, 512], F32, 2)
        bons = self.rot("bonv", [128, 8], F32, 2)
        bon0s = self.rot("bon0", [128, 8], F32, 2)
        yfls = self.rot("yfl", [128, 512], F32, 1)
        gtms = self.rot("gtm", [128, 512], F32, 2)
        sqvs = self.rot("sqv", [128, 512], F32, 1)
        ycs = self.rot("yc", [128, 512], F32, 1)
        vvs = sqvs
        stat = self.rot("stat", [128, 6, 8], F32, 2)
        yos = self.rot("yo", [128, 512], BF16, 1)
        pbanks = self.rot("pb", [128, 512], F32, 6, ps=True)
        ptr = self.rot("ptr", [128, 1024], BF16, 1, ps=True)
        Ptile, RPtile = self.tile("Ptile", [128, 8, 128], BF16)
        pfv = [pf[s].rearrange("(c p) t -> p c t", p=128) for s in range(NS)]

        for s in range(NS):
            for d in range(2):
                op("pool", lambda e: e.memset(St[:].rearrange("p c h i -> p (c h i)"), 0.0), writes=[RSt])
                op("pool", lambda e: e.memset(Sbd[:].rearrange("p c h i -> p (c h i)"), 0.0), writes=[RSbd])
                order = range(NCH) if d == 0 else range(NCH - 1, -1, -1)
                def body(ci, s=s, d=d):
                    t0 = ci * C
                    ph, Rph = phs.next()
                    if d == 0:
                        P, RP = Ps.next()
                        lo, hi = max(t0 - 1, 0), min(t0 + C + 1, T)
                        if t0 == 0:
                            op("pool", lambda e, P=P: e.memset(P[:, :, 0:1], 0.0), writes=[RP])
                        if t0 + C == T:
                            op("pool", lambda e, P=P: e.memset(P[:, :, 129:130], 0.0), writes=[RP])
                        S.dma("sp", P[:, :, lo - (t0 - 1):hi - (t0 - 1)], pfv[s][:, :, lo:hi], reads=[self.dres(("pf", s))], writes=[RP])
                        tA, RtA = tAs.next()
                        tB, RtB = tBs.next()
                        for (c0_, c1_) in ((0, 5), (5, 10), (10, 15)):
                            cs_ = slice(c0_, c1_)
                            nn = c1_ - c0_
                            op("dve", lambda e, cs_=cs_: e.tensor_tensor(out=tA[:, cs_, :], in0=P[:, cs_, 0:128], in1=P[:, cs_, 1:129], op=ALU.subtract), reads=[RP], writes=[RtA])
                            op("pool", lambda e, cs_=cs_: e.tensor_tensor(out=tB[:, cs_, :], in0=P[:, cs_, 2:130], in1=P[:, cs_, 1:129], op=ALU.subtract), reads=[RP], writes=[RtB])
                            op("dve", lambda e, cs_=cs_, nn=nn: e.tensor_tensor(out=tA[:, cs_, :], in0=tA[:, cs_, :], in1=bc(mup[:, cs_].unsqueeze(2), [128, nn, 128]), op=ALU.mult), reads=[RtA, Rmup], writes=[RtA])
                            op("pool", lambda e, cs_=cs_, nn=nn: e.tensor_tensor(out=tB[:, cs_, :], in0=tB[:, cs_, :], in1=bc(mun[:, cs_].unsqueeze(2), [128, nn, 128]), op=ALU.mult), reads=[RtB, Rmun], writes=[RtB])
                            yield 'a'
                            op("dve", lambda e, cs_=cs_: e.tensor_tensor(out=tA[:, cs_, :], in0=tA[:, cs_, :], in1=tB[:, cs_, :], op=ALU.add), reads=[RtA, RtB], writes=[RtA])
                            op("pool", lambda e, cs_=cs_: e.tensor_tensor(out=ph[:, cs_, :], in0=tA[:, cs_, :], in1=P[:, cs_, 1:129], op=ALU.add), reads=[RtA, RP], writes=[Rph])
                            yield 'a'
                        S.dma("pool", phd[s, ci], ph[:].rearrange("p c t -> p (c t)"), reads=[Rph], writes=[self.dres(("phd", s))])
                    else:
                        S.dma("sp", ph[:].rearrange("p c t -> p (c t)"), phd[s, ci], reads=[self.dres(("phd", s))], writes=[Rph])
                    rh, kh, vh = ph[:, 0:4, :], ph[:, 4:8, :], ph[:, 8:12, :]
                    yield 'a'
                    twd, Rtwd = twds.next()
                    adb, Radb = adbs.next()
                    sgd, Rsgd = sgds.next()
                    op("act", lambda e, twd=twd, ph=ph: e.activation(out=twd[:], in_=ph[:, 12, :], func=AF.Tanh), reads=[Rph], writes=[Rtwd])
                    op("dve", lambda e, adb=adb, ph=ph: e.tensor_copy(out=adb[:], in_=ph[:, 13, :]), reads=[Rph], writes=[Radb])
                    psw, Rpsw = pbanks.next()
                    psa, Rpsa = pbanks.next()

                    def mmlora(e, psw=psw, psa=psa, twd=twd, adb=adb, d=d):
                        for cc in range(4):
                            e.matmul(psw[:, cc * 128:(cc + 1) * 128], lhsT=wupz[:, d, cc * 128:(cc + 1) * 128], rhs=twd[:], start=True, stop=True)
                        for cc in range(4):
                            ins = e.matmul(psa[:, cc * 128:(cc + 1) * 128], lhsT=aupz[:, d, cc * 128:(cc + 1) * 128], rhs=adb[:], start=True, stop=True)
                        return ins
                    op("pe", mmlora, reads=[Rwupz, Raupz, Rtwd, Radb], writes=[Rpsw, Rpsa])
                    yield 'a'
                    lw, Rlw = lws.next()
                    av, Rav = avs.next()

                    def sigw(e, lw=lw, psw=psw, d=d):
                        for cc in range(4):
                            ins = e.activation(out=lw[:, cc, :], in_=psw[:, cc * 128:(cc + 1) * 128], func=AF.Sigmoid, bias=w0t[:, d, cc:cc + 1])
                        return ins
                    op("act", sigw, reads=[Rpsw, Rw0], writes=[Rlw])

                    def siga(e, av=av, psa=psa, d=d):
                        for cc in range(4):
                            ins = e.activation(out=av[:, cc, :], in_=psa[:, cc * 128:(cc + 1) * 128], func=AF.Sigmoid, bias=a0t[:, d, cc:cc + 1])
                        return ins
                    op("act", siga, reads=[Rpsa, Ra0], writes=[Rav])
                    yield 'a'
                    yield 'a'
                    pre, Rpre = pres.next()
                    cum, Rcum = cums.next()
                    cpr, Rcpr = cprs.next()

                    def scan(e, pre=pre, lw=lw):
                        for cc in range(4):
                            ins = e.tensor_tensor_scan(out=pre[:, cc, :], data0=onesf[:], data1=lw[:, cc, :], initial=0.0, op0=ALU.mult, op1=ALU.add)
                        return ins
                    op("dve", scan, reads=[Rlw, Ronesf], writes=[Rpre])
                    yield 'a'
                    if d == 0:
                        cum, Rcum = pre, Rpre
                    else:
                        op("dve", lambda e, cum=cum, pre=pre, lw=lw: e.scalar_tensor_tensor(out=cum[:], in0=pre[:], scalar=-1.0, in1=lw[:], op0=ALU.mult, op1=ALU.add),
                           reads=[Rpre, Rlw], writes=[Rcum])
                        op("dve", lambda e, cum=cum, pre=pre: e.tensor_tensor(out=cum[:], in0=cum[:], in1=bc(pre[:, :, 127:128], [128, 4, 128]), op=ALU.add),
                           reads=[Rcum, Rpre], writes=[Rcum])
                    op("pool", lambda e, cpr=cpr, cum=cum, lw=lw: e.tensor_tensor(out=cpr[:], in0=cum[:], in1=lw[:], op=ALU.subtract), reads=[Rcum, Rlw], writes=[Rcpr])
                    ec, Rec = ecs.next()
                    ecp, Recp = ecps.next()
                    ei, Rei = eis.next()
                    op("act", lambda e, ec=ec, cum=cum: e.activation(out=ec[:], in_=cum[:], func=AF.Exp, scale=-LD), reads=[Rcum], writes=[Rec])
                    op("act", lambda e, ecp=ecp, cpr=cpr: e.activation(out=ecp[:], in_=cpr[:], func=AF.Exp, scale=-LD), reads=[Rcpr], writes=[Recp])
                    op("act", lambda e, ei=ei, cum=cum: e.activation(out=ei[:], in_=cum[:], func=AF.Exp, scale=LD), reads=[Rcum], writes=[Rei])
                    yield 'a'
                    gC, RgC = gCs.next()
                    ce = 127 if d == 0 else 0
                    op("dve", lambda e, gC=gC, ec=ec, ce=ce: e.tensor_copy(out=gC[:], in_=ec[:, :, ce]), reads=[Rec], writes=[RgC])
                    yield 'a'
                    kraw, Rkraw = kraws.next()
                    sq, Rsq = sqs.next()
                    nrm, Rnrm = nrms.next()
                    kkv, Rkkv = kkvs.next()
                    tt, Rtt = tts.next()
                    kd, Rkd = kds.next()
                    bv, Rbv = bvs.next()
                    op("dve", lambda e, kraw=kraw, ph=ph: e.tensor_tensor(out=kraw[:], in0=ph[:, 4:8, :], in1=bc(kkc[:].unsqueeze(2), [128, 4, 128]), op=ALU.mult), reads=[Rph, Rkkc], writes=[Rkraw])
                    op("pool", lambda e, sq=sq, kraw=kraw: e.tensor_tensor(out=sq[:], in0=kraw[:], in1=kraw[:], op=ALU.mult), reads=[Rkraw], writes=[Rsq])
                    psn, Rpsn = pbanks.next()
                    op("pe", lambda e, psn=psn, sq=sq: e.matmul(psn[:], lhsT=blk1[:], rhs=sq[:].rearrange("p c t -> p (c t)"), start=True, stop=True), reads=[Rblk1, Rsq], writes=[Rpsn])
                    yield 'a'
                    op("act", lambda e, nrm=nrm, psn=psn: e.activation(out=nrm[:].rearrange("p c t -> p (c t)"), in_=psn[:], func=AF.Ln), reads=[Rpsn], writes=[Rnrm])
                    op("act", lambda e, nrm=nrm: e.activation(out=nrm[:], in_=nrm[:], func=AF.Exp, scale=-0.5), reads=[Rnrm], writes=[Rnrm])
                    op("dve", lambda e, nrm=nrm: e.tensor_scalar_min(out=nrm[:], in0=nrm[:], scalar1=1e12), reads=[Rnrm], writes=[Rnrm])
                    op("dve", lambda e, kkv=kkv, kraw=kraw, nrm=nrm: e.tensor_tensor(out=kkv[:], in0=kraw[:], in1=nrm[:], op=ALU.mult), reads=[Rkraw, Rnrm], writes=[Rkkv])
                    yield 'a'
                    op("pool", lambda e, tt=tt, av=av: e.tensor_tensor(out=tt[:], in0=av[:], in1=bc(kac[:].unsqueeze(2), [128, 4, 128]), op=ALU.mult), reads=[Rav, Rkac], writes=[Rtt])
                    op("pool", lambda e, tt=tt: e.tensor_tensor(out=tt[:], in0=tt[:], in1=bc(omka[:].unsqueeze(2), [128, 4, 128]), op=ALU.add), reads=[Rtt, Romka], writes=[Rtt])
                    op("dve", lambda e, kd=kd, ph=ph, tt=tt: e.tensor_tensor(out=kd[:], in0=ph[:, 4:8, :], in1=tt[:], op=ALU.mult), reads=[Rph, Rtt], writes=[Rkd])
                    yield 'a'
                    op("pool", lambda e, bv=bv, kkv=kkv, av=av: e.tensor_tensor(out=bv[:], in0=kkv[:], in1=av[:], op=ALU.mult), reads=[Rkkv, Rav], writes=[Rbv])
                    yield 'a'
                    AT, RAT = ATs.next()
                    RT, RRT = RTs.next()
                    KT, RKT = KTs.next()
                    BT, RBT = BTs.next()
                    KhT, RKhT = KhTs.next()
                    BhT, RBhT = BhTs.next()
                    vT, RvT = vTs.next()
                    Kf, RKf = Kfs.next()
                    Bf, RBf = Bfs.next()
                    rk, Rrk = rks.next()
                    op("dve", lambda e, AT=AT, kkv=kkv, ecp=ecp: e.scalar_tensor_tensor(out=AT[:], in0=kkv[:], scalar=-1.0, in1=ecp[:], op0=ALU.mult, op1=ALU.mult), reads=[Rkkv, Recp], writes=[RAT])
                    op("dve", lambda e, RT=RT, ph=ph, ec=ec: e.tensor_tensor(out=RT[:], in0=ph[:, 0:4, :], in1=ec[:], op=ALU.mult), reads=[Rph, Rec], writes=[RRT])
                    op("dve", lambda e, Kf=Kf, kd=kd, ei=ei: e.tensor_tensor(out=Kf[:], in0=kd[:], in1=ei[:], op=ALU.mult), reads=[Rkd, Rei], writes=[RKf])
                    yield 'a'
                    op("pool", lambda e, Bf=Bf, bv=bv, ei=ei: e.tensor_tensor(out=Bf[:], in0=bv[:], in1=ei[:], op=ALU.mult), reads=[Rbv, Rei], writes=[RBf])
                    op("act", lambda e, KT=KT, Kf=Kf: e.copy(out=KT[:], in_=Kf[:]), reads=[RKf], writes=[RKT])
                    op("act", lambda e, BT=BT, Bf=Bf: e.copy(out=BT[:], in_=Bf[:]), reads=[RBf], writes=[RBT])
                    yield 'a'
                    op("dve", lambda e, KhT=KhT, Kf=Kf, gC=gC: e.tensor_tensor(out=KhT[:], in0=Kf[:], in1=bc(gC[:].unsqueeze(2), [128, 4, 128]), op=ALU.mult), reads=[RKf, RgC], writes=[RKhT])
                    op("pool", lambda e, BhT=BhT, Bf=Bf, gC=gC: e.tensor_tensor(out=BhT[:], in0=Bf[:], in1=bc(gC[:].unsqueeze(2), [128, 4, 128]), op=ALU.mult), reads=[RBf, RgC], writes=[RBhT])
                    op("act", lambda e, vT=vT, ph=ph: e.copy(out=vT[:], in_=ph[:, 8:12, :]), reads=[Rph], writes=[RvT])
                    op("pool", lambda e, rk=rk, ph=ph, kd=kd: e.tensor_tensor(out=rk[:], in0=ph[:, 0:4, :], in1=kd[:], op=ALU.mult), reads=[Rph, Rkd], writes=[Rrk])
                    op("pool", lambda e, rk=rk: e.tensor_tensor(out=rk[:], in0=rk[:], in1=bc(rkc[:].unsqueeze(2), [128, 4, 128]), op=ALU.mult), reads=[Rrk, Rrkc], writes=[Rrk])
                    yield 'a'
                    ARbd, RARbd = ARbds.next()
                    Bbd, RBbd = Bbds.next()
                    for h2 in range(2):
                        pl = slice(h2 * 64, (h2 + 1) * 64)
                        op("act", lambda e, pl=pl, h2=h2, AT=AT: e.copy(out=ARbd[pl, :, h2, 0, :], in_=AT[pl, :, :]), reads=[RAT, RARbd], writes=[RARbd])
                        op("act", lambda e, pl=pl, h2=h2, RT=RT: e.copy(out=ARbd[pl, :, h2, 1, :], in_=RT[pl, :, :]), reads=[RRT, RARbd], writes=[RARbd])
                        op("act", lambda e, pl=pl, h2=h2, BT=BT: e.copy(out=Bbd[pl, :, h2, :], in_=BT[pl, :, :]), reads=[RBT, RBbd], writes=[RBbd])
                    yield 'a'
                    pt, Rpt = ptr.next()
                    VK, RVK = VKs.next()
                    Vtm, RVtm = VK[:, 0:512], RVK
                    Khat, RKhat = VK[:, 512:1024], RVK
                    Bhat, RBhat = Bhats.next()

                    def tr1(e, pt=pt, vT=vT, KhT=KhT):
                        for cc in range(4):
                            e.transpose(out=pt[:, cc * 128:(cc + 1) * 128], in_=vT[:, cc, :], identity=identb[:])
                        for cc in range(4):
                            ins = e.transpose(out=pt[:, 512 + cc * 128:512 + (cc + 1) * 128], in_=KhT[:, cc, :], identity=identb[:])
                        return ins
                    op("pe", tr1, reads=[RvT, RKhT, Ridb], writes=[Rpt])
                    yield 'a'
                    op("act", lambda e, VK=VK, pt=pt: e.copy(out=VK[:], in_=pt[:]), reads=[Rpt], writes=[RVK])
                    pt2, Rpt2 = ptr.next()

                    def tr2(e, pt2=pt2, BhT=BhT):
                        for cc in range(4):
                            ins = e.transpose(out=pt2[:, cc * 128:(cc + 1) * 128], in_=BhT[:, cc, :], identity=identb[:])
                        return ins
                    op("pe", tr2, reads=[RBhT, Ridb], writes=[Rpt2])
                    yield 'a'
                    op("act", lambda e, Bhat=Bhat, pt2=pt2: e.copy(out=Bhat[:], in_=pt2[:, 0:512]), reads=[Rpt2], writes=[RBhat])
                    psb, Rpsb = pbanks.next()
                    bonv, Rbonv = bons.next()

                    def mmbon(e, psb=psb, rk=rk):
                        for cc in range(4):
                            ins = e.transpose(out=psb[:, cc * 128:(cc + 1) * 128], in_=rk[:, cc, :], identity=identf[:])
                        return ins
                    op("pe", mmbon, reads=[Rrk, Ridf], writes=[Rpsb])
                    op("dve", lambda e, bonv=bonv, psb=psb: e.tensor_reduce(out=bonv[:], in_=psb[:].rearrange("p (h i) -> p h i", i=64), axis=AX.X, op=ALU.add),
                       reads=[Rpsb], writes=[Rbonv])
                    if d == 1:
                        op("act", lambda e, sgd=sgd, ph=ph: e.activation(out=sgd[:], in_=ph[:, 14, :], func=AF.Sigmoid), reads=[Rph], writes=[Rsgd])
                        psg, Rpsg = pbanks.next()
                        gtm, Rgtm = gtms.next()
                        op("pe", lambda e, psg=psg, sgd=sgd: e.matmul(psg[:], lhsT=sgd[:], rhs=gup[:], start=True, stop=True), reads=[Rsgd, Rgup], writes=[Rpsg])
                        op("act", lambda e, gtm=gtm, psg=psg: e.copy(out=gtm[:], in_=psg[:]), reads=[Rpsg], writes=[Rgtm])
                    yield 'B'
                    GK, RGK = GKs.next()
                    GB, RGB = GBs.next()
                    Lh, RLh = Lhs.next()
                    for cc in range(4):
                        p1, Rp1 = pbanks.next()
                        p2, Rp2 = pbanks.next()
                        p3, Rp3 = pbanks.next()

                        def mmg(e, cc=cc, p1=p1, p2=p2, p3=p3, KT=KT, BT=BT, AT=AT):
                            e.matmul(p1[:], lhsT=KT[:, cc, :], rhs=ARbd[:, cc].rearrange("p h x t -> p (h x t)"), start=True, stop=True)
                            e.matmul(p2[:], lhsT=BT[:, cc, :], rhs=ARbd[:, cc].rearrange("p h x t -> p (h x t)"), start=True, stop=True)
                            return e.matmul(p3[:, 0:256], lhsT=AT[:, cc, :], rhs=Bbd[:, cc].rearrange("p h t -> p (h t)"), start=True, stop=True)
                        op("pe", mmg, reads=[RKT, RBT, RAT, RARbd, RBbd], writes=[Rp1, Rp2, Rp3])
                        op("dve", lambda e, cc=cc, p1=p1, GK=GK, d=d: e.tensor_tensor(out=GK[:, cc].rearrange("p h x t -> p (h x t)"), in0=p1[:],
                                                                                 in1=mG[:, d].rearrange("p h x t -> p (h x t)"), op=ALU.mult), reads=[Rp1, RmG], writes=[RGK])
                        op("dve", lambda e, cc=cc, p2=p2, GB=GB, d=d: e.tensor_tensor(out=GB[:, cc].rearrange("p h x t -> p (h x t)"), in0=p2[:],
                                                                                 in1=mG[:, d].rearrange("p h x t -> p (h x t)"), op=ALU.mult), reads=[Rp2, RmG], writes=[RGB])
                        op("dve", lambda e, cc=cc, p3=p3, Lh=Lh, d=d: e.tensor_tensor(out=Lh[:, cc].rearrange("p h t -> p (h t)"), in0=p3[:, 0:256],
                                                                                 in1=mL[:, d].rearrange("p h t -> p (h t)"), op=ALU.mult), reads=[Rp3, RmL], writes=[RLh])
                        yield 'b'
                    L16, RL16 = L16s.next()
                    M16, RM16 = M16s.next()
                    for cc in range(4):
                        op("pool", lambda e, cc=cc, L16=L16, Lh=Lh: e.tensor_tensor(out=L16[:, cc].rearrange("p h t -> p (h t)"), in0=Lh[:, cc].rearrange("p h t -> p (h t)"),
                                                                              in1=bd16[:].rearrange("p h t -> p (h t)"), op=ALU.mult), reads=[RLh, Rbd16], writes=[RL16])
                        op("pool", lambda e, cc=cc, M16=M16, GB=GB: e.tensor_tensor(out=M16[:, cc], in0=GB[:, cc, :, 0, :], in1=bd16[:], op=ALU.mult), reads=[RGB, Rbd16], writes=[RM16])
                    cur = []
                    for h in range(8):
                        cur.append((L16[:, h // 2, h % 2, :], RL16, M16[:, h // 2, h % 2, :], RM16, identb[:], Ridb, identb[:], Ridb))
                    for lev in range(4):
                        for h in range(8):
                            Lp, RLp, Mp, RMp, Qp, RQp, Pp, RPp = cur[h]
                            pq, Rpq = pbanks.next()
                            lt, Rlt = lmq[h][lev % 2]

                            def mmi(e, pq=pq, Lp=Lp, Mp=Mp, Qp=Qp, Pp=Pp, lev=lev):
                                e.matmul(pq[:, 256:384], lhsT=identb[:], rhs=Qp, start=True, stop=False)
                                ins = e.matmul(pq[:, 256:384], lhsT=Lp, rhs=Qp, start=False, stop=True)
                                if lev < 3:
                                    e.matmul(pq[:, 0:128], lhsT=Mp, rhs=Lp, start=True, stop=True)
                                    ins = e.matmul(pq[:, 128:256], lhsT=Lp, rhs=Mp, start=True, stop=True)
                                return ins
                            op("pe", mmi, reads=[RLp, RMp, RQp, Ridb], writes=[Rpq])
                            lo_ = 0 if lev < 3 else 256
                            eng = "act"
                            if eng == "dve":
                                op("dve", lambda e, lt=lt, pq=pq, lo_=lo_: e.tensor_copy(out=lt[:].rearrange("p a t -> p (a t)")[:, lo_:512], in_=pq[:, lo_:512]), reads=[Rpq], writes=[Rlt])
                            else:
                                op("act", lambda e, lt=lt, pq=pq, lo_=lo_: e.copy(out=lt[:].rearrange("p a t -> p (a t)")[:, lo_:384], in_=pq[:, lo_:384]), reads=[Rpq], writes=[Rlt])
                            cur[h] = (lt[:, 0, :], Rlt, lt[:, 1, :], Rlt, lt[:, 2, :], Rlt, lt[:, 3, :], Rlt)
                            if h % 2 == 1:
                                yield 'b'
                    ptq, Rptq = ptr.next()

                    def trq(e, ptq=ptq, cur=list(cur)):
                        for h in range(8):
                            ins = e.transpose(out=ptq[:, h * 128:(h + 1) * 128], in_=cur[h][4], identity=identb[:])
                        return ins
                    op("pe", trq, reads=[cur[h][5] for h in range(8)] + [Ridb], writes=[Rptq])
                    op("act", lambda e, ptq=ptq: e.copy(out=Ptile[:].rearrange("p h t -> p (h t)"), in_=ptq[:]), reads=[Rptq], writes=[RPtile])
                    tcur = [(Ptile[:, h, :], RPtile, cur[h][4], cur[h][5]) for h in range(8)]
                    for lv in range(3):
                        last_lv = (lv == 2)
                        pas = []
                        for pr in range(4):
                            pa, Rpa = pbanks.next()
                            pas.append((pa, Rpa))

                            def mmz(e, pa=pa, pr=pr, tc=list(tcur), last_lv=last_lv, Lh=Lh, GB=GB):
                                for h2 in range(2):
                                    h = 2 * pr + h2
                                    Tk, RTk, TTk, RTTk = tc[h]
                                    ins = e.matmul(pa[:, h2 * 256 + 128:h2 * 256 + 256], lhsT=Lh[:, pr, h2, :], rhs=TTk, start=True, stop=True)
                                    if not last_lv:
                                        ins = e.matmul(pa[:, h2 * 256:h2 * 256 + 128], lhsT=GB[:, pr, h2, 0, :], rhs=Tk, start=True, stop=True)
                                return ins
                            op("pe", mmz, reads=[RLh, RGB, tcur[2 * pr][1], tcur[2 * pr][3], tcur[2 * pr + 1][1], tcur[2 * pr + 1][3]], writes=[Rpa])
                        for pr in range(4):
                            pa, Rpa = pas[pr]
                            zz, Rzz = zzt[pr]
                            if not last_lv:
                                op("dve", lambda e, zz=zz, pa=pa, lv=lv: e.tensor_tensor(out=zz[:].rearrange("p h a t -> p (h a t)"), in0=pa[:],
                                                                                        in1=offm[:, lv].rearrange("p a t -> p (a t)"), op=ALU.mult),
                                   reads=[Rpa, Roffm], writes=[Rzz])
                            else:
                                op("dve", lambda e, zz=zz, pa=pa, lv=lv: e.tensor_tensor(out=zz[:, :, 1, :], in0=pa[:].rearrange("p (h a t) -> p h a t", h=2, a=2)[:, :, 1, :],
                                                                                        in1=offm[:, lv, 0:2, :], op=ALU.mult),
                                   reads=[Rpa, Roffm], writes=[Rzz])
                        yield 'b'
                        pbs = []
                        for pr in range(4):
                            pb_, Rpb_ = pbanks.next()
                            pbs.append((pb_, Rpb_))
                            zz, Rzz = zzt[pr]

                            def mmt(e, pb_=pb_, pr=pr, tc=list(tcur), zz=zz, last_lv=last_lv):
                                for h2 in range(2):
                                    h = 2 * pr + h2
                                    Tk, RTk, TTk, RTTk = tc[h]
                                    c0 = h2 * 256
                                    e.matmul(pb_[:, c0 + 128:c0 + 256], lhsT=identb[:], rhs=TTk, start=True, stop=False)
                                    ins = e.matmul(pb_[:, c0 + 128:c0 + 256], lhsT=Tk, rhs=zz[:, h2, 1, :], start=False, stop=True)
                                    if not last_lv:
                                        e.matmul(pb_[:, c0:c0 + 128], lhsT=identb[:], rhs=Tk, start=True, stop=False)
                                        ins = e.matmul(pb_[:, c0:c0 + 128], lhsT=TTk, rhs=zz[:, h2, 0, :], start=False, stop=True)
                                return ins
                            op("pe", mmt, reads=[Rzz, Ridb, tcur[2 * pr][1], tcur[2 * pr][3], tcur[2 * pr + 1][1], tcur[2 * pr + 1][3]], writes=[Rpb_])
                        for pr in range(4):
                            pb_, Rpb_ = pbs[pr]
                            tn, Rtn = ttt[pr][lv % 2] if not last_lv else tfin[pr].next()
                            eng = "act"
                            if not last_lv:
                                src, dst = pb_[:], tn[:].rearrange("p h a t -> p (h a t)")
                            else:
                                src, dst = pb_[:].rearrange("p (h a t) -> p h a t", h=2, a=2)[:, :, 1, :], tn[:, :, 1, :]
                            if eng == "act":
                                op("act", lambda e, src=src, dst=dst: e.copy(out=dst, in_=src), reads=[Rpb_], writes=[Rtn])
                            else:
                                op("dve", lambda e, src=src, dst=dst: e.tensor_copy(out=dst, in_=src), reads=[Rpb_], writes=[Rtn])
                            for h2 in range(2):
                                tcur[2 * pr + h2] = (tn[:, h2, 0, :], Rtn, tn[:, h2, 1, :], Rtn)
                        yield 'b'
                    cur = [(None, None, None, None, tcur[h][2], tcur[h][3]) for h in range(8)]
                    yield 'C'
                    pW, RpW = pbanks.next()
                    Wsb, RWsb = Wsbs.next()
                    Usb, RUsb = Usbs.next()
                    Ysb, RYsb = Ysbs.next()

                    def mmW(e, pW=pW, AT=AT, GK=GK, Vtm=Vtm):
                        for cc in range(4):
                            e.matmul(pW[:, cc * 128:(cc + 1) * 128], lhsT=AT[:, cc, :], rhs=Sbd[:, cc].rearrange("p h i -> p (h i)"), start=True, stop=False)
                            for h2 in range(2):
                                h = 2 * cc + h2
                                ins = e.matmul(pW[:, h * 64:(h + 1) * 64], lhsT=GK[:, cc, h2, 0, :], rhs=Vtm[:, h * 64:(h + 1) * 64], start=False, stop=True)
                        return ins
                    op("pe", mmW, reads=[RAT, RSbd, RGK, RVtm], writes=[RpW])
                    op("dve", lambda e, Wsb=Wsb, pW=pW: e.tensor_copy(out=Wsb[:], in_=pW[:]), reads=[RpW], writes=[RWsb])
                    yield 'c'
                    pU, RpU = pbanks.next()

                    def mmU(e, pU=pU, Wsb=Wsb, cur=list(cur)):
                        for h in range(8):
                            ins = e.matmul(pU[:, h * 64:(h + 1) * 64], lhsT=cur[h][4], rhs=Wsb[:, h * 64:(h + 1) * 64], start=True, stop=True)
                        return ins
                    op("pe", mmU, reads=[RWsb] + [cur[h][5] for h in range(8)], writes=[RpU])
                    op("dve", lambda e, Usb=Usb, pU=pU: e.tensor_copy(out=Usb[:], in_=pU[:]), reads=[RpU], writes=[RUsb])
                    yield 'c'
                    pY, RpY = pbanks.next()

                    def mmY(e, pY=pY, RT=RT, GK=GK, GB=GB, Vtm=Vtm, Usb=Usb):
                        for cc in range(4):
                            e.matmul(pY[:, cc * 128:(cc + 1) * 128], lhsT=RT[:, cc, :], rhs=Sbd[:, cc].rearrange("p h i -> p (h i)"), start=True, stop=False)
                            for h2 in range(2):
                                h = 2 * cc + h2
                                e.matmul(pY[:, h * 64:(h + 1) * 64], lhsT=GK[:, cc, h2, 1, :], rhs=Vtm[:, h * 64:(h + 1) * 64], start=False, stop=False)
                                ins = e.matmul(pY[:, h * 64:(h + 1) * 64], lhsT=GB[:, cc, h2, 1, :], rhs=Usb[:, h * 64:(h + 1) * 64], start=False, stop=True)
                        return ins
                    op("pe", mmY, reads=[RRT, RSbd, RGK, RGB, RVtm, RUsb], writes=[RpY])
                    op("act", lambda e, Ysb=Ysb, pY=pY: e.copy(out=Ysb[:], in_=pY[:]), reads=[RpY], writes=[RYsb])
                    yield 'c'
                    pS, RpS = pbanks.next()

                    def mmS(e, pS=pS, Khat=Khat, Bhat=Bhat, Vtm=Vtm, Usb=Usb):
                        for cc in range(4):
                            cs = slice(cc * 128, (cc + 1) * 128)
                            e.matmul(pS[:, cs], lhsT=Khat[:, cs], rhs=Vtm[:, cs], start=True, stop=False)
                            ins = e.matmul(pS[:, cs], lhsT=Bhat[:, cs], rhs=Usb[:, cs], start=False, stop=True)
                        return ins
                    op("pe", mmS, reads=[RKhat, RBhat, RVtm, RUsb], writes=[RpS])
                    Sf = St[:].rearrange("p c h i -> p c (h i)")
                    op("dve", lambda e, pS=pS: e.tensor_tensor(out=stmp[:].rearrange("p c h i -> p (c h i)"), in0=pS[:], in1=bdm[:].rearrange("p c h i -> p (c h i)"), op=ALU.mult),
                       reads=[RpS, Rbdm], writes=[Rstmp])
                    op("dve", lambda e, gC=gC: e.tensor_tensor(out=Sf, in0=Sf, in1=bc(gC[:].unsqueeze(2), [128, 4, 128]), op=ALU.mult), reads=[RSt, RgC], writes=[RSt])
                    op("dve", lambda e: e.tensor_tensor(out=St[:].rearrange("p c h i -> p (c h i)"), in0=St[:].rearrange("p c h i -> p (c h i)"),
                                                        in1=stmp[:].rearrange("p c h i -> p (c h i)"), op=ALU.add), reads=[RSt, Rstmp], writes=[RSt])
                    op("act", lambda e: e.copy(out=Sbd[:].rearrange("p c h i -> p (c h i)"), in_=St[:].rearrange("p c h i -> p (c h i)")), reads=[RSt], writes=[RSbd])
                    yield 'c'
                    if d == 0:
                        S.dma("pool", yf[s, t0:t0 + C, :], Ysb[:], reads=[RYsb], writes=[self.dres(("yf", s))])
                        S.dma("pool", bon[s, t0:t0 + C, :], bonv[:], reads=[Rbonv], writes=[self.dres(("bon", s))])
                    else:
                        yfl, Ryfl = yfls.next()
                        bon0, Rbon0 = bon0s.next()
                        S.dma("sp", yfl[:], yf[s, t0:t0 + C, :], reads=[self.dres(("yf", s))], writes=[Ryfl])
                        S.dma("sp", bon0[:], bon[s, t0:t0 + C, :], reads=[self.dres(("bon", s))], writes=[Rbon0])
                        sqv, Rsqv = sqvs.next()
                        yc, Ryc = ycs.next()
                        vv, Rvv = vvs.next()
                        st, Rst = stat.next()
                        yo, Ryo = yos.next()
                        y3 = lambda t_: t_[:].rearrange("p (h i) -> p h i", i=64)
                        b3 = lambda a_: bc(a_.unsqueeze(2), [128, 8, 64])
                        op("dve", lambda e, Ysb=Ysb, yfl=yfl: e.tensor_tensor(out=Ysb[:], in0=Ysb[:], in1=yfl[:], op=ALU.add), reads=[RYsb, Ryfl], writes=[RYsb])
                        op("pool", lambda e, sqv=sqv, Ysb=Ysb: e.tensor_tensor(out=sqv[:], in0=Ysb[:], in1=Ysb[:], op=ALU.mult), reads=[RYsb], writes=[Rsqv])
                        op("dve", lambda e, st=st, Ysb=Ysb: e.tensor_reduce(out=st[:, 0, :], in_=y3(Ysb), axis=AX.X, op=ALU.add), reads=[RYsb], writes=[Rst])
                        op("dve", lambda e, st=st, sqv=sqv: e.tensor_reduce(out=st[:, 1, :], in_=y3(sqv), axis=AX.X, op=ALU.add), reads=[Rsqv, Rst], writes=[Rst])
                        op("dve", lambda e, st=st: e.tensor_scalar(out=st[:, 2, :], in0=st[:, 0, :], scalar1=1.0 / 64, scalar2=None, op0=ALU.mult), reads=[Rst], writes=[Rst])
                        op("dve", lambda e, st=st: e.tensor_tensor(out=st[:, 3, :], in0=st[:, 2, :], in1=st[:, 2, :], op=ALU.mult), reads=[Rst], writes=[Rst])
                        op("dve", lambda e, st=st: e.scalar_tensor_tensor(out=st[:, 4, :], in0=st[:, 1, :], scalar=1.0 / 64, in1=st[:, 3, :], op0=ALU.mult, op1=ALU.subtract),
                           reads=[Rst], writes=[Rst])
                        op("act", lambda e, st=st: e.activation(out=st[:, 5, :], in_=st[:, 4, :], func=AF.Sqrt, bias=gne[:, 0:1]), reads=[Rst, Rgne], writes=[Rst])
                        op("dve", lambda e, st=st: e.reciprocal(out=st[:, 5, :], in_=st[:, 5, :]), reads=[Rst], writes=[Rst])
                        yield 'c'
                        op("dve", lambda e, yc=yc, Ysb=Ysb, st=st: e.tensor_tensor(out=y3(yc), in0=y3(Ysb), in1=b3(st[:, 2, :]), op=ALU.subtract), reads=[RYsb, Rst], writes=[Ryc])
                        op("dve", lambda e, yc=yc, st=st: e.tensor_tensor(out=y3(yc), in0=y3(yc), in1=b3(st[:, 5, :]), op=ALU.mult), reads=[Ryc, Rst], writes=[Ryc])
                        op("pool", lambda e, yc=yc: e.tensor_tensor(out=yc[:], in0=yc[:], in1=gng[:], op=ALU.mult), reads=[Ryc, Rgng], writes=[Ryc])
                        op("pool", lambda e, yc=yc: e.tensor_tensor(out=yc[:], in0=yc[:], in1=gnb[:], op=ALU.add), reads=[Ryc, Rgnb], writes=[Ryc])
                        yield 'c'
                        op("dve", lambda e, bonv=bonv, bon0=bon0: e.tensor_tensor(out=bonv[:], in0=bonv[:], in1=bon0[:], op=ALU.add), reads=[Rbonv, Rbon0], writes=[Rbonv])
                        op("dve", lambda e, vv=vv, Vtm=Vtm, bonv=bonv: e.tensor_tensor(out=y3(vv), in0=Vtm.rearrange("p (h i) -> p h i", i=64), in1=b3(bonv[:]), op=ALU.mult), reads=[RVtm, Rbonv], writes=[Rvv])
                        op("pool", lambda e, yc=yc, vv=vv: e.tensor_tensor(out=yc[:], in0=yc[:], in1=vv[:], op=ALU.add), reads=[Ryc, Rvv], writes=[Ryc])
                        op("dve", lambda e, yo=yo, yc=yc, gtm=gtm: e.tensor_tensor(out=yo[:], in0=yc[:], in1=gtm[:], op=ALU.mult), reads=[Ryc, Rgtm], writes=[Ryo])
                        S.dma("pool", yr[s, t0:t0 + C, :], yo[:], reads=[Ryo], writes=[self.dres(("yr", s))])

                def run_until(g, tags):
                    while True:
                        try:
                            t_ = next(g)
                        except StopIteration:
                            return None
                        if t_ in tags:
                            return t_
                order = list(order)
                gens = [body(ci) for ci in order]
                run_until(gens[0], ('B',))
                prev = None
                for i_ in range(len(gens)):
                    g_cur = gens[i_]
                    g_nxt = gens[i_ + 1] if i_ + 1 < len(gens) else None
                    cur_done = False
                    if prev is not None:
                        prev_done = False
                        while not prev_done:
                            if run_until(prev, ('c',)) is None:
                                prev_done = True
                            if not cur_done and run_until(g_cur, ('b', 'C')) == 'C':
                                cur_done = True
                    nxt_done = g_nxt is None
                    while not (cur_done and nxt_done):
                        if not cur_done and run_until(g_cur, ('b', 'C')) == 'C':
                            cur_done = True
                        if not nxt_done and run_until(g_nxt, ('a', 'B')) == 'B':
                            nxt_done = True
                    prev = g_cur
                run_until(prev, ())
        self.end()

    def phase_p3(self, l):
        S, T, NS = self.S, self.T, self.NS
        self.begin()
        rows = T // 64
        nblk = T // 128
        nq, nk, nv, yn = (self.scr[k] for k in ("nq", "nk", "nv", "yn"))
        rp, Rrp = self.tile("rp", [120, 31], F32)
        S.dma("sp", rp[:], self.w["rpb"][l].rearrange("h a b -> (h a) b"), writes=[Rrp])
        _, _, idf, Ridf = self.make_ident()
        Rt, RRt = self.tile("Rt", [31, 8, 15], F32)
        Jm, RJ = self.tile("Jm", [31, 160], F32)
        tps = self.rot("tps", [128, 512], F32, 4, ps=True)
        tp0, Rtp0 = tps.next()
        S.op("pe", lambda e: e.transpose(out=tp0[0:31, 0:120], in_=rp[:, :], identity=idf[0:120, 0:120]), reads=[Rrp, Ridf], writes=[Rtp0])
        S.op("dve", lambda e: e.tensor_copy(out=Rt[:].rearrange("p h a -> p (h a)"), in_=tp0[0:31, 0:120]), reads=[Rtp0], writes=[RRt])

        S.op("pool", lambda e: e.memset(Jm[:], 0.0), writes=[RJ])
        S.op("pool", lambda e: e.affine_select(out=Jm[:], in_=Jm[:], pattern=[[-1, 160]], compare_op=ALU.not_equal, fill=1.0, base=48, channel_multiplier=1),
             reads=[RJ], writes=[RJ])
        TE0, RTE0 = self.tile("TE0", [64, 8, 15, 64], BF16)
        for q0 in range(0, 64, 4):
            tp, Rtp = tps.next()

            def mmT(e, tp=tp, q0=q0):
                for qi in range(4):
                    qc = q0 + qi
                    ins = e.matmul(tp[0:64, qi * 120:(qi + 1) * 120], lhsT=Jm[:, 63 - qc:127 - qc], rhs=Rt[:].rearrange("p h a -> p (h a)"),
                                   start=True, stop=True)
                return ins
            S.op("pe", mmT, reads=[RJ, RRt], writes=[Rtp])
            S.op("act", lambda e, tp=tp, q0=q0: e.activation(out=TE0[:, :, :, q0:q0 + 4].rearrange("p h a q -> p (h a) q"),
                                                             in_=tp[0:64, 0:480].rearrange("p (q x) -> p x q", q=4), func=AF.Exp),
                 reads=[Rtp], writes=[RTE0])
        A, RA = self.tile("mA", [128, 64], F32)
        Q, RQ = self.tile("mQ", [128, 64], F32)
        Q2, RQ2 = self.tile("mQ2", [128, 64], F32)
        cm, Rcm = self.tile("cm", [128, 64], F32)

        def io(e):
            e.iota(A[0:64, :], pattern=[[-1, 64]], base=0, channel_multiplier=1, allow_small_or_imprecise_dtypes=True)
            e.iota(A[64:128, :], pattern=[[-1, 64]], base=0, channel_multiplier=1, allow_small_or_imprecise_dtypes=True)
            return e.iota(Q[:], pattern=[[1, 64]], base=0, channel_multiplier=0, allow_small_or_imprecise_dtypes=True)
        S.op("pool", io, writes=[RA, RQ])
        S.op("dve", lambda e: e.tensor_scalar(out=Q2[:], in0=Q[:], scalar1=8.0, scalar2=56.0, op0=ALU.max, op1=ALU.min), reads=[RQ], writes=[RQ2])
        S.op("dve", lambda e: e.tensor_tensor(out=Q[:], in0=Q[:], in1=Q2[:], op=ALU.subtract), reads=[RQ, RQ2], writes=[RQ])
        S.op("dve", lambda e: e.tensor_tensor(out=A[:], in0=A[:], in1=Q[:], op=ALU.add), reads=[RA, RQ], writes=[RA])
        S.op("dve", lambda e: e.tensor_single_scalar(out=Q[:], in_=A[:], scalar=-8.0, op=ALU.is_ge), reads=[RA], writes=[RQ])
        S.op("dve", lambda e: e.tensor_single_scalar(out=Q2[:], in_=A[:], scalar=7.0, op=ALU.is_le), reads=[RA], writes=[RQ2])
        S.op("dve", lambda e: e.tensor_tensor(out=cm[:], in0=Q[:], in1=Q2[:], op=ALU.mult), reads=[RQ, RQ2], writes=[Rcm])
        TE, RTE = self.tile("TE", [128, 8, 14, 64], BF16)
        S.op("dve", lambda e: e.tensor_tensor(out=TE0[:].rearrange("p h a q -> p (h a) q"), in0=TE0[:].rearrange("p h a q -> p (h a) q"),
                                              in1=cm[0:64, :].unsqueeze(1).broadcast_to([64, 120, 64]), op=ALU.mult),
             reads=[RTE0, Rcm], writes=[RTE0])
        S.op("pool", lambda e: e.tensor_copy(out=TE[0:64, :, :, :].rearrange("p h a q -> p h (a q)"),
                                             in_=TE0[:, :, 0:14, :].rearrange("p h a q -> p h (a q)")), reads=[RTE0], writes=[RTE])
        S.dma("sp", TE[64:128, :, :, :].rearrange("p h a q -> p h (a q)"), TE0[:, :, 1:15, :].rearrange("p h a q -> p h (a q)"),
              reads=[RTE0], writes=[RTE])
        qT, RqT = self.tile("nqT", [128, 4, T], BF16)
        kT, RkT = self.tile("nkT", [128, 4, T], BF16)
        Ve, RVe = self.tile("nVe", [128, nblk, 8, 65], BF16)
        Vo, RVo = self.tile("nVo", [128, nblk, 8, 65], BF16)
        Vs, RVs = self.tile("nVs", [128, nblk // 2, 512], BF16)
        S.op("pool", lambda e: e.memset(Ve[:], 1.0), writes=[RVe])
        S.op("pool", lambda e: e.memset(Vo[:], 1.0), writes=[RVo])
        pss = Rot([(t_[:].rearrange("p (h q) -> p h q", q=64), r_) for (t_, r_) in tps.items])
        pos_raw = self.rot("ops", [128, 512], F32, 4, ps=True)
        Ets = self.rot("Et", [128, 8, 64], BF16, 3)
        Es = self.rot("E", [128, 4, 8, 64], BF16, 2)
        recs = self.rot("rec", [64, 8], F32, 2)
        outs = self.rot("yno", [64, 8, 64], BF16, 3)
        for s in range(NS):
            S.dma("sp", qT[:], nq[s].rearrange("(c p) t -> p c t", p=128), reads=[self.dres(("nq", s))], writes=[RqT])
            S.dma("sp", kT[:], nk[s].rearrange("(c p) t -> p c t", p=128), reads=[self.dres(("nk", s))], writes=[RkT])
            hb = nblk // 2
            for (Vx, RVx, off, nb_tot) in ((Ve, RVe, 0, nblk), (Vo, RVo, 64, nblk - 1)):
                for b0 in range(0, nb_tot, hb):
                    nb_ = min(hb, nb_tot - b0)
                    S.dma("sp", Vs[:, 0:nb_, :], nv[s, off + b0 * 128:off + (b0 + nb_) * 128, :].rearrange("(b p) c -> p b c", p=128),
                          reads=[self.dres(("nv", s))], writes=[RVs])
                    S.op("pool", lambda e, Vx=Vx, b0=b0, nb_=nb_: e.tensor_copy(
                        out=Vx[:, b0:b0 + nb_, :, 0:64].rearrange("p b h n -> p (b h) n"),
                        in_=Vs[:, 0:nb_, :].rearrange("p b (h n) -> p (b h) n", n=64)), reads=[RVs], writes=[RVx])
            pending = None
            for i in range(rows):
                rs = min(max(i - 4, 0), rows - 8)
                E, RE = Es.next()
                for blk in range(4):
                    kr = rs + 2 * blk
                    tok0 = kr * 64
                    dib = kr - i + 7
                    psA, RpsA = pss.next()
                    psB, RpsB = pss.next()
                    Et, REt = Ets.next()

                    def mms(e, psA=psA, psB=psB, tok0=tok0, i=i):
                        for h in range(8):
                            pb = (h % 2) * 64
                            ps = psA if h % 2 == 0 else psB
                            ins = e.matmul(ps[:, h // 2, :], lhsT=kT[pb:pb + 64, h // 2, tok0:tok0 + 128], rhs=qT[pb:pb + 64, h // 2, i * 64:(i + 1) * 64],
                                           start=True, stop=True)
                        return ins
                    S.op("pe", mms, reads=[RkT, RqT], writes=[RpsA, RpsB])
                    S.op("act", lambda e, psA=psA, Et=Et: e.activation(out=Et[:, 0:4, :], in_=psA[:, 0:4, :], func=AF.Exp, scale=0.125), reads=[RpsA], writes=[REt])
                    S.op("act", lambda e, psB=psB, Et=Et: e.activation(out=Et[:, 4:8, :], in_=psB[:, 0:4, :], func=AF.Exp, scale=0.125), reads=[RpsB], writes=[REt])
                    S.op("dve", lambda e, Et=Et, E=E, blk=blk, dib=dib: e.tensor_tensor(
                        out=E[:, blk, :, :].rearrange("p (two c) q -> p two c q", two=2), in0=Et[:].rearrange("p (two c) q -> p two c q", two=2),
                        in1=TE[:].rearrange("p (c two) a q -> p two c a q", two=2)[:, :, :, dib, :], op=ALU.mult),
                         reads=[REt, RTE], writes=[RE])
                def finish(i=i, rs=rs, E=E, RE=RE, s=s):
                    poA_, RpoA = pos_raw.next()
                    poB_, RpoB = pos_raw.next()
                    poA = poA_[0:64, 0:260].rearrange("p (h n) -> p h n", n=65)
                    poB = poB_[0:64, 0:260].rearrange("p (h n) -> p h n", n=65)

                    def mmo(e, E=E, rs=rs, poA=poA, poB=poB):
                        for h in range(8):
                            po = poA if h < 4 else poB
                            for blk in range(4):
                                kr = rs + 2 * blk
                                Vx = Ve if kr % 2 == 0 else Vo
                                ins = e.matmul(po[:, h % 4, :], lhsT=E[:, blk, (h % 2) * 4 + h // 2, :], rhs=Vx[:, kr // 2, h, :], start=(blk == 0), stop=(blk == 3))
                        return ins
                    S.op("pe", mmo, reads=[RE, RVe, RVo], writes=[RpoA, RpoB])
                    rec, Rrec = recs.next()
                    o, Ro = outs.next()
                    S.op("dve", lambda e, rec=rec, poA=poA: e.reciprocal(out=rec[:, 0:4], in_=poA[:, :, 64]), reads=[RpoA], writes=[Rrec])
                    S.op("dve", lambda e, rec=rec, poB=poB: e.reciprocal(out=rec[:, 4:8], in_=poB[:, :, 64]), reads=[RpoB], writes=[Rrec])
                    S.op("dve", lambda e, rec=rec, poA=poA, o=o: e.tensor_tensor(out=o[:, 0:4, :], in0=poA[:, :, 0:64],
                                                                                in1=rec[:, 0:4].unsqueeze(2).broadcast_to([64, 4, 64]), op=ALU.mult),
                         reads=[RpoA, Rrec], writes=[Ro])
                    S.op("dve", lambda e, rec=rec, poB=poB, o=o: e.tensor_tensor(out=o[:, 4:8, :], in0=poB[:, :, 0:64],
                                                                                in1=rec[:, 4:8].unsqueeze(2).broadcast_to([64, 4, 64]), op=ALU.mult),
                         reads=[RpoB, Rrec], writes=[Ro])
                    S.dma("pool", yn[s, i * 64:(i + 1) * 64, :], o[:].rearrange("p h n -> p (h n)"), reads=[Ro], writes=[self.dres(("yn", s))])
                if pending is not None:
                    pending()
                pending = finish
            pending()
            pending = None
        self.end()

    def xupdate(self, xt, Rxt, nb, aT, RaT, nkc, W, RW, pss):
        S = self.S
        for b in range(nb):
            for half in range(2):
                ps, Rps = pss.next()

                def mm(e, ps=ps, b=b, half=half):
                    for kc in range(nkc):
                        ins = e.matmul(ps[:], lhsT=aT[:, kc, b * 128:(b + 1) * 128], rhs=W[:, kc, half * 512:(half + 1) * 512],
                                       start=(kc == 0), stop=(kc == nkc - 1))
                    return ins
                S.op("pe", mm, reads=[RaT, RW], writes=[Rps])
                S.op("dve", lambda e, ps=ps, b=b, half=half: e.tensor_tensor(
                    out=xt[:, b, half * 512:(half + 1) * 512], in0=xt[:, b, half * 512:(half + 1) * 512], in1=ps[:], op=ALU.add),
                    reads=[Rps, Rxt], writes=[Rxt])

    def phase_p4(self, l):
        S, T, NS = self.S, self.T, self.NS
        self.begin()
        TT = 512
        wbr, Rwbr = self.tile("wbr", [128, 8, D], BF16)
        wout, Rwout = self.tile("wout", [128, 8, D], BF16)
        stg = self.rot("wstg", [128, 2048], F32, 2)
        ident, Rid, _, _ = self.make_ident()
        self.load_weight(wbr, Rwbr, self.w["w_br_rwkv"][l], 512, D, stg, kc0=0)
        self.load_weight(wbr, Rwbr, self.w["w_br_nat"][l], 512, D, stg, kc0=4)
        self.load_weight(wout, Rwout, self.w["w_out"][l], D, D, stg)
        xts = self.rot("xt", [128, 4, D], F32, 2)
        yts = self.rot("yt", [128, 4, 1024], BF16, 2)
        yTs = self.rot("yT", [128, 8, TT], BF16, 2)
        sgts = self.rot("sgt", [128, 16, TT], BF16, 2)
        mTs = self.rot("mT", [128, 8, TT], BF16, 2)
        m1s = self.rot("m1", [128, TT], F32, 2)
        m2s = self.rot("m2", [128, TT], F32, 2)
        pTs = self.rot("pT", [128, 1024], BF16, 2, ps=True)
        pss = self.rot("ps", [128, 512], F32, 6, ps=True)
        xsrc = self.x_in if l == 0 else self.scr["xres"]
        xres, yr, yn, sg = (self.scr[k] for k in ("xres", "yr", "yn", "sg"))
        for s in range(NS):
            for t0 in range(0, T, TT):
                xt, Rxt = xts.next()
                yt, Ryt = yts.next()
                yT, RyT = yTs.next()
                sgt, Rsgt = sgts.next()
                mT, RmT = mTs.next()
                Rx = self.dres(("xres", s, t0))
                S.dma("sp", yt[:, :, 0:512], yr[s, t0:t0 + TT, :].rearrange("(b p) c -> p b c", p=128), reads=[self.dres(("yr", s))], writes=[Ryt])
                S.dma("sp", yt[:, :, 512:1024], yn[s, t0:t0 + TT, :].rearrange("(b p) c -> p b c", p=128), reads=[self.dres(("yn", s))], writes=[Ryt])
                S.dma("sp", sgt[:], sg[s, :, t0:t0 + TT].rearrange("(c p) t -> p c t", p=128), reads=[self.dres(("sg", s))], writes=[Rsgt])
                S.dma("sp", xt[:], xsrc[s, t0:t0 + TT, :].rearrange("(b p) d -> p b d", p=128), reads=([Rx] if l > 0 else []), writes=[Rxt])
                for b in range(4):
                    pT, RpT = pTs.next()

                    def tr(e, pT=pT, b=b, yt=yt):
                        for c in range(8):
                            ins = e.transpose(out=pT[:, c * 128:(c + 1) * 128], in_=yt[:, b, c * 128:(c + 1) * 128], identity=ident[:])
                        return ins
                    S.op("pe", tr, reads=[Ryt, Rid], writes=[RpT])
                    S.op("act", lambda e, pT=pT, b=b, yT=yT: e.copy(out=yT[:, :, b * 128:(b + 1) * 128], in_=pT[:].rearrange("p (k t) -> p k t", k=8)),
                         reads=[RpT], writes=[RyT])
                for oc in range(8):
                    ps1, Rps1 = pss.next()
                    ps2, Rps2 = pss.next()
                    m1, Rm1 = m1s.next()
                    m2, Rm2 = m2s.next()

                    def mm(e, ps1=ps1, ps2=ps2, oc=oc, yT=yT):
                        for kc in range(4):
                            e.matmul(ps1[:], lhsT=wbr[:, kc, oc * 128:(oc + 1) * 128], rhs=yT[:, kc, :], start=(kc == 0), stop=(kc == 3))
                        for kc in range(4):
                            ins = e.matmul(ps2[:], lhsT=wbr[:, 4 + kc, oc * 128:(oc + 1) * 128], rhs=yT[:, 4 + kc, :], start=(kc == 0), stop=(kc == 3))
                        return ins
                    S.op("pe", mm, reads=[Rwbr, RyT], writes=[Rps1, Rps2])
                    S.op("dve", lambda e, m1=m1, ps1=ps1, oc=oc, sgt=sgt: e.tensor_tensor(out=m1[:], in0=ps1[:], in1=sgt[:, oc, :], op=ALU.mult),
                         reads=[Rps1, Rsgt], writes=[Rm1])
                    S.op("dve", lambda e, m2=m2, ps2=ps2, oc=oc, sgt=sgt: e.tensor_tensor(out=m2[:], in0=ps2[:], in1=sgt[:, 8 + oc, :], op=ALU.mult),
                         reads=[Rps2, Rsgt], writes=[Rm2])
                    S.op("pool", lambda e, m1=m1, m2=m2, oc=oc, mT=mT: e.tensor_tensor(out=mT[:, oc, :], in0=m1[:], in1=m2[:], op=ALU.add),
                         reads=[Rm1, Rm2], writes=[RmT])
                self.xupdate(xt, Rxt, 4, mT, RmT, 8, wout, Rwout, pss)
                S.dma("pool", xres[s, t0:t0 + TT, :].rearrange("(b p) d -> p b d", p=128), xt[:], reads=[Rxt], writes=[Rx])
        self.end()

    def phase_p5(self, l):
        S, T, NS = self.S, self.T, self.NS
        self.begin()
        TT = 512
        wq, Rwq = self.tile("wq", [128, 8, D], BF16)
        wo, Rwo = self.tile("wo", [128, 8, D], BF16)
        wkv, Rwkv = self.tile("wkv", [128, 8, 2 * D], BF16)
        stg = self.rot("wstg", [128, 2048], F32, 2)
        ident, Rid, _, _ = self.make_ident()
        gb, Rgb = self.tile("gb", [128, D], F32)
        gm, Rgm = self.tile("gm", [128, D], F32)
        ones, Rones = self.tile("ones", [128, 128], BF16)
        S.op("pool", lambda e: e.memset(ones[:], 1.0), writes=[Rones])
        self.load_bcast(gb, Rgb, self.w["norm_x"][l], D)
        self.load_bcast(gm, Rgm, self.w["norm_mem"][l], D)
        self.load_weight(wkv, Rwkv, self.w["w_xkv"][l], D, 2 * D, stg)
        self.load_weight(wq, Rwq, self.w["w_xq"][l], D, D, stg)
        self.load_weight(wo, Rwo, self.w["w_xo"][l], D, D, stg)
        tmp = self.norm_tmp()
        xts = self.rot("xt", [128, 4, D], F32, 2)
        hTs = self.rot("hT", [128, 8, TT], BF16, 2)
        qTs = self.rot("qT", [128, 8, TT], BF16, 1)
        oTs = self.rot("oT", [128, 8, TT], BF16, 2)
        Es = self.rot("E", [128, 2, TT], BF16, 2)
        rdens = self.rot("rden", [128, TT], F32, 2)
        mt, Rmt = self.tile("memt", [128, 2, D], F32)
        mnT, RmnT = self.tile("memnT", [128, 8, NMEM], BF16)
        kT, RkT = self.tile("kT", [128, 8, NMEM], BF16)
        Vm, RVm = self.tile("Vm", [128, 2, D], BF16)
        pss = self.rot("ps", [128, 512], F32, 6, ps=True)
        xres = self.scr["xres"]
        for s in range(NS):
            S.dma("sp", mt[:], self.mem_in[s].rearrange("(b p) d -> p b d", p=128), writes=[Rmt])
            for b in range(2):
                self.norm_to_hT(mt, Rmt, b, gm, Rgm, mnT, RmnT, b * 128, ident, Rid, tmp)
            for cc in range(8):
                ps, Rps = pss.next()

                def mmk(e, ps=ps, cc=cc):
                    for kc in range(8):
                        ins = e.matmul(ps[:, 0:NMEM], lhsT=wkv[:, kc, cc * 128:(cc + 1) * 128], rhs=mnT[:, kc, :], start=(kc == 0), stop=(kc == 7))
                    return ins
                S.op("pe", mmk, reads=[Rwkv, RmnT], writes=[Rps])
                S.op("dve", lambda e, ps=ps, cc=cc: e.tensor_copy(out=kT[:, cc, :], in_=ps[:, 0:NMEM]), reads=[Rps], writes=[RkT])
            for mb in range(2):
                for half in range(2):
                    ps, Rps = pss.next()

                    def mmv(e, ps=ps, mb=mb, half=half):
                        for kc in range(8):
                            ins = e.matmul(ps[:], lhsT=mnT[:, kc, mb * 128:(mb + 1) * 128], rhs=wkv[:, kc, D + half * 512:D + (half + 1) * 512],
                                           start=(kc == 0), stop=(kc == 7))
                        return ins
                    S.op("pe", mmv, reads=[Rwkv, RmnT], writes=[Rps])
                    S.op("dve", lambda e, ps=ps, mb=mb, half=half: e.tensor_copy(out=Vm[:, mb, half * 512:(half + 1) * 512], in_=ps[:]),
                         reads=[Rps], writes=[RVm])
            for t0 in range(0, T, TT):
                xt, Rxt = xts.next()
                hT, RhT = hTs.next()
                qT, RqT = qTs.next()
                oT, RoT = oTs.next()
                Rx = self.dres(("xres", s, t0))
                S.dma("sp", xt[:], xres[s, t0:t0 + TT, :].rearrange("(b p) d -> p b d", p=128), reads=[Rx], writes=[Rxt])
                for b in range(4):
                    self.norm_to_hT(xt, Rxt, b, gb, Rgb, hT, RhT, b * 128, ident, Rid, tmp)
                for cc in range(8):
                    ps, Rps = pss.next()

                    def mmq(e, ps=ps, cc=cc, hT=hT):
                        for kc in range(8):
                            ins = e.matmul(ps[:], lhsT=wq[:, kc, cc * 128:(cc + 1) * 128], rhs=hT[:, kc, :], start=(kc == 0), stop=(kc == 7))
                        return ins
                    S.op("pe", mmq, reads=[Rwq, RhT], writes=[Rps])
                    S.op("dve", lambda e, ps=ps, cc=cc, qT=qT: e.tensor_copy(out=qT[:, cc, :], in_=ps[:]), reads=[Rps], writes=[RqT])
                for hd in range(4):
                    E, RE = Es.next()
                    rden, Rrden = rdens.next()
                    for mb in range(2):
                        ps, Rps = pss.next()

                        def mms(e, ps=ps, mb=mb, hd=hd, qT=qT):
                            for j in range(2):
                                ins = e.matmul(ps[:], lhsT=kT[:, 2 * hd + j, mb * 128:(mb + 1) * 128], rhs=qT[:, 2 * hd + j, :], start=(j == 0), stop=(j == 1))
                            return ins
                        S.op("pe", mms, reads=[RkT, RqT], writes=[Rps])
                        S.op("act", lambda e, ps=ps, mb=mb, E=E: e.activation(out=E[:, mb, :], in_=ps[:], func=AF.Exp, scale=1.0 / 16.0),
                             reads=[Rps], writes=[RE])
                    ps, Rps = pss.next()

                    def mmd(e, ps=ps, E=E):
                        for mb in range(2):
                            ins = e.matmul(ps[:], lhsT=ones[:], rhs=E[:, mb, :], start=(mb == 0), stop=(mb == 1))
                        return ins
                    S.op("pe", mmd, reads=[Rones, RE], writes=[Rps])
                    S.op("dve", lambda e, ps=ps, rden=rden: e.reciprocal(out=rden[:], in_=ps[:]), reads=[Rps], writes=[Rrden])
                    for j in range(2):
                        ps, Rps = pss.next()

                        def mmo(e, ps=ps, hd=hd, j=j, E=E):
                            for mb in range(2):
                                c0 = hd * 256 + j * 128
                                ins = e.matmul(ps[:], lhsT=Vm[:, mb, c0:c0 + 128], rhs=E[:, mb, :], start=(mb == 0), stop=(mb == 1))
                            return ins
                        S.op("pe", mmo, reads=[RVm, RE], writes=[Rps])
                        S.op("dve", lambda e, ps=ps, hd=hd, j=j, oT=oT, rden=rden: e.tensor_tensor(out=oT[:, 2 * hd + j, :], in0=ps[:], in1=rden[:], op=ALU.mult),
                             reads=[Rps, Rrden], writes=[RoT])
                self.xupdate(xt, Rxt, 4, oT, RoT, 8, wo, Rwo, pss)
                S.dma("pool", xres[s, t0:t0 + TT, :].rearrange("(b p) d -> p b d", p=128), xt[:], reads=[Rxt], writes=[Rx])
        self.end()

    def phase_p6(self, l):
        S, T, NS = self.S, self.T, self.NS
        self.begin()
        TT = 256
        NB = TT // 128
        last = (l == self.DEPTH - 1)
        w1, Rw1 = self.tile("w1", [128, 8, DFF], BF16)
        w2, Rw2 = self.tile("w2", [128, 32, D], BF16)
        stg = self.rot("wstg", [128, 2048], F32, 2)
        ident, Rid, _, _ = self.make_ident()
        gb, Rgb = self.tile("gb", [128, D], F32)
        self.load_bcast(gb, Rgb, self.w["norm_ff"][l], D)
        if last:
            gf, Rgf = self.tile("gf", [128, D], F32)
            self.load_bcast(gf, Rgf, self.w["norm_final"], D)
        self.load_weight(w1, Rw1, self.w["w_ff1"][l], D, DFF, stg)
        self.load_weight(w2, Rw2, self.w["w_ff2"][l], DFF, D, stg)
        tmp = self.norm_tmp()
        xts = self.rot("xt", [128, NB, D], F32, 2)
        hTs = self.rot("hT", [128, 8, TT], BF16, 2)
        uTs = self.rot("uT", [128, 32, TT], BF16, 1)
        rls = self.rot("rl", [128, TT], BF16, 3)
        pss = self.rot("ps", [128, 512], F32, 6, ps=True)
        xres = self.scr["xres"]
        tiles_ = [(s, t0) for s in range(NS) for t0 in range(0, T, TT)]

        def prep_(idx):
            s, t0 = tiles_[idx]
            xt, Rxt = xts.next()
            Rx = self.dres(("xres", s, t0 // 512 * 512))
            S.dma("sp", xt[:], xres[s, t0:t0 + TT, :].rearrange("(b p) d -> p b d", p=128), reads=[Rx], writes=[Rxt])
            return xt, Rxt, Rx, [self.norm_A(xt, Rxt, b, gb, Rgb, tmp) for b in range(NB)]
        nxt_ = prep_(0)
        for idx_, (s, t0) in enumerate(tiles_):
            if True:
                xt, Rxt, Rx, hs_ = nxt_
                hT, RhT = hTs.next()
                uT, RuT = uTs.next()
                for b in range(NB):
                    self.norm_B(hs_[b][0], hs_[b][1], hT, RhT, b * 128, ident, Rid, tmp)
                nxt_ = prep_(idx_ + 1) if idx_ + 1 < len(tiles_) else None
                for fc in range(32):
                    ps, Rps = pss.next()
                    rl, Rrl = rls.next()

                    def mm1(e, ps=ps, fc=fc, hT=hT):
                        for kc in range(8):
                            ins = e.matmul(ps[:, 0:TT], lhsT=w1[:, kc, fc * 128:(fc + 1) * 128], rhs=hT[:, kc, :], start=(kc == 0), stop=(kc == 7))
                        return ins
                    S.op("pe", mm1, reads=[Rw1, RhT], writes=[Rps])
                    S.op("act", lambda e, ps=ps, rl=rl: e.activation(out=rl[:], in_=ps[:, 0:TT], func=AF.Relu), reads=[Rps], writes=[Rrl])
                    S.op("pool", lambda e, rl=rl, fc=fc, uT=uT: e.tensor_tensor(out=uT[:, fc, :], in0=rl[:], in1=rl[:], op=ALU.mult),
                         reads=[Rrl], writes=[RuT])
                self.xupdate(xt, Rxt, NB, uT, RuT, 32, w2, Rw2, pss)
                if not last:
                    S.dma("pool", xres[s, t0:t0 + TT, :].rearrange("(b p) d -> p b d", p=128), xt[:], reads=[Rxt], writes=[Rx])
                else:
                    for b in range(NB):
                        junk, Rjunk = tmp["junk"].next()
                        ss, Rss = tmp["ss"].next()
                        S.op("act", lambda e, junk=junk, ss=ss, b=b, xt=xt: e.activation(out=junk[:], in_=xt[:, b, :], func=AF.Square, accum_out=ss[:, 0:1]),
                             reads=[Rxt], writes=[Rjunk, Rss])
                        S.op("act", lambda e, ss=ss: e.activation(out=ss[:, 1:2], in_=ss[:, 0:1], func=AF.Sqrt, scale=1.0 / D, bias=self.eps_t[:, 0:1]),
                             reads=[Rss, self.Reps], writes=[Rss])
                        S.op("dve", lambda e, ss=ss: e.reciprocal(out=ss[:, 2:3], in_=ss[:, 1:2]), reads=[Rss], writes=[Rss])
                        S.op("dve", lambda e, ss=ss, b=b, xt=xt: e.scalar_tensor_tensor(out=xt[:, b, :], in0=xt[:, b, :], scalar=ss[:, 2:3], in1=gf[:],
                                                                                         op0=ALU.mult, op1=ALU.mult),
                             reads=[Rxt, Rss, Rgf], writes=[Rxt])
                    Ry = self.dres(("y", s))
                    S.dma("pool", self.y_out[s, t0:t0 + TT, :].rearrange("(b p) d -> p b d", p=128), xt[:], reads=[Rxt], writes=[Ry])
        self.end()


_CACHE = {}


def _get_nc(T, NS, DEPTH, dbg=(), stop=None):
    key = (T, NS, DEPTH, tuple(dbg), stop)
    if key not in _CACHE:
        _CACHE[key] = Builder(T, NS, DEPTH, dbg, stop).build()
    return _CACHE[key]


def kernel(**inputs):
    NCORES, NS, T, DEPTH = 8, 2, 4096, 2
    xs = np.concatenate([np.asarray(inputs["x_prompt"], np.float32), np.asarray(inputs["x_sample"], np.float32)], 0)
    ms = np.concatenate([np.asarray(inputs["mem_prompt"], np.float32), np.asarray(inputs["mem_sample"], np.float32)], 0)
    nseq = xs.shape[0]
    slots = [[c, 8 + c if 8 + c < nseq else c] for c in range(NCORES)]
    wnames = [n for n, _ in WEIGHT_SPECS] + ["norm_final"]
    wmap = {n: np.ascontiguousarray(np.asarray(inputs[n], np.float32)) for n in wnames}
    nc = _get_nc(T, NS, DEPTH)
    in_maps = []
    for c in range(NCORES):
        m = dict(wmap)
        m["x"] = np.ascontiguousarray(xs[slots[c]])
        m["mem"] = np.ascontiguousarray(ms[slots[c]])
        in_maps.append(m)
    res = run_bass_kernel_spmd(nc, in_maps, core_ids=list(range(NCORES)))
    y = np.zeros_like(xs)
    for c in range(NCORES):
        yc = res.results[c]["y"]
        y[slots[c][0]] = yc[0]
        if slots[c][1] != slots[c][0]:
            y[slots[c][1]] = yc[1]
    nb = np.asarray(inputs["x_prompt"]).shape[0]
    return (y[:nb], y[nb:])
```

```python
import numpy as np
from contextlib import ExitStack
import concourse.bass as bass
import concourse.mybir as mybir
from concourse.bass_utils import run_bass_kernel_spmd

F32 = mybir.dt.float32
BF16 = mybir.dt.bfloat16
AF = mybir.ActivationFunctionType
ALU = mybir.AluOpType
AX = mybir.AxisListType

D = 1024
DIN = 5504
RW = 1920
NQ0 = 1920
NV0 = 2944
G0 = 3456
NMEM = 256
DFF = 4096
EPS = 1e-6
GN_EPS = 1e-5 * 64
CH = 128

EPOCH = 12000
ENGS = ("pe", "dve", "act", "pool", "sp")


class Res:
    __slots__ = ("name", "last_w", "readers", "dsem", "w_is_dma")

    def __init__(self, name=""):
        self.name = name
        self.last_w = None
        self.readers = []
        self.dsem = None
        self.w_is_dma = False


class Sched:
    def __init__(self, nc, stack):
        self.nc = nc
        self.stack = stack
        self.ops = {e: [] for e in ENGS}
        self.cnt = {e: 0 for e in ENGS}
        self.esems = {e: [] for e in ENGS}
        self.known = {e: {} for e in ENGS}
        self.sems = {}
        self.nsem = 0
        self.same_engine_raw = True
        self.dma_latest = {}
        self.free_dsems = []

    def new_sem(self, tag):
        h = self.stack.enter_context(self.nc.semaphore(f"{tag}_{self.nsem}"))
        sid = self.nsem
        self.nsem += 1
        self.sems[sid] = h
        return sid

    def _eng_point(self, eng):
        n = self.cnt[eng]
        self.cnt[eng] = n + 1
        ep, v = divmod(n, EPOCH)
        while len(self.esems[eng]) <= ep:
            self.esems[eng].append(self.new_sem(f"e{eng}"))
        return (self.esems[eng][ep], v + 1)

    def _dma_point(self, res):
        d = res.dsem
        if d is None or d[1] + 16 > EPOCH:
            if self.free_dsems and d is None:
                d = self.free_dsems.pop()
            else:
                d = [self.new_sem("d"), 0]
            res.dsem = d
        d[1] += 16
        self.dma_latest[d[0]] = d[1]
        return (d[0], d[1])

    def recycle(self, resources):
        seen = set()
        for r in resources:
            d = r.dsem
            if d is not None and id(d) not in seen and d[1] + 64 < EPOCH:
                seen.add(id(d))
                self.free_dsems.append(d)
            r.dsem = None

    def _need(self, eng, waits, pt):
        sid, val = pt
        if self.known[eng].get(sid, 0) >= val:
            return
        if waits.get(sid, 0) < val:
            waits[sid] = val

    def _deps(self, eng, reads, writes, is_dma=False):
        waits = {}
        for r in reads:
            if r.last_w is not None:
                w_eng = r.last_w[2]
                if w_eng != eng or is_dma or (self.same_engine_raw and eng != "pe"):
                    self._need(eng, waits, r.last_w[:2])
        for w in writes:
            if w.last_w is not None:
                w_eng = w.last_w[2]
                same_dma_group = is_dma and w.w_is_dma and not w.readers
                if (w_eng != eng or is_dma) and not same_dma_group:
                    self._need(eng, waits, w.last_w[:2])
            for (sid, val, r_eng) in w.readers:
                if r_eng != eng or is_dma:
                    self._need(eng, waits, (sid, val))
        for sid, val in waits.items():
            self.known[eng][sid] = val
        return list(waits.items())

    def op(self, eng, fn, reads=(), writes=()):
        waits = self._deps(eng, reads, writes)
        pt = self._eng_point(eng)
        self.ops[eng].append((waits, fn, pt[0], 1))
        for r in reads:
            r.readers.append((pt[0], pt[1], eng))
        for w in writes:
            w.last_w = (pt[0], pt[1], eng)
            w.readers = []
            w.w_is_dma = False
        return pt

    def dma(self, queue, out, in_, reads=(), writes=(), **kw):
        waits = self._deps(queue, reads, writes, is_dma=True)
        pt = self._dma_point(writes[0])

        def fn(e, out=out, in_=in_, kw=kw):
            return e.dma_start(out=out, in_=in_, **kw)
        self.ops[queue].append((waits, fn, pt[0], 16))
        for r in reads:
            r.readers.append((pt[0], pt[1], "dma"))
        for w in writes:
            w.last_w = (pt[0], pt[1], "dma")
            w.readers = []
            w.w_is_dma = True
            if w is not writes[0]:
                w.dsem = writes[0].dsem
        return pt

    def final_wait(self, eng, resources):
        waits = {}
        for r in resources:
            if r.last_w is not None:
                sid, val = r.last_w[:2]
                waits[sid] = max(waits.get(sid, 0), val)
        self.ops[eng].append((list(waits.items()), None, None, 0))

    def barrier(self):
        pts = {}
        for e in ENGS:
            n = self.cnt[e]
            if n > 0:
                ep, v = divmod(n - 1, EPOCH)
                pts[self.esems[e][ep]] = v + 1
        for sid, val in self.dma_latest.items():
            pts[sid] = max(pts.get(sid, 0), val)
        for e in ENGS:
            waits = []
            for sid, val in pts.items():
                if self.known[e].get(sid, 0) < val:
                    waits.append((sid, val))
                    self.known[e][sid] = val
            self.ops[e].append((waits, None, None, 0))

    def emit(self):
        nc = self.nc
        sems = self.sems

        def run(engname):
            def body(e):
                for (waits, fn, sid_, inc) in self.ops[engname]:
                    for sid, val in waits:
                        e.wait_ge(sems[sid], val)
                    if fn is None:
                        continue
                    ins = fn(e)
                    ins.then_inc(sems[sid_], inc)
            return body

        with nc.Block() as block:
            block.tensor(run("pe"))
            block.vector(run("dve"))
            block.scalar(run("act"))
            block.gpsimd(run("pool"))
            block.sync(run("sp"))
        self.ops = {e: [] for e in ENGS}


class Rot:
    def __init__(self, items):
        self.items = items
        self.i = 0

    def next(self):
        it = self.items[self.i % len(self.items)]
        self.i += 1
        return it


WEIGHT_SPECS = [
    ("norm_mix", (D,)), ("w_in", (D, DIN)), ("mu_prev", (RW,)), ("mu_next", (RW,)),
    ("w0", (2, 512)), ("w_up", (2, 64, 512)), ("a0", (2, 512)), ("a_up", (2, 64, 512)),
    ("g_up", (128, 512)), ("k_k", (512,)), ("k_a", (512,)), ("r_k", (8, 64)),
    ("gn_g", (512,)), ("gn_b", (512,)), ("rpb", (8, 15, 31)),
    ("w_br_rwkv", (512, D)), ("w_br_nat", (512, D)), ("w_out", (D, D)),
    ("norm_x", (D,)), ("norm_mem", (D,)), ("w_xq", (D, D)), ("w_xkv", (D, 2 * D)), ("w_xo", (D, D)),
    ("norm_ff", (D,)), ("w_ff1", (D, DFF)), ("w_ff2", (DFF, D)),
]


class Builder:
    def __init__(self, T, NS, DEPTH, dbg=(), stop=None, ext_in=(), phases=None):
        self.T, self.NS, self.DEPTH = T, NS, DEPTH
        self.ext_in = set(ext_in)
        self.phases = phases or ("p1", "p3", "p2", "p4", "p5", "p6")
        self.dbg = set(dbg)
        self.stop = stop
        self.nc = bass.Bass("TRN2", target_bir_lowering=False)
        nc = self.nc
        self.x_in = nc.dram_tensor("x", [NS, T, D], F32, kind="ExternalInput").ap()
        self.mem_in = nc.dram_tensor("mem", [NS, NMEM, D], F32, kind="ExternalInput").ap()
        self.w = {}
        for name, shp in WEIGHT_SPECS:
            self.w[name] = nc.dram_tensor(name, [DEPTH] + list(shp), F32, kind="ExternalInput").ap()
        self.w["norm_final"] = nc.dram_tensor("norm_final", [D], F32, kind="ExternalInput").ap()
        self.y_out = nc.dram_tensor("y", [NS, T, D], F32, kind="ExternalOutput").ap()
        self.scr = {}
        self.scr_res = {}

    def scratch(self, name, shape, dt):
        kind = "ExternalOutput" if name in self.dbg else ("ExternalInput" if name in self.ext_in else "Internal")
        t = self.nc.dram_tensor(name, shape, dt, kind=kind).ap()
        self.scr[name] = t
        return t

    def dres(self, key):
        r = self.scr_res.get(key)
        if r is None:
            r = Res(str(key))
            self.scr_res[key] = r
        return r

    def build(self):
        nc = self.nc
        T, NS = self.T, self.NS
        self.scratch("xres", [NS, T, D], F32)
        self.scratch("pf", [NS, RW, T], F32)
        self.scratch("nq", [NS, 512, T], BF16)
        self.scratch("nk", [NS, 512, T], BF16)
        self.scratch("nv", [NS, T, 512], BF16)
        self.scratch("sg", [NS, 2048, T], BF16)
        self.scratch("yr", [NS, T, 512], BF16)
        self.scratch("yn", [NS, T, 512], BF16)
        self.scratch("yf", [NS, T, 512], F32)
        self.scratch("bon", [NS, T, 8], F32)
        self.scratch("phd", [NS, T // 128, 128, 1920], F32)
        with ExitStack() as st:
            self.S = Sched(nc, st)
            done = False
            for l in range(self.DEPTH):
                for ph in self.phases:
                    getattr(self, "phase_" + ph)(l)
                    if self.stop == (l, ph):
                        done = True
                        break
                if done:
                    break
            with ExitStack() as ph:
                self.S.final_wait("pool", list(self.scr_res.values()))
                self.S.emit()
        return nc

    def begin(self):
        self.S.barrier()
        self.phc = getattr(self, "phc", 0) + 1
        self.ph = ExitStack()
        self.ph_res = []
        return self.ph

    def end(self):
        self.S.emit()
        self.S.recycle(self.ph_res)
        self.ph.close()

    def tile(self, name, shape, dt):
        t = self.ph.enter_context(self.nc.sbuf_tensor(f"{name}_{self.phc}", shape, dt))
        r = Res(name)
        self.ph_res.append(r)
        return t, r

    def psum(self, name, shape, dt):
        t = self.ph.enter_context(self.nc.psum_tensor(f"{name}_{self.phc}", shape, dt))
        r = Res(name)
        self.ph_res.append(r)
        return t, r

    def rot(self, name, shape, dt, n, ps=False):
        f = self.psum if ps else self.tile
        return Rot([f(f"{name}{i}", shape, dt) for i in range(n)])

    def make_ident(self):
        S = self.S
        idf, Ridf = self.tile("identf", [128, 128], F32)
        idb, Ridb = self.tile("identb", [128, 128], BF16)

        S.op("pool", lambda e: e.memset(idf[:], 0.0), writes=[Ridf])
        S.op("pool", lambda e: e.affine_select(out=idf[:], in_=idf[:], pattern=[[-1, 128]], compare_op=ALU.not_equal,
                                               fill=1.0, base=0, channel_multiplier=1), reads=[Ridf], writes=[Ridf])
        S.op("dve", lambda e: e.tensor_copy(out=idb[:], in_=idf[:]), reads=[Ridf], writes=[Ridb])
        return idb, Ridb, idf, Ridf

    def load_weight(self, dst, Rdst, src2d, K, cols, stg, col0=0, kc0=0):
        S = self.S
        nk = K // 128
        srcv = src2d.rearrange("(kc p) c -> p kc c", p=128)
        engs = ("pool", "dve", "act")
        i = 0
        for kc in range(nk):
            for c0 in range(0, cols, 2048):
                cw = min(2048, cols - c0)
                stile, Rst = stg.next()
                S.dma("sp", stile[:, 0:cw], srcv[:, kc, c0:c0 + cw], writes=[Rst])
                eng = engs[i % 3]
                i += 1
                o = dst[:, kc0 + kc, col0 + c0:col0 + c0 + cw]
                if eng == "act":
                    S.op("act", lambda e, o=o, s=stile, cw=cw: e.copy(out=o, in_=s[:, 0:cw]), reads=[Rst], writes=[Rdst])
                else:
                    S.op(eng, lambda e, o=o, s=stile, cw=cw: e.tensor_copy(out=o, in_=s[:, 0:cw]), reads=[Rst], writes=[Rdst])

    def load_bcast(self, dst, Rdst, src1d, n):
        self.S.dma("sp", dst[:, 0:n], src1d.rearrange("(o n) -> o n", o=1).partition_broadcast(128), writes=[Rdst])

    def norm_to_hT(self, xt, Rxt, b, gb, Rgb, hT, RhT, tcol, ident, Rid, tmp):
        h, Rh = self.norm_A(xt, Rxt, b, gb, Rgb, tmp)
        self.norm_B(h, Rh, hT, RhT, tcol, ident, Rid, tmp)

    def norm_A(self, xt, Rxt, b, gb, Rgb, tmp):
        S = self.S
        junk, Rjunk = tmp["junk"].next()
        ss, Rss = tmp["ss"].next()
        h, Rh = tmp["h"].next()
        S.op("act", lambda e: e.activation(out=junk[:], in_=xt[:, b, :], func=AF.Square, accum_out=ss[:, 0:1]),
             reads=[Rxt], writes=[Rjunk, Rss])

        S.op("act", lambda e: e.activation(out=ss[:, 1:2], in_=ss[:, 0:1], func=AF.Sqrt, scale=1.0 / D, bias=self.eps_t[:, 0:1]),
             reads=[Rss, self.Reps], writes=[Rss])
        S.op("dve", lambda e: e.reciprocal(out=ss[:, 2:3], in_=ss[:, 1:2]), reads=[Rss], writes=[Rss])
        S.op("dve", lambda e: e.scalar_tensor_tensor(out=h[:], in0=xt[:, b, :], scalar=ss[:, 2:3], in1=gb[:],
                                                     op0=ALU.mult, op1=ALU.mult),
             reads=[Rxt, Rss, Rgb], writes=[Rh])
        return h, Rh

    def norm_B(self, h, Rh, hT, RhT, tcol, ident, Rid, tmp):
        S = self.S
        pT, RpT = tmp["pT"].next()

        def tr(e):
            for kc in range(8):
                ins = e.transpose(out=pT[:, kc * 128:(kc + 1) * 128], in_=h[:, kc * 128:(kc + 1) * 128], identity=ident[:])
            return ins
        S.op("pe", tr, reads=[Rh, Rid], writes=[RpT])
        S.op("act", lambda e: e.copy(out=hT[:, :, tcol:tcol + 128], in_=pT[:].rearrange("p (k t) -> p k t", k=8)),
             reads=[RpT], writes=[RhT])

    def norm_tmp(self, nh=2):
        self.eps_t, Re = self.tile("eps_t", [128, 2], F32)
        eps_t = self.eps_t
        self.S.op("pool", lambda e: e.memset(eps_t[:], EPS), writes=[Re])
        self.Reps = Re
        return {
            "junk": self.rot("junk", [128, D], BF16, 1),
            "ss": self.rot("ss", [128, 4], F32, 4),
            "h": self.rot("h", [128, D], BF16, nh),
            "pT": self.rot("pT", [128, 1024], BF16, 2, ps=True),
        }

    def phase_p1(self, l):
        S, T, NS = self.S, self.T, self.NS
        self.begin()
        TT = 512
        win, Rwin = self.tile("win", [128, 8, DIN], BF16)
        stg = self.rot("wstg", [128, 2048], F32, 2)
        gb, Rgb = self.tile("gb", [128, D], F32)
        ident, Rid, _, _ = self.make_ident()
        self.load_bcast(gb, Rgb, self.w["norm_mix"][l], D)
        self.load_weight(win, Rwin, self.w["w_in"][l], D, DIN, stg)
        tmp = self.norm_tmp(4)
        xts = self.rot("xt", [128, 4, D], F32, 2)
        hTs = self.rot("hT", [128, 8, TT], BF16, 2)
        pss = self.rot("ps", [128, 512], F32, 4, ps=True)
        of32 = self.rot("of32", [128, 512], F32, 3)
        obf = self.rot("obf", [128, 512], BF16, 4)
        xsrc = self.x_in if l == 0 else self.scr["xres"]
        pf, nq, nk, nv, sg = (self.scr[k] for k in ("pf", "nq", "nk", "nv", "sg"))
        ev = 0
        tiles_ = [(s, t0) for s in range(NS) for t0 in range(0, T, TT)]

        def prep_(idx):
            s, t0 = tiles_[idx]
            xt, Rxt = xts.next()
            xr = [self.dres(("xres", s, t0))] if l > 0 else []
            S.dma("sp", xt[:], xsrc[s, t0:t0 + TT, :].rearrange("(b p) d -> p b d", p=128), reads=xr, writes=[Rxt])
            return xt, Rxt, [self.norm_A(xt, Rxt, b, gb, Rgb, tmp) for b in range(4)]
        nxt_ = prep_(0)
        for idx_, (s, t0) in enumerate(tiles_):
            if True:
                xt, Rxt, hs_ = nxt_
                hT, RhT = hTs.next()
                for b in range(4):
                    self.norm_B(hs_[b][0], hs_[b][1], hT, RhT, b * 128, ident, Rid, tmp)
                nxt_ = prep_(idx_ + 1) if idx_ + 1 < len(tiles_) else None
                for cc in range(DIN // 128):
                    c0 = cc * 128
                    if NV0 <= c0 < G0:
                        continue
                    ps, Rps = pss.next()

                    def mm(e, ps=ps, c0=c0, hT=hT):
                        for kc in range(8):
                            ins = e.matmul(ps[:], lhsT=win[:, kc, c0:c0 + 128], rhs=hT[:, kc, :], start=(kc == 0), stop=(kc == 7))
                        return ins
                    S.op("pe", mm, reads=[Rwin, RhT], writes=[Rps])
                    if c0 < RW:
                        o, Ro = of32.next()
                        eng = "dve" if ev % 2 == 0 else "act"
                        ev += 1
                        if eng == "dve":
                            S.op("dve", lambda e, o=o, ps=ps: e.tensor_copy(out=o[:], in_=ps[:]), reads=[Rps], writes=[Ro])
                        else:
                            S.op("act", lambda e, o=o, ps=ps: e.copy(out=o[:], in_=ps[:]), reads=[Rps], writes=[Ro])
                        S.dma("pool", pf[s, c0:c0 + 128, t0:t0 + TT], o[:], reads=[Ro], writes=[self.dres(("pf", s))])
                    elif c0 < NV0:
                        o, Ro = obf.next()
                        S.op("dve", lambda e, o=o, ps=ps: e.tensor_copy(out=o[:], in_=ps[:]), reads=[Rps], writes=[Ro])
                        if c0 < NQ0 + 512:
                            S.dma("pool", nq[s, c0 - NQ0:c0 - NQ0 + 128, t0:t0 + TT], o[:], reads=[Ro], writes=[self.dres(("nq", s))])
                        else:
                            c1 = c0 - NQ0 - 512
                            S.dma("pool", nk[s, c1:c1 + 128, t0:t0 + TT], o[:], reads=[Ro], writes=[self.dres(("nk", s))])
                    else:
                        o, Ro = obf.next()
                        S.op("act", lambda e, o=o, ps=ps: e.activation(out=o[:], in_=ps[:], func=AF.Sigmoid), reads=[Rps], writes=[Ro])
                        c1 = c0 - G0
                        S.dma("pool", sg[s, c1:c1 + 128, t0:t0 + TT], o[:], reads=[Ro], writes=[self.dres(("sg", s))])
                for b in range(4):
                    ps, Rps = pss.next()

                    def mmv(e, ps=ps, b=b, hT=hT):
                        for kc in range(8):
                            ins = e.matmul(ps[:], lhsT=hT[:, kc, b * 128:(b + 1) * 128], rhs=win[:, kc, NV0:NV0 + 512], start=(kc == 0), stop=(kc == 7))
                        return ins
                    S.op("pe", mmv, reads=[Rwin, RhT], writes=[Rps])
                    o, Ro = obf.next()
                    S.op("dve", lambda e, o=o, ps=ps: e.tensor_copy(out=o[:], in_=ps[:]), reads=[Rps], writes=[Ro])
                    S.dma("pool", nv[s, t0 + b * 128:t0 + (b + 1) * 128, :], o[:], reads=[Ro], writes=[self.dres(("nv", s))])
        self.end()

    def phase_p2(self, l):
        S, T, NS = self.S, self.T, self.NS
        self.begin()
        C = 128
        NCH = T // C
        LD = 0.6065306597126334
        pf, yf, yr = self.scr["pf"], self.scr["yf"], self.scr["yr"]
        bon = self.scr["bon"]
        phd = self.scr["phd"]
        op = S.op
        bc = lambda ap, shape: ap.broadcast_to(shape)
        identb, Ridb, identf, Ridf = self.make_ident()
        mup, Rmup = self.tile("mup", [128, 15], F32)
        mun, Rmun = self.tile("mun", [128, 15], F32)
        w0t, Rw0 = self.tile("w0t", [128, 2, 4], F32)
        a0t, Ra0 = self.tile("a0t", [128, 2, 4], F32)
        kkc, Rkkc = self.tile("kkc", [128, 4], F32)
        kac, Rkac = self.tile("kac", [128, 4], F32)
        omka, Romka = self.tile("omka", [128, 4], F32)
        rkc, Rrkc = self.tile("rkc", [128, 4], F32)
        S.dma("sp", mup[:], self.w["mu_prev"][l].rearrange("(c p) -> p c", p=128), writes=[Rmup], allow_slow_non_contiguous=True)
        S.dma("sp", mun[:], self.w["mu_next"][l].rearrange("(c p) -> p c", p=128), writes=[Rmun], allow_slow_non_contiguous=True)
        for d in range(2):
            S.dma("sp", w0t[:, d, :], self.w["w0"][l, d].rearrange("(c p) -> p c", p=128), writes=[Rw0], allow_slow_non_contiguous=True)
            S.dma("sp", a0t[:, d, :], self.w["a0"][l, d].rearrange("(c p) -> p c", p=128), writes=[Ra0], allow_slow_non_contiguous=True)
        S.dma("sp", kkc[:], self.w["k_k"][l].rearrange("(c p) -> p c", p=128), writes=[Rkkc], allow_slow_non_contiguous=True)
        S.dma("sp", kac[:], self.w["k_a"][l].rearrange("(c p) -> p c", p=128), writes=[Rkac], allow_slow_non_contiguous=True)
        S.dma("sp", rkc[:], self.w["r_k"][l].rearrange("h n -> (h n)").rearrange("(c p) -> p c", p=128), writes=[Rrkc], allow_slow_non_contiguous=True)
        op("dve", lambda e: e.tensor_scalar(out=omka[:], in0=kac[:], scalar1=-1.0, scalar2=1.0, op0=ALU.mult, op1=ALU.add), reads=[Rkac], writes=[Romka])
        gng, Rgng = self.tile("gng", [128, 512], F32)
        gnb, Rgnb = self.tile("gnb", [128, 512], F32)
        self.load_bcast(gng, Rgng, self.w["gn_g"][l], 512)
        self.load_bcast(gnb, Rgnb, self.w["gn_b"][l], 512)
        wstg, Rwstg = self.tile("lstg", [128, 3, 512], F32)
        S.dma("sp", wstg[:, 0, :], self.w["w_up"][l].rearrange("d l c -> (d l) c"), writes=[Rwstg])
        S.dma("sp", wstg[:, 1, :], self.w["a_up"][l].rearrange("d l c -> (d l) c"), writes=[Rwstg])
        S.dma("sp", wstg[:, 2, :], self.w["g_up"][l], writes=[Rwstg])
        wupz, Rwupz = self.tile("wupz", [128, 2, 512], BF16)
        aupz, Raupz = self.tile("aupz", [128, 2, 512], BF16)
        gup, Rgup = self.tile("gup", [128, 512], BF16)
        op("pool", lambda e: e.memset(wupz[:].rearrange("p d c -> p (d c)"), 0.0), writes=[Rwupz])
        op("pool", lambda e: e.memset(aupz[:].rearrange("p d c -> p (d c)"), 0.0), writes=[Raupz])
        for d in range(2):
            op("dve", lambda e, d=d: e.tensor_copy(out=wupz[d * 64:(d + 1) * 64, d, :], in_=wstg[d * 64:(d + 1) * 64, 0, :]), reads=[Rwstg, Rwupz], writes=[Rwupz])
            op("dve", lambda e, d=d: e.tensor_copy(out=aupz[d * 64:(d + 1) * 64, d, :], in_=wstg[d * 64:(d + 1) * 64, 1, :]), reads=[Rwstg, Raupz], writes=[Raupz])
        op("dve", lambda e: e.tensor_copy(out=gup[:], in_=wstg[:, 2, :]), reads=[Rwstg], writes=[Rgup])
        onesf, Ronesf = self.tile("onesf", [128, 128], F32)
        op("pool", lambda e: e.memset(onesf[:], 1.0), writes=[Ronesf])
        blk1, Rblk1 = self.tile("blk1", [128, 128], F32)
        hsel, Rhsel = self.tile("hsel", [128, 2], F32)
        bdm, Rbdm = self.tile("bdm", [128, 4, 2, 64], F32)

        def mkz(e):
            e.memset(blk1[:], 0.0)
            e.memset(hsel[:], 0.0)
            return e.memset(bdm[:].rearrange("p c h i -> p (c h i)"), 0.0)
        op("dve", mkz, writes=[Rblk1, Rhsel, Rbdm])

        def mko(e):
            e.memset(blk1[0:64, 0:64], 1.0)
            e.memset(blk1[64:128, 64:128], 1.0)
            e.memset(hsel[0:64, 0:1], 1.0)
            e.memset(hsel[64:128, 1:2], 1.0)
            e.memset(bdm[0:64, :, 0, :], 1.0)
            return e.memset(bdm[64:128, :, 1, :], 1.0)
        op("dve", mko, reads=[Rblk1, Rhsel, Rbdm], writes=[Rblk1, Rhsel, Rbdm])
        gne, Rgne = self.tile("gne", [128, 1], F32)
        op("pool", lambda e: e.memset(gne[:], GN_EPS), writes=[Rgne])
        mbase, Rmbase = self.tile("mbase", [128, 4, 128], F32)

        op("pool", lambda e: e.memset(mbase[:].rearrange("p a x -> p (a x)"), 1.0), writes=[Rmbase])
        op("pool", lambda e: e.affine_select(out=mbase[:, 0, :], in_=mbase[:, 0, :], pattern=[[1, 128]], compare_op=ALU.is_gt, fill=0.0, base=0, channel_multiplier=-1), reads=[Rmbase], writes=[Rmbase])
        op("pool", lambda e: e.affine_select(out=mbase[:, 1, :], in_=mbase[:, 1, :], pattern=[[1, 128]], compare_op=ALU.is_ge, fill=0.0, base=0, channel_multiplier=-1), reads=[Rmbase], writes=[Rmbase])
        op("pool", lambda e: e.affine_select(out=mbase[:, 2, :], in_=mbase[:, 2, :], pattern=[[-1, 128]], compare_op=ALU.is_gt, fill=0.0, base=0, channel_multiplier=1), reads=[Rmbase], writes=[Rmbase])
        op("pool", lambda e: e.affine_select(out=mbase[:, 3, :], in_=mbase[:, 3, :], pattern=[[-1, 128]], compare_op=ALU.is_ge, fill=0.0, base=0, channel_multiplier=1), reads=[Rmbase], writes=[Rmbase])
        mG, RmG = self.tile("mG", [128, 2, 2, 2, 128], BF16)
        mL, RmL = self.tile("mL", [128, 2, 2, 128], BF16)
        for d in range(2):
            si, ii, li = (0, 1, 2) if d == 0 else (2, 3, 0)
            for h2 in range(2):
                op("dve", lambda e, d=d, h2=h2, si=si: e.tensor_copy(out=mG[:, d, h2, 0, :], in_=mbase[:, si, :]), reads=[Rmbase], writes=[RmG])
                op("dve", lambda e, d=d, h2=h2, ii=ii: e.tensor_copy(out=mG[:, d, h2, 1, :], in_=mbase[:, ii, :]), reads=[Rmbase], writes=[RmG])
                op("dve", lambda e, d=d, h2=h2, li=li: e.tensor_copy(out=mL[:, d, h2, :], in_=mbase[:, li, :]), reads=[Rmbase], writes=[RmL])
        Eg, REg = self.tile("Eg", [8, 3, 128], F32)
        op("pool", lambda e: e.memset(Eg[:].rearrange("p a x -> p (a x)"), 1.0), writes=[REg])
        for gi, b_ in enumerate((16, 32, 64)):
            op("pool", lambda e, gi=gi, b_=b_: e.affine_select(out=Eg[:, gi, :], in_=Eg[:, gi, :], pattern=[[1, 128]], compare_op=ALU.is_ge, fill=0.0, base=0, channel_multiplier=-b_),
               reads=[REg], writes=[REg])
            op("pool", lambda e, gi=gi, b_=b_: e.affine_select(out=Eg[:, gi, :], in_=Eg[:, gi, :], pattern=[[-1, 128]], compare_op=ALU.is_ge, fill=0.0, base=b_ - 1, channel_multiplier=b_),
               reads=[REg], writes=[REg])
        bdf, Rbdf = self.tile("bdf", [128, 4, 128], F32)
        op("pool", lambda e: e.memset(bdf[:, 3, :], 1.0), writes=[Rbdf])
        pbm, Rpbm = self.psum("pbm", [128, 512], F32)

        def mmE(e):
            for gi in range(3):
                ins = e.matmul(pbm[:, gi * 128:(gi + 1) * 128], lhsT=Eg[:, gi, :], rhs=Eg[:, gi, :], start=True, stop=True)
            return ins
        op("pe", mmE, reads=[REg], writes=[Rpbm])
        op("dve", lambda e: e.tensor_copy(out=bdf[:, 0:3, :].rearrange("p a x -> p (a x)"), in_=pbm[:, 0:384]), reads=[Rpbm, Rbdf], writes=[Rbdf])
        bd16, Rbd16 = self.tile("bd16", [128, 2, 128], BF16)
        offm, Roffm = self.tile("offm", [128, 3, 4, 128], BF16)
        for h2 in range(4):
            if h2 < 2:
                op("dve", lambda e, h2=h2: e.tensor_copy(out=bd16[:, h2, :], in_=bdf[:, 0, :]), reads=[Rbdf], writes=[Rbd16])
            for lv in range(3):
                op("dve", lambda e, h2=h2, lv=lv: e.tensor_tensor(out=offm[:, lv, h2, :], in0=bdf[:, lv + 1, :], in1=bdf[:, lv, :], op=ALU.subtract),
                   reads=[Rbdf], writes=[Roffm])
        Ps = self.rot("P", [128, 15, 130], F32, 1)
        tAs = self.rot("tA", [128, 15, 128], F32, 1)
        tBs = self.rot("tB", [128, 15, 128], F32, 1)
        phs = self.rot("ph", [128, 15, 128], F32, 1)
        f4 = lambda n, k=1: self.rot(n, [128, 4, 128], F32, k)
        b4 = lambda n, k=1: self.rot(n, [128, 4, 128], BF16, k)
        lws, avs, pres, cums, cprs = f4("lw"), f4("av"), f4("pre"), f4("cum"), f4("cpr")
        ecs, ecps, eis = f4("ec"), f4("ecp"), f4("ei")
        kraws, sqs, nrms, kkvs, tts, kds, bvs = (f4(n) for n in ("kraw", "sq", "nrm", "kkv", "tt", "kd", "bv"))
        Kfs, Bfs, rks = kraws, nrms, sqs
        ATs, RTs, KTs, BTs = (b4(n, 2) for n in ("AT", "RT", "KT", "BT"))
        KhTs, BhTs, vTs = (b4(n, 1) for n in ("KhT", "BhT", "vT"))
        twds = self.rot("twd", [128, 128], BF16, 2)
        adbs = self.rot("adb", [128, 128], BF16, 2)
        sgds = self.rot("sgd", [128, 128], BF16, 2)
        gCs = self.rot("gC", [128, 4], F32, 2)
        ARbds = self.rot("ARbd", [128, 4, 2, 2, 128], BF16, 2)
        Bbds = self.rot("Bbd", [128, 4, 2, 128], BF16, 2)
        for (t_, r_) in ARbds.items:
            op("pool", lambda e, t_=t_: e.memset(t_[:].rearrange("p c h x t -> p (c h x t)"), 0.0), writes=[r_])
        for (t_, r_) in Bbds.items:
            op("pool", lambda e, t_=t_: e.memset(t_[:].rearrange("p c h t -> p (c h t)"), 0.0), writes=[r_])
        VKs = self.rot("VK", [128, 1024], BF16, 2)
        Bhats = self.rot("Bhat", [128, 512], BF16, 2)
        GKs = self.rot("GK", [128, 4, 2, 2, 128], BF16, 2)
        GBs = self.rot("GB", [128, 4, 2, 2, 128], BF16, 2)
        Lhs = self.rot("Lh", [128, 4, 2, 128], BF16, 1)
        lmq = [[self.tile(f"lmq{h}_{k}", [128, 4, 128], BF16) for k in range(2)] for h in range(8)]
        zzt = [self.tile(f"zz{h}", [128, 2, 2, 128], BF16) for h in range(4)]
        ttt = [[self.tile(f"tt{h}_{k}", [128, 2, 2, 128], BF16) for k in range(2)] for h in range(4)]
        tfin = [self.rot(f"tfin{h}", [128, 2, 2, 128], BF16, 2) for h in range(4)]
        L16s = self.rot("L16", [128, 4, 2, 128], BF16, 1)
        M16s = self.rot("M16", [128, 4, 2, 128], BF16, 1)
        St, RSt = self.tile("St", [128, 4, 2, 64], F32)
        Sbd, RSbd = self.tile("Sbd", [128, 4, 2, 64], BF16)
        stmp, Rstmp = self.tile("stmp", [128, 4, 2, 64], F32)
        Wsbs = self.rot("Wsb", [128, 512], BF16, 1)
        Usbs = self.rot("Usb", [128, 512], BF16, 1)
        Ysbs = self.rot("Ysb", [128, 512], F32, 2)
        bons = self.rot("bonv", [128, 8], F32, 2)
        bon0s = self.rot("bon0", [128, 8], F32, 2)
        yfls = self.rot("yfl", [128, 512], F32, 1)
        gtms = self.rot("gtm", [128, 512], F32, 2)
        sqvs = self.rot("sqv", [128, 512], F32, 1)
        ycs = self.rot("yc", [128, 512], F32, 1)
        vvs = sqvs
        stat = self.rot("stat", [128, 6, 8], F32, 2)
        yos = self.rot("yo", [128, 512], BF16, 1)
        pbanks = self.rot("pb", [128, 512], F32, 6, ps=True)
        ptr = self.rot("ptr", [128, 1024], BF16, 1, ps=True)
        Ptile, RPtile = self.tile("Ptile", [128, 8, 128], BF16)
        pfv = [pf[s].rearrange("(c p) t -> p c t", p=128) for s in range(NS)]

        for s in range(NS):
            for d in range(2):
                op("pool", lambda e: e.memset(St[:].rearrange("p c h i -> p (c h i)"), 0.0), writes=[RSt])
                op("pool", lambda e: e.memset(Sbd[:].rearrange("p c h i -> p (c h i)"), 0.0), writes=[RSbd])
                order = range(NCH) if d == 0 else range(NCH - 1, -1, -1)
                def body(ci, s=s, d=d):
                    t0 = ci * C
                    ph, Rph = phs.next()
                    if d == 0:
                        P, RP = Ps.next()
                        lo, hi = max(t0 - 1, 0), min(t0 + C + 1, T)
                        if t0 == 0:
                            op("pool", lambda e, P=P: e.memset(P[:, :, 0:1], 0.0), writes=[RP])
                        if t0 + C == T:
                            op("pool", lambda e, P=P: e.memset(P[:, :, 129:130], 0.0), writes=[RP])
                        S.dma("sp", P[:, :, lo - (t0 - 1):hi - (t0 - 1)], pfv[s][:, :, lo:hi], reads=[self.dres(("pf", s))], writes=[RP])
                        tA, RtA = tAs.next()
                        tB, RtB = tBs.next()
                        for (c0_, c1_) in ((0, 5), (5, 10), (10, 15)):
                            cs_ = slice(c0_, c1_)
                            nn = c1_ - c0_
                            op("dve", lambda e, cs_=cs_: e.tensor_tensor(out=tA[:, cs_, :], in0=P[:, cs_, 0:128], in1=P[:, cs_, 1:129], op=ALU.subtract), reads=[RP], writes=[RtA])
                            op("pool", lambda e, cs_=cs_: e.tensor_tensor(out=tB[:, cs_, :], in0=P[:, cs_, 2:130], in1=P[:, cs_, 1:129], op=ALU.subtract), reads=[RP], writes=[RtB])
                            op("dve", lambda e, cs_=cs_, nn=nn: e.tensor_tensor(out=tA[:, cs_, :], in0=tA[:, cs_, :], in1=bc(mup[:, cs_].unsqueeze(2), [128, nn, 128]), op=ALU.mult), reads=[RtA, Rmup], writes=[RtA])
                            op("pool", lambda e, cs_=cs_, nn=nn: e.tensor_tensor(out=tB[:, cs_, :], in0=tB[:, cs_, :], in1=bc(mun[:, cs_].unsqueeze(2), [128, nn, 128]), op=ALU.mult), reads=[RtB, Rmun], writes=[RtB])
                            yield 'a'
                            op("dve", lambda e, cs_=cs_: e.tensor_tensor(out=tA[:, cs_, :], in0=tA[:, cs_, :], in1=tB[:, cs_, :], op=ALU.add), reads=[RtA, RtB], writes=[RtA])
                            op("pool", lambda e, cs_=cs_: e.tensor_tensor(out=ph[:, cs_, :], in0=tA[:, cs_, :], in1=P[:, cs_, 1:129], op=ALU.add), reads=[RtA, RP], writes=[Rph])
                            yield 'a'
                        S.dma("pool", phd[s, ci], ph[:].rearrange("p c t -> p (c t)"), reads=[Rph], writes=[self.dres(("phd", s))])
                    else:
                        S.dma("sp", ph[:].rearrange("p c t -> p (c t)"), phd[s, ci], reads=[self.dres(("phd", s))], writes=[Rph])
                    rh, kh, vh = ph[:, 0:4, :], ph[:, 4:8, :], ph[:, 8:12, :]
                    yield 'a'
                    twd, Rtwd = twds.next()
                    adb, Radb = adbs.next()
                    sgd, Rsgd = sgds.next()
                    op("act", lambda e, twd=twd, ph=ph: e.activation(out=twd[:], in_=ph[:, 12, :], func=AF.Tanh), reads=[Rph], writes=[Rtwd])
                    op("dve", lambda e, adb=adb, ph=ph: e.tensor_copy(out=adb[:], in_=ph[:, 13, :]), reads=[Rph], writes=[Radb])
                    psw, Rpsw = pbanks.next()
                    psa, Rpsa = pbanks.next()

                    def mmlora(e, psw=psw, psa=psa, twd=twd, adb=adb, d=d):
                        for cc in range(4):
                            e.matmul(psw[:, cc * 128:(cc + 1) * 128], lhsT=wupz[:, d, cc * 128:(cc + 1) * 128], rhs=twd[:], start=True, stop=True)
                        for cc in range(4):
                            ins = e.matmul(psa[:, cc * 128:(cc + 1) * 128], lhsT=aupz[:, d, cc * 128:(cc + 1) * 128], rhs=adb[:], start=True, stop=True)
                        return ins
                    op("pe", mmlora, reads=[Rwupz, Raupz, Rtwd, Radb], writes=[Rpsw, Rpsa])
                    yield 'a'
                    lw, Rlw = lws.next()
                    av, Rav = avs.next()

                    def sigw(e, lw=lw, psw=psw, d=d):
                        for cc in range(4):
                            ins = e.activation(out=lw[:, cc, :], in_=psw[:, cc * 128:(cc + 1) * 128], func=AF.Sigmoid, bias=w0t[:, d, cc:cc + 1])
                        return ins
                    op("act", sigw, reads=[Rpsw, Rw0], writes=[Rlw])

                    def siga(e, av=av, psa=psa, d=d):
                        for cc in range(4):
                            ins = e.activation(out=av[:, cc, :], in_=psa[:, cc * 128:(cc + 1) * 128], func=AF.Sigmoid, bias=a0t[:, d, cc:cc + 1])
                        return ins
                    op("act", siga, reads=[Rpsa, Ra0], writes=[Rav])
                    yield 'a'
                    yield 'a'
                    pre, Rpre = pres.next()
                    cum, Rcum = cums.next()
                    cpr, Rcpr = cprs.next()

                    def scan(e, pre=pre, lw=lw):
                        for cc in range(4):
                            ins = e.tensor_tensor_scan(out=pre[:, cc, :], data0=onesf[:], data1=lw[:, cc, :], initial=0.0, op0=ALU.mult, op1=ALU.add)
                        return ins
                    op("dve", scan, reads=[Rlw, Ronesf], writes=[Rpre])
                    yield 'a'
                    if d == 0:
                        cum, Rcum = pre, Rpre
                    else:
                        op("dve", lambda e, cum=cum, pre=pre, lw=lw: e.scalar_tensor_tensor(out=cum[:], in0=pre[:], scalar=-1.0, in1=lw[:], op0=ALU.mult, op1=ALU.add),
                           reads=[Rpre, Rlw], writes=[Rcum])
                        op("dve", lambda e, cum=cum, pre=pre: e.tensor_tensor(out=cum[:], in0=cum[:], in1=bc(pre[:, :, 127:128], [128, 4, 128]), op=ALU.add),
                           reads=[Rcum, Rpre], writes=[Rcum])
                    op("pool", lambda e, cpr=cpr, cum=cum, lw=lw: e.tensor_tensor(out=cpr[:], in0=cum[:], in1=lw[:], op=ALU.subtract), reads=[Rcum, Rlw], writes=[Rcpr])
                    ec, Rec = ecs.next()
                    ecp, Recp = ecps.next()
                    ei, Rei = eis.next()
                    op("act", lambda e, ec=ec, cum=cum: e.activation(out=ec[:], in_=cum[:], func=AF.Exp, scale=-LD), reads=[Rcum], writes=[Rec])
                    op("act", lambda e, ecp=ecp, cpr=cpr: e.activation(out=ecp[:], in_=cpr[:], func=AF.Exp, scale=-LD), reads=[Rcpr], writes=[Recp])
                    op("act", lambda e, ei=ei, cum=cum: e.activation(out=ei[:], in_=cum[:], func=AF.Exp, scale=LD), reads=[Rcum], writes=[Rei])
                    yield 'a'
                    gC, RgC = gCs.next()
                    ce = 127 if d == 0 else 0
                    op("dve", lambda e, gC=gC, ec=ec, ce=ce: e.tensor_copy(out=gC[:], in_=ec[:, :, ce]), reads=[Rec], writes=[RgC])
                    yield 'a'
                    kraw, Rkraw = kraws.next()
                    sq, Rsq = sqs.next()
                    nrm, Rnrm = nrms.next()
                    kkv, Rkkv = kkvs.next()
                    tt, Rtt = tts.next()
                    kd, Rkd = kds.next()
                    bv, Rbv = bvs.next()
                    op("dve", lambda e, kraw=kraw, ph=ph: e.tensor_tensor(out=kraw[:], in0=ph[:, 4:8, :], in1=bc(kkc[:].unsqueeze(2), [128, 4, 128]), op=ALU.mult), reads=[Rph, Rkkc], writes=[Rkraw])
                    op("pool", lambda e, sq=sq, kraw=kraw: e.tensor_tensor(out=sq[:], in0=kraw[:], in1=kraw[:], op=ALU.mult), reads=[Rkraw], writes=[Rsq])
                    psn, Rpsn = pbanks.next()
                    op("pe", lambda e, psn=psn, sq=sq: e.matmul(psn[:], lhsT=blk1[:], rhs=sq[:].rearrange("p c t -> p (c t)"), start=True, stop=True), reads=[Rblk1, Rsq], writes=[Rpsn])
                    yield 'a'
                    op("act", lambda e, nrm=nrm, psn=psn: e.activation(out=nrm[:].rearrange("p c t -> p (c t)"), in_=psn[:], func=AF.Ln), reads=[Rpsn], writes=[Rnrm])
                    op("act", lambda e, nrm=nrm: e.activation(out=nrm[:], in_=nrm[:], func=AF.Exp, scale=-0.5), reads=[Rnrm], writes=[Rnrm])
                    op("dve", lambda e, nrm=nrm: e.tensor_scalar_min(out=nrm[:], in0=nrm[:], scalar1=1e12), reads=[Rnrm], writes=[Rnrm])
                    op("dve", lambda e, kkv=kkv, kraw=kraw, nrm=nrm: e.tensor_tensor(out=kkv[:], in0=kraw[:], in1=nrm[:], op=ALU.mult), reads=[Rkraw, Rnrm], writes=[Rkkv])
                    yield 'a'
                    op("pool", lambda e, tt=tt, av=av: e.tensor_tensor(out=tt[:], in0=av[:], in1=bc(kac[:].unsqueeze(2), [128, 4, 128]), op=ALU.mult), reads=[Rav, Rkac], writes=[Rtt])
                    op("pool", lambda e, tt=tt: e.tensor_tensor(out=tt[:], in0=tt[:], in1=bc(omka[:].unsqueeze(2), [128, 4, 128]), op=ALU.add), reads=[Rtt, Romka], writes=[Rtt])
                    op("dve", lambda e, kd=kd, ph=ph, tt=tt: e.tensor_tensor(out=kd[:], in0=ph[:, 4:8, :], in1=tt[:], op=ALU.mult), reads=[Rph, Rtt], writes=[Rkd])
                    yield 'a'
                    op("pool", lambda e, bv=bv, kkv=kkv, av=av: e.tensor_tensor(out=bv[:], in0=kkv[:], in1=av[:], op=ALU.mult), reads=[Rkkv, Rav], writes=[Rbv])
                    yield 'a'
                    AT, RAT = ATs.next()
                    RT, RRT = RTs.next()
                    KT, RKT = KTs.next()
                    BT, RBT = BTs.next()
                    KhT, RKhT = KhTs.next()
                    BhT, RBhT = BhTs.next()
                    vT, RvT = vTs.next()
                    Kf, RKf = Kfs.next()
                    Bf, RBf = Bfs.next()
                    rk, Rrk = rks.next()
                    op("dve", lambda e, AT=AT, kkv=kkv, ecp=ecp: e.scalar_tensor_tensor(out=AT[:], in0=kkv[:], scalar=-1.0, in1=ecp[:], op0=ALU.mult, op1=ALU.mult), reads=[Rkkv, Recp], writes=[RAT])
                    op("dve", lambda e, RT=RT, ph=ph, ec=ec: e.tensor_tensor(out=RT[:], in0=ph[:, 0:4, :], in1=ec[:], op=ALU.mult), reads=[Rph, Rec], writes=[RRT])
                    op("dve", lambda e, Kf=Kf, kd=kd, ei=ei: e.tensor_tensor(out=Kf[:], in0=kd[:], in1=ei[:], op=ALU.mult), reads=[Rkd, Rei], writes=[RKf])
                    yield 'a'
                    op("pool", lambda e, Bf=Bf, bv=bv, ei=ei: e.tensor_tensor(out=Bf[:], in0=bv[:], in1=ei[:], op=ALU.mult), reads=[Rbv, Rei], writes=[RBf])
                    op("act", lambda e, KT=KT, Kf=Kf: e.copy(out=KT[:], in_=Kf[:]), reads=[RKf], writes=[RKT])
                    op("act", lambda e, BT=BT, Bf=Bf: e.copy(out=BT[:], in_=Bf[:]), reads=[RBf], writes=[RBT])
                    yield 'a'
                    op("dve", lambda e, KhT=KhT, Kf=Kf, gC=gC: e.tensor_tensor(out=KhT[:], in0=Kf[:], in1=bc(gC[:].unsqueeze(2), [128, 4, 128]), op=ALU.mult), reads=[RKf, RgC], writes=[RKhT])
                    op("pool", lambda e, BhT=BhT, Bf=Bf, gC=gC: e.tensor_tensor(out=BhT[:], in0=Bf[:], in1=bc(gC[:].unsqueeze(2), [128, 4, 128]), op=ALU.mult), reads=[RBf, RgC], writes=[RBhT])
                    op("act", lambda e, vT=vT, ph=ph: e.copy(out=vT[:], in_=ph[:, 8:12, :]), reads=[Rph], writes=[RvT])
                    op("pool", lambda e, rk=rk, ph=ph, kd=kd: e.tensor_tensor(out=rk[:], in0=ph[:, 0:4, :], in1=kd[:], op=ALU.mult), reads=[Rph, Rkd], writes=[Rrk])
                    op("pool", lambda e, rk=rk: e.tensor_tensor(out=rk[:], in0=rk[:], in1=bc(rkc[:].unsqueeze(2), [128, 4, 128]), op=ALU.mult), reads=[Rrk, Rrkc], writes=[Rrk])
                    yield 'a'
                    ARbd, RARbd = ARbds.next()
                    Bbd, RBbd = Bbds.next()
                    for h2 in range(2):
                        pl = slice(h2 * 64, (h2 + 1) * 64)
                        op("act", lambda e, pl=pl, h2=h2, AT=AT: e.copy(out=ARbd[pl, :, h2, 0, :], in_=AT[pl, :, :]), reads=[RAT, RARbd], writes=[RARbd])
                        op("act", lambda e, pl=pl, h2=h2, RT=RT: e.copy(out=ARbd[pl, :, h2, 1, :], in_=RT[pl, :, :]), reads=[RRT, RARbd], writes=[RARbd])
                        op("act", lambda e, pl=pl, h2=h2, BT=BT: e.copy(out=Bbd[pl, :, h2, :], in_=BT[pl, :, :]), reads=[RBT, RBbd], writes=[RBbd])
                    yield 'a'
                    pt, Rpt = ptr.next()
                    VK, RVK = VKs.next()
                    Vtm, RVtm = VK[:, 0:512], RVK
                    Khat, RKhat = VK[:, 512:1024], RVK
                    Bhat, RBhat = Bhats.next()

                    def tr1(e, pt=pt, vT=vT, KhT=KhT):
                        for cc in range(4):
                            e.transpose(out=pt[:, cc * 128:(cc + 1) * 128], in_=vT[:, cc, :], identity=identb[:])
                        for cc in range(4):
                            ins = e.transpose(out=pt[:, 512 + cc * 128:512 + (cc + 1) * 128], in_=KhT[:, cc, :], identity=identb[:])
                        return ins
                    op("pe", tr1, reads=[RvT, RKhT, Ridb], writes=[Rpt])
                    yield 'a'
                    op("act", lambda e, VK=VK, pt=pt: e.copy(out=VK[:], in_=pt[:]), reads=[Rpt], writes=[RVK])
                    pt2, Rpt2 = ptr.next()

                    def tr2(e, pt2=pt2, BhT=BhT):
                        for cc in range(4):
                            ins = e.transpose(out=pt2[:, cc * 128:(cc + 1) * 128], in_=BhT[:, cc, :], identity=identb[:])
                        return ins
                    op("pe", tr2, reads=[RBhT, Ridb], writes=[Rpt2])
                    yield 'a'
                    op("act", lambda e, Bhat=Bhat, pt2=pt2: e.copy(out=Bhat[:], in_=pt2[:, 0:512]), reads=[Rpt2], writes=[RBhat])
                    psb, Rpsb = pbanks.next()
                    bonv, Rbonv = bons.next()

                    def mmbon(e, psb=psb, rk=rk):
                        for cc in range(4):
                            ins = e.transpose(out=psb[:, cc * 128:(cc + 1) * 128], in_=rk[:, cc, :], identity=identf[:])
                        return ins
                    op("pe", mmbon, reads=[Rrk, Ridf], writes=[Rpsb])
                    op("dve", lambda e, bonv=bonv, psb=psb: e.tensor_reduce(out=bonv[:], in_=psb[:].rearrange("p (h i) -> p h i", i=64), axis=AX.X, op=ALU.add),
                       reads=[Rpsb], writes=[Rbonv])
                    if d == 1:
                        op("act", lambda e, sgd=sgd, ph=ph: e.activation(out=sgd[:], in_=ph[:, 14, :], func=AF.Sigmoid), reads=[Rph], writes=[Rsgd])
                        psg, Rpsg = pbanks.next()
                        gtm, Rgtm = gtms.next()
                        op("pe", lambda e, psg=psg, sgd=sgd: e.matmul(psg[:], lhsT=sgd[:], rhs=gup[:], start=True, stop=True), reads=[Rsgd, Rgup], writes=[Rpsg])
                        op("act", lambda e, gtm=gtm, psg=psg: e.copy(out=gtm[:], in_=psg[:]), reads=[Rpsg], writes=[Rgtm])
                    yield 'B'
                    GK, RGK = GKs.next()
                    GB, RGB = GBs.next()
                    Lh, RLh = Lhs.next()
                    for cc in range(4):
                        p1, Rp1 = pbanks.next()
                        p2, Rp2 = pbanks.next()
                        p3, Rp3 = pbanks.next()

                        def mmg(e, cc=cc, p1=p1, p2=p2, p3=p3, KT=KT, BT=BT, AT=AT):
                            e.matmul(p1[:], lhsT=KT[:, cc, :], rhs=ARbd[:, cc].rearrange("p h x t -> p (h x t)"), start=True, stop=True)
                            e.matmul(p2[:], lhsT=BT[:, cc, :], rhs=ARbd[:, cc].rearrange("p h x t -> p (h x t)"), start=True, stop=True)
                            return e.matmul(p3[:, 0:256], lhsT=AT[:, cc, :], rhs=Bbd[:, cc].rearrange("p h t -> p (h t)"), start=True, stop=True)
                        op("pe", mmg, reads=[RKT, RBT, RAT, RARbd, RBbd], writes=[Rp1, Rp2, Rp3])
                        op("dve", lambda e, cc=cc, p1=p1, GK=GK, d=d: e.tensor_tensor(out=GK[:, cc].rearrange("p h x t -> p (h x t)"), in0=p1[:],
                                                                                 in1=mG[:, d].rearrange("p h x t -> p (h x t)"), op=ALU.mult), reads=[Rp1, RmG], writes=[RGK])
                        op("dve", lambda e, cc=cc, p2=p2, GB=GB, d=d: e.tensor_tensor(out=GB[:, cc].rearrange("p h x t -> p (h x t)"), in0=p2[:],
                                                                                 in1=mG[:, d].rearrange("p h x t -> p (h x t)"), op=ALU.mult), reads=[Rp2, RmG], writes=[RGB])
                        op("dve", lambda e, cc=cc, p3=p3, Lh=Lh, d=d: e.tensor_tensor(out=Lh[:, cc].rearrange("p h t -> p (h t)"), in0=p3[:, 0:256],
                                                                                 in1=mL[:, d].rearrange("p h t -> p (h t)"), op=ALU.mult), reads=[Rp3, RmL], writes=[RLh])
                        yield 'b'
                    L16, RL16 = L16s.next()
                    M16, RM16 = M16s.next()
                    for cc in range(4):
                        op("pool", lambda e, cc=cc, L16=L16, Lh=Lh: e.tensor_tensor(out=L16[:, cc].rearrange("p h t -> p (h t)"), in0=Lh[:, cc].rearrange("p h t -> p (h t)"),
                                                                              in1=bd16[:].rearrange("p h t -> p (h t)"), op=ALU.mult), reads=[RLh, Rbd16], writes=[RL16])
                        op("pool", lambda e, cc=cc, M16=M16, GB=GB: e.tensor_tensor(out=M16[:, cc], in0=GB[:, cc, :, 0, :], in1=bd16[:], op=ALU.mult), reads=[RGB, Rbd16], writes=[RM16])
                    cur = []
                    for h in range(8):
                        cur.append((L16[:, h // 2, h % 2, :], RL16, M16[:, h // 2, h % 2, :], RM16, identb[:], Ridb, identb[:], Ridb))
                    for lev in range(4):
                        for h in range(8):
                            Lp, RLp, Mp, RMp, Qp, RQp, Pp, RPp = cur[h]
                            pq, Rpq = pbanks.next()
                            lt, Rlt = lmq[h][lev % 2]

                            def mmi(e, pq=pq, Lp=Lp, Mp=Mp, Qp=Qp, Pp=Pp, lev=lev):
                                e.matmul(pq[:, 256:384], lhsT=identb[:], rhs=Qp, start=True, stop=False)
                                ins = e.matmul(pq[:, 256:384], lhsT=Lp, rhs=Qp, start=False, stop=True)
                                if lev < 3:
                                    e.matmul(pq[:, 0:128], lhsT=Mp, rhs=Lp, start=True, stop=True)
                                    ins = e.matmul(pq[:, 128:256], lhsT=Lp, rhs=Mp, start=True, stop=True)
                                return ins
                            op("pe", mmi, reads=[RLp, RMp, RQp, Ridb], writes=[Rpq])
                            lo_ = 0 if lev < 3 else 256
                            eng = "act"
                            if eng == "dve":
                                op("dve", lambda e, lt=lt, pq=pq, lo_=lo_: e.tensor_copy(out=lt[:].rearrange("p a t -> p (a t)")[:, lo_:512], in_=pq[:, lo_:512]), reads=[Rpq], writes=[Rlt])
                            else:
                                op("act", lambda e, lt=lt, pq=pq, lo_=lo_: e.copy(out=lt[:].rearrange("p a t -> p (a t)")[:, lo_:384], in_=pq[:, lo_:384]), reads=[Rpq], writes=[Rlt])
                            cur[h] = (lt[:, 0, :], Rlt, lt[:, 1, :], Rlt, lt[:, 2, :], Rlt, lt[:, 3, :], Rlt)
                            if h % 2 == 1:
                                yield 'b'
                    ptq, Rptq = ptr.next()

                    def trq(e, ptq=ptq, cur=list(cur)):
                        for h in range(8):
                            ins = e.transpose(out=ptq[:, h * 128:(h + 1) * 128], in_=cur[h][4], identity=identb[:])
                        return ins
                    op("pe", trq, reads=[cur[h][5] for h in range(8)] + [Ridb], writes=[Rptq])
                    op("act", lambda e, ptq=ptq: e.copy(out=Ptile[:].rearrange("p h t -> p (h t)"), in_=ptq[:]), reads=[Rptq], writes=[RPtile])
                    tcur = [(Ptile[:, h, :], RPtile, cur[h][4], cur[h][5]) for h in range(8)]
                    for lv in range(3):
                        last_lv = (lv == 2)
                        pas = []
                        for pr in range(4):
                            pa, Rpa = pbanks.next()
                            pas.append((pa, Rpa))

                            def mmz(e, pa=pa, pr=pr, tc=list(tcur), last_lv=last_lv, Lh=Lh, GB=GB):
                                for h2 in range(2):
                                    h = 2 * pr + h2
                                    Tk, RTk, TTk, RTTk = tc[h]
                                    ins = e.matmul(pa[:, h2 * 256 + 128:h2 * 256 + 256], lhsT=Lh[:, pr, h2, :], rhs=TTk, start=True, stop=True)
                                    if not last_lv:
                                        ins = e.matmul(pa[:, h2 * 256:h2 * 256 + 128], lhsT=GB[:, pr, h2, 0, :], rhs=Tk, start=True, stop=True)
                                return ins
                            op("pe", mmz, reads=[RLh, RGB, tcur[2 * pr][1], tcur[2 * pr][3], tcur[2 * pr + 1][1], tcur[2 * pr + 1][3]], writes=[Rpa])
                        for pr in range(4):
                            pa, Rpa = pas[pr]
                            zz, Rzz = zzt[pr]
                            if not last_lv:
                                op("dve", lambda e, zz=zz, pa=pa, lv=lv: e.tensor_tensor(out=zz[:].rearrange("p h a t -> p (h a t)"), in0=pa[:],
                                                                                        in1=offm[:, lv].rearrange("p a t -> p (a t)"), op=ALU.mult),
                                   reads=[Rpa, Roffm], writes=[Rzz])
                            else:
                                op("dve", lambda e, zz=zz, pa=pa, lv=lv: e.tensor_tensor(out=zz[:, :, 1, :], in0=pa[:].rearrange("p (h a t) -> p h a t", h=2, a=2)[:, :, 1, :],
                                                                                        in1=offm[:, lv, 0:2, :], op=ALU.mult),
                                   reads=[Rpa, Roffm], writes=[Rzz])
                        yield 'b'
                        pbs = []
                        for pr in range(4):
                            pb_, Rpb_ = pbanks.next()
                            pbs.append((pb_, Rpb_))
                            zz, Rzz = zzt[pr]

                            def mmt(e, pb_=pb_, pr=pr, tc=list(tcur), zz=zz, last_lv=last_lv):
                                for h2 in range(2):
                                    h = 2 * pr + h2
                                    Tk, RTk, TTk, RTTk = tc[h]
                                    c0 = h2 * 256
                                    e.matmul(pb_[:, c0 + 128:c0 + 256], lhsT=identb[:], rhs=TTk, start=True, stop=False)
                                    ins = e.matmul(pb_[:, c0 + 128:c0 + 256], lhsT=Tk, rhs=zz[:, h2, 1, :], start=False, stop=True)
                                    if not last_lv:
                                        e.matmul(pb_[:, c0:c0 + 128], lhsT=identb[:], rhs=Tk, start=True, stop=False)
                                        ins = e.matmul(pb_[:, c0:c0 + 128], lhsT=TTk, rhs=zz[:, h2, 0, :], start=False, stop=True)
                                return ins
                            op("pe", mmt, reads=[Rzz, Ridb, tcur[2 * pr][1], tcur[2 * pr][3], tcur[2 * pr + 1][1], tcur[2 * pr + 1][3]], writes=[Rpb_])
                        for pr in range(4):
                            pb_, Rpb_ = pbs[pr]
                            tn, Rtn = ttt[pr][lv % 2] if not last_lv else tfin[pr].next()
                            eng = "act"
                            if not last_lv:
                                src, dst = pb_[:], tn[:].rearrange("p h a t -> p (h a t)")
                            else:
                                src, dst = pb_[:].rearrange("p (h a t) -> p h a t", h=2, a=2)[:, :, 1, :], tn[:, :, 1, :]
                            if eng == "act":
                                op("act", lambda e, src=src, dst=dst: e.copy(out=dst, in_=src), reads=[Rpb_], writes=[Rtn])
                            else:
                                op("dve", lambda e, src=src, dst=dst: e.tensor_copy(out=dst, in_=src), reads=[Rpb_], writes=[Rtn])
                            for h2 in range(2):
                                tcur[2 * pr + h2] = (tn[:, h2, 0, :], Rtn, tn[:, h2, 1, :], Rtn)
                        yield 'b'
                    cur = [(None, None, None, None, tcur[h][2], tcur[h][3]) for h in range(8)]
                    yield 'C'
                    pW, RpW = pbanks.next()
                    Wsb, RWsb = Wsbs.next()
                    Usb, RUsb = Usbs.next()
                    Ysb, RYsb = Ysbs.next()

                    def mmW(e, pW=pW, AT=AT, GK=GK, Vtm=Vtm):
                        for cc in range(4):
                            e.matmul(pW[:, cc * 128:(cc + 1) * 128], lhsT=AT[:, cc, :], rhs=Sbd[:, cc].rearrange("p h i -> p (h i)"), start=True, stop=False)
                            for h2 in range(2):
                                h = 2 * cc + h2
                                ins = e.matmul(pW[:, h * 64:(h + 1) * 64], lhsT=GK[:, cc, h2, 0, :], rhs=Vtm[:, h * 64:(h + 1) * 64], start=False, stop=True)
                        return ins
                    op("pe", mmW, reads=[RAT, RSbd, RGK, RVtm], writes=[RpW])
                    op("dve", lambda e, Wsb=Wsb, pW=pW: e.tensor_copy(out=Wsb[:], in_=pW[:]), reads=[RpW], writes=[RWsb])
                    yield 'c'
                    pU, RpU = pbanks.next()

                    def mmU(e, pU=pU, Wsb=Wsb, cur=list(cur)):
                        for h in range(8):
                            ins = e.matmul(pU[:, h * 64:(h + 1) * 64], lhsT=cur[h][4], rhs=Wsb[:, h * 64:(h + 1) * 64], start=True, stop=True)
                        return ins
                    op("pe", mmU, reads=[RWsb] + [cur[h][5] for h in range(8)], writes=[RpU])
                    op("dve", lambda e, Usb=Usb, pU=pU: e.tensor_copy(out=Usb[:], in_=pU[:]), reads=[RpU], writes=[RUsb])
                    yield 'c'
                    pY, RpY = pbanks.next()

                    def mmY(e, pY=pY, RT=RT, GK=GK, GB=GB, Vtm=Vtm, Usb=Usb):
                        for cc in range(4):
                            e.matmul(pY[:, cc * 128:(cc + 1) * 128], lhsT=RT[:, cc, :], rhs=Sbd[:, cc].rearrange("p h i -> p (h i)"), start=True, stop=False)
                            for h2 in range(2):
                                h = 2 * cc + h2
                                e.matmul(pY[:, h * 64:(h + 1) * 64], lhsT=GK[:, cc, h2, 1, :], rhs=Vtm[:, h * 64:(h + 1) * 64], start=False, stop=False)
                                ins = e.matmul(pY[:, h * 64:(h + 1) * 64], lhsT=GB[:, cc, h2, 1, :], rhs=Usb[:, h * 64:(h + 1) * 64], start=False, stop=True)
                        return ins
                    op("pe", mmY, reads=[RRT, RSbd, RGK, RGB, RVtm, RUsb], writes=[RpY])
                    op("act", lambda e, Ysb=Ysb, pY=pY: e.copy(out=Ysb[:], in_=pY[:]), reads=[RpY], writes=[RYsb])
                    yield 'c'
                    pS, RpS = pbanks.next()

                    def mmS(e, pS=pS, Khat=Khat, Bhat=Bhat, Vtm=Vtm, Usb=Usb):
                        for cc in range(4):
                            cs = slice(cc * 128, (cc + 1) * 128)
                            e.matmul(pS[:, cs], lhsT=Khat[:, cs], rhs=Vtm[:, cs], start=True, stop=False)
                            ins = e.matmul(pS[:, cs], lhsT=Bhat[:, cs], rhs=Usb[:, cs], start=False, stop=True)
                        return ins
                    op("pe", mmS, reads=[RKhat, RBhat, RVtm, RUsb], writes=[RpS])
                    Sf = St[:].rearrange("p c h i -> p c (h i)")
                    op("dve", lambda e, pS=pS: e.tensor_tensor(out=stmp[:].rearrange("p c h i -> p (c h i)"), in0=pS[:], in1=bdm[:].rearrange("p c h i -> p (c h i)"), op=ALU.mult),
                       reads=[RpS, Rbdm], writes=[Rstmp])
                    op("dve", lambda e, gC=gC: e.tensor_tensor(out=Sf, in0=Sf, in1=bc(gC[:].unsqueeze(2), [128, 4, 128]), op=ALU.mult), reads=[RSt, RgC], writes=[RSt])
                    op("dve", lambda e: e.tensor_tensor(out=St[:].rearrange("p c h i -> p (c h i)"), in0=St[:].rearrange("p c h i -> p (c h i)"),
                                                        in1=stmp[:].rearrange("p c h i -> p (c h i)"), op=ALU.add), reads=[RSt, Rstmp], writes=[RSt])
                    op("act", lambda e: e.copy(out=Sbd[:].rearrange("p c h i -> p (c h i)"), in_=St[:].rearrange("p c h i -> p (c h i)")), reads=[RSt], writes=[RSbd])
                    yield 'c'
                    if d == 0:
                        S.dma("pool", yf[s, t0:t0 + C, :], Ysb[:], reads=[RYsb], writes=[self.dres(("yf", s))])
                        S.dma("pool", bon[s, t0:t0 + C, :], bonv[:], reads=[Rbonv], writes=[self.dres(("bon", s))])
                    else:
                        yfl, Ryfl = yfls.next()
                        bon0, Rbon0 = bon0s.next()
                        S.dma("sp", yfl[:], yf[s, t0:t0 + C, :], reads=[self.dres(("yf", s))], writes=[Ryfl])
                        S.dma("sp", bon0[:], bon[s, t0:t0 + C, :], reads=[self.dres(("bon", s))], writes=[Rbon0])
                        sqv, Rsqv = sqvs.next()
                        yc, Ryc = ycs.next()
                        vv, Rvv = vvs.next()
                        st, Rst = stat.next()
                        yo, Ryo = yos.next()
                        y3 = lambda t_: t_[:].rearrange("p (h i) -> p h i", i=64)
                        b3 = lambda a_: bc(a_.unsqueeze(2), [128, 8, 64])
                        op("dve", lambda e, Ysb=Ysb, yfl=yfl: e.tensor_tensor(out=Ysb[:], in0=Ysb[:], in1=yfl[:], op=ALU.add), reads=[RYsb, Ryfl], writes=[RYsb])
                        op("pool", lambda e, sqv=sqv, Ysb=Ysb: e.tensor_tensor(out=sqv[:], in0=Ysb[:], in1=Ysb[:], op=ALU.mult), reads=[RYsb], writes=[Rsqv])
                        op("dve", lambda e, st=st, Ysb=Ysb: e.tensor_reduce(out=st[:, 0, :], in_=y3(Ysb), axis=AX.X, op=ALU.add), reads=[RYsb], writes=[Rst])
                        op("dve", lambda e, st=st, sqv=sqv: e.tensor_reduce(out=st[:, 1, :], in_=y3(sqv), axis=AX.X, op=ALU.add), reads=[Rsqv, Rst], writes=[Rst])
                        op("dve", lambda e, st=st: e.tensor_scalar(out=st[:, 2, :], in0=st[:, 0, :], scalar1=1.0 / 64, scalar2=None, op0=ALU.mult), reads=[Rst], writes=[Rst])
                        op("dve", lambda e, st=st: e.tensor_tensor(out=st[:, 3, :], in0=st[:, 2, :], in1=st[:, 2, :], op=ALU.mult), reads=[Rst], writes=[Rst])
                        op("dve", lambda e, st=st: e.scalar_tensor_tensor(out=st[:, 4, :], in0=st[:, 1, :], scalar=1.0 / 64, in1=st[:, 3, :], op0=ALU.mult, op1=ALU.subtract),
                           reads=[Rst], writes=[Rst])
                        op("act", lambda e, st=st: e.activation(out=st[:, 5, :], in_=st[:, 4, :], func=AF.Sqrt, bias=gne[:, 0:1]), reads=[Rst, Rgne], writes=[Rst])
                        op("dve", lambda e, st=st: e.reciprocal(out=st[:, 5, :], in_=st[:, 5, :]), reads=[Rst], writes=[Rst])
                        yield 'c'
                        op("dve", lambda e, yc=yc, Ysb=Ysb, st=st: e.tensor_tensor(out=y3(yc), in0=y3(Ysb), in1=b3(st[:, 2, :]), op=ALU.subtract), reads=[RYsb, Rst], writes=[Ryc])
                        op("dve", lambda e, yc=yc, st=st: e.tensor_tensor(out=y3(yc), in0=y3(yc), in1=b3(st[:, 5, :]), op=ALU.mult), reads=[Ryc, Rst], writes=[Ryc])
                        op("pool", lambda e, yc=yc: e.tensor_tensor(out=yc[:], in0=yc[:], in1=gng[:], op=ALU.mult), reads=[Ryc, Rgng], writes=[Ryc])
                        op("pool", lambda e, yc=yc: e.tensor_tensor(out=yc[:], in0=yc[:], in1=gnb[:], op=ALU.add), reads=[Ryc, Rgnb], writes=[Ryc])
                        yield 'c'
                        op("dve", lambda e, bonv=bonv, bon0=bon0: e.tensor_tensor(out=bonv[:], in0=bonv[:], in1=bon0[:], op=ALU.add), reads=[Rbonv, Rbon0], writes=[Rbonv])
                        op("dve", lambda e, vv=vv, Vtm=Vtm, bonv=bonv: e.tensor_tensor(out=y3(vv), in0=Vtm.rearrange("p (h i) -> p h i", i=64), in1=b3(bonv[:]), op=ALU.mult), reads=[RVtm, Rbonv], writes=[Rvv])
                        op("pool", lambda e, yc=yc, vv=vv: e.tensor_tensor(out=yc[:], in0=yc[:], in1=vv[:], op=ALU.add), reads=[Ryc, Rvv], writes=[Ryc])
                        op("dve", lambda e, yo=yo, yc=yc, gtm=gtm: e.tensor_tensor(out=yo[:], in0=yc[:], in1=gtm[:], op=ALU.mult), reads=[Ryc, Rgtm], writes=[Ryo])
                        S.dma("pool", yr[s, t0:t0 + C, :], yo[:], reads=[Ryo], writes=[self.dres(("yr", s))])

                def run_until(g, tags):
                    while True:
                        try:
                            t_ = next(g)
                        except StopIteration:
                            return None
                        if t_ in tags:
                            return t_
                order = list(order)
                gens = [body(ci) for ci in order]
                run_until(gens[0], ('B',))
                prev = None
                for i_ in range(len(gens)):
                    g_cur = gens[i_]
                    g_nxt = gens[i_ + 1] if i_ + 1 < len(gens) else None
                    cur_done = False
                    if prev is not None:
                        prev_done = False
                        while not prev_done:
                            if run_until(prev, ('c',)) is None:
                                prev_done = True
                            if not cur_done and run_until(g_cur, ('b', 'C')) == 'C':
                                cur_done = True
                    nxt_done = g_nxt is None
                    while not (cur_done and nxt_done):
                        if not cur_done and run_until(g_cur, ('b', 'C')) == 'C':
                            cur_done = True
                        if not nxt_done and run_until(g_nxt, ('a', 'B')) == 'B':
                            nxt_done = True
                    prev = g_cur
                run_until(prev, ())
        self.end()

    def phase_p3(self, l):
        S, T, NS = self.S, self.T, self.NS
        self.begin()
        rows = T // 64
        nblk = T // 128
        nq, nk, nv, yn = (self.scr[k] for k in ("nq", "nk", "nv", "yn"))
        rp, Rrp = self.tile("rp", [120, 31], F32)
        S.dma("sp", rp[:], self.w["rpb"][l].rearrange("h a b -> (h a) b"), writes=[Rrp])
        _, _, idf, Ridf = self.make_ident()
        Rt, RRt = self.tile("Rt", [31, 8, 15], F32)
        Jm, RJ = self.tile("Jm", [31, 160], F32)
        tps = self.rot("tps", [128, 512], F32, 4, ps=True)
        tp0, Rtp0 = tps.next()
        S.op("pe", lambda e: e.transpose(out=tp0[0:31, 0:120], in_=rp[:, :], identity=idf[0:120, 0:120]), reads=[Rrp, Ridf], writes=[Rtp0])
        S.op("dve", lambda e: e.tensor_copy(out=Rt[:].rearrange("p h a -> p (h a)"), in_=tp0[0:31, 0:120]), reads=[Rtp0], writes=[RRt])

        S.op("pool", lambda e: e.memset(Jm[:], 0.0), writes=[RJ])
        S.op("pool", lambda e: e.affine_select(out=Jm[:], in_=Jm[:], pattern=[[-1, 160]], compare_op=ALU.not_equal, fill=1.0, base=48, channel_multiplier=1),
             reads=[RJ], writes=[RJ])
        TE0, RTE0 = self.tile("TE0", [64, 8, 15, 64], BF16)
        for q0 in range(0, 64, 4):
            tp, Rtp = tps.next()

            def mmT(e, tp=tp, q0=q0):
                for qi in range(4):
                    qc = q0 + qi
                    ins = e.matmul(tp[0:64, qi * 120:(qi + 1) * 120], lhsT=Jm[:, 63 - qc:127 - qc], rhs=Rt[:].rearrange("p h a -> p (h a)"),
                                   start=True, stop=True)
                return ins
            S.op("pe", mmT, reads=[RJ, RRt], writes=[Rtp])
            S.op("act", lambda e, tp=tp, q0=q0: e.activation(out=TE0[:, :, :, q0:q0 + 4].rearrange("p h a q -> p (h a) q"),
                                                             in_=tp[0:64, 0:480].rearrange("p (q x) -> p x q", q=4), func=AF.Exp),
                 reads=[Rtp], writes=[RTE0])
        A, RA = self.tile("mA", [128, 64], F32)
        Q, RQ = self.tile("mQ", [128, 64], F32)
        Q2, RQ2 = self.tile("mQ2", [128, 64], F32)
        cm, Rcm = self.tile("cm", [128, 64], F32)

        def io(e):
            e.iota(A[0:64, :], pattern=[[-1, 64]], base=0, channel_multiplier=1, allow_small_or_imprecise_dtypes=True)
            e.iota(A[64:128, :], pattern=[[-1, 64]], base=0, channel_multiplier=1, allow_small_or_imprecise_dtypes=True)
            return e.iota(Q[:], pattern=[[1, 64]], base=0, channel_multiplier=0, allow_small_or_imprecise_dtypes=True)
        S.op("pool", io, writes=[RA, RQ])
        S.op("dve", lambda e: e.tensor_scalar(out=Q2[:], in0=Q[:], scalar1=8.0, scalar2=56.0, op0=ALU.max, op1=ALU.min), reads=[RQ], writes=[RQ2])
        S.op("dve", lambda e: e.tensor_tensor(out=Q[:], in0=Q[:], in1=Q2[:], op=ALU.subtract), reads=[RQ, RQ2], writes=[RQ])
        S.op("dve", lambda e: e.tensor_tensor(out=A[:], in0=A[:], in1=Q[:], op=ALU.add), reads=[RA, RQ], writes=[RA])
        S.op("dve", lambda e: e.tensor_single_scalar(out=Q[:], in_=A[:], scalar=-8.0, op=ALU.is_ge), reads=[RA], writes=[RQ])
        S.op("dve", lambda e: e.tensor_single_scalar(out=Q2[:], in_=A[:], scalar=7.0, op=ALU.is_le), reads=[RA], writes=[RQ2])
        S.op("dve", lambda e: e.tensor_tensor(out=cm[:], in0=Q[:], in1=Q2[:], op=ALU.mult), reads=[RQ, RQ2], writes=[Rcm])
        TE, RTE = self.tile("TE", [128, 8, 14, 64], BF16)
        S.op("dve", lambda e: e.tensor_tensor(out=TE0[:].rearrange("p h a q -> p (h a) q"), in0=TE0[:].rearrange("p h a q -> p (h a) q"),
                                              in1=cm[0:64, :].unsqueeze(1).broadcast_to([64, 120, 64]), op=ALU.mult),
             reads=[RTE0, Rcm], writes=[RTE0])
        S.op("pool", lambda e: e.tensor_copy(out=TE[0:64, :, :, :].rearrange("p h a q -> p h (a q)"),
                                             in_=TE0[:, :, 0:14, :].rearrange("p h a q -> p h (a q)")), reads=[RTE0], writes=[RTE])
        S.dma("sp", TE[64:128, :, :, :].rearrange("p h a q -> p h (a q)"), TE0[:, :, 1:15, :].rearrange("p h a q -> p h (a q)"),
              reads=[RTE0], writes=[RTE])
        qT, RqT = self.tile("nqT", [128, 4, T], BF16)
        kT, RkT = self.tile("nkT", [128, 4, T], BF16)
        Ve, RVe = self.tile("nVe", [128, nblk, 8, 65], BF16)
        Vo, RVo = self.tile("nVo", [128, nblk, 8, 65], BF16)
        Vs, RVs = self.tile("nVs", [128, nblk // 2, 512], BF16)
        S.op("pool", lambda e: e.memset(Ve[:], 1.0), writes=[RVe])
        S.op("pool", lambda e: e.memset(Vo[:], 1.0), writes=[RVo])
        pss = Rot([(t_[:].rearrange("p (h q) -> p h q", q=64), r_) for (t_, r_) in tps.items])
        pos_raw = self.rot("ops", [128, 512], F32, 4, ps=True)
        Ets = self.rot("Et", [128, 8, 64], BF16, 3)
        Es = self.rot("E", [128, 4, 8, 64], BF16, 2)
        recs = self.rot("rec", [64, 8], F32, 2)
        outs = self.rot("yno", [64, 8, 64], BF16, 3)
        for s in range(NS):
            S.dma("sp", qT[:], nq[s].rearrange("(c p) t -> p c t", p=128), reads=[self.dres(("nq", s))], writes=[RqT])
            S.dma("sp", kT[:], nk[s].rearrange("(c p) t -> p c t", p=128), reads=[self.dres(("nk", s))], writes=[RkT])
            hb = nblk // 2
            for (Vx, RVx, off, nb_tot) in ((Ve, RVe, 0, nblk), (Vo, RVo, 64, nblk - 1)):
                for b0 in range(0, nb_tot, hb):
                    nb_ = min(hb, nb_tot - b0)
                    S.dma("sp", Vs[:, 0:nb_, :], nv[s, off + b0 * 128:off + (b0 + nb_) * 128, :].rearrange("(b p) c -> p b c", p=128),
                          reads=[self.dres(("nv", s))], writes=[RVs])
                    S.op("pool", lambda e, Vx=Vx, b0=b0, nb_=nb_: e.tensor_copy(
                        out=Vx[:, b0:b0 + nb_, :, 0:64].rearrange("p b h n -> p (b h) n"),
                        in_=Vs[:, 0:nb_, :].rearrange("p b (h n) -> p (b h) n", n=64)), reads=[RVs], writes=[RVx])
            pending = None
            for i in range(rows):
                rs = min(max(i - 4, 0), rows - 8)
                E, RE = Es.next()
                for blk in range(4):
                    kr = rs + 2 * blk
                    tok0 = kr * 64
                    dib = kr - i + 7
                    psA, RpsA = pss.next()
                    psB, RpsB = pss.next()
                    Et, REt = Ets.next()

                    def mms(e, psA=psA, psB=psB, tok0=tok0, i=i):
                        for h in range(8):
                            pb = (h % 2) * 64
                            ps = psA if h % 2 == 0 else psB
                            ins = e.matmul(ps[:, h // 2, :], lhsT=kT[pb:pb + 64, h // 2, tok0:tok0 + 128], rhs=qT[pb:pb + 64, h // 2, i * 64:(i + 1) * 64],
                                           start=True, stop=True)
                        return ins
                    S.op("pe", mms, reads=[RkT, RqT], writes=[RpsA, RpsB])
                    S.op("act", lambda e, psA=psA, Et=Et: e.activation(out=Et[:, 0:4, :], in_=psA[:, 0:4, :], func=AF.Exp, scale=0.125), reads=[RpsA], writes=[REt])
                    S.op("act", lambda e, psB=psB, Et=Et: e.activation(out=Et[:, 4:8, :], in_=psB[:, 0:4, :], func=AF.Exp, scale=0.125), reads=[RpsB], writes=[REt])
                    S.op("dve", lambda e, Et=Et, E=E, blk=blk, dib=dib: e.tensor_tensor(
                        out=E[:, blk, :, :].rearrange("p (two c) q -> p two c q", two=2), in0=Et[:].rearrange("p (two c) q -> p two c q", two=2),
                        in1=TE[:].rearrange("p (c two) a q -> p two c a q", two=2)[:, :, :, dib, :], op=ALU.mult),
                         reads=[REt, RTE], writes=[RE])
                def finish(i=i, rs=rs, E=E, RE=RE, s=s):
                    poA_, RpoA = pos_raw.next()
                    poB_, RpoB = pos_raw.next()
                    poA = poA_[0:64, 0:260].rearrange("p (h n) -> p h n", n=65)
                    poB = poB_[0:64, 0:260].rearrange("p (h n) -> p h n", n=65)

                    def mmo(e, E=E, rs=rs, poA=poA, poB=poB):
                        for h in range(8):
                            po = poA if h < 4 else poB
                            for blk in range(4):
                                kr = rs + 2 * blk
                                Vx = Ve if kr % 2 == 0 else Vo
                                ins = e.matmul(po[:, h % 4, :], lhsT=E[:, blk, (h % 2) * 4 + h // 2, :], rhs=Vx[:, kr // 2, h, :], start=(blk == 0), stop=(blk == 3))
                        return ins
                    S.op("pe", mmo, reads=[RE, RVe, RVo], writes=[RpoA, RpoB])
                    rec, Rrec = recs.next()
                    o, Ro = outs.next()
                    S.op("dve", lambda e, rec=rec, poA=poA: e.reciprocal(out=rec[:, 0:4], in_=poA[:, :, 64]), reads=[RpoA], writes=[Rrec])
                    S.op("dve", lambda e, rec=rec, poB=poB: e.reciprocal(out=rec[:, 4:8], in_=poB[:, :, 64]), reads=[RpoB], writes=[Rrec])
                    S.op("dve", lambda e, rec=rec, poA=poA, o=o: e.tensor_tensor(out=o[:, 0:4, :], in0=poA[:, :, 0:64],
                                                                                in1=rec[:, 0:4].unsqueeze(2).broadcast_to([64, 4, 64]), op=ALU.mult),
                         reads=[RpoA, Rrec], writes=[Ro])
                    S.op("dve", lambda e, rec=rec, poB=poB, o=o: e.tensor_tensor(out=o[:, 4:8, :], in0=poB[:, :, 0:64],
                                                                                in1=rec[:, 4:8].unsqueeze(2).broadcast_to([64, 4, 64]), op=ALU.mult),
                         reads=[RpoB, Rrec], writes=[Ro])
                    S.dma("pool", yn[s, i * 64:(i + 1) * 64, :], o[:].rearrange("p h n -> p (h n)"), reads=[Ro], writes=[self.dres(("yn", s))])
                if pending is not None:
                    pending()
                pending = finish
            pending()
            pending = None
        self.end()

    def xupdate(self, xt, Rxt, nb, aT, RaT, nkc, W, RW, pss):
        S = self.S
        for b in range(nb):
            for half in range(2):
                ps, Rps = pss.next()

                def mm(e, ps=ps, b=b, half=half):
                    for kc in range(nkc):
                        ins = e.matmul(ps[:], lhsT=aT[:, kc, b * 128:(b + 1) * 128], rhs=W[:, kc, half * 512:(half + 1) * 512],
                                       start=(kc == 0), stop=(kc == nkc - 1))
                    return ins
                S.op("pe", mm, reads=[RaT, RW], writes=[Rps])
                S.op("dve", lambda e, ps=ps, b=b, half=half: e.tensor_tensor(
                    out=xt[:, b, half * 512:(half + 1) * 512], in0=xt[:, b, half * 512:(half + 1) * 512], in1=ps[:], op=ALU.add),
                    reads=[Rps, Rxt], writes=[Rxt])

    def phase_p4(self, l):
        S, T, NS = self.S, self.T, self.NS
        self.begin()
        TT = 512
        wbr, Rwbr = self.tile("wbr", [128, 8, D], BF16)
        wout, Rwout = self.tile("wout", [128, 8, D], BF16)
        stg = self.rot("wstg", [128, 2048], F32, 2)
        ident, Rid, _, _ = self.make_ident()
        self.load_weight(wbr, Rwbr, self.w["w_br_rwkv"][l], 512, D, stg, kc0=0)
        self.load_weight(wbr, Rwbr, self.w["w_br_nat"][l], 512, D, stg, kc0=4)
        self.load_weight(wout, Rwout, self.w["w_out"][l], D, D, stg)
        xts = self.rot("xt", [128, 4, D], F32, 2)
        yts = self.rot("yt", [128, 4, 1024], BF16, 2)
        yTs = self.rot("yT", [128, 8, TT], BF16, 2)
        sgts = self.rot("sgt", [128, 16, TT], BF16, 2)
        mTs = self.rot("mT", [128, 8, TT], BF16, 2)
        m1s = self.rot("m1", [128, TT], F32, 2)
        m2s = self.rot("m2", [128, TT], F32, 2)
        pTs = self.rot("pT", [128, 1024], BF16, 2, ps=True)
        pss = self.rot("ps", [128, 512], F32, 6, ps=True)
        xsrc = self.x_in if l == 0 else self.scr["xres"]
        xres, yr, yn, sg = (self.scr[k] for k in ("xres", "yr", "yn", "sg"))
        for s in range(NS):
            for t0 in range(0, T, TT):
                xt, Rxt = xts.next()
                yt, Ryt = yts.next()
                yT, RyT = yTs.next()
                sgt, Rsgt = sgts.next()
                mT, RmT = mTs.next()
                Rx = self.dres(("xres", s, t0))
                S.dma("sp", yt[:, :, 0:512], yr[s, t0:t0 + TT, :].rearrange("(b p) c -> p b c", p=128), reads=[self.dres(("yr", s))], writes=[Ryt])
                S.dma("sp", yt[:, :, 512:1024], yn[s, t0:t0 + TT, :].rearrange("(b p) c -> p b c", p=128), reads=[self.dres(("yn", s))], writes=[Ryt])
                S.dma("sp", sgt[:], sg[s, :, t0:t0 + TT].rearrange("(c p) t -> p c t", p=128), reads=[self.dres(("sg", s))], writes=[Rsgt])
                S.dma("sp", xt[:], xsrc[s, t0:t0 + TT, :].rearrange("(b p) d -> p b d", p=128), reads=([Rx] if l > 0 else []), writes=[Rxt])
                for b in range(4):
                    pT, RpT = pTs.next()

                    def tr(e, pT=pT, b=b, yt=yt):
                        for c in range(8):
                            ins = e.transpose(out=pT[:, c * 128:(c + 1) * 128], in_=yt[:, b, c * 128:(c + 1) * 128], identity=ident[:])
                        return ins
                    S.op("pe", tr, reads=[Ryt, Rid], writes=[RpT])
                    S.op("act", lambda e, pT=pT, b=b, yT=yT: e.copy(out=yT[:, :, b * 128:(b + 1) * 128], in_=pT[:].rearrange("p (k t) -> p k t", k=8)),
                         reads=[RpT], writes=[RyT])
                for oc in range(8):
                    ps1, Rps1 = pss.next()
                    ps2, Rps2 = pss.next()
                    m1, Rm1 = m1s.next()
                    m2, Rm2 = m2s.next()

                    def mm(e, ps1=ps1, ps2=ps2, oc=oc, yT=yT):
                        for kc in range(4):
                            e.matmul(ps1[:], lhsT=wbr[:, kc, oc * 128:(oc + 1) * 128], rhs=yT[:, kc, :], start=(kc == 0), stop=(kc == 3))
                        for kc in range(4):
                            ins = e.matmul(ps2[:], lhsT=wbr[:, 4 + kc, oc * 128:(oc + 1) * 128], rhs=yT[:, 4 + kc, :], start=(kc == 0), stop=(kc == 3))
                        return ins
                    S.op("pe", mm, reads=[Rwbr, RyT], writes=[Rps1, Rps2])
                    S.op("dve", lambda e, m1=m1, ps1=ps1, oc=oc, sgt=sgt: e.tensor_tensor(out=m1[:], in0=ps1[:], in1=sgt[:, oc, :], op=ALU.mult),
                         reads=[Rps1, Rsgt], writes=[Rm1])
                    S.op("dve", lambda e, m2=m2, ps2=ps2, oc=oc, sgt=sgt: e.tensor_tensor(out=m2[:], in0=ps2[:], in1=sgt[:, 8 + oc, :], op=ALU.mult),
                         reads=[Rps2, Rsgt], writes=[Rm2])
                    S.op("pool", lambda e, m1=m1, m2=m2, oc=oc, mT=mT: e.tensor_tensor(out=mT[:, oc, :], in0=m1[:], in1=m2[:], op=ALU.add),
                         reads=[Rm1, Rm2], writes=[RmT])
                self.xupdate(xt, Rxt, 4, mT, RmT, 8, wout, Rwout, pss)
                S.dma("pool", xres[s, t0:t0 + TT, :].rearrange("(b p) d -> p b d", p=128), xt[:], reads=[Rxt], writes=[Rx])
        self.end()

    def phase_p5(self, l):
        S, T, NS = self.S, self.T, self.NS
        self.begin()
        TT = 512
        wq, Rwq = self.tile("wq", [128, 8, D], BF16)
        wo, Rwo = self.tile("wo", [128, 8, D], BF16)
        wkv, Rwkv = self.tile("wkv", [128, 8, 2 * D], BF16)
        stg = self.rot("wstg", [128, 2048], F32, 2)
        ident, Rid, _, _ = self.make_ident()
        gb, Rgb = self.tile("gb", [128, D], F32)
        gm, Rgm = self.tile("gm", [128, D], F32)
        ones, Rones = self.tile("ones", [128, 128], BF16)
        S.op("pool", lambda e: e.memset(ones[:], 1.0), writes=[Rones])
        self.load_bcast(gb, Rgb, self.w["norm_x"][l], D)
        self.load_bcast(gm, Rgm, self.w["norm_mem"][l], D)
        self.load_weight(wkv, Rwkv, self.w["w_xkv"][l], D, 2 * D, stg)
        self.load_weight(wq, Rwq, self.w["w_xq"][l], D, D, stg)
        self.load_weight(wo, Rwo, self.w["w_xo"][l], D, D, stg)
        tmp = self.norm_tmp(4)
        xts = self.rot("xt", [128, 4, D], F32, 2)
        hTs = self.rot("hT", [128, 8, TT], BF16, 2)
        qTs = self.rot("qT", [128, 8, TT], BF16, 1)
        oTs = self.rot("oT", [128, 8, TT], BF16, 2)
        Es = self.rot("E", [128, 2, TT], BF16, 2)
        rdens = self.rot("rden", [128, TT], F32, 2)
        mt, Rmt = self.tile("memt", [128, 2, D], F32)
        mnT, RmnT = self.tile("memnT", [128, 8, NMEM], BF16)
        kT, RkT = self.tile("kT", [128, 8, NMEM], BF16)
        Vm, RVm = self.tile("Vm", [128, 2, D], BF16)
        pss = self.rot("ps", [128, 512], F32, 6, ps=True)
        xres = self.scr["xres"]
        for s in range(NS):
            S.dma("sp", mt[:], self.mem_in[s].rearrange("(b p) d -> p b d", p=128), writes=[Rmt])
            for b in range(2):
                self.norm_to_hT(mt, Rmt, b, gm, Rgm, mnT, RmnT, b * 128, ident, Rid, tmp)
            for cc in range(8):
                ps, Rps = pss.next()

                def mmk(e, ps=ps, cc=cc):
                    for kc in range(8):
                        ins = e.matmul(ps[:, 0:NMEM], lhsT=wkv[:, kc, cc * 128:(cc + 1) * 128], rhs=mnT[:, kc, :], start=(kc == 0), stop=(kc == 7))
                    return ins
                S.op("pe", mmk, reads=[Rwkv, RmnT], writes=[Rps])
                S.op("dve", lambda e, ps=ps, cc=cc: e.tensor_copy(out=kT[:, cc, :], in_=ps[:, 0:NMEM]), reads=[Rps], writes=[RkT])
            for mb in range(2):
                for half in range(2):
                    ps, Rps = pss.next()

                    def mmv(e, ps=ps, mb=mb, half=half):
                        for kc in range(8):
                            ins = e.matmul(ps[:], lhsT=mnT[:, kc, mb * 128:(mb + 1) * 128], rhs=wkv[:, kc, D + half * 512:D + (half + 1) * 512],
                                           start=(kc == 0), stop=(kc == 7))
                        return ins
                    S.op("pe", mmv, reads=[Rwkv, RmnT], writes=[Rps])
                    S.op("dve", lambda e, ps=ps, mb=mb, half=half: e.tensor_copy(out=Vm[:, mb, half * 512:(half + 1) * 512], in_=ps[:]),
                         reads=[Rps], writes=[RVm])
            t0s_ = list(range(0, T, TT))

            def prep_(j, s=s):
                t0 = t0s_[j]
                xt, Rxt = xts.next()
                Rx = self.dres(("xres", s, t0))
                S.dma("sp", xt[:], xres[s, t0:t0 + TT, :].rearrange("(b p) d -> p b d", p=128), reads=[Rx], writes=[Rxt])
                return xt, Rxt, Rx, [self.norm_A(xt, Rxt, b, gb, Rgb, tmp) for b in range(4)]
            nxt_ = prep_(0)
            for j_, t0 in enumerate(t0s_):
                xt, Rxt, Rx, hs_ = nxt_
                hT, RhT = hTs.next()
                qT, RqT = qTs.next()
                oT, RoT = oTs.next()
                for b in range(4):
                    self.norm_B(hs_[b][0], hs_[b][1], hT, RhT, b * 128, ident, Rid, tmp)
                nxt_ = prep_(j_ + 1) if j_ + 1 < len(t0s_) else None
                for cc in range(8):
                    ps, Rps = pss.next()

                    def mmq(e, ps=ps, cc=cc, hT=hT):
                        for kc in range(8):
                            ins = e.matmul(ps[:], lhsT=wq[:, kc, cc * 128:(cc + 1) * 128], rhs=hT[:, kc, :], start=(kc == 0), stop=(kc == 7))
                        return ins
                    S.op("pe", mmq, reads=[Rwq, RhT], writes=[Rps])
                    S.op("dve", lambda e, ps=ps, cc=cc, qT=qT: e.tensor_copy(out=qT[:, cc, :], in_=ps[:]), reads=[Rps], writes=[RqT])
                for hd in range(4):
                    E, RE = Es.next()
                    rden, Rrden = rdens.next()
                    for mb in range(2):
                        ps, Rps = pss.next()

                        def mms(e, ps=ps, mb=mb, hd=hd, qT=qT):
                            for j in range(2):
                                ins = e.matmul(ps[:], lhsT=kT[:, 2 * hd + j, mb * 128:(mb + 1) * 128], rhs=qT[:, 2 * hd + j, :], start=(j == 0), stop=(j == 1))
                            return ins
                        S.op("pe", mms, reads=[RkT, RqT], writes=[Rps])
                        S.op("act", lambda e, ps=ps, mb=mb, E=E: e.activation(out=E[:, mb, :], in_=ps[:], func=AF.Exp, scale=1.0 / 16.0),
                             reads=[Rps], writes=[RE])
                    ps, Rps = pss.next()

                    def mmd(e, ps=ps, E=E):
                        for mb in range(2):
                            ins = e.matmul(ps[:], lhsT=ones[:], rhs=E[:, mb, :], start=(mb == 0), stop=(mb == 1))
                        return ins
                    S.op("pe", mmd, reads=[Rones, RE], writes=[Rps])
                    S.op("dve", lambda e, ps=ps, rden=rden: e.reciprocal(out=rden[:], in_=ps[:]), reads=[Rps], writes=[Rrden])
                    for j in range(2):
                        ps, Rps = pss.next()

                        def mmo(e, ps=ps, hd=hd, j=j, E=E):
                            for mb in range(2):
                                c0 = hd * 256 + j * 128
                                ins = e.matmul(ps[:], lhsT=Vm[:, mb, c0:c0 + 128], rhs=E[:, mb, :], start=(mb == 0), stop=(mb == 1))
                            return ins
                        S.op("pe", mmo, reads=[RVm, RE], writes=[Rps])
                        S.op("dve", lambda e, ps=ps, hd=hd, j=j, oT=oT, rden=rden: e.tensor_tensor(out=oT[:, 2 * hd + j, :], in0=ps[:], in1=rden[:], op=ALU.mult),
                             reads=[Rps, Rrden], writes=[RoT])
                self.xupdate(xt, Rxt, 4, oT, RoT, 8, wo, Rwo, pss)
                S.dma("pool", xres[s, t0:t0 + TT, :].rearrange("(b p) d -> p b d", p=128), xt[:], reads=[Rxt], writes=[Rx])
        self.end()

    def phase_p6(self, l):
        S, T, NS = self.S, self.T, self.NS
        self.begin()
        TT = 256
        NB = TT // 128
        last = (l == self.DEPTH - 1)
        w1, Rw1 = self.tile("w1", [128, 8, DFF], BF16)
        w2, Rw2 = self.tile("w2", [128, 32, D], BF16)
        stg = self.rot("wstg", [128, 2048], F32, 2)
        ident, Rid, _, _ = self.make_ident()
        gb, Rgb = self.tile("gb", [128, D], F32)
        self.load_bcast(gb, Rgb, self.w["norm_ff"][l], D)
        if last:
            gf, Rgf = self.tile("gf", [128, D], F32)
            self.load_bcast(gf, Rgf, self.w["norm_final"], D)
        self.load_weight(w1, Rw1, self.w["w_ff1"][l], D, DFF, stg)
        self.load_weight(w2, Rw2, self.w["w_ff2"][l], DFF, D, stg)
        tmp = self.norm_tmp()
        xts = self.rot("xt", [128, NB, D], F32, 2)
        hTs = self.rot("hT", [128, 8, TT], BF16, 2)
        uTs = self.rot("uT", [128, 32, TT], BF16, 1)
        rls = self.rot("rl", [128, TT], BF16, 3)
        pss = self.rot("ps", [128, 512], F32, 6, ps=True)
        xres = self.scr["xres"]
        tiles_ = [(s, t0) for s in range(NS) for t0 in range(0, T, TT)]

        def prep_(idx):
            s, t0 = tiles_[idx]
            xt, Rxt = xts.next()
            Rx = self.dres(("xres", s, t0 // 512 * 512))
            S.dma("sp", xt[:], xres[s, t0:t0 + TT, :].rearrange("(b p) d -> p b d", p=128), reads=[Rx], writes=[Rxt])
            return xt, Rxt, Rx, [self.norm_A(xt, Rxt, b, gb, Rgb, tmp) for b in range(NB)]
        nxt_ = prep_(0)
        for idx_, (s, t0) in enumerate(tiles_):
            if True:
                xt, Rxt, Rx, hs_ = nxt_
                hT, RhT = hTs.next()
                uT, RuT = uTs.next()
                for b in range(NB):
                    self.norm_B(hs_[b][0], hs_[b][1], hT, RhT, b * 128, ident, Rid, tmp)
                nxt_ = prep_(idx_ + 1) if idx_ + 1 < len(tiles_) else None
                for fc in range(32):
                    ps, Rps = pss.next()
                    rl, Rrl = rls.next()

                    def mm1(e, ps=ps, fc=fc, hT=hT):
                        for kc in range(8):
                            ins = e.matmul(ps[:, 0:TT], lhsT=w1[:, kc, fc * 128:(fc + 1) * 128], rhs=hT[:, kc, :], start=(kc == 0), stop=(kc == 7))
                        return ins
                    S.op("pe", mm1, reads=[Rw1, RhT], writes=[Rps])
                    S.op("act", lambda e, ps=ps, rl=rl: e.activation(out=rl[:], in_=ps[:, 0:TT], func=AF.Relu), reads=[Rps], writes=[Rrl])
                    S.op("pool", lambda e, rl=rl, fc=fc, uT=uT: e.tensor_tensor(out=uT[:, fc, :], in0=rl[:], in1=rl[:], op=ALU.mult),
                         reads=[Rrl], writes=[RuT])
                self.xupdate(xt, Rxt, NB, uT, RuT, 32, w2, Rw2, pss)
                if not last:
                    S.dma("pool", xres[s, t0:t0 + TT, :].rearrange("(b p) d -> p b d", p=128), xt[:], reads=[Rxt], writes=[Rx])
                else:
                    for b in range(NB):
                        junk, Rjunk = tmp["junk"].next()
                        ss, Rss = tmp["ss"].next()
                        S.op("act", lambda e, junk=junk, ss=ss, b=b, xt=xt: e.activation(out=junk[:], in_=xt[:, b, :], func=AF.Square, accum_out=ss[:, 0:1]),
                             reads=[Rxt], writes=[Rjunk, Rss])
                        S.op("act", lambda e, ss=ss: e.activation(out=ss[:, 1:2], in_=ss[:, 0:1], func=AF.Sqrt, scale=1.0 / D, bias=self.eps_t[:, 0:1]),
                             reads=[Rss, self.Reps], writes=[Rss])
                        S.op("dve", lambda e, ss=ss: e.reciprocal(out=ss[:, 2:3], in_=ss[:, 1:2]), reads=[Rss], writes=[Rss])
                        S.op("dve", lambda e, ss=ss, b=b, xt=xt: e.scalar_tensor_tensor(out=xt[:, b, :], in0=xt[:, b, :], scalar=ss[:, 2:3], in1=gf[:],
                                                                                         op0=ALU.mult, op1=ALU.mult),
                             reads=[Rxt, Rss, Rgf], writes=[Rxt])
                    Ry = self.dres(("y", s))
                    S.dma("pool", self.y_out[s, t0:t0 + TT, :].rearrange("(b p) d -> p b d", p=128), xt[:], reads=[Rxt], writes=[Ry])
        self.end()


_CACHE = {}


def _get_nc(T, NS, DEPTH, dbg=(), stop=None):
    key = (T, NS, DEPTH, tuple(dbg), stop)
    if key not in _CACHE:
        _CACHE[key] = Builder(T, NS, DEPTH, dbg, stop).build()
    return _CACHE[key]


def kernel(**inputs):
    NCORES, NS, T, DEPTH = 8, 2, 4096, 2
    xs = np.concatenate([np.asarray(inputs["x_prompt"], np.float32), np.asarray(inputs["x_sample"], np.float32)], 0)
    ms = np.concatenate([np.asarray(inputs["mem_prompt"], np.float32), np.asarray(inputs["mem_sample"], np.float32)], 0)
    nseq = xs.shape[0]
    slots = [[c, 8 + c if 8 + c < nseq else c] for c in range(NCORES)]
    wnames = [n for n, _ in WEIGHT_SPECS] + ["norm_final"]
    wmap = {n: np.ascontiguousarray(np.asarray(inputs[n], np.float32)) for n in wnames}
    nc = _get_nc(T, NS, DEPTH)
    in_maps = []
    for c in range(NCORES):
        m = dict(wmap)
        m["x"] = np.ascontiguousarray(xs[slots[c]])
        m["mem"] = np.ascontiguousarray(ms[slots[c]])
        in_maps.append(m)
    res = run_bass_kernel_spmd(nc, in_maps, core_ids=list(range(NCORES)))
    y = np.zeros_like(xs)
    for c in range(NCORES):
        yc = res.results[c]["y"]
        y[slots[c][0]] = yc[0]
        if slots[c][1] != slots[c][0]:
            y[slots[c][1]] = yc[1]
    nb = np.asarray(inputs["x_prompt"]).shape[0]
    return (y[:nb], y[nb:])
```
